# Optimizing a Trainium2 kernel written in Bass

```python
import math
import jax, jax.numpy as jnp
from jax import lax
import numpy as np

D_MODEL = 1024
BATCH = 2
SEQ = 8192
DEPTH = 1

D_MIX = D_MODEL
HEAD_DIM = 64
ATT_WIDTH = D_MIX // 2
N_Q_HEADS = ATT_WIDTH // HEAD_DIM
N_KV_HEADS = 2
Q_WIDTH = N_Q_HEADS * HEAD_DIM
KV_WIDTH = N_KV_HEADS * HEAD_DIM
HY_WIDTH = D_MIX - ATT_WIDTH
HY_ORDER = 2
IN_PROJ_WIDTH = Q_WIDTH + 2 * KV_WIDTH + (HY_ORDER + 1) * HY_WIDTH
D_FF = 4 * D_MODEL

GRID_W = 64
Q_BLOCK = 128
ROPE_THETA = 10000.0

SHORT_CONV = 3
FILTER_EMB = 33
FILTER_HIDDEN = 64
N_DIRS = 2
N_FILT = HY_ORDER * N_DIRS * HY_WIDTH
SHORT_DECAY_PCT = 0.3
LONG_DECAY_PCT = 1.5
DECAY_TARGET = 1e-2

EPS = 1e-6

kernel_name = 'hymba_attn_hyena_encoder'


def rms_norm(x, g):
    xf = x.astype(jnp.float32)
    y = xf * lax.rsqrt(jnp.mean(xf * xf, axis=-1, keepdims=True) + EPS)
    return (y * g.astype(jnp.float32)).astype(x.dtype)


def axial_rope_tables(seq_len):
    rows = seq_len // GRID_W
    row = jnp.repeat(jnp.arange(rows, dtype=jnp.float32), GRID_W)
    col = jnp.tile(jnp.arange(GRID_W, dtype=jnp.float32), rows)
    half = HEAD_DIM // 2
    inv_freq = ROPE_THETA ** (-jnp.arange(0, half, 2, dtype=jnp.float32) / half)
    ang_r = row[:, None] * inv_freq[None, :]
    ang_c = col[:, None] * inv_freq[None, :]
    return (jnp.cos(ang_r), jnp.sin(ang_r), jnp.cos(ang_c), jnp.sin(ang_c))


def _rotate(x, cos, sin):
    x1, x2 = jnp.split(x, 2, axis=-1)
    cos = cos[None, :, None, :]
    sin = sin[None, :, None, :]
    return jnp.concatenate([x1 * cos - x2 * sin, x2 * cos + x1 * sin], axis=-1)


def apply_axial_rope(x, cos_r, sin_r, cos_c, sin_c):
    xf = x.astype(jnp.float32)
    half = HEAD_DIM // 2
    out = jnp.concatenate([_rotate(xf[..., :half], cos_r, sin_r),
                           _rotate(xf[..., half:], cos_c, sin_c)], axis=-1)
    return out.astype(x.dtype)


def block_attention(q, k, v):
    b, seq_len = q.shape[0], q.shape[1]
    n_blocks = seq_len // Q_BLOCK
    group = N_Q_HEADS // N_KV_HEADS
    scale = HEAD_DIM ** -0.5
    qb = q.reshape(b, n_blocks, Q_BLOCK, N_KV_HEADS, group, HEAD_DIM).transpose(1, 0, 2, 3, 4, 5)

    def one_block(q_blk):
        s = jnp.einsum('bqkgd,bskd->bkgqs', q_blk, k, preferred_element_type=jnp.float32) * scale
        p = jax.nn.softmax(s, axis=-1).astype(v.dtype)
        return jnp.einsum('bkgqs,bskd->bqkgd', p, v)

    o = lax.map(one_block, qb)
    return o.transpose(1, 0, 2, 3, 4, 5).reshape(b, seq_len, Q_WIDTH)


def short_conv(u, w, bias):
    c = u.shape[-1]
    y = lax.conv_general_dilated(u, w[:, None, :], window_strides=(1,), padding=[(1, 1)],
                                 dimension_numbers=('NWC', 'WIO', 'NWC'), feature_group_count=c)
    return y + bias


def hyena_filters_freq(seq_len, w1, b1, w2, b2, w3, b3, w4, freq, deltas):
    f32 = jnp.float32
    w1, b1, w2, b2, w3, b3, w4, freq, deltas = [a.astype(f32) for a in (w1, b1, w2, b2, w3, b3, w4, freq, deltas)]
    t = jnp.linspace(0.0, 1.0, seq_len, dtype=f32)[:, None]
    bands = (FILTER_EMB - 1) // 2
    w_ang = 2.0 * math.pi * jnp.arange(seq_len, dtype=f32) / seq_len
    band_f = jnp.linspace(1e-4, bands - 1, bands, dtype=f32)
    ang = w_ang[:, None] * band_f[None, :]
    z = jnp.concatenate([t, jnp.cos(ang), -jnp.sin(ang)], axis=-1)
    h = jnp.sin(freq * (z @ w1 + b1))
    h = jnp.sin(freq * (h @ w2 + b2))
    h = jnp.sin(freq * (h @ w3 + b3))
    h = h @ w4
    h = h * jnp.exp(-t * jnp.abs(deltas)[None, :])
    h = h.reshape(seq_len, HY_ORDER, N_DIRS, HY_WIDTH)
    fwd = h[:, :, 0]
    bwd = h[:, :, 1]
    taps = jnp.concatenate([fwd, jnp.zeros((1, HY_ORDER, HY_WIDTH), f32), bwd[:0:-1]], axis=0)
    taps = taps / jnp.sum(jnp.abs(taps), axis=0, keepdims=True)
    return jnp.fft.rfft(taps, axis=0)


def fft_long_conv(z, taps_f):
    seq_len = z.shape[1]
    zf = jnp.fft.rfft(z.astype(jnp.float32), n=2 * seq_len, axis=1)
    return jnp.fft.irfft(zf * taps_f[None], n=2 * seq_len, axis=1)[:, :seq_len]


def hyena_mixer(u, conv_w, conv_b, w1, b1, w2, b2, w3, b3, w4, freq, deltas, skip_d):
    seq_len = u.shape[1]
    uc = short_conv(u, conv_w, conv_b)
    v, x1, x2 = jnp.split(uc, HY_ORDER + 1, axis=-1)
    taps_f = hyena_filters_freq(seq_len, w1, b1, w2, b2, w3, b3, w4, freq, deltas)
    zz = v.astype(jnp.float32)
    gates = (x1, x2)
    for n in range(HY_ORDER):
        conv = fft_long_conv(zz, taps_f[:, n]) + skip_d[n].astype(jnp.float32) * zz
        zz = gates[n].astype(jnp.float32) * conv
    return zz.astype(u.dtype)


def setup_inputs(seed: int = 0) -> dict:
    key = jax.random.key(seed)
    ks = jax.random.split(key, 24)
    f32 = jnp.float32

    def nrm(k, shape, scale):
        return jax.random.normal(k, shape, f32) * scale

    def gain(k, shape):
        return 1.0 + 0.02 * jax.random.normal(k, shape, f32)

    min_decay = math.log(DECAY_TARGET) / LONG_DECAY_PCT
    max_decay = math.log(DECAY_TARGET) / SHORT_DECAY_PCT
    base = jnp.tile(jnp.linspace(min_decay, max_decay, HY_WIDTH, dtype=f32), HY_ORDER * N_DIRS)

    return {
        'x': jax.random.normal(ks[0], (BATCH, SEQ, D_MODEL), f32),
        'norm1_g': gain(ks[1], (DEPTH, D_MODEL)),
        'w_in': nrm(ks[2], (DEPTH, D_MODEL, IN_PROJ_WIDTH), D_MODEL ** -0.5),
        'q_norm_g': gain(ks[3], (DEPTH, HEAD_DIM)),
        'k_norm_g': gain(ks[4], (DEPTH, HEAD_DIM)),
        'hy_conv_w': nrm(ks[5], (DEPTH, SHORT_CONV, (HY_ORDER + 1) * HY_WIDTH), SHORT_CONV ** -0.5),
        'hy_conv_b': nrm(ks[6], (DEPTH, (HY_ORDER + 1) * HY_WIDTH), 0.02),
        'filt_w1': nrm(ks[7], (DEPTH, FILTER_EMB, FILTER_HIDDEN), FILTER_EMB ** -0.5),
        'filt_b1': nrm(ks[8], (DEPTH, FILTER_HIDDEN), 0.02),
        'filt_w2': nrm(ks[9], (DEPTH, FILTER_HIDDEN, FILTER_HIDDEN), FILTER_HIDDEN ** -0.5),
        'filt_b2': nrm(ks[10], (DEPTH, FILTER_HIDDEN), 0.02),
        'filt_w3': nrm(ks[11], (DEPTH, FILTER_HIDDEN, FILTER_HIDDEN), FILTER_HIDDEN ** -0.5),
        'filt_b3': nrm(ks[12], (DEPTH, FILTER_HIDDEN), 0.02),
        'filt_w4': nrm(ks[13], (DEPTH, FILTER_HIDDEN, N_FILT), FILTER_HIDDEN ** -0.5),
        'filt_freq': gain(ks[14], (DEPTH, FILTER_HIDDEN)),
        'filt_deltas': base[None, :] + 0.05 * jax.random.normal(ks[15], (DEPTH, N_FILT), f32),
        'hy_skip_d': nrm(ks[16], (DEPTH, HY_ORDER, HY_WIDTH), 0.1),
        'attn_out_g': gain(ks[17], (DEPTH, ATT_WIDTH)),
        'hy_out_g': gain(ks[18], (DEPTH, HY_WIDTH)),
        'w_out': nrm(ks[19], (DEPTH, D_MIX, D_MODEL), D_MIX ** -0.5),
        'norm2_g': gain(ks[20], (DEPTH, D_MODEL)),
        'w_mlp_in': nrm(ks[21], (DEPTH, D_MODEL, D_FF), D_MODEL ** -0.5),
        'w_mlp_out': nrm(ks[22], (DEPTH, D_FF, D_MODEL), D_FF ** -0.5),
        'final_g': gain(ks[23], (D_MODEL,)),
    }


def reference(x, norm1_g, w_in, q_norm_g, k_norm_g, hy_conv_w, hy_conv_b,
              filt_w1, filt_b1, filt_w2, filt_b2, filt_w3, filt_b3, filt_w4,
              filt_freq, filt_deltas, hy_skip_d, attn_out_g, hy_out_g, w_out,
              norm2_g, w_mlp_in, w_mlp_out, final_g):
    b, seq_len = x.shape[0], x.shape[1]
    cos_r, sin_r, cos_c, sin_c = axial_rope_tables(seq_len)
    h = x
    for i in range(DEPTH):
        a = rms_norm(h, norm1_g[i])
        proj = a @ w_in[i]
        q, k, v, u = jnp.split(proj, [Q_WIDTH, Q_WIDTH + KV_WIDTH, Q_WIDTH + 2 * KV_WIDTH], axis=-1)
        q = rms_norm(q.reshape(b, seq_len, N_Q_HEADS, HEAD_DIM), q_norm_g[i])
        k = rms_norm(k.reshape(b, seq_len, N_KV_HEADS, HEAD_DIM), k_norm_g[i])
        v = v.reshape(b, seq_len, N_KV_HEADS, HEAD_DIM)
        q = apply_axial_rope(q, cos_r, sin_r, cos_c, sin_c)
        k = apply_axial_rope(k, cos_r, sin_r, cos_c, sin_c)
        att = block_attention(q, k, v)
        hy = hyena_mixer(u, hy_conv_w[i], hy_conv_b[i], filt_w1[i], filt_b1[i], filt_w2[i],
                         filt_b2[i], filt_w3[i], filt_b3[i], filt_w4[i], filt_freq[i],
                         filt_deltas[i], hy_skip_d[i])
        mix = jnp.concatenate([rms_norm(att, attn_out_g[i]), rms_norm(hy, hy_out_g[i])], axis=-1)
        h = h + mix @ w_out[i]
        m = rms_norm(h, norm2_g[i])
        h = h + jnp.square(jax.nn.relu(m @ w_mlp_in[i])) @ w_mlp_out[i]
    return rms_norm(h, final_g)
```

```python
import math
import numpy as np
import ml_dtypes
import concourse.bass as bass
import concourse.mybir as mybir
from concourse.bass_utils import run_bass_kernel_spmd

F32 = mybir.dt.float32
BF16 = mybir.dt.bfloat16
ALU = mybir.AluOpType
AF = mybir.ActivationFunctionType
bf16_np = ml_dtypes.bfloat16

D = 1024
L = 8192
NB = 2
NTOK = 2048
HD = 64
NQH = 8
NKV = 2
HYW = 512
DFF = 4096
EPS = 1e-6
NFFT = 2 * L
TWO_PI = 2.0 * math.pi
MAGIC = 12582912.0


class Op:
    __slots__ = ("eng", "fn", "reads", "writes", "dma", "idx", "gid", "waits", "signal", "clock",
                 "sem", "val")


class Sched:
    CE = ("pe", "act", "dve", "pool", "sp")
    EPOCH = 12000
    NDS = 40

    def __init__(self, nc):
        self.nc = nc
        self.E = {"pe": nc.tensor, "act": nc.scalar, "dve": nc.vector, "pool": nc.gpsimd,
                  "sp": nc.sync}
        self.ops = []
        self.last_w = {}
        self.readers = {}
        self.known = {e: {} for e in self.CE}
        self.known_dma = {e: set() for e in self.CE}
        self.cnt = {e: 0 for e in self.CE}
        self.last_op = {e: None for e in self.CE}
        self.live_dma = []

    def op(self, eng, fn, reads=(), writes=(), dma=False):
        o = Op()
        o.eng, o.fn, o.reads, o.writes, o.dma = eng, fn, tuple(reads), tuple(writes), dma
        o.signal = dma
        o.sem = None
        o.val = None
        o.gid = len(self.ops)
        deps = {}
        for k in o.reads:
            w = self.last_w.get(k)
            if w is not None:
                deps[w.gid] = w
        for k in o.writes:
            w = self.last_w.get(k)
            if w is not None:
                deps[w.gid] = w
            for r in self.readers.get(k, ()):
                deps[r.gid] = r
        self._finish(o, deps.values())
        for k in o.writes:
            self.last_w[k] = o
            self.readers[k] = []
        for k in o.reads:
            lst = self.readers.setdefault(k, [])
            if not dma:
                lst[:] = [r for r in lst if r.dma or r.eng != eng]
            lst.append(o)
        return o

    def _finish(self, o, deps):
        eng = o.eng
        kn = self.known[eng]
        kd = self.known_dma[eng]
        waits = []
        best = {}
        rset = set(o.reads)
        for d in deps:
            if d.dma:
                if d.gid not in kd:
                    waits.append(d)
                    kd.add(d.gid)
                continue
            if d.eng == eng:
                if eng in ("pe", "sp"):
                    continue
            if kn.get(d.eng, -1) >= d.idx:
                continue
            b = best.get(d.eng)
            if b is None or b.idx < d.idx:
                best[d.eng] = d
        for d in best.values():
            waits.append(d)
            if kn.get(d.eng, -1) < d.idx:
                kn[d.eng] = d.idx
            for e2, i2 in d.clock.items():
                if kn.get(e2, -1) < i2:
                    kn[e2] = i2
        for d in waits:
            d.signal = True
        o.waits = waits
        o.idx = self.cnt[eng]
        self.cnt[eng] += 1
        o.clock = dict(kn)
        self.ops.append(o)
        if o.fn is not None:
            self.last_op[eng] = o
        if o.dma:
            self.live_dma.append(o)

    def barrier(self):
        lasts = [self.last_op[e] for e in self.CE if self.last_op[e] is not None]
        dmas = list(self.live_dma)
        for e in self.CE:
            o = Op()
            o.eng, o.fn, o.reads, o.writes, o.dma = e, None, (), (), False
            o.signal = False
            o.sem = None
            o.val = None
            o.gid = len(self.ops)
            deps = [d for d in lasts if not d.dma and d.eng != e] + dmas
            self._finish(o, deps)
        self.live_dma = []

    def emit(self, sems, dsems):
        sig = {e: 0 for e in self.CE}
        nd = 0
        for o in self.ops:
            E = self.E[o.eng]
            for d in o.waits:
                E.wait_ge(d.sem, d.val)
            if o.fn is None:
                continue
            if o.dma:
                s = dsems[nd % self.NDS]
                rnd = nd // self.NDS
                if rnd > 0:
                    E.wait_ge(s, 16 * rnd)
                ins = o.fn()
                ins.then_inc(s, 16)
                o.sem, o.val = s, 16 * (rnd + 1)
                nd += 1
            else:
                ins = o.fn()
                if o.signal:
                    c = sig[o.eng]
                    o.sem = sems[o.eng][c // self.EPOCH]
                    o.val = c % self.EPOCH + 1
                    ins.then_inc(o.sem, 1)
                    sig[o.eng] = c + 1
        return sig, nd


_TABLES = None


def _rope_tables():
    rows = L // 64
    row = np.repeat(np.arange(rows, dtype=np.float32), 64)
    col = np.tile(np.arange(64, dtype=np.float32), rows)
    half = HD // 2
    inv_freq = (np.float32(10000.0) ** (-np.arange(0, half, 2, dtype=np.float32) / np.float32(half))).astype(np.float32)
    ang_r = (row[:, None] * inv_freq[None, :]).astype(np.float32)
    ang_c = (col[:, None] * inv_freq[None, :]).astype(np.float32)
    cos_r, sin_r = np.cos(ang_r), np.sin(ang_r)
    cos_c, sin_c = np.cos(ang_c), np.sin(ang_c)
    cosT = np.concatenate([cos_r.T, cos_r.T, cos_c.T, cos_c.T], axis=0).astype(np.float32)
    sinT = np.concatenate([sin_r.T, sin_r.T, sin_c.T, sin_c.T], axis=0).astype(np.float32)
    return np.ascontiguousarray(cosT), np.ascontiguousarray(sinT)


def _filter_pos_tables():
    f32 = np.float32
    t = np.linspace(0.0, 1.0, L, dtype=f32)
    bands = 16
    w_ang = (f32(2.0 * math.pi) * np.arange(L, dtype=f32) / f32(L)).astype(f32)
    band_f = np.linspace(1e-4, bands - 1, bands, dtype=f32)
    ang = (w_ang[:, None] * band_f[None, :]).astype(f32)
    z = np.concatenate([t[:, None], np.cos(ang), -np.sin(ang)], axis=-1).astype(f32)
    zT = np.ascontiguousarray(z.T)
    idx = (L - np.arange(L)) % L
    zrT = np.ascontiguousarray(z[idx].T)
    trev = t[idx].copy()
    return zT, zrT, t.copy(), trev


def _fft_tables():
    N = NFFT
    n1 = np.arange(128, dtype=np.float64)
    k1 = np.arange(64, dtype=np.float64)
    th = 2 * np.pi * np.outer(n1, k1 + 0.5) / 128.0
    fpack = np.concatenate([np.cos(th), -np.sin(th), np.sin(th)], axis=1)
    n2 = np.arange(128, dtype=np.float64)
    k2 = np.arange(128, dtype=np.float64)
    kap = k1[None, :, None] + 128.0 * k2[None, None, :] + 0.5
    thg = 2 * np.pi * n2[:, None, None] * kap / N
    G = np.stack([np.cos(thg), -np.sin(thg)], axis=2)
    thf = 2 * np.pi * np.outer(k2, n2) / 128.0
    finv = np.stack([np.cos(thf), np.sin(thf), -np.sin(thf)], axis=1)
    n1h = np.arange(64, dtype=np.float64)
    phi = 2 * np.pi * (k1[:, None, None] + 0.5) * (128.0 * n1h[None, None, :] + n2[None, :, None]) / N
    M2 = np.concatenate([(2.0 / N) * np.cos(phi), -(2.0 / N) * np.sin(phi)], axis=0)
    return (fpack.astype(bf16_np), G.astype(bf16_np), finv.astype(bf16_np), M2.astype(bf16_np))


def _tables():
    global _TABLES
    if _TABLES is None:
        cosT, sinT = _rope_tables()
        zT, zrT, t, trev = _filter_pos_tables()
        fpack, G, finv, M2 = _fft_tables()
        ii = np.arange(512, dtype=np.float64)
        jj = np.arange(16, dtype=np.float64)
        tbm = np.stack([ii / (L - 1), (L - ii) / (L - 1)]).astype(np.float32)
        jtm = np.stack([512.0 * jj / (L - 1), -512.0 * jj / (L - 1)]).astype(np.float32)
        rot = np.zeros((64, 64), np.float32)
        for m in range(64):
            if (m % 32) < 16:
                rot[m + 16, m] = -1.0
            else:
                rot[m - 16, m] = 1.0
        sel = np.zeros((65, 64), np.float32)
        sel[64, :] = 1.0
        _TABLES = dict(cosT=cosT, sinT=sinT, zT=zT, zrT=zrT, t=t, trev=trev, fpack=fpack, G=G,
                       finv=finv, M2=M2, rot=rot, sel=sel, tb=tbm, jt=jtm,
                       ones=np.ones((128, 128), np.float32),
                       ident=np.eye(128, dtype=np.float32).astype(bf16_np))
    return _TABLES


def build_program(phases=(0, 1, 2, 3, 4), debug=()):
    nc = bass.Bass("TRN2", target_bir_lowering=False)
    S = Sched(nc)

    def din(name, shape, dt=F32):
        return nc.dram_tensor(name, list(shape), dt, kind="ExternalInput").ap()

    def dscr(name, shape, dt):
        return nc.dram_tensor(name, list(shape), dt, kind="Internal").ap()

    xT = din("xT", [D, L])
    xTq = din("xTq", [D, NTOK])
    w_in = din("w_in", [D, 2304])
    w_out = din("w_out", [D, D])
    w_mi = din("w_mi", [D, DFF])
    w_mo = din("w_mo", [DFF, D])
    g1 = din("g1", [128, 8]); g2 = din("g2", [128, 8]); gf = din("gf", [128, 8])
    gq = din("gq", [64, 1]); gk = din("gk", [64, 1])
    ga = din("ga", [64, 8]); gh = din("gh", [128, 4])
    cw = din("cw", [128, 12, 3]); cb = din("cb", [128, 12])
    fw1 = din("fw1", [33, 64]); fw2 = din("fw2", [64, 64]); fw3 = din("fw3", [64, 64])
    fw4 = din("fw4", [64, 2048])
    fb = din("fb", [64, 3]); ffreq = din("ffreq", [64, 1])
    fdel = din("fdel", [128, 16])
    skp = din("skp", [128, 8])
    qmask = din("qmask", [128, 4])
    c_cos = din("c_cos", [64, L]); c_sin = din("c_sin", [64, L])
    c_cosq = din("c_cosq", [64, NTOK]); c_sinq = din("c_sinq", [64, NTOK])
    c_zT = din("c_zT", [2, 33, L])
    c_t = din("c_t", [2, L])
    c_tb = din("c_tb", [2, 512])
    c_jt = din("c_jt", [2, 16])
    c_fpack = din("c_fpack", [128, 192], BF16)
    c_G = din("c_G", [128, 64, 2, 128], BF16)
    c_finv = din("c_finv", [128, 3, 128], BF16)
    c_M2 = din("c_M2", [128, 128, 64], BF16)
    c_M2q = din("c_M2q", [128, 128, 16], BF16)
    c_rot = din("c_rot", [64, 64]); c_sel = din("c_sel", [65, 64]); c_ones = din("c_ones", [128, 128])
    c_ident = din("c_ident", [128, 128], BF16)
    outT = nc.dram_tensor("outT", [D, NTOK], F32, kind="ExternalOutput").ap()

    U_d = dscr("U_d", [1536, L], BF16)
    H_d = dscr("H_d", [8, 128, 64, 4, 128], BF16)
    taps_d = dscr("taps_d", [128, NFFT], BF16)
    zcm_d = dscr("zcm_d", [128, L], BF16)
    att_d = dscr("att_d", [512, NTOK], BF16)
    hy_d = dscr("hy_d", [512, NTOK], BF16)
    Wmi_d = dscr("Wmi_d", [D, DFF], BF16)
    Wmo_d = dscr("Wmo_d", [DFF, D], BF16)

    from contextlib import ExitStack
    es = ExitStack()

    _names = {}

    def sb(name, shape, dt, stack=None):
        n_ = _names.get(name, 0)
        _names[name] = n_ + 1
        nm = name if n_ == 0 else f"{name}_r{n_}"
        return (stack or es).enter_context(nc.sbuf_tensor(nm, list(shape), dt))

    def dma(out, in_, reads, writes, eng="sp"):
        return S.op(eng, lambda: S.E[eng].dma_start(out=out, in_=in_), reads, writes, dma=True)

    def mm(out, lhsT, rhs, start, stop, reads, writes):
        return S.op("pe", lambda: nc.tensor.matmul(out, lhsT, rhs, start=start, stop=stop), reads, writes)

    rr = {"i": 0}

    def alt(engs=("dve", "pool")):
        rr["i"] += 1
        return engs[rr["i"] % len(engs)]

    def V(eng):
        return S.E[eng]

    with es:
        PS = [es.enter_context(nc.psum_tensor(f"ps{i}", [128, 512], F32)) for i in range(6)]
        PSB = [es.enter_context(nc.psum_tensor(f"psb{i}", [128, 1024], BF16)) for i in range(2)]

        ones = sb("ones", [128, 128], F32)
        ident = sb("ident", [128, 128], BF16)
        rot = sb("rot", [64, 64], F32)
        sel = sb("sel", [65, 64], F32)
        epsT = sb("epsT", [128, 1], F32)
        g1s = sb("g1s", [128, 8], F32); g2s = sb("g2s", [128, 8], F32); gfs = sb("gfs", [128, 8], F32)
        gqs = sb("gqs", [64, 1], F32); gks = sb("gks", [64, 1], F32)
        gas = sb("gas", [64, 8], F32); ghs = sb("ghs", [128, 4], F32)
        cws = sb("cws", [128, 12, 3], F32); cbs = sb("cbs", [128, 12], F32)
        skps = sb("skps", [128, 8], F32); qms = sb("qms", [128, 4], F32)
        for t_, d_ in ((ones, c_ones), (ident, c_ident), (rot, c_rot), (sel, c_sel), (g1s, g1), (g2s, g2),
                       (gfs, gf), (gqs, gq), (gks, gk), (gas, ga), (ghs, gh), (cws, cw), (cbs, cb),
                       (skps, skp), (qms, qmask)):
            dma(t_[:], d_, [], [t_.name if hasattr(t_, "name") else id(t_)])
        KEY = lambda t_: t_.name if hasattr(t_, "name") else id(t_)
        S.op("dve", lambda: nc.vector.memset(epsT[:], EPS), [], ["epsT"])

        FC = {}
        SNAP = {}

        def snap(name, src_ap, key, ncols):
            if name not in debug or name in SNAP:
                return
            SNAP[name] = nc.dram_tensor("dbg_" + name, [128, ncols], BF16, kind="ExternalOutput").ap()
            dma(SNAP[name], src_ap, [key], ["dbg_" + name])


        def load_fft_consts(stack):
            FC["fpack"] = sb("fpack", [128, 192], BF16, stack)
            FC["finv"] = sb("finv", [128, 3, 128], BF16, stack)
            FC["Gq"] = [sb(f"Gq{i}", [128, 8, 2, 128], BF16, stack) for i in range(2)]
            dma(FC["fpack"][:], c_fpack, [], ["fpack"])
            dma(FC["finv"][:], c_finv, [], ["finv"])

        def rsqrt(out, in_, scale, reads, writes, np_=128):
            S.op("act", lambda: nc.scalar.activation(out, in_, AF.Sqrt, bias=epsT[0:np_, :], scale=scale),
                 list(reads) + ["epsT"], writes)
            S.op("dve", lambda: nc.vector.reciprocal(out, out), writes, writes)

        def fft_forward(ztm, K, AT, cb_batch, zkey="ztm", akey="AT"):
            fpack = FC["fpack"]
            for c2 in range(64):
                ps = PS[c2 % 2]
                pk = f"ps{c2 % 2}"
                for cc in range(2):
                    c = c2 * 2 + cc
                    mm(ps[:, cc * 192:(cc + 1) * 192], ztm[0:K, c, :], fpack[0:K, :], True, True,
                       [zkey, "fpack"], [pk])
                e = alt(("act", "dve"))
                src = ps[:, 0:384].rearrange("p (cc j) -> p j cc", cc=2)
                dst = AT[:, :, c2 * 2:c2 * 2 + 2]
                if e == "act":
                    S.op("act", lambda dst=dst, src=src: nc.scalar.copy(dst, src), [pk], [akey])
                else:
                    S.op("dve", lambda dst=dst, src=src: nc.vector.tensor_copy(dst, src), [pk], [akey])
            for kb in range(32):
                q4 = kb // 4
                Gq = FC["Gq"][q4 % 2]
                gkey = f"Gq{q4 % 2}"
                if kb % 4 == 0:
                    dma(Gq[:], c_G[:, q4 * 8:(q4 + 1) * 8], [], [gkey])
                ps = PS[2 + kb % 2]
                pk = f"ps{2 + kb % 2}"
                for kk in range(2):
                    k1 = kb * 2 + kk
                    kl = k1 % 8
                    R = AT[:, k1, :]
                    I = AT[:, 64 + k1, :]
                    NI = AT[:, 128 + k1, :]
                    zr = ps[:, kk * 256:kk * 256 + 128]
                    zi = ps[:, kk * 256 + 128:kk * 256 + 256]
                    mm(zr, Gq[:, kl, 0, :], R, True, False, [akey, gkey], [pk])
                    mm(zr, Gq[:, kl, 1, :], NI, False, True, [akey, gkey], [pk])
                    mm(zi, Gq[:, kl, 1, :], R, True, False, [akey, gkey], [pk])
                    mm(zi, Gq[:, kl, 0, :], I, False, True, [akey, gkey], [pk])
                cb_batch(kb, ps, pk)

        def phase0():
          with ExitStack() as p0w:
            wst = [sb(f"wst{i}", [128, 8, 512], F32, p0w) for i in range(2)]
            wbf = [sb(f"wbf{i}", [128, 8, 512], BF16, p0w) for i in range(2)]
            n = 0
            jobs = []
            for cbk in range(8):
                jobs.append((w_mi.rearrange("(k p) n -> p k n", p=128)[:, :, cbk * 512:(cbk + 1) * 512],
                             Wmi_d.rearrange("(k p) n -> p k n", p=128)[:, :, cbk * 512:(cbk + 1) * 512]))
            for kr in range(4):
                for cbk in range(2):
                    jobs.append((w_mo.rearrange("(k p) n -> p k n", p=128)[:, kr * 8:(kr + 1) * 8, cbk * 512:(cbk + 1) * 512],
                                 Wmo_d.rearrange("(k p) n -> p k n", p=128)[:, kr * 8:(kr + 1) * 8, cbk * 512:(cbk + 1) * 512]))
            for src, dst in jobs:
                s_ = n % 2
                dma(wst[s_][:], src, [], [f"wst{s_}"])
                e = ("dve", "pool", "act")[n % 3]
                if e == "act":
                    S.op("act", lambda s_=s_: nc.scalar.copy(wbf[s_][:], wst[s_][:]), [f"wst{s_}"], [f"wbf{s_}"])
                else:
                    S.op(e, lambda s_=s_, e=e: V(e).tensor_copy(wbf[s_][:], wst[s_][:]), [f"wst{s_}"], [f"wbf{s_}"])
                dma(dst, wbf[s_][:], [f"wbf{s_}"], ["Wmid" if n < 8 else "Wmod"])
                n += 1
            S.barrier()
          with ExitStack() as p0:
            load_fft_consts(p0)

            w1s = sb("w1s", [33, 64], F32, p0); w2s = sb("w2s", [64, 64], F32, p0); w3s = sb("w3s", [64, 64], F32, p0)
            w4b = sb("w4b", [64, 2048], BF16, p0)
            fbs = sb("fbs", [64, 3], F32, p0); frs = sb("frs", [64, 1], F32, p0); ffb = sb("ffb", [64, 3], F32, p0)
            dls = sb("dls", [128, 16], F32, p0); ndl = sb("ndl", [128, 16], F32, p0)
            negpi = sb("negpi", [128, 1], F32, p0)
            H3 = sb("H3", [64, 2, L], BF16, p0)
            p0m = ExitStack()
            w4s = sb("w4s", [64, 2048], F32, p0m)
            for t_, d_, k_ in ((w1s, fw1, "w1s"), (w2s, fw2, "w2s"), (w3s, fw3, "w3s"), (w4s, fw4, "w4s"),
                               (fbs, fb, "fbs"), (frs, ffreq, "frs"), (dls, fdel, "dls")):
                dma(t_[:], d_, [], [k_])
            S.op("dve", lambda: nc.vector.tensor_copy(w4b[:], w4s[:]), ["w4s"], ["w4b"])
            S.op("dve", lambda: nc.vector.tensor_scalar(ffb[:], fbs[:], frs[:, 0:1], None, ALU.mult), ["fbs", "frs"], ["ffb"])
            S.op("dve", lambda: nc.vector.tensor_scalar(ndl[:], dls[:], -1.0, None, ALU.mult), ["dls"], ["ndl"])
            S.op("dve", lambda: nc.vector.tensor_tensor(ndl[:], ndl[:], dls[:], ALU.min), ["dls", "ndl"], ["ndl"])
            S.op("dve", lambda: nc.vector.memset(negpi[:], 0.0), [], ["negpi"])
            zc = [sb(f"zc{i}", [33, 512], F32, p0m) for i in range(4)]
            faL = [sb(f"fa{i}", [64, 512], F32, p0m) for i in range(2)]
            fbufL = [sb(f"fbuf{i}", [64, 512], F32, p0m) for i in range(2)]
            fhL = [sb(f"fh{i}", [64, 512], F32, p0m) for i in range(2)]

            def sin_layer(ps, li, dst, pk, v_, dkey):
                fa, fbuf = faL[v_], fbufL[v_]
                fak, fbk = f"fa{v_}", f"fbuf{v_}"
                S.op("dve", lambda: nc.vector.tensor_scalar(fa[:], ps[0:64, :], frs[:, 0:1], ffb[:, li:li + 1], ALU.mult, ALU.add),
                     [pk, "frs", "ffb"], [fak])
                S.op("dve", lambda: nc.vector.tensor_scalar(fbuf[:], fa[:], 1.0 / TWO_PI, MAGIC, ALU.mult, ALU.add), [fak], [fbk])
                S.op("dve", lambda: nc.vector.tensor_scalar(fbuf[:], fbuf[:], -MAGIC, -TWO_PI, ALU.add, ALU.mult), [fbk], [fbk])
                S.op("dve", lambda: nc.vector.tensor_tensor(fa[:], fa[:], fbuf[:], ALU.add), [fak, fbk], [fak])
                S.op("act", lambda: nc.scalar.activation(dst, fa[:], AF.Sin, bias=negpi[0:64, :], scale=1.0), [fak, "negpi"], [dkey])

            for j in range(16):
                for var in range(2):
                    s_ = (j % 2) * 2 + var
                    psm = PS[4 + var]
                    pmk = f"ps{4 + var}"
                    fh = fhL[var]
                    fhk = f"fh{var}"
                    dma(zc[s_][:], c_zT[var, :, j * 512:(j + 1) * 512], [], [f"zc{s_}"])
                    mm(psm[0:64, :], w1s[:], zc[s_][:], True, True, [f"zc{s_}", "w1s"], [pmk])
                    sin_layer(psm, 0, fh[:], pmk, var, fhk)
                    mm(psm[0:64, :], w2s[:], fh[:], True, True, [fhk, "w2s"], [pmk])
                    sin_layer(psm, 1, fh[:], pmk, var, fhk)
                    mm(psm[0:64, :], w3s[:], fh[:], True, True, [fhk, "w3s"], [pmk])
                    sin_layer(psm, 2, H3[:, var, j * 512:(j + 1) * 512], pmk, var, ("H3", var))

            S.barrier()
            p0m.close()
            tapsU = sb("tapsU", [128, NFFT], BF16, p0)
            ztmf = sb("ztmf", [128, 128, 128], BF16, p0)
            ATf = sb("ATf", [128, 192, 128], BF16, p0)
            tb = sb("tb", [128, 2, 512], F32, p0)
            jt = sb("jt", [128, 2, 16], F32, p0)
            for hf in range(2):
                dma(tb[:, hf, :], c_tb[hf:hf + 1, :].broadcast_to([128, 512]), [], ["tb"])
                dma(jt[:, hf, :], c_jt[hf:hf + 1, :].broadcast_to([128, 16]), [], ["jt"])
            wbase = [sb(f"wbase{i}", [128, 512], F32, p0) for i in range(2)]
            wcj = [sb(f"wcj{i}", [128, 16], F32, p0) for i in range(2)]
            tpf = [sb(f"tpf{i}", [128, 512], F32, p0) for i in range(2)]
            psum_ = sb("psum_", [128, 32], F32, p0)
            nrm = sb("nrm", [128, 2], F32, p0)
            Hst = [sb(f"Hst{i}", [128, 2, 4, 128], BF16, p0) for i in range(2)]
            for go in range(8):
                g, o = go // 2, go % 2
                for half in range(2):
                    col0 = o * 1024 + half * 512 + g * 128
                    ct = col0 // 128
                    S.op("act", lambda half=half, ct=ct: nc.scalar.activation(wbase[half][:], tb[:, half, :], AF.Exp, scale=ndl[:, ct:ct + 1]),
                         ["tb", "ndl"], [f"wbase{half}"])
                    S.op("act", lambda half=half, ct=ct: nc.scalar.activation(wcj[half][:], jt[:, half, :], AF.Exp, scale=ndl[:, ct:ct + 1]),
                         ["jt", "ndl"], [f"wcj{half}"])
                    for j in range(16):
                        s_ = j % 2
                        pst = PS[4 + s_]
                        ptk = f"ps{4 + s_}"
                        mm(pst[:, :], w4b[:, col0:col0 + 128], H3[:, half, j * 512:(j + 1) * 512], True, True,
                           [("H3", half), "w4b"], [ptk])
                        S.op("dve", lambda s_=s_, pst=pst, half=half, j=j: nc.vector.scalar_tensor_tensor(
                            out=tpf[s_][:], in0=pst[:, :], scalar=wcj[half][:, j:j + 1], in1=wbase[half][:], op0=ALU.mult, op1=ALU.mult),
                            [ptk, f"wcj{half}", f"wbase{half}"], [f"tpf{s_}"])
                        idx = half * 16 + j
                        if half == 1 and j == 0:
                            S.op("dve", lambda s_=s_: nc.vector.memset(tpf[s_][:, 0:1], 0.0), [f"tpf{s_}"], [f"tpf{s_}"])
                        S.op("dve", lambda idx=idx, s_=s_: nc.vector.tensor_reduce(psum_[:, idx:idx + 1], tpf[s_][:], mybir.AxisListType.X, ALU.add,
                                                                                  apply_absolute_value=True), [f"tpf{s_}"], ["psum_"])
                        S.op("pool", lambda half=half, j=j, s_=s_: nc.gpsimd.tensor_copy(tapsU[:, half * L + j * 512: half * L + (j + 1) * 512], tpf[s_][:]),
                             [f"tpf{s_}"], ["tapsU"])
                S.op("dve", lambda: nc.vector.tensor_reduce(nrm[:, 0:1], psum_[:], mybir.AxisListType.X, ALU.add), ["psum_"], ["nrm"])
                S.op("dve", lambda: nc.vector.reciprocal(nrm[:, 0:1], nrm[:, 0:1]), ["nrm"], ["nrm"])
                S.op("dve", lambda: nc.vector.tensor_scalar(nrm[:, 1:2], nrm[:, 0:1], -1.0, None, ALU.mult), ["nrm"], ["nrm"])
                S.op("dve", lambda: nc.vector.tensor_scalar(tapsU[:, 0:L], tapsU[:, 0:L], nrm[:, 0:1], None, ALU.mult), ["tapsU", "nrm"], ["tapsU"])
                S.op("pool", lambda: nc.gpsimd.tensor_scalar(tapsU[:, L:NFFT], tapsU[:, L:NFFT], nrm[:, 1:2], None, ALU.mult), ["tapsU", "nrm"], ["tapsU"])
                dma(taps_d, tapsU[:], ["tapsU"], ["taps_d"])
                tv = taps_d.rearrange("c (n1 n2) -> n1 c n2", n2=128)
                for q4 in range(4):
                    dma(ztmf[:, q4 * 32:(q4 + 1) * 32, :], tv[:, q4 * 32:(q4 + 1) * 32, :], ["taps_d"], ["ztm"])

                def cb_filter(kb, ps, pk, go=go):
                    s_ = kb % 2
                    zv = ps[:, :].rearrange("p (k r c) -> p k r c", k=2, r=2)
                    S.op("act", lambda: nc.scalar.copy(Hst[s_][:, :, 0:2, :], zv), [pk], [f"Hst{s_}"])
                    S.op("dve", lambda: nc.vector.tensor_copy(Hst[s_][:, :, 2, :], zv[:, :, 1, :]), [pk], [f"Hst{s_}"])
                    S.op("dve", lambda: nc.vector.tensor_copy(Hst[s_][:, :, 3, :], zv[:, :, 0, :]), [pk], [f"Hst{s_}"])
                    dma(H_d[go, :, kb * 2:kb * 2 + 2], Hst[s_][:], [f"Hst{s_}"], [("H_d", go)])

                fft_forward(ztmf, 128, ATf, cb_filter)
            S.barrier()
        def rms_apply(src, gvec, dst, scale, nk, tag, sqb, rs, np_=128, ss_ps=0):
            S.op("act", lambda: nc.scalar.activation(sqb[0:np_, 0:nk, :], src, AF.Square), [tag + "_src"], ["sqb"])
            for k in range(nk):
                mm(PS[ss_ps][0:np_, :], ones[0:np_, 0:np_], sqb[0:np_, k, :], k == 0, k == nk - 1, ["sqb", "ones"], [f"ps{ss_ps}"])
            S.op("act", lambda: nc.scalar.activation(rs[0:np_, :], PS[ss_ps][0:np_, :], AF.Sqrt, bias=epsT[0:np_, :], scale=scale),
                 [f"ps{ss_ps}", "epsT"], ["rs"])
            S.op("dve", lambda: nc.vector.reciprocal(rs[0:np_, :], rs[0:np_, :]), ["rs"], ["rs"])
            for k in range(nk):
                e = "dve"
                S.op(e, lambda k=k, e=e: V(e).scalar_tensor_tensor(out=dst[:, k, :], in0=src[:, k, :], scalar=gvec[:, k:k + 1],
                                                                  in1=rs[0:np_, :], op0=ALU.mult, op1=ALU.mult),
                     [tag + "_src", "rs", "gvecs"], [tag + "_dst"])

        def phase12():
          with ExitStack() as p12:
            KT = sb("KT", [64, 2, L], BF16, p12)
            Vx = sb("Vx", [128, 64, 2, 65], BF16, p12)
            QT = sb("QT", [64, 8, NTOK], BF16, p12)
            S.op("pool", lambda: nc.gpsimd.memset(Vx[:, :, :, 64:65], 1.0), [], ["Vx1"])
            with ExitStack() as p1:
                Wb = sb("Wb", [128, 8, 2304], BF16, p1)
                with ExitStack() as p1c:
                    wst = [sb(f"wsi{i}", [128, 8, 384], F32, p1c) for i in range(2)]
                    wv = w_in.rearrange("(k p) n -> p k n", p=128)
                    for bk in range(6):
                        s_ = bk % 2
                        dma(wst[s_][:], wv[:, :, bk * 384:(bk + 1) * 384], [], [f"wst{s_}"])
                        e = alt()
                        S.op(e, lambda s_=s_, e=e, bk=bk: V(e).tensor_copy(Wb[:, :, bk * 384:(bk + 1) * 384], wst[s_][:]),
                             [f"wst{s_}"], ["Wb"])
                    S.barrier()
                xin = [sb(f"xin{i}", [128, 8, 512], F32, p1) for i in range(2)]
                sqb = sb("sqb", [128, 8, 512], F32, p1)
                rs = sb("rs", [128, 512], F32, p1)
                aT = sb("aT", [128, 8, 512], BF16, p1)
                ust = sb("ust", [128, 12, 512], BF16, p1)
                hsq = sb("hsq", [64, 512], F32, p1); hrs = sb("hrs", [64, 512], F32, p1); hkn = sb("hkn", [64, 512], F32, p1)
                ht1 = sb("ht1", [64, 512], F32, p1); ht2 = sb("ht2", [64, 512], F32, p1)
                csb = [sb(f"csb{i}", [64, 2, 512], F32, p1) for i in range(2)]

                def headproc(gvec, cs, cskey, dst, dstkey):
                    src = PS[1][0:64, :]
                    S.op("act", lambda: nc.scalar.activation(hsq[:], src, AF.Square), ["ps1"], ["hsq"])
                    mm(PS[2][0:64, :], ones[0:64, 0:64], hsq[:], True, True, ["hsq", "ones"], ["ps2"])
                    S.op("act", lambda: nc.scalar.activation(hrs[:], PS[2][0:64, :], AF.Sqrt, bias=epsT[0:64, :], scale=1.0 / 64.0),
                         ["ps2", "epsT"], ["hrs"])
                    S.op("dve", lambda: nc.vector.reciprocal(hrs[:], hrs[:]), ["hrs"], ["hrs"])
                    S.op("dve", lambda: nc.vector.scalar_tensor_tensor(out=hkn[:], in0=src, scalar=gvec[:, 0:1], in1=hrs[:],
                                                                      op0=ALU.mult, op1=ALU.mult), ["ps1", "hrs", "gvecs"], ["hkn"])
                    mm(PS[3][0:64, :], rot[:], hkn[:], True, True, ["hkn", "rot"], ["ps3"])
                    S.op("pool", lambda: nc.gpsimd.tensor_tensor(ht1[:], hkn[:], cs[:, 0, :], ALU.mult), ["hkn", cskey], ["ht1"])
                    S.op("dve", lambda: nc.vector.tensor_tensor(ht2[:], PS[3][0:64, :], cs[:, 1, :], ALU.mult), ["ps3", cskey], ["ht2"])
                    S.op("pool", lambda: nc.gpsimd.tensor_tensor(dst, ht1[:], ht2[:], ALU.add), ["ht1", "ht2"], [dstkey])

                xv = xT.rearrange("(k p) t -> p k t", p=128)
                Uv = U_d.rearrange("(ct p) t -> p ct t", p=128)
                for j in range(16):
                    s_ = j % 2
                    tsl = slice(j * 512, (j + 1) * 512)
                    dma(xin[s_][:], xv[:, :, tsl], [], [f"xin{s_}", "n1_src"])
                    dma(csb[s_][:, 0, :], c_cos[:, tsl], [], [f"csb{s_}"])
                    dma(csb[s_][:, 1, :], c_sin[:, tsl], [], [f"csb{s_}"])
                    rms_apply(xin[s_][:], g1s, aT, 1.0 / D, 8, "n1", sqb, rs)
                    for kvh in range(2):
                        for k in range(8):
                            mm(PS[1][0:64, :], Wb[:, k, 512 + kvh * 64:512 + (kvh + 1) * 64], aT[:, k, :], k == 0, k == 7,
                               ["Wb", "n1_dst"], ["ps1"])
                        headproc(gks, csb[s_], f"csb{s_}", KT[:, kvh, tsl], "KT")
                    for tt in range(4):
                        for k in range(8):
                            mm(PS[4][:, tt * 128:(tt + 1) * 128], aT[:, k, tt * 128:(tt + 1) * 128], Wb[:, k, 640:768], k == 0, k == 7,
                               ["Wb", "n1_dst"], ["ps4"])
                    S.op("act", lambda j=j: nc.scalar.copy(Vx[:, j * 4:(j + 1) * 4, :, 0:64],
                                                         PS[4][:, :].rearrange("p (t h d) -> p t h d", t=4, h=2)), ["ps4"], ["Vx"])
                    for ct in range(12):
                        for k in range(8):
                            mm(PS[5][:, :], Wb[:, k, 768 + ct * 128:768 + (ct + 1) * 128], aT[:, k, :], k == 0, k == 7,
                               ["Wb", "n1_dst"], ["ps5"])
                        if ct % 2 == 0:
                            S.op("act", lambda ct=ct: nc.scalar.copy(ust[:, ct, :], PS[5][:, :]), ["ps5"], ["ust"])
                        else:
                            S.op("dve", lambda ct=ct: nc.vector.tensor_copy(ust[:, ct, :], PS[5][:, :]), ["ps5"], ["ust"])
                    dma(Uv[:, :, tsl], ust[:], ["ust"], ["U_d"])
                xqv = xTq.rearrange("(k p) t -> p k t", p=128)
                for j in range(4):
                    s_ = j % 2
                    tsl = slice(j * 512, (j + 1) * 512)
                    dma(xin[s_][:], xqv[:, :, tsl], [], [f"xin{s_}", "n1_src"])
                    dma(csb[s_][:, 0, :], c_cosq[:, tsl], [], [f"csb{s_}"])
                    dma(csb[s_][:, 1, :], c_sinq[:, tsl], [], [f"csb{s_}"])
                    rms_apply(xin[s_][:], g1s, aT, 1.0 / D, 8, "n1", sqb, rs)
                    for h in range(8):
                        for k in range(8):
                            mm(PS[1][0:64, :], Wb[:, k, h * 64:(h + 1) * 64], aT[:, k, :], k == 0, k == 7, ["Wb", "n1_dst"], ["ps1"])
                        headproc(gqs, csb[s_], f"csb{s_}", QT[:, h, tsl], "QT")
                S.barrier()
            if 2 in phases:
              with ExitStack() as p2:
                pb = [sb(f"pb{i}", [128, 512], BF16, p2) for i in range(3)]
                osb = sb("osb", [65, 512], F32, p2)
                rden = sb("rden", [64, 512], F32, p2)
                ast = [sb(f"ast{i}", [64, 512], BF16, p2) for i in range(2)]
                n_ = 0
                for qc in range(4):
                    qsl = slice(qc * 512, (qc + 1) * 512)
                    for h in range(8):
                        kvh = h // 4
                        ob = 3 + n_ % 2
                        obk = f"ps{ob}"
                        LA = 2
                        for step in range(64 + LA):
                            if step < 64:
                                kt = step
                                b3 = kt % 3
                                mm(PS[b3][:, :], KT[:, kvh, kt * 128:(kt + 1) * 128], QT[:, h, qsl], True, True, ["KT", "QT"], [f"ps{b3}"])
                                S.op("act", lambda b3=b3: nc.scalar.activation(pb[b3][:], PS[b3][:, :], AF.Exp, scale=0.125),
                                     [f"ps{b3}"], [f"pb{b3}"])
                            kt = step - LA
                            if kt >= 0:
                                b3 = kt % 3
                                mm(PS[ob][0:65, :], Vx[:, kt, kvh, :], pb[b3][:], kt == 0, kt == 63, ["Vx", "Vx1", f"pb{b3}"], [obk])
                        S.op("dve", lambda ob=ob: nc.vector.tensor_copy(osb[:], PS[ob][0:65, :]), [obk], ["osb"])
                        mm(PS[5][0:64, :], sel[:], osb[:], True, True, ["osb", "sel"], ["ps5"])
                        S.op("dve", lambda: nc.vector.reciprocal(rden[:], PS[5][0:64, :]), ["ps5"], ["rden"])
                        a_ = n_ % 2
                        S.op("pool", lambda a_=a_: nc.gpsimd.tensor_tensor(ast[a_][:], osb[0:64, :], rden[:], ALU.mult), ["osb", "rden"], [f"ast{a_}"])
                        dma(att_d[h * 64:(h + 1) * 64, qsl], ast[a_][:], [f"ast{a_}"], ["att_d"])
                        n_ += 1
                S.barrier()

        def phase3():
          with ExitStack() as p3:
            load_fft_consts(p3)
            finv = FC["finv"]
            M2s = sb("M2s", [128, 128, 64], BF16, p3)
            M2qs = sb("M2qs", [128, 128, 16], BF16, p3)
            dma(M2s[:], c_M2, [], ["M2s"])
            dma(M2qs[:], c_M2q, [], ["M2qs"])
            RA = sb("RA", [128, 192 * 128 + 8], BF16, p3)
            RB = sb("RB", [128, 128 * 128], BF16, p3)
            RC = sb("RC", [128, 128 * 128], BF16, p3)
            vt = sb("vt", [128, L], BF16, p3)
            x1t = sb("x1t", [128, L], BF16, p3)
            x2q = sb("x2q", [128, NTOK], BF16, p3)
            zzq = sb("zzq", [128, NTOK], BF16, p3)
            hyst = sb("hyst", [128, NTOK], BF16, p3)
            sct = [sb(f"sct{i}", [128, 512], F32, p3) for i in range(2)]
            Hc = [sb(f"Hc{i}", [128, 2, 4, 128], BF16, p3) for i in range(2)]
            Pb = sb("Pb", [128, 2, 2, 128], F32, p3)
            Qb = sb("Qb", [128, 2, 2, 128], F32, p3)
            Yb = [sb(f"Yb{i}", [128, 2, 2, 128], BF16, p3) for i in range(2)]
            gt = [sb(f"gt{i}", [128, 512], F32, p3) for i in range(2)]
            uraw = RA[:, 0:3 * 8194].rearrange("p (i t) -> p i t", i=3)
            AT = RA[:, 0:192 * 128].rearrange("p (j c) -> p j c", c=128)
            ztm = RB[:, :].rearrange("p (c n) -> p c n", n=128)
            Eb = RB[:, :].rearrange("p (n c) -> p n c", c=128)
            x2t = RC[:, 0:L]
            ET = RC[:, :].rearrange("p (c k) -> p c k", k=128)
            RAK, RBK, RCK = "RA", "RB", "RC"

            def masked_quarter(dst, src, dkey, skey):
                S.op("dve", lambda: nc.vector.tensor_scalar(dst[:], src[:, 0:NTOK], qms[:, 0:1], None, ALU.mult), [skey, "qms"], [dkey])
                for q in range(1, 4):
                    S.op("dve", lambda q=q: nc.vector.scalar_tensor_tensor(out=dst[:], in0=src[:, q * NTOK:(q + 1) * NTOK], scalar=qms[:, q:q + 1],
                                                                          in1=dst[:], op0=ALU.mult, op1=ALU.add), [skey, "qms", dkey], [dkey])

            def bounce_to_ztm(src, skey):
                dma(zcm_d, src[:], [skey], ["zcm_d"])
                zv = zcm_d.rearrange("c (n1 n2) -> n1 c n2", n2=128)
                for q4 in range(4):
                    dma(ztm[0:64, q4 * 32:(q4 + 1) * 32, :], zv[:, q4 * 32:(q4 + 1) * 32, :], ["zcm_d"], [RBK])

            def conv_core(go):
                def cb(kb, ps, pk):
                    s_ = kb % 2
                    dma(Hc[s_][:], H_d[go, :, kb * 2:kb * 2 + 2], [("H_d", go)], [f"Hc{s_}"])
                    zv = ps[:, :].rearrange("p (k r c) -> p k r c", k=2, r=2)
                    S.op("dve", lambda: nc.vector.tensor_tensor(Pb[:], zv, Hc[s_][:, :, 0:2, :], ALU.mult), [pk, f"Hc{s_}"], ["Pb"])
                    S.op("dve", lambda: nc.vector.tensor_tensor(Qb[:], zv, Hc[s_][:, :, 2:4, :], ALU.mult), [pk, f"Hc{s_}"], ["Qb"])
                    S.op("pool", lambda: nc.gpsimd.tensor_tensor(Yb[s_][:, :, 0, :], Pb[:, :, 0, :], Pb[:, :, 1, :], ALU.subtract), ["Pb"], [f"Yb{s_}"])
                    S.op("pool", lambda: nc.gpsimd.tensor_tensor(Yb[s_][:, :, 1, :], Qb[:, :, 0, :], Qb[:, :, 1, :], ALU.add), ["Qb"], [f"Yb{s_}"])
                    pe_ = PS[4 + kb % 2]
                    pek = f"ps{4 + kb % 2}"
                    er = pe_[:, 0:256].rearrange("p (k c) -> p k c", k=2)
                    ei = pe_[:, 256:512].rearrange("p (k c) -> p k c", k=2)
                    yr = Yb[s_][:, :, 0, :]
                    yi = Yb[s_][:, :, 1, :]
                    mm(er, finv[:, 0, :], yr, True, False, [f"Yb{s_}", "finv"], [pek])
                    mm(er, finv[:, 2, :], yi, False, True, [f"Yb{s_}", "finv"], [pek])
                    mm(ei, finv[:, 1, :], yr, True, False, [f"Yb{s_}", "finv"], [pek])
                    mm(ei, finv[:, 0, :], yi, False, True, [f"Yb{s_}", "finv"], [pek])
                    S.op("act", lambda: nc.scalar.copy(ET[:, :, kb * 2:kb * 2 + 2], pe_[:, 0:256].rearrange("p (k c) -> p c k", k=2)), [pek], [RCK])
                    S.op("dve", lambda: nc.vector.tensor_copy(ET[:, :, 64 + kb * 2:64 + kb * 2 + 2], pe_[:, 256:512].rearrange("p (k c) -> p c k", k=2)),
                         [pek], [RCK])
                fft_forward(ztm, 64, AT, cb, zkey=RBK, akey=RAK)
                for c4 in range(32):
                    pb_ = PSB[c4 % 2]
                    pbk = f"psb{c4 % 2}"
                    for cc in range(4):
                        c = c4 * 4 + cc
                        S.op("pe", lambda c=c, cc=cc, pb_=pb_: nc.tensor.transpose(pb_[:, cc * 128:(cc + 1) * 128], ET[:, c, :], ident[:]),
                             [RCK, "ident"], [pbk])
                    src = pb_[:, 0:512].rearrange("p (cc n) -> p n cc", cc=4)
                    dst = Eb[:, :, c4 * 4:c4 * 4 + 4]
                    if c4 % 2 == 0:
                        S.op("act", lambda src=src, dst=dst: nc.scalar.copy(dst, src), [pbk], [RBK])
                    else:
                        S.op("dve", lambda src=src, dst=dst: nc.vector.tensor_copy(dst, src), [pbk], [RBK])

            Udv = U_d
            for g in range(4):
                for i in range(3):
                    dma(uraw[:, i, 1:L + 1], Udv[i * 512 + g * 128:i * 512 + (g + 1) * 128, :], ["U_d"], [RAK])
                S.op("pool", lambda: nc.gpsimd.memset(uraw[:, :, 0:1], 0.0), [RAK], [RAK])
                S.op("pool", lambda: nc.gpsimd.memset(uraw[:, :, L + 1:L + 2], 0.0), [RAK], [RAK])
                for i, (dst, dkey) in enumerate(((vt, "vt"), (x1t, "x1t"), (x2t, RCK))):
                    ct = i * 4 + g
                    for q in range(16):
                        e = "dve"
                        st = sct[q % 2]
                        sk = f"sct{q % 2}"
                        o0 = q * 512
                        S.op(e, lambda e=e, st=st, i=i, o0=o0, ct=ct: V(e).tensor_scalar(st[:], uraw[:, i, o0:o0 + 512], cws[:, ct, 0:1], cbs[:, ct:ct + 1],
                                                                                      ALU.mult, ALU.add), [RAK, "cws", "cbs"], [sk])
                        S.op(e, lambda e=e, st=st, i=i, o0=o0, ct=ct: V(e).scalar_tensor_tensor(out=st[:], in0=uraw[:, i, o0 + 1:o0 + 513], scalar=cws[:, ct, 1:2],
                                                                                             in1=st[:], op0=ALU.mult, op1=ALU.add), [RAK, "cws", sk], [sk])
                        S.op(e, lambda e=e, st=st, i=i, o0=o0, ct=ct, dst=dst: V(e).scalar_tensor_tensor(out=dst[:, o0:o0 + 512], in0=uraw[:, i, o0 + 2:o0 + 514],
                                                                                                      scalar=cws[:, ct, 2:3], in1=st[:], op0=ALU.mult, op1=ALU.add),
                             [RAK, "cws", sk], [dkey])
                masked_quarter(x2q, x2t, "x2q", RCK)
                snap("vt0", vt[:], "vt", L)
                snap("x1t0", x1t[:], "x1t", L)
                bounce_to_ztm(vt, "vt")
                conv_core(g * 2 + 0)
                vt3 = vt[:, :].rearrange("p (n1 n2) -> p n1 n2", n2=128)
                x13 = x1t[:, :].rearrange("p (n1 n2) -> p n1 n2", n2=128)
                for nb in range(16):
                    ps = PS[nb % 2]
                    pk = f"ps{nb % 2}"
                    for j in range(8):
                        n2 = nb * 8 + j
                        mm(ps[:, j * 64:(j + 1) * 64], Eb[:, n2, :], M2s[:, n2, :], True, True, [RBK, "M2s"], [pk])
                    psv = ps[:, :].rearrange("p (j n) -> p n j", j=8)
                    gtv = gt[nb % 2][:, :].rearrange("p (n j) -> p n j", j=8)
                    gk_ = f"gt{nb % 2}"
                    S.op("dve", lambda psv=psv, gtv=gtv, nb=nb, g=g: nc.vector.scalar_tensor_tensor(out=gtv, in0=vt3[:, :, nb * 8:(nb + 1) * 8],
                                                                                             scalar=skps[:, g:g + 1], in1=psv, op0=ALU.mult, op1=ALU.add),
                         [pk, "vt", "skps"], [gk_])
                    S.op("pool", lambda gtv=gtv, nb=nb: nc.gpsimd.tensor_tensor(vt3[:, :, nb * 8:(nb + 1) * 8], gtv, x13[:, :, nb * 8:(nb + 1) * 8], ALU.mult),
                         [gk_, "x1t"], ["vt"])
                snap("zz0", vt[:], "vt", L)
                masked_quarter(zzq, vt, "zzq", "vt")
                bounce_to_ztm(vt, "vt")
                conv_core(g * 2 + 1)
                zq3 = zzq[:, :].rearrange("p (n1 n2) -> p n1 n2", n2=128)
                xq3 = x2q[:, :].rearrange("p (n1 n2) -> p n1 n2", n2=128)
                hy3 = hyst[:, :].rearrange("p (n1 n2) -> p n1 n2", n2=128)
                for nb in range(4):
                    ps = PS[nb % 2]
                    pk = f"ps{nb % 2}"
                    for j in range(32):
                        n2 = nb * 32 + j
                        mm(ps[:, j * 16:(j + 1) * 16], Eb[:, n2, :], M2qs[:, n2, :], True, True, [RBK, "M2qs"], [pk])
                    psv = ps[:, :].rearrange("p (j n) -> p n j", j=32)
                    gtv = gt[nb % 2][:, :].rearrange("p (n j) -> p n j", j=32)
                    gk_ = f"gt{nb % 2}"
                    S.op("dve", lambda psv=psv, gtv=gtv, nb=nb, g=g: nc.vector.scalar_tensor_tensor(out=gtv, in0=zq3[:, :, nb * 32:(nb + 1) * 32],
                                                                                             scalar=skps[:, 4 + g:5 + g], in1=psv, op0=ALU.mult, op1=ALU.add),
                         [pk, "zzq", "skps"], [gk_])
                    S.op("pool", lambda gtv=gtv, nb=nb: nc.gpsimd.tensor_tensor(hy3[:, :, nb * 32:(nb + 1) * 32], gtv, xq3[:, :, nb * 32:(nb + 1) * 32], ALU.mult),
                         [gk_, "x2q"], ["hyst"])
                dma(hy_d[g * 128:(g + 1) * 128, :], hyst[:], ["hyst"], ["hy_d"])
            S.barrier()

        def phase4():
          with ExitStack() as p4:
            WoA = sb("WoA", [64, 8, D], BF16, p4)
            WoH = sb("WoH", [128, 4, D], BF16, p4)
            with ExitStack() as p4c:
                wsa = sb("wsa", [64, 8, D], F32, p4c)
                wsh = sb("wsh", [128, 4, D], F32, p4c)
                dma(wsa[:], w_out[0:512, :].rearrange("(h d) n -> d h n", d=64), [], ["wsa"])
                dma(wsh[:], w_out[512:1024, :].rearrange("(g p) n -> p g n", p=128), [], ["wsh"])
                S.op("dve", lambda: nc.vector.tensor_copy(WoA[:], wsa[:]), ["wsa"], ["WoA"])
                S.op("pool", lambda: nc.gpsimd.tensor_copy(WoH[:], wsh[:]), ["wsh"], ["WoH"])
                S.barrier()
            attc = sb("attc", [64, 8, 512], BF16, p4)
            hyc = sb("hyc", [128, 4, 512], BF16, p4)
            sqb = sb("sqb4", [128, 8, 512], F32, p4)
            rs = sb("rs4", [128, 512], F32, p4)
            mixA = sb("mixA", [64, 8, 512], BF16, p4)
            mixH = sb("mixH", [128, 4, 512], BF16, p4)
            h1 = sb("h1", [128, 8, 512], F32, p4)
            mT = sb("mT", [128, 8, 512], BF16, p4)
            Wmi_s = [sb(f"Wmi_s{i}", [128, 8, 1024], BF16, p4) for i in range(2)]
            Wmo_s = [sb(f"Wmo_s{i}", [128, 32, 128], BF16, p4) for i in range(2)]
            fT = sb("fT", [128, 32, 512], BF16, p4)
            rb = [sb(f"rb{i}", [128, 512], F32, p4) for i in range(2)]
            xqv = xTq.rearrange("(k p) t -> p k t", p=128)
            Wmiv = Wmi_d.rearrange("(k p) n -> p k n", p=128)
            Wmov = Wmo_d.rearrange("(f p) n -> p f n", p=128)
            outv = outT.rearrange("(k p) t -> p k t", p=128)
            nw = 0
            nwo = 0
            for j in range(4):
                tsl = slice(j * 512, (j + 1) * 512)
                dma(attc[:], att_d[:, tsl].rearrange("(h d) t -> d h t", d=64), ["att_d"], ["attc", "na_src"])
                dma(hyc[:], hy_d[:, tsl].rearrange("(g p) t -> p g t", p=128), ["hy_d"], ["hyc", "nh_src"])
                dma(h1[:], xqv[:, :, tsl], [], ["h1", "n2_src", "nf_src"])
                rms_apply(attc[:], gas, mixA, 1.0 / 512.0, 8, "na", sqb, rs, np_=64)
                rms_apply(hyc[:], ghs, mixH, 1.0 / 512.0, 4, "nh", sqb, rs)
                for ct in range(8):
                    csl = slice(ct * 128, (ct + 1) * 128)
                    for h in range(8):
                        mm(PS[1][:, :], WoA[:, h, csl], mixA[:, h, :], h == 0, False, ["WoA", "na_dst"], ["ps1"])
                    for g in range(4):
                        mm(PS[1][:, :], WoH[:, g, csl], mixH[:, g, :], False, g == 3, ["WoH", "nh_dst"], ["ps1"])
                    S.op("dve", lambda ct=ct: nc.vector.tensor_tensor(h1[:, ct, :], h1[:, ct, :], PS[1][:, :], ALU.add), ["ps1", "h1"], ["h1", "n2_src", "nf_src"])
                rms_apply(h1[:], g2s, mT, 1.0 / D, 8, "n2", sqb, rs)
                for pc in range(4):
                    s_ = nw % 2
                    dma(Wmi_s[s_][:], Wmiv[:, :, pc * 1024:(pc + 1) * 1024], ["Wmid"], [f"Wmi_s{s_}"])
                    nw += 1
                    for ft in range(8):
                        f = pc * 8 + ft
                        pf = PS[2 + f % 2]
                        pfk = f"ps{2 + f % 2}"
                        for k in range(8):
                            mm(pf[:, :], Wmi_s[s_][:, k, ft * 128:(ft + 1) * 128], mT[:, k, :], k == 0, k == 7, [f"Wmi_s{s_}", "n2_dst"], [pfk])
                        r_ = f % 2
                        S.op("act", lambda pf=pf, r_=r_: nc.scalar.activation(rb[r_][:], pf[:, :], AF.Relu), [pfk], [f"rb{r_}"])
                        e = ("pool", "dve")[f % 2]
                        S.op(e, lambda e=e, r_=r_, f=f: V(e).tensor_tensor(fT[:, f, :], rb[r_][:], rb[r_][:], ALU.mult), [f"rb{r_}"], ["fT"])
                for ct in range(8):
                    s_ = nwo % 2
                    dma(Wmo_s[s_][:], Wmov[:, :, ct * 128:(ct + 1) * 128], ["Wmod"], [f"Wmo_s{s_}"])
                    nwo += 1
                    po = PS[4 + ct % 2]
                    pok = f"ps{4 + ct % 2}"
                    for f in range(32):
                        mm(po[:, :], Wmo_s[s_][:, f, :], fT[:, f, :], f == 0, f == 31, [f"Wmo_s{s_}", "fT"], [pok])
                    S.op("dve", lambda ct=ct, po=po: nc.vector.tensor_tensor(h1[:, ct, :], h1[:, ct, :], po[:, :], ALU.add), [pok, "h1"], ["h1", "n2_src", "nf_src"])
                rms_apply(h1[:], gfs, h1, 1.0 / D, 8, "nf", sqb, rs)
                dma(outv[:, :, tsl], h1[:], ["h1", "nf_dst"], ["outT"])
            S.barrier()

        if 0 in phases:
            phase0()
        if 1 in phases:
            phase12()
        if 3 in phases:
            phase3()
        if 4 in phases:
            phase4()
        if debug:
            with ExitStack() as pd:
                for name in debug:
                    if name not in ("att_d", "hy_d", "U_d", "H_d0"):
                        continue
                    src_, shp = {"att_d": (att_d, [512, NTOK]), "hy_d": (hy_d, [512, NTOK]), "U_d": (U_d[0:512, :], [512, L]),
                                 "H_d0": (H_d[0:4, :, 0:4].rearrange("a p k r c -> (a p) (k r c)"), [512, 2048])}[name]
                    dbo = nc.dram_tensor("dbg_" + name, shp, BF16, kind="ExternalOutput").ap()
                    dbs = sb("dbs_" + name, [128, 4, shp[1]], BF16, pd)
                    dma(dbs[:], src_.rearrange("(a p) t -> p a t", p=128), [name], ["dbs_" + name])
                    dma(dbo.rearrange("(a p) t -> p a t", p=128), dbs[:], ["dbs_" + name], ["dbg_" + name])
                S.barrier()
        S.barrier()
    return nc, S


def _core_inputs(inp, core):
    T = _tables()
    b, r = core // 4, core % 4
    f = lambda a: np.ascontiguousarray(np.asarray(a, dtype=np.float32))
    x = np.asarray(inp["x"], dtype=np.float32)
    xT = np.ascontiguousarray(x[b].T)
    tq = slice(r * NTOK, (r + 1) * NTOK)
    qm = np.zeros((128, 4), np.float32)
    qm[:, r] = 1.0
    m = {
        "xT": xT, "xTq": np.ascontiguousarray(xT[:, tq]),
        "w_in": f(inp["w_in"][0]), "w_out": f(inp["w_out"][0]), "w_mi": f(inp["w_mlp_in"][0]), "w_mo": f(inp["w_mlp_out"][0]),
        "g1": f(np.asarray(inp["norm1_g"][0]).reshape(8, 128).T), "g2": f(np.asarray(inp["norm2_g"][0]).reshape(8, 128).T),
        "gf": f(np.asarray(inp["final_g"]).reshape(8, 128).T),
        "gq": f(np.asarray(inp["q_norm_g"][0]).reshape(64, 1)), "gk": f(np.asarray(inp["k_norm_g"][0]).reshape(64, 1)),
        "ga": f(np.asarray(inp["attn_out_g"][0]).reshape(8, 64).T), "gh": f(np.asarray(inp["hy_out_g"][0]).reshape(4, 128).T),
        "cw": f(np.asarray(inp["hy_conv_w"][0]).T.reshape(12, 128, 3).transpose(1, 0, 2)),
        "cb": f(np.asarray(inp["hy_conv_b"][0]).reshape(12, 128).T),
        "fw1": f(inp["filt_w1"][0]), "fw2": f(inp["filt_w2"][0]), "fw3": f(inp["filt_w3"][0]), "fw4": f(inp["filt_w4"][0]),
        "fb": f(np.stack([np.asarray(inp["filt_b1"][0]), np.asarray(inp["filt_b2"][0]), np.asarray(inp["filt_b3"][0])], axis=1)),
        "ffreq": f(np.asarray(inp["filt_freq"][0]).reshape(64, 1)),
        "fdel": f(np.asarray(inp["filt_deltas"][0]).reshape(16, 128).T),
        "skp": f(np.asarray(inp["hy_skip_d"][0]).reshape(2, 4, 128).transpose(2, 0, 1).reshape(128, 8)),
        "qmask": qm,
        "c_cos": T["cosT"], "c_sin": T["sinT"],
        "c_cosq": np.ascontiguousarray(T["cosT"][:, tq]), "c_sinq": np.ascontiguousarray(T["sinT"][:, tq]),
        "c_zT": np.ascontiguousarray(np.stack([T["zT"], T["zrT"]])), "c_t": np.ascontiguousarray(np.stack([T["t"], T["trev"]])),
        "c_tb": T["tb"], "c_jt": T["jt"],
        "c_fpack": T["fpack"], "c_G": T["G"], "c_finv": T["finv"], "c_M2": T["M2"],
        "c_M2q": np.ascontiguousarray(T["M2"][:, :, r * 16:(r + 1) * 16]),
        "c_rot": T["rot"], "c_sel": T["sel"], "c_ones": T["ones"], "c_ident": T["ident"],
    }
    return m


def _emit(nc, S):
    from contextlib import ExitStack
    st = ExitStack()
    sems = {e: [st.enter_context(nc.semaphore(f"s_{e}{i}")) for i in range(4)] for e in S.CE}
    dsems = [st.enter_context(nc.semaphore(f"d{i}")) for i in range(S.NDS)]
    info = S.emit(sems, dsems)
    return st, info


def kernel(**inputs):
    nc, S = build_program()
    st, _ = _emit(nc, S)
    with st:
        in_maps = [_core_inputs(inputs, c) for c in range(8)]
        res = run_bass_kernel_spmd(nc, in_maps, core_ids=list(range(8)))
    out = np.empty((NB, L, D), np.float32)
    for c in range(8):
        b, r = c // 4, c % 4
        out[b, r * NTOK:(r + 1) * NTOK, :] = np.asarray(res.results[c]["outT"]).T
    return out
```

```python
import math
import numpy as np
import ml_dtypes
import concourse.bass as bass
import concourse.mybir as mybir
from concourse.bass_utils import run_bass_kernel_spmd

F32 = mybir.dt.float32
BF16 = mybir.dt.bfloat16
ALU = mybir.AluOpType
AF = mybir.ActivationFunctionType
bf16_np = ml_dtypes.bfloat16

D = 1024
L = 8192
NB = 2
NTOK = 2048
HD = 64
NQH = 8
NKV = 2
HYW = 512
DFF = 4096
EPS = 1e-6
NFFT = 2 * L
TWO_PI = 2.0 * math.pi
MAGIC = 12582912.0


class Op:
    __slots__ = ("eng", "fn", "reads", "writes", "dma", "idx", "gid", "waits", "signal", "clock",
                 "sem", "val")


class Sched:
    CE = ("pe", "act", "dve", "pool", "sp")
    EPOCH = 12000
    NDS = 40

    def __init__(self, nc):
        self.nc = nc
        self.E = {"pe": nc.tensor, "act": nc.scalar, "dve": nc.vector, "pool": nc.gpsimd,
                  "sp": nc.sync}
        self.ops = []
        self.last_w = {}
        self.readers = {}
        self.known = {e: {} for e in self.CE}
        self.known_dma = {e: set() for e in self.CE}
        self.cnt = {e: 0 for e in self.CE}
        self.last_op = {e: None for e in self.CE}
        self.live_dma = []

    def op(self, eng, fn, reads=(), writes=(), dma=False):
        o = Op()
        o.eng, o.fn, o.reads, o.writes, o.dma = eng, fn, tuple(reads), tuple(writes), dma
        o.signal = dma
        o.sem = None
        o.val = None
        o.gid = len(self.ops)
        deps = {}
        for k in o.reads:
            w = self.last_w.get(k)
            if w is not None:
                deps[w.gid] = w
        for k in o.writes:
            w = self.last_w.get(k)
            if w is not None:
                deps[w.gid] = w
            for r in self.readers.get(k, ()):
                deps[r.gid] = r
        self._finish(o, deps.values())
        for k in o.writes:
            self.last_w[k] = o
            self.readers[k] = []
        for k in o.reads:
            lst = self.readers.setdefault(k, [])
            if not dma:
                lst[:] = [r for r in lst if r.dma or r.eng != eng]
            lst.append(o)
        return o

    def _finish(self, o, deps):
        eng = o.eng
        kn = self.known[eng]
        kd = self.known_dma[eng]
        waits = []
        best = {}
        rset = set(o.reads)
        for d in deps:
            if d.dma:
                if d.gid not in kd:
                    waits.append(d)
                    kd.add(d.gid)
                continue
            if d.eng == eng:
                if eng in ("pe", "sp"):
                    continue
            if kn.get(d.eng, -1) >= d.idx:
                continue
            b = best.get(d.eng)
            if b is None or b.idx < d.idx:
                best[d.eng] = d
        for d in best.values():
            waits.append(d)
            if kn.get(d.eng, -1) < d.idx:
                kn[d.eng] = d.idx
            for e2, i2 in d.clock.items():
                if kn.get(e2, -1) < i2:
                    kn[e2] = i2
        for d in waits:
            d.signal = True
        o.waits = waits
        o.idx = self.cnt[eng]
        self.cnt[eng] += 1
        o.clock = dict(kn)
        self.ops.append(o)
        if o.fn is not None:
            self.last_op[eng] = o
        if o.dma:
            self.live_dma.append(o)

    def barrier(self):
        lasts = [self.last_op[e] for e in self.CE if self.last_op[e] is not None]
        dmas = list(self.live_dma)
        for e in self.CE:
            o = Op()
            o.eng, o.fn, o.reads, o.writes, o.dma = e, None, (), (), False
            o.signal = False
            o.sem = None
            o.val = None
            o.gid = len(self.ops)
            deps = [d for d in lasts if not d.dma and d.eng != e] + dmas
            self._finish(o, deps)
        self.live_dma = []

    def emit(self, sems, dsems):
        sig = {e: 0 for e in self.CE}
        nd = 0
        for o in self.ops:
            E = self.E[o.eng]
            for d in o.waits:
                E.wait_ge(d.sem, d.val)
            if o.fn is None:
                continue
            if o.dma:
                s = dsems[nd % self.NDS]
                rnd = nd // self.NDS
                if rnd > 0:
                    E.wait_ge(s, 16 * rnd)
                ins = o.fn()
                ins.then_inc(s, 16)
                o.sem, o.val = s, 16 * (rnd + 1)
                nd += 1
            else:
                ins = o.fn()
                if o.signal:
                    c = sig[o.eng]
                    o.sem = sems[o.eng][c // self.EPOCH]
                    o.val = c % self.EPOCH + 1
                    ins.then_inc(o.sem, 1)
                    sig[o.eng] = c + 1
        return sig, nd


_TABLES = None


def _rope_tables():
    rows = L // 64
    row = np.repeat(np.arange(rows, dtype=np.float32), 64)
    col = np.tile(np.arange(64, dtype=np.float32), rows)
    half = HD // 2
    inv_freq = (np.float32(10000.0) ** (-np.arange(0, half, 2, dtype=np.float32) / np.float32(half))).astype(np.float32)
    ang_r = (row[:, None] * inv_freq[None, :]).astype(np.float32)
    ang_c = (col[:, None] * inv_freq[None, :]).astype(np.float32)
    cos_r, sin_r = np.cos(ang_r), np.sin(ang_r)
    cos_c, sin_c = np.cos(ang_c), np.sin(ang_c)
    cosT = np.concatenate([cos_r.T, cos_r.T, cos_c.T, cos_c.T], axis=0).astype(np.float32)
    sinT = np.concatenate([sin_r.T, sin_r.T, sin_c.T, sin_c.T], axis=0).astype(np.float32)
    return np.ascontiguousarray(cosT), np.ascontiguousarray(sinT)


def _filter_pos_tables():
    f32 = np.float32
    t = np.linspace(0.0, 1.0, L, dtype=f32)
    bands = 16
    w_ang = (f32(2.0 * math.pi) * np.arange(L, dtype=f32) / f32(L)).astype(f32)
    band_f = np.linspace(1e-4, bands - 1, bands, dtype=f32)
    ang = (w_ang[:, None] * band_f[None, :]).astype(f32)
    z = np.concatenate([t[:, None], np.cos(ang), -np.sin(ang)], axis=-1).astype(f32)
    zT = np.ascontiguousarray(z.T)
    idx = (L - np.arange(L)) % L
    zrT = np.ascontiguousarray(z[idx].T)
    trev = t[idx].copy()
    return zT, zrT, t.copy(), trev


def _fft_tables():
    N = NFFT
    n1 = np.arange(128, dtype=np.float64)
    k1 = np.arange(64, dtype=np.float64)
    th = 2 * np.pi * np.outer(n1, k1 + 0.5) / 128.0
    fpack = np.concatenate([np.cos(th), -np.sin(th), np.sin(th)], axis=1)
    n2 = np.arange(128, dtype=np.float64)
    k2 = np.arange(128, dtype=np.float64)
    kap = k1[None, :, None] + 128.0 * k2[None, None, :] + 0.5
    thg = 2 * np.pi * n2[:, None, None] * kap / N
    G = np.stack([np.cos(thg), -np.sin(thg)], axis=2)
    thf = 2 * np.pi * np.outer(k2, n2) / 128.0
    finv = np.stack([np.cos(thf), np.sin(thf), -np.sin(thf)], axis=1)
    n1h = np.arange(64, dtype=np.float64)
    phi = 2 * np.pi * (k1[:, None, None] + 0.5) * (128.0 * n1h[None, None, :] + n2[None, :, None]) / N
    M2 = np.concatenate([(2.0 / N) * np.cos(phi), -(2.0 / N) * np.sin(phi)], axis=0)
    return (fpack.astype(bf16_np), G.astype(bf16_np), finv.astype(bf16_np), M2.astype(bf16_np))


def _tables():
    global _TABLES
    if _TABLES is None:
        cosT, sinT = _rope_tables()
        zT, zrT, t, trev = _filter_pos_tables()
        fpack, G, finv, M2 = _fft_tables()
        ii = np.arange(512, dtype=np.float64)
        jj = np.arange(16, dtype=np.float64)
        tbm = np.stack([ii / (L - 1), (L - ii) / (L - 1)]).astype(np.float32)
        jtm = np.stack([512.0 * jj / (L - 1), -512.0 * jj / (L - 1)]).astype(np.float32)
        rot = np.zeros((64, 64), np.float32)
        for m in range(64):
            if (m % 32) < 16:
                rot[m + 16, m] = -1.0
            else:
                rot[m - 16, m] = 1.0
        sel = np.zeros((65, 64), np.float32)
        sel[64, :] = 1.0
        _TABLES = dict(cosT=cosT, sinT=sinT, zT=zT, zrT=zrT, t=t, trev=trev, fpack=fpack, G=G,
                       finv=finv, M2=M2, rot=rot, sel=sel, tb=tbm, jt=jtm,
                       ones=np.ones((128, 128), np.float32),
                       ident=np.eye(128, dtype=np.float32).astype(bf16_np))
    return _TABLES


def build_program(phases=(0, 1, 2, 3, 4), debug=()):
    nc = bass.Bass("TRN2", target_bir_lowering=False)
    S = Sched(nc)

    def din(name, shape, dt=F32):
        return nc.dram_tensor(name, list(shape), dt, kind="ExternalInput").ap()

    def dscr(name, shape, dt):
        return nc.dram_tensor(name, list(shape), dt, kind="Internal").ap()

    xT = din("xT", [D, L])
    xTq = din("xTq", [D, NTOK])
    w_in = din("w_in", [D, 2304])
    w_out = din("w_out", [D, D])
    w_mi = din("w_mi", [D, DFF])
    w_mo = din("w_mo", [DFF, D])
    g1 = din("g1", [128, 8]); g2 = din("g2", [128, 8]); gf = din("gf", [128, 8])
    gq = din("gq", [64, 1]); gk = din("gk", [64, 1])
    ga = din("ga", [64, 8]); gh = din("gh", [128, 4])
    cw = din("cw", [128, 12, 3]); cb = din("cb", [128, 12])
    fw1 = din("fw1", [33, 64]); fw2 = din("fw2", [64, 64]); fw3 = din("fw3", [64, 64])
    fw4 = din("fw4", [64, 2048])
    fb = din("fb", [64, 3]); ffreq = din("ffreq", [64, 1])
    fdel = din("fdel", [128, 16])
    skp = din("skp", [128, 8])
    qmask = din("qmask", [128, 4])
    c_cos = din("c_cos", [64, L]); c_sin = din("c_sin", [64, L])
    c_cosq = din("c_cosq", [64, NTOK]); c_sinq = din("c_sinq", [64, NTOK])
    c_zT = din("c_zT", [2, 33, L])
    c_t = din("c_t", [2, L])
    c_tb = din("c_tb", [2, 512])
    c_jt = din("c_jt", [2, 16])
    c_fpack = din("c_fpack", [128, 192], BF16)
    c_G = din("c_G", [128, 64, 2, 128], BF16)
    c_finv = din("c_finv", [128, 3, 128], BF16)
    c_M2 = din("c_M2", [128, 128, 64], BF16)
    c_M2q = din("c_M2q", [128, 128, 16], BF16)
    c_rot = din("c_rot", [64, 64]); c_sel = din("c_sel", [65, 64]); c_ones = din("c_ones", [128, 128])
    c_ident = din("c_ident", [128, 128], BF16)
    outT = nc.dram_tensor("outT", [D, NTOK], F32, kind="ExternalOutput").ap()

    U_d = dscr("U_d", [1536, L], BF16)
    H_d = dscr("H_d", [8, 128, 64, 4, 128], BF16)
    taps_d = dscr("taps_d", [128, NFFT], BF16)
    zcm_d = dscr("zcm_d", [128, L], BF16)
    att_d = dscr("att_d", [512, NTOK], BF16)
    hy_d = dscr("hy_d", [512, NTOK], BF16)
    Wmi_d = dscr("Wmi_d", [D, DFF], BF16)
    Wmo_d = dscr("Wmo_d", [8, 128, 32, 128], BF16)

    from contextlib import ExitStack
    es = ExitStack()

    _names = {}

    def sb(name, shape, dt, stack=None):
        n_ = _names.get(name, 0)
        _names[name] = n_ + 1
        nm = name if n_ == 0 else f"{name}_r{n_}"
        return (stack or es).enter_context(nc.sbuf_tensor(nm, list(shape), dt))

    def dma(out, in_, reads, writes, eng="sp"):
        return S.op(eng, lambda: S.E[eng].dma_start(out=out, in_=in_), reads, writes, dma=True)

    def mm(out, lhsT, rhs, start, stop, reads, writes):
        return S.op("pe", lambda: nc.tensor.matmul(out, lhsT, rhs, start=start, stop=stop), reads, writes)

    rr = {"i": 0}

    def alt(engs=("dve", "pool")):
        rr["i"] += 1
        return engs[rr["i"] % len(engs)]

    def V(eng):
        return S.E[eng]

    with es:
        PS = [es.enter_context(nc.psum_tensor(f"ps{i}", [128, 512], F32)) for i in range(6)]
        PSB = [es.enter_context(nc.psum_tensor(f"psb{i}", [128, 1024], BF16)) for i in range(2)]

        ones = sb("ones", [128, 128], F32)
        ident = sb("ident", [128, 128], BF16)
        rot = sb("rot", [64, 64], F32)
        sel = sb("sel", [65, 64], F32)
        epsT = sb("epsT", [128, 1], F32)
        g1s = sb("g1s", [128, 8], F32); g2s = sb("g2s", [128, 8], F32); gfs = sb("gfs", [128, 8], F32)
        gqs = sb("gqs", [64, 1], F32); gks = sb("gks", [64, 1], F32)
        gas = sb("gas", [64, 8], F32); ghs = sb("ghs", [128, 4], F32)
        cws = sb("cws", [128, 12, 3], F32); cbs = sb("cbs", [128, 12], F32)
        skps = sb("skps", [128, 8], F32); qms = sb("qms", [128, 4], F32)
        for t_, d_ in ((ones, c_ones), (ident, c_ident), (rot, c_rot), (sel, c_sel), (g1s, g1), (g2s, g2),
                       (gfs, gf), (gqs, gq), (gks, gk), (gas, ga), (ghs, gh), (cws, cw), (cbs, cb),
                       (skps, skp), (qms, qmask)):
            dma(t_[:], d_, [], [t_.name if hasattr(t_, "name") else id(t_)])
        KEY = lambda t_: t_.name if hasattr(t_, "name") else id(t_)
        S.op("dve", lambda: nc.vector.memset(epsT[:], EPS), [], ["epsT"])

        FC = {}
        SNAP = {}

        def snap(name, src_ap, key, ncols):
            if name not in debug or name in SNAP:
                return
            SNAP[name] = nc.dram_tensor("dbg_" + name, [128, ncols], BF16, kind="ExternalOutput").ap()
            dma(SNAP[name], src_ap, [key], ["dbg_" + name])


        def load_fft_consts(stack):
            FC["fpack"] = sb("fpack", [128, 192], BF16, stack)
            FC["finv"] = sb("finv", [128, 3, 128], BF16, stack)
            FC["Gq"] = [sb(f"Gq{i}", [128, 8, 2, 128], BF16, stack) for i in range(2)]
            dma(FC["fpack"][:], c_fpack, [], ["fpack"])
            dma(FC["finv"][:], c_finv, [], ["finv"])

        def rsqrt(out, in_, scale, reads, writes, np_=128):
            S.op("act", lambda: nc.scalar.activation(out, in_, AF.Sqrt, bias=epsT[0:np_, :], scale=scale),
                 list(reads) + ["epsT"], writes)
            S.op("dve", lambda: nc.vector.reciprocal(out, out), writes, writes)

        def fft_forward(ztm, K, AT, cb_batch, zkey="ztm", akey="AT", pre_batch=None):
            fpack = FC["fpack"]
            for c2 in range(64):
                ps = PS[c2 % 2]
                pk = f"ps{c2 % 2}"
                for cc in range(2):
                    c = c2 * 2 + cc
                    mm(ps[:, cc * 192:(cc + 1) * 192], ztm[0:K, c, :], fpack[0:K, :], True, True,
                       [(zkey, c // 32), "fpack"], [pk])
                e = alt(("act", "dve"))
                src = ps[:, 0:384].rearrange("p (cc j) -> p j cc", cc=2)
                dst = AT[:, :, c2 * 2:c2 * 2 + 2]
                if e == "act":
                    S.op("act", lambda dst=dst, src=src: nc.scalar.copy(dst, src), [pk], [akey])
                else:
                    S.op("dve", lambda dst=dst, src=src: nc.vector.tensor_copy(dst, src), [pk], [akey])

            def load_g(q8):
                dma(FC["Gq"][q8 % 2][:], c_G[:, q8 * 8:(q8 + 1) * 8], [], [f"Gq{q8 % 2}"])

            load_g(0)
            for kb in range(32):
                q4 = kb // 4
                Gq = FC["Gq"][q4 % 2]
                gkey = f"Gq{q4 % 2}"
                if kb % 4 == 0 and q4 + 1 < 8:
                    load_g(q4 + 1)
                if pre_batch is not None:
                    pre_batch(kb)
                ps = PS[2 + kb % 2]
                pk = f"ps{2 + kb % 2}"
                for kk in range(2):
                    k1 = kb * 2 + kk
                    kl = k1 % 8
                    R = AT[:, k1, :]
                    I = AT[:, 64 + k1, :]
                    NI = AT[:, 128 + k1, :]
                    zr = ps[:, kk * 256:kk * 256 + 128]
                    zi = ps[:, kk * 256 + 128:kk * 256 + 256]
                    mm(zr, Gq[:, kl, 0, :], R, True, False, [akey, gkey], [pk])
                    mm(zr, Gq[:, kl, 1, :], NI, False, True, [akey, gkey], [pk])
                    mm(zi, Gq[:, kl, 1, :], R, True, False, [akey, gkey], [pk])
                    mm(zi, Gq[:, kl, 0, :], I, False, True, [akey, gkey], [pk])
                cb_batch(kb, ps, pk)

        import os as _os
        _SK = set(filter(None, _os.environ.get("P0SKIP", "").split(",")))

        def phase0():
          with ExitStack() as p0w:
            wst = [sb(f"wst{i}", [128, 8, 512], F32, p0w) for i in range(2)]
            wbf = [sb(f"wbf{i}", [128, 8, 512], BF16, p0w) for i in range(2)]
            n = 0
            jobs = []
            for cbk in range(8):
                jobs.append((w_mi.rearrange("(k p) n -> p k n", p=128)[:, :, cbk * 512:(cbk + 1) * 512],
                             Wmi_d.rearrange("(k p) n -> p k n", p=128)[:, :, cbk * 512:(cbk + 1) * 512]))
            for kr in range(4):
                for cbk in range(2):
                    jobs.append((w_mo.rearrange("(k p) n -> p k n", p=128)[:, kr * 8:(kr + 1) * 8, cbk * 512:(cbk + 1) * 512],
                                 (kr, cbk)))
            for src, dst in ([] if "cast" in _SK else jobs):
                s_ = n % 2
                dma(wst[s_][:], src, [], [f"wst{s_}"])
                e = ("dve", "pool", "act")[n % 3]
                if e == "act":
                    S.op("act", lambda s_=s_: nc.scalar.copy(wbf[s_][:], wst[s_][:]), [f"wst{s_}"], [f"wbf{s_}"])
                else:
                    S.op(e, lambda s_=s_, e=e: V(e).tensor_copy(wbf[s_][:], wst[s_][:]), [f"wst{s_}"], [f"wbf{s_}"])
                if n < 8:
                    dma(dst, wbf[s_][:], [f"wbf{s_}"], ["Wmid"])
                else:
                    kr_, cbk_ = dst
                    for c_ in range(4):
                        dma(Wmo_d[cbk_ * 4 + c_, :, kr_ * 8:(kr_ + 1) * 8, :], wbf[s_][:, :, c_ * 128:(c_ + 1) * 128], [f"wbf{s_}"], ["Wmod"])
                n += 1
            S.barrier()
          with ExitStack() as p0:
            load_fft_consts(p0)

            w1s = sb("w1s", [33, 64], F32, p0); w2s = sb("w2s", [64, 64], F32, p0); w3s = sb("w3s", [64, 64], F32, p0)
            w4b = sb("w4b", [64, 2048], BF16, p0)
            fbs = sb("fbs", [64, 3], F32, p0); frs = sb("frs", [64, 1], F32, p0); ffb = sb("ffb", [64, 3], F32, p0)
            dls = sb("dls", [128, 16], F32, p0); ndl = sb("ndl", [128, 16], F32, p0)
            negpi = sb("negpi", [128, 1], F32, p0)
            H3 = sb("H3", [64, 2, L], BF16, p0)
            p0m = ExitStack()
            w4s = sb("w4s", [64, 2048], F32, p0m)
            for t_, d_, k_ in ((w1s, fw1, "w1s"), (w2s, fw2, "w2s"), (w3s, fw3, "w3s"), (w4s, fw4, "w4s"),
                               (fbs, fb, "fbs"), (frs, ffreq, "frs"), (dls, fdel, "dls")):
                dma(t_[:], d_, [], [k_])
            S.op("dve", lambda: nc.vector.tensor_copy(w4b[:], w4s[:]), ["w4s"], ["w4b"])
            S.op("dve", lambda: nc.vector.tensor_scalar(ffb[:], fbs[:], frs[:, 0:1], None, ALU.mult), ["fbs", "frs"], ["ffb"])
            S.op("dve", lambda: nc.vector.tensor_scalar(ndl[:], dls[:], -1.0, None, ALU.mult), ["dls"], ["ndl"])
            S.op("dve", lambda: nc.vector.tensor_tensor(ndl[:], ndl[:], dls[:], ALU.min), ["dls", "ndl"], ["ndl"])
            S.op("dve", lambda: nc.vector.memset(negpi[:], 0.0), [], ["negpi"])
            zc = [sb(f"zc{i}", [33, 512], F32, p0m) for i in range(4)]
            faL = [sb(f"fa{i}", [64, 512], F32, p0m) for i in range(2)]
            fbufL = [sb(f"fbuf{i}", [64, 512], F32, p0m) for i in range(2)]
            fhL = [sb(f"fh{i}", [64, 512], F32, p0m) for i in range(2)]

            def sin_layer(ps, li, dst, pk, v_, dkey):
                fa, fbuf = faL[v_], fbufL[v_]
                fak, fbk = f"fa{v_}", f"fbuf{v_}"
                S.op("dve", lambda: nc.vector.tensor_scalar(fa[:], ps[0:64, :], frs[:, 0:1], ffb[:, li:li + 1], ALU.mult, ALU.add),
                     [pk, "frs", "ffb"], [fak])
                S.op("dve", lambda: nc.vector.tensor_scalar(fbuf[:], fa[:], 1.0 / TWO_PI, MAGIC, ALU.mult, ALU.add), [fak], [fbk])
                S.op("dve", lambda: nc.vector.tensor_scalar(fbuf[:], fbuf[:], -MAGIC, -TWO_PI, ALU.add, ALU.mult), [fbk], [fbk])
                S.op("dve", lambda: nc.vector.tensor_tensor(fa[:], fa[:], fbuf[:], ALU.add), [fak, fbk], [fak])
                S.op("act", lambda: nc.scalar.activation(dst, fa[:], AF.Sin, bias=negpi[0:64, :], scale=1.0), [fak, "negpi"], [dkey])

            for j in range(0 if "mlp" in _SK else 16):
                for var in range(2):
                    s_ = (j % 2) * 2 + var
                    psm = PS[4 + var]
                    pmk = f"ps{4 + var}"
                    fh = fhL[var]
                    fhk = f"fh{var}"
                    dma(zc[s_][:], c_zT[var, :, j * 512:(j + 1) * 512], [], [f"zc{s_}"])
                    mm(psm[0:64, :], w1s[:], zc[s_][:], True, True, [f"zc{s_}", "w1s"], [pmk])
                    sin_layer(psm, 0, fh[:], pmk, var, fhk)
                    mm(psm[0:64, :], w2s[:], fh[:], True, True, [fhk, "w2s"], [pmk])
                    sin_layer(psm, 1, fh[:], pmk, var, fhk)
                    mm(psm[0:64, :], w3s[:], fh[:], True, True, [fhk, "w3s"], [pmk])
                    sin_layer(psm, 2, H3[:, var, j * 512:(j + 1) * 512], pmk, var, ("H3", var))

            S.barrier()
            p0m.close()
            tapsU = sb("tapsU", [128, NFFT], BF16, p0)
            ztmf = sb("ztmf", [128, 128, 128], BF16, p0)
            ATf = sb("ATf", [128, 192, 128], BF16, p0)
            tb = sb("tb", [128, 2, 512], F32, p0)
            jt = sb("jt", [128, 2, 16], F32, p0)
            for hf in range(2):
                dma(tb[:, hf, :], c_tb[hf:hf + 1, :].broadcast_to([128, 512]), [], ["tb"])
                dma(jt[:, hf, :], c_jt[hf:hf + 1, :].broadcast_to([128, 16]), [], ["jt"])
            wbase = [sb(f"wbase{i}", [128, 512], F32, p0) for i in range(2)]
            wcj = [sb(f"wcj{i}", [128, 16], F32, p0) for i in range(2)]
            tpf = [sb(f"tpf{i}", [128, 512], F32, p0) for i in range(2)]
            psum_ = sb("psum_", [128, 32], F32, p0)
            nrm = sb("nrm", [128, 2], F32, p0)
            Hst = [sb(f"Hst{i}", [128, 2, 4, 128], BF16, p0) for i in range(4)]
            for go in range(0 if "go" in _SK else 8):
                g, o = go // 2, go % 2
                for half in range(0 if "taps" in _SK else 2):
                    col0 = o * 1024 + half * 512 + g * 128
                    ct = col0 // 128
                    S.op("act", lambda half=half, ct=ct: nc.scalar.activation(wbase[half][:], tb[:, half, :], AF.Exp, scale=ndl[:, ct:ct + 1]),
                         ["tb", "ndl"], [f"wbase{half}"])
                    S.op("act", lambda half=half, ct=ct: nc.scalar.activation(wcj[half][:], jt[:, half, :], AF.Exp, scale=ndl[:, ct:ct + 1]),
                         ["jt", "ndl"], [f"wcj{half}"])
                    for j in range(16):
                        s_ = j % 2
                        pst = PS[4 + s_]
                        ptk = f"ps{4 + s_}"
                        mm(pst[:, :], w4b[:, col0:col0 + 128], H3[:, half, j * 512:(j + 1) * 512], True, True,
                           [("H3", half), "w4b"], [ptk])
                        S.op("dve", lambda s_=s_, pst=pst, half=half, j=j: nc.vector.scalar_tensor_tensor(
                            out=tpf[s_][:], in0=pst[:, :], scalar=wcj[half][:, j:j + 1], in1=wbase[half][:], op0=ALU.mult, op1=ALU.mult),
                            [ptk, f"wcj{half}", f"wbase{half}"], [f"tpf{s_}"])
                        idx = half * 16 + j
                        if half == 1 and j == 0:
                            S.op("dve", lambda s_=s_: nc.vector.memset(tpf[s_][:, 0:1], 0.0), [f"tpf{s_}"], [f"tpf{s_}"])
                        S.op("dve", lambda idx=idx, s_=s_: nc.vector.tensor_reduce(psum_[:, idx:idx + 1], tpf[s_][:], mybir.AxisListType.X, ALU.add,
                                                                                  apply_absolute_value=True), [f"tpf{s_}"], ["psum_"])
                        S.op("pool", lambda half=half, j=j, s_=s_: nc.gpsimd.tensor_copy(tapsU[:, half * L + j * 512: half * L + (j + 1) * 512], tpf[s_][:]),
                             [f"tpf{s_}"], ["tapsU"])
                S.op("dve", lambda: nc.vector.tensor_reduce(nrm[:, 0:1], psum_[:], mybir.AxisListType.X, ALU.add), ["psum_"], ["nrm"])
                S.op("dve", lambda: nc.vector.reciprocal(nrm[:, 0:1], nrm[:, 0:1]), ["nrm"], ["nrm"])
                S.op("dve", lambda: nc.vector.tensor_scalar(nrm[:, 1:2], nrm[:, 0:1], -1.0, None, ALU.mult), ["nrm"], ["nrm"])
                S.op("dve", lambda: nc.vector.tensor_scalar(tapsU[:, 0:L], tapsU[:, 0:L], nrm[:, 0:1], None, ALU.mult), ["tapsU", "nrm"], ["tapsU"])
                S.op("pool", lambda: nc.gpsimd.tensor_scalar(tapsU[:, L:NFFT], tapsU[:, L:NFFT], nrm[:, 1:2], None, ALU.mult), ["tapsU", "nrm"], ["tapsU"])
                tv = taps_d.rearrange("c (n1 n2) -> n1 c n2", n2=128)
                for q4 in range(4):
                    dma(taps_d[q4 * 32:(q4 + 1) * 32, :], tapsU[q4 * 32:(q4 + 1) * 32, :], ["tapsU"], [("taps_d", q4)])
                for q4 in range(4):
                    dma(ztmf[:, q4 * 32:(q4 + 1) * 32, :], tv[:, q4 * 32:(q4 + 1) * 32, :], [("taps_d", q4)], [("ztm", q4)])

                def cb_filter(kb, ps, pk, go=go):
                    s_ = kb % 4
                    zv = ps[:, :].rearrange("p (k r c) -> p k r c", k=2, r=2)
                    S.op("act", lambda: nc.scalar.copy(Hst[s_][:, :, 0:2, :], zv), [pk], [f"Hst{s_}"])
                    S.op("dve", lambda: nc.vector.tensor_copy(Hst[s_][:, :, 2, :], zv[:, :, 1, :]), [pk], [f"Hst{s_}"])
                    S.op("dve", lambda: nc.vector.tensor_copy(Hst[s_][:, :, 3, :], zv[:, :, 0, :]), [pk], [f"Hst{s_}"])
                    dma(H_d[go, :, kb * 2:kb * 2 + 2], Hst[s_][:], [f"Hst{s_}"], [("H_d", go)])

                if "fft" not in _SK:
                    fft_forward(ztmf, 128, ATf, cb_filter)
            S.barrier()
        def rms_apply(src, gvec, dst, scale, nk, tag, sqb, rs, np_=128, ss_ps=0):
            S.op("act", lambda: nc.scalar.activation(sqb[0:np_, 0:nk, :], src, AF.Square), [tag + "_src"], ["sqb"])
            for k in range(nk):
                mm(PS[ss_ps][0:np_, :], ones[0:np_, 0:np_], sqb[0:np_, k, :], k == 0, k == nk - 1, ["sqb", "ones"], [f"ps{ss_ps}"])
            S.op("act", lambda: nc.scalar.activation(rs[0:np_, :], PS[ss_ps][0:np_, :], AF.Sqrt, bias=epsT[0:np_, :], scale=scale),
                 [f"ps{ss_ps}", "epsT"], ["rs"])
            S.op("dve", lambda: nc.vector.reciprocal(rs[0:np_, :], rs[0:np_, :]), ["rs"], ["rs"])
            for k in range(nk):
                e = "dve"
                S.op(e, lambda k=k, e=e: V(e).scalar_tensor_tensor(out=dst[:, k, :], in0=src[:, k, :], scalar=gvec[:, k:k + 1],
                                                                  in1=rs[0:np_, :], op0=ALU.mult, op1=ALU.mult),
                     [tag + "_src", "rs", "gvecs"], [tag + "_dst"])

        def phase12():
          with ExitStack() as p12:
            KT = sb("KT", [64, 2, L], BF16, p12)
            Vx = sb("Vx", [128, 64, 2, 65], BF16, p12)
            QT = sb("QT", [64, 8, NTOK], BF16, p12)
            S.op("pool", lambda: nc.gpsimd.memset(Vx[:, :, :, 64:65], 1.0), [], ["Vx1"])
            with ExitStack() as p1:
                Wb = sb("Wb", [128, 8, 2304], BF16, p1)
                with ExitStack() as p1c:
                    wst = [sb(f"wsi{i}", [128, 8, 384], F32, p1c) for i in range(2)]
                    wv = w_in.rearrange("(k p) n -> p k n", p=128)
                    for bk in range(6):
                        s_ = bk % 2
                        dma(wst[s_][:], wv[:, :, bk * 384:(bk + 1) * 384], [], [f"wst{s_}"])
                        e = alt()
                        S.op(e, lambda s_=s_, e=e, bk=bk: V(e).tensor_copy(Wb[:, :, bk * 384:(bk + 1) * 384], wst[s_][:]),
                             [f"wst{s_}"], ["Wb"])
                    S.barrier()
                xin = [sb(f"xin{i}", [128, 8, 512], F32, p1) for i in range(2)]
                sqb = sb("sqb", [128, 8, 512], F32, p1)
                rs = sb("rs", [128, 512], F32, p1)
                aT = sb("aT", [128, 8, 512], BF16, p1)
                ust = sb("ust", [128, 12, 512], BF16, p1)
                hsq = sb("hsq", [64, 512], F32, p1); hrs = sb("hrs", [64, 512], F32, p1); hkn = sb("hkn", [64, 512], F32, p1)
                ht1 = sb("ht1", [64, 512], F32, p1); ht2 = sb("ht2", [64, 512], F32, p1)
                csb = [sb(f"csb{i}", [64, 2, 512], F32, p1) for i in range(2)]

                def headproc(gvec, cs, cskey, dst, dstkey):
                    src = PS[1][0:64, :]
                    S.op("act", lambda: nc.scalar.activation(hsq[:], src, AF.Square), ["ps1"], ["hsq"])
                    mm(PS[2][0:64, :], ones[0:64, 0:64], hsq[:], True, True, ["hsq", "ones"], ["ps2"])
                    S.op("act", lambda: nc.scalar.activation(hrs[:], PS[2][0:64, :], AF.Sqrt, bias=epsT[0:64, :], scale=1.0 / 64.0),
                         ["ps2", "epsT"], ["hrs"])
                    S.op("dve", lambda: nc.vector.reciprocal(hrs[:], hrs[:]), ["hrs"], ["hrs"])
                    S.op("dve", lambda: nc.vector.scalar_tensor_tensor(out=hkn[:], in0=src, scalar=gvec[:, 0:1], in1=hrs[:],
                                                                      op0=ALU.mult, op1=ALU.mult), ["ps1", "hrs", "gvecs"], ["hkn"])
                    mm(PS[3][0:64, :], rot[:], hkn[:], True, True, ["hkn", "rot"], ["ps3"])
                    S.op("pool", lambda: nc.gpsimd.tensor_tensor(ht1[:], hkn[:], cs[:, 0, :], ALU.mult), ["hkn", cskey], ["ht1"])
                    S.op("dve", lambda: nc.vector.tensor_tensor(ht2[:], PS[3][0:64, :], cs[:, 1, :], ALU.mult), ["ps3", cskey], ["ht2"])
                    S.op("pool", lambda: nc.gpsimd.tensor_tensor(dst, ht1[:], ht2[:], ALU.add), ["ht1", "ht2"], [dstkey])

                xv = xT.rearrange("(k p) t -> p k t", p=128)
                Uv = U_d.rearrange("(ct p) t -> p ct t", p=128)
                for j in range(16):
                    s_ = j % 2
                    tsl = slice(j * 512, (j + 1) * 512)
                    dma(xin[s_][:], xv[:, :, tsl], [], [f"xin{s_}", "n1_src"])
                    dma(csb[s_][:, 0, :], c_cos[:, tsl], [], [f"csb{s_}"])
                    dma(csb[s_][:, 1, :], c_sin[:, tsl], [], [f"csb{s_}"])
                    rms_apply(xin[s_][:], g1s, aT, 1.0 / D, 8, "n1", sqb, rs)
                    for kvh in range(2):
                        for k in range(8):
                            mm(PS[1][0:64, :], Wb[:, k, 512 + kvh * 64:512 + (kvh + 1) * 64], aT[:, k, :], k == 0, k == 7,
                               ["Wb", "n1_dst"], ["ps1"])
                        headproc(gks, csb[s_], f"csb{s_}", KT[:, kvh, tsl], "KT")
                    for tt in range(4):
                        for k in range(8):
                            mm(PS[4][:, tt * 128:(tt + 1) * 128], aT[:, k, tt * 128:(tt + 1) * 128], Wb[:, k, 640:768], k == 0, k == 7,
                               ["Wb", "n1_dst"], ["ps4"])
                    S.op("act", lambda j=j: nc.scalar.copy(Vx[:, j * 4:(j + 1) * 4, :, 0:64],
                                                         PS[4][:, :].rearrange("p (t h d) -> p t h d", t=4, h=2)), ["ps4"], ["Vx"])
                    for ct in range(12):
                        for k in range(8):
                            mm(PS[5][:, :], Wb[:, k, 768 + ct * 128:768 + (ct + 1) * 128], aT[:, k, :], k == 0, k == 7,
                               ["Wb", "n1_dst"], ["ps5"])
                        if ct % 2 == 0:
                            S.op("act", lambda ct=ct: nc.scalar.copy(ust[:, ct, :], PS[5][:, :]), ["ps5"], ["ust"])
                        else:
                            S.op("dve", lambda ct=ct: nc.vector.tensor_copy(ust[:, ct, :], PS[5][:, :]), ["ps5"], ["ust"])
                    dma(Uv[:, :, tsl], ust[:], ["ust"], ["U_d"])
                xqv = xTq.rearrange("(k p) t -> p k t", p=128)
                for j in range(4):
                    s_ = j % 2
                    tsl = slice(j * 512, (j + 1) * 512)
                    dma(xin[s_][:], xqv[:, :, tsl], [], [f"xin{s_}", "n1_src"])
                    dma(csb[s_][:, 0, :], c_cosq[:, tsl], [], [f"csb{s_}"])
                    dma(csb[s_][:, 1, :], c_sinq[:, tsl], [], [f"csb{s_}"])
                    rms_apply(xin[s_][:], g1s, aT, 1.0 / D, 8, "n1", sqb, rs)
                    for h in range(8):
                        for k in range(8):
                            mm(PS[1][0:64, :], Wb[:, k, h * 64:(h + 1) * 64], aT[:, k, :], k == 0, k == 7, ["Wb", "n1_dst"], ["ps1"])
                        headproc(gqs, csb[s_], f"csb{s_}", QT[:, h, tsl], "QT")
                S.barrier()
            if 2 in phases:
              with ExitStack() as p2:
                pb = [sb(f"pb{i}", [128, 512], BF16, p2) for i in range(3)]
                osb = sb("osb", [65, 512], F32, p2)
                rden = sb("rden", [64, 512], F32, p2)
                ast = [sb(f"ast{i}", [64, 512], BF16, p2) for i in range(2)]
                n_ = 0
                for qc in range(4):
                    qsl = slice(qc * 512, (qc + 1) * 512)
                    for h in range(8):
                        kvh = h // 4
                        ob = 3 + n_ % 2
                        obk = f"ps{ob}"
                        LA = 2
                        for step in range(64 + LA):
                            if step < 64:
                                kt = step
                                b3 = kt % 3
                                mm(PS[b3][:, :], KT[:, kvh, kt * 128:(kt + 1) * 128], QT[:, h, qsl], True, True, ["KT", "QT"], [f"ps{b3}"])
                                S.op("act", lambda b3=b3: nc.scalar.activation(pb[b3][:], PS[b3][:, :], AF.Exp, scale=0.125),
                                     [f"ps{b3}"], [f"pb{b3}"])
                            kt = step - LA
                            if kt >= 0:
                                b3 = kt % 3
                                mm(PS[ob][0:65, :], Vx[:, kt, kvh, :], pb[b3][:], kt == 0, kt == 63, ["Vx", "Vx1", f"pb{b3}"], [obk])
                        S.op("dve", lambda ob=ob: nc.vector.tensor_copy(osb[:], PS[ob][0:65, :]), [obk], ["osb"])
                        mm(PS[5][0:64, :], sel[:], osb[:], True, True, ["osb", "sel"], ["ps5"])
                        S.op("dve", lambda: nc.vector.reciprocal(rden[:], PS[5][0:64, :]), ["ps5"], ["rden"])
                        a_ = n_ % 2
                        S.op("pool", lambda a_=a_: nc.gpsimd.tensor_tensor(ast[a_][:], osb[0:64, :], rden[:], ALU.mult), ["osb", "rden"], [f"ast{a_}"])
                        dma(att_d[h * 64:(h + 1) * 64, qsl], ast[a_][:], [f"ast{a_}"], ["att_d"])
                        n_ += 1
                S.barrier()

        def phase3():
          with ExitStack() as p3:
            load_fft_consts(p3)
            finv = FC["finv"]
            M2s = sb("M2s", [128, 128, 64], BF16, p3)
            M2qs = sb("M2qs", [128, 128, 16], BF16, p3)
            dma(M2s[:], c_M2, [], ["M2s"])
            dma(M2qs[:], c_M2q, [], ["M2qs"])
            RA = sb("RA", [128, 192 * 128 + 8], BF16, p3)
            RB = sb("RB", [128, 128 * 128], BF16, p3)
            RC = sb("RC", [128, 128 * 128], BF16, p3)
            vt = sb("vt", [128, L], BF16, p3)
            x1t = sb("x1t", [128, L], BF16, p3)
            x2q = sb("x2q", [128, NTOK], BF16, p3)
            zzq = sb("zzq", [128, NTOK], BF16, p3)
            hyst = sb("hyst", [128, NTOK], BF16, p3)
            Hc = [sb(f"Hc{i}", [128, 2, 4, 128], BF16, p3) for i in range(4)]
            Pb = sb("Pb", [128, 2, 2, 128], F32, p3)
            Qb = sb("Qb", [128, 2, 2, 128], F32, p3)
            Yb = [sb(f"Yb{i}", [128, 2, 2, 128], BF16, p3) for i in range(2)]
            gt = [sb(f"gt{i}", [128, 512], F32, p3) for i in range(2)]
            sct = gt
            uraw = RA[:, 0:3 * 8194].rearrange("p (i t) -> p i t", i=3)
            AT = RA[:, 0:192 * 128].rearrange("p (j c) -> p j c", c=128)
            ztm = RB[:, :].rearrange("p (c n) -> p c n", n=128)
            Eb = RB[:, :].rearrange("p (n c) -> p n c", c=128)
            x2t = RC[:, 0:L]
            ET = RC[:, :].rearrange("p (c k) -> p c k", k=128)
            RAK, RBK, RCK = "RA", "RB", "RC"

            def masked_quarter(dst, src, dkey, skey):
                S.op("dve", lambda: nc.vector.tensor_scalar(dst[:], src[:, 0:NTOK], qms[:, 0:1], None, ALU.mult), [skey, "qms"], [dkey])
                for q in range(1, 4):
                    S.op("dve", lambda q=q: nc.vector.scalar_tensor_tensor(out=dst[:], in0=src[:, q * NTOK:(q + 1) * NTOK], scalar=qms[:, q:q + 1],
                                                                          in1=dst[:], op0=ALU.mult, op1=ALU.add), [skey, "qms", dkey], [dkey])

            RBQ = [(RBK, q4) for q4 in range(4)]

            def bounce_to_ztm(src, skey):
                zv = zcm_d.rearrange("c (n1 n2) -> n1 c n2", n2=128)
                for q4 in range(4):
                    dma(zcm_d[q4 * 32:(q4 + 1) * 32, :], src[q4 * 32:(q4 + 1) * 32, :], [skey], [("zcm_d", q4)])
                for q4 in range(4):
                    dma(ztm[0:64, q4 * 32:(q4 + 1) * 32, :], zv[:, q4 * 32:(q4 + 1) * 32, :], [("zcm_d", q4)], [(RBK, q4)])

            def conv_core(go):
                def load_h(kb):
                    dma(Hc[kb % 4][:], H_d[go, :, kb * 2:kb * 2 + 2], [("H_d", go)], [f"Hc{kb % 4}"])

                def pre(kb):
                    if kb == 0:
                        load_h(0)
                        load_h(1)
                    if kb + 2 < 32:
                        load_h(kb + 2)

                def cb(kb, ps, pk):
                    s_ = kb % 4
                    y_ = kb % 2
                    zv = ps[:, :].rearrange("p (k r c) -> p k r c", k=2, r=2)
                    S.op("dve", lambda: nc.vector.tensor_tensor(Pb[:], zv, Hc[s_][:, :, 0:2, :], ALU.mult), [pk, f"Hc{s_}"], ["Pb"])
                    S.op("dve", lambda: nc.vector.tensor_tensor(Qb[:], zv, Hc[s_][:, :, 2:4, :], ALU.mult), [pk, f"Hc{s_}"], ["Qb"])
                    S.op("pool", lambda: nc.gpsimd.tensor_tensor(Yb[y_][:, :, 0, :], Pb[:, :, 0, :], Pb[:, :, 1, :], ALU.subtract), ["Pb"], [f"Yb{y_}"])
                    S.op("pool", lambda: nc.gpsimd.tensor_tensor(Yb[y_][:, :, 1, :], Qb[:, :, 0, :], Qb[:, :, 1, :], ALU.add), ["Qb"], [f"Yb{y_}"])
                    pe_ = PS[4 + kb % 2]
                    pek = f"ps{4 + kb % 2}"
                    er = pe_[:, 0:256].rearrange("p (k c) -> p k c", k=2)
                    ei = pe_[:, 256:512].rearrange("p (k c) -> p k c", k=2)
                    yr = Yb[y_][:, :, 0, :]
                    yi = Yb[y_][:, :, 1, :]
                    mm(er, finv[:, 0, :], yr, True, False, [f"Yb{y_}", "finv"], [pek])
                    mm(er, finv[:, 2, :], yi, False, True, [f"Yb{y_}", "finv"], [pek])
                    mm(ei, finv[:, 1, :], yr, True, False, [f"Yb{y_}", "finv"], [pek])
                    mm(ei, finv[:, 0, :], yi, False, True, [f"Yb{y_}", "finv"], [pek])
                    S.op("act", lambda: nc.scalar.copy(ET[:, :, kb * 2:kb * 2 + 2], pe_[:, 0:256].rearrange("p (k c) -> p c k", k=2)), [pek], [RCK])
                    S.op("dve", lambda: nc.vector.tensor_copy(ET[:, :, 64 + kb * 2:64 + kb * 2 + 2], pe_[:, 256:512].rearrange("p (k c) -> p c k", k=2)),
                         [pek], [RCK])
                fft_forward(ztm, 64, AT, cb, zkey=RBK, akey=RAK, pre_batch=pre)
                for c4 in range(32):
                    pb_ = PSB[c4 % 2]
                    pbk = f"psb{c4 % 2}"
                    for cc in range(4):
                        c = c4 * 4 + cc
                        S.op("pe", lambda c=c, cc=cc, pb_=pb_: nc.tensor.transpose(pb_[:, cc * 128:(cc + 1) * 128], ET[:, c, :], ident[:]),
                             [RCK, "ident"], [pbk])
                    src = pb_[:, 0:512].rearrange("p (cc n) -> p n cc", cc=4)
                    dst = Eb[:, :, c4 * 4:c4 * 4 + 4]
                    if c4 % 2 == 0:
                        S.op("act", lambda src=src, dst=dst: nc.scalar.copy(dst, src), [pbk], RBQ)
                    else:
                        S.op("dve", lambda src=src, dst=dst: nc.vector.tensor_copy(dst, src), [pbk], RBQ)

            Udv = U_d
            for g in range(4):
                for i in range(3):
                    dma(uraw[:, i, 1:L + 1], Udv[i * 512 + g * 128:i * 512 + (g + 1) * 128, :], ["U_d"], [RAK])
                S.op("pool", lambda: nc.gpsimd.memset(uraw[:, :, 0:1], 0.0), [RAK], [RAK])
                S.op("pool", lambda: nc.gpsimd.memset(uraw[:, :, L + 1:L + 2], 0.0), [RAK], [RAK])
                for i, (dst, dkey) in enumerate(((vt, "vt"), (x1t, "x1t"), (x2t, RCK))):
                    ct = i * 4 + g
                    for q in range(16):
                        e = "dve"
                        st = sct[q % 2]
                        sk = f"gt{q % 2}"
                        o0 = q * 512
                        S.op(e, lambda e=e, st=st, i=i, o0=o0, ct=ct: V(e).tensor_scalar(st[:], uraw[:, i, o0:o0 + 512], cws[:, ct, 0:1], cbs[:, ct:ct + 1],
                                                                                      ALU.mult, ALU.add), [RAK, "cws", "cbs"], [sk])
                        S.op(e, lambda e=e, st=st, i=i, o0=o0, ct=ct: V(e).scalar_tensor_tensor(out=st[:], in0=uraw[:, i, o0 + 1:o0 + 513], scalar=cws[:, ct, 1:2],
                                                                                             in1=st[:], op0=ALU.mult, op1=ALU.add), [RAK, "cws", sk], [sk])
                        S.op(e, lambda e=e, st=st, i=i, o0=o0, ct=ct, dst=dst: V(e).scalar_tensor_tensor(out=dst[:, o0:o0 + 512], in0=uraw[:, i, o0 + 2:o0 + 514],
                                                                                                      scalar=cws[:, ct, 2:3], in1=st[:], op0=ALU.mult, op1=ALU.add),
                             [RAK, "cws", sk], [dkey])
                masked_quarter(x2q, x2t, "x2q", RCK)
                snap("vt0", vt[:], "vt", L)
                snap("x1t0", x1t[:], "x1t", L)
                bounce_to_ztm(vt, "vt")
                conv_core(g * 2 + 0)
                vt3 = vt[:, :].rearrange("p (n1 n2) -> p n1 n2", n2=128)
                x13 = x1t[:, :].rearrange("p (n1 n2) -> p n1 n2", n2=128)
                for nb in range(16):
                    ps = PS[nb % 2]
                    pk = f"ps{nb % 2}"
                    for j in range(8):
                        n2 = nb * 8 + j
                        mm(ps[:, j * 64:(j + 1) * 64], Eb[:, n2, :], M2s[:, n2, :], True, True, RBQ + ["M2s"], [pk])
                    psv = ps[:, :].rearrange("p (j n) -> p n j", j=8)
                    gtv = gt[nb % 2][:, :].rearrange("p (n j) -> p n j", j=8)
                    gk_ = f"gt{nb % 2}"
                    S.op("dve", lambda psv=psv, gtv=gtv, nb=nb, g=g: nc.vector.scalar_tensor_tensor(out=gtv, in0=vt3[:, :, nb * 8:(nb + 1) * 8],
                                                                                             scalar=skps[:, g:g + 1], in1=psv, op0=ALU.mult, op1=ALU.add),
                         [pk, "vt", "skps"], [gk_])
                    S.op("pool", lambda gtv=gtv, nb=nb: nc.gpsimd.tensor_tensor(vt3[:, :, nb * 8:(nb + 1) * 8], gtv, x13[:, :, nb * 8:(nb + 1) * 8], ALU.mult),
                         [gk_, "x1t"], ["vt"])
                snap("zz0", vt[:], "vt", L)
                masked_quarter(zzq, vt, "zzq", "vt")
                bounce_to_ztm(vt, "vt")
                conv_core(g * 2 + 1)
                zq3 = zzq[:, :].rearrange("p (n1 n2) -> p n1 n2", n2=128)
                xq3 = x2q[:, :].rearrange("p (n1 n2) -> p n1 n2", n2=128)
                hy3 = hyst[:, :].rearrange("p (n1 n2) -> p n1 n2", n2=128)
                for nb in range(4):
                    ps = PS[nb % 2]
                    pk = f"ps{nb % 2}"
                    for j in range(32):
                        n2 = nb * 32 + j
                        mm(ps[:, j * 16:(j + 1) * 16], Eb[:, n2, :], M2qs[:, n2, :], True, True, RBQ + ["M2qs"], [pk])
                    psv = ps[:, :].rearrange("p (j n) -> p n j", j=32)
                    gtv = gt[nb % 2][:, :].rearrange("p (n j) -> p n j", j=32)
                    gk_ = f"gt{nb % 2}"
                    S.op("dve", lambda psv=psv, gtv=gtv, nb=nb, g=g: nc.vector.scalar_tensor_tensor(out=gtv, in0=zq3[:, :, nb * 32:(nb + 1) * 32],
                                                                                             scalar=skps[:, 4 + g:5 + g], in1=psv, op0=ALU.mult, op1=ALU.add),
                         [pk, "zzq", "skps"], [gk_])
                    S.op("pool", lambda gtv=gtv, nb=nb: nc.gpsimd.tensor_tensor(hy3[:, :, nb * 32:(nb + 1) * 32], gtv, xq3[:, :, nb * 32:(nb + 1) * 32], ALU.mult),
                         [gk_, "x2q"], ["hyst"])
                dma(hy_d[g * 128:(g + 1) * 128, :], hyst[:], ["hyst"], ["hy_d"])
            S.barrier()

        def phase4():
          with ExitStack() as p4:
            WoA = sb("WoA", [64, 8, D], BF16, p4)
            WoH = sb("WoH", [128, 4, D], BF16, p4)
            with ExitStack() as p4c:
                wsa = sb("wsa", [64, 8, D], F32, p4c)
                wsh = sb("wsh", [128, 4, D], F32, p4c)
                dma(wsa[:], w_out[0:512, :].rearrange("(h d) n -> d h n", d=64), [], ["wsa"])
                dma(wsh[:], w_out[512:1024, :].rearrange("(g p) n -> p g n", p=128), [], ["wsh"])
                S.op("dve", lambda: nc.vector.tensor_copy(WoA[:], wsa[:]), ["wsa"], ["WoA"])
                S.op("pool", lambda: nc.gpsimd.tensor_copy(WoH[:], wsh[:]), ["wsh"], ["WoH"])
                S.barrier()
            attc = sb("attc", [64, 8, 512], BF16, p4)
            hyc = sb("hyc", [128, 4, 512], BF16, p4)
            sqb = sb("sqb4", [128, 8, 512], F32, p4)
            rs = sb("rs4", [128, 512], F32, p4)
            mixA = sb("mixA", [64, 8, 512], BF16, p4)
            mixH = sb("mixH", [128, 4, 512], BF16, p4)
            h1 = sb("h1", [128, 8, 512], F32, p4)
            mT = sb("mT", [128, 8, 512], BF16, p4)
            Wmi_s = [sb(f"Wmi_s{i}", [128, 8, 1024], BF16, p4) for i in range(2)]
            Wmo_s = [sb(f"Wmo_s{i}", [128, 32, 128], BF16, p4) for i in range(2)]
            fT = sb("fT", [128, 32, 512], BF16, p4)
            rb = [sb(f"rb{i}", [128, 512], F32, p4) for i in range(2)]
            xqv = xTq.rearrange("(k p) t -> p k t", p=128)
            Wmiv = Wmi_d.rearrange("(k p) n -> p k n", p=128)
            outv = outT.rearrange("(k p) t -> p k t", p=128)
            nw = 0
            nwo = 0
            for j in range(4):
                tsl = slice(j * 512, (j + 1) * 512)
                dma(attc[:], att_d[:, tsl].rearrange("(h d) t -> d h t", d=64), ["att_d"], ["attc", "na_src"])
                dma(hyc[:], hy_d[:, tsl].rearrange("(g p) t -> p g t", p=128), ["hy_d"], ["hyc", "nh_src"])
                dma(h1[:], xqv[:, :, tsl], [], ["h1", "n2_src", "nf_src"])
                rms_apply(attc[:], gas, mixA, 1.0 / 512.0, 8, "na", sqb, rs, np_=64)
                rms_apply(hyc[:], ghs, mixH, 1.0 / 512.0, 4, "nh", sqb, rs)
                for ct in range(8):
                    csl = slice(ct * 128, (ct + 1) * 128)
                    for h in range(8):
                        mm(PS[1][:, :], WoA[:, h, csl], mixA[:, h, :], h == 0, False, ["WoA", "na_dst"], ["ps1"])
                    for g in range(4):
                        mm(PS[1][:, :], WoH[:, g, csl], mixH[:, g, :], False, g == 3, ["WoH", "nh_dst"], ["ps1"])
                    S.op("dve", lambda ct=ct: nc.vector.tensor_tensor(h1[:, ct, :], h1[:, ct, :], PS[1][:, :], ALU.add), ["ps1", "h1"], ["h1", "n2_src", "nf_src"])
                rms_apply(h1[:], g2s, mT, 1.0 / D, 8, "n2", sqb, rs)
                for pc in range(4):
                    s_ = nw % 2
                    dma(Wmi_s[s_][:], Wmiv[:, :, pc * 1024:(pc + 1) * 1024], ["Wmid"], [f"Wmi_s{s_}"])
                    nw += 1
                    for ft in range(8):
                        f = pc * 8 + ft
                        pf = PS[2 + f % 2]
                        pfk = f"ps{2 + f % 2}"
                        for k in range(8):
                            mm(pf[:, :], Wmi_s[s_][:, k, ft * 128:(ft + 1) * 128], mT[:, k, :], k == 0, k == 7, [f"Wmi_s{s_}", "n2_dst"], [pfk])
                        r_ = f % 2
                        S.op("act", lambda pf=pf, r_=r_: nc.scalar.activation(rb[r_][:], pf[:, :], AF.Relu), [pfk], [f"rb{r_}"])
                        e = ("pool", "dve")[f % 2]
                        S.op(e, lambda e=e, r_=r_, f=f: V(e).tensor_tensor(fT[:, f, :], rb[r_][:], rb[r_][:], ALU.mult), [f"rb{r_}"], ["fT"])
                for ct in range(8):
                    s_ = nwo % 2
                    dma(Wmo_s[s_][:], Wmo_d[ct], ["Wmod"], [f"Wmo_s{s_}"])
                    nwo += 1
                    po = PS[4 + ct % 2]
                    pok = f"ps{4 + ct % 2}"
                    for f in range(32):
                        mm(po[:, :], Wmo_s[s_][:, f, :], fT[:, f, :], f == 0, f == 31, [f"Wmo_s{s_}", "fT"], [pok])
                    S.op("dve", lambda ct=ct, po=po: nc.vector.tensor_tensor(h1[:, ct, :], h1[:, ct, :], po[:, :], ALU.add), [pok, "h1"], ["h1", "n2_src", "nf_src"])
                rms_apply(h1[:], gfs, h1, 1.0 / D, 8, "nf", sqb, rs)
                dma(outv[:, :, tsl], h1[:], ["h1", "nf_dst"], ["outT"])
            S.barrier()

        if 0 in phases:
            phase0()
        if 1 in phases:
            phase12()
        if 3 in phases:
            phase3()
        if 4 in phases:
            phase4()
        if debug:
            with ExitStack() as pd:
                for name in debug:
                    if name not in ("att_d", "hy_d", "U_d", "H_d0"):
                        continue
                    src_, shp = {"att_d": (att_d, [512, NTOK]), "hy_d": (hy_d, [512, NTOK]), "U_d": (U_d[0:512, :], [512, L]),
                                 "H_d0": (H_d[0:4, :, 0:4].rearrange("a p k r c -> (a p) (k r c)"), [512, 2048])}[name]
                    dbo = nc.dram_tensor("dbg_" + name, shp, BF16, kind="ExternalOutput").ap()
                    dbs = sb("dbs_" + name, [128, 4, shp[1]], BF16, pd)
                    dma(dbs[:], src_.rearrange("(a p) t -> p a t", p=128), [name], ["dbs_" + name])
                    dma(dbo.rearrange("(a p) t -> p a t", p=128), dbs[:], ["dbs_" + name], ["dbg_" + name])
                S.barrier()
        S.barrier()
    return nc, S


def _core_inputs(inp, core):
    T = _tables()
    b, r = core // 4, core % 4
    f = lambda a: np.ascontiguousarray(np.asarray(a, dtype=np.float32))
    x = np.asarray(inp["x"], dtype=np.float32)
    xT = np.ascontiguousarray(x[b].T)
    tq = slice(r * NTOK, (r + 1) * NTOK)
    qm = np.zeros((128, 4), np.float32)
    qm[:, r] = 1.0
    m = {
        "xT": xT, "xTq": np.ascontiguousarray(xT[:, tq]),
        "w_in": f(inp["w_in"][0]), "w_out": f(inp["w_out"][0]), "w_mi": f(inp["w_mlp_in"][0]), "w_mo": f(inp["w_mlp_out"][0]),
        "g1": f(np.asarray(inp["norm1_g"][0]).reshape(8, 128).T), "g2": f(np.asarray(inp["norm2_g"][0]).reshape(8, 128).T),
        "gf": f(np.asarray(inp["final_g"]).reshape(8, 128).T),
        "gq": f(np.asarray(inp["q_norm_g"][0]).reshape(64, 1)), "gk": f(np.asarray(inp["k_norm_g"][0]).reshape(64, 1)),
        "ga": f(np.asarray(inp["attn_out_g"][0]).reshape(8, 64).T), "gh": f(np.asarray(inp["hy_out_g"][0]).reshape(4, 128).T),
        "cw": f(np.asarray(inp["hy_conv_w"][0]).T.reshape(12, 128, 3).transpose(1, 0, 2)),
        "cb": f(np.asarray(inp["hy_conv_b"][0]).reshape(12, 128).T),
        "fw1": f(inp["filt_w1"][0]), "fw2": f(inp["filt_w2"][0]), "fw3": f(inp["filt_w3"][0]), "fw4": f(inp["filt_w4"][0]),
        "fb": f(np.stack([np.asarray(inp["filt_b1"][0]), np.asarray(inp["filt_b2"][0]), np.asarray(inp["filt_b3"][0])], axis=1)),
        "ffreq": f(np.asarray(inp["filt_freq"][0]).reshape(64, 1)),
        "fdel": f(np.asarray(inp["filt_deltas"][0]).reshape(16, 128).T),
        "skp": f(np.asarray(inp["hy_skip_d"][0]).reshape(2, 4, 128).transpose(2, 0, 1).reshape(128, 8)),
        "qmask": qm,
        "c_cos": T["cosT"], "c_sin": T["sinT"],
        "c_cosq": np.ascontiguousarray(T["cosT"][:, tq]), "c_sinq": np.ascontiguousarray(T["sinT"][:, tq]),
        "c_zT": np.ascontiguousarray(np.stack([T["zT"], T["zrT"]])), "c_t": np.ascontiguousarray(np.stack([T["t"], T["trev"]])),
        "c_tb": T["tb"], "c_jt": T["jt"],
        "c_fpack": T["fpack"], "c_G": T["G"], "c_finv": T["finv"], "c_M2": T["M2"],
        "c_M2q": np.ascontiguousarray(T["M2"][:, :, r * 16:(r + 1) * 16]),
        "c_rot": T["rot"], "c_sel": T["sel"], "c_ones": T["ones"], "c_ident": T["ident"],
    }
    return m


def _emit(nc, S):
    from contextlib import ExitStack
    st = ExitStack()
    sems = {e: [st.enter_context(nc.semaphore(f"s_{e}{i}")) for i in range(4)] for e in S.CE}
    dsems = [st.enter_context(nc.semaphore(f"d{i}")) for i in range(S.NDS)]
    info = S.emit(sems, dsems)
    return st, info


def kernel(**inputs):
    nc, S = build_program()
    st, _ = _emit(nc, S)
    with st:
        in_maps = [_core_inputs(inputs, c) for c in range(8)]
        res = run_bass_kernel_spmd(nc, in_maps, core_ids=list(range(8)))
    out = np.empty((NB, L, D), np.float32)
    for c in range(8):
        b, r = c // 4, c % 4
        out[b, r * NTOK:(r + 1) * NTOK, :] = np.asarray(res.results[c]["outT"]).T
    return out
```

```python
import math
import numpy as np
import ml_dtypes
import concourse.bass as bass
import concourse.mybir as mybir
from concourse.bass_utils import run_bass_kernel_spmd

F32 = mybir.dt.float32
BF16 = mybir.dt.bfloat16
ALU = mybir.AluOpType
AF = mybir.ActivationFunctionType
bf16_np = ml_dtypes.bfloat16

D = 1024
L = 8192
NB = 2
NTOK = 2048
HD = 64
NQH = 8
NKV = 2
HYW = 512
DFF = 4096
EPS = 1e-6
NFFT = 2 * L
TWO_PI = 2.0 * math.pi
MAGIC = 12582912.0


class Op:
    __slots__ = ("eng", "fn", "reads", "writes", "dma", "idx", "gid", "waits", "signal", "clock",
                 "sem", "val")


class Sched:
    CE = ("pe", "act", "dve", "pool", "sp")
    EPOCH = 12000
    NDS = 40

    def __init__(self, nc):
        self.nc = nc
        self.E = {"pe": nc.tensor, "act": nc.scalar, "dve": nc.vector, "pool": nc.gpsimd,
                  "sp": nc.sync}
        self.ops = []
        self.last_w = {}
        self.readers = {}
        self.known = {e: {} for e in self.CE}
        self.known_dma = {e: set() for e in self.CE}
        self.cnt = {e: 0 for e in self.CE}
        self.last_op = {e: None for e in self.CE}
        self.live_dma = []

    def op(self, eng, fn, reads=(), writes=(), dma=False):
        o = Op()
        o.eng, o.fn, o.reads, o.writes, o.dma = eng, fn, tuple(reads), tuple(writes), dma
        o.signal = dma
        o.sem = None
        o.val = None
        o.gid = len(self.ops)
        deps = {}
        for k in o.reads:
            w = self.last_w.get(k)
            if w is not None:
                deps[w.gid] = w
        for k in o.writes:
            w = self.last_w.get(k)
            if w is not None:
                deps[w.gid] = w
            for r in self.readers.get(k, ()):
                deps[r.gid] = r
        self._finish(o, deps.values())
        for k in o.writes:
            self.last_w[k] = o
            self.readers[k] = []
        for k in o.reads:
            lst = self.readers.setdefault(k, [])
            if not dma:
                lst[:] = [r for r in lst if r.dma or r.eng != eng]
            lst.append(o)
        return o

    def _finish(self, o, deps):
        eng = o.eng
        kn = self.known[eng]
        kd = self.known_dma[eng]
        waits = []
        best = {}
        rset = set(o.reads)
        for d in deps:
            if d.dma:
                if d.gid not in kd:
                    waits.append(d)
                    kd.add(d.gid)
                continue
            if d.eng == eng:
                if eng in ("pe", "sp"):
                    continue
            if kn.get(d.eng, -1) >= d.idx:
                continue
            b = best.get(d.eng)
            if b is None or b.idx < d.idx:
                best[d.eng] = d
        for d in best.values():
            waits.append(d)
            if kn.get(d.eng, -1) < d.idx:
                kn[d.eng] = d.idx
            for e2, i2 in d.clock.items():
                if kn.get(e2, -1) < i2:
                    kn[e2] = i2
        for d in waits:
            d.signal = True
        o.waits = waits
        o.idx = self.cnt[eng]
        self.cnt[eng] += 1
        o.clock = dict(kn)
        self.ops.append(o)
        if o.fn is not None:
            self.last_op[eng] = o
        if o.dma:
            self.live_dma.append(o)

    def barrier(self):
        lasts = [self.last_op[e] for e in self.CE if self.last_op[e] is not None]
        dmas = list(self.live_dma)
        for e in self.CE:
            o = Op()
            o.eng, o.fn, o.reads, o.writes, o.dma = e, None, (), (), False
            o.signal = False
            o.sem = None
            o.val = None
            o.gid = len(self.ops)
            deps = [d for d in lasts if not d.dma and d.eng != e] + dmas
            self._finish(o, deps)
        self.live_dma = []

    def emit(self, sems, dsems):
        sig = {e: 0 for e in self.CE}
        nd = 0
        for o in self.ops:
            E = self.E[o.eng]
            for d in o.waits:
                E.wait_ge(d.sem, d.val)
            if o.fn is None:
                continue
            if o.dma:
                s = dsems[nd % self.NDS]
                rnd = nd // self.NDS
                if rnd > 0:
                    E.wait_ge(s, 16 * rnd)
                ins = o.fn()
                ins.then_inc(s, 16)
                o.sem, o.val = s, 16 * (rnd + 1)
                nd += 1
            else:
                ins = o.fn()
                if o.signal:
                    c = sig[o.eng]
                    o.sem = sems[o.eng][c // self.EPOCH]
                    o.val = c % self.EPOCH + 1
                    ins.then_inc(o.sem, 1)
                    sig[o.eng] = c + 1
        return sig, nd


_TABLES = None


def _rope_tables():
    rows = L // 64
    row = np.repeat(np.arange(rows, dtype=np.float32), 64)
    col = np.tile(np.arange(64, dtype=np.float32), rows)
    half = HD // 2
    inv_freq = (np.float32(10000.0) ** (-np.arange(0, half, 2, dtype=np.float32) / np.float32(half))).astype(np.float32)
    ang_r = (row[:, None] * inv_freq[None, :]).astype(np.float32)
    ang_c = (col[:, None] * inv_freq[None, :]).astype(np.float32)
    cos_r, sin_r = np.cos(ang_r), np.sin(ang_r)
    cos_c, sin_c = np.cos(ang_c), np.sin(ang_c)
    cosT = np.concatenate([cos_r.T, cos_r.T, cos_c.T, cos_c.T], axis=0).astype(np.float32)
    sinT = np.concatenate([sin_r.T, sin_r.T, sin_c.T, sin_c.T], axis=0).astype(np.float32)
    return np.ascontiguousarray(cosT), np.ascontiguousarray(sinT)


def _filter_pos_tables():
    f32 = np.float32
    t = np.linspace(0.0, 1.0, L, dtype=f32)
    bands = 16
    w_ang = (f32(2.0 * math.pi) * np.arange(L, dtype=f32) / f32(L)).astype(f32)
    band_f = np.linspace(1e-4, bands - 1, bands, dtype=f32)
    ang = (w_ang[:, None] * band_f[None, :]).astype(f32)
    z = np.concatenate([t[:, None], np.cos(ang), -np.sin(ang)], axis=-1).astype(f32)
    zT = np.ascontiguousarray(z.T)
    idx = (L - np.arange(L)) % L
    zrT = np.ascontiguousarray(z[idx].T)
    trev = t[idx].copy()
    return zT, zrT, t.copy(), trev


def _fft_tables():
    N = NFFT
    n1 = np.arange(128, dtype=np.float64)
    k1 = np.arange(64, dtype=np.float64)
    th = 2 * np.pi * np.outer(n1, k1 + 0.5) / 128.0
    fpack = np.concatenate([np.cos(th), -np.sin(th), np.sin(th)], axis=1)
    n2 = np.arange(128, dtype=np.float64)
    k2 = np.arange(128, dtype=np.float64)
    kap = k1[None, :, None] + 128.0 * k2[None, None, :] + 0.5
    thg = 2 * np.pi * n2[:, None, None] * kap / N
    G = np.stack([np.cos(thg), -np.sin(thg)], axis=2)
    thf = 2 * np.pi * np.outer(k2, n2) / 128.0
    finv = np.stack([np.cos(thf), np.sin(thf), -np.sin(thf)], axis=1)
    n1h = np.arange(64, dtype=np.float64)
    phi = 2 * np.pi * (k1[:, None, None] + 0.5) * (128.0 * n1h[None, None, :] + n2[None, :, None]) / N
    M2 = np.concatenate([(2.0 / N) * np.cos(phi), -(2.0 / N) * np.sin(phi)], axis=0)
    return (fpack.astype(bf16_np), G.astype(bf16_np), finv.astype(bf16_np), M2.astype(bf16_np))


def _tables():
    global _TABLES
    if _TABLES is None:
        cosT, sinT = _rope_tables()
        zT, zrT, t, trev = _filter_pos_tables()
        fpack, G, finv, M2 = _fft_tables()
        ii = np.arange(512, dtype=np.float64)
        jj = np.arange(16, dtype=np.float64)
        tbm = np.stack([ii / (L - 1), (L - ii) / (L - 1)]).astype(np.float32)
        jtm = np.stack([512.0 * jj / (L - 1), -512.0 * jj / (L - 1)]).astype(np.float32)
        rot = np.zeros((64, 64), np.float32)
        for m in range(64):
            if (m % 32) < 16:
                rot[m + 16, m] = -1.0
            else:
                rot[m - 16, m] = 1.0
        sel = np.zeros((65, 64), np.float32)
        sel[64, :] = 1.0
        _TABLES = dict(cosT=cosT, sinT=sinT, zT=zT, zrT=zrT, t=t, trev=trev, fpack=fpack, G=G,
                       finv=finv, M2=M2, rot=rot, sel=sel, tb=tbm, jt=jtm,
                       ones=np.ones((128, 128), np.float32),
                       ident=np.eye(128, dtype=np.float32).astype(bf16_np))
    return _TABLES


def build_program(phases=(0, 1, 2, 3, 4), debug=()):
    nc = bass.Bass("TRN2", target_bir_lowering=False)
    S = Sched(nc)

    def din(name, shape, dt=F32):
        return nc.dram_tensor(name, list(shape), dt, kind="ExternalInput").ap()

    def dscr(name, shape, dt):
        return nc.dram_tensor(name, list(shape), dt, kind="Internal").ap()

    xT = din("xT", [D, L])
    xTq = din("xTq", [D, NTOK])
    w_in = din("w_in", [D, 2304])
    w_out = din("w_out", [D, D])
    w_mi = din("w_mi", [D, DFF])
    w_mo = din("w_mo", [DFF, D])
    g1 = din("g1", [128, 8]); g2 = din("g2", [128, 8]); gf = din("gf", [128, 8])
    gq = din("gq", [64, 1]); gk = din("gk", [64, 1])
    ga = din("ga", [64, 8]); gh = din("gh", [128, 4])
    cw = din("cw", [128, 12, 3]); cb = din("cb", [128, 12])
    fw1 = din("fw1", [33, 64]); fw2 = din("fw2", [64, 64]); fw3 = din("fw3", [64, 64])
    fw4 = din("fw4", [64, 2048])
    fb = din("fb", [64, 3]); ffreq = din("ffreq", [64, 1])
    fdel = din("fdel", [128, 16])
    skp = din("skp", [128, 8])
    qmask = din("qmask", [128, 4])
    c_cos = din("c_cos", [64, L]); c_sin = din("c_sin", [64, L])
    c_cosq = din("c_cosq", [64, NTOK]); c_sinq = din("c_sinq", [64, NTOK])
    c_zT = din("c_zT", [2, 33, L])
    c_t = din("c_t", [2, L])
    c_tb = din("c_tb", [2, 512])
    c_jt = din("c_jt", [2, 16])
    c_fpack = din("c_fpack", [128, 192], BF16)
    c_G = din("c_G", [128, 64, 2, 128], BF16)
    c_finv = din("c_finv", [128, 3, 128], BF16)
    c_M2 = din("c_M2", [128, 128, 64], BF16)
    c_M2q = din("c_M2q", [128, 128, 16], BF16)
    c_rot = din("c_rot", [64, 64]); c_sel = din("c_sel", [65, 64]); c_ones = din("c_ones", [128, 128])
    c_ident = din("c_ident", [128, 128], BF16)
    outT = nc.dram_tensor("outT", [D, NTOK], F32, kind="ExternalOutput").ap()

    U_d = dscr("U_d", [1536, L], BF16)
    H_d = dscr("H_d", [8, 128, 64, 4, 128], BF16)
    taps_d = dscr("taps_d", [128, NFFT], BF16)
    zcm_d = dscr("zcm_d", [128, L], BF16)
    att_d = dscr("att_d", [512, NTOK], BF16)
    hy_d = dscr("hy_d", [512, NTOK], BF16)
    Wmi_d = dscr("Wmi_d", [D, DFF], BF16)
    Wmo_d = dscr("Wmo_d", [8, 128, 32, 128], BF16)

    from contextlib import ExitStack
    es = ExitStack()

    _names = {}

    def sb(name, shape, dt, stack=None):
        n_ = _names.get(name, 0)
        _names[name] = n_ + 1
        nm = name if n_ == 0 else f"{name}_r{n_}"
        return (stack or es).enter_context(nc.sbuf_tensor(nm, list(shape), dt))

    def dma(out, in_, reads, writes, eng="sp"):
        return S.op(eng, lambda: S.E[eng].dma_start(out=out, in_=in_), reads, writes, dma=True)

    def mm(out, lhsT, rhs, start, stop, reads, writes):
        return S.op("pe", lambda: nc.tensor.matmul(out, lhsT, rhs, start=start, stop=stop), reads, writes)

    rr = {"i": 0}

    def alt(engs=("dve", "pool")):
        rr["i"] += 1
        return engs[rr["i"] % len(engs)]

    def V(eng):
        return S.E[eng]

    with es:
        PS = [es.enter_context(nc.psum_tensor(f"ps{i}", [128, 512], F32)) for i in range(6)]
        PSB = [es.enter_context(nc.psum_tensor(f"psb{i}", [128, 1024], BF16)) for i in range(2)]

        ones = sb("ones", [128, 128], F32)
        ident = sb("ident", [128, 128], BF16)
        rot = sb("rot", [64, 64], F32)
        sel = sb("sel", [65, 64], F32)
        epsT = sb("epsT", [128, 1], F32)
        g1s = sb("g1s", [128, 8], F32); g2s = sb("g2s", [128, 8], F32); gfs = sb("gfs", [128, 8], F32)
        gqs = sb("gqs", [64, 1], F32); gks = sb("gks", [64, 1], F32)
        gas = sb("gas", [64, 8], F32); ghs = sb("ghs", [128, 4], F32)
        cws = sb("cws", [128, 12, 3], F32); cbs = sb("cbs", [128, 12], F32)
        skps = sb("skps", [128, 8], F32); qms = sb("qms", [128, 4], F32)
        for t_, d_ in ((ones, c_ones), (ident, c_ident), (rot, c_rot), (sel, c_sel), (g1s, g1), (g2s, g2),
                       (gfs, gf), (gqs, gq), (gks, gk), (gas, ga), (ghs, gh), (cws, cw), (cbs, cb),
                       (skps, skp), (qms, qmask)):
            dma(t_[:], d_, [], [t_.name if hasattr(t_, "name") else id(t_)])
        KEY = lambda t_: t_.name if hasattr(t_, "name") else id(t_)
        S.op("dve", lambda: nc.vector.memset(epsT[:], EPS), [], ["epsT"])

        FC = {}
        SNAP = {}

        def snap(name, src_ap, key, ncols):
            if name not in debug or name in SNAP:
                return
            SNAP[name] = nc.dram_tensor("dbg_" + name, [128, ncols], BF16, kind="ExternalOutput").ap()
            dma(SNAP[name], src_ap, [key], ["dbg_" + name])


        def load_fft_consts(stack):
            FC["fpack"] = sb("fpack", [128, 192], BF16, stack)
            FC["finv"] = sb("finv", [128, 3, 128], BF16, stack)
            FC["Gq"] = [sb(f"Gq{i}", [128, 8, 2, 128], BF16, stack) for i in range(2)]
            dma(FC["fpack"][:], c_fpack, [], ["fpack"])
            dma(FC["finv"][:], c_finv, [], ["finv"])

        def rsqrt(out, in_, scale, reads, writes, np_=128):
            S.op("act", lambda: nc.scalar.activation(out, in_, AF.Sqrt, bias=epsT[0:np_, :], scale=scale),
                 list(reads) + ["epsT"], writes)
            S.op("dve", lambda: nc.vector.reciprocal(out, out), writes, writes)

        def fft_forward(ztm, K, AT, cb_batch, zkey="ztm", akey="AT", pre_batch=None):
            fpack = FC["fpack"]
            for c2 in range(64):
                ps = PS[c2 % 2]
                pk = f"ps{c2 % 2}"
                for cc in range(2):
                    c = c2 * 2 + cc
                    mm(ps[:, cc * 192:(cc + 1) * 192], ztm[0:K, c, :], fpack[0:K, :], True, True,
                       [(zkey, c // 32), "fpack"], [pk])
                e = alt(("act", "dve"))
                src = ps[:, 0:384].rearrange("p (cc j) -> p j cc", cc=2)
                dst = AT[:, :, c2 * 2:c2 * 2 + 2]
                if e == "act":
                    S.op("act", lambda dst=dst, src=src: nc.scalar.copy(dst, src), [pk], [akey])
                else:
                    S.op("dve", lambda dst=dst, src=src: nc.vector.tensor_copy(dst, src), [pk], [akey])

            def load_g(q8):
                dma(FC["Gq"][q8 % 2][:], c_G[:, q8 * 8:(q8 + 1) * 8], [], [f"Gq{q8 % 2}"])

            load_g(0)
            for kb in range(32):
                q4 = kb // 4
                Gq = FC["Gq"][q4 % 2]
                gkey = f"Gq{q4 % 2}"
                if kb % 4 == 0 and q4 + 1 < 8:
                    load_g(q4 + 1)
                if pre_batch is not None:
                    pre_batch(kb)
                ps = PS[2 + kb % 2]
                pk = f"ps{2 + kb % 2}"
                for kk in range(2):
                    k1 = kb * 2 + kk
                    kl = k1 % 8
                    R = AT[:, k1, :]
                    I = AT[:, 64 + k1, :]
                    NI = AT[:, 128 + k1, :]
                    zr = ps[:, kk * 256:kk * 256 + 128]
                    zi = ps[:, kk * 256 + 128:kk * 256 + 256]
                    mm(zr, Gq[:, kl, 0, :], R, True, False, [akey, gkey], [pk])
                    mm(zr, Gq[:, kl, 1, :], NI, False, True, [akey, gkey], [pk])
                    mm(zi, Gq[:, kl, 1, :], R, True, False, [akey, gkey], [pk])
                    mm(zi, Gq[:, kl, 0, :], I, False, True, [akey, gkey], [pk])
                cb_batch(kb, ps, pk)

        def phase0():
          with ExitStack() as p0w:
            wst = [sb(f"wst{i}", [128, 8, 512], F32, p0w) for i in range(2)]
            wbf = [sb(f"wbf{i}", [128, 8, 512], BF16, p0w) for i in range(2)]
            n = 0
            jobs = []
            for cbk in range(8):
                jobs.append((w_mi.rearrange("(k p) n -> p k n", p=128)[:, :, cbk * 512:(cbk + 1) * 512],
                             Wmi_d.rearrange("(k p) n -> p k n", p=128)[:, :, cbk * 512:(cbk + 1) * 512]))
            for kr in range(4):
                for cbk in range(2):
                    jobs.append((w_mo.rearrange("(k p) n -> p k n", p=128)[:, kr * 8:(kr + 1) * 8, cbk * 512:(cbk + 1) * 512],
                                 (kr, cbk)))
            for src, dst in jobs:
                s_ = n % 2
                dma(wst[s_][:], src, [], [f"wst{s_}"])
                e = ("dve", "pool", "act")[n % 3]
                if e == "act":
                    S.op("act", lambda s_=s_: nc.scalar.copy(wbf[s_][:], wst[s_][:]), [f"wst{s_}"], [f"wbf{s_}"])
                else:
                    S.op(e, lambda s_=s_, e=e: V(e).tensor_copy(wbf[s_][:], wst[s_][:]), [f"wst{s_}"], [f"wbf{s_}"])
                if n < 8:
                    dma(dst, wbf[s_][:], [f"wbf{s_}"], ["Wmid"])
                else:
                    kr_, cbk_ = dst
                    for c_ in range(4):
                        dma(Wmo_d[cbk_ * 4 + c_, :, kr_ * 8:(kr_ + 1) * 8, :], wbf[s_][:, :, c_ * 128:(c_ + 1) * 128], [f"wbf{s_}"], ["Wmod"])
                n += 1
            S.barrier()
          with ExitStack() as p0:
            load_fft_consts(p0)

            w1s = sb("w1s", [33, 64], F32, p0); w2s = sb("w2s", [64, 64], F32, p0); w3s = sb("w3s", [64, 64], F32, p0)
            w4b = sb("w4b", [64, 2048], BF16, p0)
            fbs = sb("fbs", [64, 3], F32, p0); frs = sb("frs", [64, 1], F32, p0); ffb = sb("ffb", [64, 3], F32, p0)
            dls = sb("dls", [128, 16], F32, p0); ndl = sb("ndl", [128, 16], F32, p0)
            negpi = sb("negpi", [128, 1], F32, p0)
            H3 = sb("H3", [64, 2, L], BF16, p0)
            p0m = ExitStack()
            w4s = sb("w4s", [64, 2048], F32, p0m)
            for t_, d_, k_ in ((w1s, fw1, "w1s"), (w2s, fw2, "w2s"), (w3s, fw3, "w3s"), (w4s, fw4, "w4s"),
                               (fbs, fb, "fbs"), (frs, ffreq, "frs"), (dls, fdel, "dls")):
                dma(t_[:], d_, [], [k_])
            S.op("dve", lambda: nc.vector.tensor_copy(w4b[:], w4s[:]), ["w4s"], ["w4b"])
            S.op("dve", lambda: nc.vector.tensor_scalar(ffb[:], fbs[:], frs[:, 0:1], None, ALU.mult), ["fbs", "frs"], ["ffb"])
            S.op("dve", lambda: nc.vector.tensor_scalar(ndl[:], dls[:], -1.0, None, ALU.mult), ["dls"], ["ndl"])
            S.op("dve", lambda: nc.vector.tensor_tensor(ndl[:], ndl[:], dls[:], ALU.min), ["dls", "ndl"], ["ndl"])
            S.op("dve", lambda: nc.vector.memset(negpi[:], 0.0), [], ["negpi"])
            zc = [sb(f"zc{i}", [33, 512], F32, p0m) for i in range(4)]
            faL = [sb(f"fa{i}", [64, 512], F32, p0m) for i in range(2)]
            fbufL = [sb(f"fbuf{i}", [64, 512], F32, p0m) for i in range(2)]
            fhL = [sb(f"fh{i}", [64, 512], F32, p0m) for i in range(2)]

            def sin_layer(ps, li, dst, pk, v_, dkey):
                fa, fbuf = faL[v_], fbufL[v_]
                fak, fbk = f"fa{v_}", f"fbuf{v_}"
                S.op("dve", lambda: nc.vector.tensor_scalar(fa[:], ps[0:64, :], frs[:, 0:1], ffb[:, li:li + 1], ALU.mult, ALU.add),
                     [pk, "frs", "ffb"], [fak])
                S.op("dve", lambda: nc.vector.tensor_scalar(fbuf[:], fa[:], 1.0 / TWO_PI, MAGIC, ALU.mult, ALU.add), [fak], [fbk])
                S.op("dve", lambda: nc.vector.tensor_scalar(fbuf[:], fbuf[:], -MAGIC, -TWO_PI, ALU.add, ALU.mult), [fbk], [fbk])
                S.op("dve", lambda: nc.vector.tensor_tensor(fa[:], fa[:], fbuf[:], ALU.add), [fak, fbk], [fak])
                S.op("act", lambda: nc.scalar.activation(dst, fa[:], AF.Sin, bias=negpi[0:64, :], scale=1.0), [fak, "negpi"], [dkey])

            for j in range(16):
                for var in range(2):
                    s_ = (j % 2) * 2 + var
                    psm = PS[4 + var]
                    pmk = f"ps{4 + var}"
                    fh = fhL[var]
                    fhk = f"fh{var}"
                    dma(zc[s_][:], c_zT[var, :, j * 512:(j + 1) * 512], [], [f"zc{s_}"])
                    mm(psm[0:64, :], w1s[:], zc[s_][:], True, True, [f"zc{s_}", "w1s"], [pmk])
                    sin_layer(psm, 0, fh[:], pmk, var, fhk)
                    mm(psm[0:64, :], w2s[:], fh[:], True, True, [fhk, "w2s"], [pmk])
                    sin_layer(psm, 1, fh[:], pmk, var, fhk)
                    mm(psm[0:64, :], w3s[:], fh[:], True, True, [fhk, "w3s"], [pmk])
                    sin_layer(psm, 2, H3[:, var, j * 512:(j + 1) * 512], pmk, var, ("H3", var))

            S.barrier()
            p0m.close()
            tapsU = sb("tapsU", [128, NFFT], BF16, p0)
            ztmf = sb("ztmf", [128, 128, 128], BF16, p0)
            ATf = sb("ATf", [128, 192, 128], BF16, p0)
            tb = sb("tb", [128, 2, 512], F32, p0)
            jt = sb("jt", [128, 2, 16], F32, p0)
            for hf in range(2):
                dma(tb[:, hf, :], c_tb[hf:hf + 1, :].broadcast_to([128, 512]), [], ["tb"])
                dma(jt[:, hf, :], c_jt[hf:hf + 1, :].broadcast_to([128, 16]), [], ["jt"])
            wbase = [sb(f"wbase{i}", [128, 512], F32, p0) for i in range(2)]
            wcj = [sb(f"wcj{i}", [128, 16], F32, p0) for i in range(2)]
            tpf = [sb(f"tpf{i}", [128, 512], F32, p0) for i in range(2)]
            psum_ = sb("psum_", [128, 32], F32, p0)
            nrm = sb("nrm", [128, 2], F32, p0)
            Hst = [sb(f"Hst{i}", [128, 2, 4, 128], BF16, p0) for i in range(4)]
            for go in range(8):
                g, o = go // 2, go % 2
                for half in range(2):
                    col0 = o * 1024 + half * 512 + g * 128
                    ct = col0 // 128
                    S.op("act", lambda half=half, ct=ct: nc.scalar.activation(wbase[half][:], tb[:, half, :], AF.Exp, scale=ndl[:, ct:ct + 1]),
                         ["tb", "ndl"], [f"wbase{half}"])
                    S.op("act", lambda half=half, ct=ct: nc.scalar.activation(wcj[half][:], jt[:, half, :], AF.Exp, scale=ndl[:, ct:ct + 1]),
                         ["jt", "ndl"], [f"wcj{half}"])
                    for j in range(16):
                        s_ = j % 2
                        pst = PS[4 + s_]
                        ptk = f"ps{4 + s_}"
                        mm(pst[:, :], w4b[:, col0:col0 + 128], H3[:, half, j * 512:(j + 1) * 512], True, True,
                           [("H3", half), "w4b"], [ptk])
                        S.op("dve", lambda s_=s_, pst=pst, half=half, j=j: nc.vector.scalar_tensor_tensor(
                            out=tpf[s_][:], in0=pst[:, :], scalar=wcj[half][:, j:j + 1], in1=wbase[half][:], op0=ALU.mult, op1=ALU.mult),
                            [ptk, f"wcj{half}", f"wbase{half}"], [f"tpf{s_}"])
                        idx = half * 16 + j
                        if half == 1 and j == 0:
                            S.op("dve", lambda s_=s_: nc.vector.memset(tpf[s_][:, 0:1], 0.0), [f"tpf{s_}"], [f"tpf{s_}"])
                        S.op("dve", lambda idx=idx, s_=s_: nc.vector.tensor_reduce(psum_[:, idx:idx + 1], tpf[s_][:], mybir.AxisListType.X, ALU.add,
                                                                                  apply_absolute_value=True), [f"tpf{s_}"], ["psum_"])
                        S.op("pool", lambda half=half, j=j, s_=s_: nc.gpsimd.tensor_copy(tapsU[:, half * L + j * 512: half * L + (j + 1) * 512], tpf[s_][:]),
                             [f"tpf{s_}"], ["tapsU"])
                S.op("dve", lambda: nc.vector.tensor_reduce(nrm[:, 0:1], psum_[:], mybir.AxisListType.X, ALU.add), ["psum_"], ["nrm"])
                S.op("dve", lambda: nc.vector.reciprocal(nrm[:, 0:1], nrm[:, 0:1]), ["nrm"], ["nrm"])
                S.op("dve", lambda: nc.vector.tensor_scalar(nrm[:, 1:2], nrm[:, 0:1], -1.0, None, ALU.mult), ["nrm"], ["nrm"])
                S.op("dve", lambda: nc.vector.tensor_scalar(tapsU[:, 0:L], tapsU[:, 0:L], nrm[:, 0:1], None, ALU.mult), ["tapsU", "nrm"], ["tapsU"])
                S.op("pool", lambda: nc.gpsimd.tensor_scalar(tapsU[:, L:NFFT], tapsU[:, L:NFFT], nrm[:, 1:2], None, ALU.mult), ["tapsU", "nrm"], ["tapsU"])
                tv = taps_d.rearrange("c (n1 n2) -> n1 c n2", n2=128)
                for q4 in range(4):
                    dma(taps_d[q4 * 32:(q4 + 1) * 32, :], tapsU[q4 * 32:(q4 + 1) * 32, :], ["tapsU"], [("taps_d", q4)])
                for q4 in range(4):
                    dma(ztmf[:, q4 * 32:(q4 + 1) * 32, :], tv[:, q4 * 32:(q4 + 1) * 32, :], [("taps_d", q4)], [("ztm", q4)])

                def cb_filter(kb, ps, pk, go=go):
                    s_ = kb % 4
                    zv = ps[:, :].rearrange("p (k r c) -> p k r c", k=2, r=2)
                    S.op("act", lambda: nc.scalar.copy(Hst[s_][:, :, 0:2, :], zv), [pk], [f"Hst{s_}"])
                    S.op("dve", lambda: nc.vector.tensor_copy(Hst[s_][:, :, 2, :], zv[:, :, 1, :]), [pk], [f"Hst{s_}"])
                    S.op("dve", lambda: nc.vector.tensor_copy(Hst[s_][:, :, 3, :], zv[:, :, 0, :]), [pk], [f"Hst{s_}"])
                    dma(H_d[go, :, kb * 2:kb * 2 + 2], Hst[s_][:], [f"Hst{s_}"], [("H_d", go)])

                fft_forward(ztmf, 128, ATf, cb_filter)
            S.barrier()
        def rms_apply(src, gvec, dst, scale, nk, tag, sqb, rs, np_=128, ss_ps=0):
            S.op("act", lambda: nc.scalar.activation(sqb[0:np_, 0:nk, :], src, AF.Square), [tag + "_src"], ["sqb"])
            for k in range(nk):
                mm(PS[ss_ps][0:np_, :], ones[0:np_, 0:np_], sqb[0:np_, k, :], k == 0, k == nk - 1, ["sqb", "ones"], [f"ps{ss_ps}"])
            S.op("act", lambda: nc.scalar.activation(rs[0:np_, :], PS[ss_ps][0:np_, :], AF.Sqrt, bias=epsT[0:np_, :], scale=scale),
                 [f"ps{ss_ps}", "epsT"], ["rs"])
            S.op("dve", lambda: nc.vector.reciprocal(rs[0:np_, :], rs[0:np_, :]), ["rs"], ["rs"])
            for k in range(nk):
                e = "dve"
                S.op(e, lambda k=k, e=e: V(e).scalar_tensor_tensor(out=dst[:, k, :], in0=src[:, k, :], scalar=gvec[:, k:k + 1],
                                                                  in1=rs[0:np_, :], op0=ALU.mult, op1=ALU.mult),
                     [tag + "_src", "rs", "gvecs"], [tag + "_dst"])

        def phase12():
          with ExitStack() as p12:
            KT = sb("KT", [64, 2, L], BF16, p12)
            Vx = sb("Vx", [128, 64, 2, 65], BF16, p12)
            QT = sb("QT", [64, 8, NTOK], BF16, p12)
            S.op("pool", lambda: nc.gpsimd.memset(Vx[:, :, :, 64:65], 1.0), [], ["Vx1"])
            with ExitStack() as p1:
                Wb = sb("Wb", [128, 8, 2304], BF16, p1)
                with ExitStack() as p1c:
                    wst = [sb(f"wsi{i}", [128, 8, 384], F32, p1c) for i in range(2)]
                    wv = w_in.rearrange("(k p) n -> p k n", p=128)
                    for bk in range(6):
                        s_ = bk % 2
                        dma(wst[s_][:], wv[:, :, bk * 384:(bk + 1) * 384], [], [f"wst{s_}"])
                        e = alt()
                        S.op(e, lambda s_=s_, e=e, bk=bk: V(e).tensor_copy(Wb[:, :, bk * 384:(bk + 1) * 384], wst[s_][:]),
                             [f"wst{s_}"], ["Wb"])
                    S.barrier()
                xin = [sb(f"xin{i}", [128, 8, 512], F32, p1) for i in range(2)]
                sqb = sb("sqb", [128, 8, 512], F32, p1)
                rs = sb("rs", [128, 512], F32, p1)
                aT = sb("aT", [128, 8, 512], BF16, p1)
                ust = sb("ust", [128, 12, 512], BF16, p1)
                hsq = sb("hsq", [64, 512], F32, p1); hrs = sb("hrs", [64, 512], F32, p1); hkn = sb("hkn", [64, 512], F32, p1)
                ht1 = sb("ht1", [64, 512], F32, p1); ht2 = sb("ht2", [64, 512], F32, p1)
                csb = [sb(f"csb{i}", [64, 2, 512], F32, p1) for i in range(2)]

                def headproc(gvec, cs, cskey, dst, dstkey):
                    src = PS[1][0:64, :]
                    S.op("act", lambda: nc.scalar.activation(hsq[:], src, AF.Square), ["ps1"], ["hsq"])
                    mm(PS[2][0:64, :], ones[0:64, 0:64], hsq[:], True, True, ["hsq", "ones"], ["ps2"])
                    S.op("act", lambda: nc.scalar.activation(hrs[:], PS[2][0:64, :], AF.Sqrt, bias=epsT[0:64, :], scale=1.0 / 64.0),
                         ["ps2", "epsT"], ["hrs"])
                    S.op("dve", lambda: nc.vector.reciprocal(hrs[:], hrs[:]), ["hrs"], ["hrs"])
                    S.op("dve", lambda: nc.vector.scalar_tensor_tensor(out=hkn[:], in0=src, scalar=gvec[:, 0:1], in1=hrs[:],
                                                                      op0=ALU.mult, op1=ALU.mult), ["ps1", "hrs", "gvecs"], ["hkn"])
                    mm(PS[3][0:64, :], rot[:], hkn[:], True, True, ["hkn", "rot"], ["ps3"])
                    S.op("pool", lambda: nc.gpsimd.tensor_tensor(ht1[:], hkn[:], cs[:, 0, :], ALU.mult), ["hkn", cskey], ["ht1"])
                    S.op("dve", lambda: nc.vector.tensor_tensor(ht2[:], PS[3][0:64, :], cs[:, 1, :], ALU.mult), ["ps3", cskey], ["ht2"])
                    S.op("pool", lambda: nc.gpsimd.tensor_tensor(dst, ht1[:], ht2[:], ALU.add), ["ht1", "ht2"], [dstkey])

                xv = xT.rearrange("(k p) t -> p k t", p=128)
                Uv = U_d.rearrange("(ct p) t -> p ct t", p=128)
                for j in range(16):
                    s_ = j % 2
                    tsl = slice(j * 512, (j + 1) * 512)
                    dma(xin[s_][:], xv[:, :, tsl], [], [f"xin{s_}", "n1_src"])
                    dma(csb[s_][:, 0, :], c_cos[:, tsl], [], [f"csb{s_}"])
                    dma(csb[s_][:, 1, :], c_sin[:, tsl], [], [f"csb{s_}"])
                    rms_apply(xin[s_][:], g1s, aT, 1.0 / D, 8, "n1", sqb, rs)
                    for kvh in range(2):
                        for k in range(8):
                            mm(PS[1][0:64, :], Wb[:, k, 512 + kvh * 64:512 + (kvh + 1) * 64], aT[:, k, :], k == 0, k == 7,
                               ["Wb", "n1_dst"], ["ps1"])
                        headproc(gks, csb[s_], f"csb{s_}", KT[:, kvh, tsl], "KT")
                    for tt in range(4):
                        for k in range(8):
                            mm(PS[4][:, tt * 128:(tt + 1) * 128], aT[:, k, tt * 128:(tt + 1) * 128], Wb[:, k, 640:768], k == 0, k == 7,
                               ["Wb", "n1_dst"], ["ps4"])
                    S.op("act", lambda j=j: nc.scalar.copy(Vx[:, j * 4:(j + 1) * 4, :, 0:64],
                                                         PS[4][:, :].rearrange("p (t h d) -> p t h d", t=4, h=2)), ["ps4"], ["Vx"])
                    for ct in range(12):
                        for k in range(8):
                            mm(PS[5][:, :], Wb[:, k, 768 + ct * 128:768 + (ct + 1) * 128], aT[:, k, :], k == 0, k == 7,
                               ["Wb", "n1_dst"], ["ps5"])
                        if ct % 2 == 0:
                            S.op("act", lambda ct=ct: nc.scalar.copy(ust[:, ct, :], PS[5][:, :]), ["ps5"], ["ust"])
                        else:
                            S.op("dve", lambda ct=ct: nc.vector.tensor_copy(ust[:, ct, :], PS[5][:, :]), ["ps5"], ["ust"])
                    dma(Uv[:, :, tsl], ust[:], ["ust"], ["U_d"])
                xqv = xTq.rearrange("(k p) t -> p k t", p=128)
                for j in range(4):
                    s_ = j % 2
                    tsl = slice(j * 512, (j + 1) * 512)
                    dma(xin[s_][:], xqv[:, :, tsl], [], [f"xin{s_}", "n1_src"])
                    dma(csb[s_][:, 0, :], c_cosq[:, tsl], [], [f"csb{s_}"])
                    dma(csb[s_][:, 1, :], c_sinq[:, tsl], [], [f"csb{s_}"])
                    rms_apply(xin[s_][:], g1s, aT, 1.0 / D, 8, "n1", sqb, rs)
                    for h in range(8):
                        for k in range(8):
                            mm(PS[1][0:64, :], Wb[:, k, h * 64:(h + 1) * 64], aT[:, k, :], k == 0, k == 7, ["Wb", "n1_dst"], ["ps1"])
                        headproc(gqs, csb[s_], f"csb{s_}", QT[:, h, tsl], "QT")
                S.barrier()
            if 2 in phases:
              with ExitStack() as p2:
                pb = [sb(f"pb{i}", [128, 512], BF16, p2) for i in range(3)]
                osb = sb("osb", [65, 512], F32, p2)
                rden = sb("rden", [64, 512], F32, p2)
                ast = [sb(f"ast{i}", [64, 512], BF16, p2) for i in range(2)]
                n_ = 0
                for qc in range(4):
                    qsl = slice(qc * 512, (qc + 1) * 512)
                    for h in range(8):
                        kvh = h // 4
                        ob = 3 + n_ % 2
                        obk = f"ps{ob}"
                        LA = 2
                        for step in range(64 + LA):
                            if step < 64:
                                kt = step
                                b3 = kt % 3
                                mm(PS[b3][:, :], KT[:, kvh, kt * 128:(kt + 1) * 128], QT[:, h, qsl], True, True, ["KT", "QT"], [f"ps{b3}"])
                                S.op("act", lambda b3=b3: nc.scalar.activation(pb[b3][:], PS[b3][:, :], AF.Exp, scale=0.125),
                                     [f"ps{b3}"], [f"pb{b3}"])
                            kt = step - LA
                            if kt >= 0:
                                b3 = kt % 3
                                mm(PS[ob][0:65, :], Vx[:, kt, kvh, :], pb[b3][:], kt == 0, kt == 63, ["Vx", "Vx1", f"pb{b3}"], [obk])
                        S.op("dve", lambda ob=ob: nc.vector.tensor_copy(osb[:], PS[ob][0:65, :]), [obk], ["osb"])
                        mm(PS[5][0:64, :], sel[:], osb[:], True, True, ["osb", "sel"], ["ps5"])
                        S.op("dve", lambda: nc.vector.reciprocal(rden[:], PS[5][0:64, :]), ["ps5"], ["rden"])
                        a_ = n_ % 2
                        S.op("pool", lambda a_=a_: nc.gpsimd.tensor_tensor(ast[a_][:], osb[0:64, :], rden[:], ALU.mult), ["osb", "rden"], [f"ast{a_}"])
                        dma(att_d[h * 64:(h + 1) * 64, qsl], ast[a_][:], [f"ast{a_}"], ["att_d"])
                        n_ += 1
                S.barrier()

        def phase3():
          with ExitStack() as p3:
            load_fft_consts(p3)
            finv = FC["finv"]
            M2s = sb("M2s", [128, 128, 64], BF16, p3)
            M2qs = sb("M2qs", [128, 128, 16], BF16, p3)
            dma(M2s[:], c_M2, [], ["M2s"])
            dma(M2qs[:], c_M2q, [], ["M2qs"])
            RA = sb("RA", [128, 192 * 128 + 8], BF16, p3)
            RB = sb("RB", [128, 128 * 128], BF16, p3)
            RC = sb("RC", [128, 128 * 128], BF16, p3)
            vt = sb("vt", [128, L], BF16, p3)
            x1t = sb("x1t", [128, L], BF16, p3)
            x2q = sb("x2q", [128, NTOK], BF16, p3)
            zzq = sb("zzq", [128, NTOK], BF16, p3)
            hyst = sb("hyst", [128, NTOK], BF16, p3)
            Hc = [sb(f"Hc{i}", [128, 2, 4, 128], BF16, p3) for i in range(4)]
            Pb = sb("Pb", [128, 2, 2, 128], F32, p3)
            Qb = sb("Qb", [128, 2, 2, 128], F32, p3)
            Yb = [sb(f"Yb{i}", [128, 2, 2, 128], BF16, p3) for i in range(2)]
            gt = [sb(f"gt{i}", [128, 512], F32, p3) for i in range(2)]
            sct = gt
            uraw = RA[:, 0:3 * 8194].rearrange("p (i t) -> p i t", i=3)
            AT = RA[:, 0:192 * 128].rearrange("p (j c) -> p j c", c=128)
            ztm = RB[:, :].rearrange("p (c n) -> p c n", n=128)
            Eb = RB[:, :].rearrange("p (n c) -> p n c", c=128)
            x2t = RC[:, 0:L]
            ET = RC[:, :].rearrange("p (c k) -> p c k", k=128)
            RAK, RBK, RCK = "RA", "RB", "RC"

            def masked_quarter(dst, src, dkey, skey):
                S.op("dve", lambda: nc.vector.tensor_scalar(dst[:], src[:, 0:NTOK], qms[:, 0:1], None, ALU.mult), [skey, "qms"], [dkey])
                for q in range(1, 4):
                    S.op("dve", lambda q=q: nc.vector.scalar_tensor_tensor(out=dst[:], in0=src[:, q * NTOK:(q + 1) * NTOK], scalar=qms[:, q:q + 1],
                                                                          in1=dst[:], op0=ALU.mult, op1=ALU.add), [skey, "qms", dkey], [dkey])

            RBQ = [(RBK, q4) for q4 in range(4)]

            def bounce_to_ztm(src, skey):
                zv = zcm_d.rearrange("c (n1 n2) -> n1 c n2", n2=128)
                for q4 in range(4):
                    dma(zcm_d[q4 * 32:(q4 + 1) * 32, :], src[q4 * 32:(q4 + 1) * 32, :], [skey], [("zcm_d", q4)])
                for q4 in range(4):
                    dma(ztm[0:64, q4 * 32:(q4 + 1) * 32, :], zv[:, q4 * 32:(q4 + 1) * 32, :], [("zcm_d", q4)], [(RBK, q4)])

            def conv_core(go):
                def load_h(kb):
                    dma(Hc[kb % 4][:], H_d[go, :, kb * 2:kb * 2 + 2], [("H_d", go)], [f"Hc{kb % 4}"])

                def pre(kb):
                    if kb == 0:
                        load_h(0)
                        load_h(1)
                    if kb + 2 < 32:
                        load_h(kb + 2)

                def cb(kb, ps, pk):
                    s_ = kb % 4
                    y_ = kb % 2
                    zv = ps[:, :].rearrange("p (k r c) -> p k r c", k=2, r=2)
                    S.op("dve", lambda: nc.vector.tensor_tensor(Pb[:], zv, Hc[s_][:, :, 0:2, :], ALU.mult), [pk, f"Hc{s_}"], ["Pb"])
                    S.op("dve", lambda: nc.vector.tensor_tensor(Qb[:], zv, Hc[s_][:, :, 2:4, :], ALU.mult), [pk, f"Hc{s_}"], ["Qb"])
                    S.op("pool", lambda: nc.gpsimd.tensor_tensor(Yb[y_][:, :, 0, :], Pb[:, :, 0, :], Pb[:, :, 1, :], ALU.subtract), ["Pb"], [f"Yb{y_}"])
                    S.op("pool", lambda: nc.gpsimd.tensor_tensor(Yb[y_][:, :, 1, :], Qb[:, :, 0, :], Qb[:, :, 1, :], ALU.add), ["Qb"], [f"Yb{y_}"])
                    pe_ = PS[4 + kb % 2]
                    pek = f"ps{4 + kb % 2}"
                    er = pe_[:, 0:256].rearrange("p (k c) -> p k c", k=2)
                    ei = pe_[:, 256:512].rearrange("p (k c) -> p k c", k=2)
                    yr = Yb[y_][:, :, 0, :]
                    yi = Yb[y_][:, :, 1, :]
                    mm(er, finv[:, 0, :], yr, True, False, [f"Yb{y_}", "finv"], [pek])
                    mm(er, finv[:, 2, :], yi, False, True, [f"Yb{y_}", "finv"], [pek])
                    mm(ei, finv[:, 1, :], yr, True, False, [f"Yb{y_}", "finv"], [pek])
                    mm(ei, finv[:, 0, :], yi, False, True, [f"Yb{y_}", "finv"], [pek])
                    S.op("act", lambda: nc.scalar.copy(ET[:, :, kb * 2:kb * 2 + 2], pe_[:, 0:256].rearrange("p (k c) -> p c k", k=2)), [pek], [RCK])
                    S.op("dve", lambda: nc.vector.tensor_copy(ET[:, :, 64 + kb * 2:64 + kb * 2 + 2], pe_[:, 256:512].rearrange("p (k c) -> p c k", k=2)),
                         [pek], [RCK])
                fft_forward(ztm, 64, AT, cb, zkey=RBK, akey=RAK, pre_batch=pre)
                for c4 in range(32):
                    pb_ = PSB[c4 % 2]
                    pbk = f"psb{c4 % 2}"
                    for cc in range(4):
                        c = c4 * 4 + cc
                        S.op("pe", lambda c=c, cc=cc, pb_=pb_: nc.tensor.transpose(pb_[:, cc * 128:(cc + 1) * 128], ET[:, c, :], ident[:]),
                             [RCK, "ident"], [pbk])
                    src = pb_[:, 0:512].rearrange("p (cc n) -> p n cc", cc=4)
                    dst = Eb[:, :, c4 * 4:c4 * 4 + 4]
                    if c4 % 2 == 0:
                        S.op("act", lambda src=src, dst=dst: nc.scalar.copy(dst, src), [pbk], RBQ)
                    else:
                        S.op("dve", lambda src=src, dst=dst: nc.vector.tensor_copy(dst, src), [pbk], RBQ)

            Udv = U_d
            for g in range(4):
                for i in range(3):
                    dma(uraw[:, i, 1:L + 1], Udv[i * 512 + g * 128:i * 512 + (g + 1) * 128, :], ["U_d"], [RAK])
                S.op("pool", lambda: nc.gpsimd.memset(uraw[:, :, 0:1], 0.0), [RAK], [RAK])
                S.op("pool", lambda: nc.gpsimd.memset(uraw[:, :, L + 1:L + 2], 0.0), [RAK], [RAK])
                for i, (dst, dkey) in enumerate(((vt, "vt"), (x1t, "x1t"), (x2t, RCK))):
                    ct = i * 4 + g
                    for q in range(16):
                        e = "dve"
                        st = sct[q % 2]
                        sk = f"gt{q % 2}"
                        o0 = q * 512
                        S.op(e, lambda e=e, st=st, i=i, o0=o0, ct=ct: V(e).tensor_scalar(st[:], uraw[:, i, o0:o0 + 512], cws[:, ct, 0:1], cbs[:, ct:ct + 1],
                                                                                      ALU.mult, ALU.add), [RAK, "cws", "cbs"], [sk])
                        S.op(e, lambda e=e, st=st, i=i, o0=o0, ct=ct: V(e).scalar_tensor_tensor(out=st[:], in0=uraw[:, i, o0 + 1:o0 + 513], scalar=cws[:, ct, 1:2],
                                                                                             in1=st[:], op0=ALU.mult, op1=ALU.add), [RAK, "cws", sk], [sk])
                        S.op(e, lambda e=e, st=st, i=i, o0=o0, ct=ct, dst=dst: V(e).scalar_tensor_tensor(out=dst[:, o0:o0 + 512], in0=uraw[:, i, o0 + 2:o0 + 514],
                                                                                                      scalar=cws[:, ct, 2:3], in1=st[:], op0=ALU.mult, op1=ALU.add),
                             [RAK, "cws", sk], [dkey])
                masked_quarter(x2q, x2t, "x2q", RCK)
                snap("vt0", vt[:], "vt", L)
                snap("x1t0", x1t[:], "x1t", L)
                bounce_to_ztm(vt, "vt")
                conv_core(g * 2 + 0)
                vt3 = vt[:, :].rearrange("p (n1 n2) -> p n1 n2", n2=128)
                x13 = x1t[:, :].rearrange("p (n1 n2) -> p n1 n2", n2=128)
                for nb in range(16):
                    ps = PS[nb % 2]
                    pk = f"ps{nb % 2}"
                    for j in range(8):
                        n2 = nb * 8 + j
                        mm(ps[:, j * 64:(j + 1) * 64], Eb[:, n2, :], M2s[:, n2, :], True, True, RBQ + ["M2s"], [pk])
                    psv = ps[:, :].rearrange("p (j n) -> p n j", j=8)
                    gtv = gt[nb % 2][:, :].rearrange("p (n j) -> p n j", j=8)
                    gk_ = f"gt{nb % 2}"
                    S.op("dve", lambda psv=psv, gtv=gtv, nb=nb, g=g: nc.vector.scalar_tensor_tensor(out=gtv, in0=vt3[:, :, nb * 8:(nb + 1) * 8],
                                                                                             scalar=skps[:, g:g + 1], in1=psv, op0=ALU.mult, op1=ALU.add),
                         [pk, "vt", "skps"], [gk_])
                    S.op("pool", lambda gtv=gtv, nb=nb: nc.gpsimd.tensor_tensor(vt3[:, :, nb * 8:(nb + 1) * 8], gtv, x13[:, :, nb * 8:(nb + 1) * 8], ALU.mult),
                         [gk_, "x1t"], ["vt"])
                snap("zz0", vt[:], "vt", L)
                masked_quarter(zzq, vt, "zzq", "vt")
                bounce_to_ztm(vt, "vt")
                conv_core(g * 2 + 1)
                zq3 = zzq[:, :].rearrange("p (n1 n2) -> p n1 n2", n2=128)
                xq3 = x2q[:, :].rearrange("p (n1 n2) -> p n1 n2", n2=128)
                hy3 = hyst[:, :].rearrange("p (n1 n2) -> p n1 n2", n2=128)
                for nb in range(4):
                    ps = PS[nb % 2]
                    pk = f"ps{nb % 2}"
                    for j in range(32):
                        n2 = nb * 32 + j
                        mm(ps[:, j * 16:(j + 1) * 16], Eb[:, n2, :], M2qs[:, n2, :], True, True, RBQ + ["M2qs"], [pk])
                    psv = ps[:, :].rearrange("p (j n) -> p n j", j=32)
                    gtv = gt[nb % 2][:, :].rearrange("p (n j) -> p n j", j=32)
                    gk_ = f"gt{nb % 2}"
                    S.op("dve", lambda psv=psv, gtv=gtv, nb=nb, g=g: nc.vector.scalar_tensor_tensor(out=gtv, in0=zq3[:, :, nb * 32:(nb + 1) * 32],
                                                                                             scalar=skps[:, 4 + g:5 + g], in1=psv, op0=ALU.mult, op1=ALU.add),
                         [pk, "zzq", "skps"], [gk_])
                    S.op("pool", lambda gtv=gtv, nb=nb: nc.gpsimd.tensor_tensor(hy3[:, :, nb * 32:(nb + 1) * 32], gtv, xq3[:, :, nb * 32:(nb + 1) * 32], ALU.mult),
                         [gk_, "x2q"], ["hyst"])
                dma(hy_d[g * 128:(g + 1) * 128, :], hyst[:], ["hyst"], ["hy_d"])
            S.barrier()

        def phase4():
          with ExitStack() as p4:
            WoA = sb("WoA", [64, 8, D], BF16, p4)
            WoH = sb("WoH", [128, 4, D], BF16, p4)
            with ExitStack() as p4c:
                wsa = sb("wsa", [64, 8, D], F32, p4c)
                wsh = sb("wsh", [128, 4, D], F32, p4c)
                dma(wsa[:], w_out[0:512, :].rearrange("(h d) n -> d h n", d=64), [], ["wsa"])
                dma(wsh[:], w_out[512:1024, :].rearrange("(g p) n -> p g n", p=128), [], ["wsh"])
                S.op("dve", lambda: nc.vector.tensor_copy(WoA[:], wsa[:]), ["wsa"], ["WoA"])
                S.op("pool", lambda: nc.gpsimd.tensor_copy(WoH[:], wsh[:]), ["wsh"], ["WoH"])
                S.barrier()
            attc = sb("attc", [64, 8, 512], BF16, p4)
            hyc = sb("hyc", [128, 4, 512], BF16, p4)
            sqb = sb("sqb4", [128, 8, 512], F32, p4)
            rs = sb("rs4", [128, 512], F32, p4)
            mixA = sb("mixA", [64, 8, 512], BF16, p4)
            mixH = sb("mixH", [128, 4, 512], BF16, p4)
            h1 = sb("h1", [128, 8, 512], F32, p4)
            mT = sb("mT", [128, 8, 512], BF16, p4)
            Wmi_s = [sb(f"Wmi_s{i}", [128, 8, 1024], BF16, p4) for i in range(2)]
            Wmo_s = [sb(f"Wmo_s{i}", [128, 32, 128], BF16, p4) for i in range(2)]
            fT = sb("fT", [128, 32, 512], BF16, p4)
            rb = [sb(f"rb{i}", [128, 512], F32, p4) for i in range(2)]
            xqv = xTq.rearrange("(k p) t -> p k t", p=128)
            Wmiv = Wmi_d.rearrange("(k p) n -> p k n", p=128)
            outv = outT.rearrange("(k p) t -> p k t", p=128)
            nw = 0
            nwo = 0
            for j in range(4):
                tsl = slice(j * 512, (j + 1) * 512)
                dma(attc[:], att_d[:, tsl].rearrange("(h d) t -> d h t", d=64), ["att_d"], ["attc", "na_src"])
                dma(hyc[:], hy_d[:, tsl].rearrange("(g p) t -> p g t", p=128), ["hy_d"], ["hyc", "nh_src"])
                dma(h1[:], xqv[:, :, tsl], [], ["h1", "n2_src", "nf_src"])
                rms_apply(attc[:], gas, mixA, 1.0 / 512.0, 8, "na", sqb, rs, np_=64)
                rms_apply(hyc[:], ghs, mixH, 1.0 / 512.0, 4, "nh", sqb, rs)
                for ct in range(8):
                    csl = slice(ct * 128, (ct + 1) * 128)
                    for h in range(8):
                        mm(PS[1][:, :], WoA[:, h, csl], mixA[:, h, :], h == 0, False, ["WoA", "na_dst"], ["ps1"])
                    for g in range(4):
                        mm(PS[1][:, :], WoH[:, g, csl], mixH[:, g, :], False, g == 3, ["WoH", "nh_dst"], ["ps1"])
                    S.op("dve", lambda ct=ct: nc.vector.tensor_tensor(h1[:, ct, :], h1[:, ct, :], PS[1][:, :], ALU.add), ["ps1", "h1"], ["h1", "n2_src", "nf_src"])
                rms_apply(h1[:], g2s, mT, 1.0 / D, 8, "n2", sqb, rs)
                for pc in range(4):
                    s_ = nw % 2
                    dma(Wmi_s[s_][:], Wmiv[:, :, pc * 1024:(pc + 1) * 1024], ["Wmid"], [f"Wmi_s{s_}"])
                    nw += 1
                    for ft in range(8):
                        f = pc * 8 + ft
                        pf = PS[2 + f % 2]
                        pfk = f"ps{2 + f % 2}"
                        for k in range(8):
                            mm(pf[:, :], Wmi_s[s_][:, k, ft * 128:(ft + 1) * 128], mT[:, k, :], k == 0, k == 7, [f"Wmi_s{s_}", "n2_dst"], [pfk])
                        r_ = f % 2
                        S.op("act", lambda pf=pf, r_=r_: nc.scalar.activation(rb[r_][:], pf[:, :], AF.Relu), [pfk], [f"rb{r_}"])
                        e = ("pool", "dve")[f % 2]
                        S.op(e, lambda e=e, r_=r_, f=f: V(e).tensor_tensor(fT[:, f, :], rb[r_][:], rb[r_][:], ALU.mult), [f"rb{r_}"], ["fT"])
                for ct in range(8):
                    s_ = nwo % 2
                    dma(Wmo_s[s_][:], Wmo_d[ct], ["Wmod"], [f"Wmo_s{s_}"])
                    nwo += 1
                    po = PS[4 + ct % 2]
                    pok = f"ps{4 + ct % 2}"
                    for f in range(32):
                        mm(po[:, :], Wmo_s[s_][:, f, :], fT[:, f, :], f == 0, f == 31, [f"Wmo_s{s_}", "fT"], [pok])
                    S.op("dve", lambda ct=ct, po=po: nc.vector.tensor_tensor(h1[:, ct, :], h1[:, ct, :], po[:, :], ALU.add), [pok, "h1"], ["h1", "n2_src", "nf_src"])
                rms_apply(h1[:], gfs, h1, 1.0 / D, 8, "nf", sqb, rs)
                dma(outv[:, :, tsl], h1[:], ["h1", "nf_dst"], ["outT"])
            S.barrier()

        if 0 in phases:
            phase0()
        if 1 in phases:
            phase12()
        if 3 in phases:
            phase3()
        if 4 in phases:
            phase4()
        if debug:
            with ExitStack() as pd:
                for name in debug:
                    if name not in ("att_d", "hy_d", "U_d", "H_d0"):
                        continue
                    src_, shp = {"att_d": (att_d, [512, NTOK]), "hy_d": (hy_d, [512, NTOK]), "U_d": (U_d[0:512, :], [512, L]),
                                 "H_d0": (H_d[0:4, :, 0:4].rearrange("a p k r c -> (a p) (k r c)"), [512, 2048])}[name]
                    dbo = nc.dram_tensor("dbg_" + name, shp, BF16, kind="ExternalOutput").ap()
                    dbs = sb("dbs_" + name, [128, 4, shp[1]], BF16, pd)
                    dma(dbs[:], src_.rearrange("(a p) t -> p a t", p=128), [name], ["dbs_" + name])
                    dma(dbo.rearrange("(a p) t -> p a t", p=128), dbs[:], ["dbs_" + name], ["dbg_" + name])
                S.barrier()
        S.barrier()
    return nc, S


def _core_inputs(inp, core):
    T = _tables()
    b, r = core // 4, core % 4
    f = lambda a: np.ascontiguousarray(np.asarray(a, dtype=np.float32))
    x = np.asarray(inp["x"], dtype=np.float32)
    xT = np.ascontiguousarray(x[b].T)
    tq = slice(r * NTOK, (r + 1) * NTOK)
    qm = np.zeros((128, 4), np.float32)
    qm[:, r] = 1.0
    m = {
        "xT": xT, "xTq": np.ascontiguousarray(xT[:, tq]),
        "w_in": f(inp["w_in"][0]), "w_out": f(inp["w_out"][0]), "w_mi": f(inp["w_mlp_in"][0]), "w_mo": f(inp["w_mlp_out"][0]),
        "g1": f(np.asarray(inp["norm1_g"][0]).reshape(8, 128).T), "g2": f(np.asarray(inp["norm2_g"][0]).reshape(8, 128).T),
        "gf": f(np.asarray(inp["final_g"]).reshape(8, 128).T),
        "gq": f(np.asarray(inp["q_norm_g"][0]).reshape(64, 1)), "gk": f(np.asarray(inp["k_norm_g"][0]).reshape(64, 1)),
        "ga": f(np.asarray(inp["attn_out_g"][0]).reshape(8, 64).T), "gh": f(np.asarray(inp["hy_out_g"][0]).reshape(4, 128).T),
        "cw": f(np.asarray(inp["hy_conv_w"][0]).T.reshape(12, 128, 3).transpose(1, 0, 2)),
        "cb": f(np.asarray(inp["hy_conv_b"][0]).reshape(12, 128).T),
        "fw1": f(inp["filt_w1"][0]), "fw2": f(inp["filt_w2"][0]), "fw3": f(inp["filt_w3"][0]), "fw4": f(inp["filt_w4"][0]),
        "fb": f(np.stack([np.asarray(inp["filt_b1"][0]), np.asarray(inp["filt_b2"][0]), np.asarray(inp["filt_b3"][0])], axis=1)),
        "ffreq": f(np.asarray(inp["filt_freq"][0]).reshape(64, 1)),
        "fdel": f(np.asarray(inp["filt_deltas"][0]).reshape(16, 128).T),
        "skp": f(np.asarray(inp["hy_skip_d"][0]).reshape(2, 4, 128).transpose(2, 0, 1).reshape(128, 8)),
        "qmask": qm,
        "c_cos": T["cosT"], "c_sin": T["sinT"],
        "c_cosq": np.ascontiguousarray(T["cosT"][:, tq]), "c_sinq": np.ascontiguousarray(T["sinT"][:, tq]),
        "c_zT": np.ascontiguousarray(np.stack([T["zT"], T["zrT"]])), "c_t": np.ascontiguousarray(np.stack([T["t"], T["trev"]])),
        "c_tb": T["tb"], "c_jt": T["jt"],
        "c_fpack": T["fpack"], "c_G": T["G"], "c_finv": T["finv"], "c_M2": T["M2"],
        "c_M2q": np.ascontiguousarray(T["M2"][:, :, r * 16:(r + 1) * 16]),
        "c_rot": T["rot"], "c_sel": T["sel"], "c_ones": T["ones"], "c_ident": T["ident"],
    }
    return m


def _emit(nc, S):
    from contextlib import ExitStack
    st = ExitStack()
    sems = {e: [st.enter_context(nc.semaphore(f"s_{e}{i}")) for i in range(4)] for e in S.CE}
    dsems = [st.enter_context(nc.semaphore(f"d{i}")) for i in range(S.NDS)]
    info = S.emit(sems, dsems)
    return st, info


def kernel(**inputs):
    nc, S = build_program()
    st, _ = _emit(nc, S)
    with st:
        in_maps = [_core_inputs(inputs, c) for c in range(8)]
        res = run_bass_kernel_spmd(nc, in_maps, core_ids=list(range(8)))
    out = np.empty((NB, L, D), np.float32)
    for c in range(8):
        b, r = c // 4, c % 4
        out[b, r * NTOK:(r + 1) * NTOK, :] = np.asarray(res.results[c]["outT"]).T
    return out
```

```python
import math
import numpy as np
import ml_dtypes
import concourse.bass as bass
import concourse.mybir as mybir
from concourse.bass_utils import run_bass_kernel_spmd

F32 = mybir.dt.float32
BF16 = mybir.dt.bfloat16
ALU = mybir.AluOpType
AF = mybir.ActivationFunctionType
bf16_np = ml_dtypes.bfloat16

D = 1024
L = 8192
NB = 2
NTOK = 2048
HD = 64
NQH = 8
NKV = 2
HYW = 512
DFF = 4096
EPS = 1e-6
NFFT = 2 * L
TWO_PI = 2.0 * math.pi
MAGIC = 12582912.0


class Op:
    __slots__ = ("eng", "fn", "reads", "writes", "dma", "idx", "gid", "waits", "signal", "clock",
                 "sem", "val")


class Sched:
    CE = ("pe", "act", "dve", "pool", "sp")
    EPOCH = 12000
    NDS = 40

    def __init__(self, nc):
        self.nc = nc
        self.E = {"pe": nc.tensor, "act": nc.scalar, "dve": nc.vector, "pool": nc.gpsimd,
                  "sp": nc.sync}
        self.ops = []
        self.last_w = {}
        self.readers = {}
        self.known = {e: {} for e in self.CE}
        self.known_dma = {e: set() for e in self.CE}
        self.cnt = {e: 0 for e in self.CE}
        self.last_op = {e: None for e in self.CE}
        self.live_dma = []

    def op(self, eng, fn, reads=(), writes=(), dma=False):
        o = Op()
        o.eng, o.fn, o.reads, o.writes, o.dma = eng, fn, tuple(reads), tuple(writes), dma
        o.signal = dma
        o.sem = None
        o.val = None
        o.gid = len(self.ops)
        deps = {}
        for k in o.reads:
            w = self.last_w.get(k)
            if w is not None:
                deps[w.gid] = w
        for k in o.writes:
            w = self.last_w.get(k)
            if w is not None:
                deps[w.gid] = w
            for r in self.readers.get(k, ()):
                deps[r.gid] = r
        self._finish(o, deps.values())
        for k in o.writes:
            self.last_w[k] = o
            self.readers[k] = []
        for k in o.reads:
            lst = self.readers.setdefault(k, [])
            if not dma:
                lst[:] = [r for r in lst if r.dma or r.eng != eng]
            lst.append(o)
        return o

    def _finish(self, o, deps):
        eng = o.eng
        kn = self.known[eng]
        kd = self.known_dma[eng]
        waits = []
        best = {}
        rset = set(o.reads)
        for d in deps:
            if d.dma:
                if d.gid not in kd:
                    waits.append(d)
                    kd.add(d.gid)
                continue
            if d.eng == eng:
                if eng in ("pe", "sp"):
                    continue
            if kn.get(d.eng, -1) >= d.idx:
                continue
            b = best.get(d.eng)
            if b is None or b.idx < d.idx:
                best[d.eng] = d
        for d in best.values():
            waits.append(d)
            if kn.get(d.eng, -1) < d.idx:
                kn[d.eng] = d.idx
            for e2, i2 in d.clock.items():
                if kn.get(e2, -1) < i2:
                    kn[e2] = i2
        for d in waits:
            d.signal = True
        o.waits = waits
        o.idx = self.cnt[eng]
        self.cnt[eng] += 1
        o.clock = dict(kn)
        self.ops.append(o)
        if o.fn is not None:
            self.last_op[eng] = o
        if o.dma:
            self.live_dma.append(o)

    def barrier(self):
        lasts = [self.last_op[e] for e in self.CE if self.last_op[e] is not None]
        dmas = list(self.live_dma)
        for e in self.CE:
            o = Op()
            o.eng, o.fn, o.reads, o.writes, o.dma = e, None, (), (), False
            o.signal = False
            o.sem = None
            o.val = None
            o.gid = len(self.ops)
            deps = [d for d in lasts if not d.dma and d.eng != e] + dmas
            self._finish(o, deps)
        self.live_dma = []

    def emit(self, sems, dsems):
        sig = {e: 0 for e in self.CE}
        nd = 0
        for o in self.ops:
            E = self.E[o.eng]
            for d in o.waits:
                E.wait_ge(d.sem, d.val)
            if o.fn is None:
                continue
            if o.dma:
                s = dsems[nd % self.NDS]
                rnd = nd // self.NDS
                if rnd > 0:
                    E.wait_ge(s, 16 * rnd)
                ins = o.fn()
                ins.then_inc(s, 16)
                o.sem, o.val = s, 16 * (rnd + 1)
                nd += 1
            else:
                ins = o.fn()
                if o.signal:
                    c = sig[o.eng]
                    o.sem = sems[o.eng][c // self.EPOCH]
                    o.val = c % self.EPOCH + 1
                    ins.then_inc(o.sem, 1)
                    sig[o.eng] = c + 1
        return sig, nd


_TABLES = None


def _rope_tables():
    rows = L // 64
    row = np.repeat(np.arange(rows, dtype=np.float32), 64)
    col = np.tile(np.arange(64, dtype=np.float32), rows)
    half = HD // 2
    inv_freq = (np.float32(10000.0) ** (-np.arange(0, half, 2, dtype=np.float32) / np.float32(half))).astype(np.float32)
    ang_r = (row[:, None] * inv_freq[None, :]).astype(np.float32)
    ang_c = (col[:, None] * inv_freq[None, :]).astype(np.float32)
    cos_r, sin_r = np.cos(ang_r), np.sin(ang_r)
    cos_c, sin_c = np.cos(ang_c), np.sin(ang_c)
    cosT = np.concatenate([cos_r.T, cos_r.T, cos_c.T, cos_c.T], axis=0).astype(np.float32)
    sinT = np.concatenate([sin_r.T, sin_r.T, sin_c.T, sin_c.T], axis=0).astype(np.float32)
    return np.ascontiguousarray(cosT), np.ascontiguousarray(sinT)


def _filter_pos_tables():
    f32 = np.float32
    t = np.linspace(0.0, 1.0, L, dtype=f32)
    bands = 16
    w_ang = (f32(2.0 * math.pi) * np.arange(L, dtype=f32) / f32(L)).astype(f32)
    band_f = np.linspace(1e-4, bands - 1, bands, dtype=f32)
    ang = (w_ang[:, None] * band_f[None, :]).astype(f32)
    z = np.concatenate([t[:, None], np.cos(ang), -np.sin(ang)], axis=-1).astype(f32)
    zT = np.ascontiguousarray(z.T)
    idx = (L - np.arange(L)) % L
    zrT = np.ascontiguousarray(z[idx].T)
    trev = t[idx].copy()
    return zT, zrT, t.copy(), trev


def _fft_tables():
    N = NFFT
    n1 = np.arange(128, dtype=np.float64)
    k1 = np.arange(64, dtype=np.float64)
    th = 2 * np.pi * np.outer(n1, k1 + 0.5) / 128.0
    fpack = np.concatenate([np.cos(th), -np.sin(th), np.sin(th)], axis=1)
    n2 = np.arange(128, dtype=np.float64)
    k2 = np.arange(128, dtype=np.float64)
    kap = k1[None, :, None] + 128.0 * k2[None, None, :] + 0.5
    thg = 2 * np.pi * n2[:, None, None] * kap / N
    G = np.stack([np.cos(thg), -np.sin(thg)], axis=2)
    thf = 2 * np.pi * np.outer(k2, n2) / 128.0
    finv = np.stack([np.cos(thf), np.sin(thf), -np.sin(thf)], axis=1)
    n1h = np.arange(64, dtype=np.float64)
    phi = 2 * np.pi * (k1[:, None, None] + 0.5) * (128.0 * n1h[None, None, :] + n2[None, :, None]) / N
    M2 = np.concatenate([(2.0 / N) * np.cos(phi), -(2.0 / N) * np.sin(phi)], axis=0)
    return (fpack.astype(bf16_np), G.astype(bf16_np), finv.astype(bf16_np), M2.astype(bf16_np))


def _tables():
    global _TABLES
    if _TABLES is None:
        cosT, sinT = _rope_tables()
        zT, zrT, t, trev = _filter_pos_tables()
        fpack, G, finv, M2 = _fft_tables()
        ii = np.arange(512, dtype=np.float64)
        jj = np.arange(16, dtype=np.float64)
        tbm = np.stack([ii / (L - 1), (L - ii) / (L - 1)]).astype(np.float32)
        jtm = np.stack([512.0 * jj / (L - 1), -512.0 * jj / (L - 1)]).astype(np.float32)
        rot = np.zeros((64, 64), np.float32)
        for m in range(64):
            if (m % 32) < 16:
                rot[m + 16, m] = -1.0
            else:
                rot[m - 16, m] = 1.0
        sel = np.zeros((65, 64), np.float32)
        sel[64, :] = 1.0
        _TABLES = dict(cosT=cosT, sinT=sinT, zT=zT, zrT=zrT, t=t, trev=trev, fpack=fpack, G=G,
                       finv=finv, M2=M2, rot=rot, sel=sel, tb=tbm, jt=jtm,
                       ones=np.ones((128, 128), np.float32),
                       ident=np.eye(128, dtype=np.float32).astype(bf16_np))
    return _TABLES


def build_program(phases=(0, 1, 2, 3, 4), debug=()):
    nc = bass.Bass("TRN2", target_bir_lowering=False)
    S = Sched(nc)

    def din(name, shape, dt=F32):
        return nc.dram_tensor(name, list(shape), dt, kind="ExternalInput").ap()

    def dscr(name, shape, dt):
        return nc.dram_tensor(name, list(shape), dt, kind="Internal").ap()

    xT = din("xT", [D, L])
    xTq = din("xTq", [D, NTOK])
    w_in = din("w_in", [D, 2304])
    w_out = din("w_out", [D, D])
    w_mi = din("w_mi", [D, DFF])
    w_mo = din("w_mo", [DFF, D])
    g1 = din("g1", [128, 8]); g2 = din("g2", [128, 8]); gf = din("gf", [128, 8])
    gq = din("gq", [64, 1]); gk = din("gk", [64, 1])
    ga = din("ga", [64, 8]); gh = din("gh", [128, 4])
    cw = din("cw", [128, 12, 3]); cb = din("cb", [128, 12])
    fw1 = din("fw1", [33, 64]); fw2 = din("fw2", [64, 64]); fw3 = din("fw3", [64, 64])
    fw4 = din("fw4", [64, 2048])
    fb = din("fb", [64, 3]); ffreq = din("ffreq", [64, 1])
    fdel = din("fdel", [128, 16])
    skp = din("skp", [128, 8])
    qmask = din("qmask", [128, 4])
    c_cos = din("c_cos", [64, L]); c_sin = din("c_sin", [64, L])
    c_cosq = din("c_cosq", [64, NTOK]); c_sinq = din("c_sinq", [64, NTOK])
    c_zT = din("c_zT", [2, 33, L])
    c_t = din("c_t", [2, L])
    c_tb = din("c_tb", [2, 512])
    c_jt = din("c_jt", [2, 16])
    c_fpack = din("c_fpack", [128, 192], BF16)
    c_G = din("c_G", [128, 64, 2, 128], BF16)
    c_finv = din("c_finv", [128, 3, 128], BF16)
    c_M2 = din("c_M2", [128, 128, 64], BF16)
    c_M2q = din("c_M2q", [128, 128, 16], BF16)
    c_rot = din("c_rot", [64, 64]); c_sel = din("c_sel", [65, 64]); c_ones = din("c_ones", [128, 128])
    c_ident = din("c_ident", [128, 128], BF16)
    outT = nc.dram_tensor("outT", [D, NTOK], F32, kind="ExternalOutput").ap()

    U_d = dscr("U_d", [1536, L], BF16)
    H_d = dscr("H_d", [8, 128, 64, 4, 128], BF16)
    taps_d = dscr("taps_d", [128, NFFT], BF16)
    zcm_d = dscr("zcm_d", [128, L], BF16)
    att_d = dscr("att_d", [512, NTOK], BF16)
    hy_d = dscr("hy_d", [512, NTOK], BF16)
    Wmi_d = dscr("Wmi_d", [D, DFF], BF16)
    Wmo_d = dscr("Wmo_d", [8, 128, 32, 128], BF16)

    from contextlib import ExitStack
    es = ExitStack()

    _names = {}

    def sb(name, shape, dt, stack=None):
        n_ = _names.get(name, 0)
        _names[name] = n_ + 1
        nm = name if n_ == 0 else f"{name}_r{n_}"
        return (stack or es).enter_context(nc.sbuf_tensor(nm, list(shape), dt))

    def dma(out, in_, reads, writes, eng="sp"):
        return S.op(eng, lambda: S.E[eng].dma_start(out=out, in_=in_), reads, writes, dma=True)

    def mm(out, lhsT, rhs, start, stop, reads, writes):
        return S.op("pe", lambda: nc.tensor.matmul(out, lhsT, rhs, start=start, stop=stop), reads, writes)

    rr = {"i": 0}

    def alt(engs=("dve", "pool")):
        rr["i"] += 1
        return engs[rr["i"] % len(engs)]

    def V(eng):
        return S.E[eng]

    with es:
        PSW = [es.enter_context(nc.psum_tensor(f"psw{i}", [128, 1024], F32)) for i in range(3)]
        PS = [PSW[i // 2][:, (i % 2) * 512:(i % 2 + 1) * 512] for i in range(6)]
        PSB = [es.enter_context(nc.psum_tensor(f"psb{i}", [128, 1024], BF16)) for i in range(2)]

        ones = sb("ones", [128, 128], F32)
        ident = sb("ident", [128, 128], BF16)
        rot = sb("rot", [64, 64], F32)
        sel = sb("sel", [65, 64], F32)
        epsT = sb("epsT", [128, 1], F32)
        g1s = sb("g1s", [128, 8], F32); g2s = sb("g2s", [128, 8], F32); gfs = sb("gfs", [128, 8], F32)
        gqs = sb("gqs", [64, 1], F32); gks = sb("gks", [64, 1], F32)
        gas = sb("gas", [64, 8], F32); ghs = sb("ghs", [128, 4], F32)
        cws = sb("cws", [128, 12, 3], F32); cbs = sb("cbs", [128, 12], F32)
        skps = sb("skps", [128, 8], F32); qms = sb("qms", [128, 4], F32)
        for t_, d_ in ((ones, c_ones), (ident, c_ident), (rot, c_rot), (sel, c_sel), (g1s, g1), (g2s, g2),
                       (gfs, gf), (gqs, gq), (gks, gk), (gas, ga), (ghs, gh), (cws, cw), (cbs, cb),
                       (skps, skp), (qms, qmask)):
            dma(t_[:], d_, [], [t_.name if hasattr(t_, "name") else id(t_)])
        KEY = lambda t_: t_.name if hasattr(t_, "name") else id(t_)
        S.op("dve", lambda: nc.vector.memset(epsT[:], EPS), [], ["epsT"])

        FC = {}
        SNAP = {}

        def snap(name, src_ap, key, ncols):
            if name not in debug or name in SNAP:
                return
            SNAP[name] = nc.dram_tensor("dbg_" + name, [128, ncols], BF16, kind="ExternalOutput").ap()
            dma(SNAP[name], src_ap, [key], ["dbg_" + name])


        def load_fft_consts(stack):
            FC["fpack"] = sb("fpack", [128, 192], BF16, stack)
            FC["finv"] = sb("finv", [128, 3, 128], BF16, stack)
            FC["Gq"] = [sb(f"Gq{i}", [128, 8, 2, 128], BF16, stack) for i in range(2)]
            dma(FC["fpack"][:], c_fpack, [], ["fpack"])
            dma(FC["finv"][:], c_finv, [], ["finv"])

        def rsqrt(out, in_, scale, reads, writes, np_=128):
            S.op("act", lambda: nc.scalar.activation(out, in_, AF.Sqrt, bias=epsT[0:np_, :], scale=scale),
                 list(reads) + ["epsT"], writes)
            S.op("dve", lambda: nc.vector.reciprocal(out, out), writes, writes)

        def fft_forward(ztm, K, AT, cb_batch, zkey="ztm", akey="AT", pre_batch=None):
            fpack = FC["fpack"]
            for c2 in range(64):
                ps = PS[c2 % 2]
                pk = f"ps{c2 % 2}"
                for cc in range(2):
                    c = c2 * 2 + cc
                    mm(ps[:, cc * 192:(cc + 1) * 192], ztm[0:K, c, :], fpack[0:K, :], True, True,
                       [(zkey, c // 32), "fpack"], [pk])
                e = alt(("act", "dve"))
                src = ps[:, 0:384].rearrange("p (cc j) -> p j cc", cc=2)
                dst = AT[:, :, c2 * 2:c2 * 2 + 2]
                if e == "act":
                    S.op("act", lambda dst=dst, src=src: nc.scalar.copy(dst, src), [pk], [akey])
                else:
                    S.op("dve", lambda dst=dst, src=src: nc.vector.tensor_copy(dst, src), [pk], [akey])

            def load_g(q8):
                dma(FC["Gq"][q8 % 2][:], c_G[:, q8 * 8:(q8 + 1) * 8], [], [f"Gq{q8 % 2}"])

            load_g(0)
            for kb in range(32):
                q4 = kb // 4
                Gq = FC["Gq"][q4 % 2]
                gkey = f"Gq{q4 % 2}"
                if kb % 4 == 0 and q4 + 1 < 8:
                    load_g(q4 + 1)
                if pre_batch is not None:
                    pre_batch(kb)
                ps = PS[2 + kb % 2]
                pk = f"ps{2 + kb % 2}"
                for kk in range(2):
                    k1 = kb * 2 + kk
                    kl = k1 % 8
                    R = AT[:, k1, :]
                    I = AT[:, 64 + k1, :]
                    NI = AT[:, 128 + k1, :]
                    zr = ps[:, kk * 256:kk * 256 + 128]
                    zi = ps[:, kk * 256 + 128:kk * 256 + 256]
                    mm(zr, Gq[:, kl, 0, :], R, True, False, [akey, gkey], [pk])
                    mm(zr, Gq[:, kl, 1, :], NI, False, True, [akey, gkey], [pk])
                    mm(zi, Gq[:, kl, 1, :], R, True, False, [akey, gkey], [pk])
                    mm(zi, Gq[:, kl, 0, :], I, False, True, [akey, gkey], [pk])
                cb_batch(kb, ps, pk)

        def phase0():
          with ExitStack() as p0w:
            wst = [sb(f"wst{i}", [128, 8, 512], F32, p0w) for i in range(2)]
            wbf = [sb(f"wbf{i}", [128, 8, 512], BF16, p0w) for i in range(2)]
            n = 0
            jobs = []
            for cbk in range(8):
                jobs.append((w_mi.rearrange("(k p) n -> p k n", p=128)[:, :, cbk * 512:(cbk + 1) * 512],
                             Wmi_d.rearrange("(k p) n -> p k n", p=128)[:, :, cbk * 512:(cbk + 1) * 512]))
            for kr in range(4):
                for cbk in range(2):
                    jobs.append((w_mo.rearrange("(k p) n -> p k n", p=128)[:, kr * 8:(kr + 1) * 8, cbk * 512:(cbk + 1) * 512],
                                 (kr, cbk)))
            for src, dst in jobs:
                s_ = n % 2
                dma(wst[s_][:], src, [], [f"wst{s_}"])
                e = ("dve", "act")[n % 2]
                if e == "act":
                    S.op("act", lambda s_=s_: nc.scalar.copy(wbf[s_][:], wst[s_][:]), [f"wst{s_}"], [f"wbf{s_}"])
                else:
                    S.op(e, lambda s_=s_, e=e: V(e).tensor_copy(wbf[s_][:], wst[s_][:]), [f"wst{s_}"], [f"wbf{s_}"])
                if n < 8:
                    dma(dst, wbf[s_][:], [f"wbf{s_}"], ["Wmid"])
                else:
                    kr_, cbk_ = dst
                    for c_ in range(4):
                        dma(Wmo_d[cbk_ * 4 + c_, :, kr_ * 8:(kr_ + 1) * 8, :], wbf[s_][:, :, c_ * 128:(c_ + 1) * 128], [f"wbf{s_}"], ["Wmod"])
                n += 1
            S.barrier()
          with ExitStack() as p0:
            load_fft_consts(p0)

            w1s = sb("w1s", [33, 64], F32, p0); w2s = sb("w2s", [64, 64], F32, p0); w3s = sb("w3s", [64, 64], F32, p0)
            w4b = sb("w4b", [64, 2048], BF16, p0)
            fbs = sb("fbs", [64, 3], F32, p0); frs = sb("frs", [64, 1], F32, p0); ffb = sb("ffb", [64, 3], F32, p0)
            dls = sb("dls", [128, 16], F32, p0); ndl = sb("ndl", [128, 16], F32, p0)
            negpi = sb("negpi", [128, 1], F32, p0)
            H3 = sb("H3", [64, 2, L], BF16, p0)
            p0m = ExitStack()
            w4s = sb("w4s", [64, 2048], F32, p0m)
            for t_, d_, k_ in ((w1s, fw1, "w1s"), (w2s, fw2, "w2s"), (w3s, fw3, "w3s"), (w4s, fw4, "w4s"),
                               (fbs, fb, "fbs"), (frs, ffreq, "frs"), (dls, fdel, "dls")):
                dma(t_[:], d_, [], [k_])
            S.op("dve", lambda: nc.vector.tensor_copy(w4b[:], w4s[:]), ["w4s"], ["w4b"])
            S.op("dve", lambda: nc.vector.tensor_scalar(ffb[:], fbs[:], frs[:, 0:1], None, ALU.mult), ["fbs", "frs"], ["ffb"])
            S.op("dve", lambda: nc.vector.tensor_scalar(ndl[:], dls[:], -1.0, None, ALU.mult), ["dls"], ["ndl"])
            S.op("dve", lambda: nc.vector.tensor_tensor(ndl[:], ndl[:], dls[:], ALU.min), ["dls", "ndl"], ["ndl"])
            S.op("dve", lambda: nc.vector.memset(negpi[:], 0.0), [], ["negpi"])
            zc = [sb(f"zc{i}", [33, 512], F32, p0m) for i in range(4)]
            faL = [sb(f"fa{i}", [64, 512], F32, p0m) for i in range(2)]
            fbufL = [sb(f"fbuf{i}", [64, 512], F32, p0m) for i in range(2)]
            fhL = [sb(f"fh{i}", [64, 512], F32, p0m) for i in range(2)]

            def sin_layer(ps, li, dst, pk, v_, dkey):
                fa, fbuf = faL[v_], fbufL[v_]
                fak, fbk = f"fa{v_}", f"fbuf{v_}"
                S.op("dve", lambda: nc.vector.tensor_scalar(fa[:], ps[0:64, :], frs[:, 0:1], ffb[:, li:li + 1], ALU.mult, ALU.add),
                     [pk, "frs", "ffb"], [fak])
                S.op("dve", lambda: nc.vector.tensor_scalar(fbuf[:], fa[:], 1.0 / TWO_PI, MAGIC, ALU.mult, ALU.add), [fak], [fbk])
                S.op("dve", lambda: nc.vector.tensor_scalar(fbuf[:], fbuf[:], -MAGIC, -TWO_PI, ALU.add, ALU.mult), [fbk], [fbk])
                S.op("dve", lambda: nc.vector.tensor_tensor(fa[:], fa[:], fbuf[:], ALU.add), [fak, fbk], [fak])
                S.op("act", lambda: nc.scalar.activation(dst, fa[:], AF.Sin, bias=negpi[0:64, :], scale=1.0), [fak, "negpi"], [dkey])

            for j in range(16):
                for var in range(2):
                    s_ = (j % 2) * 2 + var
                    psm = PS[4 + var]
                    pmk = f"ps{4 + var}"
                    fh = fhL[var]
                    fhk = f"fh{var}"
                    dma(zc[s_][:], c_zT[var, :, j * 512:(j + 1) * 512], [], [f"zc{s_}"])
                    mm(psm[0:64, :], w1s[:], zc[s_][:], True, True, [f"zc{s_}", "w1s"], [pmk])
                    sin_layer(psm, 0, fh[:], pmk, var, fhk)
                    mm(psm[0:64, :], w2s[:], fh[:], True, True, [fhk, "w2s"], [pmk])
                    sin_layer(psm, 1, fh[:], pmk, var, fhk)
                    mm(psm[0:64, :], w3s[:], fh[:], True, True, [fhk, "w3s"], [pmk])
                    sin_layer(psm, 2, H3[:, var, j * 512:(j + 1) * 512], pmk, var, ("H3", var))

            S.barrier()
            p0m.close()
            tapsU = sb("tapsU", [128, NFFT], BF16, p0)
            ztmf = sb("ztmf", [128, 128, 128], BF16, p0)
            ATf = sb("ATf", [128, 192, 128], BF16, p0)
            tb = sb("tb", [128, 2, 512], F32, p0)
            jt = sb("jt", [128, 2, 16], F32, p0)
            for hf in range(2):
                dma(tb[:, hf, :], c_tb[hf:hf + 1, :].broadcast_to([128, 512]), [], ["tb"])
                dma(jt[:, hf, :], c_jt[hf:hf + 1, :].broadcast_to([128, 16]), [], ["jt"])
            wbase = [sb(f"wbase{i}", [128, 512], F32, p0) for i in range(2)]
            wcj = [sb(f"wcj{i}", [128, 16], F32, p0) for i in range(2)]
            tpf = [sb(f"tpf{i}", [128, 512], F32, p0) for i in range(2)]
            psum_ = sb("psum_", [128, 32], F32, p0)
            nrm = sb("nrm", [128, 2], F32, p0)
            Hst = [sb(f"Hst{i}", [128, 2, 4, 128], BF16, p0) for i in range(4)]
            for go in range(8):
                g, o = go // 2, go % 2
                for half in range(2):
                    col0 = o * 1024 + half * 512 + g * 128
                    ct = col0 // 128
                    S.op("act", lambda half=half, ct=ct: nc.scalar.activation(wbase[half][:], tb[:, half, :], AF.Exp, scale=ndl[:, ct:ct + 1]),
                         ["tb", "ndl"], [f"wbase{half}"])
                    S.op("act", lambda half=half, ct=ct: nc.scalar.activation(wcj[half][:], jt[:, half, :], AF.Exp, scale=ndl[:, ct:ct + 1]),
                         ["jt", "ndl"], [f"wcj{half}"])
                    for j in range(16):
                        s_ = j % 2
                        pst = PS[4 + s_]
                        ptk = f"ps{4 + s_}"
                        mm(pst[:, :], w4b[:, col0:col0 + 128], H3[:, half, j * 512:(j + 1) * 512], True, True,
                           [("H3", half), "w4b"], [ptk])
                        S.op("dve", lambda s_=s_, pst=pst, half=half, j=j: nc.vector.scalar_tensor_tensor(
                            out=tpf[s_][:], in0=pst[:, :], scalar=wcj[half][:, j:j + 1], in1=wbase[half][:], op0=ALU.mult, op1=ALU.mult),
                            [ptk, f"wcj{half}", f"wbase{half}"], [f"tpf{s_}"])
                        idx = half * 16 + j
                        if half == 1 and j == 0:
                            S.op("dve", lambda s_=s_: nc.vector.memset(tpf[s_][:, 0:1], 0.0), [f"tpf{s_}"], [f"tpf{s_}"])
                        S.op("dve", lambda idx=idx, s_=s_: nc.vector.tensor_reduce(psum_[:, idx:idx + 1], tpf[s_][:], mybir.AxisListType.X, ALU.add,
                                                                                  apply_absolute_value=True), [f"tpf{s_}"], ["psum_"])
                        S.op("act", lambda half=half, j=j, s_=s_: nc.scalar.copy(tapsU[:, half * L + j * 512: half * L + (j + 1) * 512], tpf[s_][:]),
                             [f"tpf{s_}"], ["tapsU"])
                S.op("dve", lambda: nc.vector.tensor_reduce(nrm[:, 0:1], psum_[:], mybir.AxisListType.X, ALU.add), ["psum_"], ["nrm"])
                S.op("dve", lambda: nc.vector.reciprocal(nrm[:, 0:1], nrm[:, 0:1]), ["nrm"], ["nrm"])
                S.op("dve", lambda: nc.vector.tensor_scalar(nrm[:, 1:2], nrm[:, 0:1], -1.0, None, ALU.mult), ["nrm"], ["nrm"])
                S.op("dve", lambda: nc.vector.tensor_scalar(tapsU[:, 0:L], tapsU[:, 0:L], nrm[:, 0:1], None, ALU.mult), ["tapsU", "nrm"], ["tapsU"])
                S.op("act", lambda: nc.scalar.mul(tapsU[:, L:NFFT], tapsU[:, L:NFFT], nrm[:, 1:2]), ["tapsU", "nrm"], ["tapsU"])
                tv = taps_d.rearrange("c (n1 n2) -> n1 c n2", n2=128)
                for q4 in range(4):
                    dma(taps_d[q4 * 32:(q4 + 1) * 32, :], tapsU[q4 * 32:(q4 + 1) * 32, :], ["tapsU"], [("taps_d", q4)])
                for q4 in range(4):
                    dma(ztmf[:, q4 * 32:(q4 + 1) * 32, :], tv[:, q4 * 32:(q4 + 1) * 32, :], [("taps_d", q4)], [("ztm", q4)])

                def cb_filter(kb, ps, pk, go=go):
                    s_ = kb % 4
                    zv = ps[:, :].rearrange("p (k r c) -> p k r c", k=2, r=2)
                    S.op("act", lambda: nc.scalar.copy(Hst[s_][:, :, 0:2, :], zv), [pk], [f"Hst{s_}"])
                    S.op("dve", lambda: nc.vector.tensor_copy(Hst[s_][:, :, 2, :], zv[:, :, 1, :]), [pk], [f"Hst{s_}"])
                    S.op("dve", lambda: nc.vector.tensor_copy(Hst[s_][:, :, 3, :], zv[:, :, 0, :]), [pk], [f"Hst{s_}"])
                    dma(H_d[go, :, kb * 2:kb * 2 + 2], Hst[s_][:], [f"Hst{s_}"], [("H_d", go)])

                fft_forward(ztmf, 128, ATf, cb_filter)
            S.barrier()
        def rms_apply(src, gvec, dst, scale, nk, tag, sqb, rs, np_=128, ss_ps=0):
            S.op("act", lambda: nc.scalar.activation(sqb[0:np_, 0:nk, :], src, AF.Square), [tag + "_src"], ["sqb"])
            for k in range(nk):
                mm(PS[ss_ps][0:np_, :], ones[0:np_, 0:np_], sqb[0:np_, k, :], k == 0, k == nk - 1, ["sqb", "ones"], [f"ps{ss_ps}"])
            S.op("act", lambda: nc.scalar.activation(rs[0:np_, :], PS[ss_ps][0:np_, :], AF.Sqrt, bias=epsT[0:np_, :], scale=scale),
                 [f"ps{ss_ps}", "epsT"], ["rs"])
            S.op("dve", lambda: nc.vector.reciprocal(rs[0:np_, :], rs[0:np_, :]), ["rs"], ["rs"])
            for k in range(nk):
                e = "dve"
                S.op(e, lambda k=k, e=e: V(e).scalar_tensor_tensor(out=dst[:, k, :], in0=src[:, k, :], scalar=gvec[:, k:k + 1],
                                                                  in1=rs[0:np_, :], op0=ALU.mult, op1=ALU.mult),
                     [tag + "_src", "rs", "gvecs"], [tag + "_dst"])

        def phase12():
          with ExitStack() as p12:
            KT = sb("KT", [64, 2, L], BF16, p12)
            Vx = sb("Vx", [128, 64, 2, 65], BF16, p12)
            QT = sb("QT", [64, 8, NTOK], BF16, p12)
            S.op("pool", lambda: nc.gpsimd.memset(Vx[:, :, :, 64:65], 1.0), [], ["Vx1"])
            with ExitStack() as p1:
                Wb = sb("Wb", [128, 8, 2304], BF16, p1)
                with ExitStack() as p1c:
                    wst = [sb(f"wsi{i}", [128, 8, 384], F32, p1c) for i in range(2)]
                    wv = w_in.rearrange("(k p) n -> p k n", p=128)
                    for bk in range(6):
                        s_ = bk % 2
                        dma(wst[s_][:], wv[:, :, bk * 384:(bk + 1) * 384], [], [f"wst{s_}"])
                        if bk % 2 == 0:
                            S.op("dve", lambda s_=s_, bk=bk: nc.vector.tensor_copy(Wb[:, :, bk * 384:(bk + 1) * 384], wst[s_][:]), [f"wst{s_}"], ["Wb"])
                        else:
                            S.op("act", lambda s_=s_, bk=bk: nc.scalar.copy(Wb[:, :, bk * 384:(bk + 1) * 384], wst[s_][:]), [f"wst{s_}"], ["Wb"])
                    S.barrier()
                xin = [sb(f"xin{i}", [128, 8, 512], F32, p1) for i in range(2)]
                sqb = sb("sqb", [128, 8, 512], F32, p1)
                rs = sb("rs", [128, 512], F32, p1)
                aT = sb("aT", [128, 8, 512], BF16, p1)
                ust = sb("ust", [128, 12, 512], BF16, p1)
                hsq = sb("hsq", [64, 512], F32, p1); hrs = sb("hrs", [64, 512], F32, p1); hkn = sb("hkn", [64, 512], F32, p1)
                ht1 = sb("ht1", [64, 512], F32, p1); ht2 = sb("ht2", [64, 512], F32, p1)
                csb = [sb(f"csb{i}", [64, 2, 512], F32, p1) for i in range(2)]

                def headproc(gvec, cs, cskey, dst, dstkey):
                    src = PS[1][0:64, :]
                    S.op("act", lambda: nc.scalar.activation(hsq[:], src, AF.Square), ["ps1"], ["hsq"])
                    mm(PS[2][0:64, :], ones[0:64, 0:64], hsq[:], True, True, ["hsq", "ones"], ["ps2"])
                    S.op("act", lambda: nc.scalar.activation(hrs[:], PS[2][0:64, :], AF.Sqrt, bias=epsT[0:64, :], scale=1.0 / 64.0),
                         ["ps2", "epsT"], ["hrs"])
                    S.op("dve", lambda: nc.vector.reciprocal(hrs[:], hrs[:]), ["hrs"], ["hrs"])
                    S.op("dve", lambda: nc.vector.scalar_tensor_tensor(out=hkn[:], in0=src, scalar=gvec[:, 0:1], in1=hrs[:],
                                                                      op0=ALU.mult, op1=ALU.mult), ["ps1", "hrs", "gvecs"], ["hkn"])
                    mm(PS[3][0:64, :], rot[:], hkn[:], True, True, ["hkn", "rot"], ["ps3"])
                    S.op("pool", lambda: nc.gpsimd.tensor_tensor(ht1[:], hkn[:], cs[:, 0, :], ALU.mult), ["hkn", cskey], ["ht1"])
                    S.op("dve", lambda: nc.vector.tensor_tensor(ht2[:], PS[3][0:64, :], cs[:, 1, :], ALU.mult), ["ps3", cskey], ["ht2"])
                    S.op("pool", lambda: nc.gpsimd.tensor_tensor(dst, ht1[:], ht2[:], ALU.add), ["ht1", "ht2"], [dstkey])

                xv = xT.rearrange("(k p) t -> p k t", p=128)
                Uv = U_d.rearrange("(ct p) t -> p ct t", p=128)
                for j in range(16):
                    s_ = j % 2
                    tsl = slice(j * 512, (j + 1) * 512)
                    dma(xin[s_][:], xv[:, :, tsl], [], [f"xin{s_}", "n1_src"])
                    dma(csb[s_][:, 0, :], c_cos[:, tsl], [], [f"csb{s_}"])
                    dma(csb[s_][:, 1, :], c_sin[:, tsl], [], [f"csb{s_}"])
                    rms_apply(xin[s_][:], g1s, aT, 1.0 / D, 8, "n1", sqb, rs)
                    for kvh in range(2):
                        for k in range(8):
                            mm(PS[1][0:64, :], Wb[:, k, 512 + kvh * 64:512 + (kvh + 1) * 64], aT[:, k, :], k == 0, k == 7,
                               ["Wb", "n1_dst"], ["ps1"])
                        headproc(gks, csb[s_], f"csb{s_}", KT[:, kvh, tsl], "KT")
                    for tt in range(4):
                        for k in range(8):
                            mm(PS[4][:, tt * 128:(tt + 1) * 128], aT[:, k, tt * 128:(tt + 1) * 128], Wb[:, k, 640:768], k == 0, k == 7,
                               ["Wb", "n1_dst"], ["ps4"])
                    S.op("act", lambda j=j: nc.scalar.copy(Vx[:, j * 4:(j + 1) * 4, :, 0:64],
                                                         PS[4][:, :].rearrange("p (t h d) -> p t h d", t=4, h=2)), ["ps4"], ["Vx"])
                    for ct in range(12):
                        for k in range(8):
                            mm(PS[5][:, :], Wb[:, k, 768 + ct * 128:768 + (ct + 1) * 128], aT[:, k, :], k == 0, k == 7,
                               ["Wb", "n1_dst"], ["ps5"])
                        if ct % 2 == 0:
                            S.op("act", lambda ct=ct: nc.scalar.copy(ust[:, ct, :], PS[5][:, :]), ["ps5"], ["ust"])
                        else:
                            S.op("dve", lambda ct=ct: nc.vector.tensor_copy(ust[:, ct, :], PS[5][:, :]), ["ps5"], ["ust"])
                    dma(Uv[:, :, tsl], ust[:], ["ust"], ["U_d"])
                xqv = xTq.rearrange("(k p) t -> p k t", p=128)
                for j in range(4):
                    s_ = j % 2
                    tsl = slice(j * 512, (j + 1) * 512)
                    dma(xin[s_][:], xqv[:, :, tsl], [], [f"xin{s_}", "n1_src"])
                    dma(csb[s_][:, 0, :], c_cosq[:, tsl], [], [f"csb{s_}"])
                    dma(csb[s_][:, 1, :], c_sinq[:, tsl], [], [f"csb{s_}"])
                    rms_apply(xin[s_][:], g1s, aT, 1.0 / D, 8, "n1", sqb, rs)
                    for h in range(8):
                        for k in range(8):
                            mm(PS[1][0:64, :], Wb[:, k, h * 64:(h + 1) * 64], aT[:, k, :], k == 0, k == 7, ["Wb", "n1_dst"], ["ps1"])
                        headproc(gqs, csb[s_], f"csb{s_}", QT[:, h, tsl], "QT")
                S.barrier()
            if 2 in phases:
              with ExitStack() as p2:
                pb = [sb(f"pb{i}", [128, 1024], BF16, p2) for i in range(2)]
                osb = sb("osb", [65, 512], F32, p2)
                rden = sb("rden", [64, 512], F32, p2)
                ast = [sb(f"ast{i}", [64, 512], BF16, p2) for i in range(2)]
                n_ = 0
                NG = 32
                for qc in range(4):
                    qsl = slice(qc * 512, (qc + 1) * 512)
                    for h in range(8):
                        kvh = h // 4
                        for step in range(NG + 1):
                            if step < NG:
                                g_ = step
                                sl = g_ % 2
                                for t_ in range(2):
                                    kt = 2 * g_ + t_
                                    mm(PS[2 * sl + t_][:, :], KT[:, kvh, kt * 128:(kt + 1) * 128], QT[:, h, qsl], True, True,
                                       ["KT", "QT"], [f"ps{2 * sl + t_}"])
                                S.op("act", lambda sl=sl: nc.scalar.activation(pb[sl][:], PSW[sl][:, :], AF.Exp, scale=0.125),
                                     [f"ps{2 * sl}", f"ps{2 * sl + 1}"], [f"pb{sl}"])
                            g_ = step - 1
                            if g_ >= 0:
                                sl = g_ % 2
                                for t_ in range(2):
                                    kt = 2 * g_ + t_
                                    mm(PS[4][0:65, :], Vx[:, kt, kvh, :], pb[sl][:, t_ * 512:(t_ + 1) * 512], kt == 0, kt == 63,
                                       ["Vx", "Vx1", f"pb{sl}"], ["ps4"])
                        S.op("dve", lambda: nc.vector.tensor_copy(osb[:], PS[4][0:65, :]), ["ps4"], ["osb"])
                        mm(PS[5][0:64, :], sel[:], osb[:], True, True, ["osb", "sel"], ["ps5"])
                        S.op("dve", lambda: nc.vector.reciprocal(rden[:], PS[5][0:64, :]), ["ps5"], ["rden"])
                        a_ = n_ % 2
                        S.op("pool", lambda a_=a_: nc.gpsimd.tensor_tensor(ast[a_][:], osb[0:64, :], rden[:], ALU.mult), ["osb", "rden"], [f"ast{a_}"])
                        dma(att_d[h * 64:(h + 1) * 64, qsl], ast[a_][:], [f"ast{a_}"], ["att_d"])
                        n_ += 1
                S.barrier()

        def phase3():
          with ExitStack() as p3:
            load_fft_consts(p3)
            finv = FC["finv"]
            M2s = sb("M2s", [128, 128, 64], BF16, p3)
            M2qs = sb("M2qs", [128, 128, 16], BF16, p3)
            dma(M2s[:], c_M2, [], ["M2s"])
            dma(M2qs[:], c_M2q, [], ["M2qs"])
            RA = sb("RA", [128, 192 * 128 + 8], BF16, p3)
            RB = sb("RB", [128, 128 * 128], BF16, p3)
            RC = sb("RC", [128, 128 * 128], BF16, p3)
            vt = sb("vt", [128, L], BF16, p3)
            x1t = sb("x1t", [128, L], BF16, p3)
            x2q = sb("x2q", [128, NTOK], BF16, p3)
            zzq = sb("zzq", [128, NTOK], BF16, p3)
            hyst = sb("hyst", [128, NTOK], BF16, p3)
            Hc = [sb(f"Hc{i}", [128, 2, 4, 128], BF16, p3) for i in range(4)]
            Pb = sb("Pb", [128, 2, 2, 128], F32, p3)
            Qb = sb("Qb", [128, 2, 2, 128], F32, p3)
            Yb = [sb(f"Yb{i}", [128, 2, 2, 128], BF16, p3) for i in range(2)]
            gt = [sb(f"gt{i}", [128, 512], F32, p3) for i in range(2)]
            sct = gt
            uraw = RA[:, 0:3 * 8194].rearrange("p (i t) -> p i t", i=3)
            AT = RA[:, 0:192 * 128].rearrange("p (j c) -> p j c", c=128)
            ztm = RB[:, :].rearrange("p (c n) -> p c n", n=128)
            Eb = RB[:, :].rearrange("p (n c) -> p n c", c=128)
            x2t = RC[:, 0:L]
            ET = RC[:, :].rearrange("p (c k) -> p c k", k=128)
            RAK, RBK, RCK = "RA", "RB", "RC"

            def masked_quarter(dst, src, dkey, skey):
                S.op("dve", lambda: nc.vector.tensor_scalar(dst[:], src[:, 0:NTOK], qms[:, 0:1], None, ALU.mult), [skey, "qms"], [dkey])
                for q in range(1, 4):
                    S.op("dve", lambda q=q: nc.vector.scalar_tensor_tensor(out=dst[:], in0=src[:, q * NTOK:(q + 1) * NTOK], scalar=qms[:, q:q + 1],
                                                                          in1=dst[:], op0=ALU.mult, op1=ALU.add), [skey, "qms", dkey], [dkey])

            RBQ = [(RBK, q4) for q4 in range(4)]

            def bounce_to_ztm(src, skey):
                zv = zcm_d.rearrange("c (n1 n2) -> n1 c n2", n2=128)
                for q4 in range(4):
                    dma(zcm_d[q4 * 32:(q4 + 1) * 32, :], src[q4 * 32:(q4 + 1) * 32, :], [skey], [("zcm_d", q4)])
                for q4 in range(4):
                    dma(ztm[0:64, q4 * 32:(q4 + 1) * 32, :], zv[:, q4 * 32:(q4 + 1) * 32, :], [("zcm_d", q4)], [(RBK, q4)])

            def conv_core(go):
                def load_h(kb):
                    dma(Hc[kb % 4][:], H_d[go, :, kb * 2:kb * 2 + 2], [("H_d", go)], [f"Hc{kb % 4}"])

                def pre(kb):
                    if kb == 0:
                        load_h(0)
                        load_h(1)
                    if kb + 2 < 32:
                        load_h(kb + 2)

                def cb(kb, ps, pk):
                    s_ = kb % 4
                    y_ = kb % 2
                    zv = ps[:, :].rearrange("p (k r c) -> p k r c", k=2, r=2)
                    S.op("dve", lambda: nc.vector.tensor_tensor(Pb[:], zv, Hc[s_][:, :, 0:2, :], ALU.mult), [pk, f"Hc{s_}"], ["Pb"])
                    S.op("dve", lambda: nc.vector.tensor_tensor(Qb[:], zv, Hc[s_][:, :, 2:4, :], ALU.mult), [pk, f"Hc{s_}"], ["Qb"])
                    S.op("pool", lambda: nc.gpsimd.tensor_tensor(Yb[y_][:, :, 0, :], Pb[:, :, 0, :], Pb[:, :, 1, :], ALU.subtract), ["Pb"], [f"Yb{y_}"])
                    S.op("pool", lambda: nc.gpsimd.tensor_tensor(Yb[y_][:, :, 1, :], Qb[:, :, 0, :], Qb[:, :, 1, :], ALU.add), ["Qb"], [f"Yb{y_}"])
                    pe_ = PS[4 + kb % 2]
                    pek = f"ps{4 + kb % 2}"
                    er = pe_[:, 0:256].rearrange("p (k c) -> p k c", k=2)
                    ei = pe_[:, 256:512].rearrange("p (k c) -> p k c", k=2)
                    yr = Yb[y_][:, :, 0, :]
                    yi = Yb[y_][:, :, 1, :]
                    mm(er, finv[:, 0, :], yr, True, False, [f"Yb{y_}", "finv"], [pek])
                    mm(er, finv[:, 2, :], yi, False, True, [f"Yb{y_}", "finv"], [pek])
                    mm(ei, finv[:, 1, :], yr, True, False, [f"Yb{y_}", "finv"], [pek])
                    mm(ei, finv[:, 0, :], yi, False, True, [f"Yb{y_}", "finv"], [pek])
                    S.op("act", lambda: nc.scalar.copy(ET[:, :, kb * 2:kb * 2 + 2], pe_[:, 0:256].rearrange("p (k c) -> p c k", k=2)), [pek], [RCK])
                    S.op("dve", lambda: nc.vector.tensor_copy(ET[:, :, 64 + kb * 2:64 + kb * 2 + 2], pe_[:, 256:512].rearrange("p (k c) -> p c k", k=2)),
                         [pek], [RCK])
                fft_forward(ztm, 64, AT, cb, zkey=RBK, akey=RAK, pre_batch=pre)
                for c4 in range(32):
                    pb_ = PSB[c4 % 2]
                    pbk = f"psb{c4 % 2}"
                    for cc in range(4):
                        c = c4 * 4 + cc
                        S.op("pe", lambda c=c, cc=cc, pb_=pb_: nc.tensor.transpose(pb_[:, cc * 128:(cc + 1) * 128], ET[:, c, :], ident[:]),
                             [RCK, "ident"], [pbk])
                    src = pb_[:, 0:512].rearrange("p (cc n) -> p n cc", cc=4)
                    dst = Eb[:, :, c4 * 4:c4 * 4 + 4]
                    if c4 % 2 == 0:
                        S.op("act", lambda src=src, dst=dst: nc.scalar.copy(dst, src), [pbk], RBQ)
                    else:
                        S.op("dve", lambda src=src, dst=dst: nc.vector.tensor_copy(dst, src), [pbk], RBQ)

            Udv = U_d
            for g in range(4):
                for i in range(3):
                    dma(uraw[:, i, 1:L + 1], Udv[i * 512 + g * 128:i * 512 + (g + 1) * 128, :], ["U_d"], [RAK])
                S.op("pool", lambda: nc.gpsimd.memset(uraw[:, :, 0:1], 0.0), [RAK], [RAK])
                S.op("pool", lambda: nc.gpsimd.memset(uraw[:, :, L + 1:L + 2], 0.0), [RAK], [RAK])
                for i, (dst, dkey) in enumerate(((vt, "vt"), (x1t, "x1t"), (x2t, RCK))):
                    ct = i * 4 + g
                    for q in range(16):
                        e = "dve"
                        st = sct[q % 2]
                        sk = f"gt{q % 2}"
                        o0 = q * 512
                        S.op(e, lambda e=e, st=st, i=i, o0=o0, ct=ct: V(e).tensor_scalar(st[:], uraw[:, i, o0:o0 + 512], cws[:, ct, 0:1], cbs[:, ct:ct + 1],
                                                                                      ALU.mult, ALU.add), [RAK, "cws", "cbs"], [sk])
                        S.op(e, lambda e=e, st=st, i=i, o0=o0, ct=ct: V(e).scalar_tensor_tensor(out=st[:], in0=uraw[:, i, o0 + 1:o0 + 513], scalar=cws[:, ct, 1:2],
                                                                                             in1=st[:], op0=ALU.mult, op1=ALU.add), [RAK, "cws", sk], [sk])
                        S.op(e, lambda e=e, st=st, i=i, o0=o0, ct=ct, dst=dst: V(e).scalar_tensor_tensor(out=dst[:, o0:o0 + 512], in0=uraw[:, i, o0 + 2:o0 + 514],
                                                                                                      scalar=cws[:, ct, 2:3], in1=st[:], op0=ALU.mult, op1=ALU.add),
                             [RAK, "cws", sk], [dkey])
                masked_quarter(x2q, x2t, "x2q", RCK)
                snap("vt0", vt[:], "vt", L)
                snap("x1t0", x1t[:], "x1t", L)
                bounce_to_ztm(vt, "vt")
                conv_core(g * 2 + 0)
                vt3 = vt[:, :].rearrange("p (n1 n2) -> p n1 n2", n2=128)
                x13 = x1t[:, :].rearrange("p (n1 n2) -> p n1 n2", n2=128)
                for nb in range(16):
                    ps = PS[nb % 2]
                    pk = f"ps{nb % 2}"
                    for j in range(8):
                        n2 = nb * 8 + j
                        mm(ps[:, j * 64:(j + 1) * 64], Eb[:, n2, :], M2s[:, n2, :], True, True, RBQ + ["M2s"], [pk])
                    psv = ps[:, :].rearrange("p (j n) -> p n j", j=8)
                    gtv = gt[nb % 2][:, :].rearrange("p (n j) -> p n j", j=8)
                    gk_ = f"gt{nb % 2}"
                    S.op("dve", lambda psv=psv, gtv=gtv, nb=nb, g=g: nc.vector.scalar_tensor_tensor(out=gtv, in0=vt3[:, :, nb * 8:(nb + 1) * 8],
                                                                                             scalar=skps[:, g:g + 1], in1=psv, op0=ALU.mult, op1=ALU.add),
                         [pk, "vt", "skps"], [gk_])
                    S.op("pool", lambda gtv=gtv, nb=nb: nc.gpsimd.tensor_tensor(vt3[:, :, nb * 8:(nb + 1) * 8], gtv, x13[:, :, nb * 8:(nb + 1) * 8], ALU.mult),
                         [gk_, "x1t"], ["vt"])
                snap("zz0", vt[:], "vt", L)
                masked_quarter(zzq, vt, "zzq", "vt")
                bounce_to_ztm(vt, "vt")
                conv_core(g * 2 + 1)
                zq3 = zzq[:, :].rearrange("p (n1 n2) -> p n1 n2", n2=128)
                xq3 = x2q[:, :].rearrange("p (n1 n2) -> p n1 n2", n2=128)
                hy3 = hyst[:, :].rearrange("p (n1 n2) -> p n1 n2", n2=128)
                for nb in range(4):
                    ps = PS[nb % 2]
                    pk = f"ps{nb % 2}"
                    for j in range(32):
                        n2 = nb * 32 + j
                        mm(ps[:, j * 16:(j + 1) * 16], Eb[:, n2, :], M2qs[:, n2, :], True, True, RBQ + ["M2qs"], [pk])
                    psv = ps[:, :].rearrange("p (j n) -> p n j", j=32)
                    gtv = gt[nb % 2][:, :].rearrange("p (n j) -> p n j", j=32)
                    gk_ = f"gt{nb % 2}"
                    S.op("dve", lambda psv=psv, gtv=gtv, nb=nb, g=g: nc.vector.scalar_tensor_tensor(out=gtv, in0=zq3[:, :, nb * 32:(nb + 1) * 32],
                                                                                             scalar=skps[:, 4 + g:5 + g], in1=psv, op0=ALU.mult, op1=ALU.add),
                         [pk, "zzq", "skps"], [gk_])
                    S.op("pool", lambda gtv=gtv, nb=nb: nc.gpsimd.tensor_tensor(hy3[:, :, nb * 32:(nb + 1) * 32], gtv, xq3[:, :, nb * 32:(nb + 1) * 32], ALU.mult),
                         [gk_, "x2q"], ["hyst"])
                dma(hy_d[g * 128:(g + 1) * 128, :], hyst[:], ["hyst"], ["hy_d"])
            S.barrier()

        def phase4():
          with ExitStack() as p4:
            WoA = sb("WoA", [64, 8, D], BF16, p4)
            WoH = sb("WoH", [128, 4, D], BF16, p4)
            with ExitStack() as p4c:
                wsa = sb("wsa", [64, 8, D], F32, p4c)
                wsh = sb("wsh", [128, 4, D], F32, p4c)
                dma(wsa[:], w_out[0:512, :].rearrange("(h d) n -> d h n", d=64), [], ["wsa"])
                dma(wsh[:], w_out[512:1024, :].rearrange("(g p) n -> p g n", p=128), [], ["wsh"])
                S.op("dve", lambda: nc.vector.tensor_copy(WoA[:], wsa[:]), ["wsa"], ["WoA"])
                S.op("pool", lambda: nc.gpsimd.tensor_copy(WoH[:], wsh[:]), ["wsh"], ["WoH"])
                S.barrier()
            attc = sb("attc", [64, 8, 512], BF16, p4)
            hyc = sb("hyc", [128, 4, 512], BF16, p4)
            sqb = sb("sqb4", [128, 8, 512], F32, p4)
            rs = sb("rs4", [128, 512], F32, p4)
            mixA = sb("mixA", [64, 8, 512], BF16, p4)
            mixH = sb("mixH", [128, 4, 512], BF16, p4)
            h1 = sb("h1", [128, 8, 512], F32, p4)
            mT = sb("mT", [128, 8, 512], BF16, p4)
            Wmi_s = [sb(f"Wmi_s{i}", [128, 8, 1024], BF16, p4) for i in range(2)]
            Wmo_s = [sb(f"Wmo_s{i}", [128, 32, 128], BF16, p4) for i in range(2)]
            fT = sb("fT", [128, 32, 512], BF16, p4)
            rb = [sb(f"rb{i}", [128, 512], F32, p4) for i in range(2)]
            xqv = xTq.rearrange("(k p) t -> p k t", p=128)
            Wmiv = Wmi_d.rearrange("(k p) n -> p k n", p=128)
            outv = outT.rearrange("(k p) t -> p k t", p=128)
            nw = 0
            nwo = 0
            for j in range(4):
                tsl = slice(j * 512, (j + 1) * 512)
                dma(attc[:], att_d[:, tsl].rearrange("(h d) t -> d h t", d=64), ["att_d"], ["attc", "na_src"])
                dma(hyc[:], hy_d[:, tsl].rearrange("(g p) t -> p g t", p=128), ["hy_d"], ["hyc", "nh_src"])
                dma(h1[:], xqv[:, :, tsl], [], ["h1", "n2_src", "nf_src"])
                rms_apply(attc[:], gas, mixA, 1.0 / 512.0, 8, "na", sqb, rs, np_=64)
                rms_apply(hyc[:], ghs, mixH, 1.0 / 512.0, 4, "nh", sqb, rs)
                for ct in range(8):
                    csl = slice(ct * 128, (ct + 1) * 128)
                    for h in range(8):
                        mm(PS[1][:, :], WoA[:, h, csl], mixA[:, h, :], h == 0, False, ["WoA", "na_dst"], ["ps1"])
                    for g in range(4):
                        mm(PS[1][:, :], WoH[:, g, csl], mixH[:, g, :], False, g == 3, ["WoH", "nh_dst"], ["ps1"])
                    S.op("dve", lambda ct=ct: nc.vector.tensor_tensor(h1[:, ct, :], h1[:, ct, :], PS[1][:, :], ALU.add), ["ps1", "h1"], ["h1", "n2_src", "nf_src"])
                rms_apply(h1[:], g2s, mT, 1.0 / D, 8, "n2", sqb, rs)
                for pc in range(4):
                    s_ = nw % 2
                    dma(Wmi_s[s_][:], Wmiv[:, :, pc * 1024:(pc + 1) * 1024], ["Wmid"], [f"Wmi_s{s_}"])
                    nw += 1
                    for ft in range(8):
                        f = pc * 8 + ft
                        pf = PS[2 + f % 2]
                        pfk = f"ps{2 + f % 2}"
                        for k in range(8):
                            mm(pf[:, :], Wmi_s[s_][:, k, ft * 128:(ft + 1) * 128], mT[:, k, :], k == 0, k == 7, [f"Wmi_s{s_}", "n2_dst"], [pfk])
                        r_ = f % 2
                        S.op("act", lambda pf=pf, r_=r_: nc.scalar.activation(rb[r_][:], pf[:, :], AF.Relu), [pfk], [f"rb{r_}"])
                        e = ("pool", "dve")[f % 2]
                        S.op(e, lambda e=e, r_=r_, f=f: V(e).tensor_tensor(fT[:, f, :], rb[r_][:], rb[r_][:], ALU.mult), [f"rb{r_}"], ["fT"])
                for ct in range(8):
                    s_ = nwo % 2
                    dma(Wmo_s[s_][:], Wmo_d[ct], ["Wmod"], [f"Wmo_s{s_}"])
                    nwo += 1
                    po = PS[4 + ct % 2]
                    pok = f"ps{4 + ct % 2}"
                    for f in range(32):
                        mm(po[:, :], Wmo_s[s_][:, f, :], fT[:, f, :], f == 0, f == 31, [f"Wmo_s{s_}", "fT"], [pok])
                    S.op("dve", lambda ct=ct, po=po: nc.vector.tensor_tensor(h1[:, ct, :], h1[:, ct, :], po[:, :], ALU.add), [pok, "h1"], ["h1", "n2_src", "nf_src"])
                rms_apply(h1[:], gfs, h1, 1.0 / D, 8, "nf", sqb, rs)
                dma(outv[:, :, tsl], h1[:], ["h1", "nf_dst"], ["outT"])
            S.barrier()

        if 0 in phases:
            phase0()
        if 1 in phases:
            phase12()
        if 3 in phases:
            phase3()
        if 4 in phases:
            phase4()
        if debug:
            with ExitStack() as pd:
                for name in debug:
                    if name not in ("att_d", "hy_d", "U_d", "H_d0"):
                        continue
                    src_, shp = {"att_d": (att_d, [512, NTOK]), "hy_d": (hy_d, [512, NTOK]), "U_d": (U_d[0:512, :], [512, L]),
                                 "H_d0": (H_d[0:4, :, 0:4].rearrange("a p k r c -> (a p) (k r c)"), [512, 2048])}[name]
                    dbo = nc.dram_tensor("dbg_" + name, shp, BF16, kind="ExternalOutput").ap()
                    dbs = sb("dbs_" + name, [128, 4, shp[1]], BF16, pd)
                    dma(dbs[:], src_.rearrange("(a p) t -> p a t", p=128), [name], ["dbs_" + name])
                    dma(dbo.rearrange("(a p) t -> p a t", p=128), dbs[:], ["dbs_" + name], ["dbg_" + name])
                S.barrier()
        S.barrier()
    return nc, S


def _core_inputs(inp, core):
    T = _tables()
    b, r = core // 4, core % 4
    f = lambda a: np.ascontiguousarray(np.asarray(a, dtype=np.float32))
    x = np.asarray(inp["x"], dtype=np.float32)
    xT = np.ascontiguousarray(x[b].T)
    tq = slice(r * NTOK, (r + 1) * NTOK)
    qm = np.zeros((128, 4), np.float32)
    qm[:, r] = 1.0
    m = {
        "xT": xT, "xTq": np.ascontiguousarray(xT[:, tq]),
        "w_in": f(inp["w_in"][0]), "w_out": f(inp["w_out"][0]), "w_mi": f(inp["w_mlp_in"][0]), "w_mo": f(inp["w_mlp_out"][0]),
        "g1": f(np.asarray(inp["norm1_g"][0]).reshape(8, 128).T), "g2": f(np.asarray(inp["norm2_g"][0]).reshape(8, 128).T),
        "gf": f(np.asarray(inp["final_g"]).reshape(8, 128).T),
        "gq": f(np.asarray(inp["q_norm_g"][0]).reshape(64, 1)), "gk": f(np.asarray(inp["k_norm_g"][0]).reshape(64, 1)),
        "ga": f(np.asarray(inp["attn_out_g"][0]).reshape(8, 64).T), "gh": f(np.asarray(inp["hy_out_g"][0]).reshape(4, 128).T),
        "cw": f(np.asarray(inp["hy_conv_w"][0]).T.reshape(12, 128, 3).transpose(1, 0, 2)),
        "cb": f(np.asarray(inp["hy_conv_b"][0]).reshape(12, 128).T),
        "fw1": f(inp["filt_w1"][0]), "fw2": f(inp["filt_w2"][0]), "fw3": f(inp["filt_w3"][0]), "fw4": f(inp["filt_w4"][0]),
        "fb": f(np.stack([np.asarray(inp["filt_b1"][0]), np.asarray(inp["filt_b2"][0]), np.asarray(inp["filt_b3"][0])], axis=1)),
        "ffreq": f(np.asarray(inp["filt_freq"][0]).reshape(64, 1)),
        "fdel": f(np.asarray(inp["filt_deltas"][0]).reshape(16, 128).T),
        "skp": f(np.asarray(inp["hy_skip_d"][0]).reshape(2, 4, 128).transpose(2, 0, 1).reshape(128, 8)),
        "qmask": qm,
        "c_cos": T["cosT"], "c_sin": T["sinT"],
        "c_cosq": np.ascontiguousarray(T["cosT"][:, tq]), "c_sinq": np.ascontiguousarray(T["sinT"][:, tq]),
        "c_zT": np.ascontiguousarray(np.stack([T["zT"], T["zrT"]])), "c_t": np.ascontiguousarray(np.stack([T["t"], T["trev"]])),
        "c_tb": T["tb"], "c_jt": T["jt"],
        "c_fpack": T["fpack"], "c_G": T["G"], "c_finv": T["finv"], "c_M2": T["M2"],
        "c_M2q": np.ascontiguousarray(T["M2"][:, :, r * 16:(r + 1) * 16]),
        "c_rot": T["rot"], "c_sel": T["sel"], "c_ones": T["ones"], "c_ident": T["ident"],
    }
    return m


def _emit(nc, S):
    from contextlib import ExitStack
    st = ExitStack()
    sems = {e: [st.enter_context(nc.semaphore(f"s_{e}{i}")) for i in range(4)] for e in S.CE}
    dsems = [st.enter_context(nc.semaphore(f"d{i}")) for i in range(S.NDS)]
    info = S.emit(sems, dsems)
    return st, info


def kernel(**inputs):
    nc, S = build_program()
    st, _ = _emit(nc, S)
    with st:
        in_maps = [_core_inputs(inputs, c) for c in range(8)]
        res = run_bass_kernel_spmd(nc, in_maps, core_ids=list(range(8)))
    out = np.empty((NB, L, D), np.float32)
    for c in range(8):
        b, r = c // 4, c % 4
        out[b, r * NTOK:(r + 1) * NTOK, :] = np.asarray(res.results[c]["outT"]).T
    return out
```

```python
import math
import numpy as np
import ml_dtypes
import concourse.bass as bass
import concourse.mybir as mybir
from concourse.bass_utils import run_bass_kernel_spmd

F32 = mybir.dt.float32
BF16 = mybir.dt.bfloat16
ALU = mybir.AluOpType
AF = mybir.ActivationFunctionType
bf16_np = ml_dtypes.bfloat16

D = 1024
L = 8192
NB = 2
NTOK = 2048
HD = 64
NQH = 8
NKV = 2
HYW = 512
DFF = 4096
EPS = 1e-6
NFFT = 2 * L
TWO_PI = 2.0 * math.pi
MAGIC = 12582912.0


class Op:
    __slots__ = ("eng", "fn", "reads", "writes", "dma", "idx", "gid", "waits", "signal", "clock",
                 "sem", "val")


class Sched:
    CE = ("pe", "act", "dve", "pool", "sp")
    EPOCH = 12000
    NDS = 40

    def __init__(self, nc):
        self.nc = nc
        self.E = {"pe": nc.tensor, "act": nc.scalar, "dve": nc.vector, "pool": nc.gpsimd,
                  "sp": nc.sync}
        self.ops = []
        self.last_w = {}
        self.readers = {}
        self.known = {e: {} for e in self.CE}
        self.known_dma = {e: set() for e in self.CE}
        self.cnt = {e: 0 for e in self.CE}
        self.last_op = {e: None for e in self.CE}
        self.live_dma = []

    def op(self, eng, fn, reads=(), writes=(), dma=False):
        o = Op()
        o.eng, o.fn, o.reads, o.writes, o.dma = eng, fn, tuple(reads), tuple(writes), dma
        o.signal = dma
        o.sem = None
        o.val = None
        o.gid = len(self.ops)
        deps = {}
        for k in o.reads:
            w = self.last_w.get(k)
            if w is not None:
                deps[w.gid] = w
        for k in o.writes:
            w = self.last_w.get(k)
            if w is not None:
                deps[w.gid] = w
            for r in self.readers.get(k, ()):
                deps[r.gid] = r
        self._finish(o, deps.values())
        for k in o.writes:
            self.last_w[k] = o
            self.readers[k] = []
        for k in o.reads:
            lst = self.readers.setdefault(k, [])
            if not dma:
                lst[:] = [r for r in lst if r.dma or r.eng != eng]
            lst.append(o)
        return o

    def _finish(self, o, deps):
        eng = o.eng
        kn = self.known[eng]
        kd = self.known_dma[eng]
        waits = []
        best = {}
        rset = set(o.reads)
        for d in deps:
            if d.dma:
                if d.gid not in kd:
                    waits.append(d)
                    kd.add(d.gid)
                continue
            if d.eng == eng:
                if eng in ("pe", "sp"):
                    continue
            if kn.get(d.eng, -1) >= d.idx:
                continue
            b = best.get(d.eng)
            if b is None or b.idx < d.idx:
                best[d.eng] = d
        for d in best.values():
            waits.append(d)
            if kn.get(d.eng, -1) < d.idx:
                kn[d.eng] = d.idx
            for e2, i2 in d.clock.items():
                if kn.get(e2, -1) < i2:
                    kn[e2] = i2
        for d in waits:
            d.signal = True
        o.waits = waits
        o.idx = self.cnt[eng]
        self.cnt[eng] += 1
        o.clock = dict(kn)
        self.ops.append(o)
        if o.fn is not None:
            self.last_op[eng] = o
        if o.dma:
            self.live_dma.append(o)

    def barrier(self):
        lasts = [self.last_op[e] for e in self.CE if self.last_op[e] is not None]
        dmas = list(self.live_dma)
        for e in self.CE:
            o = Op()
            o.eng, o.fn, o.reads, o.writes, o.dma = e, None, (), (), False
            o.signal = False
            o.sem = None
            o.val = None
            o.gid = len(self.ops)
            deps = [d for d in lasts if not d.dma and d.eng != e] + dmas
            self._finish(o, deps)
        self.live_dma = []

    def emit(self, sems, dsems):
        sig = {e: 0 for e in self.CE}
        nd = 0
        for o in self.ops:
            E = self.E[o.eng]
            for d in o.waits:
                E.wait_ge(d.sem, d.val)
            if o.fn is None:
                continue
            if o.dma:
                s = dsems[nd % self.NDS]
                rnd = nd // self.NDS
                if rnd > 0:
                    E.wait_ge(s, 16 * rnd)
                ins = o.fn()
                ins.then_inc(s, 16)
                o.sem, o.val = s, 16 * (rnd + 1)
                nd += 1
            else:
                ins = o.fn()
                if o.signal:
                    c = sig[o.eng]
                    o.sem = sems[o.eng][c // self.EPOCH]
                    o.val = c % self.EPOCH + 1
                    ins.then_inc(o.sem, 1)
                    sig[o.eng] = c + 1
        return sig, nd


_TABLES = None


def _rope_tables():
    rows = L // 64
    row = np.repeat(np.arange(rows, dtype=np.float32), 64)
    col = np.tile(np.arange(64, dtype=np.float32), rows)
    half = HD // 2
    inv_freq = (np.float32(10000.0) ** (-np.arange(0, half, 2, dtype=np.float32) / np.float32(half))).astype(np.float32)
    ang_r = (row[:, None] * inv_freq[None, :]).astype(np.float32)
    ang_c = (col[:, None] * inv_freq[None, :]).astype(np.float32)
    cos_r, sin_r = np.cos(ang_r), np.sin(ang_r)
    cos_c, sin_c = np.cos(ang_c), np.sin(ang_c)
    cosT = np.concatenate([cos_r.T, cos_r.T, cos_c.T, cos_c.T], axis=0).astype(np.float32)
    sinT = np.concatenate([sin_r.T, sin_r.T, sin_c.T, sin_c.T], axis=0).astype(np.float32)
    return np.ascontiguousarray(cosT), np.ascontiguousarray(sinT)


def _filter_pos_tables():
    f32 = np.float32
    t = np.linspace(0.0, 1.0, L, dtype=f32)
    bands = 16
    w_ang = (f32(2.0 * math.pi) * np.arange(L, dtype=f32) / f32(L)).astype(f32)
    band_f = np.linspace(1e-4, bands - 1, bands, dtype=f32)
    ang = (w_ang[:, None] * band_f[None, :]).astype(f32)
    z = np.concatenate([t[:, None], np.cos(ang), -np.sin(ang)], axis=-1).astype(f32)
    zT = np.ascontiguousarray(z.T)
    idx = (L - np.arange(L)) % L
    zrT = np.ascontiguousarray(z[idx].T)
    trev = t[idx].copy()
    return zT, zrT, t.copy(), trev


def _fft_tables():
    N = NFFT
    n1 = np.arange(128, dtype=np.float64)
    k1 = np.arange(64, dtype=np.float64)
    th = 2 * np.pi * np.outer(n1, k1 + 0.5) / 128.0
    fpack = np.concatenate([np.cos(th), -np.sin(th), np.sin(th)], axis=1)
    n2 = np.arange(128, dtype=np.float64)
    k2 = np.arange(128, dtype=np.float64)
    kap = k1[None, :, None] + 128.0 * k2[None, None, :] + 0.5
    thg = 2 * np.pi * n2[:, None, None] * kap / N
    G = np.stack([np.cos(thg), -np.sin(thg)], axis=2)
    thf = 2 * np.pi * np.outer(k2, n2) / 128.0
    finv = np.stack([np.cos(thf), np.sin(thf), -np.sin(thf)], axis=1)
    n1h = np.arange(64, dtype=np.float64)
    phi = 2 * np.pi * (k1[:, None, None] + 0.5) * (128.0 * n1h[None, None, :] + n2[None, :, None]) / N
    M2 = np.concatenate([(2.0 / N) * np.cos(phi), -(2.0 / N) * np.sin(phi)], axis=0)
    return (fpack.astype(bf16_np), G.astype(bf16_np), finv.astype(bf16_np), M2.astype(bf16_np))


def _tables():
    global _TABLES
    if _TABLES is None:
        cosT, sinT = _rope_tables()
        zT, zrT, t, trev = _filter_pos_tables()
        fpack, G, finv, M2 = _fft_tables()
        ii = np.arange(512, dtype=np.float64)
        jj = np.arange(16, dtype=np.float64)
        tbm = np.stack([ii / (L - 1), (L - ii) / (L - 1)]).astype(np.float32)
        jtm = np.stack([512.0 * jj / (L - 1), -512.0 * jj / (L - 1)]).astype(np.float32)
        rot = np.zeros((64, 64), np.float32)
        for m in range(64):
            if (m % 32) < 16:
                rot[m + 16, m] = -1.0
            else:
                rot[m - 16, m] = 1.0
        sel = np.zeros((65, 64), np.float32)
        sel[64, :] = 1.0
        _TABLES = dict(cosT=cosT, sinT=sinT, zT=zT, zrT=zrT, t=t, trev=trev, fpack=fpack, G=G,
                       finv=finv, M2=M2, rot=rot, sel=sel, tb=tbm, jt=jtm,
                       ones=np.ones((128, 128), np.float32),
                       ident=np.eye(128, dtype=np.float32).astype(bf16_np))
    return _TABLES


def build_program(phases=(0, 1, 2, 3, 4), debug=()):
    nc = bass.Bass("TRN2", target_bir_lowering=False)
    S = Sched(nc)

    def din(name, shape, dt=F32):
        return nc.dram_tensor(name, list(shape), dt, kind="ExternalInput").ap()

    def dscr(name, shape, dt):
        return nc.dram_tensor(name, list(shape), dt, kind="Internal").ap()

    xT = din("xT", [D, L])
    xTq = din("xTq", [D, NTOK])
    w_in = din("w_in", [D, 2304])
    w_out = din("w_out", [D, D])
    w_mi = din("w_mi", [D, DFF])
    w_mo = din("w_mo", [DFF, D])
    g1 = din("g1", [128, 8]); g2 = din("g2", [128, 8]); gf = din("gf", [128, 8])
    gq = din("gq", [64, 1]); gk = din("gk", [64, 1])
    ga = din("ga", [64, 8]); gh = din("gh", [128, 4])
    cw = din("cw", [128, 12, 3]); cb = din("cb", [128, 12])
    fw1 = din("fw1", [33, 64]); fw2 = din("fw2", [64, 64]); fw3 = din("fw3", [64, 64])
    fw4 = din("fw4", [64, 2048])
    fb = din("fb", [64, 3]); ffreq = din("ffreq", [64, 1])
    fdel = din("fdel", [128, 16])
    skp = din("skp", [128, 8])
    qmask = din("qmask", [128, 4])
    c_cos = din("c_cos", [64, L]); c_sin = din("c_sin", [64, L])
    c_cosq = din("c_cosq", [64, NTOK]); c_sinq = din("c_sinq", [64, NTOK])
    c_zT = din("c_zT", [2, 33, L])
    c_t = din("c_t", [2, L])
    c_tb = din("c_tb", [2, 512])
    c_jt = din("c_jt", [2, 16])
    c_fpack = din("c_fpack", [128, 192], BF16)
    c_G = din("c_G", [128, 64, 2, 128], BF16)
    c_finv = din("c_finv", [128, 3, 128], BF16)
    c_M2 = din("c_M2", [128, 128, 64], BF16)
    c_M2q = din("c_M2q", [128, 128, 16], BF16)
    c_rot = din("c_rot", [64, 64]); c_sel = din("c_sel", [65, 64]); c_ones = din("c_ones", [128, 128])
    c_ident = din("c_ident", [128, 128], BF16)
    outT = nc.dram_tensor("outT", [D, NTOK], F32, kind="ExternalOutput").ap()

    U_d = dscr("U_d", [1536, L], BF16)
    H_d = dscr("H_d", [8, 128, 64, 4, 128], BF16)
    taps_d = dscr("taps_d", [128, NFFT], BF16)
    zcm_d = dscr("zcm_d", [128, L], BF16)
    att_d = dscr("att_d", [512, NTOK], BF16)
    hy_d = dscr("hy_d", [512, NTOK], BF16)
    Wmi_d = dscr("Wmi_d", [D, DFF], BF16)
    Wmo_d = dscr("Wmo_d", [8, 128, 32, 128], BF16)

    from contextlib import ExitStack
    es = ExitStack()

    _names = {}

    def sb(name, shape, dt, stack=None):
        n_ = _names.get(name, 0)
        _names[name] = n_ + 1
        nm = name if n_ == 0 else f"{name}_r{n_}"
        return (stack or es).enter_context(nc.sbuf_tensor(nm, list(shape), dt))

    def dma(out, in_, reads, writes, eng="sp"):
        return S.op(eng, lambda: S.E[eng].dma_start(out=out, in_=in_), reads, writes, dma=True)

    def mm(out, lhsT, rhs, start, stop, reads, writes):
        return S.op("pe", lambda: nc.tensor.matmul(out, lhsT, rhs, start=start, stop=stop), reads, writes)

    rr = {"i": 0}

    def alt(engs=("dve", "pool")):
        rr["i"] += 1
        return engs[rr["i"] % len(engs)]

    def V(eng):
        return S.E[eng]

    with es:
        PSW = [es.enter_context(nc.psum_tensor(f"psw{i}", [128, 1024], F32)) for i in range(3)]
        PS = [PSW[i // 2][:, (i % 2) * 512:(i % 2 + 1) * 512] for i in range(6)]
        PSB = [es.enter_context(nc.psum_tensor(f"psb{i}", [128, 1024], BF16)) for i in range(2)]

        ones = sb("ones", [128, 128], F32)
        ident = sb("ident", [128, 128], BF16)
        rot = sb("rot", [64, 64], F32)
        sel = sb("sel", [65, 64], F32)
        epsT = sb("epsT", [128, 1], F32)
        g1s = sb("g1s", [128, 8], F32); g2s = sb("g2s", [128, 8], F32); gfs = sb("gfs", [128, 8], F32)
        gqs = sb("gqs", [64, 1], F32); gks = sb("gks", [64, 1], F32)
        gas = sb("gas", [64, 8], F32); ghs = sb("ghs", [128, 4], F32)
        cws = sb("cws", [128, 12, 3], F32); cbs = sb("cbs", [128, 12], F32)
        skps = sb("skps", [128, 8], F32); qms = sb("qms", [128, 4], F32)
        for t_, d_ in ((ones, c_ones), (ident, c_ident), (rot, c_rot), (sel, c_sel), (g1s, g1), (g2s, g2),
                       (gfs, gf), (gqs, gq), (gks, gk), (gas, ga), (ghs, gh), (cws, cw), (cbs, cb),
                       (skps, skp), (qms, qmask)):
            dma(t_[:], d_, [], [t_.name if hasattr(t_, "name") else id(t_)])
        KEY = lambda t_: t_.name if hasattr(t_, "name") else id(t_)
        S.op("dve", lambda: nc.vector.memset(epsT[:], EPS), [], ["epsT"])

        FC = {}
        SNAP = {}

        def snap(name, src_ap, key, ncols):
            if name not in debug or name in SNAP:
                return
            SNAP[name] = nc.dram_tensor("dbg_" + name, [128, ncols], BF16, kind="ExternalOutput").ap()
            dma(SNAP[name], src_ap, [key], ["dbg_" + name])


        def load_fft_consts(stack):
            FC["fpack"] = sb("fpack", [128, 192], BF16, stack)
            FC["finv"] = sb("finv", [128, 3, 128], BF16, stack)
            FC["Gq"] = [sb(f"Gq{i}", [128, 8, 2, 128], BF16, stack) for i in range(2)]
            dma(FC["fpack"][:], c_fpack, [], ["fpack"])
            dma(FC["finv"][:], c_finv, [], ["finv"])

        def rsqrt(out, in_, scale, reads, writes, np_=128):
            S.op("act", lambda: nc.scalar.activation(out, in_, AF.Sqrt, bias=epsT[0:np_, :], scale=scale),
                 list(reads) + ["epsT"], writes)
            S.op("dve", lambda: nc.vector.reciprocal(out, out), writes, writes)

        def fft_forward(ztm, K, AT, cb_batch, zkey="ztm", akey="AT", pre_batch=None):
            fpack = FC["fpack"]
            for c2 in range(64):
                ps = PS[c2 % 2]
                pk = f"ps{c2 % 2}"
                for cc in range(2):
                    c = c2 * 2 + cc
                    mm(ps[:, cc * 192:(cc + 1) * 192], ztm[0:K, c, :], fpack[0:K, :], True, True,
                       [(zkey, c // 32), "fpack"], [pk])
                e = alt(("act", "dve"))
                src = ps[:, 0:384].rearrange("p (cc j) -> p j cc", cc=2)
                dst = AT[:, :, c2 * 2:c2 * 2 + 2]
                if e == "act":
                    S.op("act", lambda dst=dst, src=src: nc.scalar.copy(dst, src), [pk], [akey])
                else:
                    S.op("dve", lambda dst=dst, src=src: nc.vector.tensor_copy(dst, src), [pk], [akey])

            def load_g(q8):
                dma(FC["Gq"][q8 % 2][:], c_G[:, q8 * 8:(q8 + 1) * 8], [], [f"Gq{q8 % 2}"])

            load_g(0)
            for kb in range(32):
                q4 = kb // 4
                Gq = FC["Gq"][q4 % 2]
                gkey = f"Gq{q4 % 2}"
                if kb % 4 == 0 and q4 + 1 < 8:
                    load_g(q4 + 1)
                if pre_batch is not None:
                    pre_batch(kb)
                ps = PS[2 + kb % 2]
                pk = f"ps{2 + kb % 2}"
                for kk in range(2):
                    k1 = kb * 2 + kk
                    kl = k1 % 8
                    R = AT[:, k1, :]
                    I = AT[:, 64 + k1, :]
                    NI = AT[:, 128 + k1, :]
                    zr = ps[:, kk * 256:kk * 256 + 128]
                    zi = ps[:, kk * 256 + 128:kk * 256 + 256]
                    mm(zr, Gq[:, kl, 0, :], R, True, False, [akey, gkey], [pk])
                    mm(zr, Gq[:, kl, 1, :], NI, False, True, [akey, gkey], [pk])
                    mm(zi, Gq[:, kl, 1, :], R, True, False, [akey, gkey], [pk])
                    mm(zi, Gq[:, kl, 0, :], I, False, True, [akey, gkey], [pk])
                cb_batch(kb, ps, pk)

        def phase0():
          with ExitStack() as p0w:
            wst = [sb(f"wst{i}", [128, 8, 512], F32, p0w) for i in range(2)]
            wbf = [sb(f"wbf{i}", [128, 8, 512], BF16, p0w) for i in range(2)]
            n = 0
            jobs = []
            for cbk in range(8):
                jobs.append((w_mi.rearrange("(k p) n -> p k n", p=128)[:, :, cbk * 512:(cbk + 1) * 512],
                             Wmi_d.rearrange("(k p) n -> p k n", p=128)[:, :, cbk * 512:(cbk + 1) * 512]))
            for kr in range(4):
                for cbk in range(2):
                    jobs.append((w_mo.rearrange("(k p) n -> p k n", p=128)[:, kr * 8:(kr + 1) * 8, cbk * 512:(cbk + 1) * 512],
                                 (kr, cbk)))
            for src, dst in jobs:
                s_ = n % 2
                dma(wst[s_][:], src, [], [f"wst{s_}"])
                e = ("dve", "act")[n % 2]
                if e == "act":
                    S.op("act", lambda s_=s_: nc.scalar.copy(wbf[s_][:], wst[s_][:]), [f"wst{s_}"], [f"wbf{s_}"])
                else:
                    S.op(e, lambda s_=s_, e=e: V(e).tensor_copy(wbf[s_][:], wst[s_][:]), [f"wst{s_}"], [f"wbf{s_}"])
                if n < 8:
                    dma(dst, wbf[s_][:], [f"wbf{s_}"], ["Wmid"])
                else:
                    kr_, cbk_ = dst
                    for c_ in range(4):
                        dma(Wmo_d[cbk_ * 4 + c_, :, kr_ * 8:(kr_ + 1) * 8, :], wbf[s_][:, :, c_ * 128:(c_ + 1) * 128], [f"wbf{s_}"], ["Wmod"])
                n += 1
            S.barrier()
          with ExitStack() as p0:
            load_fft_consts(p0)

            w1s = sb("w1s", [33, 64], F32, p0); w2s = sb("w2s", [64, 64], F32, p0); w3s = sb("w3s", [64, 64], F32, p0)
            w4b = sb("w4b", [64, 2048], BF16, p0)
            fbs = sb("fbs", [64, 3], F32, p0); frs = sb("frs", [64, 1], F32, p0); ffb = sb("ffb", [64, 3], F32, p0)
            dls = sb("dls", [128, 16], F32, p0); ndl = sb("ndl", [128, 16], F32, p0)
            negpi = sb("negpi", [128, 1], F32, p0)
            H3 = sb("H3", [64, 2, L], BF16, p0)
            p0m = ExitStack()
            w4s = sb("w4s", [64, 2048], F32, p0m)
            for t_, d_, k_ in ((w1s, fw1, "w1s"), (w2s, fw2, "w2s"), (w3s, fw3, "w3s"), (w4s, fw4, "w4s"),
                               (fbs, fb, "fbs"), (frs, ffreq, "frs"), (dls, fdel, "dls")):
                dma(t_[:], d_, [], [k_])
            S.op("dve", lambda: nc.vector.tensor_copy(w4b[:], w4s[:]), ["w4s"], ["w4b"])
            S.op("dve", lambda: nc.vector.tensor_scalar(ffb[:], fbs[:], frs[:, 0:1], None, ALU.mult), ["fbs", "frs"], ["ffb"])
            S.op("dve", lambda: nc.vector.tensor_scalar(ndl[:], dls[:], -1.0, None, ALU.mult), ["dls"], ["ndl"])
            S.op("dve", lambda: nc.vector.tensor_tensor(ndl[:], ndl[:], dls[:], ALU.min), ["dls", "ndl"], ["ndl"])
            S.op("dve", lambda: nc.vector.memset(negpi[:], 0.0), [], ["negpi"])
            zc = [sb(f"zc{i}", [33, 512], F32, p0m) for i in range(4)]
            faL = [sb(f"fa{i}", [64, 512], F32, p0m) for i in range(2)]
            fbufL = [sb(f"fbuf{i}", [64, 512], F32, p0m) for i in range(2)]
            fhL = [sb(f"fh{i}", [64, 512], F32, p0m) for i in range(2)]

            def sin_layer(ps, li, dst, pk, v_, dkey):
                fa, fbuf = faL[v_], fbufL[v_]
                fak, fbk = f"fa{v_}", f"fbuf{v_}"
                S.op("dve", lambda: nc.vector.tensor_scalar(fa[:], ps[0:64, :], frs[:, 0:1], ffb[:, li:li + 1], ALU.mult, ALU.add),
                     [pk, "frs", "ffb"], [fak])
                S.op("dve", lambda: nc.vector.tensor_scalar(fbuf[:], fa[:], 1.0 / TWO_PI, MAGIC, ALU.mult, ALU.add), [fak], [fbk])
                S.op("dve", lambda: nc.vector.tensor_scalar(fbuf[:], fbuf[:], -MAGIC, -TWO_PI, ALU.add, ALU.mult), [fbk], [fbk])
                S.op("dve", lambda: nc.vector.tensor_tensor(fa[:], fa[:], fbuf[:], ALU.add), [fak, fbk], [fak])
                S.op("act", lambda: nc.scalar.activation(dst, fa[:], AF.Sin, bias=negpi[0:64, :], scale=1.0), [fak, "negpi"], [dkey])

            for j in range(16):
                for var in range(2):
                    s_ = (j % 2) * 2 + var
                    psm = PS[4 + var]
                    pmk = f"ps{4 + var}"
                    fh = fhL[var]
                    fhk = f"fh{var}"
                    dma(zc[s_][:], c_zT[var, :, j * 512:(j + 1) * 512], [], [f"zc{s_}"])
                    mm(psm[0:64, :], w1s[:], zc[s_][:], True, True, [f"zc{s_}", "w1s"], [pmk])
                    sin_layer(psm, 0, fh[:], pmk, var, fhk)
                    mm(psm[0:64, :], w2s[:], fh[:], True, True, [fhk, "w2s"], [pmk])
                    sin_layer(psm, 1, fh[:], pmk, var, fhk)
                    mm(psm[0:64, :], w3s[:], fh[:], True, True, [fhk, "w3s"], [pmk])
                    sin_layer(psm, 2, H3[:, var, j * 512:(j + 1) * 512], pmk, var, ("H3", var))

            S.barrier()
            p0m.close()
            tapsU = sb("tapsU", [128, NFFT], BF16, p0)
            ztmf = sb("ztmf", [128, 128, 128], BF16, p0)
            ATf = sb("ATf", [128, 192, 128], BF16, p0)
            tb = sb("tb", [128, 2, 512], F32, p0)
            jt = sb("jt", [128, 2, 16], F32, p0)
            for hf in range(2):
                dma(tb[:, hf, :], c_tb[hf:hf + 1, :].broadcast_to([128, 512]), [], ["tb"])
                dma(jt[:, hf, :], c_jt[hf:hf + 1, :].broadcast_to([128, 16]), [], ["jt"])
            wbase = [sb(f"wbase{i}", [128, 512], F32, p0) for i in range(2)]
            wcj = [sb(f"wcj{i}", [128, 16], F32, p0) for i in range(2)]
            tpf = [sb(f"tpf{i}", [128, 512], F32, p0) for i in range(2)]
            psum_ = sb("psum_", [128, 32], F32, p0)
            nrm = sb("nrm", [128, 2], F32, p0)
            Hst = [sb(f"Hst{i}", [128, 2, 4, 128], BF16, p0) for i in range(4)]
            for go in range(8):
                g, o = go // 2, go % 2
                for half in range(2):
                    col0 = o * 1024 + half * 512 + g * 128
                    ct = col0 // 128
                    S.op("act", lambda half=half, ct=ct: nc.scalar.activation(wbase[half][:], tb[:, half, :], AF.Exp, scale=ndl[:, ct:ct + 1]),
                         ["tb", "ndl"], [f"wbase{half}"])
                    S.op("act", lambda half=half, ct=ct: nc.scalar.activation(wcj[half][:], jt[:, half, :], AF.Exp, scale=ndl[:, ct:ct + 1]),
                         ["jt", "ndl"], [f"wcj{half}"])
                    for j in range(16):
                        s_ = j % 2
                        pst = PS[4 + s_]
                        ptk = f"ps{4 + s_}"
                        mm(pst[:, :], w4b[:, col0:col0 + 128], H3[:, half, j * 512:(j + 1) * 512], True, True,
                           [("H3", half), "w4b"], [ptk])
                        S.op("dve", lambda s_=s_, pst=pst, half=half, j=j: nc.vector.scalar_tensor_tensor(
                            out=tpf[s_][:], in0=pst[:, :], scalar=wcj[half][:, j:j + 1], in1=wbase[half][:], op0=ALU.mult, op1=ALU.mult),
                            [ptk, f"wcj{half}", f"wbase{half}"], [f"tpf{s_}"])
                        idx = half * 16 + j
                        if half == 1 and j == 0:
                            S.op("dve", lambda s_=s_: nc.vector.memset(tpf[s_][:, 0:1], 0.0), [f"tpf{s_}"], [f"tpf{s_}"])
                        S.op("dve", lambda idx=idx, s_=s_: nc.vector.tensor_reduce(psum_[:, idx:idx + 1], tpf[s_][:], mybir.AxisListType.X, ALU.add,
                                                                                  apply_absolute_value=True), [f"tpf{s_}"], ["psum_"])
                        S.op("act", lambda half=half, j=j, s_=s_: nc.scalar.copy(tapsU[:, half * L + j * 512: half * L + (j + 1) * 512], tpf[s_][:]),
                             [f"tpf{s_}"], ["tapsU"])
                S.op("dve", lambda: nc.vector.tensor_reduce(nrm[:, 0:1], psum_[:], mybir.AxisListType.X, ALU.add), ["psum_"], ["nrm"])
                S.op("dve", lambda: nc.vector.reciprocal(nrm[:, 0:1], nrm[:, 0:1]), ["nrm"], ["nrm"])
                S.op("dve", lambda: nc.vector.tensor_scalar(nrm[:, 1:2], nrm[:, 0:1], -1.0, None, ALU.mult), ["nrm"], ["nrm"])
                S.op("dve", lambda: nc.vector.tensor_scalar(tapsU[:, 0:L], tapsU[:, 0:L], nrm[:, 0:1], None, ALU.mult), ["tapsU", "nrm"], ["tapsU"])
                S.op("act", lambda: nc.scalar.mul(tapsU[:, L:NFFT], tapsU[:, L:NFFT], nrm[:, 1:2]), ["tapsU", "nrm"], ["tapsU"])
                tv = taps_d.rearrange("c (n1 n2) -> n1 c n2", n2=128)
                for q4 in range(4):
                    dma(taps_d[q4 * 32:(q4 + 1) * 32, :], tapsU[q4 * 32:(q4 + 1) * 32, :], ["tapsU"], [("taps_d", q4)])
                for q4 in range(4):
                    dma(ztmf[:, q4 * 32:(q4 + 1) * 32, :], tv[:, q4 * 32:(q4 + 1) * 32, :], [("taps_d", q4)], [("ztm", q4)])

                def cb_filter(kb, ps, pk, go=go):
                    s_ = kb % 4
                    zv = ps[:, :].rearrange("p (k r c) -> p k r c", k=2, r=2)
                    S.op("act", lambda: nc.scalar.copy(Hst[s_][:, :, 0:2, :], zv), [pk], [f"Hst{s_}"])
                    S.op("dve", lambda: nc.vector.tensor_copy(Hst[s_][:, :, 2, :], zv[:, :, 1, :]), [pk], [f"Hst{s_}"])
                    S.op("dve", lambda: nc.vector.tensor_copy(Hst[s_][:, :, 3, :], zv[:, :, 0, :]), [pk], [f"Hst{s_}"])
                    dma(H_d[go, :, kb * 2:kb * 2 + 2], Hst[s_][:], [f"Hst{s_}"], [("H_d", go)])

                fft_forward(ztmf, 128, ATf, cb_filter)
            S.barrier()
        def rms_apply(src, gvec, dst, scale, nk, tag, sqb, rs, np_=128, ss_ps=0):
            S.op("act", lambda: nc.scalar.activation(sqb[0:np_, 0:nk, :], src, AF.Square), [tag + "_src"], ["sqb"])
            for k in range(nk):
                mm(PS[ss_ps][0:np_, :], ones[0:np_, 0:np_], sqb[0:np_, k, :], k == 0, k == nk - 1, ["sqb", "ones"], [f"ps{ss_ps}"])
            S.op("act", lambda: nc.scalar.activation(rs[0:np_, :], PS[ss_ps][0:np_, :], AF.Sqrt, bias=epsT[0:np_, :], scale=scale),
                 [f"ps{ss_ps}", "epsT"], ["rs"])
            S.op("dve", lambda: nc.vector.reciprocal(rs[0:np_, :], rs[0:np_, :]), ["rs"], ["rs"])
            for k in range(nk):
                e = "dve"
                S.op(e, lambda k=k, e=e: V(e).scalar_tensor_tensor(out=dst[:, k, :], in0=src[:, k, :], scalar=gvec[:, k:k + 1],
                                                                  in1=rs[0:np_, :], op0=ALU.mult, op1=ALU.mult),
                     [tag + "_src", "rs", "gvecs"], [tag + "_dst"])

        def phase12():
          with ExitStack() as p12:
            KT = sb("KT", [128, 2, L], BF16, p12)
            Vx = sb("Vx", [128, 64, 2, 64], BF16, p12)
            QT = sb("QT", [128, 4, NTOK], BF16, p12)
            onesb = sb("onesb", [128, 64], BF16, p12)
            S.op("pool", lambda: nc.gpsimd.memset(onesb[:], 1.0), [], ["onesb"])
            with ExitStack() as p1:
                Wb = sb("Wb", [128, 8, 2304], BF16, p1)
                with ExitStack() as p1c:
                    wst = [sb(f"wsi{i}", [128, 8, 384], F32, p1c) for i in range(2)]
                    wv = w_in.rearrange("(k p) n -> p k n", p=128)
                    for bk in range(6):
                        s_ = bk % 2
                        dma(wst[s_][:], wv[:, :, bk * 384:(bk + 1) * 384], [], [f"wst{s_}"])
                        if bk % 2 == 0:
                            S.op("dve", lambda s_=s_, bk=bk: nc.vector.tensor_copy(Wb[:, :, bk * 384:(bk + 1) * 384], wst[s_][:]), [f"wst{s_}"], ["Wb"])
                        else:
                            S.op("act", lambda s_=s_, bk=bk: nc.scalar.copy(Wb[:, :, bk * 384:(bk + 1) * 384], wst[s_][:]), [f"wst{s_}"], ["Wb"])
                    S.barrier()
                xin = [sb(f"xin{i}", [128, 8, 512], F32, p1) for i in range(2)]
                sqb = sb("sqb", [128, 8, 512], F32, p1)
                rs = sb("rs", [128, 512], F32, p1)
                aT = sb("aT", [128, 8, 512], BF16, p1)
                ust = sb("ust", [128, 12, 512], BF16, p1)
                hsq = sb("hsq", [64, 512], F32, p1); hrs = sb("hrs", [64, 512], F32, p1); hkn = sb("hkn", [64, 512], F32, p1)
                ht1 = sb("ht1", [64, 512], F32, p1); ht2 = sb("ht2", [64, 512], F32, p1)
                csb = [sb(f"csb{i}", [64, 2, 512], F32, p1) for i in range(2)]
                qstg = [sb(f"qstg{i}", [64, 512], BF16, p1) for i in range(2)]

                def headproc(gvec, cs, cskey, dst, dstkey):
                    src = PS[1][0:64, :]
                    S.op("act", lambda: nc.scalar.activation(hsq[:], src, AF.Square), ["ps1"], ["hsq"])
                    mm(PS[2][0:64, :], ones[0:64, 0:64], hsq[:], True, True, ["hsq", "ones"], ["ps2"])
                    S.op("act", lambda: nc.scalar.activation(hrs[:], PS[2][0:64, :], AF.Sqrt, bias=epsT[0:64, :], scale=1.0 / 64.0),
                         ["ps2", "epsT"], ["hrs"])
                    S.op("dve", lambda: nc.vector.reciprocal(hrs[:], hrs[:]), ["hrs"], ["hrs"])
                    S.op("dve", lambda: nc.vector.scalar_tensor_tensor(out=hkn[:], in0=src, scalar=gvec[:, 0:1], in1=hrs[:],
                                                                      op0=ALU.mult, op1=ALU.mult), ["ps1", "hrs", "gvecs"], ["hkn"])
                    mm(PS[3][0:64, :], rot[:], hkn[:], True, True, ["hkn", "rot"], ["ps3"])
                    S.op("pool", lambda: nc.gpsimd.tensor_tensor(ht1[:], hkn[:], cs[:, 0, :], ALU.mult), ["hkn", cskey], ["ht1"])
                    S.op("dve", lambda: nc.vector.tensor_tensor(ht2[:], PS[3][0:64, :], cs[:, 1, :], ALU.mult), ["ps3", cskey], ["ht2"])
                    S.op("pool", lambda: nc.gpsimd.tensor_tensor(dst, ht1[:], ht2[:], ALU.add), ["ht1", "ht2"], [dstkey])

                xv = xT.rearrange("(k p) t -> p k t", p=128)
                Uv = U_d.rearrange("(ct p) t -> p ct t", p=128)
                for j in range(16):
                    s_ = j % 2
                    tsl = slice(j * 512, (j + 1) * 512)
                    dma(xin[s_][:], xv[:, :, tsl], [], [f"xin{s_}", "n1_src"])
                    dma(csb[s_][:, 0, :], c_cos[:, tsl], [], [f"csb{s_}"])
                    dma(csb[s_][:, 1, :], c_sin[:, tsl], [], [f"csb{s_}"])
                    rms_apply(xin[s_][:], g1s, aT, 1.0 / D, 8, "n1", sqb, rs)
                    for kvh in range(2):
                        for k in range(8):
                            mm(PS[1][0:64, :], Wb[:, k, 512 + kvh * 64:512 + (kvh + 1) * 64], aT[:, k, :], k == 0, k == 7,
                               ["Wb", "n1_dst"], ["ps1"])
                        headproc(gks, csb[s_], f"csb{s_}", KT[0:64, kvh, tsl], "KT")
                    for tt in range(4):
                        for k in range(8):
                            mm(PS[4][:, tt * 128:(tt + 1) * 128], aT[:, k, tt * 128:(tt + 1) * 128], Wb[:, k, 640:768], k == 0, k == 7,
                               ["Wb", "n1_dst"], ["ps4"])
                    S.op("act", lambda j=j: nc.scalar.copy(Vx[:, j * 4:(j + 1) * 4, :, :],
                                                         PS[4][:, :].rearrange("p (t h d) -> p t h d", t=4, h=2)), ["ps4"], ["Vx"])
                    for ct in range(12):
                        for k in range(8):
                            mm(PS[5][:, :], Wb[:, k, 768 + ct * 128:768 + (ct + 1) * 128], aT[:, k, :], k == 0, k == 7,
                               ["Wb", "n1_dst"], ["ps5"])
                        if ct % 2 == 0:
                            S.op("act", lambda ct=ct: nc.scalar.copy(ust[:, ct, :], PS[5][:, :]), ["ps5"], ["ust"])
                        else:
                            S.op("dve", lambda ct=ct: nc.vector.tensor_copy(ust[:, ct, :], PS[5][:, :]), ["ps5"], ["ust"])
                    dma(Uv[:, :, tsl], ust[:], ["ust"], ["U_d"])
                xqv = xTq.rearrange("(k p) t -> p k t", p=128)
                for j in range(4):
                    s_ = j % 2
                    tsl = slice(j * 512, (j + 1) * 512)
                    dma(xin[s_][:], xqv[:, :, tsl], [], [f"xin{s_}", "n1_src"])
                    dma(csb[s_][:, 0, :], c_cosq[:, tsl], [], [f"csb{s_}"])
                    dma(csb[s_][:, 1, :], c_sinq[:, tsl], [], [f"csb{s_}"])
                    rms_apply(xin[s_][:], g1s, aT, 1.0 / D, 8, "n1", sqb, rs)
                    for h in range(8):
                        for k in range(8):
                            mm(PS[1][0:64, :], Wb[:, k, h * 64:(h + 1) * 64], aT[:, k, :], k == 0, k == 7, ["Wb", "n1_dst"], ["ps1"])
                        if h % 2 == 0:
                            headproc(gqs, csb[s_], f"csb{s_}", QT[0:64, h // 2, tsl], "QT")
                        else:
                            qs_ = (h // 2) % 2
                            headproc(gqs, csb[s_], f"csb{s_}", qstg[qs_][:], f"qstg{qs_}")
                            dma(QT[64:128, h // 2, tsl], qstg[qs_][:], [f"qstg{qs_}"], ["QT2"])
                dma(KT[64:128, :, :], KT[0:64, :, :], ["KT"], ["KT2"])
                S.barrier()
            if 2 in phases:
              with ExitStack() as p2:
                pb = [sb(f"pb{i}", [128, 1024], BF16, p2) for i in range(2)]
                rden = sb("rden", [128, 512], F32, p2)
                ast = [sb(f"ast{i}", [128, 512], BF16, p2) for i in range(2)]
                n_ = 0
                for qc in range(4):
                    qsl = slice(qc * 512, (qc + 1) * 512)
                    for hp in range(4):
                        kvh = hp // 2
                        for step in range(64 + 1):
                            if step < 64:
                                kt = step
                                sl = kt % 2
                                ksl = slice(kt * 128, (kt + 1) * 128)
                                mm(PS[2 * sl][:, :], KT[0:64, kvh, ksl], QT[0:64, hp, qsl], True, True, ["KT", "QT"], [f"ps{2 * sl}"])
                                mm(PS[2 * sl + 1][:, :], KT[64:128, kvh, ksl], QT[64:128, hp, qsl], True, True, ["KT2", "QT2"], [f"ps{2 * sl + 1}"])
                                S.op("act", lambda sl=sl: nc.scalar.activation(pb[sl][:], PSW[sl][:, :], AF.Exp, scale=0.125),
                                     [f"ps{2 * sl}", f"ps{2 * sl + 1}"], [f"pb{sl}"])
                            kt = step - 1
                            if kt >= 0:
                                sl = kt % 2
                                st_, sp_ = kt == 0, kt == 63
                                pA = pb[sl][:, 0:512]
                                pB = pb[sl][:, 512:1024]
                                vv = Vx[:, kt, kvh, :]
                                S.op("pe", lambda vv=vv, pA=pA, st_=st_, sp_=sp_: nc.tensor.matmul(PS[4][0:64, :], vv, pA, start=st_, stop=sp_, tile_position=(0, 0)),
                                     ["Vx", f"pb{sl}"], ["ps4"])
                                S.op("pe", lambda vv=vv, pB=pB, st_=st_, sp_=sp_: nc.tensor.matmul(PS[4][64:128, :], vv, pB, start=st_, stop=sp_, tile_position=(0, 64)),
                                     ["Vx", f"pb{sl}"], ["ps4"])
                                S.op("pe", lambda pA=pA, st_=st_, sp_=sp_: nc.tensor.matmul(PS[5][0:64, :], onesb[:], pA, start=st_, stop=sp_, tile_position=(0, 0)),
                                     ["onesb", f"pb{sl}"], ["ps5"])
                                S.op("pe", lambda pB=pB, st_=st_, sp_=sp_: nc.tensor.matmul(PS[5][64:128, :], onesb[:], pB, start=st_, stop=sp_, tile_position=(0, 64)),
                                     ["onesb", f"pb{sl}"], ["ps5"])
                        S.op("dve", lambda: nc.vector.reciprocal(rden[:], PS[5][:, :]), ["ps5"], ["rden"])
                        a_ = n_ % 2
                        S.op("dve", lambda a_=a_: nc.vector.tensor_tensor(ast[a_][:], PS[4][:, :], rden[:], ALU.mult), ["ps4", "rden"], [f"ast{a_}"])
                        dma(att_d[hp * 128:(hp + 1) * 128, qsl], ast[a_][:], [f"ast{a_}"], ["att_d"])
                        n_ += 1
                S.barrier()

        def phase3():
          with ExitStack() as p3:
            load_fft_consts(p3)
            finv = FC["finv"]
            M2s = sb("M2s", [128, 128, 64], BF16, p3)
            M2qs = sb("M2qs", [128, 128, 16], BF16, p3)
            dma(M2s[:], c_M2, [], ["M2s"])
            dma(M2qs[:], c_M2q, [], ["M2qs"])
            RA = sb("RA", [128, 192 * 128 + 8], BF16, p3)
            RB = sb("RB", [128, 128 * 128], BF16, p3)
            RC = sb("RC", [128, 128 * 128], BF16, p3)
            vt = sb("vt", [128, L], BF16, p3)
            x1t = sb("x1t", [128, L], BF16, p3)
            x2q = sb("x2q", [128, NTOK], BF16, p3)
            zzq = sb("zzq", [128, NTOK], BF16, p3)
            hyst = sb("hyst", [128, NTOK], BF16, p3)
            Hc = [sb(f"Hc{i}", [128, 2, 4, 128], BF16, p3) for i in range(4)]
            Pb = sb("Pb", [128, 2, 2, 128], F32, p3)
            Qb = sb("Qb", [128, 2, 2, 128], F32, p3)
            Yb = [sb(f"Yb{i}", [128, 2, 2, 128], BF16, p3) for i in range(2)]
            gt = [sb(f"gt{i}", [128, 512], F32, p3) for i in range(2)]
            sct = gt
            uraw = RA[:, 0:3 * 8194].rearrange("p (i t) -> p i t", i=3)
            AT = RA[:, 0:192 * 128].rearrange("p (j c) -> p j c", c=128)
            ztm = RB[:, :].rearrange("p (c n) -> p c n", n=128)
            Eb = RB[:, :].rearrange("p (n c) -> p n c", c=128)
            x2t = RC[:, 0:L]
            ET = RC[:, :].rearrange("p (c k) -> p c k", k=128)
            RAK, RBK, RCK = "RA", "RB", "RC"

            def masked_quarter(dst, src, dkey, skey):
                S.op("dve", lambda: nc.vector.tensor_scalar(dst[:], src[:, 0:NTOK], qms[:, 0:1], None, ALU.mult), [skey, "qms"], [dkey])
                for q in range(1, 4):
                    S.op("dve", lambda q=q: nc.vector.scalar_tensor_tensor(out=dst[:], in0=src[:, q * NTOK:(q + 1) * NTOK], scalar=qms[:, q:q + 1],
                                                                          in1=dst[:], op0=ALU.mult, op1=ALU.add), [skey, "qms", dkey], [dkey])

            RBQ = [(RBK, q4) for q4 in range(4)]

            def bounce_to_ztm(src, skey):
                zv = zcm_d.rearrange("c (n1 n2) -> n1 c n2", n2=128)
                for q4 in range(4):
                    dma(zcm_d[q4 * 32:(q4 + 1) * 32, :], src[q4 * 32:(q4 + 1) * 32, :], [skey], [("zcm_d", q4)])
                for q4 in range(4):
                    dma(ztm[0:64, q4 * 32:(q4 + 1) * 32, :], zv[:, q4 * 32:(q4 + 1) * 32, :], [("zcm_d", q4)], [(RBK, q4)])

            def conv_core(go):
                def load_h(kb):
                    dma(Hc[kb % 4][:], H_d[go, :, kb * 2:kb * 2 + 2], [("H_d", go)], [f"Hc{kb % 4}"])

                def pre(kb):
                    if kb == 0:
                        load_h(0)
                        load_h(1)
                    if kb + 2 < 32:
                        load_h(kb + 2)

                def cb(kb, ps, pk):
                    s_ = kb % 4
                    y_ = kb % 2
                    zv = ps[:, :].rearrange("p (k r c) -> p k r c", k=2, r=2)
                    S.op("dve", lambda: nc.vector.tensor_tensor(Pb[:], zv, Hc[s_][:, :, 0:2, :], ALU.mult), [pk, f"Hc{s_}"], ["Pb"])
                    S.op("dve", lambda: nc.vector.tensor_tensor(Qb[:], zv, Hc[s_][:, :, 2:4, :], ALU.mult), [pk, f"Hc{s_}"], ["Qb"])
                    S.op("pool", lambda: nc.gpsimd.tensor_tensor(Yb[y_][:, :, 0, :], Pb[:, :, 0, :], Pb[:, :, 1, :], ALU.subtract), ["Pb"], [f"Yb{y_}"])
                    S.op("pool", lambda: nc.gpsimd.tensor_tensor(Yb[y_][:, :, 1, :], Qb[:, :, 0, :], Qb[:, :, 1, :], ALU.add), ["Qb"], [f"Yb{y_}"])
                    pe_ = PS[4 + kb % 2]
                    pek = f"ps{4 + kb % 2}"
                    er = pe_[:, 0:256].rearrange("p (k c) -> p k c", k=2)
                    ei = pe_[:, 256:512].rearrange("p (k c) -> p k c", k=2)
                    yr = Yb[y_][:, :, 0, :]
                    yi = Yb[y_][:, :, 1, :]
                    mm(er, finv[:, 0, :], yr, True, False, [f"Yb{y_}", "finv"], [pek])
                    mm(er, finv[:, 2, :], yi, False, True, [f"Yb{y_}", "finv"], [pek])
                    mm(ei, finv[:, 1, :], yr, True, False, [f"Yb{y_}", "finv"], [pek])
                    mm(ei, finv[:, 0, :], yi, False, True, [f"Yb{y_}", "finv"], [pek])
                    S.op("act", lambda: nc.scalar.copy(ET[:, :, kb * 2:kb * 2 + 2], pe_[:, 0:256].rearrange("p (k c) -> p c k", k=2)), [pek], [RCK])
                    S.op("dve", lambda: nc.vector.tensor_copy(ET[:, :, 64 + kb * 2:64 + kb * 2 + 2], pe_[:, 256:512].rearrange("p (k c) -> p c k", k=2)),
                         [pek], [RCK])
                fft_forward(ztm, 64, AT, cb, zkey=RBK, akey=RAK, pre_batch=pre)
                for c4 in range(32):
                    pb_ = PSB[c4 % 2]
                    pbk = f"psb{c4 % 2}"
                    for cc in range(4):
                        c = c4 * 4 + cc
                        S.op("pe", lambda c=c, cc=cc, pb_=pb_: nc.tensor.transpose(pb_[:, cc * 128:(cc + 1) * 128], ET[:, c, :], ident[:]),
                             [RCK, "ident"], [pbk])
                    src = pb_[:, 0:512].rearrange("p (cc n) -> p n cc", cc=4)
                    dst = Eb[:, :, c4 * 4:c4 * 4 + 4]
                    if c4 % 2 == 0:
                        S.op("act", lambda src=src, dst=dst: nc.scalar.copy(dst, src), [pbk], RBQ)
                    else:
                        S.op("dve", lambda src=src, dst=dst: nc.vector.tensor_copy(dst, src), [pbk], RBQ)

            Udv = U_d
            for g in range(4):
                for i in range(3):
                    dma(uraw[:, i, 1:L + 1], Udv[i * 512 + g * 128:i * 512 + (g + 1) * 128, :], ["U_d"], [RAK])
                S.op("pool", lambda: nc.gpsimd.memset(uraw[:, :, 0:1], 0.0), [RAK], [RAK])
                S.op("pool", lambda: nc.gpsimd.memset(uraw[:, :, L + 1:L + 2], 0.0), [RAK], [RAK])
                for i, (dst, dkey) in enumerate(((vt, "vt"), (x1t, "x1t"), (x2t, RCK))):
                    ct = i * 4 + g
                    for q in range(16):
                        e = "dve"
                        st = sct[q % 2]
                        sk = f"gt{q % 2}"
                        o0 = q * 512
                        S.op(e, lambda e=e, st=st, i=i, o0=o0, ct=ct: V(e).tensor_scalar(st[:], uraw[:, i, o0:o0 + 512], cws[:, ct, 0:1], cbs[:, ct:ct + 1],
                                                                                      ALU.mult, ALU.add), [RAK, "cws", "cbs"], [sk])
                        S.op(e, lambda e=e, st=st, i=i, o0=o0, ct=ct: V(e).scalar_tensor_tensor(out=st[:], in0=uraw[:, i, o0 + 1:o0 + 513], scalar=cws[:, ct, 1:2],
                                                                                             in1=st[:], op0=ALU.mult, op1=ALU.add), [RAK, "cws", sk], [sk])
                        S.op(e, lambda e=e, st=st, i=i, o0=o0, ct=ct, dst=dst: V(e).scalar_tensor_tensor(out=dst[:, o0:o0 + 512], in0=uraw[:, i, o0 + 2:o0 + 514],
                                                                                                      scalar=cws[:, ct, 2:3], in1=st[:], op0=ALU.mult, op1=ALU.add),
                             [RAK, "cws", sk], [dkey])
                masked_quarter(x2q, x2t, "x2q", RCK)
                snap("vt0", vt[:], "vt", L)
                snap("x1t0", x1t[:], "x1t", L)
                bounce_to_ztm(vt, "vt")
                conv_core(g * 2 + 0)
                vt3 = vt[:, :].rearrange("p (n1 n2) -> p n1 n2", n2=128)
                x13 = x1t[:, :].rearrange("p (n1 n2) -> p n1 n2", n2=128)
                for nb in range(16):
                    ps = PS[nb % 2]
                    pk = f"ps{nb % 2}"
                    for j in range(8):
                        n2 = nb * 8 + j
                        mm(ps[:, j * 64:(j + 1) * 64], Eb[:, n2, :], M2s[:, n2, :], True, True, RBQ + ["M2s"], [pk])
                    psv = ps[:, :].rearrange("p (j n) -> p n j", j=8)
                    gtv = gt[nb % 2][:, :].rearrange("p (n j) -> p n j", j=8)
                    gk_ = f"gt{nb % 2}"
                    S.op("dve", lambda psv=psv, gtv=gtv, nb=nb, g=g: nc.vector.scalar_tensor_tensor(out=gtv, in0=vt3[:, :, nb * 8:(nb + 1) * 8],
                                                                                             scalar=skps[:, g:g + 1], in1=psv, op0=ALU.mult, op1=ALU.add),
                         [pk, "vt", "skps"], [gk_])
                    S.op("pool", lambda gtv=gtv, nb=nb: nc.gpsimd.tensor_tensor(vt3[:, :, nb * 8:(nb + 1) * 8], gtv, x13[:, :, nb * 8:(nb + 1) * 8], ALU.mult),
                         [gk_, "x1t"], ["vt"])
                snap("zz0", vt[:], "vt", L)
                masked_quarter(zzq, vt, "zzq", "vt")
                bounce_to_ztm(vt, "vt")
                conv_core(g * 2 + 1)
                zq3 = zzq[:, :].rearrange("p (n1 n2) -> p n1 n2", n2=128)
                xq3 = x2q[:, :].rearrange("p (n1 n2) -> p n1 n2", n2=128)
                hy3 = hyst[:, :].rearrange("p (n1 n2) -> p n1 n2", n2=128)
                for nb in range(4):
                    ps = PS[nb % 2]
                    pk = f"ps{nb % 2}"
                    for j in range(32):
                        n2 = nb * 32 + j
                        mm(ps[:, j * 16:(j + 1) * 16], Eb[:, n2, :], M2qs[:, n2, :], True, True, RBQ + ["M2qs"], [pk])
                    psv = ps[:, :].rearrange("p (j n) -> p n j", j=32)
                    gtv = gt[nb % 2][:, :].rearrange("p (n j) -> p n j", j=32)
                    gk_ = f"gt{nb % 2}"
                    S.op("dve", lambda psv=psv, gtv=gtv, nb=nb, g=g: nc.vector.scalar_tensor_tensor(out=gtv, in0=zq3[:, :, nb * 32:(nb + 1) * 32],
                                                                                             scalar=skps[:, 4 + g:5 + g], in1=psv, op0=ALU.mult, op1=ALU.add),
                         [pk, "zzq", "skps"], [gk_])
                    S.op("pool", lambda gtv=gtv, nb=nb: nc.gpsimd.tensor_tensor(hy3[:, :, nb * 32:(nb + 1) * 32], gtv, xq3[:, :, nb * 32:(nb + 1) * 32], ALU.mult),
                         [gk_, "x2q"], ["hyst"])
                dma(hy_d[g * 128:(g + 1) * 128, :], hyst[:], ["hyst"], ["hy_d"])
            S.barrier()

        def phase4():
          with ExitStack() as p4:
            WoA = sb("WoA", [64, 8, D], BF16, p4)
            WoH = sb("WoH", [128, 4, D], BF16, p4)
            with ExitStack() as p4c:
                wsa = sb("wsa", [64, 8, D], F32, p4c)
                wsh = sb("wsh", [128, 4, D], F32, p4c)
                dma(wsa[:], w_out[0:512, :].rearrange("(h d) n -> d h n", d=64), [], ["wsa"])
                dma(wsh[:], w_out[512:1024, :].rearrange("(g p) n -> p g n", p=128), [], ["wsh"])
                S.op("dve", lambda: nc.vector.tensor_copy(WoA[:], wsa[:]), ["wsa"], ["WoA"])
                S.op("pool", lambda: nc.gpsimd.tensor_copy(WoH[:], wsh[:]), ["wsh"], ["WoH"])
                S.barrier()
            attc = sb("attc", [64, 8, 512], BF16, p4)
            hyc = sb("hyc", [128, 4, 512], BF16, p4)
            sqb = sb("sqb4", [128, 8, 512], F32, p4)
            rs = sb("rs4", [128, 512], F32, p4)
            mixA = sb("mixA", [64, 8, 512], BF16, p4)
            mixH = sb("mixH", [128, 4, 512], BF16, p4)
            h1 = sb("h1", [128, 8, 512], F32, p4)
            mT = sb("mT", [128, 8, 512], BF16, p4)
            Wmi_s = [sb(f"Wmi_s{i}", [128, 8, 1024], BF16, p4) for i in range(2)]
            Wmo_s = [sb(f"Wmo_s{i}", [128, 32, 128], BF16, p4) for i in range(2)]
            fT = sb("fT", [128, 32, 512], BF16, p4)
            rb = [sb(f"rb{i}", [128, 512], F32, p4) for i in range(2)]
            xqv = xTq.rearrange("(k p) t -> p k t", p=128)
            Wmiv = Wmi_d.rearrange("(k p) n -> p k n", p=128)
            outv = outT.rearrange("(k p) t -> p k t", p=128)
            nw = 0
            nwo = 0
            for j in range(4):
                tsl = slice(j * 512, (j + 1) * 512)
                dma(attc[:], att_d[:, tsl].rearrange("(h d) t -> d h t", d=64), ["att_d"], ["attc", "na_src"])
                dma(hyc[:], hy_d[:, tsl].rearrange("(g p) t -> p g t", p=128), ["hy_d"], ["hyc", "nh_src"])
                dma(h1[:], xqv[:, :, tsl], [], ["h1", "n2_src", "nf_src"])
                rms_apply(attc[:], gas, mixA, 1.0 / 512.0, 8, "na", sqb, rs, np_=64)
                rms_apply(hyc[:], ghs, mixH, 1.0 / 512.0, 4, "nh", sqb, rs)
                for ct in range(8):
                    csl = slice(ct * 128, (ct + 1) * 128)
                    for h in range(8):
                        mm(PS[1][:, :], WoA[:, h, csl], mixA[:, h, :], h == 0, False, ["WoA", "na_dst"], ["ps1"])
                    for g in range(4):
                        mm(PS[1][:, :], WoH[:, g, csl], mixH[:, g, :], False, g == 3, ["WoH", "nh_dst"], ["ps1"])
                    S.op("dve", lambda ct=ct: nc.vector.tensor_tensor(h1[:, ct, :], h1[:, ct, :], PS[1][:, :], ALU.add), ["ps1", "h1"], ["h1", "n2_src", "nf_src"])
                rms_apply(h1[:], g2s, mT, 1.0 / D, 8, "n2", sqb, rs)
                for pc in range(4):
                    s_ = nw % 2
                    dma(Wmi_s[s_][:], Wmiv[:, :, pc * 1024:(pc + 1) * 1024], ["Wmid"], [f"Wmi_s{s_}"])
                    nw += 1
                    for ft in range(8):
                        f = pc * 8 + ft
                        pf = PS[2 + f % 2]
                        pfk = f"ps{2 + f % 2}"
                        for k in range(8):
                            mm(pf[:, :], Wmi_s[s_][:, k, ft * 128:(ft + 1) * 128], mT[:, k, :], k == 0, k == 7, [f"Wmi_s{s_}", "n2_dst"], [pfk])
                        r_ = f % 2
                        S.op("act", lambda pf=pf, r_=r_: nc.scalar.activation(rb[r_][:], pf[:, :], AF.Relu), [pfk], [f"rb{r_}"])
                        e = ("pool", "dve")[f % 2]
                        S.op(e, lambda e=e, r_=r_, f=f: V(e).tensor_tensor(fT[:, f, :], rb[r_][:], rb[r_][:], ALU.mult), [f"rb{r_}"], ["fT"])
                for ct in range(8):
                    s_ = nwo % 2
                    dma(Wmo_s[s_][:], Wmo_d[ct], ["Wmod"], [f"Wmo_s{s_}"])
                    nwo += 1
                    po = PS[4 + ct % 2]
                    pok = f"ps{4 + ct % 2}"
                    for f in range(32):
                        mm(po[:, :], Wmo_s[s_][:, f, :], fT[:, f, :], f == 0, f == 31, [f"Wmo_s{s_}", "fT"], [pok])
                    S.op("dve", lambda ct=ct, po=po: nc.vector.tensor_tensor(h1[:, ct, :], h1[:, ct, :], po[:, :], ALU.add), [pok, "h1"], ["h1", "n2_src", "nf_src"])
                rms_apply(h1[:], gfs, h1, 1.0 / D, 8, "nf", sqb, rs)
                dma(outv[:, :, tsl], h1[:], ["h1", "nf_dst"], ["outT"])
            S.barrier()

        if 0 in phases:
            phase0()
        if 1 in phases:
            phase12()
        if 3 in phases:
            phase3()
        if 4 in phases:
            phase4()
        if debug:
            with ExitStack() as pd:
                for name in debug:
                    if name not in ("att_d", "hy_d", "U_d", "H_d0"):
                        continue
                    src_, shp = {"att_d": (att_d, [512, NTOK]), "hy_d": (hy_d, [512, NTOK]), "U_d": (U_d[0:512, :], [512, L]),
                                 "H_d0": (H_d[0:4, :, 0:4].rearrange("a p k r c -> (a p) (k r c)"), [512, 2048])}[name]
                    dbo = nc.dram_tensor("dbg_" + name, shp, BF16, kind="ExternalOutput").ap()
                    dbs = sb("dbs_" + name, [128, 4, shp[1]], BF16, pd)
                    dma(dbs[:], src_.rearrange("(a p) t -> p a t", p=128), [name], ["dbs_" + name])
                    dma(dbo.rearrange("(a p) t -> p a t", p=128), dbs[:], ["dbs_" + name], ["dbg_" + name])
                S.barrier()
        S.barrier()
    return nc, S


def _core_inputs(inp, core):
    T = _tables()
    b, r = core // 4, core % 4
    f = lambda a: np.ascontiguousarray(np.asarray(a, dtype=np.float32))
    x = np.asarray(inp["x"], dtype=np.float32)
    xT = np.ascontiguousarray(x[b].T)
    tq = slice(r * NTOK, (r + 1) * NTOK)
    qm = np.zeros((128, 4), np.float32)
    qm[:, r] = 1.0
    m = {
        "xT": xT, "xTq": np.ascontiguousarray(xT[:, tq]),
        "w_in": f(inp["w_in"][0]), "w_out": f(inp["w_out"][0]), "w_mi": f(inp["w_mlp_in"][0]), "w_mo": f(inp["w_mlp_out"][0]),
        "g1": f(np.asarray(inp["norm1_g"][0]).reshape(8, 128).T), "g2": f(np.asarray(inp["norm2_g"][0]).reshape(8, 128).T),
        "gf": f(np.asarray(inp["final_g"]).reshape(8, 128).T),
        "gq": f(np.asarray(inp["q_norm_g"][0]).reshape(64, 1)), "gk": f(np.asarray(inp["k_norm_g"][0]).reshape(64, 1)),
        "ga": f(np.asarray(inp["attn_out_g"][0]).reshape(8, 64).T), "gh": f(np.asarray(inp["hy_out_g"][0]).reshape(4, 128).T),
        "cw": f(np.asarray(inp["hy_conv_w"][0]).T.reshape(12, 128, 3).transpose(1, 0, 2)),
        "cb": f(np.asarray(inp["hy_conv_b"][0]).reshape(12, 128).T),
        "fw1": f(inp["filt_w1"][0]), "fw2": f(inp["filt_w2"][0]), "fw3": f(inp["filt_w3"][0]), "fw4": f(inp["filt_w4"][0]),
        "fb": f(np.stack([np.asarray(inp["filt_b1"][0]), np.asarray(inp["filt_b2"][0]), np.asarray(inp["filt_b3"][0])], axis=1)),
        "ffreq": f(np.asarray(inp["filt_freq"][0]).reshape(64, 1)),
        "fdel": f(np.asarray(inp["filt_deltas"][0]).reshape(16, 128).T),
        "skp": f(np.asarray(inp["hy_skip_d"][0]).reshape(2, 4, 128).transpose(2, 0, 1).reshape(128, 8)),
        "qmask": qm,
        "c_cos": T["cosT"], "c_sin": T["sinT"],
        "c_cosq": np.ascontiguousarray(T["cosT"][:, tq]), "c_sinq": np.ascontiguousarray(T["sinT"][:, tq]),
        "c_zT": np.ascontiguousarray(np.stack([T["zT"], T["zrT"]])), "c_t": np.ascontiguousarray(np.stack([T["t"], T["trev"]])),
        "c_tb": T["tb"], "c_jt": T["jt"],
        "c_fpack": T["fpack"], "c_G": T["G"], "c_finv": T["finv"], "c_M2": T["M2"],
        "c_M2q": np.ascontiguousarray(T["M2"][:, :, r * 16:(r + 1) * 16]),
        "c_rot": T["rot"], "c_sel": T["sel"], "c_ones": T["ones"], "c_ident": T["ident"],
    }
    return m


def _emit(nc, S):
    from contextlib import ExitStack
    st = ExitStack()
    sems = {e: [st.enter_context(nc.semaphore(f"s_{e}{i}")) for i in range(4)] for e in S.CE}
    dsems = [st.enter_context(nc.semaphore(f"d{i}")) for i in range(S.NDS)]
    info = S.emit(sems, dsems)
    return st, info


def kernel(**inputs):
    nc, S = build_program()
    st, _ = _emit(nc, S)
    with st:
        in_maps = [_core_inputs(inputs, c) for c in range(8)]
        res = run_bass_kernel_spmd(nc, in_maps, core_ids=list(range(8)))
    out = np.empty((NB, L, D), np.float32)
    for c in range(8):
        b, r = c // 4, c % 4
        out[b, r * NTOK:(r + 1) * NTOK, :] = np.asarray(res.results[c]["outT"]).T
    return out
```

```python
import math
import numpy as np
import ml_dtypes
import concourse.bass as bass
import concourse.mybir as mybir
from concourse.bass_utils import run_bass_kernel_spmd

F32 = mybir.dt.float32
BF16 = mybir.dt.bfloat16
ALU = mybir.AluOpType
AF = mybir.ActivationFunctionType
bf16_np = ml_dtypes.bfloat16

D = 1024
L = 8192
NB = 2
NTOK = 2048
HD = 64
NQH = 8
NKV = 2
HYW = 512
DFF = 4096
EPS = 1e-6
NFFT = 2 * L
TWO_PI = 2.0 * math.pi
MAGIC = 12582912.0


class Op:
    __slots__ = ("eng", "fn", "reads", "writes", "dma", "idx", "gid", "waits", "signal", "clock",
                 "sem", "val")


class Sched:
    CE = ("pe", "act", "dve", "pool", "sp")
    EPOCH = 12000
    NDS = 40

    def __init__(self, nc):
        self.nc = nc
        self.E = {"pe": nc.tensor, "act": nc.scalar, "dve": nc.vector, "pool": nc.gpsimd,
                  "sp": nc.sync}
        self.ops = []
        self.last_w = {}
        self.readers = {}
        self.known = {e: {} for e in self.CE}
        self.known_dma = {e: set() for e in self.CE}
        self.cnt = {e: 0 for e in self.CE}
        self.last_op = {e: None for e in self.CE}
        self.live_dma = []

    def op(self, eng, fn, reads=(), writes=(), dma=False):
        o = Op()
        o.eng, o.fn, o.reads, o.writes, o.dma = eng, fn, tuple(reads), tuple(writes), dma
        o.signal = dma
        o.sem = None
        o.val = None
        o.gid = len(self.ops)
        deps = {}
        for k in o.reads:
            w = self.last_w.get(k)
            if w is not None:
                deps[w.gid] = w
        for k in o.writes:
            w = self.last_w.get(k)
            if w is not None:
                deps[w.gid] = w
            for r in self.readers.get(k, ()):
                deps[r.gid] = r
        self._finish(o, deps.values())
        for k in o.writes:
            self.last_w[k] = o
            self.readers[k] = []
        for k in o.reads:
            lst = self.readers.setdefault(k, [])
            if not dma:
                lst[:] = [r for r in lst if r.dma or r.eng != eng]
            lst.append(o)
        return o

    def _finish(self, o, deps):
        eng = o.eng
        kn = self.known[eng]
        kd = self.known_dma[eng]
        waits = []
        best = {}
        rset = set(o.reads)
        for d in deps:
            if d.dma:
                if d.gid not in kd:
                    waits.append(d)
                    kd.add(d.gid)
                continue
            if d.eng == eng:
                if eng in ("pe", "sp"):
                    continue
            if kn.get(d.eng, -1) >= d.idx:
                continue
            b = best.get(d.eng)
            if b is None or b.idx < d.idx:
                best[d.eng] = d
        for d in best.values():
            waits.append(d)
            if kn.get(d.eng, -1) < d.idx:
                kn[d.eng] = d.idx
            for e2, i2 in d.clock.items():
                if kn.get(e2, -1) < i2:
                    kn[e2] = i2
        for d in waits:
            d.signal = True
        o.waits = waits
        o.idx = self.cnt[eng]
        self.cnt[eng] += 1
        o.clock = dict(kn)
        self.ops.append(o)
        if o.fn is not None:
            self.last_op[eng] = o
        if o.dma:
            self.live_dma.append(o)

    def barrier(self):
        lasts = [self.last_op[e] for e in self.CE if self.last_op[e] is not None]
        dmas = list(self.live_dma)
        for e in self.CE:
            o = Op()
            o.eng, o.fn, o.reads, o.writes, o.dma = e, None, (), (), False
            o.signal = False
            o.sem = None
            o.val = None
            o.gid = len(self.ops)
            deps = [d for d in lasts if not d.dma and d.eng != e] + dmas
            self._finish(o, deps)
        self.live_dma = []

    def emit(self, sems, dsems):
        sig = {e: 0 for e in self.CE}
        nd = 0
        for o in self.ops:
            E = self.E[o.eng]
            for d in o.waits:
                E.wait_ge(d.sem, d.val)
            if o.fn is None:
                continue
            if o.dma:
                s = dsems[nd % self.NDS]
                rnd = nd // self.NDS
                if rnd > 0:
                    E.wait_ge(s, 16 * rnd)
                ins = o.fn()
                ins.then_inc(s, 16)
                o.sem, o.val = s, 16 * (rnd + 1)
                nd += 1
            else:
                ins = o.fn()
                if o.signal:
                    c = sig[o.eng]
                    o.sem = sems[o.eng][c // self.EPOCH]
                    o.val = c % self.EPOCH + 1
                    ins.then_inc(o.sem, 1)
                    sig[o.eng] = c + 1
        return sig, nd


_TABLES = None


def _rope_tables():
    rows = L // 64
    row = np.repeat(np.arange(rows, dtype=np.float32), 64)
    col = np.tile(np.arange(64, dtype=np.float32), rows)
    half = HD // 2
    inv_freq = (np.float32(10000.0) ** (-np.arange(0, half, 2, dtype=np.float32) / np.float32(half))).astype(np.float32)
    ang_r = (row[:, None] * inv_freq[None, :]).astype(np.float32)
    ang_c = (col[:, None] * inv_freq[None, :]).astype(np.float32)
    cos_r, sin_r = np.cos(ang_r), np.sin(ang_r)
    cos_c, sin_c = np.cos(ang_c), np.sin(ang_c)
    cosT = np.concatenate([cos_r.T, cos_r.T, cos_c.T, cos_c.T], axis=0).astype(np.float32)
    sinT = np.concatenate([sin_r.T, sin_r.T, sin_c.T, sin_c.T], axis=0).astype(np.float32)
    return np.ascontiguousarray(cosT), np.ascontiguousarray(sinT)


def _filter_pos_tables():
    f32 = np.float32
    t = np.linspace(0.0, 1.0, L, dtype=f32)
    bands = 16
    w_ang = (f32(2.0 * math.pi) * np.arange(L, dtype=f32) / f32(L)).astype(f32)
    band_f = np.linspace(1e-4, bands - 1, bands, dtype=f32)
    ang = (w_ang[:, None] * band_f[None, :]).astype(f32)
    z = np.concatenate([t[:, None], np.cos(ang), -np.sin(ang)], axis=-1).astype(f32)
    zT = np.ascontiguousarray(z.T)
    idx = (L - np.arange(L)) % L
    zrT = np.ascontiguousarray(z[idx].T)
    trev = t[idx].copy()
    return zT, zrT, t.copy(), trev


def _fft_tables():
    N = NFFT
    n1 = np.arange(128, dtype=np.float64)
    k1 = np.arange(64, dtype=np.float64)
    th = 2 * np.pi * np.outer(n1, k1 + 0.5) / 128.0
    fpack = np.concatenate([np.cos(th), -np.sin(th), np.sin(th)], axis=1)
    n2 = np.arange(128, dtype=np.float64)
    k2 = np.arange(128, dtype=np.float64)
    kap = k1[None, :, None] + 128.0 * k2[None, None, :] + 0.5
    thg = 2 * np.pi * n2[:, None, None] * kap / N
    G = np.stack([np.cos(thg), -np.sin(thg)], axis=2)
    thf = 2 * np.pi * np.outer(k2, n2) / 128.0
    finv = np.stack([np.cos(thf), np.sin(thf), -np.sin(thf)], axis=1)
    n1h = np.arange(64, dtype=np.float64)
    phi = 2 * np.pi * (k1[:, None, None] + 0.5) * (128.0 * n1h[None, None, :] + n2[None, :, None]) / N
    M2 = np.concatenate([(2.0 / N) * np.cos(phi), -(2.0 / N) * np.sin(phi)], axis=0)
    return (fpack.astype(bf16_np), G.astype(bf16_np), finv.astype(bf16_np), M2.astype(bf16_np))


def _tables():
    global _TABLES
    if _TABLES is None:
        cosT, sinT = _rope_tables()
        zT, zrT, t, trev = _filter_pos_tables()
        fpack, G, finv, M2 = _fft_tables()
        ii = np.arange(512, dtype=np.float64)
        jj = np.arange(16, dtype=np.float64)
        tbm = np.stack([ii / (L - 1), (L - ii) / (L - 1)]).astype(np.float32)
        jtm = np.stack([512.0 * jj / (L - 1), -512.0 * jj / (L - 1)]).astype(np.float32)
        rot = np.zeros((64, 64), np.float32)
        for m in range(64):
            if (m % 32) < 16:
                rot[m + 16, m] = -1.0
            else:
                rot[m - 16, m] = 1.0
        sel = np.zeros((65, 64), np.float32)
        sel[64, :] = 1.0
        _TABLES = dict(cosT=cosT, sinT=sinT, zT=zT, zrT=zrT, t=t, trev=trev, fpack=fpack, G=G,
                       finv=finv, M2=M2, rot=rot, sel=sel, tb=tbm, jt=jtm,
                       ones=np.ones((128, 128), np.float32),
                       ident=np.eye(128, dtype=np.float32).astype(bf16_np))
    return _TABLES


def build_program(phases=(0, 1, 2, 3, 4), debug=()):
    nc = bass.Bass("TRN2", target_bir_lowering=False)
    S = Sched(nc)

    def din(name, shape, dt=F32):
        return nc.dram_tensor(name, list(shape), dt, kind="ExternalInput").ap()

    def dscr(name, shape, dt):
        return nc.dram_tensor(name, list(shape), dt, kind="Internal").ap()

    xT = din("xT", [D, L])
    xTq = din("xTq", [D, NTOK])
    w_in = din("w_in", [D, 2304])
    w_out = din("w_out", [D, D])
    w_mi = din("w_mi", [D, DFF])
    w_mo = din("w_mo", [DFF, D])
    g1 = din("g1", [128, 8]); g2 = din("g2", [128, 8]); gf = din("gf", [128, 8])
    gq = din("gq", [64, 1]); gk = din("gk", [64, 1])
    ga = din("ga", [64, 8]); gh = din("gh", [128, 4])
    cw = din("cw", [128, 12, 3]); cb = din("cb", [128, 12])
    fw1 = din("fw1", [33, 64]); fw2 = din("fw2", [64, 64]); fw3 = din("fw3", [64, 64])
    fw4 = din("fw4", [64, 2048])
    fb = din("fb", [64, 3]); ffreq = din("ffreq", [64, 1])
    fdel = din("fdel", [128, 16])
    skp = din("skp", [128, 8])
    qmask = din("qmask", [128, 4])
    c_cos = din("c_cos", [64, L]); c_sin = din("c_sin", [64, L])
    c_cosq = din("c_cosq", [64, NTOK]); c_sinq = din("c_sinq", [64, NTOK])
    c_zT = din("c_zT", [2, 33, L])
    c_t = din("c_t", [2, L])
    c_tb = din("c_tb", [2, 512])
    c_jt = din("c_jt", [2, 16])
    c_fpack = din("c_fpack", [128, 192], BF16)
    c_G = din("c_G", [128, 64, 2, 128], BF16)
    c_finv = din("c_finv", [128, 3, 128], BF16)
    c_M2 = din("c_M2", [128, 128, 64], BF16)
    c_M2q = din("c_M2q", [128, 128, 16], BF16)
    c_rot = din("c_rot", [64, 64]); c_sel = din("c_sel", [65, 64]); c_ones = din("c_ones", [128, 128])
    c_ident = din("c_ident", [128, 128], BF16)
    outT = nc.dram_tensor("outT", [D, NTOK], F32, kind="ExternalOutput").ap()

    U_d = dscr("U_d", [1536, L], BF16)
    H_d = dscr("H_d", [8, 128, 64, 4, 128], BF16)
    taps_d = dscr("taps_d", [128, NFFT], BF16)
    zcm_d = dscr("zcm_d", [128, L], BF16)
    att_d = dscr("att_d", [512, NTOK], BF16)
    hy_d = dscr("hy_d", [512, NTOK], BF16)
    Wmi_d = dscr("Wmi_d", [D, DFF], BF16)
    Wmo_d = dscr("Wmo_d", [8, 128, 32, 128], BF16)

    from contextlib import ExitStack
    es = ExitStack()

    _names = {}

    def sb(name, shape, dt, stack=None):
        n_ = _names.get(name, 0)
        _names[name] = n_ + 1
        nm = name if n_ == 0 else f"{name}_r{n_}"
        return (stack or es).enter_context(nc.sbuf_tensor(nm, list(shape), dt))

    def dma(out, in_, reads, writes, eng="sp"):
        return S.op(eng, lambda: S.E[eng].dma_start(out=out, in_=in_), reads, writes, dma=True)

    def mm(out, lhsT, rhs, start, stop, reads, writes):
        return S.op("pe", lambda: nc.tensor.matmul(out, lhsT, rhs, start=start, stop=stop), reads, writes)

    rr = {"i": 0}

    def alt(engs=("dve", "pool")):
        rr["i"] += 1
        return engs[rr["i"] % len(engs)]

    def V(eng):
        return S.E[eng]

    with es:
        PSW = [es.enter_context(nc.psum_tensor(f"psw{i}", [128, 1024], F32)) for i in range(3)]
        PS = [PSW[i // 2][:, (i % 2) * 512:(i % 2 + 1) * 512] for i in range(6)]
        PSB = [es.enter_context(nc.psum_tensor(f"psb{i}", [128, 1024], BF16)) for i in range(2)]
        PSX = [PSB[i].bitcast(F32)[:, :] for i in range(2)]

        ones = sb("ones", [128, 128], F32)
        ident = sb("ident", [128, 128], BF16)
        rot = sb("rot", [64, 64], F32)
        sel = sb("sel", [65, 64], F32)
        epsT = sb("epsT", [128, 1], F32)
        onesbf = sb("onesbf", [128, 128], BF16)
        g1s = sb("g1s", [128, 8], F32); g2s = sb("g2s", [128, 8], F32); gfs = sb("gfs", [128, 8], F32)
        gqs = sb("gqs", [64, 1], F32); gks = sb("gks", [64, 1], F32)
        gas = sb("gas", [64, 8], F32); ghs = sb("ghs", [128, 4], F32)
        cws = sb("cws", [128, 12, 3], F32); cbs = sb("cbs", [128, 12], F32)
        skps = sb("skps", [128, 8], F32); qms = sb("qms", [128, 4], F32)
        for t_, d_ in ((ones, c_ones), (ident, c_ident), (rot, c_rot), (sel, c_sel), (g1s, g1), (g2s, g2),
                       (gfs, gf), (gqs, gq), (gks, gk), (gas, ga), (ghs, gh), (cws, cw), (cbs, cb),
                       (skps, skp), (qms, qmask)):
            dma(t_[:], d_, [], [t_.name if hasattr(t_, "name") else id(t_)])
        KEY = lambda t_: t_.name if hasattr(t_, "name") else id(t_)
        S.op("dve", lambda: nc.vector.memset(epsT[:], EPS), [], ["epsT"])
        S.op("pool", lambda: nc.gpsimd.memset(onesbf[:], 1.0), [], ["onesbf"])

        FC = {}
        SNAP = {}

        def snap(name, src_ap, key, ncols):
            if name not in debug or name in SNAP:
                return
            SNAP[name] = nc.dram_tensor("dbg_" + name, [128, ncols], BF16, kind="ExternalOutput").ap()
            dma(SNAP[name], src_ap, [key], ["dbg_" + name])


        def load_fft_consts(stack):
            FC["fpack"] = sb("fpack", [128, 192], BF16, stack)
            FC["finv"] = sb("finv", [128, 3, 128], BF16, stack)
            FC["Gq"] = [sb(f"Gq{i}", [128, 8, 2, 128], BF16, stack) for i in range(2)]
            dma(FC["fpack"][:], c_fpack, [], ["fpack"])
            dma(FC["finv"][:], c_finv, [], ["finv"])

        def rsqrt(out, in_, scale, reads, writes, np_=128):
            S.op("act", lambda: nc.scalar.activation(out, in_, AF.Sqrt, bias=epsT[0:np_, :], scale=scale),
                 list(reads) + ["epsT"], writes)
            S.op("dve", lambda: nc.vector.reciprocal(out, out), writes, writes)

        def fft_forward(ztm, K, AT, cb_batch, zkey="ztm", akey="AT", pre_batch=None):
            fpack = FC["fpack"]
            for c2 in range(64):
                ps = PS[c2 % 2]
                pk = f"ps{c2 % 2}"
                for cc in range(2):
                    c = c2 * 2 + cc
                    mm(ps[:, cc * 192:(cc + 1) * 192], ztm[0:K, c, :], fpack[0:K, :], True, True,
                       [(zkey, c // 32), "fpack"], [pk])
                e = alt(("act", "dve"))
                src = ps[:, 0:384].rearrange("p (cc j) -> p j cc", cc=2)
                dst = AT[:, :, c2 * 2:c2 * 2 + 2]
                if e == "act":
                    S.op("act", lambda dst=dst, src=src: nc.scalar.copy(dst, src), [pk], [akey])
                else:
                    S.op("dve", lambda dst=dst, src=src: nc.vector.tensor_copy(dst, src), [pk], [akey])

            def load_g(q8):
                dma(FC["Gq"][q8 % 2][:], c_G[:, q8 * 8:(q8 + 1) * 8], [], [f"Gq{q8 % 2}"])

            load_g(0)
            for kb in range(32):
                q4 = kb // 4
                Gq = FC["Gq"][q4 % 2]
                gkey = f"Gq{q4 % 2}"
                if kb % 4 == 0 and q4 + 1 < 8:
                    load_g(q4 + 1)
                if pre_batch is not None:
                    pre_batch(kb)
                ps = PS[2 + kb % 2]
                pk = f"ps{2 + kb % 2}"
                for kk in range(2):
                    k1 = kb * 2 + kk
                    kl = k1 % 8
                    R = AT[:, k1, :]
                    I = AT[:, 64 + k1, :]
                    NI = AT[:, 128 + k1, :]
                    zr = ps[:, kk * 256:kk * 256 + 128]
                    zi = ps[:, kk * 256 + 128:kk * 256 + 256]
                    mm(zr, Gq[:, kl, 0, :], R, True, False, [akey, gkey], [pk])
                    mm(zr, Gq[:, kl, 1, :], NI, False, True, [akey, gkey], [pk])
                    mm(zi, Gq[:, kl, 1, :], R, True, False, [akey, gkey], [pk])
                    mm(zi, Gq[:, kl, 0, :], I, False, True, [akey, gkey], [pk])
                cb_batch(kb, ps, pk)

        def phase0():
          with ExitStack() as p0w:
            wst = [sb(f"wst{i}", [128, 8, 512], F32, p0w) for i in range(2)]
            wbf = [sb(f"wbf{i}", [128, 8, 512], BF16, p0w) for i in range(2)]
            n = 0
            jobs = []
            for cbk in range(8):
                jobs.append((w_mi.rearrange("(k p) n -> p k n", p=128)[:, :, cbk * 512:(cbk + 1) * 512],
                             Wmi_d.rearrange("(k p) n -> p k n", p=128)[:, :, cbk * 512:(cbk + 1) * 512]))
            for kr in range(4):
                for cbk in range(2):
                    jobs.append((w_mo.rearrange("(k p) n -> p k n", p=128)[:, kr * 8:(kr + 1) * 8, cbk * 512:(cbk + 1) * 512],
                                 (kr, cbk)))
            for src, dst in jobs:
                s_ = n % 2
                dma(wst[s_][:], src, [], [f"wst{s_}"])
                e = ("dve", "act")[n % 2]
                if e == "act":
                    S.op("act", lambda s_=s_: nc.scalar.copy(wbf[s_][:], wst[s_][:]), [f"wst{s_}"], [f"wbf{s_}"])
                else:
                    S.op(e, lambda s_=s_, e=e: V(e).tensor_copy(wbf[s_][:], wst[s_][:]), [f"wst{s_}"], [f"wbf{s_}"])
                if n < 8:
                    dma(dst, wbf[s_][:], [f"wbf{s_}"], ["Wmid"])
                else:
                    kr_, cbk_ = dst
                    for c_ in range(4):
                        dma(Wmo_d[cbk_ * 4 + c_, :, kr_ * 8:(kr_ + 1) * 8, :], wbf[s_][:, :, c_ * 128:(c_ + 1) * 128], [f"wbf{s_}"], ["Wmod"])
                n += 1
            S.barrier()
          with ExitStack() as p0:
            load_fft_consts(p0)

            w1s = sb("w1s", [33, 64], F32, p0); w2s = sb("w2s", [64, 64], F32, p0); w3s = sb("w3s", [64, 64], F32, p0)
            w4b = sb("w4b", [64, 2048], BF16, p0)
            fbs = sb("fbs", [64, 3], F32, p0); frs = sb("frs", [64, 1], F32, p0); ffb = sb("ffb", [64, 3], F32, p0)
            dls = sb("dls", [128, 16], F32, p0); ndl = sb("ndl", [128, 16], F32, p0)
            negpi = sb("negpi", [128, 1], F32, p0)
            H3 = sb("H3", [64, 2, L], BF16, p0)
            p0m = ExitStack()
            w4s = sb("w4s", [64, 2048], F32, p0m)
            for t_, d_, k_ in ((w1s, fw1, "w1s"), (w2s, fw2, "w2s"), (w3s, fw3, "w3s"), (w4s, fw4, "w4s"),
                               (fbs, fb, "fbs"), (frs, ffreq, "frs"), (dls, fdel, "dls")):
                dma(t_[:], d_, [], [k_])
            S.op("dve", lambda: nc.vector.tensor_copy(w4b[:], w4s[:]), ["w4s"], ["w4b"])
            S.op("dve", lambda: nc.vector.tensor_scalar(ffb[:], fbs[:], frs[:, 0:1], None, ALU.mult), ["fbs", "frs"], ["ffb"])
            S.op("dve", lambda: nc.vector.tensor_scalar(ndl[:], dls[:], -1.0, None, ALU.mult), ["dls"], ["ndl"])
            S.op("dve", lambda: nc.vector.tensor_tensor(ndl[:], ndl[:], dls[:], ALU.min), ["dls", "ndl"], ["ndl"])
            S.op("dve", lambda: nc.vector.memset(negpi[:], 0.0), [], ["negpi"])
            zc = [sb(f"zc{i}", [33, 512], F32, p0m) for i in range(4)]
            faL = [sb(f"fa{i}", [64, 512], F32, p0m) for i in range(2)]
            fbufL = [sb(f"fbuf{i}", [64, 512], F32, p0m) for i in range(2)]
            fhL = [sb(f"fh{i}", [64, 512], F32, p0m) for i in range(2)]

            def sin_layer(ps, li, dst, pk, v_, dkey):
                fa, fbuf = faL[v_], fbufL[v_]
                fak, fbk = f"fa{v_}", f"fbuf{v_}"
                S.op("dve", lambda: nc.vector.tensor_scalar(fa[:], ps[0:64, :], frs[:, 0:1], ffb[:, li:li + 1], ALU.mult, ALU.add),
                     [pk, "frs", "ffb"], [fak])
                S.op("dve", lambda: nc.vector.tensor_scalar(fbuf[:], fa[:], 1.0 / TWO_PI, MAGIC, ALU.mult, ALU.add), [fak], [fbk])
                S.op("dve", lambda: nc.vector.tensor_scalar(fbuf[:], fbuf[:], -MAGIC, -TWO_PI, ALU.add, ALU.mult), [fbk], [fbk])
                S.op("dve", lambda: nc.vector.tensor_tensor(fa[:], fa[:], fbuf[:], ALU.add), [fak, fbk], [fak])
                S.op("act", lambda: nc.scalar.activation(dst, fa[:], AF.Sin, bias=negpi[0:64, :], scale=1.0), [fak, "negpi"], [dkey])

            for j in range(16):
                for var in range(2):
                    s_ = (j % 2) * 2 + var
                    psm = PS[4 + var]
                    pmk = f"ps{4 + var}"
                    fh = fhL[var]
                    fhk = f"fh{var}"
                    dma(zc[s_][:], c_zT[var, :, j * 512:(j + 1) * 512], [], [f"zc{s_}"])
                    mm(psm[0:64, :], w1s[:], zc[s_][:], True, True, [f"zc{s_}", "w1s"], [pmk])
                    sin_layer(psm, 0, fh[:], pmk, var, fhk)
                    mm(psm[0:64, :], w2s[:], fh[:], True, True, [fhk, "w2s"], [pmk])
                    sin_layer(psm, 1, fh[:], pmk, var, fhk)
                    mm(psm[0:64, :], w3s[:], fh[:], True, True, [fhk, "w3s"], [pmk])
                    sin_layer(psm, 2, H3[:, var, j * 512:(j + 1) * 512], pmk, var, ("H3", var))

            S.barrier()
            p0m.close()
            tapsU = sb("tapsU", [128, NFFT], BF16, p0)
            ztmf = sb("ztmf", [128, 128, 128], BF16, p0)
            ATf = sb("ATf", [128, 192, 128], BF16, p0)
            tb = sb("tb", [128, 2, 512], F32, p0)
            jt = sb("jt", [128, 2, 16], F32, p0)
            for hf in range(2):
                dma(tb[:, hf, :], c_tb[hf:hf + 1, :].broadcast_to([128, 512]), [], ["tb"])
                dma(jt[:, hf, :], c_jt[hf:hf + 1, :].broadcast_to([128, 16]), [], ["jt"])
            wbase = [sb(f"wbase{i}", [128, 512], F32, p0) for i in range(2)]
            wcj = [sb(f"wcj{i}", [128, 16], F32, p0) for i in range(2)]
            tpf = [sb(f"tpf{i}", [128, 512], F32, p0) for i in range(2)]
            psum_ = sb("psum_", [128, 32], F32, p0)
            nrm = sb("nrm", [128, 2], F32, p0)
            Hst = [sb(f"Hst{i}", [128, 2, 4, 128], BF16, p0) for i in range(4)]
            for go in range(8):
                g, o = go // 2, go % 2
                for half in range(2):
                    col0 = o * 1024 + half * 512 + g * 128
                    ct = col0 // 128
                    S.op("act", lambda half=half, ct=ct: nc.scalar.activation(wbase[half][:], tb[:, half, :], AF.Exp, scale=ndl[:, ct:ct + 1]),
                         ["tb", "ndl"], [f"wbase{half}"])
                    S.op("act", lambda half=half, ct=ct: nc.scalar.activation(wcj[half][:], jt[:, half, :], AF.Exp, scale=ndl[:, ct:ct + 1]),
                         ["jt", "ndl"], [f"wcj{half}"])
                    for j in range(16):
                        s_ = j % 2
                        pst = PS[4 + s_]
                        ptk = f"ps{4 + s_}"
                        mm(pst[:, :], w4b[:, col0:col0 + 128], H3[:, half, j * 512:(j + 1) * 512], True, True,
                           [("H3", half), "w4b"], [ptk])
                        S.op("dve", lambda s_=s_, pst=pst, half=half, j=j: nc.vector.scalar_tensor_tensor(
                            out=tpf[s_][:], in0=pst[:, :], scalar=wcj[half][:, j:j + 1], in1=wbase[half][:], op0=ALU.mult, op1=ALU.mult),
                            [ptk, f"wcj{half}", f"wbase{half}"], [f"tpf{s_}"])
                        idx = half * 16 + j
                        if half == 1 and j == 0:
                            S.op("dve", lambda s_=s_: nc.vector.memset(tpf[s_][:, 0:1], 0.0), [f"tpf{s_}"], [f"tpf{s_}"])
                        S.op("dve", lambda idx=idx, s_=s_: nc.vector.tensor_reduce(psum_[:, idx:idx + 1], tpf[s_][:], mybir.AxisListType.X, ALU.add,
                                                                                  apply_absolute_value=True), [f"tpf{s_}"], ["psum_"])
                        S.op("act", lambda half=half, j=j, s_=s_: nc.scalar.copy(tapsU[:, half * L + j * 512: half * L + (j + 1) * 512], tpf[s_][:]),
                             [f"tpf{s_}"], ["tapsU"])
                S.op("dve", lambda: nc.vector.tensor_reduce(nrm[:, 0:1], psum_[:], mybir.AxisListType.X, ALU.add), ["psum_"], ["nrm"])
                S.op("dve", lambda: nc.vector.reciprocal(nrm[:, 0:1], nrm[:, 0:1]), ["nrm"], ["nrm"])
                S.op("dve", lambda: nc.vector.tensor_scalar(nrm[:, 1:2], nrm[:, 0:1], -1.0, None, ALU.mult), ["nrm"], ["nrm"])
                S.op("dve", lambda: nc.vector.tensor_scalar(tapsU[:, 0:L], tapsU[:, 0:L], nrm[:, 0:1], None, ALU.mult), ["tapsU", "nrm"], ["tapsU"])
                S.op("act", lambda: nc.scalar.mul(tapsU[:, L:NFFT], tapsU[:, L:NFFT], nrm[:, 1:2]), ["tapsU", "nrm"], ["tapsU"])
                tv = taps_d.rearrange("c (n1 n2) -> n1 c n2", n2=128)
                for q4 in range(4):
                    dma(taps_d[q4 * 32:(q4 + 1) * 32, :], tapsU[q4 * 32:(q4 + 1) * 32, :], ["tapsU"], [("taps_d", q4)])
                for q4 in range(4):
                    dma(ztmf[:, q4 * 32:(q4 + 1) * 32, :], tv[:, q4 * 32:(q4 + 1) * 32, :], [("taps_d", q4)], [("ztm", q4)])

                def cb_filter(kb, ps, pk, go=go):
                    s_ = kb % 4
                    zv = ps[:, :].rearrange("p (k r c) -> p k r c", k=2, r=2)
                    S.op("act", lambda: nc.scalar.copy(Hst[s_][:, :, 0:2, :], zv), [pk], [f"Hst{s_}"])
                    S.op("dve", lambda: nc.vector.tensor_copy(Hst[s_][:, :, 2, :], zv[:, :, 1, :]), [pk], [f"Hst{s_}"])
                    S.op("dve", lambda: nc.vector.tensor_copy(Hst[s_][:, :, 3, :], zv[:, :, 0, :]), [pk], [f"Hst{s_}"])
                    dma(H_d[go, :, kb * 2:kb * 2 + 2], Hst[s_][:], [f"Hst{s_}"], [("H_d", go)])

                fft_forward(ztmf, 128, ATf, cb_filter)
            S.barrier()
        def rms_apply(src, gvec, dst, scale, nk, tag, sqb, rs, np_=128, ss_ps=0, sqk="sqb", rsk="rs"):
            rms_a(src, nk, tag, sqb, np_, sqk)
            rms_b(src, gvec, dst, scale, nk, tag, sqb, rs, np_, ss_ps, sqk, rsk)

        def rms_a(src, nk, tag, sqb, np_=128, sqk="sqb"):
            S.op("act", lambda: nc.scalar.activation(sqb[0:np_, 0:nk, :], src, AF.Square), [tag + "_src"], [sqk])

        def rms_b(src, gvec, dst, scale, nk, tag, sqb, rs, np_=128, ss_ps=0, sqk="sqb", rsk="rs"):
            for k in range(nk):
                mm(PS[ss_ps][0:np_, :], onesbf[0:np_, 0:np_], sqb[0:np_, k, :], k == 0, k == nk - 1, [sqk, "onesbf"], [f"ps{ss_ps}"])
            S.op("act", lambda: nc.scalar.activation(rs[0:np_, :], PS[ss_ps][0:np_, :], AF.Sqrt, bias=epsT[0:np_, :], scale=scale),
                 [f"ps{ss_ps}", "epsT"], [rsk])
            S.op("dve", lambda: nc.vector.reciprocal(rs[0:np_, :], rs[0:np_, :]), [rsk], [rsk])
            for k in range(nk):
                S.op("dve", lambda k=k: nc.vector.scalar_tensor_tensor(out=dst[:, k, :], in0=src[:, k, :], scalar=gvec[:, k:k + 1],
                                                                      in1=rs[0:np_, :], op0=ALU.mult, op1=ALU.mult),
                     [tag + "_src", rsk, "gvecs"], [tag + "_dst"])

        def phase12():
          with ExitStack() as p12:
            KT = sb("KT", [128, 2, L], BF16, p12)
            Vx = sb("Vx", [128, 64, 2, 64], BF16, p12)
            QT = sb("QT", [128, 4, NTOK], BF16, p12)
            onesb = sb("onesb", [128, 64], BF16, p12)
            S.op("pool", lambda: nc.gpsimd.memset(onesb[:], 1.0), [], ["onesb"])
            with ExitStack() as p1:
                Wb = sb("Wb", [128, 8, 2304], BF16, p1)
                with ExitStack() as p1c:
                    wst = [sb(f"wsi{i}", [128, 8, 384], F32, p1c) for i in range(2)]
                    wv = w_in.rearrange("(k p) n -> p k n", p=128)
                    for bk in range(6):
                        s_ = bk % 2
                        dma(wst[s_][:], wv[:, :, bk * 384:(bk + 1) * 384], [], [f"wst{s_}"])
                        if bk % 2 == 0:
                            S.op("dve", lambda s_=s_, bk=bk: nc.vector.tensor_copy(Wb[:, :, bk * 384:(bk + 1) * 384], wst[s_][:]), [f"wst{s_}"], ["Wb"])
                        else:
                            S.op("act", lambda s_=s_, bk=bk: nc.scalar.copy(Wb[:, :, bk * 384:(bk + 1) * 384], wst[s_][:]), [f"wst{s_}"], ["Wb"])
                    S.barrier()
                xin = [sb(f"xin{i}", [128, 8, 512], F32, p1) for i in range(2)]
                sqbL = [sb(f"sqb{i}", [128, 8, 512], BF16, p1) for i in range(2)]
                rsL = [sb(f"rs{i}", [128, 512], F32, p1) for i in range(2)]
                aTL = [sb(f"aT{i}", [128, 8, 512], BF16, p1) for i in range(2)]
                ust = sb("ust", [128, 12, 512], BF16, p1)
                hsq = sb("hsq", [64, 512], BF16, p1); hrs = sb("hrs", [64, 512], F32, p1); hkn = sb("hkn", [64, 512], F32, p1)
                ht1 = sb("ht1", [64, 512], F32, p1); ht2 = sb("ht2", [64, 512], F32, p1)
                csb = [sb(f"csb{i}", [64, 2, 512], F32, p1) for i in range(2)]
                qstg = [sb(f"qstg{i}", [64, 512], BF16, p1) for i in range(2)]

                def hp1(srcb, gvec):
                    src = PS[srcb][0:64, :]
                    sk_ = f"ps{srcb}"
                    S.op("act", lambda: nc.scalar.activation(hsq[:], src, AF.Square), [sk_], ["hsq"])
                    mm(PS[3][0:64, :], onesbf[0:64, 0:64], hsq[:], True, True, ["hsq", "onesbf"], ["ps3"])
                    S.op("act", lambda: nc.scalar.activation(hrs[:], PS[3][0:64, :], AF.Sqrt, bias=epsT[0:64, :], scale=1.0 / 64.0),
                         ["ps3", "epsT"], ["hrs"])
                    S.op("dve", lambda: nc.vector.reciprocal(hrs[:], hrs[:]), ["hrs"], ["hrs"])
                    S.op("dve", lambda: nc.vector.scalar_tensor_tensor(out=hkn[:], in0=src, scalar=gvec[:, 0:1], in1=hrs[:],
                                                                      op0=ALU.mult, op1=ALU.mult), [sk_, "hrs", "gvecs"], ["hkn"])

                def hp2(cs, cskey, dst, dstkey):
                    mm(PS[3][0:64, :], rot[:], hkn[:], True, True, ["hkn", "rot"], ["ps3"])
                    S.op("pool", lambda: nc.gpsimd.tensor_tensor(ht1[:], hkn[:], cs[:, 0, :], ALU.mult), ["hkn", cskey], ["ht1"])
                    S.op("dve", lambda: nc.vector.tensor_tensor(ht2[:], PS[3][0:64, :], cs[:, 1, :], ALU.mult), ["ps3", cskey], ["ht2"])
                    S.op("pool", lambda: nc.gpsimd.tensor_tensor(dst, ht1[:], ht2[:], ALU.add), ["ht1", "ht2"], [dstkey])

                xv = xT.rearrange("(k p) t -> p k t", p=128)
                xqv = xTq.rearrange("(k p) t -> p k t", p=128)
                Uv = U_d.rearrange("(ct p) t -> p ct t", p=128)
                work = [("seq", j) for j in range(16)] + [("own", j) for j in range(4)]

                def norm_a(w):
                    kind, j = work[w]
                    s_ = w % 2
                    tsl = slice(j * 512, (j + 1) * 512)
                    srcv, cc, ss_ = (xv, c_cos, c_sin) if kind == "seq" else (xqv, c_cosq, c_sinq)
                    dma(xin[s_][:], srcv[:, :, tsl], [], [f"xin{s_}", f"n1{s_}_src"])
                    dma(csb[s_][:, 0, :], cc[:, tsl], [], [f"csb{s_}"])
                    dma(csb[s_][:, 1, :], ss_[:, tsl], [], [f"csb{s_}"])
                    rms_a(xin[s_][:], 8, f"n1{s_}", sqbL[s_], 128, f"sqb{s_}")

                def norm_b(w):
                    s_ = w % 2
                    rms_b(xin[s_][:], g1s, aTL[s_], 1.0 / D, 8, f"n1{s_}", sqbL[s_], rsL[s_], 128, 0, f"sqb{s_}", f"rs{s_}")

                def emit_proj(w, mid_hook):
                    kind, j = work[w]
                    s_ = w % 2
                    aT = aTL[s_]
                    ak = f"n1{s_}_dst"
                    cs, csk = csb[s_], f"csb{s_}"
                    tsl = slice(j * 512, (j + 1) * 512)
                    if kind == "seq":
                        for kvh in range(2):
                            for k in range(8):
                                mm(PS[1 + kvh][0:64, :], Wb[:, k, 512 + kvh * 64:512 + (kvh + 1) * 64], aT[:, k, :], k == 0, k == 7,
                                   ["Wb", ak], [f"ps{1 + kvh}"])
                        for tt in range(4):
                            for k in range(8):
                                mm(PS[4][:, tt * 128:(tt + 1) * 128], aT[:, k, tt * 128:(tt + 1) * 128], Wb[:, k, 640:768], k == 0, k == 7,
                                   ["Wb", ak], ["ps4"])
                        S.op("act", lambda j=j: nc.scalar.copy(Vx[:, j * 4:(j + 1) * 4, :, :],
                                                             PS[4][:, :].rearrange("p (t h d) -> p t h d", t=4, h=2)), ["ps4"], ["Vx"])
                        for ct in range(12):
                            pu, puk = (PS[5], "ps5") if ct % 2 == 0 else (PSX[0], "psb0")
                            for k in range(8):
                                mm(pu[:, :], Wb[:, k, 768 + ct * 128:768 + (ct + 1) * 128], aT[:, k, :], k == 0, k == 7, ["Wb", ak], [puk])
                            if ct % 2 == 0:
                                S.op("act", lambda ct=ct, pu=pu: nc.scalar.copy(ust[:, ct, :], pu[:, :]), [puk], ["ust"])
                            else:
                                S.op("dve", lambda ct=ct, pu=pu: nc.vector.tensor_copy(ust[:, ct, :], pu[:, :]), [puk], ["ust"])
                            if ct == 1:
                                hp1(1, gks)
                            if ct == 3:
                                hp2(cs, csk, KT[0:64, 0, tsl], "KT")
                                hp1(2, gks)
                            if ct == 5:
                                hp2(cs, csk, KT[0:64, 1, tsl], "KT")
                            if ct == 7:
                                mid_hook()
                        dma(Uv[:, :, tsl], ust[:], ["ust"], ["U_d"])
                    else:
                        def qdst(h):
                            if h % 2 == 0:
                                return QT[0:64, h // 2, tsl], "QT", None
                            qs_ = (h // 2) % 2
                            return qstg[qs_][:], f"qstg{qs_}", qs_

                        def qproj(h):
                            hb = 1 + h % 2
                            for k in range(8):
                                mm(PS[hb][0:64, :], Wb[:, k, h * 64:(h + 1) * 64], aT[:, k, :], k == 0, k == 7, ["Wb", ak], [f"ps{hb}"])

                        qproj(0)
                        for h in range(8):
                            hp1(1 + h % 2, gqs)
                            if h + 1 < 8:
                                qproj(h + 1)
                            if h == 3:
                                mid_hook()
                            d_, dk_, qs_ = qdst(h)
                            hp2(cs, csk, d_, dk_)
                            if qs_ is not None:
                                dma(QT[64:128, h // 2, tsl], qstg[qs_][:], [f"qstg{qs_}"], ["QT2"])

                norm_a(0)
                norm_b(0)
                for w in range(len(work)):
                    if w + 1 < len(work):
                        norm_a(w + 1)
                        emit_proj(w, lambda w=w: norm_b(w + 1))
                    else:
                        emit_proj(w, lambda: None)
                dma(KT[64:128, :, :], KT[0:64, :, :], ["KT"], ["KT2"])
                S.barrier()
            if 2 in phases:
              with ExitStack() as p2:
                pb = [sb(f"pb{i}", [128, 1024], BF16, p2) for i in range(2)]
                rden = sb("rden", [128, 512], F32, p2)
                ast = [sb(f"ast{i}", [128, 512], BF16, p2) for i in range(2)]
                n_ = 0
                for qc in range(4):
                    qsl = slice(qc * 512, (qc + 1) * 512)
                    for hp in range(4):
                        kvh = hp // 2
                        for step in range(64 + 1):
                            if step < 64:
                                kt = step
                                sl = kt % 2
                                ksl = slice(kt * 128, (kt + 1) * 128)
                                mm(PS[2 * sl][:, :], KT[0:64, kvh, ksl], QT[0:64, hp, qsl], True, True, ["KT", "QT"], [f"ps{2 * sl}"])
                                mm(PS[2 * sl + 1][:, :], KT[64:128, kvh, ksl], QT[64:128, hp, qsl], True, True, ["KT2", "QT2"], [f"ps{2 * sl + 1}"])
                                S.op("act", lambda sl=sl: nc.scalar.activation(pb[sl][:], PSW[sl][:, :], AF.Exp, scale=0.125),
                                     [f"ps{2 * sl}", f"ps{2 * sl + 1}"], [f"pb{sl}"])
                            kt = step - 1
                            if kt >= 0:
                                sl = kt % 2
                                st_, sp_ = kt == 0, kt == 63
                                pA = pb[sl][:, 0:512]
                                pB = pb[sl][:, 512:1024]
                                vv = Vx[:, kt, kvh, :]
                                S.op("pe", lambda vv=vv, pA=pA, st_=st_, sp_=sp_: nc.tensor.matmul(PS[4][0:64, :], vv, pA, start=st_, stop=sp_, tile_position=(0, 0)),
                                     ["Vx", f"pb{sl}"], ["ps4"])
                                S.op("pe", lambda vv=vv, pB=pB, st_=st_, sp_=sp_: nc.tensor.matmul(PS[4][64:128, :], vv, pB, start=st_, stop=sp_, tile_position=(0, 64)),
                                     ["Vx", f"pb{sl}"], ["ps4"])
                                S.op("pe", lambda pA=pA, st_=st_, sp_=sp_: nc.tensor.matmul(PS[5][0:64, :], onesb[:], pA, start=st_, stop=sp_, tile_position=(0, 0)),
                                     ["onesb", f"pb{sl}"], ["ps5"])
                                S.op("pe", lambda pB=pB, st_=st_, sp_=sp_: nc.tensor.matmul(PS[5][64:128, :], onesb[:], pB, start=st_, stop=sp_, tile_position=(0, 64)),
                                     ["onesb", f"pb{sl}"], ["ps5"])
                        S.op("dve", lambda: nc.vector.reciprocal(rden[:], PS[5][:, :]), ["ps5"], ["rden"])
                        a_ = n_ % 2
                        S.op("dve", lambda a_=a_: nc.vector.tensor_tensor(ast[a_][:], PS[4][:, :], rden[:], ALU.mult), ["ps4", "rden"], [f"ast{a_}"])
                        dma(att_d[hp * 128:(hp + 1) * 128, qsl], ast[a_][:], [f"ast{a_}"], ["att_d"])
                        n_ += 1
                S.barrier()

        def phase3():
          with ExitStack() as p3:
            load_fft_consts(p3)
            finv = FC["finv"]
            M2s = sb("M2s", [128, 128, 64], BF16, p3)
            M2qs = sb("M2qs", [128, 128, 16], BF16, p3)
            dma(M2s[:], c_M2, [], ["M2s"])
            dma(M2qs[:], c_M2q, [], ["M2qs"])
            RA = sb("RA", [128, 192 * 128 + 8], BF16, p3)
            RB = sb("RB", [128, 128 * 128], BF16, p3)
            RC = sb("RC", [128, 128 * 128], BF16, p3)
            vt = sb("vt", [128, L], BF16, p3)
            x1t = sb("x1t", [128, L], BF16, p3)
            x2q = sb("x2q", [128, NTOK], BF16, p3)
            zzq = sb("zzq", [128, NTOK], BF16, p3)
            hyst = sb("hyst", [128, NTOK], BF16, p3)
            Hc = [sb(f"Hc{i}", [128, 2, 4, 128], BF16, p3) for i in range(4)]
            Pb = sb("Pb", [128, 2, 2, 128], F32, p3)
            Qb = sb("Qb", [128, 2, 2, 128], F32, p3)
            Yb = [sb(f"Yb{i}", [128, 2, 2, 128], BF16, p3) for i in range(2)]
            gt = [sb(f"gt{i}", [128, 512], F32, p3) for i in range(2)]
            sct = gt
            uraw = RA[:, 0:3 * 8194].rearrange("p (i t) -> p i t", i=3)
            AT = RA[:, 0:192 * 128].rearrange("p (j c) -> p j c", c=128)
            ztm = RB[:, :].rearrange("p (c n) -> p c n", n=128)
            Eb = RB[:, :].rearrange("p (n c) -> p n c", c=128)
            x2t = RC[:, 0:L]
            ET = RC[:, :].rearrange("p (c k) -> p c k", k=128)
            RAK, RBK, RCK = "RA", "RB", "RC"

            def masked_quarter(dst, src, dkey, skey):
                S.op("dve", lambda: nc.vector.tensor_scalar(dst[:], src[:, 0:NTOK], qms[:, 0:1], None, ALU.mult), [skey, "qms"], [dkey])
                for q in range(1, 4):
                    S.op("dve", lambda q=q: nc.vector.scalar_tensor_tensor(out=dst[:], in0=src[:, q * NTOK:(q + 1) * NTOK], scalar=qms[:, q:q + 1],
                                                                          in1=dst[:], op0=ALU.mult, op1=ALU.add), [skey, "qms", dkey], [dkey])

            RBQ = [(RBK, q4) for q4 in range(4)]

            def bounce_to_ztm(src, skey):
                zv = zcm_d.rearrange("c (n1 n2) -> n1 c n2", n2=128)
                for q4 in range(4):
                    dma(zcm_d[q4 * 32:(q4 + 1) * 32, :], src[q4 * 32:(q4 + 1) * 32, :], [skey], [("zcm_d", q4)])
                for q4 in range(4):
                    dma(ztm[0:64, q4 * 32:(q4 + 1) * 32, :], zv[:, q4 * 32:(q4 + 1) * 32, :], [("zcm_d", q4)], [(RBK, q4)])

            def conv_core(go):
                def load_h(kb):
                    dma(Hc[kb % 4][:], H_d[go, :, kb * 2:kb * 2 + 2], [("H_d", go)], [f"Hc{kb % 4}"])

                def pre(kb):
                    if kb == 0:
                        load_h(0)
                        load_h(1)
                    if kb + 2 < 32:
                        load_h(kb + 2)

                def cb(kb, ps, pk):
                    s_ = kb % 4
                    y_ = kb % 2
                    zv = ps[:, :].rearrange("p (k r c) -> p k r c", k=2, r=2)
                    S.op("dve", lambda: nc.vector.tensor_tensor(Pb[:], zv, Hc[s_][:, :, 0:2, :], ALU.mult), [pk, f"Hc{s_}"], ["Pb"])
                    S.op("dve", lambda: nc.vector.tensor_tensor(Qb[:], zv, Hc[s_][:, :, 2:4, :], ALU.mult), [pk, f"Hc{s_}"], ["Qb"])
                    S.op("pool", lambda: nc.gpsimd.tensor_tensor(Yb[y_][:, :, 0, :], Pb[:, :, 0, :], Pb[:, :, 1, :], ALU.subtract), ["Pb"], [f"Yb{y_}"])
                    S.op("pool", lambda: nc.gpsimd.tensor_tensor(Yb[y_][:, :, 1, :], Qb[:, :, 0, :], Qb[:, :, 1, :], ALU.add), ["Qb"], [f"Yb{y_}"])
                    pe_ = PS[4 + kb % 2]
                    pek = f"ps{4 + kb % 2}"
                    er = pe_[:, 0:256].rearrange("p (k c) -> p k c", k=2)
                    ei = pe_[:, 256:512].rearrange("p (k c) -> p k c", k=2)
                    yr = Yb[y_][:, :, 0, :]
                    yi = Yb[y_][:, :, 1, :]
                    mm(er, finv[:, 0, :], yr, True, False, [f"Yb{y_}", "finv"], [pek])
                    mm(er, finv[:, 2, :], yi, False, True, [f"Yb{y_}", "finv"], [pek])
                    mm(ei, finv[:, 1, :], yr, True, False, [f"Yb{y_}", "finv"], [pek])
                    mm(ei, finv[:, 0, :], yi, False, True, [f"Yb{y_}", "finv"], [pek])
                    S.op("act", lambda: nc.scalar.copy(ET[:, :, kb * 2:kb * 2 + 2], pe_[:, 0:256].rearrange("p (k c) -> p c k", k=2)), [pek], [RCK])
                    S.op("dve", lambda: nc.vector.tensor_copy(ET[:, :, 64 + kb * 2:64 + kb * 2 + 2], pe_[:, 256:512].rearrange("p (k c) -> p c k", k=2)),
                         [pek], [RCK])
                fft_forward(ztm, 64, AT, cb, zkey=RBK, akey=RAK, pre_batch=pre)
                for c4 in range(32):
                    pb_ = PSB[c4 % 2]
                    pbk = f"psb{c4 % 2}"
                    for cc in range(4):
                        c = c4 * 4 + cc
                        S.op("pe", lambda c=c, cc=cc, pb_=pb_: nc.tensor.transpose(pb_[:, cc * 128:(cc + 1) * 128], ET[:, c, :], ident[:]),
                             [RCK, "ident"], [pbk])
                    src = pb_[:, 0:512].rearrange("p (cc n) -> p n cc", cc=4)
                    dst = Eb[:, :, c4 * 4:c4 * 4 + 4]
                    if c4 % 2 == 0:
                        S.op("act", lambda src=src, dst=dst: nc.scalar.copy(dst, src), [pbk], RBQ)
                    else:
                        S.op("dve", lambda src=src, dst=dst: nc.vector.tensor_copy(dst, src), [pbk], RBQ)

            Udv = U_d
            for g in range(4):
                for i in range(3):
                    dma(uraw[:, i, 1:L + 1], Udv[i * 512 + g * 128:i * 512 + (g + 1) * 128, :], ["U_d"], [RAK])
                S.op("pool", lambda: nc.gpsimd.memset(uraw[:, :, 0:1], 0.0), [RAK], [RAK])
                S.op("pool", lambda: nc.gpsimd.memset(uraw[:, :, L + 1:L + 2], 0.0), [RAK], [RAK])
                for i, (dst, dkey) in enumerate(((vt, "vt"), (x1t, "x1t"), (x2t, RCK))):
                    ct = i * 4 + g
                    for q in range(16):
                        e = "dve"
                        st = sct[q % 2]
                        sk = f"gt{q % 2}"
                        o0 = q * 512
                        S.op(e, lambda e=e, st=st, i=i, o0=o0, ct=ct: V(e).tensor_scalar(st[:], uraw[:, i, o0:o0 + 512], cws[:, ct, 0:1], cbs[:, ct:ct + 1],
                                                                                      ALU.mult, ALU.add), [RAK, "cws", "cbs"], [sk])
                        S.op(e, lambda e=e, st=st, i=i, o0=o0, ct=ct: V(e).scalar_tensor_tensor(out=st[:], in0=uraw[:, i, o0 + 1:o0 + 513], scalar=cws[:, ct, 1:2],
                                                                                             in1=st[:], op0=ALU.mult, op1=ALU.add), [RAK, "cws", sk], [sk])
                        S.op(e, lambda e=e, st=st, i=i, o0=o0, ct=ct, dst=dst: V(e).scalar_tensor_tensor(out=dst[:, o0:o0 + 512], in0=uraw[:, i, o0 + 2:o0 + 514],
                                                                                                      scalar=cws[:, ct, 2:3], in1=st[:], op0=ALU.mult, op1=ALU.add),
                             [RAK, "cws", sk], [dkey])
                masked_quarter(x2q, x2t, "x2q", RCK)
                snap("vt0", vt[:], "vt", L)
                snap("x1t0", x1t[:], "x1t", L)
                bounce_to_ztm(vt, "vt")
                conv_core(g * 2 + 0)
                vt3 = vt[:, :].rearrange("p (n1 n2) -> p n1 n2", n2=128)
                x13 = x1t[:, :].rearrange("p (n1 n2) -> p n1 n2", n2=128)
                for nb in range(16):
                    ps = PS[nb % 2]
                    pk = f"ps{nb % 2}"
                    for j in range(8):
                        n2 = nb * 8 + j
                        mm(ps[:, j * 64:(j + 1) * 64], Eb[:, n2, :], M2s[:, n2, :], True, True, RBQ + ["M2s"], [pk])
                    psv = ps[:, :].rearrange("p (j n) -> p n j", j=8)
                    gtv = gt[nb % 2][:, :].rearrange("p (n j) -> p n j", j=8)
                    gk_ = f"gt{nb % 2}"
                    S.op("dve", lambda psv=psv, gtv=gtv, nb=nb, g=g: nc.vector.scalar_tensor_tensor(out=gtv, in0=vt3[:, :, nb * 8:(nb + 1) * 8],
                                                                                             scalar=skps[:, g:g + 1], in1=psv, op0=ALU.mult, op1=ALU.add),
                         [pk, "vt", "skps"], [gk_])
                    S.op("pool", lambda gtv=gtv, nb=nb: nc.gpsimd.tensor_tensor(vt3[:, :, nb * 8:(nb + 1) * 8], gtv, x13[:, :, nb * 8:(nb + 1) * 8], ALU.mult),
                         [gk_, "x1t"], ["vt"])
                snap("zz0", vt[:], "vt", L)
                masked_quarter(zzq, vt, "zzq", "vt")
                bounce_to_ztm(vt, "vt")
                conv_core(g * 2 + 1)
                zq3 = zzq[:, :].rearrange("p (n1 n2) -> p n1 n2", n2=128)
                xq3 = x2q[:, :].rearrange("p (n1 n2) -> p n1 n2", n2=128)
                hy3 = hyst[:, :].rearrange("p (n1 n2) -> p n1 n2", n2=128)
                for nb in range(4):
                    ps = PS[nb % 2]
                    pk = f"ps{nb % 2}"
                    for j in range(32):
                        n2 = nb * 32 + j
                        mm(ps[:, j * 16:(j + 1) * 16], Eb[:, n2, :], M2qs[:, n2, :], True, True, RBQ + ["M2qs"], [pk])
                    psv = ps[:, :].rearrange("p (j n) -> p n j", j=32)
                    gtv = gt[nb % 2][:, :].rearrange("p (n j) -> p n j", j=32)
                    gk_ = f"gt{nb % 2}"
                    S.op("dve", lambda psv=psv, gtv=gtv, nb=nb, g=g: nc.vector.scalar_tensor_tensor(out=gtv, in0=zq3[:, :, nb * 32:(nb + 1) * 32],
                                                                                             scalar=skps[:, 4 + g:5 + g], in1=psv, op0=ALU.mult, op1=ALU.add),
                         [pk, "zzq", "skps"], [gk_])
                    S.op("pool", lambda gtv=gtv, nb=nb: nc.gpsimd.tensor_tensor(hy3[:, :, nb * 32:(nb + 1) * 32], gtv, xq3[:, :, nb * 32:(nb + 1) * 32], ALU.mult),
                         [gk_, "x2q"], ["hyst"])
                dma(hy_d[g * 128:(g + 1) * 128, :], hyst[:], ["hyst"], ["hy_d"])
            S.barrier()

        def phase4():
          with ExitStack() as p4:
            WoA = sb("WoA", [64, 8, D], BF16, p4)
            WoH = sb("WoH", [128, 4, D], BF16, p4)
            with ExitStack() as p4c:
                wsa = sb("wsa", [64, 8, D], F32, p4c)
                wsh = sb("wsh", [128, 4, D], F32, p4c)
                dma(wsa[:], w_out[0:512, :].rearrange("(h d) n -> d h n", d=64), [], ["wsa"])
                dma(wsh[:], w_out[512:1024, :].rearrange("(g p) n -> p g n", p=128), [], ["wsh"])
                S.op("dve", lambda: nc.vector.tensor_copy(WoA[:], wsa[:]), ["wsa"], ["WoA"])
                S.op("pool", lambda: nc.gpsimd.tensor_copy(WoH[:], wsh[:]), ["wsh"], ["WoH"])
                S.barrier()
            attc = sb("attc", [64, 8, 512], BF16, p4)
            hyc = sb("hyc", [128, 4, 512], BF16, p4)
            sqb = sb("sqb4", [128, 8, 512], BF16, p4)
            rs = sb("rs4", [128, 512], F32, p4)
            mixA = sb("mixA", [64, 8, 512], BF16, p4)
            mixH = sb("mixH", [128, 4, 512], BF16, p4)
            h1 = sb("h1", [128, 8, 512], F32, p4)
            mT = sb("mT", [128, 8, 512], BF16, p4)
            Wmi_s = [sb(f"Wmi_s{i}", [128, 8, 1024], BF16, p4) for i in range(2)]
            Wmo_s = [sb(f"Wmo_s{i}", [128, 32, 128], BF16, p4) for i in range(2)]
            fT = sb("fT", [128, 32, 512], BF16, p4)
            rb = [sb(f"rb{i}", [128, 512], F32, p4) for i in range(2)]
            xqv = xTq.rearrange("(k p) t -> p k t", p=128)
            Wmiv = Wmi_d.rearrange("(k p) n -> p k n", p=128)
            outv = outT.rearrange("(k p) t -> p k t", p=128)
            nw = 0
            nwo = 0
            for j in range(4):
                tsl = slice(j * 512, (j + 1) * 512)
                dma(attc[:], att_d[:, tsl].rearrange("(h d) t -> d h t", d=64), ["att_d"], ["attc", "na_src"])
                dma(hyc[:], hy_d[:, tsl].rearrange("(g p) t -> p g t", p=128), ["hy_d"], ["hyc", "nh_src"])
                dma(h1[:], xqv[:, :, tsl], [], ["h1", "n2_src", "nf_src"])
                rms_apply(attc[:], gas, mixA, 1.0 / 512.0, 8, "na", sqb, rs, np_=64)
                rms_apply(hyc[:], ghs, mixH, 1.0 / 512.0, 4, "nh", sqb, rs)
                for ct in range(8):
                    csl = slice(ct * 128, (ct + 1) * 128)
                    for h in range(8):
                        mm(PS[1][:, :], WoA[:, h, csl], mixA[:, h, :], h == 0, False, ["WoA", "na_dst"], ["ps1"])
                    for g in range(4):
                        mm(PS[1][:, :], WoH[:, g, csl], mixH[:, g, :], False, g == 3, ["WoH", "nh_dst"], ["ps1"])
                    S.op("dve", lambda ct=ct: nc.vector.tensor_tensor(h1[:, ct, :], h1[:, ct, :], PS[1][:, :], ALU.add), ["ps1", "h1"], ["h1", "n2_src", "nf_src"])
                rms_apply(h1[:], g2s, mT, 1.0 / D, 8, "n2", sqb, rs)
                for pc in range(4):
                    s_ = nw % 2
                    dma(Wmi_s[s_][:], Wmiv[:, :, pc * 1024:(pc + 1) * 1024], ["Wmid"], [f"Wmi_s{s_}"])
                    nw += 1
                    for ft in range(8):
                        f = pc * 8 + ft
                        pf = PS[2 + f % 2]
                        pfk = f"ps{2 + f % 2}"
                        for k in range(8):
                            mm(pf[:, :], Wmi_s[s_][:, k, ft * 128:(ft + 1) * 128], mT[:, k, :], k == 0, k == 7, [f"Wmi_s{s_}", "n2_dst"], [pfk])
                        r_ = f % 2
                        S.op("act", lambda pf=pf, r_=r_: nc.scalar.activation(rb[r_][:], pf[:, :], AF.Relu), [pfk], [f"rb{r_}"])
                        e = ("pool", "dve")[f % 2]
                        S.op(e, lambda e=e, r_=r_, f=f: V(e).tensor_tensor(fT[:, f, :], rb[r_][:], rb[r_][:], ALU.mult), [f"rb{r_}"], ["fT"])
                for ct in range(8):
                    s_ = nwo % 2
                    dma(Wmo_s[s_][:], Wmo_d[ct], ["Wmod"], [f"Wmo_s{s_}"])
                    nwo += 1
                    po = PS[4 + ct % 2]
                    pok = f"ps{4 + ct % 2}"
                    for f in range(32):
                        mm(po[:, :], Wmo_s[s_][:, f, :], fT[:, f, :], f == 0, f == 31, [f"Wmo_s{s_}", "fT"], [pok])
                    S.op("dve", lambda ct=ct, po=po: nc.vector.tensor_tensor(h1[:, ct, :], h1[:, ct, :], po[:, :], ALU.add), [pok, "h1"], ["h1", "n2_src", "nf_src"])
                rms_apply(h1[:], gfs, h1, 1.0 / D, 8, "nf", sqb, rs)
                dma(outv[:, :, tsl], h1[:], ["h1", "nf_dst"], ["outT"])
            S.barrier()

        if 0 in phases:
            phase0()
        if 1 in phases:
            phase12()
        if 3 in phases:
            phase3()
        if 4 in phases:
            phase4()
        if debug:
            with ExitStack() as pd:
                for name in debug:
                    if name not in ("att_d", "hy_d", "U_d", "H_d0"):
                        continue
                    src_, shp = {"att_d": (att_d, [512, NTOK]), "hy_d": (hy_d, [512, NTOK]), "U_d": (U_d[0:512, :], [512, L]),
                                 "H_d0": (H_d[0:4, :, 0:4].rearrange("a p k r c -> (a p) (k r c)"), [512, 2048])}[name]
                    dbo = nc.dram_tensor("dbg_" + name, shp, BF16, kind="ExternalOutput").ap()
                    dbs = sb("dbs_" + name, [128, 4, shp[1]], BF16, pd)
                    dma(dbs[:], src_.rearrange("(a p) t -> p a t", p=128), [name], ["dbs_" + name])
                    dma(dbo.rearrange("(a p) t -> p a t", p=128), dbs[:], ["dbs_" + name], ["dbg_" + name])
                S.barrier()
        S.barrier()
    return nc, S


def _core_inputs(inp, core):
    T = _tables()
    b, r = core // 4, core % 4
    f = lambda a: np.ascontiguousarray(np.asarray(a, dtype=np.float32))
    x = np.asarray(inp["x"], dtype=np.float32)
    xT = np.ascontiguousarray(x[b].T)
    tq = slice(r * NTOK, (r + 1) * NTOK)
    qm = np.zeros((128, 4), np.float32)
    qm[:, r] = 1.0
    m = {
        "xT": xT, "xTq": np.ascontiguousarray(xT[:, tq]),
        "w_in": f(inp["w_in"][0]), "w_out": f(inp["w_out"][0]), "w_mi": f(inp["w_mlp_in"][0]), "w_mo": f(inp["w_mlp_out"][0]),
        "g1": f(np.asarray(inp["norm1_g"][0]).reshape(8, 128).T), "g2": f(np.asarray(inp["norm2_g"][0]).reshape(8, 128).T),
        "gf": f(np.asarray(inp["final_g"]).reshape(8, 128).T),
        "gq": f(np.asarray(inp["q_norm_g"][0]).reshape(64, 1)), "gk": f(np.asarray(inp["k_norm_g"][0]).reshape(64, 1)),
        "ga": f(np.asarray(inp["attn_out_g"][0]).reshape(8, 64).T), "gh": f(np.asarray(inp["hy_out_g"][0]).reshape(4, 128).T),
        "cw": f(np.asarray(inp["hy_conv_w"][0]).T.reshape(12, 128, 3).transpose(1, 0, 2)),
        "cb": f(np.asarray(inp["hy_conv_b"][0]).reshape(12, 128).T),
        "fw1": f(inp["filt_w1"][0]), "fw2": f(inp["filt_w2"][0]), "fw3": f(inp["filt_w3"][0]), "fw4": f(inp["filt_w4"][0]),
        "fb": f(np.stack([np.asarray(inp["filt_b1"][0]), np.asarray(inp["filt_b2"][0]), np.asarray(inp["filt_b3"][0])], axis=1)),
        "ffreq": f(np.asarray(inp["filt_freq"][0]).reshape(64, 1)),
        "fdel": f(np.asarray(inp["filt_deltas"][0]).reshape(16, 128).T),
        "skp": f(np.asarray(inp["hy_skip_d"][0]).reshape(2, 4, 128).transpose(2, 0, 1).reshape(128, 8)),
        "qmask": qm,
        "c_cos": T["cosT"], "c_sin": T["sinT"],
        "c_cosq": np.ascontiguousarray(T["cosT"][:, tq]), "c_sinq": np.ascontiguousarray(T["sinT"][:, tq]),
        "c_zT": np.ascontiguousarray(np.stack([T["zT"], T["zrT"]])), "c_t": np.ascontiguousarray(np.stack([T["t"], T["trev"]])),
        "c_tb": T["tb"], "c_jt": T["jt"],
        "c_fpack": T["fpack"], "c_G": T["G"], "c_finv": T["finv"], "c_M2": T["M2"],
        "c_M2q": np.ascontiguousarray(T["M2"][:, :, r * 16:(r + 1) * 16]),
        "c_rot": T["rot"], "c_sel": T["sel"], "c_ones": T["ones"], "c_ident": T["ident"],
    }
    return m


def _emit(nc, S):
    from contextlib import ExitStack
    st = ExitStack()
    sems = {e: [st.enter_context(nc.semaphore(f"s_{e}{i}")) for i in range(4)] for e in S.CE}
    dsems = [st.enter_context(nc.semaphore(f"d{i}")) for i in range(S.NDS)]
    info = S.emit(sems, dsems)
    return st, info


def kernel(**inputs):
    nc, S = build_program()
    st, _ = _emit(nc, S)
    with st:
        in_maps = [_core_inputs(inputs, c) for c in range(8)]
        res = run_bass_kernel_spmd(nc, in_maps, core_ids=list(range(8)))
    out = np.empty((NB, L, D), np.float32)
    for c in range(8):
        b, r = c // 4, c % 4
        out[b, r * NTOK:(r + 1) * NTOK, :] = np.asarray(res.results[c]["outT"]).T
    return out
```

```python
import math
import numpy as np
import ml_dtypes
import concourse.bass as bass
import concourse.mybir as mybir
from concourse.bass_utils import run_bass_kernel_spmd

F32 = mybir.dt.float32
BF16 = mybir.dt.bfloat16
ALU = mybir.AluOpType
AF = mybir.ActivationFunctionType
bf16_np = ml_dtypes.bfloat16

D = 1024
L = 8192
NB = 2
NTOK = 2048
HD = 64
NQH = 8
NKV = 2
HYW = 512
DFF = 4096
EPS = 1e-6
NFFT = 2 * L
TWO_PI = 2.0 * math.pi
MAGIC = 12582912.0


class Op:
    __slots__ = ("eng", "fn", "reads", "writes", "dma", "idx", "gid", "waits", "signal", "clock",
                 "sem", "val")


class Sched:
    CE = ("pe", "act", "dve", "pool", "sp")
    EPOCH = 12000
    NDS = 40

    def __init__(self, nc):
        self.nc = nc
        self.E = {"pe": nc.tensor, "act": nc.scalar, "dve": nc.vector, "pool": nc.gpsimd,
                  "sp": nc.sync}
        self.ops = []
        self.last_w = {}
        self.readers = {}
        self.known = {e: {} for e in self.CE}
        self.known_dma = {e: set() for e in self.CE}
        self.cnt = {e: 0 for e in self.CE}
        self.last_op = {e: None for e in self.CE}
        self.live_dma = []

    def op(self, eng, fn, reads=(), writes=(), dma=False):
        o = Op()
        o.eng, o.fn, o.reads, o.writes, o.dma = eng, fn, tuple(reads), tuple(writes), dma
        o.signal = dma
        o.sem = None
        o.val = None
        o.gid = len(self.ops)
        deps = {}
        for k in o.reads:
            w = self.last_w.get(k)
            if w is not None:
                deps[w.gid] = w
        for k in o.writes:
            w = self.last_w.get(k)
            if w is not None:
                deps[w.gid] = w
            for r in self.readers.get(k, ()):
                deps[r.gid] = r
        self._finish(o, deps.values())
        for k in o.writes:
            self.last_w[k] = o
            self.readers[k] = []
        for k in o.reads:
            lst = self.readers.setdefault(k, [])
            if not dma:
                lst[:] = [r for r in lst if r.dma or r.eng != eng]
            lst.append(o)
        return o

    def _finish(self, o, deps):
        eng = o.eng
        kn = self.known[eng]
        kd = self.known_dma[eng]
        waits = []
        best = {}
        rset = set(o.reads)
        for d in deps:
            if d.dma:
                if d.gid not in kd:
                    waits.append(d)
                    kd.add(d.gid)
                continue
            if d.eng == eng:
                if eng in ("pe", "sp"):
                    continue
            if kn.get(d.eng, -1) >= d.idx:
                continue
            b = best.get(d.eng)
            if b is None or b.idx < d.idx:
                best[d.eng] = d
        for d in best.values():
            waits.append(d)
            if kn.get(d.eng, -1) < d.idx:
                kn[d.eng] = d.idx
            for e2, i2 in d.clock.items():
                if kn.get(e2, -1) < i2:
                    kn[e2] = i2
        for d in waits:
            d.signal = True
        o.waits = waits
        o.idx = self.cnt[eng]
        self.cnt[eng] += 1
        o.clock = dict(kn)
        self.ops.append(o)
        if o.fn is not None:
            self.last_op[eng] = o
        if o.dma:
            self.live_dma.append(o)

    def barrier(self):
        lasts = [self.last_op[e] for e in self.CE if self.last_op[e] is not None]
        dmas = list(self.live_dma)
        for e in self.CE:
            o = Op()
            o.eng, o.fn, o.reads, o.writes, o.dma = e, None, (), (), False
            o.signal = False
            o.sem = None
            o.val = None
            o.gid = len(self.ops)
            deps = [d for d in lasts if not d.dma and d.eng != e] + dmas
            self._finish(o, deps)
        self.live_dma = []

    def emit(self, sems, dsems):
        sig = {e: 0 for e in self.CE}
        nd = 0
        for o in self.ops:
            E = self.E[o.eng]
            for d in o.waits:
                E.wait_ge(d.sem, d.val)
            if o.fn is None:
                continue
            if o.dma:
                s = dsems[nd % self.NDS]
                rnd = nd // self.NDS
                if rnd > 0:
                    E.wait_ge(s, 16 * rnd)
                ins = o.fn()
                ins.then_inc(s, 16)
                o.sem, o.val = s, 16 * (rnd + 1)
                nd += 1
            else:
                ins = o.fn()
                if o.signal:
                    c = sig[o.eng]
                    o.sem = sems[o.eng][c // self.EPOCH]
                    o.val = c % self.EPOCH + 1
                    ins.then_inc(o.sem, 1)
                    sig[o.eng] = c + 1
        return sig, nd


_TABLES = None


def _rope_tables():
    rows = L // 64
    row = np.repeat(np.arange(rows, dtype=np.float32), 64)
    col = np.tile(np.arange(64, dtype=np.float32), rows)
    half = HD // 2
    inv_freq = (np.float32(10000.0) ** (-np.arange(0, half, 2, dtype=np.float32) / np.float32(half))).astype(np.float32)
    ang_r = (row[:, None] * inv_freq[None, :]).astype(np.float32)
    ang_c = (col[:, None] * inv_freq[None, :]).astype(np.float32)
    cos_r, sin_r = np.cos(ang_r), np.sin(ang_r)
    cos_c, sin_c = np.cos(ang_c), np.sin(ang_c)
    cosT = np.concatenate([cos_r.T, cos_r.T, cos_c.T, cos_c.T], axis=0).astype(np.float32)
    sinT = np.concatenate([sin_r.T, sin_r.T, sin_c.T, sin_c.T], axis=0).astype(np.float32)
    return np.ascontiguousarray(cosT), np.ascontiguousarray(sinT)


def _filter_pos_tables():
    f32 = np.float32
    t = np.linspace(0.0, 1.0, L, dtype=f32)
    bands = 16
    w_ang = (f32(2.0 * math.pi) * np.arange(L, dtype=f32) / f32(L)).astype(f32)
    band_f = np.linspace(1e-4, bands - 1, bands, dtype=f32)
    ang = (w_ang[:, None] * band_f[None, :]).astype(f32)
    z = np.concatenate([t[:, None], np.cos(ang), -np.sin(ang)], axis=-1).astype(f32)
    zT = np.ascontiguousarray(z.T)
    idx = (L - np.arange(L)) % L
    zrT = np.ascontiguousarray(z[idx].T)
    trev = t[idx].copy()
    return zT, zrT, t.copy(), trev


def _fft_tables():
    N = NFFT
    n1 = np.arange(128, dtype=np.float64)
    k1 = np.arange(64, dtype=np.float64)
    th = 2 * np.pi * np.outer(n1, k1 + 0.5) / 128.0
    fpack = np.concatenate([np.cos(th), -np.sin(th), np.sin(th)], axis=1)
    n2 = np.arange(128, dtype=np.float64)
    k2 = np.arange(128, dtype=np.float64)
    kap = k1[None, :, None] + 128.0 * k2[None, None, :] + 0.5
    thg = 2 * np.pi * n2[:, None, None] * kap / N
    G = np.stack([np.cos(thg), -np.sin(thg)], axis=2)
    thf = 2 * np.pi * np.outer(k2, n2) / 128.0
    finv = np.stack([np.cos(thf), np.sin(thf), -np.sin(thf), -np.cos(thf)], axis=1)
    n1h = np.arange(64, dtype=np.float64)
    phi = 2 * np.pi * (k1[:, None, None] + 0.5) * (128.0 * n1h[None, None, :] + n2[None, :, None]) / N
    M2 = np.concatenate([(2.0 / N) * np.cos(phi), -(2.0 / N) * np.sin(phi)], axis=0)
    return (fpack.astype(bf16_np), G.astype(bf16_np), finv.astype(bf16_np), M2.astype(bf16_np))


def _tables():
    global _TABLES
    if _TABLES is None:
        cosT, sinT = _rope_tables()
        zT, zrT, t, trev = _filter_pos_tables()
        fpack, G, finv, M2 = _fft_tables()
        ii = np.arange(512, dtype=np.float64)
        jj = np.arange(16, dtype=np.float64)
        tbm = np.stack([ii / (L - 1), (L - ii) / (L - 1)]).astype(np.float32)
        jtm = np.stack([512.0 * jj / (L - 1), -512.0 * jj / (L - 1)]).astype(np.float32)
        rot = np.zeros((64, 64), np.float32)
        for m in range(64):
            if (m % 32) < 16:
                rot[m + 16, m] = -1.0
            else:
                rot[m - 16, m] = 1.0
        sel = np.zeros((65, 64), np.float32)
        sel[64, :] = 1.0
        _TABLES = dict(cosT=cosT, sinT=sinT, zT=zT, zrT=zrT, t=t, trev=trev, fpack=fpack, G=G,
                       finv=finv, M2=M2, rot=rot, sel=sel, tb=tbm, jt=jtm,
                       ones=np.ones((128, 128), np.float32),
                       ident=np.eye(128, dtype=np.float32).astype(bf16_np))
    return _TABLES


def build_program(phases=(0, 1, 2, 3, 4), debug=()):
    nc = bass.Bass("TRN2", target_bir_lowering=False)
    S = Sched(nc)

    def din(name, shape, dt=F32):
        return nc.dram_tensor(name, list(shape), dt, kind="ExternalInput").ap()

    def dscr(name, shape, dt):
        return nc.dram_tensor(name, list(shape), dt, kind="Internal").ap()

    xT = din("xT", [D, L])
    xTq = din("xTq", [D, NTOK])
    w_in = din("w_in", [D, 2304])
    w_out = din("w_out", [D, D])
    w_mi = din("w_mi", [D, DFF])
    w_mo = din("w_mo", [DFF, D])
    g1 = din("g1", [128, 8]); g2 = din("g2", [128, 8]); gf = din("gf", [128, 8])
    gq = din("gq", [64, 1]); gk = din("gk", [64, 1])
    ga = din("ga", [64, 8]); gh = din("gh", [128, 4])
    cw = din("cw", [128, 12, 3]); cb = din("cb", [128, 12])
    fw1 = din("fw1", [33, 64]); fw2 = din("fw2", [64, 64]); fw3 = din("fw3", [64, 64])
    fw4 = din("fw4", [64, 2048])
    fb = din("fb", [64, 3]); ffreq = din("ffreq", [64, 1])
    fdel = din("fdel", [128, 16])
    skp = din("skp", [128, 8])
    qmask = din("qmask", [128, 4])
    c_cos = din("c_cos", [64, L]); c_sin = din("c_sin", [64, L])
    c_cosq = din("c_cosq", [64, NTOK]); c_sinq = din("c_sinq", [64, NTOK])
    c_zT = din("c_zT", [2, 33, L])
    c_t = din("c_t", [2, L])
    c_tb = din("c_tb", [2, 512])
    c_jt = din("c_jt", [2, 16])
    c_fpack = din("c_fpack", [128, 192], BF16)
    c_G = din("c_G", [128, 64, 2, 128], BF16)
    c_finv = din("c_finv", [128, 4, 128], BF16)
    c_M2 = din("c_M2", [128, 128, 64], BF16)
    c_M2q = din("c_M2q", [128, 128, 16], BF16)
    c_rot = din("c_rot", [64, 64]); c_sel = din("c_sel", [65, 64]); c_ones = din("c_ones", [128, 128])
    c_ident = din("c_ident", [128, 128], BF16)
    outT = nc.dram_tensor("outT", [D, NTOK], F32, kind="ExternalOutput").ap()

    U_d = dscr("U_d", [1536, L], BF16)
    H_d = dscr("H_d", [8, 128, 64, 4, 128], BF16)
    taps_d = dscr("taps_d", [128, NFFT], BF16)
    zcm_d = dscr("zcm_d", [128, L], BF16)
    att_d = dscr("att_d", [512, NTOK], BF16)
    hy_d = dscr("hy_d", [512, NTOK], BF16)
    Wmi_d = dscr("Wmi_d", [D, DFF], BF16)
    Wmo_d = dscr("Wmo_d", [8, 128, 32, 128], BF16)

    from contextlib import ExitStack
    es = ExitStack()

    _names = {}

    def sb(name, shape, dt, stack=None):
        n_ = _names.get(name, 0)
        _names[name] = n_ + 1
        nm = name if n_ == 0 else f"{name}_r{n_}"
        return (stack or es).enter_context(nc.sbuf_tensor(nm, list(shape), dt))

    def dma(out, in_, reads, writes, eng="sp"):
        return S.op(eng, lambda: S.E[eng].dma_start(out=out, in_=in_), reads, writes, dma=True)

    def mm(out, lhsT, rhs, start, stop, reads, writes):
        return S.op("pe", lambda: nc.tensor.matmul(out, lhsT, rhs, start=start, stop=stop), reads, writes)

    rr = {"i": 0}

    def alt(engs=("dve", "pool")):
        rr["i"] += 1
        return engs[rr["i"] % len(engs)]

    def V(eng):
        return S.E[eng]

    with es:
        PSW = [es.enter_context(nc.psum_tensor(f"psw{i}", [128, 1024], F32)) for i in range(3)]
        PS = [PSW[i // 2][:, (i % 2) * 512:(i % 2 + 1) * 512] for i in range(6)]
        PSB = [es.enter_context(nc.psum_tensor(f"psb{i}", [128, 1024], BF16)) for i in range(2)]
        PSX = [PSB[i].bitcast(F32)[:, :] for i in range(2)]

        ones = sb("ones", [128, 128], F32)
        ident = sb("ident", [128, 128], BF16)
        rot = sb("rot", [64, 64], F32)
        sel = sb("sel", [65, 64], F32)
        epsT = sb("epsT", [128, 1], F32)
        onesbf = sb("onesbf", [128, 128], BF16)
        g1s = sb("g1s", [128, 8], F32); g2s = sb("g2s", [128, 8], F32); gfs = sb("gfs", [128, 8], F32)
        gqs = sb("gqs", [64, 1], F32); gks = sb("gks", [64, 1], F32)
        gas = sb("gas", [64, 8], F32); ghs = sb("ghs", [128, 4], F32)
        cws = sb("cws", [128, 12, 3], F32); cbs = sb("cbs", [128, 12], F32)
        skps = sb("skps", [128, 8], F32); qms = sb("qms", [128, 4], F32)
        for t_, d_ in ((ones, c_ones), (ident, c_ident), (rot, c_rot), (sel, c_sel), (g1s, g1), (g2s, g2),
                       (gfs, gf), (gqs, gq), (gks, gk), (gas, ga), (ghs, gh), (cws, cw), (cbs, cb),
                       (skps, skp), (qms, qmask)):
            dma(t_[:], d_, [], [t_.name if hasattr(t_, "name") else id(t_)])
        KEY = lambda t_: t_.name if hasattr(t_, "name") else id(t_)
        S.op("dve", lambda: nc.vector.memset(epsT[:], EPS), [], ["epsT"])
        S.op("pool", lambda: nc.gpsimd.memset(onesbf[:], 1.0), [], ["onesbf"])

        FC = {}
        SNAP = {}

        def snap(name, src_ap, key, ncols):
            if name not in debug or name in SNAP:
                return
            SNAP[name] = nc.dram_tensor("dbg_" + name, [128, ncols], BF16, kind="ExternalOutput").ap()
            dma(SNAP[name], src_ap, [key], ["dbg_" + name])


        def load_fft_consts(stack):
            FC["fpack"] = sb("fpack", [128, 192], BF16, stack)
            FC["finv"] = sb("finv", [128, 4, 128], BF16, stack)
            FC["Gq"] = [sb(f"Gq{i}", [128, 8, 2, 128], BF16, stack) for i in range(2)]
            dma(FC["fpack"][:], c_fpack, [], ["fpack"])
            dma(FC["finv"][:], c_finv, [], ["finv"])

        def rsqrt(out, in_, scale, reads, writes, np_=128):
            S.op("act", lambda: nc.scalar.activation(out, in_, AF.Sqrt, bias=epsT[0:np_, :], scale=scale),
                 list(reads) + ["epsT"], writes)
            S.op("dve", lambda: nc.vector.reciprocal(out, out), writes, writes)

        def fft_forward(ztm, K, AT, cb_batch, zkey="ztm", akey="AT", pre_batch=None):
            fpack = FC["fpack"]
            for c2 in range(64):
                ps = PS[c2 % 2]
                pk = f"ps{c2 % 2}"
                for cc in range(2):
                    c = c2 * 2 + cc
                    mm(ps[:, cc * 192:(cc + 1) * 192], ztm[0:K, c, :], fpack[0:K, :], True, True,
                       [(zkey, c // 32), "fpack"], [pk])
                e = alt(("act", "dve"))
                src = ps[:, 0:384].rearrange("p (cc j) -> p j cc", cc=2)
                dst = AT[:, :, c2 * 2:c2 * 2 + 2]
                if e == "act":
                    S.op("act", lambda dst=dst, src=src: nc.scalar.copy(dst, src), [pk], [akey])
                else:
                    S.op("dve", lambda dst=dst, src=src: nc.vector.tensor_copy(dst, src), [pk], [akey])

            def load_g(q8):
                dma(FC["Gq"][q8 % 2][:], c_G[:, q8 * 8:(q8 + 1) * 8], [], [f"Gq{q8 % 2}"])

            load_g(0)
            for kb in range(32):
                q4 = kb // 4
                Gq = FC["Gq"][q4 % 2]
                gkey = f"Gq{q4 % 2}"
                if kb % 4 == 0 and q4 + 1 < 8:
                    load_g(q4 + 1)
                if pre_batch is not None:
                    pre_batch(kb)
                ps = PS[2 + kb % 2]
                pk = f"ps{2 + kb % 2}"
                for kk in range(2):
                    k1 = kb * 2 + kk
                    kl = k1 % 8
                    R = AT[:, k1, :]
                    I = AT[:, 64 + k1, :]
                    NI = AT[:, 128 + k1, :]
                    zr = ps[:, kk * 256:kk * 256 + 128]
                    zi = ps[:, kk * 256 + 128:kk * 256 + 256]
                    mm(zr, Gq[:, kl, 0, :], R, True, False, [akey, gkey], [pk])
                    mm(zr, Gq[:, kl, 1, :], NI, False, True, [akey, gkey], [pk])
                    mm(zi, Gq[:, kl, 1, :], R, True, False, [akey, gkey], [pk])
                    mm(zi, Gq[:, kl, 0, :], I, False, True, [akey, gkey], [pk])
                cb_batch(kb, ps, pk)

        def phase0():
          with ExitStack() as p0w:
            wst = [sb(f"wst{i}", [128, 8, 512], F32, p0w) for i in range(2)]
            wbf = [sb(f"wbf{i}", [128, 8, 512], BF16, p0w) for i in range(2)]
            n = 0
            jobs = []
            for cbk in range(8):
                jobs.append((w_mi.rearrange("(k p) n -> p k n", p=128)[:, :, cbk * 512:(cbk + 1) * 512],
                             Wmi_d.rearrange("(k p) n -> p k n", p=128)[:, :, cbk * 512:(cbk + 1) * 512]))
            for kr in range(4):
                for cbk in range(2):
                    jobs.append((w_mo.rearrange("(k p) n -> p k n", p=128)[:, kr * 8:(kr + 1) * 8, cbk * 512:(cbk + 1) * 512],
                                 (kr, cbk)))
            for src, dst in jobs:
                s_ = n % 2
                dma(wst[s_][:], src, [], [f"wst{s_}"])
                e = ("dve", "act")[n % 2]
                if e == "act":
                    S.op("act", lambda s_=s_: nc.scalar.copy(wbf[s_][:], wst[s_][:]), [f"wst{s_}"], [f"wbf{s_}"])
                else:
                    S.op(e, lambda s_=s_, e=e: V(e).tensor_copy(wbf[s_][:], wst[s_][:]), [f"wst{s_}"], [f"wbf{s_}"])
                if n < 8:
                    dma(dst, wbf[s_][:], [f"wbf{s_}"], ["Wmid"])
                else:
                    kr_, cbk_ = dst
                    for c_ in range(4):
                        dma(Wmo_d[cbk_ * 4 + c_, :, kr_ * 8:(kr_ + 1) * 8, :], wbf[s_][:, :, c_ * 128:(c_ + 1) * 128], [f"wbf{s_}"], ["Wmod"])
                n += 1
            S.barrier()
          with ExitStack() as p0:
            load_fft_consts(p0)

            w1s = sb("w1s", [33, 64], F32, p0); w2s = sb("w2s", [64, 64], F32, p0); w3s = sb("w3s", [64, 64], F32, p0)
            w4b = sb("w4b", [64, 2048], BF16, p0)
            fbs = sb("fbs", [64, 3], F32, p0); frs = sb("frs", [64, 1], F32, p0); ffb = sb("ffb", [64, 3], F32, p0)
            dls = sb("dls", [128, 16], F32, p0); ndl = sb("ndl", [128, 16], F32, p0)
            negpi = sb("negpi", [128, 1], F32, p0)
            H3 = sb("H3", [64, 2, L], BF16, p0)
            p0m = ExitStack()
            w4s = sb("w4s", [64, 2048], F32, p0m)
            for t_, d_, k_ in ((w1s, fw1, "w1s"), (w2s, fw2, "w2s"), (w3s, fw3, "w3s"), (w4s, fw4, "w4s"),
                               (fbs, fb, "fbs"), (frs, ffreq, "frs"), (dls, fdel, "dls")):
                dma(t_[:], d_, [], [k_])
            S.op("dve", lambda: nc.vector.tensor_copy(w4b[:], w4s[:]), ["w4s"], ["w4b"])
            S.op("dve", lambda: nc.vector.tensor_scalar(ffb[:], fbs[:], frs[:, 0:1], None, ALU.mult), ["fbs", "frs"], ["ffb"])
            S.op("dve", lambda: nc.vector.tensor_scalar(ndl[:], dls[:], -1.0, None, ALU.mult), ["dls"], ["ndl"])
            S.op("dve", lambda: nc.vector.tensor_tensor(ndl[:], ndl[:], dls[:], ALU.min), ["dls", "ndl"], ["ndl"])
            S.op("dve", lambda: nc.vector.memset(negpi[:], 0.0), [], ["negpi"])
            zc = [sb(f"zc{i}", [33, 512], F32, p0m) for i in range(4)]
            faL = [sb(f"fa{i}", [64, 512], F32, p0m) for i in range(2)]
            fbufL = [sb(f"fbuf{i}", [64, 512], F32, p0m) for i in range(2)]
            fhL = [sb(f"fh{i}", [64, 512], F32, p0m) for i in range(2)]

            def sin_layer(ps, li, dst, pk, v_, dkey):
                fa, fbuf = faL[v_], fbufL[v_]
                fak, fbk = f"fa{v_}", f"fbuf{v_}"
                S.op("dve", lambda: nc.vector.tensor_scalar(fa[:], ps[0:64, :], frs[:, 0:1], ffb[:, li:li + 1], ALU.mult, ALU.add),
                     [pk, "frs", "ffb"], [fak])
                S.op("dve", lambda: nc.vector.tensor_scalar(fbuf[:], fa[:], 1.0 / TWO_PI, MAGIC, ALU.mult, ALU.add), [fak], [fbk])
                S.op("dve", lambda: nc.vector.tensor_scalar(fbuf[:], fbuf[:], -MAGIC, -TWO_PI, ALU.add, ALU.mult), [fbk], [fbk])
                S.op("dve", lambda: nc.vector.tensor_tensor(fa[:], fa[:], fbuf[:], ALU.add), [fak, fbk], [fak])
                S.op("act", lambda: nc.scalar.activation(dst, fa[:], AF.Sin, bias=negpi[0:64, :], scale=1.0), [fak, "negpi"], [dkey])

            for j in range(16):
                for var in range(2):
                    s_ = (j % 2) * 2 + var
                    psm = PS[4 + var]
                    pmk = f"ps{4 + var}"
                    fh = fhL[var]
                    fhk = f"fh{var}"
                    dma(zc[s_][:], c_zT[var, :, j * 512:(j + 1) * 512], [], [f"zc{s_}"])
                    mm(psm[0:64, :], w1s[:], zc[s_][:], True, True, [f"zc{s_}", "w1s"], [pmk])
                    sin_layer(psm, 0, fh[:], pmk, var, fhk)
                    mm(psm[0:64, :], w2s[:], fh[:], True, True, [fhk, "w2s"], [pmk])
                    sin_layer(psm, 1, fh[:], pmk, var, fhk)
                    mm(psm[0:64, :], w3s[:], fh[:], True, True, [fhk, "w3s"], [pmk])
                    sin_layer(psm, 2, H3[:, var, j * 512:(j + 1) * 512], pmk, var, ("H3", var))

            S.barrier()
            p0m.close()
            tapsU = sb("tapsU", [128, NFFT], BF16, p0)
            ztmf = sb("ztmf", [128, 128, 128], BF16, p0)
            ATf = sb("ATf", [128, 192, 128], BF16, p0)
            tb = sb("tb", [128, 2, 512], F32, p0)
            jt = sb("jt", [128, 2, 16], F32, p0)
            for hf in range(2):
                dma(tb[:, hf, :], c_tb[hf:hf + 1, :].broadcast_to([128, 512]), [], ["tb"])
                dma(jt[:, hf, :], c_jt[hf:hf + 1, :].broadcast_to([128, 16]), [], ["jt"])
            wbase = [sb(f"wbase{i}", [128, 512], F32, p0) for i in range(2)]
            wcj = [sb(f"wcj{i}", [128, 16], F32, p0) for i in range(2)]
            tpf = [sb(f"tpf{i}", [128, 512], F32, p0) for i in range(2)]
            psum_ = sb("psum_", [128, 32], F32, p0)
            nrm = sb("nrm", [128, 2], F32, p0)
            Hst = [sb(f"Hst{i}", [128, 2, 4, 128], BF16, p0) for i in range(4)]
            for go in range(8):
                g, o = go // 2, go % 2
                for half in range(2):
                    col0 = o * 1024 + half * 512 + g * 128
                    ct = col0 // 128
                    S.op("act", lambda half=half, ct=ct: nc.scalar.activation(wbase[half][:], tb[:, half, :], AF.Exp, scale=ndl[:, ct:ct + 1]),
                         ["tb", "ndl"], [f"wbase{half}"])
                    S.op("act", lambda half=half, ct=ct: nc.scalar.activation(wcj[half][:], jt[:, half, :], AF.Exp, scale=ndl[:, ct:ct + 1]),
                         ["jt", "ndl"], [f"wcj{half}"])
                    for j in range(16):
                        s_ = j % 2
                        pst = PS[4 + s_]
                        ptk = f"ps{4 + s_}"
                        mm(pst[:, :], w4b[:, col0:col0 + 128], H3[:, half, j * 512:(j + 1) * 512], True, True,
                           [("H3", half), "w4b"], [ptk])
                        S.op("dve", lambda s_=s_, pst=pst, half=half, j=j: nc.vector.scalar_tensor_tensor(
                            out=tpf[s_][:], in0=pst[:, :], scalar=wcj[half][:, j:j + 1], in1=wbase[half][:], op0=ALU.mult, op1=ALU.mult),
                            [ptk, f"wcj{half}", f"wbase{half}"], [f"tpf{s_}"])
                        idx = half * 16 + j
                        if half == 1 and j == 0:
                            S.op("dve", lambda s_=s_: nc.vector.memset(tpf[s_][:, 0:1], 0.0), [f"tpf{s_}"], [f"tpf{s_}"])
                        S.op("dve", lambda idx=idx, s_=s_: nc.vector.tensor_reduce(psum_[:, idx:idx + 1], tpf[s_][:], mybir.AxisListType.X, ALU.add,
                                                                                  apply_absolute_value=True), [f"tpf{s_}"], ["psum_"])
                        S.op("act", lambda half=half, j=j, s_=s_: nc.scalar.copy(tapsU[:, half * L + j * 512: half * L + (j + 1) * 512], tpf[s_][:]),
                             [f"tpf{s_}"], ["tapsU"])
                S.op("dve", lambda: nc.vector.tensor_reduce(nrm[:, 0:1], psum_[:], mybir.AxisListType.X, ALU.add), ["psum_"], ["nrm"])
                S.op("dve", lambda: nc.vector.reciprocal(nrm[:, 0:1], nrm[:, 0:1]), ["nrm"], ["nrm"])
                S.op("dve", lambda: nc.vector.tensor_scalar(nrm[:, 1:2], nrm[:, 0:1], -1.0, None, ALU.mult), ["nrm"], ["nrm"])
                S.op("dve", lambda: nc.vector.tensor_scalar(tapsU[:, 0:L], tapsU[:, 0:L], nrm[:, 0:1], None, ALU.mult), ["tapsU", "nrm"], ["tapsU"])
                S.op("act", lambda: nc.scalar.mul(tapsU[:, L:NFFT], tapsU[:, L:NFFT], nrm[:, 1:2]), ["tapsU", "nrm"], ["tapsU"])
                tv = taps_d.rearrange("c (n1 n2) -> n1 c n2", n2=128)
                for q4 in range(4):
                    dma(taps_d[q4 * 32:(q4 + 1) * 32, :], tapsU[q4 * 32:(q4 + 1) * 32, :], ["tapsU"], [("taps_d", q4)])
                for q4 in range(4):
                    dma(ztmf[:, q4 * 32:(q4 + 1) * 32, :], tv[:, q4 * 32:(q4 + 1) * 32, :], [("taps_d", q4)], [("ztm", q4)])

                def cb_filter(kb, ps, pk, go=go):
                    s_ = kb % 4
                    zv = ps[:, :].rearrange("p (k r c) -> p k r c", k=2, r=2)
                    S.op("act", lambda: nc.scalar.copy(Hst[s_][:, :, 0:2, :], zv), [pk], [f"Hst{s_}"])
                    S.op("dve", lambda: nc.vector.tensor_copy(Hst[s_][:, :, 2, :], zv[:, :, 1, :]), [pk], [f"Hst{s_}"])
                    S.op("dve", lambda: nc.vector.tensor_copy(Hst[s_][:, :, 3, :], zv[:, :, 0, :]), [pk], [f"Hst{s_}"])
                    dma(H_d[go, :, kb * 2:kb * 2 + 2], Hst[s_][:], [f"Hst{s_}"], [("H_d", go)])

                fft_forward(ztmf, 128, ATf, cb_filter)
            S.barrier()
        def rms_apply(src, gvec, dst, scale, nk, tag, sqb, rs, np_=128, ss_ps=0, sqk="sqb", rsk="rs"):
            rms_a(src, nk, tag, sqb, np_, sqk)
            rms_b(src, gvec, dst, scale, nk, tag, sqb, rs, np_, ss_ps, sqk, rsk)

        def rms_a(src, nk, tag, sqb, np_=128, sqk="sqb"):
            S.op("act", lambda: nc.scalar.activation(sqb[0:np_, 0:nk, :], src, AF.Square), [tag + "_src"], [sqk])

        def rms_b(src, gvec, dst, scale, nk, tag, sqb, rs, np_=128, ss_ps=0, sqk="sqb", rsk="rs"):
            for k in range(nk):
                mm(PS[ss_ps][0:np_, :], onesbf[0:np_, 0:np_], sqb[0:np_, k, :], k == 0, k == nk - 1, [sqk, "onesbf"], [f"ps{ss_ps}"])
            S.op("act", lambda: nc.scalar.activation(rs[0:np_, :], PS[ss_ps][0:np_, :], AF.Sqrt, bias=epsT[0:np_, :], scale=scale),
                 [f"ps{ss_ps}", "epsT"], [rsk])
            S.op("dve", lambda: nc.vector.reciprocal(rs[0:np_, :], rs[0:np_, :]), [rsk], [rsk])
            for k in range(nk):
                S.op("dve", lambda k=k: nc.vector.scalar_tensor_tensor(out=dst[:, k, :], in0=src[:, k, :], scalar=gvec[:, k:k + 1],
                                                                      in1=rs[0:np_, :], op0=ALU.mult, op1=ALU.mult),
                     [tag + "_src", rsk, "gvecs"], [tag + "_dst"])

        def phase12():
          with ExitStack() as p12:
            KT = sb("KT", [128, 2, L], BF16, p12)
            Vx = sb("Vx", [128, 64, 2, 64], BF16, p12)
            QT = sb("QT", [128, 4, NTOK], BF16, p12)
            onesb = sb("onesb", [128, 64], BF16, p12)
            S.op("pool", lambda: nc.gpsimd.memset(onesb[:], 1.0), [], ["onesb"])
            with ExitStack() as p1:
                Wb = sb("Wb", [128, 8, 2304], BF16, p1)
                with ExitStack() as p1c:
                    wst = [sb(f"wsi{i}", [128, 8, 384], F32, p1c) for i in range(2)]
                    wv = w_in.rearrange("(k p) n -> p k n", p=128)
                    for bk in range(6):
                        s_ = bk % 2
                        dma(wst[s_][:], wv[:, :, bk * 384:(bk + 1) * 384], [], [f"wst{s_}"])
                        if bk % 2 == 0:
                            S.op("dve", lambda s_=s_, bk=bk: nc.vector.tensor_copy(Wb[:, :, bk * 384:(bk + 1) * 384], wst[s_][:]), [f"wst{s_}"], ["Wb"])
                        else:
                            S.op("act", lambda s_=s_, bk=bk: nc.scalar.copy(Wb[:, :, bk * 384:(bk + 1) * 384], wst[s_][:]), [f"wst{s_}"], ["Wb"])
                    S.barrier()
                xin = [sb(f"xin{i}", [128, 8, 512], F32, p1) for i in range(2)]
                sqbL = [sb(f"sqb{i}", [128, 8, 512], BF16, p1) for i in range(2)]
                rsL = [sb(f"rs{i}", [128, 512], F32, p1) for i in range(2)]
                aTL = [sb(f"aT{i}", [128, 8, 512], BF16, p1) for i in range(2)]
                ust = sb("ust", [128, 12, 512], BF16, p1)
                hsq = sb("hsq", [64, 512], BF16, p1); hrs = sb("hrs", [64, 512], F32, p1); hkn = sb("hkn", [64, 512], F32, p1)
                ht1 = sb("ht1", [64, 512], F32, p1); ht2 = sb("ht2", [64, 512], F32, p1)
                csb = [sb(f"csb{i}", [64, 2, 512], F32, p1) for i in range(2)]
                qstg = [sb(f"qstg{i}", [64, 512], BF16, p1) for i in range(2)]

                def hp1(srcb, gvec):
                    src = PS[srcb][0:64, :]
                    sk_ = f"ps{srcb}"
                    S.op("act", lambda: nc.scalar.activation(hsq[:], src, AF.Square), [sk_], ["hsq"])
                    mm(PS[3][0:64, :], onesbf[0:64, 0:64], hsq[:], True, True, ["hsq", "onesbf"], ["ps3"])
                    S.op("act", lambda: nc.scalar.activation(hrs[:], PS[3][0:64, :], AF.Sqrt, bias=epsT[0:64, :], scale=1.0 / 64.0),
                         ["ps3", "epsT"], ["hrs"])
                    S.op("dve", lambda: nc.vector.reciprocal(hrs[:], hrs[:]), ["hrs"], ["hrs"])
                    S.op("dve", lambda: nc.vector.scalar_tensor_tensor(out=hkn[:], in0=src, scalar=gvec[:, 0:1], in1=hrs[:],
                                                                      op0=ALU.mult, op1=ALU.mult), [sk_, "hrs", "gvecs"], ["hkn"])

                def hp2(cs, cskey, dst, dstkey):
                    mm(PS[3][0:64, :], rot[:], hkn[:], True, True, ["hkn", "rot"], ["ps3"])
                    S.op("pool", lambda: nc.gpsimd.tensor_tensor(ht1[:], hkn[:], cs[:, 0, :], ALU.mult), ["hkn", cskey], ["ht1"])
                    S.op("dve", lambda: nc.vector.tensor_tensor(ht2[:], PS[3][0:64, :], cs[:, 1, :], ALU.mult), ["ps3", cskey], ["ht2"])
                    S.op("pool", lambda: nc.gpsimd.tensor_tensor(dst, ht1[:], ht2[:], ALU.add), ["ht1", "ht2"], [dstkey])

                xv = xT.rearrange("(k p) t -> p k t", p=128)
                xqv = xTq.rearrange("(k p) t -> p k t", p=128)
                Uv = U_d.rearrange("(ct p) t -> p ct t", p=128)
                work = [("seq", j) for j in range(16)] + [("own", j) for j in range(4)]

                def norm_a(w):
                    kind, j = work[w]
                    s_ = w % 2
                    tsl = slice(j * 512, (j + 1) * 512)
                    srcv, cc, ss_ = (xv, c_cos, c_sin) if kind == "seq" else (xqv, c_cosq, c_sinq)
                    dma(xin[s_][:], srcv[:, :, tsl], [], [f"xin{s_}", f"n1{s_}_src"])
                    dma(csb[s_][:, 0, :], cc[:, tsl], [], [f"csb{s_}"])
                    dma(csb[s_][:, 1, :], ss_[:, tsl], [], [f"csb{s_}"])
                    rms_a(xin[s_][:], 8, f"n1{s_}", sqbL[s_], 128, f"sqb{s_}")

                def norm_b(w):
                    s_ = w % 2
                    rms_b(xin[s_][:], g1s, aTL[s_], 1.0 / D, 8, f"n1{s_}", sqbL[s_], rsL[s_], 128, 0, f"sqb{s_}", f"rs{s_}")

                def emit_proj(w, mid_hook):
                    kind, j = work[w]
                    s_ = w % 2
                    aT = aTL[s_]
                    ak = f"n1{s_}_dst"
                    cs, csk = csb[s_], f"csb{s_}"
                    tsl = slice(j * 512, (j + 1) * 512)
                    if kind == "seq":
                        for kvh in range(2):
                            for k in range(8):
                                mm(PS[1 + kvh][0:64, :], Wb[:, k, 512 + kvh * 64:512 + (kvh + 1) * 64], aT[:, k, :], k == 0, k == 7,
                                   ["Wb", ak], [f"ps{1 + kvh}"])
                        for tt in range(4):
                            for k in range(8):
                                mm(PS[4][:, tt * 128:(tt + 1) * 128], aT[:, k, tt * 128:(tt + 1) * 128], Wb[:, k, 640:768], k == 0, k == 7,
                                   ["Wb", ak], ["ps4"])
                        S.op("act", lambda j=j: nc.scalar.copy(Vx[:, j * 4:(j + 1) * 4, :, :],
                                                             PS[4][:, :].rearrange("p (t h d) -> p t h d", t=4, h=2)), ["ps4"], ["Vx"])
                        for ct in range(12):
                            pu, puk = (PS[5], "ps5") if ct % 2 == 0 else (PSX[0], "psb0")
                            for k in range(8):
                                mm(pu[:, :], Wb[:, k, 768 + ct * 128:768 + (ct + 1) * 128], aT[:, k, :], k == 0, k == 7, ["Wb", ak], [puk])
                            if ct % 2 == 0:
                                S.op("act", lambda ct=ct, pu=pu: nc.scalar.copy(ust[:, ct, :], pu[:, :]), [puk], ["ust"])
                            else:
                                S.op("dve", lambda ct=ct, pu=pu: nc.vector.tensor_copy(ust[:, ct, :], pu[:, :]), [puk], ["ust"])
                            if ct == 1:
                                hp1(1, gks)
                            if ct == 3:
                                hp2(cs, csk, KT[0:64, 0, tsl], "KT")
                                hp1(2, gks)
                            if ct == 5:
                                hp2(cs, csk, KT[0:64, 1, tsl], "KT")
                            if ct == 7:
                                mid_hook()
                        dma(Uv[:, :, tsl], ust[:], ["ust"], ["U_d"])
                    else:
                        def qdst(h):
                            if h % 2 == 0:
                                return QT[0:64, h // 2, tsl], "QT", None
                            qs_ = (h // 2) % 2
                            return qstg[qs_][:], f"qstg{qs_}", qs_

                        def qproj(h):
                            hb = 1 + h % 2
                            for k in range(8):
                                mm(PS[hb][0:64, :], Wb[:, k, h * 64:(h + 1) * 64], aT[:, k, :], k == 0, k == 7, ["Wb", ak], [f"ps{hb}"])

                        qproj(0)
                        for h in range(8):
                            hp1(1 + h % 2, gqs)
                            if h + 1 < 8:
                                qproj(h + 1)
                            if h == 3:
                                mid_hook()
                            d_, dk_, qs_ = qdst(h)
                            hp2(cs, csk, d_, dk_)
                            if qs_ is not None:
                                dma(QT[64:128, h // 2, tsl], qstg[qs_][:], [f"qstg{qs_}"], ["QT2"])

                norm_a(0)
                norm_b(0)
                for w in range(len(work)):
                    if w + 1 < len(work):
                        norm_a(w + 1)
                        emit_proj(w, lambda w=w: norm_b(w + 1))
                    else:
                        emit_proj(w, lambda: None)
                dma(KT[64:128, :, :], KT[0:64, :, :], ["KT"], ["KT2"])
                S.barrier()
            if 2 in phases:
              with ExitStack() as p2:
                pb = [sb(f"pb{i}", [128, 1024], BF16, p2) for i in range(2)]
                rden = sb("rden", [128, 512], F32, p2)
                ast = [sb(f"ast{i}", [128, 512], BF16, p2) for i in range(2)]
                n_ = 0
                for qc in range(4):
                    qsl = slice(qc * 512, (qc + 1) * 512)
                    for hp in range(4):
                        kvh = hp // 2
                        for step in range(64 + 1):
                            if step < 64:
                                kt = step
                                sl = kt % 2
                                ksl = slice(kt * 128, (kt + 1) * 128)
                                mm(PS[2 * sl][:, :], KT[0:64, kvh, ksl], QT[0:64, hp, qsl], True, True, ["KT", "QT"], [f"ps{2 * sl}"])
                                mm(PS[2 * sl + 1][:, :], KT[64:128, kvh, ksl], QT[64:128, hp, qsl], True, True, ["KT2", "QT2"], [f"ps{2 * sl + 1}"])
                                S.op("act", lambda sl=sl: nc.scalar.activation(pb[sl][:], PSW[sl][:, :], AF.Exp, scale=0.125),
                                     [f"ps{2 * sl}", f"ps{2 * sl + 1}"], [f"pb{sl}"])
                            kt = step - 1
                            if kt >= 0:
                                sl = kt % 2
                                st_, sp_ = kt == 0, kt == 63
                                pA = pb[sl][:, 0:512]
                                pB = pb[sl][:, 512:1024]
                                vv = Vx[:, kt, kvh, :]
                                S.op("pe", lambda vv=vv, pA=pA, st_=st_, sp_=sp_: nc.tensor.matmul(PS[4][0:64, :], vv, pA, start=st_, stop=sp_, tile_position=(0, 0)),
                                     ["Vx", f"pb{sl}"], ["ps4"])
                                S.op("pe", lambda vv=vv, pB=pB, st_=st_, sp_=sp_: nc.tensor.matmul(PS[4][64:128, :], vv, pB, start=st_, stop=sp_, tile_position=(0, 64)),
                                     ["Vx", f"pb{sl}"], ["ps4"])
                                S.op("pe", lambda pA=pA, st_=st_, sp_=sp_: nc.tensor.matmul(PS[5][0:64, :], onesb[:], pA, start=st_, stop=sp_, tile_position=(0, 0)),
                                     ["onesb", f"pb{sl}"], ["ps5"])
                                S.op("pe", lambda pB=pB, st_=st_, sp_=sp_: nc.tensor.matmul(PS[5][64:128, :], onesb[:], pB, start=st_, stop=sp_, tile_position=(0, 64)),
                                     ["onesb", f"pb{sl}"], ["ps5"])
                        S.op("dve", lambda: nc.vector.reciprocal(rden[:], PS[5][:, :]), ["ps5"], ["rden"])
                        a_ = n_ % 2
                        S.op("dve", lambda a_=a_: nc.vector.tensor_tensor(ast[a_][:], PS[4][:, :], rden[:], ALU.mult), ["ps4", "rden"], [f"ast{a_}"])
                        dma(att_d[hp * 128:(hp + 1) * 128, qsl], ast[a_][:], [f"ast{a_}"], ["att_d"])
                        n_ += 1
                S.barrier()

        def phase3():
          with ExitStack() as p3:
            load_fft_consts(p3)
            finv = FC["finv"]
            M2s = sb("M2s", [128, 128, 64], BF16, p3)
            M2qs = sb("M2qs", [128, 128, 16], BF16, p3)
            dma(M2s[:], c_M2, [], ["M2s"])
            dma(M2qs[:], c_M2q, [], ["M2qs"])
            RA = sb("RA", [128, 192 * 128 + 8], BF16, p3)
            RB = sb("RB", [128, 128 * 128], BF16, p3)
            RC = sb("RC", [128, 128 * 128], BF16, p3)
            vt = sb("vt", [128, L], BF16, p3)
            x1t = sb("x1t", [128, L], BF16, p3)
            x2q = sb("x2q", [128, NTOK], BF16, p3)
            zzq = sb("zzq", [128, NTOK], BF16, p3)
            hyst = sb("hyst", [128, NTOK], BF16, p3)
            Hc = [sb(f"Hc{i}", [128, 2, 4, 128], BF16, p3) for i in range(4)]
            Pb = [sb(f"Pb{i}", [128, 2, 2, 128], BF16, p3) for i in range(2)]
            Qb = [sb(f"Qb{i}", [128, 2, 2, 128], BF16, p3) for i in range(2)]
            gt = [sb(f"gt{i}", [128, 512], F32, p3) for i in range(2)]
            whi = sb("whi", [128, 9], BF16, p3)
            whf = sb("whf", [128, 9], F32, p3)
            wlf = sb("wlf", [128, 9], F32, p3)
            sct = gt
            uraw = RA[:, 0:3 * 8194].rearrange("p (i t) -> p i t", i=3)
            AT = RA[:, 0:192 * 128].rearrange("p (j c) -> p j c", c=128)
            ztm = RB[:, :].rearrange("p (c n) -> p c n", n=128)
            Eb = RB[:, :].rearrange("p (n c) -> p n c", c=128)
            x2t = RC[:, 0:L]
            ET = RC[:, :].rearrange("p (c k) -> p c k", k=128)
            RAK, RBK, RCK = "RA", "RB", "RC"

            def masked_quarter(dst, src, dkey, skey):
                S.op("dve", lambda: nc.vector.tensor_scalar(dst[:], src[:, 0:NTOK], qms[:, 0:1], None, ALU.mult), [skey, "qms"], [dkey])
                for q in range(1, 4):
                    S.op("dve", lambda q=q: nc.vector.scalar_tensor_tensor(out=dst[:], in0=src[:, q * NTOK:(q + 1) * NTOK], scalar=qms[:, q:q + 1],
                                                                          in1=dst[:], op0=ALU.mult, op1=ALU.add), [skey, "qms", dkey], [dkey])

            RBQ = [(RBK, q4) for q4 in range(4)]

            def bounce_to_ztm(src, skey):
                zv = zcm_d.rearrange("c (n1 n2) -> n1 c n2", n2=128)
                for q4 in range(4):
                    dma(zcm_d[q4 * 32:(q4 + 1) * 32, :], src[q4 * 32:(q4 + 1) * 32, :], [skey], [("zcm_d", q4)])
                for q4 in range(4):
                    dma(ztm[0:64, q4 * 32:(q4 + 1) * 32, :], zv[:, q4 * 32:(q4 + 1) * 32, :], [("zcm_d", q4)], [(RBK, q4)])

            def conv_core(go):
                def load_h(kb):
                    dma(Hc[kb % 4][:], H_d[go, :, kb * 2:kb * 2 + 2], [("H_d", go)], [f"Hc{kb % 4}"])

                def pre(kb):
                    if kb == 0:
                        load_h(0)
                        load_h(1)
                    if kb + 2 < 32:
                        load_h(kb + 2)

                def cb(kb, ps, pk):
                    s_ = kb % 4
                    y_ = kb % 2
                    zv = ps[:, :].rearrange("p (k r c) -> p k r c", k=2, r=2)
                    S.op("dve", lambda: nc.vector.tensor_tensor(Pb[y_][:], zv, Hc[s_][:, :, 0:2, :], ALU.mult), [pk, f"Hc{s_}"], [f"Pb{y_}"])
                    S.op("dve", lambda: nc.vector.tensor_tensor(Qb[y_][:], zv, Hc[s_][:, :, 2:4, :], ALU.mult), [pk, f"Hc{s_}"], [f"Qb{y_}"])
                    pe_ = PS[4 + kb % 2]
                    pek = f"ps{4 + kb % 2}"
                    er = pe_[:, 0:256].rearrange("p (k c) -> p k c", k=2)
                    ei = pe_[:, 256:512].rearrange("p (k c) -> p k c", k=2)
                    p0, p1 = Pb[y_][:, :, 0, :], Pb[y_][:, :, 1, :]
                    q0, q1 = Qb[y_][:, :, 0, :], Qb[y_][:, :, 1, :]
                    rk = [f"Pb{y_}", f"Qb{y_}", "finv"]
                    FR, FI, NFI, NFR = finv[:, 0, :], finv[:, 1, :], finv[:, 2, :], finv[:, 3, :]
                    mm(er, FR, p0, True, False, rk, [pek]); mm(er, NFR, p1, False, False, rk, [pek])
                    mm(er, NFI, q0, False, False, rk, [pek]); mm(er, NFI, q1, False, True, rk, [pek])
                    mm(ei, FI, p0, True, False, rk, [pek]); mm(ei, NFI, p1, False, False, rk, [pek])
                    mm(ei, FR, q0, False, False, rk, [pek]); mm(ei, FR, q1, False, True, rk, [pek])
                    S.op("act", lambda: nc.scalar.copy(ET[:, :, kb * 2:kb * 2 + 2], pe_[:, 0:256].rearrange("p (k c) -> p c k", k=2)), [pek], [RCK])
                    S.op("act", lambda: nc.scalar.copy(ET[:, :, 64 + kb * 2:64 + kb * 2 + 2], pe_[:, 256:512].rearrange("p (k c) -> p c k", k=2)),
                         [pek], [RCK])
                fft_forward(ztm, 64, AT, cb, zkey=RBK, akey=RAK, pre_batch=pre)
                for c4 in range(32):
                    pb_ = PSB[c4 % 2]
                    pbk = f"psb{c4 % 2}"
                    for cc in range(4):
                        c = c4 * 4 + cc
                        S.op("pe", lambda c=c, cc=cc, pb_=pb_: nc.tensor.transpose(pb_[:, cc * 128:(cc + 1) * 128], ET[:, c, :], ident[:]),
                             [RCK, "ident"], [pbk])
                    src = pb_[:, 0:512].rearrange("p (cc n) -> p n cc", cc=4)
                    dst = Eb[:, :, c4 * 4:c4 * 4 + 4]
                    if c4 % 2 == 0:
                        S.op("act", lambda src=src, dst=dst: nc.scalar.copy(dst, src), [pbk], RBQ)
                    else:
                        S.op("dve", lambda src=src, dst=dst: nc.vector.tensor_copy(dst, src), [pbk], RBQ)

            Udv = U_d
            for g in range(4):
                for i in range(3):
                    dma(uraw[:, i, 1:L + 1], Udv[i * 512 + g * 128:i * 512 + (g + 1) * 128, :], ["U_d"], [RAK])
                S.op("pool", lambda: nc.gpsimd.memset(uraw[:, :, 0:1], 0.0), [RAK], [RAK])
                S.op("pool", lambda: nc.gpsimd.memset(uraw[:, :, L + 1:L + 2], 0.0), [RAK], [RAK])
                DG = hyst[:, 0:9 * 128].rearrange("p (i m) -> p i m", m=128)
                DGL = zzq[:, 0:9 * 128].rearrange("p (i m) -> p i m", m=128)
                wsel = cws[:, g:12:4, :]
                S.op("dve", lambda wsel=wsel: nc.vector.tensor_copy(whi[:].rearrange("p (i j) -> p i j", j=3), wsel), ["cws"], ["whi"])
                S.op("dve", lambda: nc.vector.tensor_copy(whf[:], whi[:]), ["whi"], ["whf"])
                S.op("dve", lambda wsel=wsel: nc.vector.tensor_tensor(wlf[:].rearrange("p (i j) -> p i j", j=3), wsel,
                                                                     whf[:].rearrange("p (i j) -> p i j", j=3), ALU.subtract), ["cws", "whf"], ["wlf"])
                for idx in range(9):
                    S.op("dve", lambda idx=idx: nc.vector.tensor_scalar(DG[:, idx, :], ident[:], whf[:, idx:idx + 1], None, ALU.mult), ["whf", "ident"], ["hyst"])
                    S.op("dve", lambda idx=idx: nc.vector.tensor_scalar(DGL[:, idx, :], ident[:], wlf[:, idx:idx + 1], None, ALU.mult), ["wlf", "ident"], ["zzq"])
                nq = 0
                for i, (dst, dkey) in enumerate(((vt, "vt"), (x1t, "x1t"), (x2t, RCK))):
                    ct = i * 4 + g
                    for q in range(16):
                        o0 = q * 512
                        ps = PS[nq % 2]
                        pk = f"ps{nq % 2}"
                        for j in range(3):
                            mm(ps[:, :], DG[:, i * 3 + j, :], uraw[:, i, o0 + j:o0 + j + 512], j == 0, False, ["hyst", RAK], [pk])
                        for j in range(3):
                            mm(ps[:, :], DGL[:, i * 3 + j, :], uraw[:, i, o0 + j:o0 + j + 512], False, j == 2, ["zzq", RAK], [pk])
                        if nq % 2 == 0:
                            S.op("act", lambda ps=ps, dst=dst, o0=o0, ct=ct: nc.scalar.activation(dst[:, o0:o0 + 512], ps[:, :], AF.Identity, bias=cbs[:, ct:ct + 1]),
                                 [pk, "cbs"], [dkey])
                        else:
                            S.op("dve", lambda ps=ps, dst=dst, o0=o0, ct=ct: nc.vector.tensor_scalar(dst[:, o0:o0 + 512], ps[:, :], cbs[:, ct:ct + 1], None, ALU.add),
                                 [pk, "cbs"], [dkey])
                        nq += 1
                masked_quarter(x2q, x2t, "x2q", RCK)
                snap("vt0", vt[:], "vt", L)
                snap("x1t0", x1t[:], "x1t", L)
                bounce_to_ztm(vt, "vt")
                conv_core(g * 2 + 0)
                vt3 = vt[:, :].rearrange("p (n1 n2) -> p n1 n2", n2=128)
                x13 = x1t[:, :].rearrange("p (n1 n2) -> p n1 n2", n2=128)
                for nb in range(16):
                    ps = PS[nb % 2]
                    pk = f"ps{nb % 2}"
                    for j in range(8):
                        n2 = nb * 8 + j
                        mm(ps[:, j * 64:(j + 1) * 64], Eb[:, n2, :], M2s[:, n2, :], True, True, RBQ + ["M2s"], [pk])
                    psv = ps[:, :].rearrange("p (j n) -> p n j", j=8)
                    gtv = gt[nb % 2][:, :].rearrange("p (n j) -> p n j", j=8)
                    gk_ = f"gt{nb % 2}"
                    S.op("dve", lambda psv=psv, gtv=gtv, nb=nb, g=g: nc.vector.scalar_tensor_tensor(out=gtv, in0=vt3[:, :, nb * 8:(nb + 1) * 8],
                                                                                             scalar=skps[:, g:g + 1], in1=psv, op0=ALU.mult, op1=ALU.add),
                         [pk, "vt", "skps"], [gk_])
                    S.op("pool", lambda gtv=gtv, nb=nb: nc.gpsimd.tensor_tensor(vt3[:, :, nb * 8:(nb + 1) * 8], gtv, x13[:, :, nb * 8:(nb + 1) * 8], ALU.mult),
                         [gk_, "x1t"], ["vt"])
                snap("zz0", vt[:], "vt", L)
                masked_quarter(zzq, vt, "zzq", "vt")
                bounce_to_ztm(vt, "vt")
                conv_core(g * 2 + 1)
                zq3 = zzq[:, :].rearrange("p (n1 n2) -> p n1 n2", n2=128)
                xq3 = x2q[:, :].rearrange("p (n1 n2) -> p n1 n2", n2=128)
                hy3 = hyst[:, :].rearrange("p (n1 n2) -> p n1 n2", n2=128)
                for nb in range(4):
                    ps = PS[nb % 2]
                    pk = f"ps{nb % 2}"
                    for j in range(32):
                        n2 = nb * 32 + j
                        mm(ps[:, j * 16:(j + 1) * 16], Eb[:, n2, :], M2qs[:, n2, :], True, True, RBQ + ["M2qs"], [pk])
                    psv = ps[:, :].rearrange("p (j n) -> p n j", j=32)
                    gtv = gt[nb % 2][:, :].rearrange("p (n j) -> p n j", j=32)
                    gk_ = f"gt{nb % 2}"
                    S.op("dve", lambda psv=psv, gtv=gtv, nb=nb, g=g: nc.vector.scalar_tensor_tensor(out=gtv, in0=zq3[:, :, nb * 32:(nb + 1) * 32],
                                                                                             scalar=skps[:, 4 + g:5 + g], in1=psv, op0=ALU.mult, op1=ALU.add),
                         [pk, "zzq", "skps"], [gk_])
                    S.op("pool", lambda gtv=gtv, nb=nb: nc.gpsimd.tensor_tensor(hy3[:, :, nb * 32:(nb + 1) * 32], gtv, xq3[:, :, nb * 32:(nb + 1) * 32], ALU.mult),
                         [gk_, "x2q"], ["hyst"])
                dma(hy_d[g * 128:(g + 1) * 128, :], hyst[:], ["hyst"], ["hy_d"])
            S.barrier()

        def phase4():
          with ExitStack() as p4:
            WoA = sb("WoA", [64, 8, D], BF16, p4)
            WoH = sb("WoH", [128, 4, D], BF16, p4)
            with ExitStack() as p4c:
                wsa = sb("wsa", [64, 8, D], F32, p4c)
                wsh = sb("wsh", [128, 4, D], F32, p4c)
                dma(wsa[:], w_out[0:512, :].rearrange("(h d) n -> d h n", d=64), [], ["wsa"])
                dma(wsh[:], w_out[512:1024, :].rearrange("(g p) n -> p g n", p=128), [], ["wsh"])
                S.op("dve", lambda: nc.vector.tensor_copy(WoA[:], wsa[:]), ["wsa"], ["WoA"])
                S.op("pool", lambda: nc.gpsimd.tensor_copy(WoH[:], wsh[:]), ["wsh"], ["WoH"])
                S.barrier()
            attc = sb("attc", [64, 8, 512], BF16, p4)
            hyc = sb("hyc", [128, 4, 512], BF16, p4)
            sqb = sb("sqb4", [128, 8, 512], BF16, p4)
            rs = sb("rs4", [128, 512], F32, p4)
            mixA = sb("mixA", [64, 8, 512], BF16, p4)
            mixH = sb("mixH", [128, 4, 512], BF16, p4)
            h1 = sb("h1", [128, 8, 512], F32, p4)
            mT = sb("mT", [128, 8, 512], BF16, p4)
            Wmi_s = [sb(f"Wmi_s{i}", [128, 8, 1024], BF16, p4) for i in range(2)]
            Wmo_s = [sb(f"Wmo_s{i}", [128, 32, 128], BF16, p4) for i in range(2)]
            fT = sb("fT", [128, 32, 512], BF16, p4)
            rb = [sb(f"rb{i}", [128, 512], F32, p4) for i in range(2)]
            xqv = xTq.rearrange("(k p) t -> p k t", p=128)
            Wmiv = Wmi_d.rearrange("(k p) n -> p k n", p=128)
            outv = outT.rearrange("(k p) t -> p k t", p=128)
            nw = 0
            nwo = 0
            for j in range(4):
                tsl = slice(j * 512, (j + 1) * 512)
                dma(attc[:], att_d[:, tsl].rearrange("(h d) t -> d h t", d=64), ["att_d"], ["attc", "na_src"])
                dma(hyc[:], hy_d[:, tsl].rearrange("(g p) t -> p g t", p=128), ["hy_d"], ["hyc", "nh_src"])
                dma(h1[:], xqv[:, :, tsl], [], ["h1", "n2_src", "nf_src"])
                rms_apply(attc[:], gas, mixA, 1.0 / 512.0, 8, "na", sqb, rs, np_=64)
                rms_apply(hyc[:], ghs, mixH, 1.0 / 512.0, 4, "nh", sqb, rs)
                for ct in range(8):
                    csl = slice(ct * 128, (ct + 1) * 128)
                    for h in range(8):
                        mm(PS[1][:, :], WoA[:, h, csl], mixA[:, h, :], h == 0, False, ["WoA", "na_dst"], ["ps1"])
                    for g in range(4):
                        mm(PS[1][:, :], WoH[:, g, csl], mixH[:, g, :], False, g == 3, ["WoH", "nh_dst"], ["ps1"])
                    S.op("dve", lambda ct=ct: nc.vector.tensor_tensor(h1[:, ct, :], h1[:, ct, :], PS[1][:, :], ALU.add), ["ps1", "h1"], ["h1", "n2_src", "nf_src"])
                rms_apply(h1[:], g2s, mT, 1.0 / D, 8, "n2", sqb, rs)
                for pc in range(4):
                    s_ = nw % 2
                    dma(Wmi_s[s_][:], Wmiv[:, :, pc * 1024:(pc + 1) * 1024], ["Wmid"], [f"Wmi_s{s_}"])
                    nw += 1
                    for ft in range(8):
                        f = pc * 8 + ft
                        pf = PS[2 + f % 2]
                        pfk = f"ps{2 + f % 2}"
                        for k in range(8):
                            mm(pf[:, :], Wmi_s[s_][:, k, ft * 128:(ft + 1) * 128], mT[:, k, :], k == 0, k == 7, [f"Wmi_s{s_}", "n2_dst"], [pfk])
                        r_ = f % 2
                        S.op("act", lambda pf=pf, r_=r_: nc.scalar.activation(rb[r_][:], pf[:, :], AF.Relu), [pfk], [f"rb{r_}"])
                        e = ("pool", "dve")[f % 2]
                        S.op(e, lambda e=e, r_=r_, f=f: V(e).tensor_tensor(fT[:, f, :], rb[r_][:], rb[r_][:], ALU.mult), [f"rb{r_}"], ["fT"])
                for ct in range(8):
                    s_ = nwo % 2
                    dma(Wmo_s[s_][:], Wmo_d[ct], ["Wmod"], [f"Wmo_s{s_}"])
                    nwo += 1
                    po = PS[4 + ct % 2]
                    pok = f"ps{4 + ct % 2}"
                    for f in range(32):
                        mm(po[:, :], Wmo_s[s_][:, f, :], fT[:, f, :], f == 0, f == 31, [f"Wmo_s{s_}", "fT"], [pok])
                    S.op("dve", lambda ct=ct, po=po: nc.vector.tensor_tensor(h1[:, ct, :], h1[:, ct, :], po[:, :], ALU.add), [pok, "h1"], ["h1", "n2_src", "nf_src"])
                rms_apply(h1[:], gfs, h1, 1.0 / D, 8, "nf", sqb, rs)
                dma(outv[:, :, tsl], h1[:], ["h1", "nf_dst"], ["outT"])
            S.barrier()

        if 0 in phases:
            phase0()
        if 1 in phases:
            phase12()
        if 3 in phases:
            phase3()
        if 4 in phases:
            phase4()
        if debug:
            with ExitStack() as pd:
                for name in debug:
                    if name not in ("att_d", "hy_d", "U_d", "H_d0"):
                        continue
                    src_, shp = {"att_d": (att_d, [512, NTOK]), "hy_d": (hy_d, [512, NTOK]), "U_d": (U_d[0:512, :], [512, L]),
                                 "H_d0": (H_d[0:4, :, 0:4].rearrange("a p k r c -> (a p) (k r c)"), [512, 2048])}[name]
                    dbo = nc.dram_tensor("dbg_" + name, shp, BF16, kind="ExternalOutput").ap()
                    dbs = sb("dbs_" + name, [128, 4, shp[1]], BF16, pd)
                    dma(dbs[:], src_.rearrange("(a p) t -> p a t", p=128), [name], ["dbs_" + name])
                    dma(dbo.rearrange("(a p) t -> p a t", p=128), dbs[:], ["dbs_" + name], ["dbg_" + name])
                S.barrier()
        S.barrier()
    return nc, S


def _core_inputs(inp, core):
    T = _tables()
    b, r = core // 4, core % 4
    f = lambda a: np.ascontiguousarray(np.asarray(a, dtype=np.float32))
    x = np.asarray(inp["x"], dtype=np.float32)
    xT = np.ascontiguousarray(x[b].T)
    tq = slice(r * NTOK, (r + 1) * NTOK)
    qm = np.zeros((128, 4), np.float32)
    qm[:, r] = 1.0
    m = {
        "xT": xT, "xTq": np.ascontiguousarray(xT[:, tq]),
        "w_in": f(inp["w_in"][0]), "w_out": f(inp["w_out"][0]), "w_mi": f(inp["w_mlp_in"][0]), "w_mo": f(inp["w_mlp_out"][0]),
        "g1": f(np.asarray(inp["norm1_g"][0]).reshape(8, 128).T), "g2": f(np.asarray(inp["norm2_g"][0]).reshape(8, 128).T),
        "gf": f(np.asarray(inp["final_g"]).reshape(8, 128).T),
        "gq": f(np.asarray(inp["q_norm_g"][0]).reshape(64, 1)), "gk": f(np.asarray(inp["k_norm_g"][0]).reshape(64, 1)),
        "ga": f(np.asarray(inp["attn_out_g"][0]).reshape(8, 64).T), "gh": f(np.asarray(inp["hy_out_g"][0]).reshape(4, 128).T),
        "cw": f(np.asarray(inp["hy_conv_w"][0]).T.reshape(12, 128, 3).transpose(1, 0, 2)),
        "cb": f(np.asarray(inp["hy_conv_b"][0]).reshape(12, 128).T),
        "fw1": f(inp["filt_w1"][0]), "fw2": f(inp["filt_w2"][0]), "fw3": f(inp["filt_w3"][0]), "fw4": f(inp["filt_w4"][0]),
        "fb": f(np.stack([np.asarray(inp["filt_b1"][0]), np.asarray(inp["filt_b2"][0]), np.asarray(inp["filt_b3"][0])], axis=1)),
        "ffreq": f(np.asarray(inp["filt_freq"][0]).reshape(64, 1)),
        "fdel": f(np.asarray(inp["filt_deltas"][0]).reshape(16, 128).T),
        "skp": f(np.asarray(inp["hy_skip_d"][0]).reshape(2, 4, 128).transpose(2, 0, 1).reshape(128, 8)),
        "qmask": qm,
        "c_cos": T["cosT"], "c_sin": T["sinT"],
        "c_cosq": np.ascontiguousarray(T["cosT"][:, tq]), "c_sinq": np.ascontiguousarray(T["sinT"][:, tq]),
        "c_zT": np.ascontiguousarray(np.stack([T["zT"], T["zrT"]])), "c_t": np.ascontiguousarray(np.stack([T["t"], T["trev"]])),
        "c_tb": T["tb"], "c_jt": T["jt"],
        "c_fpack": T["fpack"], "c_G": T["G"], "c_finv": T["finv"], "c_M2": T["M2"],
        "c_M2q": np.ascontiguousarray(T["M2"][:, :, r * 16:(r + 1) * 16]),
        "c_rot": T["rot"], "c_sel": T["sel"], "c_ones": T["ones"], "c_ident": T["ident"],
    }
    return m


def _emit(nc, S):
    from contextlib import ExitStack
    st = ExitStack()
    sems = {e: [st.enter_context(nc.semaphore(f"s_{e}{i}")) for i in range(4)] for e in S.CE}
    dsems = [st.enter_context(nc.semaphore(f"d{i}")) for i in range(S.NDS)]
    info = S.emit(sems, dsems)
    return st, info


def kernel(**inputs):
    nc, S = build_program()
    st, _ = _emit(nc, S)
    with st:
        in_maps = [_core_inputs(inputs, c) for c in range(8)]
        res = run_bass_kernel_spmd(nc, in_maps, core_ids=list(range(8)))
    out = np.empty((NB, L, D), np.float32)
    for c in range(8):
        b, r = c // 4, c % 4
        out[b, r * NTOK:(r + 1) * NTOK, :] = np.asarray(res.results[c]["outT"]).T
    return out
```

```python
import math
import numpy as np
import ml_dtypes
import concourse.bass as bass
import concourse.mybir as mybir
from concourse.bass_utils import run_bass_kernel_spmd

F32 = mybir.dt.float32
BF16 = mybir.dt.bfloat16
ALU = mybir.AluOpType
AF = mybir.ActivationFunctionType
bf16_np = ml_dtypes.bfloat16

D = 1024
L = 8192
NB = 2
NTOK = 2048
HD = 64
NQH = 8
NKV = 2
HYW = 512
DFF = 4096
EPS = 1e-6
NFFT = 2 * L
TWO_PI = 2.0 * math.pi
MAGIC = 12582912.0


class Op:
    __slots__ = ("eng", "fn", "reads", "writes", "dma", "idx", "gid", "waits", "signal", "clock",
                 "sem", "val")


class Sched:
    CE = ("pe", "act", "dve", "pool", "sp")
    EPOCH = 12000
    NDS = 40

    def __init__(self, nc):
        self.nc = nc
        self.E = {"pe": nc.tensor, "act": nc.scalar, "dve": nc.vector, "pool": nc.gpsimd,
                  "sp": nc.sync}
        self.ops = []
        self.last_w = {}
        self.readers = {}
        self.known = {e: {} for e in self.CE}
        self.known_dma = {e: set() for e in self.CE}
        self.cnt = {e: 0 for e in self.CE}
        self.last_op = {e: None for e in self.CE}
        self.live_dma = []

    def op(self, eng, fn, reads=(), writes=(), dma=False):
        o = Op()
        o.eng, o.fn, o.reads, o.writes, o.dma = eng, fn, tuple(reads), tuple(writes), dma
        o.signal = dma
        o.sem = None
        o.val = None
        o.gid = len(self.ops)
        deps = {}
        for k in o.reads:
            w = self.last_w.get(k)
            if w is not None:
                deps[w.gid] = w
        for k in o.writes:
            w = self.last_w.get(k)
            if w is not None:
                deps[w.gid] = w
            for r in self.readers.get(k, ()):
                deps[r.gid] = r
        self._finish(o, deps.values())
        for k in o.writes:
            self.last_w[k] = o
            self.readers[k] = []
        for k in o.reads:
            lst = self.readers.setdefault(k, [])
            if not dma:
                lst[:] = [r for r in lst if r.dma or r.eng != eng]
            lst.append(o)
        return o

    def _finish(self, o, deps):
        eng = o.eng
        kn = self.known[eng]
        kd = self.known_dma[eng]
        waits = []
        best = {}
        rset = set(o.reads)
        for d in deps:
            if d.dma:
                if d.gid not in kd:
                    waits.append(d)
                    kd.add(d.gid)
                continue
            if d.eng == eng:
                if eng in ("pe", "sp"):
                    continue
            if kn.get(d.eng, -1) >= d.idx:
                continue
            b = best.get(d.eng)
            if b is None or b.idx < d.idx:
                best[d.eng] = d
        for d in best.values():
            waits.append(d)
            if kn.get(d.eng, -1) < d.idx:
                kn[d.eng] = d.idx
            for e2, i2 in d.clock.items():
                if kn.get(e2, -1) < i2:
                    kn[e2] = i2
        for d in waits:
            d.signal = True
        o.waits = waits
        o.idx = self.cnt[eng]
        self.cnt[eng] += 1
        o.clock = dict(kn)
        self.ops.append(o)
        if o.fn is not None:
            self.last_op[eng] = o
        if o.dma:
            self.live_dma.append(o)

    def barrier(self):
        lasts = [self.last_op[e] for e in self.CE if self.last_op[e] is not None]
        dmas = list(self.live_dma)
        for e in self.CE:
            o = Op()
            o.eng, o.fn, o.reads, o.writes, o.dma = e, None, (), (), False
            o.signal = False
            o.sem = None
            o.val = None
            o.gid = len(self.ops)
            deps = [d for d in lasts if not d.dma and d.eng != e] + dmas
            self._finish(o, deps)
        self.live_dma = []

    def emit(self, sems, dsems):
        sig = {e: 0 for e in self.CE}
        nd = 0
        for o in self.ops:
            E = self.E[o.eng]
            for d in o.waits:
                E.wait_ge(d.sem, d.val)
            if o.fn is None:
                continue
            if o.dma:
                s = dsems[nd % self.NDS]
                rnd = nd // self.NDS
                if rnd > 0:
                    E.wait_ge(s, 16 * rnd)
                ins = o.fn()
                ins.then_inc(s, 16)
                o.sem, o.val = s, 16 * (rnd + 1)
                nd += 1
            else:
                ins = o.fn()
                if o.signal:
                    c = sig[o.eng]
                    o.sem = sems[o.eng][c // self.EPOCH]
                    o.val = c % self.EPOCH + 1
                    ins.then_inc(o.sem, 1)
                    sig[o.eng] = c + 1
        return sig, nd


_TABLES = None


def _rope_tables():
    rows = L // 64
    row = np.repeat(np.arange(rows, dtype=np.float32), 64)
    col = np.tile(np.arange(64, dtype=np.float32), rows)
    half = HD // 2
    inv_freq = (np.float32(10000.0) ** (-np.arange(0, half, 2, dtype=np.float32) / np.float32(half))).astype(np.float32)
    ang_r = (row[:, None] * inv_freq[None, :]).astype(np.float32)
    ang_c = (col[:, None] * inv_freq[None, :]).astype(np.float32)
    cos_r, sin_r = np.cos(ang_r), np.sin(ang_r)
    cos_c, sin_c = np.cos(ang_c), np.sin(ang_c)
    cosT = np.concatenate([cos_r.T, cos_r.T, cos_c.T, cos_c.T], axis=0).astype(np.float32)
    sinT = np.concatenate([sin_r.T, sin_r.T, sin_c.T, sin_c.T], axis=0).astype(np.float32)
    return np.ascontiguousarray(cosT), np.ascontiguousarray(sinT)


def _filter_pos_tables():
    f32 = np.float32
    t = np.linspace(0.0, 1.0, L, dtype=f32)
    bands = 16
    w_ang = (f32(2.0 * math.pi) * np.arange(L, dtype=f32) / f32(L)).astype(f32)
    band_f = np.linspace(1e-4, bands - 1, bands, dtype=f32)
    ang = (w_ang[:, None] * band_f[None, :]).astype(f32)
    z = np.concatenate([t[:, None], np.cos(ang), -np.sin(ang)], axis=-1).astype(f32)
    zT = np.ascontiguousarray(z.T)
    idx = (L - np.arange(L)) % L
    zrT = np.ascontiguousarray(z[idx].T)
    trev = t[idx].copy()
    return zT, zrT, t.copy(), trev


def _fft_tables():
    N = NFFT
    n1 = np.arange(128, dtype=np.float64)
    k1 = np.arange(64, dtype=np.float64)
    th = 2 * np.pi * np.outer(n1, k1 + 0.5) / 128.0
    fpack = np.concatenate([np.cos(th), -np.sin(th), np.sin(th)], axis=1)
    n2 = np.arange(128, dtype=np.float64)
    k2 = np.arange(128, dtype=np.float64)
    kap = k1[None, :, None] + 128.0 * k2[None, None, :] + 0.5
    thg = 2 * np.pi * n2[:, None, None] * kap / N
    G = np.stack([np.cos(thg), -np.sin(thg)], axis=2)
    thf = 2 * np.pi * np.outer(k2, n2) / 128.0
    finv = np.stack([np.cos(thf), np.sin(thf), -np.sin(thf), -np.cos(thf)], axis=1)
    n1h = np.arange(64, dtype=np.float64)
    phi = 2 * np.pi * (k1[:, None, None] + 0.5) * (128.0 * n1h[None, None, :] + n2[None, :, None]) / N
    M2 = np.concatenate([(2.0 / N) * np.cos(phi), -(2.0 / N) * np.sin(phi)], axis=0)
    return (fpack.astype(bf16_np), G.astype(bf16_np), finv.astype(bf16_np), M2.astype(bf16_np))


def _tables():
    global _TABLES
    if _TABLES is None:
        cosT, sinT = _rope_tables()
        zT, zrT, t, trev = _filter_pos_tables()
        fpack, G, finv, M2 = _fft_tables()
        ii = np.arange(512, dtype=np.float64)
        jj = np.arange(16, dtype=np.float64)
        tbm = np.stack([ii / (L - 1), (L - ii) / (L - 1)]).astype(np.float32)
        jtm = np.stack([512.0 * jj / (L - 1), -512.0 * jj / (L - 1)]).astype(np.float32)
        rot = np.zeros((64, 64), np.float32)
        for m in range(64):
            if (m % 32) < 16:
                rot[m + 16, m] = -1.0
            else:
                rot[m - 16, m] = 1.0
        sel = np.zeros((65, 64), np.float32)
        sel[64, :] = 1.0
        _TABLES = dict(cosT=cosT, sinT=sinT, zT=zT, zrT=zrT, t=t, trev=trev, fpack=fpack, G=G,
                       finv=finv, M2=M2, rot=rot, sel=sel, tb=tbm, jt=jtm,
                       ones=np.ones((128, 128), np.float32),
                       ident=np.eye(128, dtype=np.float32).astype(bf16_np))
    return _TABLES


def build_program(phases=(0, 1, 2, 3, 4), debug=()):
    nc = bass.Bass("TRN2", target_bir_lowering=False)
    S = Sched(nc)

    def din(name, shape, dt=F32):
        return nc.dram_tensor(name, list(shape), dt, kind="ExternalInput").ap()

    def dscr(name, shape, dt):
        return nc.dram_tensor(name, list(shape), dt, kind="Internal").ap()

    xT = din("xT", [D, L])
    xTq = din("xTq", [D, NTOK])
    w_in = din("w_in", [D, 2304])
    w_out = din("w_out", [D, D])
    w_mi = din("w_mi", [D, DFF])
    w_mo = din("w_mo", [DFF, D])
    g1 = din("g1", [128, 8]); g2 = din("g2", [128, 8]); gf = din("gf", [128, 8])
    gq = din("gq", [64, 1]); gk = din("gk", [64, 1])
    ga = din("ga", [64, 8]); gh = din("gh", [128, 4])
    cw = din("cw", [128, 12, 3]); cb = din("cb", [128, 12])
    fw1 = din("fw1", [33, 64]); fw2 = din("fw2", [64, 64]); fw3 = din("fw3", [64, 64])
    fw4 = din("fw4", [64, 2048])
    fb = din("fb", [64, 3]); ffreq = din("ffreq", [64, 1])
    fdel = din("fdel", [128, 16])
    skp = din("skp", [128, 8])
    qmask = din("qmask", [128, 4])
    c_cos = din("c_cos", [64, L]); c_sin = din("c_sin", [64, L])
    c_cosq = din("c_cosq", [64, NTOK]); c_sinq = din("c_sinq", [64, NTOK])
    c_zT = din("c_zT", [2, 33, L])
    c_t = din("c_t", [2, L])
    c_tb = din("c_tb", [2, 512])
    c_jt = din("c_jt", [2, 16])
    c_fpack = din("c_fpack", [128, 192], BF16)
    c_G = din("c_G", [128, 64, 2, 128], BF16)
    c_finv = din("c_finv", [128, 4, 128], BF16)
    c_M2 = din("c_M2", [128, 128, 64], BF16)
    c_M2q = din("c_M2q", [128, 128, 16], BF16)
    c_rot = din("c_rot", [64, 64]); c_sel = din("c_sel", [65, 64]); c_ones = din("c_ones", [128, 128])
    c_ident = din("c_ident", [128, 128], BF16)
    outT = nc.dram_tensor("outT", [D, NTOK], F32, kind="ExternalOutput").ap()

    U_d = dscr("U_d", [1536, L], BF16)
    H_d = dscr("H_d", [8, 128, 64, 4, 128], BF16)
    taps_d = dscr("taps_d", [128, NFFT], BF16)
    zcm_d = dscr("zcm_d", [128, L], BF16)
    att_d = dscr("att_d", [512, NTOK], BF16)
    hy_d = dscr("hy_d", [512, NTOK], BF16)
    Wmi_d = dscr("Wmi_d", [D, DFF], BF16)
    Wmo_d = dscr("Wmo_d", [8, 128, 32, 128], BF16)

    from contextlib import ExitStack
    es = ExitStack()

    _names = {}

    def sb(name, shape, dt, stack=None):
        n_ = _names.get(name, 0)
        _names[name] = n_ + 1
        nm = name if n_ == 0 else f"{name}_r{n_}"
        return (stack or es).enter_context(nc.sbuf_tensor(nm, list(shape), dt))

    def dma(out, in_, reads, writes, eng="sp"):
        return S.op(eng, lambda: S.E[eng].dma_start(out=out, in_=in_), reads, writes, dma=True)

    def mm(out, lhsT, rhs, start, stop, reads, writes):
        return S.op("pe", lambda: nc.tensor.matmul(out, lhsT, rhs, start=start, stop=stop), reads, writes)

    rr = {"i": 0}

    def alt(engs=("dve", "pool")):
        rr["i"] += 1
        return engs[rr["i"] % len(engs)]

    def V(eng):
        return S.E[eng]

    with es:
        PSW = [es.enter_context(nc.psum_tensor(f"psw{i}", [128, 1024], F32)) for i in range(3)]
        PS = [PSW[i // 2][:, (i % 2) * 512:(i % 2 + 1) * 512] for i in range(6)]
        PSB = [es.enter_context(nc.psum_tensor(f"psb{i}", [128, 1024], BF16)) for i in range(2)]
        PSX = [PSB[i].bitcast(F32)[:, :] for i in range(2)]

        ones = sb("ones", [128, 128], F32)
        ident = sb("ident", [128, 128], BF16)
        rot = sb("rot", [64, 64], F32)
        sel = sb("sel", [65, 64], F32)
        epsT = sb("epsT", [128, 1], F32)
        onesbf = sb("onesbf", [128, 128], BF16)
        g1s = sb("g1s", [128, 8], F32); g2s = sb("g2s", [128, 8], F32); gfs = sb("gfs", [128, 8], F32)
        gqs = sb("gqs", [64, 1], F32); gks = sb("gks", [64, 1], F32)
        gas = sb("gas", [64, 8], F32); ghs = sb("ghs", [128, 4], F32)
        cws = sb("cws", [128, 12, 3], F32); cbs = sb("cbs", [128, 12], F32)
        skps = sb("skps", [128, 8], F32); qms = sb("qms", [128, 4], F32)
        for t_, d_ in ((ones, c_ones), (ident, c_ident), (rot, c_rot), (sel, c_sel), (g1s, g1), (g2s, g2),
                       (gfs, gf), (gqs, gq), (gks, gk), (gas, ga), (ghs, gh), (cws, cw), (cbs, cb),
                       (skps, skp), (qms, qmask)):
            dma(t_[:], d_, [], [t_.name if hasattr(t_, "name") else id(t_)])
        KEY = lambda t_: t_.name if hasattr(t_, "name") else id(t_)
        S.op("dve", lambda: nc.vector.memset(epsT[:], EPS), [], ["epsT"])
        S.op("pool", lambda: nc.gpsimd.memset(onesbf[:], 1.0), [], ["onesbf"])

        FC = {}
        SNAP = {}

        def snap(name, src_ap, key, ncols):
            if name not in debug or name in SNAP:
                return
            SNAP[name] = nc.dram_tensor("dbg_" + name, [128, ncols], BF16, kind="ExternalOutput").ap()
            dma(SNAP[name], src_ap, [key], ["dbg_" + name])


        def load_fft_consts(stack):
            FC["fpack"] = sb("fpack", [128, 192], BF16, stack)
            FC["finv"] = sb("finv", [128, 4, 128], BF16, stack)
            FC["Gq"] = [sb(f"Gq{i}", [128, 8, 2, 128], BF16, stack) for i in range(2)]
            dma(FC["fpack"][:], c_fpack, [], ["fpack"])
            dma(FC["finv"][:], c_finv, [], ["finv"])

        def rsqrt(out, in_, scale, reads, writes, np_=128):
            S.op("act", lambda: nc.scalar.activation(out, in_, AF.Sqrt, bias=epsT[0:np_, :], scale=scale),
                 list(reads) + ["epsT"], writes)
            S.op("dve", lambda: nc.vector.reciprocal(out, out), writes, writes)

        def fft_forward(ztm, K, AT, cb_batch, zkey="ztm", akey="AT", pre_batch=None):
            fpack = FC["fpack"]
            for c2 in range(64):
                ps = PS[c2 % 2]
                pk = f"ps{c2 % 2}"
                for cc in range(2):
                    c = c2 * 2 + cc
                    mm(ps[:, cc * 192:(cc + 1) * 192], ztm[0:K, c, :], fpack[0:K, :], True, True,
                       [(zkey, c // 32), "fpack"], [pk])
                e = alt(("act", "dve"))
                src = ps[:, 0:384].rearrange("p (cc j) -> p j cc", cc=2)
                dst = AT[:, :, c2 * 2:c2 * 2 + 2]
                if e == "act":
                    S.op("act", lambda dst=dst, src=src: nc.scalar.copy(dst, src), [pk], [akey])
                else:
                    S.op("dve", lambda dst=dst, src=src: nc.vector.tensor_copy(dst, src), [pk], [akey])

            def load_g(q8):
                dma(FC["Gq"][q8 % 2][:], c_G[:, q8 * 8:(q8 + 1) * 8], [], [f"Gq{q8 % 2}"])

            load_g(0)
            for kb in range(32):
                q4 = kb // 4
                Gq = FC["Gq"][q4 % 2]
                gkey = f"Gq{q4 % 2}"
                if kb % 4 == 0 and q4 + 1 < 8:
                    load_g(q4 + 1)
                if pre_batch is not None:
                    pre_batch(kb)
                ps = PS[2 + kb % 2]
                pk = f"ps{2 + kb % 2}"
                for kk in range(2):
                    k1 = kb * 2 + kk
                    kl = k1 % 8
                    R = AT[:, k1, :]
                    I = AT[:, 64 + k1, :]
                    NI = AT[:, 128 + k1, :]
                    zr = ps[:, kk * 256:kk * 256 + 128]
                    zi = ps[:, kk * 256 + 128:kk * 256 + 256]
                    mm(zr, Gq[:, kl, 0, :], R, True, False, [akey, gkey], [pk])
                    mm(zr, Gq[:, kl, 1, :], NI, False, True, [akey, gkey], [pk])
                    mm(zi, Gq[:, kl, 1, :], R, True, False, [akey, gkey], [pk])
                    mm(zi, Gq[:, kl, 0, :], I, False, True, [akey, gkey], [pk])
                cb_batch(kb, ps, pk)

        cast_jobs = []
        for cbk in range(8):
            cast_jobs.append((w_mi.rearrange("(k p) n -> p k n", p=128)[:, :, cbk * 512:(cbk + 1) * 512],
                              Wmi_d.rearrange("(k p) n -> p k n", p=128)[:, :, cbk * 512:(cbk + 1) * 512], None))
        for kr in range(4):
            for cbk in range(2):
                cast_jobs.append((w_mo.rearrange("(k p) n -> p k n", p=128)[:, kr * 8:(kr + 1) * 8, cbk * 512:(cbk + 1) * 512],
                                  None, (kr, cbk)))

        def emit_cast(n, wst, wbf):
            src, dst, tl = cast_jobs[n]
            s_ = n % 2
            dma(wst[s_][:], src, [], [f"wst{s_}"])
            S.op("dve", lambda s_=s_: nc.vector.tensor_copy(wbf[s_][:], wst[s_][:]), [f"wst{s_}"], [f"wbf{s_}"])
            if tl is None:
                dma(dst, wbf[s_][:], [f"wbf{s_}"], ["Wmid"])
            else:
                kr_, cbk_ = tl
                for c_ in range(4):
                    dma(Wmo_d[cbk_ * 4 + c_, :, kr_ * 8:(kr_ + 1) * 8, :], wbf[s_][:, :, c_ * 128:(c_ + 1) * 128], [f"wbf{s_}"], ["Wmod"])

        def phase0():
          with ExitStack() as p0:
            load_fft_consts(p0)

            w1s = sb("w1s", [33, 64], F32, p0); w2s = sb("w2s", [64, 64], F32, p0); w3s = sb("w3s", [64, 64], F32, p0)
            w4b = sb("w4b", [64, 2048], BF16, p0)
            fbs = sb("fbs", [64, 3], F32, p0); frs = sb("frs", [64, 1], F32, p0); ffb = sb("ffb", [64, 3], F32, p0)
            dls = sb("dls", [128, 16], F32, p0); ndl = sb("ndl", [128, 16], F32, p0)
            negpi = sb("negpi", [128, 1], F32, p0)
            H3 = sb("H3", [64, 2, L], BF16, p0)
            p0m = ExitStack()
            w4s = sb("w4s", [64, 2048], F32, p0m)
            for t_, d_, k_ in ((w1s, fw1, "w1s"), (w2s, fw2, "w2s"), (w3s, fw3, "w3s"), (w4s, fw4, "w4s"),
                               (fbs, fb, "fbs"), (frs, ffreq, "frs"), (dls, fdel, "dls")):
                dma(t_[:], d_, [], [k_])
            S.op("dve", lambda: nc.vector.tensor_copy(w4b[:], w4s[:]), ["w4s"], ["w4b"])
            S.op("dve", lambda: nc.vector.tensor_scalar(ffb[:], fbs[:], frs[:, 0:1], None, ALU.mult), ["fbs", "frs"], ["ffb"])
            S.op("dve", lambda: nc.vector.tensor_scalar(ndl[:], dls[:], -1.0, None, ALU.mult), ["dls"], ["ndl"])
            S.op("dve", lambda: nc.vector.tensor_tensor(ndl[:], ndl[:], dls[:], ALU.min), ["dls", "ndl"], ["ndl"])
            S.op("dve", lambda: nc.vector.memset(negpi[:], 0.0), [], ["negpi"])
            zc = [sb(f"zc{i}", [33, 512], F32, p0m) for i in range(4)]
            faL = [sb(f"fa{i}", [64, 512], F32, p0m) for i in range(2)]
            fbufL = [sb(f"fbuf{i}", [64, 512], F32, p0m) for i in range(2)]
            fhL = [sb(f"fh{i}", [64, 512], F32, p0m) for i in range(2)]

            def sin_layer(ps, li, dst, pk, v_, dkey):
                fa, fbuf = faL[v_], fbufL[v_]
                fak, fbk = f"fa{v_}", f"fbuf{v_}"
                S.op("dve", lambda: nc.vector.tensor_scalar(fa[:], ps[0:64, :], frs[:, 0:1], ffb[:, li:li + 1], ALU.mult, ALU.add),
                     [pk, "frs", "ffb"], [fak])
                S.op("dve", lambda: nc.vector.tensor_scalar(fbuf[:], fa[:], 1.0 / TWO_PI, MAGIC, ALU.mult, ALU.add), [fak], [fbk])
                S.op("dve", lambda: nc.vector.tensor_scalar(fbuf[:], fbuf[:], -MAGIC, -TWO_PI, ALU.add, ALU.mult), [fbk], [fbk])
                S.op("dve", lambda: nc.vector.tensor_tensor(fa[:], fa[:], fbuf[:], ALU.add), [fak, fbk], [fak])
                S.op("act", lambda: nc.scalar.activation(dst, fa[:], AF.Sin, bias=negpi[0:64, :], scale=1.0), [fak, "negpi"], [dkey])

            for j in range(16):
                for var in range(2):
                    s_ = (j % 2) * 2 + var
                    psm = PS[4 + var]
                    pmk = f"ps{4 + var}"
                    fh = fhL[var]
                    fhk = f"fh{var}"
                    dma(zc[s_][:], c_zT[var, :, j * 512:(j + 1) * 512], [], [f"zc{s_}"])
                    mm(psm[0:64, :], w1s[:], zc[s_][:], True, True, [f"zc{s_}", "w1s"], [pmk])
                    sin_layer(psm, 0, fh[:], pmk, var, fhk)
                    mm(psm[0:64, :], w2s[:], fh[:], True, True, [fhk, "w2s"], [pmk])
                    sin_layer(psm, 1, fh[:], pmk, var, fhk)
                    mm(psm[0:64, :], w3s[:], fh[:], True, True, [fhk, "w3s"], [pmk])
                    sin_layer(psm, 2, H3[:, var, j * 512:(j + 1) * 512], pmk, var, ("H3", var))

            S.barrier()
            p0m.close()
            tapsUL = [sb(f"tapsU{i}", [128, NFFT], BF16, p0) for i in range(2)]
            ztmf = sb("ztmf", [128, 128, 128], BF16, p0)
            ATf = sb("ATf", [128, 192, 128], BF16, p0)
            tb = sb("tb", [128, 2, 512], F32, p0)
            jt = sb("jt", [128, 2, 16], F32, p0)
            for hf in range(2):
                dma(tb[:, hf, :], c_tb[hf:hf + 1, :].broadcast_to([128, 512]), [], ["tb"])
                dma(jt[:, hf, :], c_jt[hf:hf + 1, :].broadcast_to([128, 16]), [], ["jt"])
            wbase1 = sb("wbase", [128, 512], F32, p0)
            wbase = [wbase1, wbase1]
            wcj = [sb(f"wcj{i}", [128, 16], F32, p0) for i in range(2)]
            tpf = [sb(f"tpf{i}", [128, 512], F32, p0) for i in range(2)]
            psum_ = sb("psum_", [128, 32], F32, p0)
            nrm = sb("nrm", [128, 2], F32, p0)
            Hst = [sb(f"Hst{i}", [128, 2, 4, 128], BF16, p0) for i in range(2)]

            def taps_chunk(go, cidx):
                g, o = go // 2, go % 2
                half, j = cidx // 16, cidx % 16
                tU = tapsUL[go % 2]
                tk = f"tapsU{go % 2}"
                col0 = o * 1024 + half * 512 + g * 128
                ct = col0 // 128
                if j == 0:
                    S.op("act", lambda: nc.scalar.activation(wbase[half][:], tb[:, half, :], AF.Exp, scale=ndl[:, ct:ct + 1]),
                         ["tb", "ndl"], ["wbase"])
                    S.op("act", lambda: nc.scalar.activation(wcj[half][:], jt[:, half, :], AF.Exp, scale=ndl[:, ct:ct + 1]),
                         ["jt", "ndl"], [f"wcj{half}"])
                s_ = j % 2
                pst = PS[4 + s_]
                ptk = f"ps{4 + s_}"
                mm(pst[:, :], w4b[:, col0:col0 + 128], H3[:, half, j * 512:(j + 1) * 512], True, True, [("H3", half), "w4b"], [ptk])
                S.op("dve", lambda: nc.vector.scalar_tensor_tensor(out=tpf[s_][:], in0=pst[:, :], scalar=wcj[half][:, j:j + 1], in1=wbase[half][:],
                                                                  op0=ALU.mult, op1=ALU.mult), [ptk, f"wcj{half}", "wbase"], [f"tpf{s_}"])
                if half == 1 and j == 0:
                    S.op("dve", lambda: nc.vector.memset(tpf[s_][:, 0:1], 0.0), [f"tpf{s_}"], [f"tpf{s_}"])
                S.op("dve", lambda: nc.vector.tensor_reduce(psum_[:, cidx:cidx + 1], tpf[s_][:], mybir.AxisListType.X, ALU.add,
                                                            apply_absolute_value=True), [f"tpf{s_}"], ["psum_"])
                S.op("act", lambda: nc.scalar.copy(tU[:, half * L + j * 512: half * L + (j + 1) * 512], tpf[s_][:]), [f"tpf{s_}"], [tk])

            def taps_finish(go):
                tU = tapsUL[go % 2]
                tk = f"tapsU{go % 2}"
                S.op("dve", lambda: nc.vector.tensor_reduce(nrm[:, 0:1], psum_[:], mybir.AxisListType.X, ALU.add), ["psum_"], ["nrm"])
                S.op("dve", lambda: nc.vector.reciprocal(nrm[:, 0:1], nrm[:, 0:1]), ["nrm"], ["nrm"])
                S.op("dve", lambda: nc.vector.tensor_scalar(nrm[:, 1:2], nrm[:, 0:1], -1.0, None, ALU.mult), ["nrm"], ["nrm"])
                S.op("dve", lambda: nc.vector.tensor_scalar(tU[:, 0:L], tU[:, 0:L], nrm[:, 0:1], None, ALU.mult), [tk, "nrm"], [tk])
                S.op("act", lambda: nc.scalar.mul(tU[:, L:NFFT], tU[:, L:NFFT], nrm[:, 1:2]), [tk, "nrm"], [tk])
                tv = taps_d.rearrange("c (n1 n2) -> n1 c n2", n2=128)
                for q4 in range(4):
                    dma(taps_d[q4 * 32:(q4 + 1) * 32, :], tU[q4 * 32:(q4 + 1) * 32, :], [tk], [("taps_d", q4)])
                for q4 in range(4):
                    dma(ztmf[:, q4 * 32:(q4 + 1) * 32, :], tv[:, q4 * 32:(q4 + 1) * 32, :], [("taps_d", q4)], [("ztm", q4)])

            for cidx in range(32):
                taps_chunk(0, cidx)
            taps_finish(0)
            for go in range(8):
                def cb_filter(kb, ps, pk, go=go):
                    s_ = kb % 2
                    zv = ps[:, :].rearrange("p (k r c) -> p k r c", k=2, r=2)
                    S.op("act", lambda: nc.scalar.copy(Hst[s_][:, :, 0:2, :], zv), [pk], [f"Hst{s_}"])
                    S.op("dve", lambda: nc.vector.tensor_copy(Hst[s_][:, :, 2, :], zv[:, :, 1, :]), [pk], [f"Hst{s_}"])
                    S.op("dve", lambda: nc.vector.tensor_copy(Hst[s_][:, :, 3, :], zv[:, :, 0, :]), [pk], [f"Hst{s_}"])
                    dma(H_d[go, :, kb * 2:kb * 2 + 2], Hst[s_][:], [f"Hst{s_}"], [("H_d", go)])

                def pre_filter(kb, go=go):
                    if go + 1 < 8:
                        if kb < 16:
                            taps_chunk(go + 1, 2 * kb)
                            taps_chunk(go + 1, 2 * kb + 1)
                        elif kb == 16:
                            taps_finish(go + 1)

                fft_forward(ztmf, 128, ATf, cb_filter, pre_batch=pre_filter)
            S.barrier()
        def rms_apply(src, gvec, dst, scale, nk, tag, sqb, rs, np_=128, ss_ps=0, sqk="sqb", rsk="rs"):
            rms_a(src, nk, tag, sqb, np_, sqk)
            rms_b(src, gvec, dst, scale, nk, tag, sqb, rs, np_, ss_ps, sqk, rsk)

        def rms_a(src, nk, tag, sqb, np_=128, sqk="sqb"):
            S.op("act", lambda: nc.scalar.activation(sqb[0:np_, 0:nk, :], src, AF.Square), [tag + "_src"], [sqk])

        def rms_b(src, gvec, dst, scale, nk, tag, sqb, rs, np_=128, ss_ps=0, sqk="sqb", rsk="rs"):
            for k in range(nk):
                mm(PS[ss_ps][0:np_, :], onesbf[0:np_, 0:np_], sqb[0:np_, k, :], k == 0, k == nk - 1, [sqk, "onesbf"], [f"ps{ss_ps}"])
            S.op("act", lambda: nc.scalar.activation(rs[0:np_, :], PS[ss_ps][0:np_, :], AF.Sqrt, bias=epsT[0:np_, :], scale=scale),
                 [f"ps{ss_ps}", "epsT"], [rsk])
            S.op("dve", lambda: nc.vector.reciprocal(rs[0:np_, :], rs[0:np_, :]), [rsk], [rsk])
            for k in range(nk):
                S.op("dve", lambda k=k: nc.vector.scalar_tensor_tensor(out=dst[:, k, :], in0=src[:, k, :], scalar=gvec[:, k:k + 1],
                                                                      in1=rs[0:np_, :], op0=ALU.mult, op1=ALU.mult),
                     [tag + "_src", rsk, "gvecs"], [tag + "_dst"])

        def phase12():
          with ExitStack() as p12:
            KT = sb("KT", [128, 2, L], BF16, p12)
            Vx = sb("Vx", [128, 64, 2, 64], BF16, p12)
            QT = sb("QT", [128, 4, NTOK], BF16, p12)
            onesb = sb("onesb", [128, 64], BF16, p12)
            S.op("pool", lambda: nc.gpsimd.memset(onesb[:], 1.0), [], ["onesb"])
            with ExitStack() as p1:
                Wb = sb("Wb", [128, 8, 2304], BF16, p1)
                with ExitStack() as p1c:
                    wst = [sb(f"wsi{i}", [128, 8, 384], F32, p1c) for i in range(2)]
                    wv = w_in.rearrange("(k p) n -> p k n", p=128)
                    for bk in range(6):
                        s_ = bk % 2
                        dma(wst[s_][:], wv[:, :, bk * 384:(bk + 1) * 384], [], [f"wst{s_}"])
                        if bk % 2 == 0:
                            S.op("dve", lambda s_=s_, bk=bk: nc.vector.tensor_copy(Wb[:, :, bk * 384:(bk + 1) * 384], wst[s_][:]), [f"wst{s_}"], ["Wb"])
                        else:
                            S.op("act", lambda s_=s_, bk=bk: nc.scalar.copy(Wb[:, :, bk * 384:(bk + 1) * 384], wst[s_][:]), [f"wst{s_}"], ["Wb"])
                    S.barrier()
                xin = [sb(f"xin{i}", [128, 8, 512], F32, p1) for i in range(2)]
                sqbL = [sb(f"sqb{i}", [128, 8, 512], BF16, p1) for i in range(2)]
                rsL = [sb(f"rs{i}", [128, 512], F32, p1) for i in range(2)]
                aTL = [sb(f"aT{i}", [128, 8, 512], BF16, p1) for i in range(2)]
                ust = sb("ust", [128, 12, 512], BF16, p1)
                hsq = sb("hsq", [64, 512], BF16, p1); hrs = sb("hrs", [64, 512], F32, p1); hkn = sb("hkn", [64, 512], F32, p1)
                ht1 = sb("ht1", [64, 512], F32, p1); ht2 = sb("ht2", [64, 512], F32, p1)
                csb = [sb(f"csb{i}", [64, 2, 512], F32, p1) for i in range(2)]
                qstg = [sb(f"qstg{i}", [64, 512], BF16, p1) for i in range(2)]

                def hp1(srcb, gvec):
                    src = PS[srcb][0:64, :]
                    sk_ = f"ps{srcb}"
                    S.op("act", lambda: nc.scalar.activation(hsq[:], src, AF.Square), [sk_], ["hsq"])
                    mm(PS[3][0:64, :], onesbf[0:64, 0:64], hsq[:], True, True, ["hsq", "onesbf"], ["ps3"])
                    S.op("act", lambda: nc.scalar.activation(hrs[:], PS[3][0:64, :], AF.Sqrt, bias=epsT[0:64, :], scale=1.0 / 64.0),
                         ["ps3", "epsT"], ["hrs"])
                    S.op("dve", lambda: nc.vector.reciprocal(hrs[:], hrs[:]), ["hrs"], ["hrs"])
                    S.op("dve", lambda: nc.vector.scalar_tensor_tensor(out=hkn[:], in0=src, scalar=gvec[:, 0:1], in1=hrs[:],
                                                                      op0=ALU.mult, op1=ALU.mult), [sk_, "hrs", "gvecs"], ["hkn"])

                def hp2(cs, cskey, dst, dstkey):
                    mm(PS[3][0:64, :], rot[:], hkn[:], True, True, ["hkn", "rot"], ["ps3"])
                    S.op("pool", lambda: nc.gpsimd.tensor_tensor(ht1[:], hkn[:], cs[:, 0, :], ALU.mult), ["hkn", cskey], ["ht1"])
                    S.op("dve", lambda: nc.vector.tensor_tensor(ht2[:], PS[3][0:64, :], cs[:, 1, :], ALU.mult), ["ps3", cskey], ["ht2"])
                    S.op("pool", lambda: nc.gpsimd.tensor_tensor(dst, ht1[:], ht2[:], ALU.add), ["ht1", "ht2"], [dstkey])

                xv = xT.rearrange("(k p) t -> p k t", p=128)
                xqv = xTq.rearrange("(k p) t -> p k t", p=128)
                Uv = U_d.rearrange("(ct p) t -> p ct t", p=128)
                work = [("seq", j) for j in range(16)] + [("own", j) for j in range(4)]

                def norm_a(w):
                    kind, j = work[w]
                    s_ = w % 2
                    tsl = slice(j * 512, (j + 1) * 512)
                    srcv, cc, ss_ = (xv, c_cos, c_sin) if kind == "seq" else (xqv, c_cosq, c_sinq)
                    dma(xin[s_][:], srcv[:, :, tsl], [], [f"xin{s_}", f"n1{s_}_src"])
                    dma(csb[s_][:, 0, :], cc[:, tsl], [], [f"csb{s_}"])
                    dma(csb[s_][:, 1, :], ss_[:, tsl], [], [f"csb{s_}"])
                    rms_a(xin[s_][:], 8, f"n1{s_}", sqbL[s_], 128, f"sqb{s_}")

                def norm_b(w):
                    s_ = w % 2
                    rms_b(xin[s_][:], g1s, aTL[s_], 1.0 / D, 8, f"n1{s_}", sqbL[s_], rsL[s_], 128, 0, f"sqb{s_}", f"rs{s_}")

                def emit_proj(w, mid_hook):
                    kind, j = work[w]
                    s_ = w % 2
                    aT = aTL[s_]
                    ak = f"n1{s_}_dst"
                    cs, csk = csb[s_], f"csb{s_}"
                    tsl = slice(j * 512, (j + 1) * 512)
                    if kind == "seq":
                        for kvh in range(2):
                            for k in range(8):
                                mm(PS[1 + kvh][0:64, :], Wb[:, k, 512 + kvh * 64:512 + (kvh + 1) * 64], aT[:, k, :], k == 0, k == 7,
                                   ["Wb", ak], [f"ps{1 + kvh}"])
                        for tt in range(4):
                            for k in range(8):
                                mm(PS[4][:, tt * 128:(tt + 1) * 128], aT[:, k, tt * 128:(tt + 1) * 128], Wb[:, k, 640:768], k == 0, k == 7,
                                   ["Wb", ak], ["ps4"])
                        S.op("act", lambda j=j: nc.scalar.copy(Vx[:, j * 4:(j + 1) * 4, :, :],
                                                             PS[4][:, :].rearrange("p (t h d) -> p t h d", t=4, h=2)), ["ps4"], ["Vx"])
                        for ct in range(12):
                            pu, puk = (PS[5], "ps5") if ct % 2 == 0 else (PSX[0], "psb0")
                            for k in range(8):
                                mm(pu[:, :], Wb[:, k, 768 + ct * 128:768 + (ct + 1) * 128], aT[:, k, :], k == 0, k == 7, ["Wb", ak], [puk])
                            if ct % 2 == 0:
                                S.op("act", lambda ct=ct, pu=pu: nc.scalar.copy(ust[:, ct, :], pu[:, :]), [puk], ["ust"])
                            else:
                                S.op("dve", lambda ct=ct, pu=pu: nc.vector.tensor_copy(ust[:, ct, :], pu[:, :]), [puk], ["ust"])
                            if ct == 1:
                                hp1(1, gks)
                            if ct == 3:
                                hp2(cs, csk, KT[0:64, 0, tsl], "KT")
                                hp1(2, gks)
                            if ct == 5:
                                hp2(cs, csk, KT[0:64, 1, tsl], "KT")
                            if ct == 7:
                                mid_hook()
                        dma(Uv[:, :, tsl], ust[:], ["ust"], ["U_d"])
                    else:
                        def qdst(h):
                            if h % 2 == 0:
                                return QT[0:64, h // 2, tsl], "QT", None
                            qs_ = (h // 2) % 2
                            return qstg[qs_][:], f"qstg{qs_}", qs_

                        def qproj(h):
                            hb = 1 + h % 2
                            for k in range(8):
                                mm(PS[hb][0:64, :], Wb[:, k, h * 64:(h + 1) * 64], aT[:, k, :], k == 0, k == 7, ["Wb", ak], [f"ps{hb}"])

                        qproj(0)
                        for h in range(8):
                            hp1(1 + h % 2, gqs)
                            if h + 1 < 8:
                                qproj(h + 1)
                            if h == 3:
                                mid_hook()
                            d_, dk_, qs_ = qdst(h)
                            hp2(cs, csk, d_, dk_)
                            if qs_ is not None:
                                dma(QT[64:128, h // 2, tsl], qstg[qs_][:], [f"qstg{qs_}"], ["QT2"])

                norm_a(0)
                norm_b(0)
                for w in range(len(work)):
                    if w + 1 < len(work):
                        norm_a(w + 1)
                        emit_proj(w, lambda w=w: norm_b(w + 1))
                    else:
                        emit_proj(w, lambda: None)
                dma(KT[64:128, :, :], KT[0:64, :, :], ["KT"], ["KT2"])
                S.barrier()
            if 2 in phases:
              with ExitStack() as p2:
                pb = [sb(f"pb{i}", [128, 1024], BF16, p2) for i in range(2)]
                cwst = [sb(f"wst{i}", [128, 8, 512], F32, p2) for i in range(2)]
                cwbf = [sb(f"wbf{i}", [128, 8, 512], BF16, p2) for i in range(2)]
                rden = sb("rden", [128, 512], F32, p2)
                ast = [sb(f"ast{i}", [128, 512], BF16, p2) for i in range(2)]
                n_ = 0
                for qc in range(4):
                    qsl = slice(qc * 512, (qc + 1) * 512)
                    for hp in range(4):
                        kvh = hp // 2
                        emit_cast(qc * 4 + hp, cwst, cwbf)
                        for step in range(64 + 1):
                            if step < 64:
                                kt = step
                                sl = kt % 2
                                ksl = slice(kt * 128, (kt + 1) * 128)
                                mm(PS[2 * sl][:, :], KT[0:64, kvh, ksl], QT[0:64, hp, qsl], True, True, ["KT", "QT"], [f"ps{2 * sl}"])
                                mm(PS[2 * sl + 1][:, :], KT[64:128, kvh, ksl], QT[64:128, hp, qsl], True, True, ["KT2", "QT2"], [f"ps{2 * sl + 1}"])
                                S.op("act", lambda sl=sl: nc.scalar.activation(pb[sl][:], PSW[sl][:, :], AF.Exp, scale=0.125),
                                     [f"ps{2 * sl}", f"ps{2 * sl + 1}"], [f"pb{sl}"])
                            kt = step - 1
                            if kt >= 0:
                                sl = kt % 2
                                st_, sp_ = kt == 0, kt == 63
                                pA = pb[sl][:, 0:512]
                                pB = pb[sl][:, 512:1024]
                                vv = Vx[:, kt, kvh, :]
                                S.op("pe", lambda vv=vv, pA=pA, st_=st_, sp_=sp_: nc.tensor.matmul(PS[4][0:64, :], vv, pA, start=st_, stop=sp_, tile_position=(0, 0)),
                                     ["Vx", f"pb{sl}"], ["ps4"])
                                S.op("pe", lambda vv=vv, pB=pB, st_=st_, sp_=sp_: nc.tensor.matmul(PS[4][64:128, :], vv, pB, start=st_, stop=sp_, tile_position=(0, 64)),
                                     ["Vx", f"pb{sl}"], ["ps4"])
                                S.op("pe", lambda pA=pA, st_=st_, sp_=sp_: nc.tensor.matmul(PS[5][0:64, :], onesb[:], pA, start=st_, stop=sp_, tile_position=(0, 0)),
                                     ["onesb", f"pb{sl}"], ["ps5"])
                                S.op("pe", lambda pB=pB, st_=st_, sp_=sp_: nc.tensor.matmul(PS[5][64:128, :], onesb[:], pB, start=st_, stop=sp_, tile_position=(0, 64)),
                                     ["onesb", f"pb{sl}"], ["ps5"])
                        S.op("dve", lambda: nc.vector.reciprocal(rden[:], PS[5][:, :]), ["ps5"], ["rden"])
                        a_ = n_ % 2
                        S.op("dve", lambda a_=a_: nc.vector.tensor_tensor(ast[a_][:], PS[4][:, :], rden[:], ALU.mult), ["ps4", "rden"], [f"ast{a_}"])
                        dma(att_d[hp * 128:(hp + 1) * 128, qsl], ast[a_][:], [f"ast{a_}"], ["att_d"])
                        n_ += 1
                S.barrier()

        def phase3():
          with ExitStack() as p3:
            load_fft_consts(p3)
            finv = FC["finv"]
            M2s = sb("M2s", [128, 128, 64], BF16, p3)
            M2qs = sb("M2qs", [128, 128, 16], BF16, p3)
            dma(M2s[:], c_M2, [], ["M2s"])
            dma(M2qs[:], c_M2q, [], ["M2qs"])
            RA = sb("RA", [128, 192 * 128 + 8], BF16, p3)
            RB = sb("RB", [128, 128 * 128], BF16, p3)
            RC = sb("RC", [128, 128 * 128], BF16, p3)
            vt = sb("vt", [128, L], BF16, p3)
            x1t = sb("x1t", [128, L], BF16, p3)
            x2q = sb("x2q", [128, NTOK], BF16, p3)
            zzq = sb("zzq", [128, NTOK], BF16, p3)
            hyst = sb("hyst", [128, NTOK], BF16, p3)
            Hc = [sb(f"Hc{i}", [128, 2, 4, 128], BF16, p3) for i in range(4)]
            Pb = [sb(f"Pb{i}", [128, 2, 2, 128], BF16, p3) for i in range(2)]
            Qb = [sb(f"Qb{i}", [128, 2, 2, 128], BF16, p3) for i in range(2)]
            gt = [sb(f"gt{i}", [128, 512], F32, p3) for i in range(2)]
            whi = sb("whi", [128, 9], BF16, p3)
            whf = sb("whf", [128, 9], F32, p3)
            wlf = sb("wlf", [128, 9], F32, p3)
            sct = gt
            uraw = RA[:, 0:3 * 8194].rearrange("p (i t) -> p i t", i=3)
            AT = RA[:, 0:192 * 128].rearrange("p (j c) -> p j c", c=128)
            ztm = RB[:, :].rearrange("p (c n) -> p c n", n=128)
            Eb = RB[:, :].rearrange("p (n c) -> p n c", c=128)
            x2t = RC[:, 0:L]
            ET = RC[:, :].rearrange("p (c k) -> p c k", k=128)
            RAK, RBK, RCK = "RA", "RB", "RC"

            def masked_quarter(dst, src, dkey, skey):
                S.op("dve", lambda: nc.vector.tensor_scalar(dst[:], src[:, 0:NTOK], qms[:, 0:1], None, ALU.mult), [skey, "qms"], [dkey])
                for q in range(1, 4):
                    S.op("dve", lambda q=q: nc.vector.scalar_tensor_tensor(out=dst[:], in0=src[:, q * NTOK:(q + 1) * NTOK], scalar=qms[:, q:q + 1],
                                                                          in1=dst[:], op0=ALU.mult, op1=ALU.add), [skey, "qms", dkey], [dkey])

            RBQ = [(RBK, q4) for q4 in range(4)]

            def bounce_to_ztm(src, skey):
                zv = zcm_d.rearrange("c (n1 n2) -> n1 c n2", n2=128)
                for q4 in range(4):
                    dma(zcm_d[q4 * 32:(q4 + 1) * 32, :], src[q4 * 32:(q4 + 1) * 32, :], [skey], [("zcm_d", q4)])
                for q4 in range(4):
                    dma(ztm[0:64, q4 * 32:(q4 + 1) * 32, :], zv[:, q4 * 32:(q4 + 1) * 32, :], [("zcm_d", q4)], [(RBK, q4)])

            def conv_core(go):
                def load_h(kb):
                    dma(Hc[kb % 4][:], H_d[go, :, kb * 2:kb * 2 + 2], [("H_d", go)], [f"Hc{kb % 4}"])

                def pre(kb):
                    if kb == 0:
                        load_h(0)
                        load_h(1)
                    if kb + 2 < 32:
                        load_h(kb + 2)

                def cb(kb, ps, pk):
                    s_ = kb % 4
                    y_ = kb % 2
                    zv = ps[:, :].rearrange("p (k r c) -> p k r c", k=2, r=2)
                    S.op("dve", lambda: nc.vector.tensor_tensor(Pb[y_][:], zv, Hc[s_][:, :, 0:2, :], ALU.mult), [pk, f"Hc{s_}"], [f"Pb{y_}"])
                    S.op("dve", lambda: nc.vector.tensor_tensor(Qb[y_][:], zv, Hc[s_][:, :, 2:4, :], ALU.mult), [pk, f"Hc{s_}"], [f"Qb{y_}"])
                    pe_ = PS[4 + kb % 2]
                    pek = f"ps{4 + kb % 2}"
                    er = pe_[:, 0:256].rearrange("p (k c) -> p k c", k=2)
                    ei = pe_[:, 256:512].rearrange("p (k c) -> p k c", k=2)
                    p0, p1 = Pb[y_][:, :, 0, :], Pb[y_][:, :, 1, :]
                    q0, q1 = Qb[y_][:, :, 0, :], Qb[y_][:, :, 1, :]
                    rk = [f"Pb{y_}", f"Qb{y_}", "finv"]
                    FR, FI, NFI, NFR = finv[:, 0, :], finv[:, 1, :], finv[:, 2, :], finv[:, 3, :]
                    mm(er, FR, p0, True, False, rk, [pek]); mm(er, NFR, p1, False, False, rk, [pek])
                    mm(er, NFI, q0, False, False, rk, [pek]); mm(er, NFI, q1, False, True, rk, [pek])
                    mm(ei, FI, p0, True, False, rk, [pek]); mm(ei, NFI, p1, False, False, rk, [pek])
                    mm(ei, FR, q0, False, False, rk, [pek]); mm(ei, FR, q1, False, True, rk, [pek])
                    S.op("act", lambda: nc.scalar.copy(ET[:, :, kb * 2:kb * 2 + 2], pe_[:, 0:256].rearrange("p (k c) -> p c k", k=2)), [pek], [RCK])
                    S.op("act", lambda: nc.scalar.copy(ET[:, :, 64 + kb * 2:64 + kb * 2 + 2], pe_[:, 256:512].rearrange("p (k c) -> p c k", k=2)),
                         [pek], [RCK])
                fft_forward(ztm, 64, AT, cb, zkey=RBK, akey=RAK, pre_batch=pre)
                for c4 in range(32):
                    pb_ = PSB[c4 % 2]
                    pbk = f"psb{c4 % 2}"
                    for cc in range(4):
                        c = c4 * 4 + cc
                        S.op("pe", lambda c=c, cc=cc, pb_=pb_: nc.tensor.transpose(pb_[:, cc * 128:(cc + 1) * 128], ET[:, c, :], ident[:]),
                             [RCK, "ident"], [pbk])
                    src = pb_[:, 0:512].rearrange("p (cc n) -> p n cc", cc=4)
                    dst = Eb[:, :, c4 * 4:c4 * 4 + 4]
                    if c4 % 2 == 0:
                        S.op("act", lambda src=src, dst=dst: nc.scalar.copy(dst, src), [pbk], RBQ)
                    else:
                        S.op("dve", lambda src=src, dst=dst: nc.vector.tensor_copy(dst, src), [pbk], RBQ)

            Udv = U_d
            for g in range(4):
                for i in range(3):
                    dma(uraw[:, i, 1:L + 1], Udv[i * 512 + g * 128:i * 512 + (g + 1) * 128, :], ["U_d"], [RAK])
                S.op("pool", lambda: nc.gpsimd.memset(uraw[:, :, 0:1], 0.0), [RAK], [RAK])
                S.op("pool", lambda: nc.gpsimd.memset(uraw[:, :, L + 1:L + 2], 0.0), [RAK], [RAK])
                DG = hyst[:, 0:9 * 128].rearrange("p (i m) -> p i m", m=128)
                DGL = zzq[:, 0:9 * 128].rearrange("p (i m) -> p i m", m=128)
                wsel = cws[:, g:12:4, :]
                S.op("dve", lambda wsel=wsel: nc.vector.tensor_copy(whi[:].rearrange("p (i j) -> p i j", j=3), wsel), ["cws"], ["whi"])
                S.op("dve", lambda: nc.vector.tensor_copy(whf[:], whi[:]), ["whi"], ["whf"])
                S.op("dve", lambda wsel=wsel: nc.vector.tensor_tensor(wlf[:].rearrange("p (i j) -> p i j", j=3), wsel,
                                                                     whf[:].rearrange("p (i j) -> p i j", j=3), ALU.subtract), ["cws", "whf"], ["wlf"])
                for idx in range(9):
                    S.op("dve", lambda idx=idx: nc.vector.tensor_scalar(DG[:, idx, :], ident[:], whf[:, idx:idx + 1], None, ALU.mult), ["whf", "ident"], ["hyst"])
                    S.op("dve", lambda idx=idx: nc.vector.tensor_scalar(DGL[:, idx, :], ident[:], wlf[:, idx:idx + 1], None, ALU.mult), ["wlf", "ident"], ["zzq"])
                nq = 0
                for i, (dst, dkey) in enumerate(((vt, "vt"), (x1t, "x1t"), (x2t, RCK))):
                    ct = i * 4 + g
                    for q in range(16):
                        o0 = q * 512
                        ps = PS[nq % 2]
                        pk = f"ps{nq % 2}"
                        for j in range(3):
                            mm(ps[:, :], DG[:, i * 3 + j, :], uraw[:, i, o0 + j:o0 + j + 512], j == 0, False, ["hyst", RAK], [pk])
                        for j in range(3):
                            mm(ps[:, :], DGL[:, i * 3 + j, :], uraw[:, i, o0 + j:o0 + j + 512], False, j == 2, ["zzq", RAK], [pk])
                        if nq % 2 == 0:
                            S.op("act", lambda ps=ps, dst=dst, o0=o0, ct=ct: nc.scalar.activation(dst[:, o0:o0 + 512], ps[:, :], AF.Identity, bias=cbs[:, ct:ct + 1]),
                                 [pk, "cbs"], [dkey])
                        else:
                            S.op("dve", lambda ps=ps, dst=dst, o0=o0, ct=ct: nc.vector.tensor_scalar(dst[:, o0:o0 + 512], ps[:, :], cbs[:, ct:ct + 1], None, ALU.add),
                                 [pk, "cbs"], [dkey])
                        nq += 1
                masked_quarter(x2q, x2t, "x2q", RCK)
                snap("vt0", vt[:], "vt", L)
                snap("x1t0", x1t[:], "x1t", L)
                bounce_to_ztm(vt, "vt")
                conv_core(g * 2 + 0)
                vt3 = vt[:, :].rearrange("p (n1 n2) -> p n1 n2", n2=128)
                x13 = x1t[:, :].rearrange("p (n1 n2) -> p n1 n2", n2=128)
                for nb in range(16):
                    ps = PS[nb % 2]
                    pk = f"ps{nb % 2}"
                    for j in range(8):
                        n2 = nb * 8 + j
                        mm(ps[:, j * 64:(j + 1) * 64], Eb[:, n2, :], M2s[:, n2, :], True, True, RBQ + ["M2s"], [pk])
                    psv = ps[:, :].rearrange("p (j n) -> p n j", j=8)
                    gtv = gt[nb % 2][:, :].rearrange("p (n j) -> p n j", j=8)
                    gk_ = f"gt{nb % 2}"
                    S.op("dve", lambda psv=psv, gtv=gtv, nb=nb, g=g: nc.vector.scalar_tensor_tensor(out=gtv, in0=vt3[:, :, nb * 8:(nb + 1) * 8],
                                                                                             scalar=skps[:, g:g + 1], in1=psv, op0=ALU.mult, op1=ALU.add),
                         [pk, "vt", "skps"], [gk_])
                    S.op("pool", lambda gtv=gtv, nb=nb: nc.gpsimd.tensor_tensor(vt3[:, :, nb * 8:(nb + 1) * 8], gtv, x13[:, :, nb * 8:(nb + 1) * 8], ALU.mult),
                         [gk_, "x1t"], ["vt"])
                snap("zz0", vt[:], "vt", L)
                masked_quarter(zzq, vt, "zzq", "vt")
                bounce_to_ztm(vt, "vt")
                conv_core(g * 2 + 1)
                zq3 = zzq[:, :].rearrange("p (n1 n2) -> p n1 n2", n2=128)
                xq3 = x2q[:, :].rearrange("p (n1 n2) -> p n1 n2", n2=128)
                hy3 = hyst[:, :].rearrange("p (n1 n2) -> p n1 n2", n2=128)
                for nb in range(4):
                    ps = PS[nb % 2]
                    pk = f"ps{nb % 2}"
                    for j in range(32):
                        n2 = nb * 32 + j
                        mm(ps[:, j * 16:(j + 1) * 16], Eb[:, n2, :], M2qs[:, n2, :], True, True, RBQ + ["M2qs"], [pk])
                    psv = ps[:, :].rearrange("p (j n) -> p n j", j=32)
                    gtv = gt[nb % 2][:, :].rearrange("p (n j) -> p n j", j=32)
                    gk_ = f"gt{nb % 2}"
                    S.op("dve", lambda psv=psv, gtv=gtv, nb=nb, g=g: nc.vector.scalar_tensor_tensor(out=gtv, in0=zq3[:, :, nb * 32:(nb + 1) * 32],
                                                                                             scalar=skps[:, 4 + g:5 + g], in1=psv, op0=ALU.mult, op1=ALU.add),
                         [pk, "zzq", "skps"], [gk_])
                    S.op("pool", lambda gtv=gtv, nb=nb: nc.gpsimd.tensor_tensor(hy3[:, :, nb * 32:(nb + 1) * 32], gtv, xq3[:, :, nb * 32:(nb + 1) * 32], ALU.mult),
                         [gk_, "x2q"], ["hyst"])
                dma(hy_d[g * 128:(g + 1) * 128, :], hyst[:], ["hyst"], ["hy_d"])
            S.barrier()

        def phase4():
          with ExitStack() as p4:
            WoA = sb("WoA", [64, 8, D], BF16, p4)
            WoH = sb("WoH", [128, 4, D], BF16, p4)
            with ExitStack() as p4c:
                wsa = sb("wsa", [64, 8, D], F32, p4c)
                wsh = sb("wsh", [128, 4, D], F32, p4c)
                dma(wsa[:], w_out[0:512, :].rearrange("(h d) n -> d h n", d=64), [], ["wsa"])
                dma(wsh[:], w_out[512:1024, :].rearrange("(g p) n -> p g n", p=128), [], ["wsh"])
                S.op("dve", lambda: nc.vector.tensor_copy(WoA[:], wsa[:]), ["wsa"], ["WoA"])
                S.op("pool", lambda: nc.gpsimd.tensor_copy(WoH[:], wsh[:]), ["wsh"], ["WoH"])
                S.barrier()
            attc = sb("attc", [64, 8, 512], BF16, p4)
            hyc = sb("hyc", [128, 4, 512], BF16, p4)
            sqb = sb("sqb4", [128, 8, 512], BF16, p4)
            rs = sb("rs4", [128, 512], F32, p4)
            mixA = sb("mixA", [64, 8, 512], BF16, p4)
            mixH = sb("mixH", [128, 4, 512], BF16, p4)
            h1 = sb("h1", [128, 8, 512], F32, p4)
            mT = sb("mT", [128, 8, 512], BF16, p4)
            Wmi_s = [sb(f"Wmi_s{i}", [128, 8, 1024], BF16, p4) for i in range(2)]
            Wmo_s = [sb(f"Wmo_s{i}", [128, 32, 128], BF16, p4) for i in range(2)]
            fT = sb("fT", [128, 32, 512], BF16, p4)
            rb = [sb(f"rb{i}", [128, 512], F32, p4) for i in range(2)]
            xqv = xTq.rearrange("(k p) t -> p k t", p=128)
            Wmiv = Wmi_d.rearrange("(k p) n -> p k n", p=128)
            outv = outT.rearrange("(k p) t -> p k t", p=128)
            nw = 0
            nwo = 0
            for j in range(4):
                tsl = slice(j * 512, (j + 1) * 512)
                dma(attc[:], att_d[:, tsl].rearrange("(h d) t -> d h t", d=64), ["att_d"], ["attc", "na_src"])
                dma(hyc[:], hy_d[:, tsl].rearrange("(g p) t -> p g t", p=128), ["hy_d"], ["hyc", "nh_src"])
                dma(h1[:], xqv[:, :, tsl], [], ["h1", "n2_src", "nf_src"])
                rms_apply(attc[:], gas, mixA, 1.0 / 512.0, 8, "na", sqb, rs, np_=64)
                rms_apply(hyc[:], ghs, mixH, 1.0 / 512.0, 4, "nh", sqb, rs)
                for ct in range(8):
                    csl = slice(ct * 128, (ct + 1) * 128)
                    for h in range(8):
                        mm(PS[1][:, :], WoA[:, h, csl], mixA[:, h, :], h == 0, False, ["WoA", "na_dst"], ["ps1"])
                    for g in range(4):
                        mm(PS[1][:, :], WoH[:, g, csl], mixH[:, g, :], False, g == 3, ["WoH", "nh_dst"], ["ps1"])
                    S.op("dve", lambda ct=ct: nc.vector.tensor_tensor(h1[:, ct, :], h1[:, ct, :], PS[1][:, :], ALU.add), ["ps1", "h1"], ["h1", "n2_src", "nf_src"])
                rms_apply(h1[:], g2s, mT, 1.0 / D, 8, "n2", sqb, rs)
                for pc in range(4):
                    s_ = nw % 2
                    dma(Wmi_s[s_][:], Wmiv[:, :, pc * 1024:(pc + 1) * 1024], ["Wmid"], [f"Wmi_s{s_}"])
                    nw += 1
                    for ft in range(8):
                        f = pc * 8 + ft
                        pf = PS[2 + f % 2]
                        pfk = f"ps{2 + f % 2}"
                        for k in range(8):
                            mm(pf[:, :], Wmi_s[s_][:, k, ft * 128:(ft + 1) * 128], mT[:, k, :], k == 0, k == 7, [f"Wmi_s{s_}", "n2_dst"], [pfk])
                        r_ = f % 2
                        S.op("act", lambda pf=pf, r_=r_: nc.scalar.activation(rb[r_][:], pf[:, :], AF.Relu), [pfk], [f"rb{r_}"])
                        e = ("pool", "dve")[f % 2]
                        S.op(e, lambda e=e, r_=r_, f=f: V(e).tensor_tensor(fT[:, f, :], rb[r_][:], rb[r_][:], ALU.mult), [f"rb{r_}"], ["fT"])
                for ct in range(8):
                    s_ = nwo % 2
                    dma(Wmo_s[s_][:], Wmo_d[ct], ["Wmod"], [f"Wmo_s{s_}"])
                    nwo += 1
                    po = PS[4 + ct % 2]
                    pok = f"ps{4 + ct % 2}"
                    for f in range(32):
                        mm(po[:, :], Wmo_s[s_][:, f, :], fT[:, f, :], f == 0, f == 31, [f"Wmo_s{s_}", "fT"], [pok])
                    S.op("dve", lambda ct=ct, po=po: nc.vector.tensor_tensor(h1[:, ct, :], h1[:, ct, :], po[:, :], ALU.add), [pok, "h1"], ["h1", "n2_src", "nf_src"])
                rms_apply(h1[:], gfs, h1, 1.0 / D, 8, "nf", sqb, rs)
                dma(outv[:, :, tsl], h1[:], ["h1", "nf_dst"], ["outT"])
            S.barrier()

        if 0 in phases:
            phase0()
        if 1 in phases:
            phase12()
        if 3 in phases:
            phase3()
        if 4 in phases:
            phase4()
        if debug:
            with ExitStack() as pd:
                for name in debug:
                    if name not in ("att_d", "hy_d", "U_d", "H_d0"):
                        continue
                    src_, shp = {"att_d": (att_d, [512, NTOK]), "hy_d": (hy_d, [512, NTOK]), "U_d": (U_d[0:512, :], [512, L]),
                                 "H_d0": (H_d[0:4, :, 0:4].rearrange("a p k r c -> (a p) (k r c)"), [512, 2048])}[name]
                    dbo = nc.dram_tensor("dbg_" + name, shp, BF16, kind="ExternalOutput").ap()
                    dbs = sb("dbs_" + name, [128, 4, shp[1]], BF16, pd)
                    dma(dbs[:], src_.rearrange("(a p) t -> p a t", p=128), [name], ["dbs_" + name])
                    dma(dbo.rearrange("(a p) t -> p a t", p=128), dbs[:], ["dbs_" + name], ["dbg_" + name])
                S.barrier()
        S.barrier()
    return nc, S


def _core_inputs(inp, core):
    T = _tables()
    b, r = core // 4, core % 4
    f = lambda a: np.ascontiguousarray(np.asarray(a, dtype=np.float32))
    x = np.asarray(inp["x"], dtype=np.float32)
    xT = np.ascontiguousarray(x[b].T)
    tq = slice(r * NTOK, (r + 1) * NTOK)
    qm = np.zeros((128, 4), np.float32)
    qm[:, r] = 1.0
    m = {
        "xT": xT, "xTq": np.ascontiguousarray(xT[:, tq]),
        "w_in": f(inp["w_in"][0]), "w_out": f(inp["w_out"][0]), "w_mi": f(inp["w_mlp_in"][0]), "w_mo": f(inp["w_mlp_out"][0]),
        "g1": f(np.asarray(inp["norm1_g"][0]).reshape(8, 128).T), "g2": f(np.asarray(inp["norm2_g"][0]).reshape(8, 128).T),
        "gf": f(np.asarray(inp["final_g"]).reshape(8, 128).T),
        "gq": f(np.asarray(inp["q_norm_g"][0]).reshape(64, 1)), "gk": f(np.asarray(inp["k_norm_g"][0]).reshape(64, 1)),
        "ga": f(np.asarray(inp["attn_out_g"][0]).reshape(8, 64).T), "gh": f(np.asarray(inp["hy_out_g"][0]).reshape(4, 128).T),
        "cw": f(np.asarray(inp["hy_conv_w"][0]).T.reshape(12, 128, 3).transpose(1, 0, 2)),
        "cb": f(np.asarray(inp["hy_conv_b"][0]).reshape(12, 128).T),
        "fw1": f(inp["filt_w1"][0]), "fw2": f(inp["filt_w2"][0]), "fw3": f(inp["filt_w3"][0]), "fw4": f(inp["filt_w4"][0]),
        "fb": f(np.stack([np.asarray(inp["filt_b1"][0]), np.asarray(inp["filt_b2"][0]), np.asarray(inp["filt_b3"][0])], axis=1)),
        "ffreq": f(np.asarray(inp["filt_freq"][0]).reshape(64, 1)),
        "fdel": f(np.asarray(inp["filt_deltas"][0]).reshape(16, 128).T),
        "skp": f(np.asarray(inp["hy_skip_d"][0]).reshape(2, 4, 128).transpose(2, 0, 1).reshape(128, 8)),
        "qmask": qm,
        "c_cos": T["cosT"], "c_sin": T["sinT"],
        "c_cosq": np.ascontiguousarray(T["cosT"][:, tq]), "c_sinq": np.ascontiguousarray(T["sinT"][:, tq]),
        "c_zT": np.ascontiguousarray(np.stack([T["zT"], T["zrT"]])), "c_t": np.ascontiguousarray(np.stack([T["t"], T["trev"]])),
        "c_tb": T["tb"], "c_jt": T["jt"],
        "c_fpack": T["fpack"], "c_G": T["G"], "c_finv": T["finv"], "c_M2": T["M2"],
        "c_M2q": np.ascontiguousarray(T["M2"][:, :, r * 16:(r + 1) * 16]),
        "c_rot": T["rot"], "c_sel": T["sel"], "c_ones": T["ones"], "c_ident": T["ident"],
    }
    return m


def _emit(nc, S):
    from contextlib import ExitStack
    st = ExitStack()
    sems = {e: [st.enter_context(nc.semaphore(f"s_{e}{i}")) for i in range(4)] for e in S.CE}
    dsems = [st.enter_context(nc.semaphore(f"d{i}")) for i in range(S.NDS)]
    info = S.emit(sems, dsems)
    return st, info


def kernel(**inputs):
    nc, S = build_program()
    st, _ = _emit(nc, S)
    with st:
        in_maps = [_core_inputs(inputs, c) for c in range(8)]
        res = run_bass_kernel_spmd(nc, in_maps, core_ids=list(range(8)))
    out = np.empty((NB, L, D), np.float32)
    for c in range(8):
        b, r = c // 4, c % 4
        out[b, r * NTOK:(r + 1) * NTOK, :] = np.asarray(res.results[c]["outT"]).T
    return out
```

```python
import math
import numpy as np
import ml_dtypes
import concourse.bass as bass
import concourse.mybir as mybir
from concourse.bass_utils import run_bass_kernel_spmd

F32 = mybir.dt.float32
BF16 = mybir.dt.bfloat16
ALU = mybir.AluOpType
AF = mybir.ActivationFunctionType
bf16_np = ml_dtypes.bfloat16

D = 1024
L = 8192
NB = 2
NTOK = 2048
HD = 64
NQH = 8
NKV = 2
HYW = 512
DFF = 4096
EPS = 1e-6
NFFT = 2 * L
TWO_PI = 2.0 * math.pi
MAGIC = 12582912.0


class Op:
    __slots__ = ("eng", "fn", "reads", "writes", "dma", "idx", "gid", "waits", "signal", "clock",
                 "sem", "val")


class Sched:
    CE = ("pe", "act", "dve", "pool", "sp")
    EPOCH = 12000
    NDS = 40

    def __init__(self, nc):
        self.nc = nc
        self.E = {"pe": nc.tensor, "act": nc.scalar, "dve": nc.vector, "pool": nc.gpsimd,
                  "sp": nc.sync}
        self.ops = []
        self.last_w = {}
        self.readers = {}
        self.known = {e: {} for e in self.CE}
        self.known_dma = {e: set() for e in self.CE}
        self.cnt = {e: 0 for e in self.CE}
        self.last_op = {e: None for e in self.CE}
        self.live_dma = []

    def op(self, eng, fn, reads=(), writes=(), dma=False):
        o = Op()
        o.eng, o.fn, o.reads, o.writes, o.dma = eng, fn, tuple(reads), tuple(writes), dma
        o.signal = dma
        o.sem = None
        o.val = None
        o.gid = len(self.ops)
        deps = {}
        for k in o.reads:
            w = self.last_w.get(k)
            if w is not None:
                deps[w.gid] = w
        for k in o.writes:
            w = self.last_w.get(k)
            if w is not None:
                deps[w.gid] = w
            for r in self.readers.get(k, ()):
                deps[r.gid] = r
        self._finish(o, deps.values())
        for k in o.writes:
            self.last_w[k] = o
            self.readers[k] = []
        for k in o.reads:
            lst = self.readers.setdefault(k, [])
            if not dma:
                lst[:] = [r for r in lst if r.dma or r.eng != eng]
            lst.append(o)
        return o

    def _finish(self, o, deps):
        eng = o.eng
        kn = self.known[eng]
        kd = self.known_dma[eng]
        waits = []
        best = {}
        rset = set(o.reads)
        for d in deps:
            if d.dma:
                if d.gid not in kd:
                    waits.append(d)
                    kd.add(d.gid)
                continue
            if d.eng == eng:
                if eng in ("pe", "sp"):
                    continue
            if kn.get(d.eng, -1) >= d.idx:
                continue
            b = best.get(d.eng)
            if b is None or b.idx < d.idx:
                best[d.eng] = d
        for d in best.values():
            waits.append(d)
            if kn.get(d.eng, -1) < d.idx:
                kn[d.eng] = d.idx
            for e2, i2 in d.clock.items():
                if kn.get(e2, -1) < i2:
                    kn[e2] = i2
        for d in waits:
            d.signal = True
        o.waits = waits
        o.idx = self.cnt[eng]
        self.cnt[eng] += 1
        o.clock = dict(kn)
        self.ops.append(o)
        if o.fn is not None:
            self.last_op[eng] = o
        if o.dma:
            self.live_dma.append(o)

    def barrier(self):
        lasts = [self.last_op[e] for e in self.CE if self.last_op[e] is not None]
        dmas = list(self.live_dma)
        for e in self.CE:
            o = Op()
            o.eng, o.fn, o.reads, o.writes, o.dma = e, None, (), (), False
            o.signal = False
            o.sem = None
            o.val = None
            o.gid = len(self.ops)
            deps = [d for d in lasts if not d.dma and d.eng != e] + dmas
            self._finish(o, deps)
        self.live_dma = []

    def emit(self, sems, dsems):
        sig = {e: 0 for e in self.CE}
        nd = 0
        for o in self.ops:
            E = self.E[o.eng]
            for d in o.waits:
                E.wait_ge(d.sem, d.val)
            if o.fn is None:
                continue
            if o.dma:
                s = dsems[nd % self.NDS]
                rnd = nd // self.NDS
                if rnd > 0:
                    E.wait_ge(s, 16 * rnd)
                ins = o.fn()
                ins.then_inc(s, 16)
                o.sem, o.val = s, 16 * (rnd + 1)
                nd += 1
            else:
                ins = o.fn()
                if o.signal:
                    c = sig[o.eng]
                    o.sem = sems[o.eng][c // self.EPOCH]
                    o.val = c % self.EPOCH + 1
                    ins.then_inc(o.sem, 1)
                    sig[o.eng] = c + 1
        return sig, nd


_TABLES = None


def _rope_tables():
    rows = L // 64
    row = np.repeat(np.arange(rows, dtype=np.float32), 64)
    col = np.tile(np.arange(64, dtype=np.float32), rows)
    half = HD // 2
    inv_freq = (np.float32(10000.0) ** (-np.arange(0, half, 2, dtype=np.float32) / np.float32(half))).astype(np.float32)
    ang_r = (row[:, None] * inv_freq[None, :]).astype(np.float32)
    ang_c = (col[:, None] * inv_freq[None, :]).astype(np.float32)
    cos_r, sin_r = np.cos(ang_r), np.sin(ang_r)
    cos_c, sin_c = np.cos(ang_c), np.sin(ang_c)
    cosT = np.concatenate([cos_r.T, cos_r.T, cos_c.T, cos_c.T], axis=0).astype(np.float32)
    sinT = np.concatenate([sin_r.T, sin_r.T, sin_c.T, sin_c.T], axis=0).astype(np.float32)
    return np.ascontiguousarray(cosT), np.ascontiguousarray(sinT)


def _filter_pos_tables():
    f32 = np.float32
    t = np.linspace(0.0, 1.0, L, dtype=f32)
    bands = 16
    w_ang = (f32(2.0 * math.pi) * np.arange(L, dtype=f32) / f32(L)).astype(f32)
    band_f = np.linspace(1e-4, bands - 1, bands, dtype=f32)
    ang = (w_ang[:, None] * band_f[None, :]).astype(f32)
    z = np.concatenate([t[:, None], np.cos(ang), -np.sin(ang)], axis=-1).astype(f32)
    zT = np.ascontiguousarray(z.T)
    idx = (L - np.arange(L)) % L
    zrT = np.ascontiguousarray(z[idx].T)
    trev = t[idx].copy()
    return zT, zrT, t.copy(), trev


def _fft_tables():
    N = NFFT
    n1 = np.arange(128, dtype=np.float64)
    k1 = np.arange(64, dtype=np.float64)
    th = 2 * np.pi * np.outer(n1, k1 + 0.5) / 128.0
    fpack = np.stack([np.sin(th), np.cos(th), -np.sin(th)], axis=2).reshape(128, 192)
    n2 = np.arange(128, dtype=np.float64)
    k2 = np.arange(128, dtype=np.float64)
    kap = k1[None, :, None] + 128.0 * k2[None, None, :] + 0.5
    thg = 2 * np.pi * n2[:, None, None] * kap / N
    G = np.stack([np.cos(thg), -np.sin(thg)], axis=2)
    thf = 2 * np.pi * np.outer(k2, n2) / 128.0
    finv = np.stack([np.cos(thf), np.sin(thf), -np.sin(thf), -np.cos(thf)], axis=1)
    n1h = np.arange(64, dtype=np.float64)
    phi = 2 * np.pi * (k1[:, None, None] + 0.5) * (128.0 * n1h[None, None, :] + n2[None, :, None]) / N
    M2 = np.concatenate([(2.0 / N) * np.cos(phi), -(2.0 / N) * np.sin(phi)], axis=0)
    return (fpack.astype(bf16_np), G.astype(bf16_np), finv.astype(bf16_np), M2.astype(bf16_np))


def _tables():
    global _TABLES
    if _TABLES is None:
        cosT, sinT = _rope_tables()
        zT, zrT, t, trev = _filter_pos_tables()
        fpack, G, finv, M2 = _fft_tables()
        ii = np.arange(512, dtype=np.float64)
        jj = np.arange(16, dtype=np.float64)
        tbm = np.stack([ii / (L - 1), (L - ii) / (L - 1)]).astype(np.float32)
        jtm = np.stack([512.0 * jj / (L - 1), -512.0 * jj / (L - 1)]).astype(np.float32)
        rot = np.zeros((64, 64), np.float32)
        for m in range(64):
            if (m % 32) < 16:
                rot[m + 16, m] = -1.0
            else:
                rot[m - 16, m] = 1.0
        sel = np.zeros((65, 64), np.float32)
        sel[64, :] = 1.0
        _TABLES = dict(cosT=cosT, sinT=sinT, zT=zT, zrT=zrT, t=t, trev=trev, fpack=fpack, G=G,
                       finv=finv, M2=M2, rot=rot, sel=sel, tb=tbm, jt=jtm,
                       ones=np.ones((128, 128), np.float32),
                       ident=np.eye(128, dtype=np.float32).astype(bf16_np))
    return _TABLES


def build_program(phases=(0, 1, 2, 3, 4), debug=()):
    nc = bass.Bass("TRN2", target_bir_lowering=False)
    S = Sched(nc)

    def din(name, shape, dt=F32):
        return nc.dram_tensor(name, list(shape), dt, kind="ExternalInput").ap()

    def dscr(name, shape, dt):
        return nc.dram_tensor(name, list(shape), dt, kind="Internal").ap()

    xT = din("xT", [D, L])
    xTq = din("xTq", [D, NTOK])
    w_in = din("w_in", [D, 2304])
    w_out = din("w_out", [D, D])
    w_mi = din("w_mi", [D, DFF])
    w_mo = din("w_mo", [DFF, D])
    g1 = din("g1", [128, 8]); g2 = din("g2", [128, 8]); gf = din("gf", [128, 8])
    gq = din("gq", [64, 1]); gk = din("gk", [64, 1])
    ga = din("ga", [64, 8]); gh = din("gh", [128, 4])
    cw = din("cw", [128, 12, 3]); cb = din("cb", [128, 12])
    fw1 = din("fw1", [33, 64]); fw2 = din("fw2", [64, 64]); fw3 = din("fw3", [64, 64])
    fw4 = din("fw4", [64, 2048])
    fb = din("fb", [64, 3]); ffreq = din("ffreq", [64, 1])
    fdel = din("fdel", [128, 16])
    skp = din("skp", [128, 8])
    qmask = din("qmask", [128, 4])
    c_cos = din("c_cos", [64, L]); c_sin = din("c_sin", [64, L])
    c_cosq = din("c_cosq", [64, NTOK]); c_sinq = din("c_sinq", [64, NTOK])
    c_zT = din("c_zT", [2, 33, L])
    c_t = din("c_t", [2, L])
    c_tb = din("c_tb", [2, 512])
    c_jt = din("c_jt", [2, 16])
    c_fpack = din("c_fpack", [128, 192], BF16)
    c_G = din("c_G", [128, 64, 2, 128], BF16)
    c_finv = din("c_finv", [128, 4, 128], BF16)
    c_M2 = din("c_M2", [128, 128, 64], BF16)
    c_M2q = din("c_M2q", [128, 128, 16], BF16)
    c_rot = din("c_rot", [64, 64]); c_sel = din("c_sel", [65, 64]); c_ones = din("c_ones", [128, 128])
    c_ident = din("c_ident", [128, 128], BF16)
    outT = nc.dram_tensor("outT", [D, NTOK], F32, kind="ExternalOutput").ap()

    U_d = dscr("U_d", [1536, L], BF16)
    H_d = dscr("H_d", [8, 128, 64, 4, 128], BF16)
    taps_d = dscr("taps_d", [128, NFFT], BF16)
    zcm_d = dscr("zcm_d", [128, L], BF16)
    att_d = dscr("att_d", [512, NTOK], BF16)
    hy_d = dscr("hy_d", [512, NTOK], BF16)
    Wmi_d = dscr("Wmi_d", [D, DFF], BF16)
    Wmo_d = dscr("Wmo_d", [8, 128, 32, 128], BF16)

    from contextlib import ExitStack
    es = ExitStack()

    _names = {}

    def sb(name, shape, dt, stack=None):
        n_ = _names.get(name, 0)
        _names[name] = n_ + 1
        nm = name if n_ == 0 else f"{name}_r{n_}"
        return (stack or es).enter_context(nc.sbuf_tensor(nm, list(shape), dt))

    def dma(out, in_, reads, writes, eng="sp"):
        return S.op(eng, lambda: S.E[eng].dma_start(out=out, in_=in_), reads, writes, dma=True)

    def mm(out, lhsT, rhs, start, stop, reads, writes):
        return S.op("pe", lambda: nc.tensor.matmul(out, lhsT, rhs, start=start, stop=stop), reads, writes)

    rr = {"i": 0}

    def alt(engs=("dve", "pool")):
        rr["i"] += 1
        return engs[rr["i"] % len(engs)]

    def V(eng):
        return S.E[eng]

    with es:
        PSW = [es.enter_context(nc.psum_tensor(f"psw{i}", [128, 1024], F32)) for i in range(3)]
        PS = [PSW[i // 2][:, (i % 2) * 512:(i % 2 + 1) * 512] for i in range(6)]
        PSB = [es.enter_context(nc.psum_tensor(f"psb{i}", [128, 1024], BF16)) for i in range(2)]
        PSX = [PSB[i].bitcast(F32)[:, :] for i in range(2)]

        ones = sb("ones", [128, 128], F32)
        ident = sb("ident", [128, 128], BF16)
        rot = sb("rot", [64, 64], F32)
        sel = sb("sel", [65, 64], F32)
        epsT = sb("epsT", [128, 1], F32)
        onesbf = sb("onesbf", [128, 128], BF16)
        g1s = sb("g1s", [128, 8], F32); g2s = sb("g2s", [128, 8], F32); gfs = sb("gfs", [128, 8], F32)
        gqs = sb("gqs", [64, 1], F32); gks = sb("gks", [64, 1], F32)
        gas = sb("gas", [64, 8], F32); ghs = sb("ghs", [128, 4], F32)
        cws = sb("cws", [128, 12, 3], F32); cbs = sb("cbs", [128, 12], F32)
        skps = sb("skps", [128, 8], F32); qms = sb("qms", [128, 4], F32)
        for t_, d_ in ((ones, c_ones), (ident, c_ident), (rot, c_rot), (sel, c_sel), (g1s, g1), (g2s, g2),
                       (gfs, gf), (gqs, gq), (gks, gk), (gas, ga), (ghs, gh), (cws, cw), (cbs, cb),
                       (skps, skp), (qms, qmask)):
            dma(t_[:], d_, [], [t_.name if hasattr(t_, "name") else id(t_)])
        KEY = lambda t_: t_.name if hasattr(t_, "name") else id(t_)
        S.op("dve", lambda: nc.vector.memset(epsT[:], EPS), [], ["epsT"])
        S.op("pool", lambda: nc.gpsimd.memset(onesbf[:], 1.0), [], ["onesbf"])

        FC = {}
        SNAP = {}

        def snap(name, src_ap, key, ncols):
            if name not in debug or name in SNAP:
                return
            SNAP[name] = nc.dram_tensor("dbg_" + name, [128, ncols], BF16, kind="ExternalOutput").ap()
            dma(SNAP[name], src_ap, [key], ["dbg_" + name])


        def load_fft_consts(stack):
            FC["fpack"] = sb("fpack", [128, 192], BF16, stack)
            FC["finv"] = sb("finv", [128, 4, 128], BF16, stack)
            FC["Gq"] = [sb(f"Gq{i}", [128, 8, 2, 128], BF16, stack) for i in range(2)]
            dma(FC["fpack"][:], c_fpack, [], ["fpack"])
            dma(FC["finv"][:], c_finv, [], ["finv"])

        def rsqrt(out, in_, scale, reads, writes, np_=128):
            S.op("act", lambda: nc.scalar.activation(out, in_, AF.Sqrt, bias=epsT[0:np_, :], scale=scale),
                 list(reads) + ["epsT"], writes)
            S.op("dve", lambda: nc.vector.reciprocal(out, out), writes, writes)

        def fft_forward(ztm, K, AT, cb_batch, zkey="ztm", akey="AT", pre_batch=None, cb_late=None):
            fpack = FC["fpack"]
            for c2 in range(64):
                ps = PS[c2 % 4]
                pk = f"ps{c2 % 4}"
                for cc in range(2):
                    c = c2 * 2 + cc
                    mm(ps[:, cc * 192:(cc + 1) * 192], ztm[0:K, c, :], fpack[0:K, :], True, True,
                       [(zkey, c // 32), "fpack"], [pk])
                e = alt(("act", "dve"))
                src = ps[:, 0:384].rearrange("p (cc j) -> p j cc", cc=2)
                dst = AT[:, :, c2 * 2:c2 * 2 + 2]
                if e == "act":
                    S.op("act", lambda dst=dst, src=src: nc.scalar.copy(dst, src), [pk], [akey])
                else:
                    S.op("dve", lambda dst=dst, src=src: nc.vector.tensor_copy(dst, src), [pk], [akey])

            def load_g(q8):
                dma(FC["Gq"][q8 % 2][:], c_G[:, q8 * 8:(q8 + 1) * 8], [], [f"Gq{q8 % 2}"])

            load_g(0)
            for kb in range(32):
                q4 = kb // 4
                Gq = FC["Gq"][q4 % 2]
                gkey = f"Gq{q4 % 2}"
                if kb % 4 == 0 and q4 + 1 < 8:
                    load_g(q4 + 1)
                if pre_batch is not None:
                    pre_batch(kb)
                ps = PS[2 + kb % 2]
                pk = f"ps{2 + kb % 2}"
                for kk in range(2):
                    k1 = kb * 2 + kk
                    kl = k1 % 8
                    zp = ps[:, kk * 256:(kk + 1) * 256].rearrange("p (r c) -> p r c", r=2)
                    mm(zp, Gq[:, kl, 0, :], AT[:, 3 * k1 + 1:3 * k1 + 3, :], True, False, [akey, gkey], [pk])
                    mm(zp, Gq[:, kl, 1, :], AT[:, 3 * k1:3 * k1 + 2, :], False, True, [akey, gkey], [pk])
                cb_batch(kb, ps, pk)
                if cb_late is not None and kb >= 1:
                    cb_late(kb - 1)
            if cb_late is not None:
                cb_late(31)

        cast_jobs = []
        for cbk in range(8):
            cast_jobs.append((w_mi.rearrange("(k p) n -> p k n", p=128)[:, :, cbk * 512:(cbk + 1) * 512],
                              Wmi_d.rearrange("(k p) n -> p k n", p=128)[:, :, cbk * 512:(cbk + 1) * 512], None))
        for kr in range(4):
            for cbk in range(2):
                cast_jobs.append((w_mo.rearrange("(k p) n -> p k n", p=128)[:, kr * 8:(kr + 1) * 8, cbk * 512:(cbk + 1) * 512],
                                  None, (kr, cbk)))

        def emit_cast(n, wst, wbf):
            src, dst, tl = cast_jobs[n]
            s_ = n % 2
            dma(wst[s_][:], src, [], [f"wst{s_}"])
            S.op("dve", lambda s_=s_: nc.vector.tensor_copy(wbf[s_][:], wst[s_][:]), [f"wst{s_}"], [f"wbf{s_}"])
            if tl is None:
                dma(dst, wbf[s_][:], [f"wbf{s_}"], ["Wmid"])
            else:
                kr_, cbk_ = tl
                for c_ in range(4):
                    dma(Wmo_d[cbk_ * 4 + c_, :, kr_ * 8:(kr_ + 1) * 8, :], wbf[s_][:, :, c_ * 128:(c_ + 1) * 128], [f"wbf{s_}"], ["Wmod"])

        def phase0():
          with ExitStack() as p0:
            load_fft_consts(p0)

            w1s = sb("w1s", [33, 64], F32, p0); w2s = sb("w2s", [64, 64], F32, p0); w3s = sb("w3s", [64, 64], F32, p0)
            w4b = sb("w4b", [64, 2048], BF16, p0)
            fbs = sb("fbs", [64, 3], F32, p0); frs = sb("frs", [64, 1], F32, p0); ffb = sb("ffb", [64, 3], F32, p0)
            dls = sb("dls", [128, 16], F32, p0); ndl = sb("ndl", [128, 16], F32, p0)
            negpi = sb("negpi", [128, 1], F32, p0)
            H3 = sb("H3", [64, 2, L], BF16, p0)
            p0m = ExitStack()
            w4s = sb("w4s", [64, 2048], F32, p0m)
            for t_, d_, k_ in ((w1s, fw1, "w1s"), (w2s, fw2, "w2s"), (w3s, fw3, "w3s"), (w4s, fw4, "w4s"),
                               (fbs, fb, "fbs"), (frs, ffreq, "frs"), (dls, fdel, "dls")):
                dma(t_[:], d_, [], [k_])
            S.op("dve", lambda: nc.vector.tensor_copy(w4b[:], w4s[:]), ["w4s"], ["w4b"])
            S.op("dve", lambda: nc.vector.tensor_scalar(ffb[:], fbs[:], frs[:, 0:1], None, ALU.mult), ["fbs", "frs"], ["ffb"])
            S.op("dve", lambda: nc.vector.tensor_scalar(ndl[:], dls[:], -1.0, None, ALU.mult), ["dls"], ["ndl"])
            S.op("dve", lambda: nc.vector.tensor_tensor(ndl[:], ndl[:], dls[:], ALU.min), ["dls", "ndl"], ["ndl"])
            S.op("dve", lambda: nc.vector.memset(negpi[:], 0.0), [], ["negpi"])
            zc = [sb(f"zc{i}", [33, 512], F32, p0m) for i in range(4)]
            faL = [sb(f"fa{i}", [64, 512], F32, p0m) for i in range(2)]
            fbufL = [sb(f"fbuf{i}", [64, 512], F32, p0m) for i in range(2)]
            fhL = [sb(f"fh{i}", [64, 512], F32, p0m) for i in range(2)]

            def sin_layer(ps, li, dst, pk, v_, dkey):
                fa, fbuf = faL[v_], fbufL[v_]
                fak, fbk = f"fa{v_}", f"fbuf{v_}"
                S.op("dve", lambda: nc.vector.tensor_scalar(fa[:], ps[0:64, :], frs[:, 0:1], ffb[:, li:li + 1], ALU.mult, ALU.add),
                     [pk, "frs", "ffb"], [fak])
                S.op("dve", lambda: nc.vector.tensor_scalar(fbuf[:], fa[:], 1.0 / TWO_PI, MAGIC, ALU.mult, ALU.add), [fak], [fbk])
                S.op("dve", lambda: nc.vector.tensor_scalar(fbuf[:], fbuf[:], -MAGIC, -TWO_PI, ALU.add, ALU.mult), [fbk], [fbk])
                S.op("dve", lambda: nc.vector.tensor_tensor(fa[:], fa[:], fbuf[:], ALU.add), [fak, fbk], [fak])
                S.op("act", lambda: nc.scalar.activation(dst, fa[:], AF.Sin, bias=negpi[0:64, :], scale=1.0), [fak, "negpi"], [dkey])

            for j in range(16):
                for var in range(2):
                    s_ = (j % 2) * 2 + var
                    psm = PS[4 + var]
                    pmk = f"ps{4 + var}"
                    fh = fhL[var]
                    fhk = f"fh{var}"
                    dma(zc[s_][:], c_zT[var, :, j * 512:(j + 1) * 512], [], [f"zc{s_}"])
                    mm(psm[0:64, :], w1s[:], zc[s_][:], True, True, [f"zc{s_}", "w1s"], [pmk])
                    sin_layer(psm, 0, fh[:], pmk, var, fhk)
                    mm(psm[0:64, :], w2s[:], fh[:], True, True, [fhk, "w2s"], [pmk])
                    sin_layer(psm, 1, fh[:], pmk, var, fhk)
                    mm(psm[0:64, :], w3s[:], fh[:], True, True, [fhk, "w3s"], [pmk])
                    sin_layer(psm, 2, H3[:, var, j * 512:(j + 1) * 512], pmk, var, ("H3", var))

            S.barrier()
            p0m.close()
            tapsUL = [sb(f"tapsU{i}", [128, NFFT], BF16, p0) for i in range(2)]
            ztmf = sb("ztmf", [128, 128, 128], BF16, p0)
            ATf = sb("ATf", [128, 192, 128], BF16, p0)
            tb = sb("tb", [128, 2, 512], F32, p0)
            jt = sb("jt", [128, 2, 16], F32, p0)
            for hf in range(2):
                dma(tb[:, hf, :], c_tb[hf:hf + 1, :].broadcast_to([128, 512]), [], ["tb"])
                dma(jt[:, hf, :], c_jt[hf:hf + 1, :].broadcast_to([128, 16]), [], ["jt"])
            wbase1 = sb("wbase", [128, 512], F32, p0)
            wbase = [wbase1, wbase1]
            wcj = [sb(f"wcj{i}", [128, 16], F32, p0) for i in range(2)]
            tpf = [sb(f"tpf{i}", [128, 512], F32, p0) for i in range(2)]
            psum_ = sb("psum_", [128, 32], F32, p0)
            nrm = sb("nrm", [128, 2], F32, p0)
            Hst = [sb(f"Hst{i}", [128, 2, 4, 128], BF16, p0) for i in range(2)]

            def taps_chunk(go, cidx):
                g, o = go // 2, go % 2
                half, j = cidx // 16, cidx % 16
                tU = tapsUL[go % 2]
                tk = f"tapsU{go % 2}"
                col0 = o * 1024 + half * 512 + g * 128
                ct = col0 // 128
                if j == 0:
                    S.op("act", lambda: nc.scalar.activation(wbase[half][:], tb[:, half, :], AF.Exp, scale=ndl[:, ct:ct + 1]),
                         ["tb", "ndl"], ["wbase"])
                    S.op("act", lambda: nc.scalar.activation(wcj[half][:], jt[:, half, :], AF.Exp, scale=ndl[:, ct:ct + 1]),
                         ["jt", "ndl"], [f"wcj{half}"])
                s_ = j % 2
                pst = PS[4 + s_]
                ptk = f"ps{4 + s_}"
                mm(pst[:, :], w4b[:, col0:col0 + 128], H3[:, half, j * 512:(j + 1) * 512], True, True, [("H3", half), "w4b"], [ptk])
                S.op("dve", lambda: nc.vector.scalar_tensor_tensor(out=tpf[s_][:], in0=pst[:, :], scalar=wcj[half][:, j:j + 1], in1=wbase[half][:],
                                                                  op0=ALU.mult, op1=ALU.mult), [ptk, f"wcj{half}", "wbase"], [f"tpf{s_}"])
                if half == 1 and j == 0:
                    S.op("dve", lambda: nc.vector.memset(tpf[s_][:, 0:1], 0.0), [f"tpf{s_}"], [f"tpf{s_}"])
                S.op("dve", lambda: nc.vector.tensor_reduce(psum_[:, cidx:cidx + 1], tpf[s_][:], mybir.AxisListType.X, ALU.add,
                                                            apply_absolute_value=True), [f"tpf{s_}"], ["psum_"])
                S.op("act", lambda: nc.scalar.copy(tU[:, half * L + j * 512: half * L + (j + 1) * 512], tpf[s_][:]), [f"tpf{s_}"], [tk])

            def taps_finish(go):
                tU = tapsUL[go % 2]
                tk = f"tapsU{go % 2}"
                S.op("dve", lambda: nc.vector.tensor_reduce(nrm[:, 0:1], psum_[:], mybir.AxisListType.X, ALU.add), ["psum_"], ["nrm"])
                S.op("dve", lambda: nc.vector.reciprocal(nrm[:, 0:1], nrm[:, 0:1]), ["nrm"], ["nrm"])
                S.op("dve", lambda: nc.vector.tensor_scalar(nrm[:, 1:2], nrm[:, 0:1], -1.0, None, ALU.mult), ["nrm"], ["nrm"])
                S.op("dve", lambda: nc.vector.tensor_scalar(tU[:, 0:L], tU[:, 0:L], nrm[:, 0:1], None, ALU.mult), [tk, "nrm"], [tk])
                S.op("act", lambda: nc.scalar.mul(tU[:, L:NFFT], tU[:, L:NFFT], nrm[:, 1:2]), [tk, "nrm"], [tk])
                tv = taps_d.rearrange("c (n1 n2) -> n1 c n2", n2=128)
                for q4 in range(4):
                    dma(taps_d[q4 * 32:(q4 + 1) * 32, :], tU[q4 * 32:(q4 + 1) * 32, :], [tk], [("taps_d", q4)])
                for q4 in range(4):
                    dma(ztmf[:, q4 * 32:(q4 + 1) * 32, :], tv[:, q4 * 32:(q4 + 1) * 32, :], [("taps_d", q4)], [("ztm", q4)])

            for cidx in range(32):
                taps_chunk(0, cidx)
            taps_finish(0)
            for go in range(8):
                def cb_filter(kb, ps, pk, go=go):
                    s_ = kb % 2
                    zv = ps[:, :].rearrange("p (k r c) -> p k r c", k=2, r=2)
                    S.op("act", lambda: nc.scalar.copy(Hst[s_][:, :, 0:2, :], zv), [pk], [f"Hst{s_}"])
                    S.op("dve", lambda: nc.vector.tensor_copy(Hst[s_][:, :, 2, :], zv[:, :, 1, :]), [pk], [f"Hst{s_}"])
                    S.op("dve", lambda: nc.vector.tensor_copy(Hst[s_][:, :, 3, :], zv[:, :, 0, :]), [pk], [f"Hst{s_}"])
                    dma(H_d[go, :, kb * 2:kb * 2 + 2], Hst[s_][:], [f"Hst{s_}"], [("H_d", go)])

                def pre_filter(kb, go=go):
                    if go + 1 < 8:
                        if kb < 16:
                            taps_chunk(go + 1, 2 * kb)
                            taps_chunk(go + 1, 2 * kb + 1)
                        elif kb == 16:
                            taps_finish(go + 1)

                fft_forward(ztmf, 128, ATf, cb_filter, pre_batch=pre_filter)
            S.barrier()
        def rms_apply(src, gvec, dst, scale, nk, tag, sqb, rs, np_=128, ss_ps=0, sqk="sqb", rsk="rs"):
            rms_a(src, nk, tag, sqb, np_, sqk)
            rms_b(src, gvec, dst, scale, nk, tag, sqb, rs, np_, ss_ps, sqk, rsk)

        def rms_a(src, nk, tag, sqb, np_=128, sqk="sqb"):
            S.op("act", lambda: nc.scalar.activation(sqb[0:np_, 0:nk, :], src, AF.Square), [tag + "_src"], [sqk])

        def rms_b(src, gvec, dst, scale, nk, tag, sqb, rs, np_=128, ss_ps=0, sqk="sqb", rsk="rs"):
            for k in range(nk):
                mm(PS[ss_ps][0:np_, :], onesbf[0:np_, 0:np_], sqb[0:np_, k, :], k == 0, k == nk - 1, [sqk, "onesbf"], [f"ps{ss_ps}"])
            S.op("act", lambda: nc.scalar.activation(rs[0:np_, :], PS[ss_ps][0:np_, :], AF.Sqrt, bias=epsT[0:np_, :], scale=scale),
                 [f"ps{ss_ps}", "epsT"], [rsk])
            S.op("dve", lambda: nc.vector.reciprocal(rs[0:np_, :], rs[0:np_, :]), [rsk], [rsk])
            for k in range(nk):
                S.op("dve", lambda k=k: nc.vector.scalar_tensor_tensor(out=dst[:, k, :], in0=src[:, k, :], scalar=gvec[:, k:k + 1],
                                                                      in1=rs[0:np_, :], op0=ALU.mult, op1=ALU.mult),
                     [tag + "_src", rsk, "gvecs"], [tag + "_dst"])

        def phase12():
          with ExitStack() as p12:
            KT = sb("KT", [128, 2, L], BF16, p12)
            Vx = sb("Vx", [128, 64, 2, 64], BF16, p12)
            QT = sb("QT", [128, 4, NTOK], BF16, p12)
            onesb = sb("onesb", [128, 64], BF16, p12)
            S.op("pool", lambda: nc.gpsimd.memset(onesb[:], 1.0), [], ["onesb"])
            with ExitStack() as p1:
                Wb = sb("Wb", [128, 8, 2304], BF16, p1)
                with ExitStack() as p1c:
                    wst = [sb(f"wsi{i}", [128, 8, 384], F32, p1c) for i in range(2)]
                    wv = w_in.rearrange("(k p) n -> p k n", p=128)
                    for bk in range(6):
                        s_ = bk % 2
                        dma(wst[s_][:], wv[:, :, bk * 384:(bk + 1) * 384], [], [f"wst{s_}"])
                        if bk % 2 == 0:
                            S.op("dve", lambda s_=s_, bk=bk: nc.vector.tensor_copy(Wb[:, :, bk * 384:(bk + 1) * 384], wst[s_][:]), [f"wst{s_}"], ["Wb"])
                        else:
                            S.op("act", lambda s_=s_, bk=bk: nc.scalar.copy(Wb[:, :, bk * 384:(bk + 1) * 384], wst[s_][:]), [f"wst{s_}"], ["Wb"])
                    S.barrier()
                xin = [sb(f"xin{i}", [128, 8, 512], F32, p1) for i in range(2)]
                sqbL = [sb(f"sqb{i}", [128, 8, 512], BF16, p1) for i in range(2)]
                rsL = [sb(f"rs{i}", [128, 512], F32, p1) for i in range(2)]
                aTL = [sb(f"aT{i}", [128, 8, 512], BF16, p1) for i in range(2)]
                ust = sb("ust", [128, 12, 512], BF16, p1)
                hsq = sb("hsq", [64, 512], BF16, p1); hrs = sb("hrs", [64, 512], F32, p1); hkn = sb("hkn", [64, 512], F32, p1)
                ht1 = sb("ht1", [64, 512], F32, p1); ht2 = sb("ht2", [64, 512], F32, p1)
                csb = [sb(f"csb{i}", [64, 2, 512], F32, p1) for i in range(2)]
                qstg = [sb(f"qstg{i}", [64, 512], BF16, p1) for i in range(2)]

                def hp1(srcb, gvec):
                    src = PS[srcb][0:64, :]
                    sk_ = f"ps{srcb}"
                    S.op("act", lambda: nc.scalar.activation(hsq[:], src, AF.Square), [sk_], ["hsq"])
                    mm(PS[3][0:64, :], onesbf[0:64, 0:64], hsq[:], True, True, ["hsq", "onesbf"], ["ps3"])
                    S.op("act", lambda: nc.scalar.activation(hrs[:], PS[3][0:64, :], AF.Sqrt, bias=epsT[0:64, :], scale=1.0 / 64.0),
                         ["ps3", "epsT"], ["hrs"])
                    S.op("dve", lambda: nc.vector.reciprocal(hrs[:], hrs[:]), ["hrs"], ["hrs"])
                    S.op("dve", lambda: nc.vector.scalar_tensor_tensor(out=hkn[:], in0=src, scalar=gvec[:, 0:1], in1=hrs[:],
                                                                      op0=ALU.mult, op1=ALU.mult), [sk_, "hrs", "gvecs"], ["hkn"])

                def hp2(cs, cskey, dst, dstkey):
                    mm(PS[3][0:64, :], rot[:], hkn[:], True, True, ["hkn", "rot"], ["ps3"])
                    S.op("pool", lambda: nc.gpsimd.tensor_tensor(ht1[:], hkn[:], cs[:, 0, :], ALU.mult), ["hkn", cskey], ["ht1"])
                    S.op("dve", lambda: nc.vector.tensor_tensor(ht2[:], PS[3][0:64, :], cs[:, 1, :], ALU.mult), ["ps3", cskey], ["ht2"])
                    S.op("pool", lambda: nc.gpsimd.tensor_tensor(dst, ht1[:], ht2[:], ALU.add), ["ht1", "ht2"], [dstkey])

                xv = xT.rearrange("(k p) t -> p k t", p=128)
                xqv = xTq.rearrange("(k p) t -> p k t", p=128)
                Uv = U_d.rearrange("(ct p) t -> p ct t", p=128)
                work = [("seq", j) for j in range(16)] + [("own", j) for j in range(4)]

                def norm_a(w):
                    kind, j = work[w]
                    s_ = w % 2
                    tsl = slice(j * 512, (j + 1) * 512)
                    srcv, cc, ss_ = (xv, c_cos, c_sin) if kind == "seq" else (xqv, c_cosq, c_sinq)
                    dma(xin[s_][:], srcv[:, :, tsl], [], [f"xin{s_}", f"n1{s_}_src"])
                    dma(csb[s_][:, 0, :], cc[:, tsl], [], [f"csb{s_}"])
                    dma(csb[s_][:, 1, :], ss_[:, tsl], [], [f"csb{s_}"])
                    rms_a(xin[s_][:], 8, f"n1{s_}", sqbL[s_], 128, f"sqb{s_}")

                def norm_b(w):
                    s_ = w % 2
                    rms_b(xin[s_][:], g1s, aTL[s_], 1.0 / D, 8, f"n1{s_}", sqbL[s_], rsL[s_], 128, 0, f"sqb{s_}", f"rs{s_}")

                def emit_proj(w, mid_hook):
                    kind, j = work[w]
                    s_ = w % 2
                    aT = aTL[s_]
                    ak = f"n1{s_}_dst"
                    cs, csk = csb[s_], f"csb{s_}"
                    tsl = slice(j * 512, (j + 1) * 512)
                    if kind == "seq":
                        for kvh in range(2):
                            for k in range(8):
                                mm(PS[1 + kvh][0:64, :], Wb[:, k, 512 + kvh * 64:512 + (kvh + 1) * 64], aT[:, k, :], k == 0, k == 7,
                                   ["Wb", ak], [f"ps{1 + kvh}"])
                        for tt in range(4):
                            for k in range(8):
                                mm(PS[4][:, tt * 128:(tt + 1) * 128], aT[:, k, tt * 128:(tt + 1) * 128], Wb[:, k, 640:768], k == 0, k == 7,
                                   ["Wb", ak], ["ps4"])
                        S.op("act", lambda j=j: nc.scalar.copy(Vx[:, j * 4:(j + 1) * 4, :, :],
                                                             PS[4][:, :].rearrange("p (t h d) -> p t h d", t=4, h=2)), ["ps4"], ["Vx"])
                        for ct in range(12):
                            pu, puk = (PS[5], "ps5") if ct % 2 == 0 else (PSX[0], "psb0")
                            for k in range(8):
                                mm(pu[:, :], Wb[:, k, 768 + ct * 128:768 + (ct + 1) * 128], aT[:, k, :], k == 0, k == 7, ["Wb", ak], [puk])
                            if ct % 2 == 0:
                                S.op("act", lambda ct=ct, pu=pu: nc.scalar.copy(ust[:, ct, :], pu[:, :]), [puk], ["ust"])
                            else:
                                S.op("dve", lambda ct=ct, pu=pu: nc.vector.tensor_copy(ust[:, ct, :], pu[:, :]), [puk], ["ust"])
                            if ct == 1:
                                hp1(1, gks)
                            if ct == 3:
                                hp2(cs, csk, KT[0:64, 0, tsl], "KT")
                                hp1(2, gks)
                            if ct == 5:
                                hp2(cs, csk, KT[0:64, 1, tsl], "KT")
                            if ct == 7:
                                mid_hook()
                        dma(Uv[:, :, tsl], ust[:], ["ust"], ["U_d"])
                    else:
                        def qdst(h):
                            if h % 2 == 0:
                                return QT[0:64, h // 2, tsl], "QT", None
                            qs_ = (h // 2) % 2
                            return qstg[qs_][:], f"qstg{qs_}", qs_

                        def qproj(h):
                            hb = 1 + h % 2
                            for k in range(8):
                                mm(PS[hb][0:64, :], Wb[:, k, h * 64:(h + 1) * 64], aT[:, k, :], k == 0, k == 7, ["Wb", ak], [f"ps{hb}"])

                        qproj(0)
                        for h in range(8):
                            hp1(1 + h % 2, gqs)
                            if h + 1 < 8:
                                qproj(h + 1)
                            if h == 3:
                                mid_hook()
                            d_, dk_, qs_ = qdst(h)
                            hp2(cs, csk, d_, dk_)
                            if qs_ is not None:
                                dma(QT[64:128, h // 2, tsl], qstg[qs_][:], [f"qstg{qs_}"], ["QT2"])

                norm_a(0)
                norm_b(0)
                for w in range(len(work)):
                    if w + 1 < len(work):
                        norm_a(w + 1)
                        emit_proj(w, lambda w=w: norm_b(w + 1))
                    else:
                        emit_proj(w, lambda: None)
                dma(KT[64:128, :, :], KT[0:64, :, :], ["KT"], ["KT2"])
                S.barrier()
            if 2 in phases:
              with ExitStack() as p2:
                pb = [sb(f"pb{i}", [128, 1024], BF16, p2) for i in range(2)]
                cwst = [sb(f"wst{i}", [128, 8, 512], F32, p2) for i in range(2)]
                cwbf = [sb(f"wbf{i}", [128, 8, 512], BF16, p2) for i in range(2)]
                rden = sb("rden", [128, 512], F32, p2)
                ast = [sb(f"ast{i}", [128, 512], BF16, p2) for i in range(2)]
                n_ = 0
                for qc in range(4):
                    qsl = slice(qc * 512, (qc + 1) * 512)
                    for hp in range(4):
                        kvh = hp // 2
                        emit_cast(qc * 4 + hp, cwst, cwbf)
                        for step in range(64 + 1):
                            if step < 64:
                                kt = step
                                sl = kt % 2
                                ksl = slice(kt * 128, (kt + 1) * 128)
                                mm(PS[2 * sl][:, :], KT[0:64, kvh, ksl], QT[0:64, hp, qsl], True, True, ["KT", "QT"], [f"ps{2 * sl}"])
                                mm(PS[2 * sl + 1][:, :], KT[64:128, kvh, ksl], QT[64:128, hp, qsl], True, True, ["KT2", "QT2"], [f"ps{2 * sl + 1}"])
                                S.op("act", lambda sl=sl: nc.scalar.activation(pb[sl][:], PSW[sl][:, :], AF.Exp, scale=0.125),
                                     [f"ps{2 * sl}", f"ps{2 * sl + 1}"], [f"pb{sl}"])
                            kt = step - 1
                            if kt >= 0:
                                sl = kt % 2
                                st_, sp_ = kt == 0, kt == 63
                                pA = pb[sl][:, 0:512]
                                pB = pb[sl][:, 512:1024]
                                vv = Vx[:, kt, kvh, :]
                                S.op("pe", lambda vv=vv, pA=pA, st_=st_, sp_=sp_: nc.tensor.matmul(PS[4][0:64, :], vv, pA, start=st_, stop=sp_, tile_position=(0, 0)),
                                     ["Vx", f"pb{sl}"], ["ps4"])
                                S.op("pe", lambda vv=vv, pB=pB, st_=st_, sp_=sp_: nc.tensor.matmul(PS[4][64:128, :], vv, pB, start=st_, stop=sp_, tile_position=(0, 64)),
                                     ["Vx", f"pb{sl}"], ["ps4"])
                                S.op("pe", lambda pA=pA, st_=st_, sp_=sp_: nc.tensor.matmul(PS[5][0:64, :], onesb[:], pA, start=st_, stop=sp_, tile_position=(0, 0)),
                                     ["onesb", f"pb{sl}"], ["ps5"])
                                S.op("pe", lambda pB=pB, st_=st_, sp_=sp_: nc.tensor.matmul(PS[5][64:128, :], onesb[:], pB, start=st_, stop=sp_, tile_position=(0, 64)),
                                     ["onesb", f"pb{sl}"], ["ps5"])
                        S.op("dve", lambda: nc.vector.reciprocal(rden[:], PS[5][:, :]), ["ps5"], ["rden"])
                        a_ = n_ % 2
                        S.op("dve", lambda a_=a_: nc.vector.tensor_tensor(ast[a_][:], PS[4][:, :], rden[:], ALU.mult), ["ps4", "rden"], [f"ast{a_}"])
                        dma(att_d[hp * 128:(hp + 1) * 128, qsl], ast[a_][:], [f"ast{a_}"], ["att_d"])
                        n_ += 1
                S.barrier()

        def phase3():
          with ExitStack() as p3:
            load_fft_consts(p3)
            finv = FC["finv"]
            M2s = sb("M2s", [128, 128, 64], BF16, p3)
            M2qs = sb("M2qs", [128, 128, 16], BF16, p3)
            dma(M2s[:], c_M2, [], ["M2s"])
            dma(M2qs[:], c_M2q, [], ["M2qs"])
            RA = sb("RA", [128, 192 * 128 + 8], BF16, p3)
            RB = sb("RB", [128, 128 * 128], BF16, p3)
            RC = sb("RC", [128, 128 * 128], BF16, p3)
            vt = sb("vt", [128, L], BF16, p3)
            x1t = sb("x1t", [128, L], BF16, p3)
            x2q = sb("x2q", [128, NTOK], BF16, p3)
            zzq = sb("zzq", [128, NTOK], BF16, p3)
            hyst = sb("hyst", [128, NTOK], BF16, p3)
            Hc = [sb(f"Hc{i}", [128, 2, 4, 128], BF16, p3) for i in range(4)]
            Pb = [sb(f"Pb{i}", [128, 2, 2, 128], BF16, p3) for i in range(2)]
            Qb = [sb(f"Qb{i}", [128, 2, 2, 128], BF16, p3) for i in range(2)]
            gt = [sb(f"gt{i}", [128, 512], F32, p3) for i in range(2)]
            whi = sb("whi", [128, 9], BF16, p3)
            whf = sb("whf", [128, 9], F32, p3)
            wlf = sb("wlf", [128, 9], F32, p3)
            sct = gt
            uraw = RA[:, 0:3 * 8194].rearrange("p (i t) -> p i t", i=3)
            AT = RA[:, 0:192 * 128].rearrange("p (j c) -> p j c", c=128)
            ztm = RB[:, :].rearrange("p (c n) -> p c n", n=128)
            Eb = RB[:, :].rearrange("p (n c) -> p n c", c=128)
            x2t = RC[:, 0:L]
            ET = RC[:, :].rearrange("p (c k) -> p c k", k=128)
            RAK, RBK, RCK = "RA", "RB", "RC"

            def masked_quarter(dst, src, dkey, skey):
                S.op("dve", lambda: nc.vector.tensor_scalar(dst[:], src[:, 0:NTOK], qms[:, 0:1], None, ALU.mult), [skey, "qms"], [dkey])
                for q in range(1, 4):
                    S.op("dve", lambda q=q: nc.vector.scalar_tensor_tensor(out=dst[:], in0=src[:, q * NTOK:(q + 1) * NTOK], scalar=qms[:, q:q + 1],
                                                                          in1=dst[:], op0=ALU.mult, op1=ALU.add), [skey, "qms", dkey], [dkey])

            RBQ = [(RBK, q4) for q4 in range(4)]

            def bounce_to_ztm(src, skey):
                zv = zcm_d.rearrange("c (n1 n2) -> n1 c n2", n2=128)
                for q4 in range(4):
                    dma(zcm_d[q4 * 32:(q4 + 1) * 32, :], src[q4 * 32:(q4 + 1) * 32, :], [skey], [("zcm_d", q4)])
                for q4 in range(4):
                    dma(ztm[0:64, q4 * 32:(q4 + 1) * 32, :], zv[:, q4 * 32:(q4 + 1) * 32, :], [("zcm_d", q4)], [(RBK, q4)])

            def conv_core(go):
                def load_h(kb):
                    dma(Hc[kb % 4][:], H_d[go, :, kb * 2:kb * 2 + 2], [("H_d", go)], [f"Hc{kb % 4}"])

                def pre(kb):
                    if kb == 0:
                        load_h(0)
                        load_h(1)
                    if kb + 2 < 32:
                        load_h(kb + 2)

                def cb(kb, ps, pk):
                    s_ = kb % 4
                    y_ = kb % 2
                    zv = ps[:, :].rearrange("p (k r c) -> p k r c", k=2, r=2)
                    S.op("dve", lambda: nc.vector.tensor_tensor(Pb[y_][:], zv, Hc[s_][:, :, 0:2, :], ALU.mult), [pk, f"Hc{s_}"], [f"Pb{y_}"])
                    S.op("dve", lambda: nc.vector.tensor_tensor(Qb[y_][:], zv, Hc[s_][:, :, 2:4, :], ALU.mult), [pk, f"Hc{s_}"], [f"Qb{y_}"])

                def cb2(kb):
                    s_ = kb % 4
                    y_ = kb % 2
                    pe_ = PS[4 + kb % 2]
                    pek = f"ps{4 + kb % 2}"
                    er = pe_[:, 0:256].rearrange("p (k c) -> p k c", k=2)
                    ei = pe_[:, 256:512].rearrange("p (k c) -> p k c", k=2)
                    p0, p1 = Pb[y_][:, :, 0, :], Pb[y_][:, :, 1, :]
                    q0, q1 = Qb[y_][:, :, 0, :], Qb[y_][:, :, 1, :]
                    rk = [f"Pb{y_}", f"Qb{y_}", "finv"]
                    FR, FI, NFI, NFR = finv[:, 0, :], finv[:, 1, :], finv[:, 2, :], finv[:, 3, :]
                    mm(er, FR, p0, True, False, rk, [pek]); mm(er, NFR, p1, False, False, rk, [pek])
                    mm(er, NFI, q0, False, False, rk, [pek]); mm(er, NFI, q1, False, True, rk, [pek])
                    mm(ei, FI, p0, True, False, rk, [pek]); mm(ei, NFI, p1, False, False, rk, [pek])
                    mm(ei, FR, q0, False, False, rk, [pek]); mm(ei, FR, q1, False, True, rk, [pek])
                    S.op("act", lambda: nc.scalar.copy(ET[:, :, kb * 2:kb * 2 + 2], pe_[:, 0:256].rearrange("p (k c) -> p c k", k=2)), [pek], [RCK])
                    S.op("act", lambda: nc.scalar.copy(ET[:, :, 64 + kb * 2:64 + kb * 2 + 2], pe_[:, 256:512].rearrange("p (k c) -> p c k", k=2)),
                         [pek], [RCK])
                fft_forward(ztm, 64, AT, cb, zkey=RBK, akey=RAK, pre_batch=pre, cb_late=cb2)
                TB = [(PSB[0], "psb0"), (PSB[1], "psb1"), (PSW[0].bitcast(BF16), "ps0"), (PSW[1].bitcast(BF16), "ps2")]
                for c4 in range(32):
                    pb_, pbk = TB[c4 % 4]
                    for cc in range(4):
                        c = c4 * 4 + cc
                        S.op("pe", lambda c=c, cc=cc, pb_=pb_: nc.tensor.transpose(pb_[:, cc * 128:(cc + 1) * 128], ET[:, c, :], ident[:]),
                             [RCK, "ident"], [pbk])
                    src = pb_[:, 0:512].rearrange("p (cc n) -> p n cc", cc=4)
                    dst = Eb[:, :, c4 * 4:c4 * 4 + 4]
                    if c4 % 2 == 0:
                        S.op("act", lambda src=src, dst=dst: nc.scalar.copy(dst, src), [pbk], RBQ)
                    else:
                        S.op("dve", lambda src=src, dst=dst: nc.vector.tensor_copy(dst, src), [pbk], RBQ)

            Udv = U_d
            for g in range(4):
                for i in range(3):
                    dma(uraw[:, i, 1:L + 1], Udv[i * 512 + g * 128:i * 512 + (g + 1) * 128, :], ["U_d"], [RAK])
                S.op("pool", lambda: nc.gpsimd.memset(uraw[:, :, 0:1], 0.0), [RAK], [RAK])
                S.op("pool", lambda: nc.gpsimd.memset(uraw[:, :, L + 1:L + 2], 0.0), [RAK], [RAK])
                DG = hyst[:, 0:9 * 128].rearrange("p (i m) -> p i m", m=128)
                DGL = zzq[:, 0:9 * 128].rearrange("p (i m) -> p i m", m=128)
                wsel = cws[:, g:12:4, :]
                S.op("dve", lambda wsel=wsel: nc.vector.tensor_copy(whi[:].rearrange("p (i j) -> p i j", j=3), wsel), ["cws"], ["whi"])
                S.op("dve", lambda: nc.vector.tensor_copy(whf[:], whi[:]), ["whi"], ["whf"])
                S.op("dve", lambda wsel=wsel: nc.vector.tensor_tensor(wlf[:].rearrange("p (i j) -> p i j", j=3), wsel,
                                                                     whf[:].rearrange("p (i j) -> p i j", j=3), ALU.subtract), ["cws", "whf"], ["wlf"])
                for idx in range(9):
                    S.op("dve", lambda idx=idx: nc.vector.tensor_scalar(DG[:, idx, :], ident[:], whf[:, idx:idx + 1], None, ALU.mult), ["whf", "ident"], ["hyst"])
                    S.op("dve", lambda idx=idx: nc.vector.tensor_scalar(DGL[:, idx, :], ident[:], wlf[:, idx:idx + 1], None, ALU.mult), ["wlf", "ident"], ["zzq"])
                nq = 0
                for i, (dst, dkey) in enumerate(((vt, "vt"), (x1t, "x1t"), (x2t, RCK))):
                    ct = i * 4 + g
                    for q in range(16):
                        o0 = q * 512
                        ps = PS[nq % 2]
                        pk = f"ps{nq % 2}"
                        for j in range(3):
                            mm(ps[:, :], DG[:, i * 3 + j, :], uraw[:, i, o0 + j:o0 + j + 512], j == 0, False, ["hyst", RAK], [pk])
                        for j in range(3):
                            mm(ps[:, :], DGL[:, i * 3 + j, :], uraw[:, i, o0 + j:o0 + j + 512], False, j == 2, ["zzq", RAK], [pk])
                        if nq % 2 == 0:
                            S.op("act", lambda ps=ps, dst=dst, o0=o0, ct=ct: nc.scalar.activation(dst[:, o0:o0 + 512], ps[:, :], AF.Identity, bias=cbs[:, ct:ct + 1]),
                                 [pk, "cbs"], [dkey])
                        else:
                            S.op("dve", lambda ps=ps, dst=dst, o0=o0, ct=ct: nc.vector.tensor_scalar(dst[:, o0:o0 + 512], ps[:, :], cbs[:, ct:ct + 1], None, ALU.add),
                                 [pk, "cbs"], [dkey])
                        nq += 1
                masked_quarter(x2q, x2t, "x2q", RCK)
                snap("vt0", vt[:], "vt", L)
                snap("x1t0", x1t[:], "x1t", L)
                bounce_to_ztm(vt, "vt")
                conv_core(g * 2 + 0)
                vt3 = vt[:, :].rearrange("p (n1 n2) -> p n1 n2", n2=128)
                x13 = x1t[:, :].rearrange("p (n1 n2) -> p n1 n2", n2=128)
                for nb in range(16):
                    ps = PS[nb % 2]
                    pk = f"ps{nb % 2}"
                    for j in range(8):
                        n2 = nb * 8 + j
                        mm(ps[:, j * 64:(j + 1) * 64], Eb[:, n2, :], M2s[:, n2, :], True, True, RBQ + ["M2s"], [pk])
                    psv = ps[:, :].rearrange("p (j n) -> p n j", j=8)
                    gtv = gt[nb % 2][:, :].rearrange("p (n j) -> p n j", j=8)
                    gk_ = f"gt{nb % 2}"
                    S.op("dve", lambda psv=psv, gtv=gtv, nb=nb, g=g: nc.vector.scalar_tensor_tensor(out=gtv, in0=vt3[:, :, nb * 8:(nb + 1) * 8],
                                                                                             scalar=skps[:, g:g + 1], in1=psv, op0=ALU.mult, op1=ALU.add),
                         [pk, "vt", "skps"], [gk_])
                    S.op("pool", lambda gtv=gtv, nb=nb: nc.gpsimd.tensor_tensor(vt3[:, :, nb * 8:(nb + 1) * 8], gtv, x13[:, :, nb * 8:(nb + 1) * 8], ALU.mult),
                         [gk_, "x1t"], ["vt"])
                snap("zz0", vt[:], "vt", L)
                masked_quarter(zzq, vt, "zzq", "vt")
                bounce_to_ztm(vt, "vt")
                conv_core(g * 2 + 1)
                zq3 = zzq[:, :].rearrange("p (n1 n2) -> p n1 n2", n2=128)
                xq3 = x2q[:, :].rearrange("p (n1 n2) -> p n1 n2", n2=128)
                hy3 = hyst[:, :].rearrange("p (n1 n2) -> p n1 n2", n2=128)
                for nb in range(4):
                    ps = PS[nb % 2]
                    pk = f"ps{nb % 2}"
                    for j in range(32):
                        n2 = nb * 32 + j
                        mm(ps[:, j * 16:(j + 1) * 16], Eb[:, n2, :], M2qs[:, n2, :], True, True, RBQ + ["M2qs"], [pk])
                    psv = ps[:, :].rearrange("p (j n) -> p n j", j=32)
                    gtv = gt[nb % 2][:, :].rearrange("p (n j) -> p n j", j=32)
                    gk_ = f"gt{nb % 2}"
                    S.op("dve", lambda psv=psv, gtv=gtv, nb=nb, g=g: nc.vector.scalar_tensor_tensor(out=gtv, in0=zq3[:, :, nb * 32:(nb + 1) * 32],
                                                                                             scalar=skps[:, 4 + g:5 + g], in1=psv, op0=ALU.mult, op1=ALU.add),
                         [pk, "zzq", "skps"], [gk_])
                    S.op("pool", lambda gtv=gtv, nb=nb: nc.gpsimd.tensor_tensor(hy3[:, :, nb * 32:(nb + 1) * 32], gtv, xq3[:, :, nb * 32:(nb + 1) * 32], ALU.mult),
                         [gk_, "x2q"], ["hyst"])
                dma(hy_d[g * 128:(g + 1) * 128, :], hyst[:], ["hyst"], ["hy_d"])
            S.barrier()

        def phase4():
          with ExitStack() as p4:
            WoA = sb("WoA", [64, 8, D], BF16, p4)
            WoH = sb("WoH", [128, 4, D], BF16, p4)
            with ExitStack() as p4c:
                wsa = sb("wsa", [64, 8, D], F32, p4c)
                wsh = sb("wsh", [128, 4, D], F32, p4c)
                dma(wsa[:], w_out[0:512, :].rearrange("(h d) n -> d h n", d=64), [], ["wsa"])
                dma(wsh[:], w_out[512:1024, :].rearrange("(g p) n -> p g n", p=128), [], ["wsh"])
                S.op("dve", lambda: nc.vector.tensor_copy(WoA[:], wsa[:]), ["wsa"], ["WoA"])
                S.op("pool", lambda: nc.gpsimd.tensor_copy(WoH[:], wsh[:]), ["wsh"], ["WoH"])
                S.barrier()
            attc = sb("attc", [64, 8, 512], BF16, p4)
            hyc = sb("hyc", [128, 4, 512], BF16, p4)
            sqb = sb("sqb4", [128, 8, 512], BF16, p4)
            rs = sb("rs4", [128, 512], F32, p4)
            mixA = sb("mixA", [64, 8, 512], BF16, p4)
            mixH = sb("mixH", [128, 4, 512], BF16, p4)
            h1 = sb("h1", [128, 8, 512], F32, p4)
            mT = sb("mT", [128, 8, 512], BF16, p4)
            Wmi_s = [sb(f"Wmi_s{i}", [128, 8, 1024], BF16, p4) for i in range(2)]
            Wmo_s = [sb(f"Wmo_s{i}", [128, 32, 128], BF16, p4) for i in range(2)]
            fT = sb("fT", [128, 32, 512], BF16, p4)
            rb = [sb(f"rb{i}", [128, 512], F32, p4) for i in range(2)]
            xqv = xTq.rearrange("(k p) t -> p k t", p=128)
            Wmiv = Wmi_d.rearrange("(k p) n -> p k n", p=128)
            outv = outT.rearrange("(k p) t -> p k t", p=128)
            nw = 0
            nwo = 0
            for j in range(4):
                tsl = slice(j * 512, (j + 1) * 512)
                dma(attc[:], att_d[:, tsl].rearrange("(h d) t -> d h t", d=64), ["att_d"], ["attc", "na_src"])
                dma(hyc[:], hy_d[:, tsl].rearrange("(g p) t -> p g t", p=128), ["hy_d"], ["hyc", "nh_src"])
                dma(h1[:], xqv[:, :, tsl], [], ["h1", "n2_src", "nf_src"])
                rms_apply(attc[:], gas, mixA, 1.0 / 512.0, 8, "na", sqb, rs, np_=64)
                rms_apply(hyc[:], ghs, mixH, 1.0 / 512.0, 4, "nh", sqb, rs)
                for ct in range(8):
                    csl = slice(ct * 128, (ct + 1) * 128)
                    for h in range(8):
                        mm(PS[1][:, :], WoA[:, h, csl], mixA[:, h, :], h == 0, False, ["WoA", "na_dst"], ["ps1"])
                    for g in range(4):
                        mm(PS[1][:, :], WoH[:, g, csl], mixH[:, g, :], False, g == 3, ["WoH", "nh_dst"], ["ps1"])
                    S.op("dve", lambda ct=ct: nc.vector.tensor_tensor(h1[:, ct, :], h1[:, ct, :], PS[1][:, :], ALU.add), ["ps1", "h1"], ["h1", "n2_src", "nf_src"])
                rms_apply(h1[:], g2s, mT, 1.0 / D, 8, "n2", sqb, rs)
                for pc in range(4):
                    s_ = nw % 2
                    dma(Wmi_s[s_][:], Wmiv[:, :, pc * 1024:(pc + 1) * 1024], ["Wmid"], [f"Wmi_s{s_}"])
                    nw += 1
                    for ft in range(8):
                        f = pc * 8 + ft
                        pf = PS[2 + f % 2]
                        pfk = f"ps{2 + f % 2}"
                        for k in range(8):
                            mm(pf[:, :], Wmi_s[s_][:, k, ft * 128:(ft + 1) * 128], mT[:, k, :], k == 0, k == 7, [f"Wmi_s{s_}", "n2_dst"], [pfk])
                        r_ = f % 2
                        S.op("act", lambda pf=pf, r_=r_: nc.scalar.activation(rb[r_][:], pf[:, :], AF.Relu), [pfk], [f"rb{r_}"])
                        e = ("pool", "dve")[f % 2]
                        S.op(e, lambda e=e, r_=r_, f=f: V(e).tensor_tensor(fT[:, f, :], rb[r_][:], rb[r_][:], ALU.mult), [f"rb{r_}"], ["fT"])
                for ct in range(8):
                    s_ = nwo % 2
                    dma(Wmo_s[s_][:], Wmo_d[ct], ["Wmod"], [f"Wmo_s{s_}"])
                    nwo += 1
                    po = PS[4 + ct % 2]
                    pok = f"ps{4 + ct % 2}"
                    for f in range(32):
                        mm(po[:, :], Wmo_s[s_][:, f, :], fT[:, f, :], f == 0, f == 31, [f"Wmo_s{s_}", "fT"], [pok])
                    S.op("dve", lambda ct=ct, po=po: nc.vector.tensor_tensor(h1[:, ct, :], h1[:, ct, :], po[:, :], ALU.add), [pok, "h1"], ["h1", "n2_src", "nf_src"])
                rms_apply(h1[:], gfs, h1, 1.0 / D, 8, "nf", sqb, rs)
                dma(outv[:, :, tsl], h1[:], ["h1", "nf_dst"], ["outT"])
            S.barrier()

        if 0 in phases:
            phase0()
        if 1 in phases:
            phase12()
        if 3 in phases:
            phase3()
        if 4 in phases:
            phase4()
        if debug:
            with ExitStack() as pd:
                for name in debug:
                    if name not in ("att_d", "hy_d", "U_d", "H_d0"):
                        continue
                    src_, shp = {"att_d": (att_d, [512, NTOK]), "hy_d": (hy_d, [512, NTOK]), "U_d": (U_d[0:512, :], [512, L]),
                                 "H_d0": (H_d[0:4, :, 0:4].rearrange("a p k r c -> (a p) (k r c)"), [512, 2048])}[name]
                    dbo = nc.dram_tensor("dbg_" + name, shp, BF16, kind="ExternalOutput").ap()
                    dbs = sb("dbs_" + name, [128, 4, shp[1]], BF16, pd)
                    dma(dbs[:], src_.rearrange("(a p) t -> p a t", p=128), [name], ["dbs_" + name])
                    dma(dbo.rearrange("(a p) t -> p a t", p=128), dbs[:], ["dbs_" + name], ["dbg_" + name])
                S.barrier()
        S.barrier()
    return nc, S


def _core_inputs(inp, core):
    T = _tables()
    b, r = core // 4, core % 4
    f = lambda a: np.ascontiguousarray(np.asarray(a, dtype=np.float32))
    x = np.asarray(inp["x"], dtype=np.float32)
    xT = np.ascontiguousarray(x[b].T)
    tq = slice(r * NTOK, (r + 1) * NTOK)
    qm = np.zeros((128, 4), np.float32)
    qm[:, r] = 1.0
    m = {
        "xT": xT, "xTq": np.ascontiguousarray(xT[:, tq]),
        "w_in": f(inp["w_in"][0]), "w_out": f(inp["w_out"][0]), "w_mi": f(inp["w_mlp_in"][0]), "w_mo": f(inp["w_mlp_out"][0]),
        "g1": f(np.asarray(inp["norm1_g"][0]).reshape(8, 128).T), "g2": f(np.asarray(inp["norm2_g"][0]).reshape(8, 128).T),
        "gf": f(np.asarray(inp["final_g"]).reshape(8, 128).T),
        "gq": f(np.asarray(inp["q_norm_g"][0]).reshape(64, 1)), "gk": f(np.asarray(inp["k_norm_g"][0]).reshape(64, 1)),
        "ga": f(np.asarray(inp["attn_out_g"][0]).reshape(8, 64).T), "gh": f(np.asarray(inp["hy_out_g"][0]).reshape(4, 128).T),
        "cw": f(np.asarray(inp["hy_conv_w"][0]).T.reshape(12, 128, 3).transpose(1, 0, 2)),
        "cb": f(np.asarray(inp["hy_conv_b"][0]).reshape(12, 128).T),
        "fw1": f(inp["filt_w1"][0]), "fw2": f(inp["filt_w2"][0]), "fw3": f(inp["filt_w3"][0]), "fw4": f(inp["filt_w4"][0]),
        "fb": f(np.stack([np.asarray(inp["filt_b1"][0]), np.asarray(inp["filt_b2"][0]), np.asarray(inp["filt_b3"][0])], axis=1)),
        "ffreq": f(np.asarray(inp["filt_freq"][0]).reshape(64, 1)),
        "fdel": f(np.asarray(inp["filt_deltas"][0]).reshape(16, 128).T),
        "skp": f(np.asarray(inp["hy_skip_d"][0]).reshape(2, 4, 128).transpose(2, 0, 1).reshape(128, 8)),
        "qmask": qm,
        "c_cos": T["cosT"], "c_sin": T["sinT"],
        "c_cosq": np.ascontiguousarray(T["cosT"][:, tq]), "c_sinq": np.ascontiguousarray(T["sinT"][:, tq]),
        "c_zT": np.ascontiguousarray(np.stack([T["zT"], T["zrT"]])), "c_t": np.ascontiguousarray(np.stack([T["t"], T["trev"]])),
        "c_tb": T["tb"], "c_jt": T["jt"],
        "c_fpack": T["fpack"], "c_G": T["G"], "c_finv": T["finv"], "c_M2": T["M2"],
        "c_M2q": np.ascontiguousarray(T["M2"][:, :, r * 16:(r + 1) * 16]),
        "c_rot": T["rot"], "c_sel": T["sel"], "c_ones": T["ones"], "c_ident": T["ident"],
    }
    return m


def _emit(nc, S):
    from contextlib import ExitStack
    st = ExitStack()
    sems = {e: [st.enter_context(nc.semaphore(f"s_{e}{i}")) for i in range(4)] for e in S.CE}
    dsems = [st.enter_context(nc.semaphore(f"d{i}")) for i in range(S.NDS)]
    info = S.emit(sems, dsems)
    return st, info


def kernel(**inputs):
    nc, S = build_program()
    st, _ = _emit(nc, S)
    with st:
        in_maps = [_core_inputs(inputs, c) for c in range(8)]
        res = run_bass_kernel_spmd(nc, in_maps, core_ids=list(range(8)))
    out = np.empty((NB, L, D), np.float32)
    for c in range(8):
        b, r = c // 4, c % 4
        out[b, r * NTOK:(r + 1) * NTOK, :] = np.asarray(res.results[c]["outT"]).T
    return out
```

```python
import math
import numpy as np
import ml_dtypes
import concourse.bass as bass
import concourse.mybir as mybir
from concourse.bass_utils import run_bass_kernel_spmd

F32 = mybir.dt.float32
BF16 = mybir.dt.bfloat16
ALU = mybir.AluOpType
AF = mybir.ActivationFunctionType
bf16_np = ml_dtypes.bfloat16

D = 1024
L = 8192
NB = 2
NTOK = 2048
HD = 64
NQH = 8
NKV = 2
HYW = 512
DFF = 4096
EPS = 1e-6
NFFT = 2 * L
TWO_PI = 2.0 * math.pi
MAGIC = 12582912.0


class Op:
    __slots__ = ("eng", "fn", "reads", "writes", "dma", "idx", "gid", "waits", "signal", "clock",
                 "sem", "val")


class Sched:
    CE = ("pe", "act", "dve", "pool", "sp")
    EPOCH = 12000
    NDS = 40

    def __init__(self, nc):
        self.nc = nc
        self.E = {"pe": nc.tensor, "act": nc.scalar, "dve": nc.vector, "pool": nc.gpsimd,
                  "sp": nc.sync}
        self.ops = []
        self.last_w = {}
        self.readers = {}
        self.known = {e: {} for e in self.CE}
        self.known_dma = {e: set() for e in self.CE}
        self.cnt = {e: 0 for e in self.CE}
        self.last_op = {e: None for e in self.CE}
        self.live_dma = []

    def op(self, eng, fn, reads=(), writes=(), dma=False):
        o = Op()
        o.eng, o.fn, o.reads, o.writes, o.dma = eng, fn, tuple(reads), tuple(writes), dma
        o.signal = dma
        o.sem = None
        o.val = None
        o.gid = len(self.ops)
        deps = {}
        for k in o.reads:
            w = self.last_w.get(k)
            if w is not None:
                deps[w.gid] = w
        for k in o.writes:
            w = self.last_w.get(k)
            if w is not None:
                deps[w.gid] = w
            for r in self.readers.get(k, ()):
                deps[r.gid] = r
        self._finish(o, deps.values())
        for k in o.writes:
            self.last_w[k] = o
            self.readers[k] = []
        for k in o.reads:
            lst = self.readers.setdefault(k, [])
            if not dma:
                lst[:] = [r for r in lst if r.dma or r.eng != eng]
            lst.append(o)
        return o

    def _finish(self, o, deps):
        eng = o.eng
        kn = self.known[eng]
        kd = self.known_dma[eng]
        waits = []
        best = {}
        rset = set(o.reads)
        for d in deps:
            if d.dma:
                if d.gid not in kd:
                    waits.append(d)
                    kd.add(d.gid)
                continue
            if d.eng == eng:
                if eng in ("pe", "sp"):
                    continue
            if kn.get(d.eng, -1) >= d.idx:
                continue
            b = best.get(d.eng)
            if b is None or b.idx < d.idx:
                best[d.eng] = d
        for d in best.values():
            waits.append(d)
            if kn.get(d.eng, -1) < d.idx:
                kn[d.eng] = d.idx
            for e2, i2 in d.clock.items():
                if kn.get(e2, -1) < i2:
                    kn[e2] = i2
        for d in waits:
            d.signal = True
        o.waits = waits
        o.idx = self.cnt[eng]
        self.cnt[eng] += 1
        o.clock = dict(kn)
        self.ops.append(o)
        if o.fn is not None:
            self.last_op[eng] = o
        if o.dma:
            self.live_dma.append(o)

    def barrier(self):
        lasts = [self.last_op[e] for e in self.CE if self.last_op[e] is not None]
        dmas = list(self.live_dma)
        for e in self.CE:
            o = Op()
            o.eng, o.fn, o.reads, o.writes, o.dma = e, None, (), (), False
            o.signal = False
            o.sem = None
            o.val = None
            o.gid = len(self.ops)
            deps = [d for d in lasts if not d.dma and d.eng != e] + dmas
            self._finish(o, deps)
        self.live_dma = []

    def emit(self, sems, dsems):
        sig = {e: 0 for e in self.CE}
        nd = 0
        for o in self.ops:
            E = self.E[o.eng]
            for d in o.waits:
                E.wait_ge(d.sem, d.val)
            if o.fn is None:
                continue
            if o.dma:
                s = dsems[nd % self.NDS]
                rnd = nd // self.NDS
                if rnd > 0:
                    E.wait_ge(s, 16 * rnd)
                ins = o.fn()
                ins.then_inc(s, 16)
                o.sem, o.val = s, 16 * (rnd + 1)
                nd += 1
            else:
                ins = o.fn()
                if o.signal:
                    c = sig[o.eng]
                    o.sem = sems[o.eng][c // self.EPOCH]
                    o.val = c % self.EPOCH + 1
                    ins.then_inc(o.sem, 1)
                    sig[o.eng] = c + 1
        return sig, nd


_TABLES = None


def _rope_tables():
    rows = L // 64
    row = np.repeat(np.arange(rows, dtype=np.float32), 64)
    col = np.tile(np.arange(64, dtype=np.float32), rows)
    half = HD // 2
    inv_freq = (np.float32(10000.0) ** (-np.arange(0, half, 2, dtype=np.float32) / np.float32(half))).astype(np.float32)
    ang_r = (row[:, None] * inv_freq[None, :]).astype(np.float32)
    ang_c = (col[:, None] * inv_freq[None, :]).astype(np.float32)
    cos_r, sin_r = np.cos(ang_r), np.sin(ang_r)
    cos_c, sin_c = np.cos(ang_c), np.sin(ang_c)
    cosT = np.concatenate([cos_r.T, cos_r.T, cos_c.T, cos_c.T], axis=0).astype(np.float32)
    sinT = np.concatenate([sin_r.T, sin_r.T, sin_c.T, sin_c.T], axis=0).astype(np.float32)
    return np.ascontiguousarray(cosT), np.ascontiguousarray(sinT)


def _filter_pos_tables():
    f32 = np.float32
    t = np.linspace(0.0, 1.0, L, dtype=f32)
    bands = 16
    w_ang = (f32(2.0 * math.pi) * np.arange(L, dtype=f32) / f32(L)).astype(f32)
    band_f = np.linspace(1e-4, bands - 1, bands, dtype=f32)
    ang = (w_ang[:, None] * band_f[None, :]).astype(f32)
    z = np.concatenate([t[:, None], np.cos(ang), -np.sin(ang)], axis=-1).astype(f32)
    zT = np.ascontiguousarray(z.T)
    idx = (L - np.arange(L)) % L
    zrT = np.ascontiguousarray(z[idx].T)
    trev = t[idx].copy()
    return zT, zrT, t.copy(), trev


def _fft_tables():
    N = NFFT
    n1 = np.arange(128, dtype=np.float64)
    k1 = np.arange(64, dtype=np.float64)
    th = 2 * np.pi * np.outer(n1, k1 + 0.5) / 128.0
    fpack = np.stack([np.sin(th), np.cos(th), -np.sin(th)], axis=2).reshape(128, 192)
    n2 = np.arange(128, dtype=np.float64)
    k2 = np.arange(128, dtype=np.float64)
    kap = k1[None, :, None] + 128.0 * k2[None, None, :] + 0.5
    thg = 2 * np.pi * n2[:, None, None] * kap / N
    G = np.stack([np.cos(thg), -np.sin(thg)], axis=2)
    thf = 2 * np.pi * np.outer(k2, n2) / 128.0
    finv = np.stack([np.cos(thf), np.sin(thf), -np.sin(thf), -np.cos(thf)], axis=1)
    n1h = np.arange(64, dtype=np.float64)
    phi = 2 * np.pi * (k1[:, None, None] + 0.5) * (128.0 * n1h[None, None, :] + n2[None, :, None]) / N
    M2 = np.concatenate([(2.0 / N) * np.cos(phi), -(2.0 / N) * np.sin(phi)], axis=0)
    return (fpack.astype(bf16_np), G.astype(bf16_np), finv.astype(bf16_np), M2.astype(bf16_np))


def _tables():
    global _TABLES
    if _TABLES is None:
        cosT, sinT = _rope_tables()
        zT, zrT, t, trev = _filter_pos_tables()
        fpack, G, finv, M2 = _fft_tables()
        ii = np.arange(512, dtype=np.float64)
        jj = np.arange(16, dtype=np.float64)
        tbm = np.stack([ii / (L - 1), (L - ii) / (L - 1)]).astype(np.float32)
        jtm = np.stack([512.0 * jj / (L - 1), -512.0 * jj / (L - 1)]).astype(np.float32)
        rot = np.zeros((64, 64), np.float32)
        for m in range(64):
            if (m % 32) < 16:
                rot[m + 16, m] = -1.0
            else:
                rot[m - 16, m] = 1.0
        sel = np.zeros((65, 64), np.float32)
        sel[64, :] = 1.0
        _TABLES = dict(cosT=cosT, sinT=sinT, zT=zT, zrT=zrT, t=t, trev=trev, fpack=fpack, G=G,
                       finv=finv, M2=M2, rot=rot, sel=sel, tb=tbm, jt=jtm,
                       ones=np.ones((128, 128), np.float32),
                       ident=np.eye(128, dtype=np.float32).astype(bf16_np))
    return _TABLES


def build_program(phases=(0, 1, 2, 3, 4), debug=()):
    nc = bass.Bass("TRN2", target_bir_lowering=False)
    S = Sched(nc)

    def din(name, shape, dt=F32):
        return nc.dram_tensor(name, list(shape), dt, kind="ExternalInput").ap()

    def dscr(name, shape, dt):
        return nc.dram_tensor(name, list(shape), dt, kind="Internal").ap()

    xT = din("xT", [D, L])
    xTq = din("xTq", [D, NTOK])
    w_in = din("w_in", [D, 2304])
    w_out = din("w_out", [D, D])
    w_mi = din("w_mi", [D, DFF])
    w_mo = din("w_mo", [DFF, D])
    g1 = din("g1", [128, 8]); g2 = din("g2", [128, 8]); gf = din("gf", [128, 8])
    gq = din("gq", [64, 1]); gk = din("gk", [64, 1])
    ga = din("ga", [64, 8]); gh = din("gh", [128, 4])
    cw = din("cw", [128, 12, 3]); cb = din("cb", [128, 12])
    fw1 = din("fw1", [33, 64]); fw2 = din("fw2", [64, 64]); fw3 = din("fw3", [64, 64])
    fw4 = din("fw4", [64, 2048])
    fb = din("fb", [64, 3]); ffreq = din("ffreq", [64, 1])
    fdel = din("fdel", [128, 16])
    skp = din("skp", [128, 8])
    qmask = din("qmask", [128, 4])
    c_cos = din("c_cos", [64, L]); c_sin = din("c_sin", [64, L])
    c_cosq = din("c_cosq", [64, NTOK]); c_sinq = din("c_sinq", [64, NTOK])
    c_zT = din("c_zT", [2, 33, L])
    c_t = din("c_t", [2, L])
    c_tb = din("c_tb", [2, 512])
    c_jt = din("c_jt", [2, 16])
    c_fpack = din("c_fpack", [128, 192], BF16)
    c_G = din("c_G", [128, 64, 2, 128], BF16)
    c_finv = din("c_finv", [128, 4, 128], BF16)
    c_M2 = din("c_M2", [128, 128, 64], BF16)
    c_M2q = din("c_M2q", [128, 128, 16], BF16)
    c_rot = din("c_rot", [64, 64]); c_sel = din("c_sel", [65, 64]); c_ones = din("c_ones", [128, 128])
    c_ident = din("c_ident", [128, 128], BF16)
    outT = nc.dram_tensor("outT", [D, NTOK], F32, kind="ExternalOutput").ap()

    U_d = dscr("U_d", [1536, L], BF16)
    H_d = dscr("H_d", [8, 128, 64, 4, 128], BF16)
    taps_d = dscr("taps_d", [128, NFFT], BF16)
    zcm_d = dscr("zcm_d", [128, L], BF16)
    att_d = dscr("att_d", [512, NTOK], BF16)
    hy_d = dscr("hy_d", [512, NTOK], BF16)
    Wmi_d = dscr("Wmi_d", [D, DFF], BF16)
    Wmo_d = dscr("Wmo_d", [8, 128, 32, 128], BF16)

    from contextlib import ExitStack
    es = ExitStack()

    _names = {}

    def sb(name, shape, dt, stack=None):
        n_ = _names.get(name, 0)
        _names[name] = n_ + 1
        nm = name if n_ == 0 else f"{name}_r{n_}"
        return (stack or es).enter_context(nc.sbuf_tensor(nm, list(shape), dt))

    def dma(out, in_, reads, writes, eng="sp"):
        return S.op(eng, lambda: S.E[eng].dma_start(out=out, in_=in_), reads, writes, dma=True)

    def mm(out, lhsT, rhs, start, stop, reads, writes):
        return S.op("pe", lambda: nc.tensor.matmul(out, lhsT, rhs, start=start, stop=stop), reads, writes)

    rr = {"i": 0}

    def alt(engs=("dve", "pool")):
        rr["i"] += 1
        return engs[rr["i"] % len(engs)]

    def V(eng):
        return S.E[eng]

    with es:
        PSW = [es.enter_context(nc.psum_tensor(f"psw{i}", [128, 1024], F32)) for i in range(3)]
        PS = [PSW[i // 2][:, (i % 2) * 512:(i % 2 + 1) * 512] for i in range(6)]
        PSB = [es.enter_context(nc.psum_tensor(f"psb{i}", [128, 1024], BF16)) for i in range(2)]
        PSX = [PSB[i].bitcast(F32)[:, :] for i in range(2)]

        ones = sb("ones", [128, 128], F32)
        ident = sb("ident", [128, 128], BF16)
        rot = sb("rot", [64, 64], F32)
        sel = sb("sel", [65, 64], F32)
        epsT = sb("epsT", [128, 1], F32)
        onesbf = sb("onesbf", [128, 128], BF16)
        g1s = sb("g1s", [128, 8], F32); g2s = sb("g2s", [128, 8], F32); gfs = sb("gfs", [128, 8], F32)
        gqs = sb("gqs", [64, 1], F32); gks = sb("gks", [64, 1], F32)
        gas = sb("gas", [64, 8], F32); ghs = sb("ghs", [128, 4], F32)
        cws = sb("cws", [128, 12, 3], F32); cbs = sb("cbs", [128, 12], F32)
        skps = sb("skps", [128, 8], F32); qms = sb("qms", [128, 4], F32)
        for t_, d_ in ((ones, c_ones), (ident, c_ident), (rot, c_rot), (sel, c_sel), (g1s, g1), (g2s, g2),
                       (gfs, gf), (gqs, gq), (gks, gk), (gas, ga), (ghs, gh), (cws, cw), (cbs, cb),
                       (skps, skp), (qms, qmask)):
            dma(t_[:], d_, [], [t_.name if hasattr(t_, "name") else id(t_)])
        KEY = lambda t_: t_.name if hasattr(t_, "name") else id(t_)
        S.op("dve", lambda: nc.vector.memset(epsT[:], EPS), [], ["epsT"])
        S.op("pool", lambda: nc.gpsimd.memset(onesbf[:], 1.0), [], ["onesbf"])

        FC = {}
        SNAP = {}

        def snap(name, src_ap, key, ncols):
            if name not in debug or name in SNAP:
                return
            SNAP[name] = nc.dram_tensor("dbg_" + name, [128, ncols], BF16, kind="ExternalOutput").ap()
            dma(SNAP[name], src_ap, [key], ["dbg_" + name])


        def load_fft_consts(stack):
            FC["fpack"] = sb("fpack", [128, 192], BF16, stack)
            FC["finv"] = sb("finv", [128, 4, 128], BF16, stack)
            FC["Gq"] = [sb(f"Gq{i}", [128, 8, 2, 128], BF16, stack) for i in range(2)]
            dma(FC["fpack"][:], c_fpack, [], ["fpack"])
            dma(FC["finv"][:], c_finv, [], ["finv"])

        def rsqrt(out, in_, scale, reads, writes, np_=128):
            S.op("act", lambda: nc.scalar.activation(out, in_, AF.Sqrt, bias=epsT[0:np_, :], scale=scale),
                 list(reads) + ["epsT"], writes)
            S.op("dve", lambda: nc.vector.reciprocal(out, out), writes, writes)

        def fft_forward(ztm, K, AT, cb_batch, zkey="ztm", akey="AT", pre_batch=None, cb_late=None):
            fpack = FC["fpack"]
            for c2 in range(64):
                ps = PS[c2 % 4]
                pk = f"ps{c2 % 4}"
                for cc in range(2):
                    c = c2 * 2 + cc
                    mm(ps[:, cc * 192:(cc + 1) * 192], ztm[0:K, c, :], fpack[0:K, :], True, True,
                       [(zkey, c // 32), "fpack"], [pk])
                e = alt(("act", "dve"))
                src = ps[:, 0:384].rearrange("p (cc j) -> p j cc", cc=2)
                dst = AT[:, :, c2 * 2:c2 * 2 + 2]
                if e == "act":
                    S.op("act", lambda dst=dst, src=src: nc.scalar.copy(dst, src), [pk], [akey])
                else:
                    S.op("dve", lambda dst=dst, src=src: nc.vector.tensor_copy(dst, src), [pk], [akey])

            def load_g(q8):
                dma(FC["Gq"][q8 % 2][:], c_G[:, q8 * 8:(q8 + 1) * 8], [], [f"Gq{q8 % 2}"])

            load_g(0)
            for kb in range(32):
                q4 = kb // 4
                Gq = FC["Gq"][q4 % 2]
                gkey = f"Gq{q4 % 2}"
                if kb % 4 == 0 and q4 + 1 < 8:
                    load_g(q4 + 1)
                if pre_batch is not None:
                    pre_batch(kb)
                ps = PS[2 + kb % 2]
                pk = f"ps{2 + kb % 2}"
                for kk in range(2):
                    k1 = kb * 2 + kk
                    kl = k1 % 8
                    zp = ps[:, kk * 256:(kk + 1) * 256].rearrange("p (r c) -> p r c", r=2)
                    mm(zp, Gq[:, kl, 0, :], AT[:, 3 * k1 + 1:3 * k1 + 3, :], True, False, [akey, gkey], [pk])
                    mm(zp, Gq[:, kl, 1, :], AT[:, 3 * k1:3 * k1 + 2, :], False, True, [akey, gkey], [pk])
                cb_batch(kb, ps, pk)
                if cb_late is not None and kb >= 1:
                    cb_late(kb - 1)
            if cb_late is not None:
                cb_late(31)

        cast_jobs = []
        for cbk in range(8):
            cast_jobs.append((w_mi.rearrange("(k p) n -> p k n", p=128)[:, :, cbk * 512:(cbk + 1) * 512],
                              Wmi_d.rearrange("(k p) n -> p k n", p=128)[:, :, cbk * 512:(cbk + 1) * 512], None))
        for kr in range(4):
            for cbk in range(2):
                cast_jobs.append((w_mo.rearrange("(k p) n -> p k n", p=128)[:, kr * 8:(kr + 1) * 8, cbk * 512:(cbk + 1) * 512],
                                  None, (kr, cbk)))

        def emit_cast(n, wst, wbf):
            src, dst, tl = cast_jobs[n]
            s_ = n % 2
            dma(wst[s_][:], src, [], [f"wst{s_}"])
            S.op("dve", lambda s_=s_: nc.vector.tensor_copy(wbf[s_][:], wst[s_][:]), [f"wst{s_}"], [f"wbf{s_}"])
            if tl is None:
                dma(dst, wbf[s_][:], [f"wbf{s_}"], ["Wmid"])
            else:
                kr_, cbk_ = tl
                for c_ in range(4):
                    dma(Wmo_d[cbk_ * 4 + c_, :, kr_ * 8:(kr_ + 1) * 8, :], wbf[s_][:, :, c_ * 128:(c_ + 1) * 128], [f"wbf{s_}"], ["Wmod"])

        def phase0():
          with ExitStack() as p0:
            load_fft_consts(p0)

            w1s = sb("w1s", [33, 64], F32, p0); w2s = sb("w2s", [64, 64], F32, p0); w3s = sb("w3s", [64, 64], F32, p0)
            w4b = sb("w4b", [64, 2048], BF16, p0)
            fbs = sb("fbs", [64, 3], F32, p0); frs = sb("frs", [64, 1], F32, p0); ffb = sb("ffb", [64, 3], F32, p0)
            dls = sb("dls", [128, 16], F32, p0); ndl = sb("ndl", [128, 16], F32, p0)
            negpi = sb("negpi", [128, 1], F32, p0)
            H3 = sb("H3", [64, 2, L], BF16, p0)
            p0m = ExitStack()
            w4s = sb("w4s", [64, 2048], F32, p0m)
            for t_, d_, k_ in ((w1s, fw1, "w1s"), (w2s, fw2, "w2s"), (w3s, fw3, "w3s"), (w4s, fw4, "w4s"),
                               (fbs, fb, "fbs"), (frs, ffreq, "frs"), (dls, fdel, "dls")):
                dma(t_[:], d_, [], [k_])
            S.op("dve", lambda: nc.vector.tensor_copy(w4b[:], w4s[:]), ["w4s"], ["w4b"])
            S.op("dve", lambda: nc.vector.tensor_scalar(ffb[:], fbs[:], frs[:, 0:1], None, ALU.mult), ["fbs", "frs"], ["ffb"])
            S.op("dve", lambda: nc.vector.tensor_scalar(ndl[:], dls[:], -1.0, None, ALU.mult), ["dls"], ["ndl"])
            S.op("dve", lambda: nc.vector.tensor_tensor(ndl[:], ndl[:], dls[:], ALU.min), ["dls", "ndl"], ["ndl"])
            S.op("dve", lambda: nc.vector.memset(negpi[:], 0.0), [], ["negpi"])
            NCH = 8
            zc = [sb(f"zc{i}", [33, 512], F32, p0m) for i in range(NCH)]
            ybL = [sb(f"yb{i}", [64, 512], F32, p0m) for i in range(NCH)]
            ttL = [sb(f"tt{i}", [64, 512], F32, p0m) for i in range(NCH)]
            nrL = [sb(f"nr{i}", [64, 512], F32, p0m) for i in range(NCH)]
            fhL = [sb(f"fh{i}", [64, 512], F32, p0m) for i in range(NCH)]
            frs2 = sb("frs2", [64, 1], F32, p0m)
            ffb2 = sb("ffb2", [64, 3], F32, p0m)
            S.op("dve", lambda: nc.vector.tensor_scalar(frs2[:], frs[:], 1.0 / TWO_PI, None, ALU.mult), ["frs"], ["frs2"])
            S.op("dve", lambda: nc.vector.tensor_scalar(ffb2[:], fbs[:], frs2[:, 0:1], None, ALU.mult), ["fbs", "frs2"], ["ffb2"])
            BANKS = [(PS[i], f"ps{i}") for i in range(6)] + [(PSX[0], "psb0"), (PSX[1], "psb1")]
            wL = [(w1s, "w1s"), (w2s, "w2s"), (w3s, "w3s")]

            def mlp_layer(c_, li, var, j):
                psm, pmk = BANKS[c_]
                src, srck = (zc[c_], f"zc{c_}") if li == 0 else (fhL[c_], f"fh{c_}")
                w_, wk_ = wL[li]
                mm(psm[0:64, :], w_[:], src[:], True, True, [srck, wk_], [pmk])
                S.op("act", lambda: nc.scalar.activation(ybL[c_][:], psm[0:64, :], AF.Identity, bias=ffb2[:, li:li + 1], scale=frs2[:, 0:1]),
                     [pmk, "frs2", "ffb2"], [f"yb{c_}"])
                S.op("dve", lambda: nc.vector.tensor_scalar(ttL[c_][:], ybL[c_][:], MAGIC, None, ALU.add), [f"yb{c_}"], [f"tt{c_}"])
                S.op("dve", lambda: nc.vector.scalar_tensor_tensor(out=nrL[c_][:], in0=ttL[c_][:], scalar=MAGIC, in1=ybL[c_][:],
                                                                  op0=ALU.subtract, op1=ALU.subtract), [f"tt{c_}", f"yb{c_}"], [f"nr{c_}"])
                if li < 2:
                    dst, dk = fhL[c_][:], f"fh{c_}"
                else:
                    dst, dk = H3[:, var, j * 512:(j + 1) * 512], ("H3", var)
                S.op("act", lambda: nc.scalar.activation(dst, nrL[c_][:], AF.Sin, bias=negpi[0:64, :], scale=-TWO_PI), [f"nr{c_}", "negpi"], [dk])

            for jg in range(4):
                chains = [(c_, c_ % 2, jg * 4 + c_ // 2) for c_ in range(NCH)]
                for c_, var, j in chains:
                    dma(zc[c_][:], c_zT[var, :, j * 512:(j + 1) * 512], [], [f"zc{c_}"])
                for li in range(3):
                    for c_, var, j in chains:
                        mlp_layer(c_, li, var, j)

            S.barrier()
            p0m.close()
            tapsUL = [sb(f"tapsU{i}", [128, NFFT], BF16, p0) for i in range(2)]
            ztmf = sb("ztmf", [128, 128, 128], BF16, p0)
            ATf = sb("ATf", [128, 192, 128], BF16, p0)
            tb = sb("tb", [128, 2, 512], F32, p0)
            jt = sb("jt", [128, 2, 16], F32, p0)
            for hf in range(2):
                dma(tb[:, hf, :], c_tb[hf:hf + 1, :].broadcast_to([128, 512]), [], ["tb"])
                dma(jt[:, hf, :], c_jt[hf:hf + 1, :].broadcast_to([128, 16]), [], ["jt"])
            wbase1 = sb("wbase", [128, 512], F32, p0)
            wbase = [wbase1, wbase1]
            wcj = [sb(f"wcj{i}", [128, 16], F32, p0) for i in range(2)]
            tpf = [sb(f"tpf{i}", [128, 512], F32, p0) for i in range(2)]
            psum_ = sb("psum_", [128, 32], F32, p0)
            nrm = sb("nrm", [128, 2], F32, p0)
            Hst = [sb(f"Hst{i}", [128, 2, 4, 128], BF16, p0) for i in range(2)]

            def taps_chunk(go, cidx):
                g, o = go // 2, go % 2
                half, j = cidx // 16, cidx % 16
                tU = tapsUL[go % 2]
                tk = f"tapsU{go % 2}"
                col0 = o * 1024 + half * 512 + g * 128
                ct = col0 // 128
                if j == 0:
                    S.op("act", lambda: nc.scalar.activation(wbase[half][:], tb[:, half, :], AF.Exp, scale=ndl[:, ct:ct + 1]),
                         ["tb", "ndl"], ["wbase"])
                    S.op("act", lambda: nc.scalar.activation(wcj[half][:], jt[:, half, :], AF.Exp, scale=ndl[:, ct:ct + 1]),
                         ["jt", "ndl"], [f"wcj{half}"])
                s_ = j % 2
                pst = PS[4 + s_]
                ptk = f"ps{4 + s_}"
                mm(pst[:, :], w4b[:, col0:col0 + 128], H3[:, half, j * 512:(j + 1) * 512], True, True, [("H3", half), "w4b"], [ptk])
                S.op("dve", lambda: nc.vector.scalar_tensor_tensor(out=tpf[s_][:], in0=pst[:, :], scalar=wcj[half][:, j:j + 1], in1=wbase[half][:],
                                                                  op0=ALU.mult, op1=ALU.mult), [ptk, f"wcj{half}", "wbase"], [f"tpf{s_}"])
                if half == 1 and j == 0:
                    S.op("dve", lambda: nc.vector.memset(tpf[s_][:, 0:1], 0.0), [f"tpf{s_}"], [f"tpf{s_}"])
                S.op("dve", lambda: nc.vector.tensor_reduce(psum_[:, cidx:cidx + 1], tpf[s_][:], mybir.AxisListType.X, ALU.add,
                                                            apply_absolute_value=True), [f"tpf{s_}"], ["psum_"])
                S.op("act", lambda: nc.scalar.copy(tU[:, half * L + j * 512: half * L + (j + 1) * 512], tpf[s_][:]), [f"tpf{s_}"], [tk])

            def taps_finish(go):
                tU = tapsUL[go % 2]
                tk = f"tapsU{go % 2}"
                S.op("dve", lambda: nc.vector.tensor_reduce(nrm[:, 0:1], psum_[:], mybir.AxisListType.X, ALU.add), ["psum_"], ["nrm"])
                S.op("dve", lambda: nc.vector.reciprocal(nrm[:, 0:1], nrm[:, 0:1]), ["nrm"], ["nrm"])
                S.op("dve", lambda: nc.vector.tensor_scalar(nrm[:, 1:2], nrm[:, 0:1], -1.0, None, ALU.mult), ["nrm"], ["nrm"])
                S.op("dve", lambda: nc.vector.tensor_scalar(tU[:, 0:L], tU[:, 0:L], nrm[:, 0:1], None, ALU.mult), [tk, "nrm"], [tk])
                S.op("act", lambda: nc.scalar.mul(tU[:, L:NFFT], tU[:, L:NFFT], nrm[:, 1:2]), [tk, "nrm"], [tk])
                tv = taps_d.rearrange("c (n1 n2) -> n1 c n2", n2=128)
                for q4 in range(4):
                    dma(taps_d[q4 * 32:(q4 + 1) * 32, :], tU[q4 * 32:(q4 + 1) * 32, :], [tk], [("taps_d", q4)], eng="pool")
                for q4 in range(4):
                    dma(ztmf[:, q4 * 32:(q4 + 1) * 32, :], tv[:, q4 * 32:(q4 + 1) * 32, :], [("taps_d", q4)], [("ztm", q4)], eng="pool")

            for cidx in range(32):
                taps_chunk(0, cidx)
            taps_finish(0)
            for go in range(8):
                def cb_filter(kb, ps, pk, go=go):
                    s_ = kb % 2
                    zv = ps[:, :].rearrange("p (k r c) -> p k r c", k=2, r=2)
                    S.op("act", lambda: nc.scalar.copy(Hst[s_][:, :, 0:2, :], zv), [pk], [f"Hst{s_}"])
                    S.op("dve", lambda: nc.vector.tensor_copy(Hst[s_][:, :, 2, :], zv[:, :, 1, :]), [pk], [f"Hst{s_}"])
                    S.op("dve", lambda: nc.vector.tensor_copy(Hst[s_][:, :, 3, :], zv[:, :, 0, :]), [pk], [f"Hst{s_}"])
                    dma(H_d[go, :, kb * 2:kb * 2 + 2], Hst[s_][:], [f"Hst{s_}"], [("H_d", go)])

                def pre_filter(kb, go=go):
                    if go + 1 < 8:
                        if kb < 16:
                            taps_chunk(go + 1, 2 * kb)
                            taps_chunk(go + 1, 2 * kb + 1)
                        elif kb == 16:
                            taps_finish(go + 1)

                fft_forward(ztmf, 128, ATf, cb_filter, pre_batch=pre_filter)
            S.barrier()
        def rms_apply(src, gvec, dst, scale, nk, tag, sqb, rs, np_=128, ss_ps=0, sqk="sqb", rsk="rs"):
            rms_a(src, nk, tag, sqb, np_, sqk)
            rms_b(src, gvec, dst, scale, nk, tag, sqb, rs, np_, ss_ps, sqk, rsk)

        def rms_a(src, nk, tag, sqb, np_=128, sqk="sqb"):
            S.op("act", lambda: nc.scalar.activation(sqb[0:np_, 0:nk, :], src, AF.Square), [tag + "_src"], [sqk])

        def rms_b(src, gvec, dst, scale, nk, tag, sqb, rs, np_=128, ss_ps=0, sqk="sqb", rsk="rs"):
            for k in range(nk):
                mm(PS[ss_ps][0:np_, :], onesbf[0:np_, 0:np_], sqb[0:np_, k, :], k == 0, k == nk - 1, [sqk, "onesbf"], [f"ps{ss_ps}"])
            S.op("act", lambda: nc.scalar.activation(rs[0:np_, :], PS[ss_ps][0:np_, :], AF.Sqrt, bias=epsT[0:np_, :], scale=scale),
                 [f"ps{ss_ps}", "epsT"], [rsk])
            S.op("dve", lambda: nc.vector.reciprocal(rs[0:np_, :], rs[0:np_, :]), [rsk], [rsk])
            for k in range(nk):
                S.op("dve", lambda k=k: nc.vector.scalar_tensor_tensor(out=dst[:, k, :], in0=src[:, k, :], scalar=gvec[:, k:k + 1],
                                                                      in1=rs[0:np_, :], op0=ALU.mult, op1=ALU.mult),
                     [tag + "_src", rsk, "gvecs"], [tag + "_dst"])

        def phase12():
          with ExitStack() as p12:
            KT = sb("KT", [128, 2, L], BF16, p12)
            Vx = sb("Vx", [128, 64, 2, 64], BF16, p12)
            QT = sb("QT", [128, 4, NTOK], BF16, p12)
            onesb = sb("onesb", [128, 64], BF16, p12)
            S.op("pool", lambda: nc.gpsimd.memset(onesb[:], 1.0), [], ["onesb"])
            with ExitStack() as p1:
                Wb = sb("Wb", [128, 8, 2304], BF16, p1)
                with ExitStack() as p1c:
                    wst = [sb(f"wsi{i}", [128, 8, 384], F32, p1c) for i in range(2)]
                    wv = w_in.rearrange("(k p) n -> p k n", p=128)
                    for bk in range(6):
                        s_ = bk % 2
                        dma(wst[s_][:], wv[:, :, bk * 384:(bk + 1) * 384], [], [f"wst{s_}"])
                        if bk % 2 == 0:
                            S.op("dve", lambda s_=s_, bk=bk: nc.vector.tensor_copy(Wb[:, :, bk * 384:(bk + 1) * 384], wst[s_][:]), [f"wst{s_}"], ["Wb"])
                        else:
                            S.op("act", lambda s_=s_, bk=bk: nc.scalar.copy(Wb[:, :, bk * 384:(bk + 1) * 384], wst[s_][:]), [f"wst{s_}"], ["Wb"])
                    S.barrier()
                xin = [sb(f"xin{i}", [128, 8, 512], F32, p1) for i in range(2)]
                sqbL = [sb(f"sqb{i}", [128, 8, 512], BF16, p1) for i in range(2)]
                rsL = [sb(f"rs{i}", [128, 512], F32, p1) for i in range(2)]
                aTL = [sb(f"aT{i}", [128, 8, 512], BF16, p1) for i in range(2)]
                ust = sb("ust", [128, 12, 512], BF16, p1)
                hsq = sb("hsq", [64, 512], BF16, p1); hrs = sb("hrs", [64, 512], F32, p1); hkn = sb("hkn", [64, 512], F32, p1)
                ht1 = sb("ht1", [64, 512], F32, p1); ht2 = sb("ht2", [64, 512], F32, p1)
                csb = [sb(f"csb{i}", [64, 2, 512], F32, p1) for i in range(2)]
                qstg = [sb(f"qstg{i}", [64, 512], BF16, p1) for i in range(2)]

                def hp1(srcb, gvec):
                    src = PS[srcb][0:64, :]
                    sk_ = f"ps{srcb}"
                    S.op("act", lambda: nc.scalar.activation(hsq[:], src, AF.Square), [sk_], ["hsq"])
                    mm(PS[3][0:64, :], onesbf[0:64, 0:64], hsq[:], True, True, ["hsq", "onesbf"], ["ps3"])
                    S.op("act", lambda: nc.scalar.activation(hrs[:], PS[3][0:64, :], AF.Sqrt, bias=epsT[0:64, :], scale=1.0 / 64.0),
                         ["ps3", "epsT"], ["hrs"])
                    S.op("dve", lambda: nc.vector.reciprocal(hrs[:], hrs[:]), ["hrs"], ["hrs"])
                    S.op("dve", lambda: nc.vector.scalar_tensor_tensor(out=hkn[:], in0=src, scalar=gvec[:, 0:1], in1=hrs[:],
                                                                      op0=ALU.mult, op1=ALU.mult), [sk_, "hrs", "gvecs"], ["hkn"])

                def hp2(cs, cskey, dst, dstkey):
                    mm(PS[3][0:64, :], rot[:], hkn[:], True, True, ["hkn", "rot"], ["ps3"])
                    S.op("pool", lambda: nc.gpsimd.tensor_tensor(ht1[:], hkn[:], cs[:, 0, :], ALU.mult), ["hkn", cskey], ["ht1"])
                    S.op("dve", lambda: nc.vector.tensor_tensor(ht2[:], PS[3][0:64, :], cs[:, 1, :], ALU.mult), ["ps3", cskey], ["ht2"])
                    S.op("pool", lambda: nc.gpsimd.tensor_tensor(dst, ht1[:], ht2[:], ALU.add), ["ht1", "ht2"], [dstkey])

                xv = xT.rearrange("(k p) t -> p k t", p=128)
                xqv = xTq.rearrange("(k p) t -> p k t", p=128)
                Uv = U_d.rearrange("(ct p) t -> p ct t", p=128)
                work = [("seq", j) for j in range(16)] + [("own", j) for j in range(4)]

                def norm_a(w):
                    kind, j = work[w]
                    s_ = w % 2
                    tsl = slice(j * 512, (j + 1) * 512)
                    srcv, cc, ss_ = (xv, c_cos, c_sin) if kind == "seq" else (xqv, c_cosq, c_sinq)
                    dma(xin[s_][:], srcv[:, :, tsl], [], [f"xin{s_}", f"n1{s_}_src"])
                    dma(csb[s_][:, 0, :], cc[:, tsl], [], [f"csb{s_}"])
                    dma(csb[s_][:, 1, :], ss_[:, tsl], [], [f"csb{s_}"])
                    rms_a(xin[s_][:], 8, f"n1{s_}", sqbL[s_], 128, f"sqb{s_}")

                def norm_b(w):
                    s_ = w % 2
                    rms_b(xin[s_][:], g1s, aTL[s_], 1.0 / D, 8, f"n1{s_}", sqbL[s_], rsL[s_], 128, 0, f"sqb{s_}", f"rs{s_}")

                def emit_proj(w, mid_hook):
                    kind, j = work[w]
                    s_ = w % 2
                    aT = aTL[s_]
                    ak = f"n1{s_}_dst"
                    cs, csk = csb[s_], f"csb{s_}"
                    tsl = slice(j * 512, (j + 1) * 512)
                    if kind == "seq":
                        for kvh in range(2):
                            for k in range(8):
                                mm(PS[1 + kvh][0:64, :], Wb[:, k, 512 + kvh * 64:512 + (kvh + 1) * 64], aT[:, k, :], k == 0, k == 7,
                                   ["Wb", ak], [f"ps{1 + kvh}"])
                        for tt in range(4):
                            for k in range(8):
                                mm(PS[4][:, tt * 128:(tt + 1) * 128], aT[:, k, tt * 128:(tt + 1) * 128], Wb[:, k, 640:768], k == 0, k == 7,
                                   ["Wb", ak], ["ps4"])
                        S.op("act", lambda j=j: nc.scalar.copy(Vx[:, j * 4:(j + 1) * 4, :, :],
                                                             PS[4][:, :].rearrange("p (t h d) -> p t h d", t=4, h=2)), ["ps4"], ["Vx"])
                        for ct in range(12):
                            pu, puk = (PS[5], "ps5") if ct % 2 == 0 else (PSX[0], "psb0")
                            for k in range(8):
                                mm(pu[:, :], Wb[:, k, 768 + ct * 128:768 + (ct + 1) * 128], aT[:, k, :], k == 0, k == 7, ["Wb", ak], [puk])
                            if ct % 2 == 0:
                                S.op("act", lambda ct=ct, pu=pu: nc.scalar.copy(ust[:, ct, :], pu[:, :]), [puk], ["ust"])
                            else:
                                S.op("dve", lambda ct=ct, pu=pu: nc.vector.tensor_copy(ust[:, ct, :], pu[:, :]), [puk], ["ust"])
                            if ct == 1:
                                hp1(1, gks)
                            if ct == 3:
                                hp2(cs, csk, KT[0:64, 0, tsl], "KT")
                                hp1(2, gks)
                            if ct == 5:
                                hp2(cs, csk, KT[0:64, 1, tsl], "KT")
                            if ct == 7:
                                mid_hook()
                        dma(Uv[:, :, tsl], ust[:], ["ust"], ["U_d"])
                    else:
                        def qdst(h):
                            if h % 2 == 0:
                                return QT[0:64, h // 2, tsl], "QT", None
                            qs_ = (h // 2) % 2
                            return qstg[qs_][:], f"qstg{qs_}", qs_

                        def qproj(h):
                            hb = 1 + h % 2
                            for k in range(8):
                                mm(PS[hb][0:64, :], Wb[:, k, h * 64:(h + 1) * 64], aT[:, k, :], k == 0, k == 7, ["Wb", ak], [f"ps{hb}"])

                        qproj(0)
                        for h in range(8):
                            hp1(1 + h % 2, gqs)
                            if h + 1 < 8:
                                qproj(h + 1)
                            if h == 3:
                                mid_hook()
                            d_, dk_, qs_ = qdst(h)
                            hp2(cs, csk, d_, dk_)
                            if qs_ is not None:
                                dma(QT[64:128, h // 2, tsl], qstg[qs_][:], [f"qstg{qs_}"], ["QT2"])

                norm_a(0)
                norm_b(0)
                for w in range(len(work)):
                    if w + 1 < len(work):
                        norm_a(w + 1)
                        emit_proj(w, lambda w=w: norm_b(w + 1))
                    else:
                        emit_proj(w, lambda: None)
                dma(KT[64:128, :, :], KT[0:64, :, :], ["KT"], ["KT2"])
                S.barrier()
            if 2 in phases:
              with ExitStack() as p2:
                pb = [sb(f"pb{i}", [128, 1024], BF16, p2) for i in range(2)]
                cwst = [sb(f"wst{i}", [128, 8, 512], F32, p2) for i in range(2)]
                cwbf = [sb(f"wbf{i}", [128, 8, 512], BF16, p2) for i in range(2)]
                rden = sb("rden", [128, 512], F32, p2)
                ast = [sb(f"ast{i}", [128, 512], BF16, p2) for i in range(2)]
                n_ = 0
                for qc in range(4):
                    qsl = slice(qc * 512, (qc + 1) * 512)
                    for hp in range(4):
                        kvh = hp // 2
                        emit_cast(qc * 4 + hp, cwst, cwbf)
                        for step in range(64 + 1):
                            if step < 64:
                                kt = step
                                sl = kt % 2
                                ksl = slice(kt * 128, (kt + 1) * 128)
                                mm(PS[2 * sl][:, :], KT[0:64, kvh, ksl], QT[0:64, hp, qsl], True, True, ["KT", "QT"], [f"ps{2 * sl}"])
                                mm(PS[2 * sl + 1][:, :], KT[64:128, kvh, ksl], QT[64:128, hp, qsl], True, True, ["KT2", "QT2"], [f"ps{2 * sl + 1}"])
                                S.op("act", lambda sl=sl: nc.scalar.activation(pb[sl][:], PSW[sl][:, :], AF.Exp, scale=0.125),
                                     [f"ps{2 * sl}", f"ps{2 * sl + 1}"], [f"pb{sl}"])
                            kt = step - 1
                            if kt >= 0:
                                sl = kt % 2
                                st_, sp_ = kt == 0, kt == 63
                                pA = pb[sl][:, 0:512]
                                pB = pb[sl][:, 512:1024]
                                vv = Vx[:, kt, kvh, :]
                                S.op("pe", lambda vv=vv, pA=pA, st_=st_, sp_=sp_: nc.tensor.matmul(PS[4][0:64, :], vv, pA, start=st_, stop=sp_, tile_position=(0, 0)),
                                     ["Vx", f"pb{sl}"], ["ps4"])
                                S.op("pe", lambda vv=vv, pB=pB, st_=st_, sp_=sp_: nc.tensor.matmul(PS[4][64:128, :], vv, pB, start=st_, stop=sp_, tile_position=(0, 64)),
                                     ["Vx", f"pb{sl}"], ["ps4"])
                                S.op("pe", lambda pA=pA, st_=st_, sp_=sp_: nc.tensor.matmul(PS[5][0:64, :], onesb[:], pA, start=st_, stop=sp_, tile_position=(0, 0)),
                                     ["onesb", f"pb{sl}"], ["ps5"])
                                S.op("pe", lambda pB=pB, st_=st_, sp_=sp_: nc.tensor.matmul(PS[5][64:128, :], onesb[:], pB, start=st_, stop=sp_, tile_position=(0, 64)),
                                     ["onesb", f"pb{sl}"], ["ps5"])
                        S.op("dve", lambda: nc.vector.reciprocal(rden[:], PS[5][:, :]), ["ps5"], ["rden"])
                        a_ = n_ % 2
                        S.op("dve", lambda a_=a_: nc.vector.tensor_tensor(ast[a_][:], PS[4][:, :], rden[:], ALU.mult), ["ps4", "rden"], [f"ast{a_}"])
                        dma(att_d[hp * 128:(hp + 1) * 128, qsl], ast[a_][:], [f"ast{a_}"], ["att_d"])
                        n_ += 1
                S.barrier()

        def phase3():
          with ExitStack() as p3:
            load_fft_consts(p3)
            finv = FC["finv"]
            M2s = sb("M2s", [128, 128, 64], BF16, p3)
            M2qs = sb("M2qs", [128, 128, 16], BF16, p3)
            dma(M2s[:], c_M2, [], ["M2s"])
            dma(M2qs[:], c_M2q, [], ["M2qs"])
            RA = sb("RA", [128, 192 * 128 + 8], BF16, p3)
            RB = sb("RB", [128, 128 * 128], BF16, p3)
            RC = sb("RC", [128, 128 * 128], BF16, p3)
            vt = sb("vt", [128, L], BF16, p3)
            x1t = sb("x1t", [128, L], BF16, p3)
            x2q = sb("x2q", [128, NTOK], BF16, p3)
            zzq = sb("zzq", [128, NTOK], BF16, p3)
            hyst = sb("hyst", [128, NTOK], BF16, p3)
            Hc = [sb(f"Hc{i}", [128, 2, 4, 128], BF16, p3) for i in range(4)]
            Pb = [sb(f"Pb{i}", [128, 2, 2, 128], BF16, p3) for i in range(2)]
            Qb = [sb(f"Qb{i}", [128, 2, 2, 128], BF16, p3) for i in range(2)]
            gt = [sb(f"gt{i}", [128, 512], F32, p3) for i in range(2)]
            whi = sb("whi", [128, 9], BF16, p3)
            whf = sb("whf", [128, 9], F32, p3)
            wlf = sb("wlf", [128, 9], F32, p3)
            sct = gt
            uraw = RA[:, 0:3 * 8194].rearrange("p (i t) -> p i t", i=3)
            AT = RA[:, 0:192 * 128].rearrange("p (j c) -> p j c", c=128)
            ztm = RB[:, :].rearrange("p (c n) -> p c n", n=128)
            Eb = RB[:, :].rearrange("p (n c) -> p n c", c=128)
            x2t = RC[:, 0:L]
            ET = RC[:, :].rearrange("p (c k) -> p c k", k=128)
            RAK, RBK, RCK = "RA", "RB", "RC"

            def masked_quarter(dst, src, dkey, skey):
                S.op("dve", lambda: nc.vector.tensor_scalar(dst[:], src[:, 0:NTOK], qms[:, 0:1], None, ALU.mult), [skey, "qms"], [dkey])
                for q in range(1, 4):
                    S.op("dve", lambda q=q: nc.vector.scalar_tensor_tensor(out=dst[:], in0=src[:, q * NTOK:(q + 1) * NTOK], scalar=qms[:, q:q + 1],
                                                                          in1=dst[:], op0=ALU.mult, op1=ALU.add), [skey, "qms", dkey], [dkey])

            RBQ = [(RBK, q4) for q4 in range(4)]

            def bounce_to_ztm(src, skey):
                zv = zcm_d.rearrange("c (n1 n2) -> n1 c n2", n2=128)
                for q4 in range(4):
                    dma(zcm_d[q4 * 32:(q4 + 1) * 32, :], src[q4 * 32:(q4 + 1) * 32, :], [skey], [("zcm_d", q4)], eng="pool")
                for q4 in range(4):
                    dma(ztm[0:64, q4 * 32:(q4 + 1) * 32, :], zv[:, q4 * 32:(q4 + 1) * 32, :], [("zcm_d", q4)], [(RBK, q4)], eng="pool")

            def conv_core(go):
                def load_h(kb):
                    dma(Hc[kb % 4][:], H_d[go, :, kb * 2:kb * 2 + 2], [("H_d", go)], [f"Hc{kb % 4}"])

                def pre(kb):
                    if kb == 0:
                        load_h(0)
                        load_h(1)
                    if kb + 2 < 32:
                        load_h(kb + 2)

                def cb(kb, ps, pk):
                    s_ = kb % 4
                    y_ = kb % 2
                    zv = ps[:, :].rearrange("p (k r c) -> p k r c", k=2, r=2)
                    S.op("dve", lambda: nc.vector.tensor_tensor(Pb[y_][:], zv, Hc[s_][:, :, 0:2, :], ALU.mult), [pk, f"Hc{s_}"], [f"Pb{y_}"])
                    S.op("dve", lambda: nc.vector.tensor_tensor(Qb[y_][:], zv, Hc[s_][:, :, 2:4, :], ALU.mult), [pk, f"Hc{s_}"], [f"Qb{y_}"])

                def cb2(kb):
                    s_ = kb % 4
                    y_ = kb % 2
                    pe_ = PS[4 + kb % 2]
                    pek = f"ps{4 + kb % 2}"
                    er = pe_[:, 0:256].rearrange("p (k c) -> p k c", k=2)
                    ei = pe_[:, 256:512].rearrange("p (k c) -> p k c", k=2)
                    p0, p1 = Pb[y_][:, :, 0, :], Pb[y_][:, :, 1, :]
                    q0, q1 = Qb[y_][:, :, 0, :], Qb[y_][:, :, 1, :]
                    rk = [f"Pb{y_}", f"Qb{y_}", "finv"]
                    FR, FI, NFI, NFR = finv[:, 0, :], finv[:, 1, :], finv[:, 2, :], finv[:, 3, :]
                    mm(er, FR, p0, True, False, rk, [pek]); mm(er, NFR, p1, False, False, rk, [pek])
                    mm(er, NFI, q0, False, False, rk, [pek]); mm(er, NFI, q1, False, True, rk, [pek])
                    mm(ei, FI, p0, True, False, rk, [pek]); mm(ei, NFI, p1, False, False, rk, [pek])
                    mm(ei, FR, q0, False, False, rk, [pek]); mm(ei, FR, q1, False, True, rk, [pek])
                    S.op("act", lambda: nc.scalar.copy(ET[:, :, kb * 2:kb * 2 + 2], pe_[:, 0:256].rearrange("p (k c) -> p c k", k=2)), [pek], [RCK])
                    S.op("act", lambda: nc.scalar.copy(ET[:, :, 64 + kb * 2:64 + kb * 2 + 2], pe_[:, 256:512].rearrange("p (k c) -> p c k", k=2)),
                         [pek], [RCK])
                fft_forward(ztm, 64, AT, cb, zkey=RBK, akey=RAK, pre_batch=pre, cb_late=cb2)
                TB = [(PSB[0], "psb0"), (PSB[1], "psb1"), (PSW[0].bitcast(BF16), "ps0"), (PSW[1].bitcast(BF16), "ps2")]
                for c4 in range(32):
                    pb_, pbk = TB[c4 % 4]
                    for cc in range(4):
                        c = c4 * 4 + cc
                        S.op("pe", lambda c=c, cc=cc, pb_=pb_: nc.tensor.transpose(pb_[:, cc * 128:(cc + 1) * 128], ET[:, c, :], ident[:]),
                             [RCK, "ident"], [pbk])
                    src = pb_[:, 0:512].rearrange("p (cc n) -> p n cc", cc=4)
                    dst = Eb[:, :, c4 * 4:c4 * 4 + 4]
                    if c4 % 2 == 0:
                        S.op("act", lambda src=src, dst=dst: nc.scalar.copy(dst, src), [pbk], RBQ)
                    else:
                        S.op("dve", lambda src=src, dst=dst: nc.vector.tensor_copy(dst, src), [pbk], RBQ)

            Udv = U_d
            for g in range(4):
                for i in range(3):
                    dma(uraw[:, i, 1:L + 1], Udv[i * 512 + g * 128:i * 512 + (g + 1) * 128, :], ["U_d"], [RAK])
                S.op("pool", lambda: nc.gpsimd.memset(uraw[:, :, 0:1], 0.0), [RAK], [RAK])
                S.op("pool", lambda: nc.gpsimd.memset(uraw[:, :, L + 1:L + 2], 0.0), [RAK], [RAK])
                DG = hyst[:, 0:9 * 128].rearrange("p (i m) -> p i m", m=128)
                DGL = zzq[:, 0:9 * 128].rearrange("p (i m) -> p i m", m=128)
                wsel = cws[:, g:12:4, :]
                S.op("dve", lambda wsel=wsel: nc.vector.tensor_copy(whi[:].rearrange("p (i j) -> p i j", j=3), wsel), ["cws"], ["whi"])
                S.op("dve", lambda: nc.vector.tensor_copy(whf[:], whi[:]), ["whi"], ["whf"])
                S.op("dve", lambda wsel=wsel: nc.vector.tensor_tensor(wlf[:].rearrange("p (i j) -> p i j", j=3), wsel,
                                                                     whf[:].rearrange("p (i j) -> p i j", j=3), ALU.subtract), ["cws", "whf"], ["wlf"])
                for idx in range(9):
                    S.op("dve", lambda idx=idx: nc.vector.tensor_scalar(DG[:, idx, :], ident[:], whf[:, idx:idx + 1], None, ALU.mult), ["whf", "ident"], ["hyst"])
                    S.op("dve", lambda idx=idx: nc.vector.tensor_scalar(DGL[:, idx, :], ident[:], wlf[:, idx:idx + 1], None, ALU.mult), ["wlf", "ident"], ["zzq"])
                nq = 0
                for i, (dst, dkey) in enumerate(((vt, "vt"), (x1t, "x1t"), (x2t, RCK))):
                    ct = i * 4 + g
                    for q in range(16):
                        o0 = q * 512
                        ps = PS[nq % 2]
                        pk = f"ps{nq % 2}"
                        for j in range(3):
                            mm(ps[:, :], DG[:, i * 3 + j, :], uraw[:, i, o0 + j:o0 + j + 512], j == 0, False, ["hyst", RAK], [pk])
                        for j in range(3):
                            mm(ps[:, :], DGL[:, i * 3 + j, :], uraw[:, i, o0 + j:o0 + j + 512], False, j == 2, ["zzq", RAK], [pk])
                        if nq % 2 == 0:
                            S.op("act", lambda ps=ps, dst=dst, o0=o0, ct=ct: nc.scalar.activation(dst[:, o0:o0 + 512], ps[:, :], AF.Identity, bias=cbs[:, ct:ct + 1]),
                                 [pk, "cbs"], [dkey])
                        else:
                            S.op("dve", lambda ps=ps, dst=dst, o0=o0, ct=ct: nc.vector.tensor_scalar(dst[:, o0:o0 + 512], ps[:, :], cbs[:, ct:ct + 1], None, ALU.add),
                                 [pk, "cbs"], [dkey])
                        nq += 1
                masked_quarter(x2q, x2t, "x2q", RCK)
                snap("vt0", vt[:], "vt", L)
                snap("x1t0", x1t[:], "x1t", L)
                bounce_to_ztm(vt, "vt")
                conv_core(g * 2 + 0)
                vt3 = vt[:, :].rearrange("p (n1 n2) -> p n1 n2", n2=128)
                x13 = x1t[:, :].rearrange("p (n1 n2) -> p n1 n2", n2=128)
                for nb in range(16):
                    ps = PS[nb % 2]
                    pk = f"ps{nb % 2}"
                    for j in range(8):
                        n2 = nb * 8 + j
                        mm(ps[:, j * 64:(j + 1) * 64], Eb[:, n2, :], M2s[:, n2, :], True, True, RBQ + ["M2s"], [pk])
                    psv = ps[:, :].rearrange("p (j n) -> p n j", j=8)
                    gtv = gt[nb % 2][:, :].rearrange("p (n j) -> p n j", j=8)
                    gk_ = f"gt{nb % 2}"
                    S.op("dve", lambda psv=psv, gtv=gtv, nb=nb, g=g: nc.vector.scalar_tensor_tensor(out=gtv, in0=vt3[:, :, nb * 8:(nb + 1) * 8],
                                                                                             scalar=skps[:, g:g + 1], in1=psv, op0=ALU.mult, op1=ALU.add),
                         [pk, "vt", "skps"], [gk_])
                    S.op("pool", lambda gtv=gtv, nb=nb: nc.gpsimd.tensor_tensor(vt3[:, :, nb * 8:(nb + 1) * 8], gtv, x13[:, :, nb * 8:(nb + 1) * 8], ALU.mult),
                         [gk_, "x1t"], ["vt"])
                snap("zz0", vt[:], "vt", L)
                masked_quarter(zzq, vt, "zzq", "vt")
                bounce_to_ztm(vt, "vt")
                conv_core(g * 2 + 1)
                zq3 = zzq[:, :].rearrange("p (n1 n2) -> p n1 n2", n2=128)
                xq3 = x2q[:, :].rearrange("p (n1 n2) -> p n1 n2", n2=128)
                hy3 = hyst[:, :].rearrange("p (n1 n2) -> p n1 n2", n2=128)
                for nb in range(4):
                    ps = PS[nb % 2]
                    pk = f"ps{nb % 2}"
                    for j in range(32):
                        n2 = nb * 32 + j
                        mm(ps[:, j * 16:(j + 1) * 16], Eb[:, n2, :], M2qs[:, n2, :], True, True, RBQ + ["M2qs"], [pk])
                    psv = ps[:, :].rearrange("p (j n) -> p n j", j=32)
                    gtv = gt[nb % 2][:, :].rearrange("p (n j) -> p n j", j=32)
                    gk_ = f"gt{nb % 2}"
                    S.op("dve", lambda psv=psv, gtv=gtv, nb=nb, g=g: nc.vector.scalar_tensor_tensor(out=gtv, in0=zq3[:, :, nb * 32:(nb + 1) * 32],
                                                                                             scalar=skps[:, 4 + g:5 + g], in1=psv, op0=ALU.mult, op1=ALU.add),
                         [pk, "zzq", "skps"], [gk_])
                    S.op("pool", lambda gtv=gtv, nb=nb: nc.gpsimd.tensor_tensor(hy3[:, :, nb * 32:(nb + 1) * 32], gtv, xq3[:, :, nb * 32:(nb + 1) * 32], ALU.mult),
                         [gk_, "x2q"], ["hyst"])
                dma(hy_d[g * 128:(g + 1) * 128, :], hyst[:], ["hyst"], ["hy_d"])
            S.barrier()

        def phase4():
          with ExitStack() as p4:
            WoA = sb("WoA", [64, 8, D], BF16, p4)
            WoH = sb("WoH", [128, 4, D], BF16, p4)
            with ExitStack() as p4c:
                wsa = sb("wsa", [64, 8, D], F32, p4c)
                wsh = sb("wsh", [128, 4, D], F32, p4c)
                dma(wsa[:], w_out[0:512, :].rearrange("(h d) n -> d h n", d=64), [], ["wsa"])
                dma(wsh[:], w_out[512:1024, :].rearrange("(g p) n -> p g n", p=128), [], ["wsh"])
                S.op("dve", lambda: nc.vector.tensor_copy(WoA[:], wsa[:]), ["wsa"], ["WoA"])
                S.op("pool", lambda: nc.gpsimd.tensor_copy(WoH[:], wsh[:]), ["wsh"], ["WoH"])
                S.barrier()
            attc = sb("attc", [64, 8, 512], BF16, p4)
            hyc = sb("hyc", [128, 4, 512], BF16, p4)
            sqb = sb("sqb4", [128, 8, 512], BF16, p4)
            rs = sb("rs4", [128, 512], F32, p4)
            mixA = sb("mixA", [64, 8, 512], BF16, p4)
            mixH = sb("mixH", [128, 4, 512], BF16, p4)
            h1 = sb("h1", [128, 8, 512], F32, p4)
            mT = sb("mT", [128, 8, 512], BF16, p4)
            Wmi_s = [sb(f"Wmi_s{i}", [128, 8, 1024], BF16, p4) for i in range(2)]
            Wmo_s = [sb(f"Wmo_s{i}", [128, 32, 128], BF16, p4) for i in range(2)]
            fT = sb("fT", [128, 32, 512], BF16, p4)
            rb = [sb(f"rb{i}", [128, 512], F32, p4) for i in range(2)]
            xqv = xTq.rearrange("(k p) t -> p k t", p=128)
            Wmiv = Wmi_d.rearrange("(k p) n -> p k n", p=128)
            outv = outT.rearrange("(k p) t -> p k t", p=128)
            nw = 0
            nwo = 0
            for j in range(4):
                tsl = slice(j * 512, (j + 1) * 512)
                dma(attc[:], att_d[:, tsl].rearrange("(h d) t -> d h t", d=64), ["att_d"], ["attc", "na_src"])
                dma(hyc[:], hy_d[:, tsl].rearrange("(g p) t -> p g t", p=128), ["hy_d"], ["hyc", "nh_src"])
                dma(h1[:], xqv[:, :, tsl], [], ["h1", "n2_src", "nf_src"])
                rms_apply(attc[:], gas, mixA, 1.0 / 512.0, 8, "na", sqb, rs, np_=64)
                rms_apply(hyc[:], ghs, mixH, 1.0 / 512.0, 4, "nh", sqb, rs)
                for ct in range(8):
                    csl = slice(ct * 128, (ct + 1) * 128)
                    for h in range(8):
                        mm(PS[1][:, :], WoA[:, h, csl], mixA[:, h, :], h == 0, False, ["WoA", "na_dst"], ["ps1"])
                    for g in range(4):
                        mm(PS[1][:, :], WoH[:, g, csl], mixH[:, g, :], False, g == 3, ["WoH", "nh_dst"], ["ps1"])
                    S.op("dve", lambda ct=ct: nc.vector.tensor_tensor(h1[:, ct, :], h1[:, ct, :], PS[1][:, :], ALU.add), ["ps1", "h1"], ["h1", "n2_src", "nf_src"])
                rms_apply(h1[:], g2s, mT, 1.0 / D, 8, "n2", sqb, rs)
                for pc in range(4):
                    s_ = nw % 2
                    dma(Wmi_s[s_][:], Wmiv[:, :, pc * 1024:(pc + 1) * 1024], ["Wmid"], [f"Wmi_s{s_}"])
                    nw += 1
                    for ft in range(8):
                        f = pc * 8 + ft
                        pf = PS[2 + f % 2]
                        pfk = f"ps{2 + f % 2}"
                        for k in range(8):
                            mm(pf[:, :], Wmi_s[s_][:, k, ft * 128:(ft + 1) * 128], mT[:, k, :], k == 0, k == 7, [f"Wmi_s{s_}", "n2_dst"], [pfk])
                        r_ = f % 2
                        S.op("act", lambda pf=pf, r_=r_: nc.scalar.activation(rb[r_][:], pf[:, :], AF.Relu), [pfk], [f"rb{r_}"])
                        e = ("pool", "dve")[f % 2]
                        S.op(e, lambda e=e, r_=r_, f=f: V(e).tensor_tensor(fT[:, f, :], rb[r_][:], rb[r_][:], ALU.mult), [f"rb{r_}"], ["fT"])
                for ct in range(8):
                    s_ = nwo % 2
                    dma(Wmo_s[s_][:], Wmo_d[ct], ["Wmod"], [f"Wmo_s{s_}"])
                    nwo += 1
                    po = PS[4 + ct % 2]
                    pok = f"ps{4 + ct % 2}"
                    for f in range(32):
                        mm(po[:, :], Wmo_s[s_][:, f, :], fT[:, f, :], f == 0, f == 31, [f"Wmo_s{s_}", "fT"], [pok])
                    S.op("dve", lambda ct=ct, po=po: nc.vector.tensor_tensor(h1[:, ct, :], h1[:, ct, :], po[:, :], ALU.add), [pok, "h1"], ["h1", "n2_src", "nf_src"])
                rms_apply(h1[:], gfs, h1, 1.0 / D, 8, "nf", sqb, rs)
                dma(outv[:, :, tsl], h1[:], ["h1", "nf_dst"], ["outT"])
            S.barrier()

        if 0 in phases:
            phase0()
        if 1 in phases:
            phase12()
        if 3 in phases:
            phase3()
        if 4 in phases:
            phase4()
        if debug:
            with ExitStack() as pd:
                for name in debug:
                    if name not in ("att_d", "hy_d", "U_d", "H_d0"):
                        continue
                    src_, shp = {"att_d": (att_d, [512, NTOK]), "hy_d": (hy_d, [512, NTOK]), "U_d": (U_d[0:512, :], [512, L]),
                                 "H_d0": (H_d[0:4, :, 0:4].rearrange("a p k r c -> (a p) (k r c)"), [512, 2048])}[name]
                    dbo = nc.dram_tensor("dbg_" + name, shp, BF16, kind="ExternalOutput").ap()
                    dbs = sb("dbs_" + name, [128, 4, shp[1]], BF16, pd)
                    dma(dbs[:], src_.rearrange("(a p) t -> p a t", p=128), [name], ["dbs_" + name])
                    dma(dbo.rearrange("(a p) t -> p a t", p=128), dbs[:], ["dbs_" + name], ["dbg_" + name])
                S.barrier()
        S.barrier()
    return nc, S


def _core_inputs(inp, core):
    T = _tables()
    b, r = core // 4, core % 4
    f = lambda a: np.ascontiguousarray(np.asarray(a, dtype=np.float32))
    x = np.asarray(inp["x"], dtype=np.float32)
    xT = np.ascontiguousarray(x[b].T)
    tq = slice(r * NTOK, (r + 1) * NTOK)
    qm = np.zeros((128, 4), np.float32)
    qm[:, r] = 1.0
    m = {
        "xT": xT, "xTq": np.ascontiguousarray(xT[:, tq]),
        "w_in": f(inp["w_in"][0]), "w_out": f(inp["w_out"][0]), "w_mi": f(inp["w_mlp_in"][0]), "w_mo": f(inp["w_mlp_out"][0]),
        "g1": f(np.asarray(inp["norm1_g"][0]).reshape(8, 128).T), "g2": f(np.asarray(inp["norm2_g"][0]).reshape(8, 128).T),
        "gf": f(np.asarray(inp["final_g"]).reshape(8, 128).T),
        "gq": f(np.asarray(inp["q_norm_g"][0]).reshape(64, 1)), "gk": f(np.asarray(inp["k_norm_g"][0]).reshape(64, 1)),
        "ga": f(np.asarray(inp["attn_out_g"][0]).reshape(8, 64).T), "gh": f(np.asarray(inp["hy_out_g"][0]).reshape(4, 128).T),
        "cw": f(np.asarray(inp["hy_conv_w"][0]).T.reshape(12, 128, 3).transpose(1, 0, 2)),
        "cb": f(np.asarray(inp["hy_conv_b"][0]).reshape(12, 128).T),
        "fw1": f(inp["filt_w1"][0]), "fw2": f(inp["filt_w2"][0]), "fw3": f(inp["filt_w3"][0]), "fw4": f(inp["filt_w4"][0]),
        "fb": f(np.stack([np.asarray(inp["filt_b1"][0]), np.asarray(inp["filt_b2"][0]), np.asarray(inp["filt_b3"][0])], axis=1)),
        "ffreq": f(np.asarray(inp["filt_freq"][0]).reshape(64, 1)),
        "fdel": f(np.asarray(inp["filt_deltas"][0]).reshape(16, 128).T),
        "skp": f(np.asarray(inp["hy_skip_d"][0]).reshape(2, 4, 128).transpose(2, 0, 1).reshape(128, 8)),
        "qmask": qm,
        "c_cos": T["cosT"], "c_sin": T["sinT"],
        "c_cosq": np.ascontiguousarray(T["cosT"][:, tq]), "c_sinq": np.ascontiguousarray(T["sinT"][:, tq]),
        "c_zT": np.ascontiguousarray(np.stack([T["zT"], T["zrT"]])), "c_t": np.ascontiguousarray(np.stack([T["t"], T["trev"]])),
        "c_tb": T["tb"], "c_jt": T["jt"],
        "c_fpack": T["fpack"], "c_G": T["G"], "c_finv": T["finv"], "c_M2": T["M2"],
        "c_M2q": np.ascontiguousarray(T["M2"][:, :, r * 16:(r + 1) * 16]),
        "c_rot": T["rot"], "c_sel": T["sel"], "c_ones": T["ones"], "c_ident": T["ident"],
    }
    return m


def _emit(nc, S):
    from contextlib import ExitStack
    st = ExitStack()
    sems = {e: [st.enter_context(nc.semaphore(f"s_{e}{i}")) for i in range(4)] for e in S.CE}
    dsems = [st.enter_context(nc.semaphore(f"d{i}")) for i in range(S.NDS)]
    info = S.emit(sems, dsems)
    return st, info


def kernel(**inputs):
    nc, S = build_program()
    st, _ = _emit(nc, S)
    with st:
        in_maps = [_core_inputs(inputs, c) for c in range(8)]
        res = run_bass_kernel_spmd(nc, in_maps, core_ids=list(range(8)))
    out = np.empty((NB, L, D), np.float32)
    for c in range(8):
        b, r = c // 4, c % 4
        out[b, r * NTOK:(r + 1) * NTOK, :] = np.asarray(res.results[c]["outT"]).T
    return out
```

```python
import math
import numpy as np
import ml_dtypes
import concourse.bass as bass
import concourse.mybir as mybir
from concourse.bass_utils import run_bass_kernel_spmd

F32 = mybir.dt.float32
BF16 = mybir.dt.bfloat16
ALU = mybir.AluOpType
AF = mybir.ActivationFunctionType
bf16_np = ml_dtypes.bfloat16

D = 1024
L = 8192
NB = 2
NTOK = 2048
HD = 64
NQH = 8
NKV = 2
HYW = 512
DFF = 4096
EPS = 1e-6
NFFT = 2 * L
TWO_PI = 2.0 * math.pi
MAGIC = 12582912.0


class Op:
    __slots__ = ("eng", "fn", "reads", "writes", "dma", "idx", "gid", "waits", "signal", "clock",
                 "sem", "val")


class Sched:
    CE = ("pe", "act", "dve", "pool", "sp")
    EPOCH = 12000
    NDS = 40

    def __init__(self, nc):
        self.nc = nc
        self.E = {"pe": nc.tensor, "act": nc.scalar, "dve": nc.vector, "pool": nc.gpsimd,
                  "sp": nc.sync}
        self.ops = []
        self.last_w = {}
        self.readers = {}
        self.known = {e: {} for e in self.CE}
        self.known_dma = {e: set() for e in self.CE}
        self.cnt = {e: 0 for e in self.CE}
        self.last_op = {e: None for e in self.CE}
        self.live_dma = []

    def op(self, eng, fn, reads=(), writes=(), dma=False):
        o = Op()
        o.eng, o.fn, o.reads, o.writes, o.dma = eng, fn, tuple(reads), tuple(writes), dma
        o.signal = dma
        o.sem = None
        o.val = None
        o.gid = len(self.ops)
        deps = {}
        for k in o.reads:
            w = self.last_w.get(k)
            if w is not None:
                deps[w.gid] = w
        for k in o.writes:
            w = self.last_w.get(k)
            if w is not None:
                deps[w.gid] = w
            for r in self.readers.get(k, ()):
                deps[r.gid] = r
        self._finish(o, deps.values())
        for k in o.writes:
            self.last_w[k] = o
            self.readers[k] = []
        for k in o.reads:
            lst = self.readers.setdefault(k, [])
            if not dma:
                lst[:] = [r for r in lst if r.dma or r.eng != eng]
            lst.append(o)
        return o

    def _finish(self, o, deps):
        eng = o.eng
        kn = self.known[eng]
        kd = self.known_dma[eng]
        waits = []
        best = {}
        rset = set(o.reads)
        for d in deps:
            if d.dma:
                if d.gid not in kd:
                    waits.append(d)
                    kd.add(d.gid)
                continue
            if d.eng == eng:
                if eng in ("pe", "sp"):
                    continue
            if kn.get(d.eng, -1) >= d.idx:
                continue
            b = best.get(d.eng)
            if b is None or b.idx < d.idx:
                best[d.eng] = d
        for d in best.values():
            waits.append(d)
            if kn.get(d.eng, -1) < d.idx:
                kn[d.eng] = d.idx
            for e2, i2 in d.clock.items():
                if kn.get(e2, -1) < i2:
                    kn[e2] = i2
        for d in waits:
            d.signal = True
        o.waits = waits
        o.idx = self.cnt[eng]
        self.cnt[eng] += 1
        o.clock = dict(kn)
        self.ops.append(o)
        if o.fn is not None:
            self.last_op[eng] = o
        if o.dma:
            self.live_dma.append(o)

    def barrier(self):
        lasts = [self.last_op[e] for e in self.CE if self.last_op[e] is not None]
        dmas = list(self.live_dma)
        for e in self.CE:
            o = Op()
            o.eng, o.fn, o.reads, o.writes, o.dma = e, None, (), (), False
            o.signal = False
            o.sem = None
            o.val = None
            o.gid = len(self.ops)
            deps = [d for d in lasts if not d.dma and d.eng != e] + dmas
            self._finish(o, deps)
        self.live_dma = []

    def emit(self, sems, dsems):
        sig = {e: 0 for e in self.CE}
        nd = 0
        for o in self.ops:
            E = self.E[o.eng]
            for d in o.waits:
                E.wait_ge(d.sem, d.val)
            if o.fn is None:
                continue
            if o.dma:
                s = dsems[nd % self.NDS]
                rnd = nd // self.NDS
                if rnd > 0:
                    E.wait_ge(s, 16 * rnd)
                ins = o.fn()
                ins.then_inc(s, 16)
                o.sem, o.val = s, 16 * (rnd + 1)
                nd += 1
            else:
                ins = o.fn()
                if o.signal:
                    c = sig[o.eng]
                    o.sem = sems[o.eng][c // self.EPOCH]
                    o.val = c % self.EPOCH + 1
                    ins.then_inc(o.sem, 1)
                    sig[o.eng] = c + 1
        return sig, nd


_TABLES = None


def _rope_tables():
    rows = L // 64
    row = np.repeat(np.arange(rows, dtype=np.float32), 64)
    col = np.tile(np.arange(64, dtype=np.float32), rows)
    half = HD // 2
    inv_freq = (np.float32(10000.0) ** (-np.arange(0, half, 2, dtype=np.float32) / np.float32(half))).astype(np.float32)
    ang_r = (row[:, None] * inv_freq[None, :]).astype(np.float32)
    ang_c = (col[:, None] * inv_freq[None, :]).astype(np.float32)
    cos_r, sin_r = np.cos(ang_r), np.sin(ang_r)
    cos_c, sin_c = np.cos(ang_c), np.sin(ang_c)
    cosT = np.concatenate([cos_r.T, cos_r.T, cos_c.T, cos_c.T], axis=0).astype(np.float32)
    sinT = np.concatenate([sin_r.T, sin_r.T, sin_c.T, sin_c.T], axis=0).astype(np.float32)
    return np.ascontiguousarray(cosT), np.ascontiguousarray(sinT)


def _filter_pos_tables():
    f32 = np.float32
    t = np.linspace(0.0, 1.0, L, dtype=f32)
    bands = 16
    w_ang = (f32(2.0 * math.pi) * np.arange(L, dtype=f32) / f32(L)).astype(f32)
    band_f = np.linspace(1e-4, bands - 1, bands, dtype=f32)
    ang = (w_ang[:, None] * band_f[None, :]).astype(f32)
    z = np.concatenate([t[:, None], np.cos(ang), -np.sin(ang)], axis=-1).astype(f32)
    zT = np.ascontiguousarray(z.T)
    idx = (L - np.arange(L)) % L
    zrT = np.ascontiguousarray(z[idx].T)
    trev = t[idx].copy()
    return zT, zrT, t.copy(), trev


def _fft_tables():
    N = NFFT
    n1 = np.arange(128, dtype=np.float64)
    k1 = np.arange(64, dtype=np.float64)
    th = 2 * np.pi * np.outer(n1, k1 + 0.5) / 128.0
    fpack = np.stack([np.sin(th), np.cos(th), -np.sin(th)], axis=2).reshape(128, 192)
    n2 = np.arange(128, dtype=np.float64)
    k2 = np.arange(128, dtype=np.float64)
    kap = k1[None, :, None] + 128.0 * k2[None, None, :] + 0.5
    thg = 2 * np.pi * n2[:, None, None] * kap / N
    G = np.stack([np.cos(thg), -np.sin(thg)], axis=2)
    thf = 2 * np.pi * np.outer(k2, n2) / 128.0
    finv = np.stack([np.cos(thf), np.sin(thf), -np.sin(thf), -np.cos(thf)], axis=1)
    n1h = np.arange(64, dtype=np.float64)
    phi = 2 * np.pi * (k1[:, None, None] + 0.5) * (128.0 * n1h[None, None, :] + n2[None, :, None]) / N
    M2 = np.concatenate([(2.0 / N) * np.cos(phi), -(2.0 / N) * np.sin(phi)], axis=0)
    return (fpack.astype(bf16_np), G.astype(bf16_np), finv.astype(bf16_np), M2.astype(bf16_np))


def _tables():
    global _TABLES
    if _TABLES is None:
        cosT, sinT = _rope_tables()
        zT, zrT, t, trev = _filter_pos_tables()
        fpack, G, finv, M2 = _fft_tables()
        ii = np.arange(512, dtype=np.float64)
        jj = np.arange(16, dtype=np.float64)
        tbm = np.stack([ii / (L - 1), (L - ii) / (L - 1)]).astype(np.float32)
        jtm = np.stack([512.0 * jj / (L - 1), -512.0 * jj / (L - 1)]).astype(np.float32)
        rot = np.zeros((64, 64), np.float32)
        for m in range(64):
            if (m % 32) < 16:
                rot[m + 16, m] = -1.0
            else:
                rot[m - 16, m] = 1.0
        sel = np.zeros((65, 64), np.float32)
        sel[64, :] = 1.0
        _TABLES = dict(cosT=cosT, sinT=sinT, zT=zT, zrT=zrT, t=t, trev=trev, fpack=fpack, G=G,
                       finv=finv, M2=M2, rot=rot, sel=sel, tb=tbm, jt=jtm,
                       ones=np.ones((128, 128), np.float32),
                       ident=np.eye(128, dtype=np.float32).astype(bf16_np))
    return _TABLES


def build_program(phases=(0, 1, 2, 3, 4), debug=()):
    nc = bass.Bass("TRN2", target_bir_lowering=False)
    S = Sched(nc)

    def din(name, shape, dt=F32):
        return nc.dram_tensor(name, list(shape), dt, kind="ExternalInput").ap()

    def dscr(name, shape, dt):
        return nc.dram_tensor(name, list(shape), dt, kind="Internal").ap()

    xT = din("xT", [D, L])
    xTq = din("xTq", [D, NTOK])
    w_in = din("w_in", [D, 2304])
    w_out = din("w_out", [D, D])
    w_mi = din("w_mi", [D, DFF])
    w_mo = din("w_mo", [DFF, D])
    g1 = din("g1", [128, 8]); g2 = din("g2", [128, 8]); gf = din("gf", [128, 8])
    gq = din("gq", [64, 1]); gk = din("gk", [64, 1])
    ga = din("ga", [64, 8]); gh = din("gh", [128, 4])
    cw = din("cw", [128, 12, 3]); cb = din("cb", [128, 12])
    fw1 = din("fw1", [33, 64]); fw2 = din("fw2", [64, 64]); fw3 = din("fw3", [64, 64])
    fw4 = din("fw4", [64, 2048])
    fb = din("fb", [64, 3]); ffreq = din("ffreq", [64, 1])
    fdel = din("fdel", [128, 16])
    skp = din("skp", [128, 8])
    qmask = din("qmask", [128, 4])
    c_cos = din("c_cos", [64, L]); c_sin = din("c_sin", [64, L])
    c_cosq = din("c_cosq", [64, NTOK]); c_sinq = din("c_sinq", [64, NTOK])
    c_zT = din("c_zT", [2, 33, L])
    c_t = din("c_t", [2, L])
    c_tb = din("c_tb", [2, 512])
    c_jt = din("c_jt", [2, 16])
    c_fpack = din("c_fpack", [128, 192], BF16)
    c_G = din("c_G", [128, 64, 2, 128], BF16)
    c_finv = din("c_finv", [128, 4, 128], BF16)
    c_M2 = din("c_M2", [128, 128, 64], BF16)
    c_M2q = din("c_M2q", [128, 128, 16], BF16)
    c_rot = din("c_rot", [64, 64]); c_sel = din("c_sel", [65, 64]); c_ones = din("c_ones", [128, 128])
    c_ident = din("c_ident", [128, 128], BF16)
    outT = nc.dram_tensor("outT", [D, NTOK], F32, kind="ExternalOutput").ap()

    U_d = dscr("U_d", [1536, L], BF16)
    H_d = dscr("H_d", [8, 128, 64, 4, 128], BF16)
    taps_d = dscr("taps_d", [128, NFFT], BF16)
    zcm_d = dscr("zcm_d", [128, L], BF16)
    att_d = dscr("att_d", [512, NTOK], BF16)
    hy_d = dscr("hy_d", [512, NTOK], BF16)
    Wmi_d = dscr("Wmi_d", [D, DFF], BF16)
    Wmo_d = dscr("Wmo_d", [8, 128, 32, 128], BF16)

    from contextlib import ExitStack
    es = ExitStack()

    _names = {}

    def sb(name, shape, dt, stack=None):
        n_ = _names.get(name, 0)
        _names[name] = n_ + 1
        nm = name if n_ == 0 else f"{name}_r{n_}"
        return (stack or es).enter_context(nc.sbuf_tensor(nm, list(shape), dt))

    def dma(out, in_, reads, writes, eng="sp"):
        return S.op(eng, lambda: S.E[eng].dma_start(out=out, in_=in_), reads, writes, dma=True)

    def mm(out, lhsT, rhs, start, stop, reads, writes):
        return S.op("pe", lambda: nc.tensor.matmul(out, lhsT, rhs, start=start, stop=stop), reads, writes)

    rr = {"i": 0}

    def alt(engs=("dve", "pool")):
        rr["i"] += 1
        return engs[rr["i"] % len(engs)]

    def V(eng):
        return S.E[eng]

    with es:
        PSW = [es.enter_context(nc.psum_tensor(f"psw{i}", [128, 1024], F32)) for i in range(3)]
        PS = [PSW[i // 2][:, (i % 2) * 512:(i % 2 + 1) * 512] for i in range(6)]
        PSB = [es.enter_context(nc.psum_tensor(f"psb{i}", [128, 1024], BF16)) for i in range(2)]
        PSX = [PSB[i].bitcast(F32)[:, :] for i in range(2)]

        ones = sb("ones", [128, 128], F32)
        ident = sb("ident", [128, 128], BF16)
        rot = sb("rot", [64, 64], F32)
        sel = sb("sel", [65, 64], F32)
        epsT = sb("epsT", [128, 1], F32)
        onesbf = sb("onesbf", [128, 128], BF16)
        g1s = sb("g1s", [128, 8], F32); g2s = sb("g2s", [128, 8], F32); gfs = sb("gfs", [128, 8], F32)
        gqs = sb("gqs", [64, 1], F32); gks = sb("gks", [64, 1], F32)
        gas = sb("gas", [64, 8], F32); ghs = sb("ghs", [128, 4], F32)
        cws = sb("cws", [128, 12, 3], F32); cbs = sb("cbs", [128, 12], F32)
        skps = sb("skps", [128, 8], F32); qms = sb("qms", [128, 4], F32)
        for t_, d_ in ((ones, c_ones), (ident, c_ident), (rot, c_rot), (sel, c_sel), (g1s, g1), (g2s, g2),
                       (gfs, gf), (gqs, gq), (gks, gk), (gas, ga), (ghs, gh), (cws, cw), (cbs, cb),
                       (skps, skp), (qms, qmask)):
            dma(t_[:], d_, [], [t_.name if hasattr(t_, "name") else id(t_)])
        KEY = lambda t_: t_.name if hasattr(t_, "name") else id(t_)
        S.op("dve", lambda: nc.vector.memset(epsT[:], EPS), [], ["epsT"])
        S.op("pool", lambda: nc.gpsimd.memset(onesbf[:], 1.0), [], ["onesbf"])

        FC = {}
        SNAP = {}

        def snap(name, src_ap, key, ncols):
            if name not in debug or name in SNAP:
                return
            SNAP[name] = nc.dram_tensor("dbg_" + name, [128, ncols], BF16, kind="ExternalOutput").ap()
            dma(SNAP[name], src_ap, [key], ["dbg_" + name])


        def load_fft_consts(stack):
            FC["fpack"] = sb("fpack", [128, 192], BF16, stack)
            FC["finv"] = sb("finv", [128, 4, 128], BF16, stack)
            FC["Gq"] = [sb(f"Gq{i}", [128, 8, 2, 128], BF16, stack) for i in range(2)]
            dma(FC["fpack"][:], c_fpack, [], ["fpack"])
            dma(FC["finv"][:], c_finv, [], ["finv"])

        def rsqrt(out, in_, scale, reads, writes, np_=128):
            S.op("act", lambda: nc.scalar.activation(out, in_, AF.Sqrt, bias=epsT[0:np_, :], scale=scale),
                 list(reads) + ["epsT"], writes)
            S.op("dve", lambda: nc.vector.reciprocal(out, out), writes, writes)

        def fft_forward(ztm, K, AT, cb_batch, zkey="ztm", akey="AT", pre_batch=None, cb_late=None):
            fpack = FC["fpack"]
            for c2 in range(64):
                ps = PS[c2 % 4]
                pk = f"ps{c2 % 4}"
                for cc in range(2):
                    c = c2 * 2 + cc
                    mm(ps[:, cc * 192:(cc + 1) * 192], ztm[0:K, c, :], fpack[0:K, :], True, True,
                       [(zkey, c // 32), "fpack"], [pk])
                e = alt(("act", "dve"))
                src = ps[:, 0:384].rearrange("p (cc j) -> p j cc", cc=2)
                dst = AT[:, :, c2 * 2:c2 * 2 + 2]
                if e == "act":
                    S.op("act", lambda dst=dst, src=src: nc.scalar.copy(dst, src), [pk], [akey])
                else:
                    S.op("dve", lambda dst=dst, src=src: nc.vector.tensor_copy(dst, src), [pk], [akey])

            def load_g(q8):
                dma(FC["Gq"][q8 % 2][:], c_G[:, q8 * 8:(q8 + 1) * 8], [], [f"Gq{q8 % 2}"])

            load_g(0)
            for kb in range(32):
                q4 = kb // 4
                Gq = FC["Gq"][q4 % 2]
                gkey = f"Gq{q4 % 2}"
                if kb % 4 == 0 and q4 + 1 < 8:
                    load_g(q4 + 1)
                if pre_batch is not None:
                    pre_batch(kb)
                ps = PS[2 + kb % 2]
                pk = f"ps{2 + kb % 2}"
                for kk in range(2):
                    k1 = kb * 2 + kk
                    kl = k1 % 8
                    zp = ps[:, kk * 256:(kk + 1) * 256].rearrange("p (r c) -> p r c", r=2)
                    mm(zp, Gq[:, kl, 0, :], AT[:, 3 * k1 + 1:3 * k1 + 3, :], True, False, [akey, gkey], [pk])
                    mm(zp, Gq[:, kl, 1, :], AT[:, 3 * k1:3 * k1 + 2, :], False, True, [akey, gkey], [pk])
                cb_batch(kb, ps, pk)
                if cb_late is not None and kb >= 1:
                    cb_late(kb - 1)
            if cb_late is not None:
                cb_late(31)

        cast_jobs = []
        for cbk in range(8):
            cast_jobs.append((w_mi.rearrange("(k p) n -> p k n", p=128)[:, :, cbk * 512:(cbk + 1) * 512],
                              Wmi_d.rearrange("(k p) n -> p k n", p=128)[:, :, cbk * 512:(cbk + 1) * 512], None))
        for kr in range(4):
            for cbk in range(2):
                cast_jobs.append((w_mo.rearrange("(k p) n -> p k n", p=128)[:, kr * 8:(kr + 1) * 8, cbk * 512:(cbk + 1) * 512],
                                  None, (kr, cbk)))

        def emit_cast(n, wst, wbf):
            src, dst, tl = cast_jobs[n]
            s_ = n % 2
            dma(wst[s_][:], src, [], [f"wst{s_}"])
            S.op("dve", lambda s_=s_: nc.vector.tensor_copy(wbf[s_][:], wst[s_][:]), [f"wst{s_}"], [f"wbf{s_}"])
            if tl is None:
                dma(dst, wbf[s_][:], [f"wbf{s_}"], ["Wmid"])
            else:
                kr_, cbk_ = tl
                for c_ in range(4):
                    dma(Wmo_d[cbk_ * 4 + c_, :, kr_ * 8:(kr_ + 1) * 8, :], wbf[s_][:, :, c_ * 128:(c_ + 1) * 128], [f"wbf{s_}"], ["Wmod"])

        def phase0():
          with ExitStack() as p0:
            load_fft_consts(p0)

            w1s = sb("w1s", [33, 64], F32, p0); w2s = sb("w2s", [64, 64], F32, p0); w3s = sb("w3s", [64, 64], F32, p0)
            w4b = sb("w4b", [64, 2048], BF16, p0)
            fbs = sb("fbs", [64, 3], F32, p0); frs = sb("frs", [64, 1], F32, p0); ffb = sb("ffb", [64, 3], F32, p0)
            dls = sb("dls", [128, 16], F32, p0); ndl = sb("ndl", [128, 16], F32, p0)
            negpi = sb("negpi", [128, 1], F32, p0)
            H3 = sb("H3", [64, 2, L], BF16, p0)
            p0m = ExitStack()
            w4s = sb("w4s", [64, 2048], F32, p0m)
            for t_, d_, k_ in ((w1s, fw1, "w1s"), (w2s, fw2, "w2s"), (w3s, fw3, "w3s"), (w4s, fw4, "w4s"),
                               (fbs, fb, "fbs"), (frs, ffreq, "frs"), (dls, fdel, "dls")):
                dma(t_[:], d_, [], [k_])
            S.op("dve", lambda: nc.vector.tensor_copy(w4b[:], w4s[:]), ["w4s"], ["w4b"])
            S.op("dve", lambda: nc.vector.tensor_scalar(ffb[:], fbs[:], frs[:, 0:1], None, ALU.mult), ["fbs", "frs"], ["ffb"])
            S.op("dve", lambda: nc.vector.tensor_scalar(ndl[:], dls[:], -1.0, None, ALU.mult), ["dls"], ["ndl"])
            S.op("dve", lambda: nc.vector.tensor_tensor(ndl[:], ndl[:], dls[:], ALU.min), ["dls", "ndl"], ["ndl"])
            S.op("dve", lambda: nc.vector.memset(negpi[:], 0.0), [], ["negpi"])
            NCH = 8
            zc = [sb(f"zc{i}", [33, 512], F32, p0m) for i in range(NCH)]
            ybL = [sb(f"yb{i}", [64, 512], F32, p0m) for i in range(NCH)]
            ttL = [sb(f"tt{i}", [64, 512], F32, p0m) for i in range(NCH)]
            nrL = [sb(f"nr{i}", [64, 512], F32, p0m) for i in range(NCH)]
            fhL = [sb(f"fh{i}", [64, 512], F32, p0m) for i in range(NCH)]
            frs2 = sb("frs2", [64, 1], F32, p0m)
            ffb2 = sb("ffb2", [64, 3], F32, p0m)
            S.op("dve", lambda: nc.vector.tensor_scalar(frs2[:], frs[:], 1.0 / TWO_PI, None, ALU.mult), ["frs"], ["frs2"])
            S.op("dve", lambda: nc.vector.tensor_scalar(ffb2[:], fbs[:], frs2[:, 0:1], None, ALU.mult), ["fbs", "frs2"], ["ffb2"])
            BANKS = [(PS[i], f"ps{i}") for i in range(6)] + [(PSX[0], "psb0"), (PSX[1], "psb1")]
            wL = [(w1s, "w1s"), (w2s, "w2s"), (w3s, "w3s")]

            def mlp_layer(c_, li, var, j):
                psm, pmk = BANKS[c_]
                src, srck = (zc[c_], f"zc{c_}") if li == 0 else (fhL[c_], f"fh{c_}")
                w_, wk_ = wL[li]
                mm(psm[0:64, :], w_[:], src[:], True, True, [srck, wk_], [pmk])
                S.op("act", lambda: nc.scalar.activation(ybL[c_][:], psm[0:64, :], AF.Identity, bias=ffb2[:, li:li + 1], scale=frs2[:, 0:1]),
                     [pmk, "frs2", "ffb2"], [f"yb{c_}"])
                S.op("dve", lambda: nc.vector.tensor_scalar(ttL[c_][:], ybL[c_][:], MAGIC, None, ALU.add), [f"yb{c_}"], [f"tt{c_}"])
                S.op("dve", lambda: nc.vector.scalar_tensor_tensor(out=nrL[c_][:], in0=ttL[c_][:], scalar=MAGIC, in1=ybL[c_][:],
                                                                  op0=ALU.subtract, op1=ALU.subtract), [f"tt{c_}", f"yb{c_}"], [f"nr{c_}"])
                if li < 2:
                    dst, dk = fhL[c_][:], f"fh{c_}"
                else:
                    dst, dk = H3[:, var, j * 512:(j + 1) * 512], ("H3", var)
                S.op("act", lambda: nc.scalar.activation(dst, nrL[c_][:], AF.Sin, bias=negpi[0:64, :], scale=-TWO_PI), [f"nr{c_}", "negpi"], [dk])

            for jg in range(4):
                chains = [(c_, c_ % 2, jg * 4 + c_ // 2) for c_ in range(NCH)]
                for c_, var, j in chains:
                    dma(zc[c_][:], c_zT[var, :, j * 512:(j + 1) * 512], [], [f"zc{c_}"])
                for li in range(3):
                    for c_, var, j in chains:
                        mlp_layer(c_, li, var, j)

            S.barrier()
            p0m.close()
            tapsUL = [sb(f"tapsU{i}", [128, NFFT], BF16, p0) for i in range(2)]
            ztmf = sb("ztmf", [128, 128, 128], BF16, p0)
            ATf = sb("ATf", [128, 192, 128], BF16, p0)
            tb = sb("tb", [128, 2, 512], F32, p0)
            jt = sb("jt", [128, 2, 16], F32, p0)
            for hf in range(2):
                dma(tb[:, hf, :], c_tb[hf:hf + 1, :].broadcast_to([128, 512]), [], ["tb"])
                dma(jt[:, hf, :], c_jt[hf:hf + 1, :].broadcast_to([128, 16]), [], ["jt"])
            wbase1 = sb("wbase", [128, 512], F32, p0)
            wbase = [wbase1, wbase1]
            wcj = [sb(f"wcj{i}", [128, 16], F32, p0) for i in range(2)]
            tpf = [sb(f"tpf{i}", [128, 512], F32, p0) for i in range(2)]
            psum_ = sb("psum_", [128, 32], F32, p0)
            nrm = sb("nrm", [128, 2], F32, p0)
            Hst = [sb(f"Hst{i}", [128, 2, 4, 128], BF16, p0) for i in range(2)]

            def taps_chunk(go, cidx):
                g, o = go // 2, go % 2
                half, j = cidx // 16, cidx % 16
                tU = tapsUL[go % 2]
                tk = f"tapsU{go % 2}"
                col0 = o * 1024 + half * 512 + g * 128
                ct = col0 // 128
                if j == 0:
                    S.op("act", lambda: nc.scalar.activation(wbase[half][:], tb[:, half, :], AF.Exp, scale=ndl[:, ct:ct + 1]),
                         ["tb", "ndl"], ["wbase"])
                    S.op("act", lambda: nc.scalar.activation(wcj[half][:], jt[:, half, :], AF.Exp, scale=ndl[:, ct:ct + 1]),
                         ["jt", "ndl"], [f"wcj{half}"])
                s_ = j % 2
                pst = PS[4 + s_]
                ptk = f"ps{4 + s_}"
                mm(pst[:, :], w4b[:, col0:col0 + 128], H3[:, half, j * 512:(j + 1) * 512], True, True, [("H3", half), "w4b"], [ptk])
                S.op("dve", lambda: nc.vector.scalar_tensor_tensor(out=tpf[s_][:], in0=pst[:, :], scalar=wcj[half][:, j:j + 1], in1=wbase[half][:],
                                                                  op0=ALU.mult, op1=ALU.mult), [ptk, f"wcj{half}", "wbase"], [f"tpf{s_}"])
                if half == 1 and j == 0:
                    S.op("dve", lambda: nc.vector.memset(tpf[s_][:, 0:1], 0.0), [f"tpf{s_}"], [f"tpf{s_}"])
                S.op("dve", lambda: nc.vector.tensor_reduce(psum_[:, cidx:cidx + 1], tpf[s_][:], mybir.AxisListType.X, ALU.add,
                                                            apply_absolute_value=True), [f"tpf{s_}"], ["psum_"])
                S.op("act", lambda: nc.scalar.copy(tU[:, half * L + j * 512: half * L + (j + 1) * 512], tpf[s_][:]), [f"tpf{s_}"], [tk])

            def taps_finish(go):
                tU = tapsUL[go % 2]
                tk = f"tapsU{go % 2}"
                S.op("dve", lambda: nc.vector.tensor_reduce(nrm[:, 0:1], psum_[:], mybir.AxisListType.X, ALU.add), ["psum_"], ["nrm"])
                S.op("dve", lambda: nc.vector.reciprocal(nrm[:, 0:1], nrm[:, 0:1]), ["nrm"], ["nrm"])
                S.op("dve", lambda: nc.vector.tensor_scalar(nrm[:, 1:2], nrm[:, 0:1], -1.0, None, ALU.mult), ["nrm"], ["nrm"])
                S.op("dve", lambda: nc.vector.tensor_scalar(tU[:, 0:L], tU[:, 0:L], nrm[:, 0:1], None, ALU.mult), [tk, "nrm"], [tk])
                S.op("act", lambda: nc.scalar.mul(tU[:, L:NFFT], tU[:, L:NFFT], nrm[:, 1:2]), [tk, "nrm"], [tk])
                tv = taps_d.rearrange("c (n1 n2) -> n1 c n2", n2=128)
                for q4 in range(4):
                    dma(taps_d[q4 * 32:(q4 + 1) * 32, :], tU[q4 * 32:(q4 + 1) * 32, :], [tk], [("taps_d", q4)], eng="pool")
                for q4 in range(4):
                    dma(ztmf[:, q4 * 32:(q4 + 1) * 32, :], tv[:, q4 * 32:(q4 + 1) * 32, :], [("taps_d", q4)], [("ztm", q4)], eng="pool")

            for cidx in range(32):
                taps_chunk(0, cidx)
            taps_finish(0)
            for go in range(8):
                def cb_filter(kb, ps, pk, go=go):
                    s_ = kb % 2
                    zv = ps[:, :].rearrange("p (k r c) -> p k r c", k=2, r=2)
                    S.op("act", lambda: nc.scalar.copy(Hst[s_][:, :, 0:2, :], zv), [pk], [f"Hst{s_}"])
                    S.op("dve", lambda: nc.vector.tensor_copy(Hst[s_][:, :, 2, :], zv[:, :, 1, :]), [pk], [f"Hst{s_}"])
                    S.op("dve", lambda: nc.vector.tensor_copy(Hst[s_][:, :, 3, :], zv[:, :, 0, :]), [pk], [f"Hst{s_}"])
                    dma(H_d[go, :, kb * 2:kb * 2 + 2], Hst[s_][:], [f"Hst{s_}"], [("H_d", go)])

                def pre_filter(kb, go=go):
                    if go + 1 < 8:
                        if kb < 16:
                            taps_chunk(go + 1, 2 * kb)
                            taps_chunk(go + 1, 2 * kb + 1)
                        elif kb == 16:
                            taps_finish(go + 1)

                fft_forward(ztmf, 128, ATf, cb_filter, pre_batch=pre_filter)
            S.barrier()
        def rms_apply(src, gvec, dst, scale, nk, tag, sqb, rs, np_=128, ss_ps=0, sqk="sqb", rsk="rs"):
            rms_a(src, nk, tag, sqb, np_, sqk)
            rms_b(src, gvec, dst, scale, nk, tag, sqb, rs, np_, ss_ps, sqk, rsk)

        def rms_a(src, nk, tag, sqb, np_=128, sqk="sqb"):
            S.op("act", lambda: nc.scalar.activation(sqb[0:np_, 0:nk, :], src, AF.Square), [tag + "_src"], [sqk])

        def rms_b(src, gvec, dst, scale, nk, tag, sqb, rs, np_=128, ss_ps=0, sqk="sqb", rsk="rs"):
            for k in range(nk):
                mm(PS[ss_ps][0:np_, :], onesbf[0:np_, 0:np_], sqb[0:np_, k, :], k == 0, k == nk - 1, [sqk, "onesbf"], [f"ps{ss_ps}"])
            S.op("act", lambda: nc.scalar.activation(rs[0:np_, :], PS[ss_ps][0:np_, :], AF.Sqrt, bias=epsT[0:np_, :], scale=scale),
                 [f"ps{ss_ps}", "epsT"], [rsk])
            S.op("dve", lambda: nc.vector.reciprocal(rs[0:np_, :], rs[0:np_, :]), [rsk], [rsk])
            for k in range(nk):
                S.op("dve", lambda k=k: nc.vector.scalar_tensor_tensor(out=dst[:, k, :], in0=src[:, k, :], scalar=gvec[:, k:k + 1],
                                                                      in1=rs[0:np_, :], op0=ALU.mult, op1=ALU.mult),
                     [tag + "_src", rsk, "gvecs"], [tag + "_dst"])

        def phase12():
          with ExitStack() as p12:
            KT = sb("KT", [128, 2, L], BF16, p12)
            Vx = sb("Vx", [128, 64, 2, 64], BF16, p12)
            QT = sb("QT", [128, 4, NTOK], BF16, p12)
            onesb = sb("onesb", [128, 64], BF16, p12)
            S.op("pool", lambda: nc.gpsimd.memset(onesb[:], 1.0), [], ["onesb"])
            with ExitStack() as p1:
                Wb = sb("Wb", [128, 8, 2304], BF16, p1)
                with ExitStack() as p1c:
                    wst = [sb(f"wsi{i}", [128, 8, 384], F32, p1c) for i in range(2)]
                    wv = w_in.rearrange("(k p) n -> p k n", p=128)
                    for bk in range(6):
                        s_ = bk % 2
                        dma(wst[s_][:], wv[:, :, bk * 384:(bk + 1) * 384], [], [f"wst{s_}"])
                        if bk % 2 == 0:
                            S.op("dve", lambda s_=s_, bk=bk: nc.vector.tensor_copy(Wb[:, :, bk * 384:(bk + 1) * 384], wst[s_][:]), [f"wst{s_}"], ["Wb"])
                        else:
                            S.op("act", lambda s_=s_, bk=bk: nc.scalar.copy(Wb[:, :, bk * 384:(bk + 1) * 384], wst[s_][:]), [f"wst{s_}"], ["Wb"])
                    S.barrier()
                xin = [sb(f"xin{i}", [128, 8, 512], F32, p1) for i in range(2)]
                sqbL = [sb(f"sqb{i}", [128, 8, 512], BF16, p1) for i in range(2)]
                rsL = [sb(f"rs{i}", [128, 512], F32, p1) for i in range(2)]
                aTL = [sb(f"aT{i}", [128, 8, 512], BF16, p1) for i in range(2)]
                ust = sb("ust", [128, 12, 512], BF16, p1)
                hsq = sb("hsq", [64, 512], BF16, p1); hrs = sb("hrs", [64, 512], F32, p1); hkn = sb("hkn", [64, 512], F32, p1)
                ht1 = sb("ht1", [64, 512], F32, p1); ht2 = sb("ht2", [64, 512], F32, p1)
                csb = [sb(f"csb{i}", [64, 2, 512], F32, p1) for i in range(2)]
                qstg = [sb(f"qstg{i}", [64, 512], BF16, p1) for i in range(2)]

                def hp1(srcb, gvec):
                    src = PS[srcb][0:64, :]
                    sk_ = f"ps{srcb}"
                    S.op("act", lambda: nc.scalar.activation(hsq[:], src, AF.Square), [sk_], ["hsq"])
                    mm(PS[3][0:64, :], onesbf[0:64, 0:64], hsq[:], True, True, ["hsq", "onesbf"], ["ps3"])
                    S.op("act", lambda: nc.scalar.activation(hrs[:], PS[3][0:64, :], AF.Sqrt, bias=epsT[0:64, :], scale=1.0 / 64.0),
                         ["ps3", "epsT"], ["hrs"])
                    S.op("dve", lambda: nc.vector.reciprocal(hrs[:], hrs[:]), ["hrs"], ["hrs"])
                    S.op("dve", lambda: nc.vector.scalar_tensor_tensor(out=hkn[:], in0=src, scalar=gvec[:, 0:1], in1=hrs[:],
                                                                      op0=ALU.mult, op1=ALU.mult), [sk_, "hrs", "gvecs"], ["hkn"])

                def hp2(cs, cskey, dst, dstkey):
                    mm(PS[3][0:64, :], rot[:], hkn[:], True, True, ["hkn", "rot"], ["ps3"])
                    S.op("pool", lambda: nc.gpsimd.tensor_tensor(ht1[:], hkn[:], cs[:, 0, :], ALU.mult), ["hkn", cskey], ["ht1"])
                    S.op("dve", lambda: nc.vector.tensor_tensor(ht2[:], PS[3][0:64, :], cs[:, 1, :], ALU.mult), ["ps3", cskey], ["ht2"])
                    S.op("pool", lambda: nc.gpsimd.tensor_tensor(dst, ht1[:], ht2[:], ALU.add), ["ht1", "ht2"], [dstkey])

                xv = xT.rearrange("(k p) t -> p k t", p=128)
                xqv = xTq.rearrange("(k p) t -> p k t", p=128)
                Uv = U_d.rearrange("(ct p) t -> p ct t", p=128)
                work = [("seq", j) for j in range(16)] + [("own", j) for j in range(4)]

                def norm_a(w):
                    kind, j = work[w]
                    s_ = w % 2
                    tsl = slice(j * 512, (j + 1) * 512)
                    srcv, cc, ss_ = (xv, c_cos, c_sin) if kind == "seq" else (xqv, c_cosq, c_sinq)
                    dma(xin[s_][:], srcv[:, :, tsl], [], [f"xin{s_}", f"n1{s_}_src"])
                    dma(csb[s_][:, 0, :], cc[:, tsl], [], [f"csb{s_}"])
                    dma(csb[s_][:, 1, :], ss_[:, tsl], [], [f"csb{s_}"])
                    rms_a(xin[s_][:], 8, f"n1{s_}", sqbL[s_], 128, f"sqb{s_}")

                def norm_b(w):
                    s_ = w % 2
                    rms_b(xin[s_][:], g1s, aTL[s_], 1.0 / D, 8, f"n1{s_}", sqbL[s_], rsL[s_], 128, 0, f"sqb{s_}", f"rs{s_}")

                def emit_proj(w, mid_hook):
                    kind, j = work[w]
                    s_ = w % 2
                    aT = aTL[s_]
                    ak = f"n1{s_}_dst"
                    cs, csk = csb[s_], f"csb{s_}"
                    tsl = slice(j * 512, (j + 1) * 512)
                    if kind == "seq":
                        for kvh in range(2):
                            for k in range(8):
                                mm(PS[1 + kvh][0:64, :], Wb[:, k, 512 + kvh * 64:512 + (kvh + 1) * 64], aT[:, k, :], k == 0, k == 7,
                                   ["Wb", ak], [f"ps{1 + kvh}"])
                        for tt in range(4):
                            for k in range(8):
                                mm(PS[4][:, tt * 128:(tt + 1) * 128], aT[:, k, tt * 128:(tt + 1) * 128], Wb[:, k, 640:768], k == 0, k == 7,
                                   ["Wb", ak], ["ps4"])
                        S.op("act", lambda j=j: nc.scalar.copy(Vx[:, j * 4:(j + 1) * 4, :, :],
                                                             PS[4][:, :].rearrange("p (t h d) -> p t h d", t=4, h=2)), ["ps4"], ["Vx"])
                        for ct in range(12):
                            pu, puk = (PS[5], "ps5") if ct % 2 == 0 else (PSX[0], "psb0")
                            for k in range(8):
                                mm(pu[:, :], Wb[:, k, 768 + ct * 128:768 + (ct + 1) * 128], aT[:, k, :], k == 0, k == 7, ["Wb", ak], [puk])
                            if ct % 2 == 0:
                                S.op("act", lambda ct=ct, pu=pu: nc.scalar.copy(ust[:, ct, :], pu[:, :]), [puk], ["ust"])
                            else:
                                S.op("dve", lambda ct=ct, pu=pu: nc.vector.tensor_copy(ust[:, ct, :], pu[:, :]), [puk], ["ust"])
                            if ct == 1:
                                hp1(1, gks)
                            if ct == 3:
                                hp2(cs, csk, KT[0:64, 0, tsl], "KT")
                                hp1(2, gks)
                            if ct == 5:
                                hp2(cs, csk, KT[0:64, 1, tsl], "KT")
                            if ct == 7:
                                mid_hook()
                        dma(Uv[:, :, tsl], ust[:], ["ust"], ["U_d"])
                    else:
                        def qdst(h):
                            if h % 2 == 0:
                                return QT[0:64, h // 2, tsl], "QT", None
                            qs_ = (h // 2) % 2
                            return qstg[qs_][:], f"qstg{qs_}", qs_

                        def qproj(h):
                            hb = 1 + h % 2
                            for k in range(8):
                                mm(PS[hb][0:64, :], Wb[:, k, h * 64:(h + 1) * 64], aT[:, k, :], k == 0, k == 7, ["Wb", ak], [f"ps{hb}"])

                        qproj(0)
                        for h in range(8):
                            hp1(1 + h % 2, gqs)
                            if h + 1 < 8:
                                qproj(h + 1)
                            if h == 3:
                                mid_hook()
                            d_, dk_, qs_ = qdst(h)
                            hp2(cs, csk, d_, dk_)
                            if qs_ is not None:
                                dma(QT[64:128, h // 2, tsl], qstg[qs_][:], [f"qstg{qs_}"], ["QT2"])

                norm_a(0)
                norm_b(0)
                for w in range(len(work)):
                    if w + 1 < len(work):
                        norm_a(w + 1)
                        emit_proj(w, lambda w=w: norm_b(w + 1))
                    else:
                        emit_proj(w, lambda: None)
                dma(KT[64:128, :, :], KT[0:64, :, :], ["KT"], ["KT2"])
                S.barrier()
            if 2 in phases:
              with ExitStack() as p2:
                pb = [sb(f"pb{i}", [128, 1024], BF16, p2) for i in range(2)]
                cwst = [sb(f"wst{i}", [128, 8, 512], F32, p2) for i in range(2)]
                cwbf = [sb(f"wbf{i}", [128, 8, 512], BF16, p2) for i in range(2)]
                rden = [sb(f"rden{i}", [128, 512], F32, p2) for i in range(2)]
                ast = [sb(f"ast{i}", [128, 512], BF16, p2) for i in range(2)]
                n_ = 0
                for qc in range(4):
                    qsl = slice(qc * 512, (qc + 1) * 512)
                    for hp in range(4):
                        kvh = hp // 2
                        (OB, obk), (DB, dbk) = ((PS[4], "ps4"), (PS[5], "ps5")) if n_ % 2 == 0 else ((PSX[0], "psb0"), (PSX[1], "psb1"))
                        emit_cast(qc * 4 + hp, cwst, cwbf)
                        for step in range(64 + 1):
                            if step < 64:
                                kt = step
                                sl = kt % 2
                                ksl = slice(kt * 128, (kt + 1) * 128)
                                mm(PS[2 * sl][:, :], KT[0:64, kvh, ksl], QT[0:64, hp, qsl], True, True, ["KT", "QT"], [f"ps{2 * sl}"])
                                mm(PS[2 * sl + 1][:, :], KT[64:128, kvh, ksl], QT[64:128, hp, qsl], True, True, ["KT2", "QT2"], [f"ps{2 * sl + 1}"])
                                S.op("act", lambda sl=sl: nc.scalar.activation(pb[sl][:], PSW[sl][:, :], AF.Exp, scale=0.125),
                                     [f"ps{2 * sl}", f"ps{2 * sl + 1}"], [f"pb{sl}"])
                            kt = step - 1
                            if kt >= 0:
                                sl = kt % 2
                                st_, sp_ = kt == 0, kt == 63
                                pA = pb[sl][:, 0:512]
                                pB = pb[sl][:, 512:1024]
                                vv = Vx[:, kt, kvh, :]
                                S.op("pe", lambda vv=vv, pA=pA, st_=st_, sp_=sp_, OB=OB: nc.tensor.matmul(OB[0:64, :], vv, pA, start=st_, stop=sp_, tile_position=(0, 0)),
                                     ["Vx", f"pb{sl}"], [obk])
                                S.op("pe", lambda vv=vv, pB=pB, st_=st_, sp_=sp_, OB=OB: nc.tensor.matmul(OB[64:128, :], vv, pB, start=st_, stop=sp_, tile_position=(0, 64)),
                                     ["Vx", f"pb{sl}"], [obk])
                                S.op("pe", lambda pA=pA, st_=st_, sp_=sp_, DB=DB: nc.tensor.matmul(DB[0:64, :], onesb[:], pA, start=st_, stop=sp_, tile_position=(0, 0)),
                                     ["onesb", f"pb{sl}"], [dbk])
                                S.op("pe", lambda pB=pB, st_=st_, sp_=sp_, DB=DB: nc.tensor.matmul(DB[64:128, :], onesb[:], pB, start=st_, stop=sp_, tile_position=(0, 64)),
                                     ["onesb", f"pb{sl}"], [dbk])
                        a_ = n_ % 2
                        S.op("dve", lambda DB=DB, a_=a_: nc.vector.reciprocal(rden[a_][:], DB[:, :]), [dbk], [f"rden{a_}"])
                        S.op("dve", lambda a_=a_, OB=OB: nc.vector.tensor_tensor(ast[a_][:], OB[:, :], rden[a_][:], ALU.mult), [obk, f"rden{a_}"], [f"ast{a_}"])
                        dma(att_d[hp * 128:(hp + 1) * 128, qsl], ast[a_][:], [f"ast{a_}"], ["att_d"])
                        n_ += 1
                S.barrier()

        def phase3():
          with ExitStack() as p3:
            load_fft_consts(p3)
            finv = FC["finv"]
            M2s = sb("M2s", [128, 128, 64], BF16, p3)
            M2qs = sb("M2qs", [128, 128, 16], BF16, p3)
            dma(M2s[:], c_M2, [], ["M2s"])
            dma(M2qs[:], c_M2q, [], ["M2qs"])
            RA = sb("RA", [128, 192 * 128 + 8], BF16, p3)
            RB = sb("RB", [128, 128 * 128], BF16, p3)
            RC = sb("RC", [128, 128 * 128], BF16, p3)
            vt = sb("vt", [128, L], BF16, p3)
            x1t = sb("x1t", [128, L], BF16, p3)
            x2q = sb("x2q", [128, NTOK], BF16, p3)
            zzq = sb("zzq", [128, NTOK], BF16, p3)
            hyst = sb("hyst", [128, NTOK], BF16, p3)
            Hc = [sb(f"Hc{i}", [128, 2, 4, 128], BF16, p3) for i in range(4)]
            Pb = [sb(f"Pb{i}", [128, 2, 2, 128], BF16, p3) for i in range(2)]
            Qb = [sb(f"Qb{i}", [128, 2, 2, 128], BF16, p3) for i in range(2)]
            gt = [sb(f"gt{i}", [128, 512], F32, p3) for i in range(2)]
            whi = sb("whi", [128, 9], BF16, p3)
            whf = sb("whf", [128, 9], F32, p3)
            wlf = sb("wlf", [128, 9], F32, p3)
            sct = gt
            uraw = RA[:, 0:3 * 8194].rearrange("p (i t) -> p i t", i=3)
            AT = RA[:, 0:192 * 128].rearrange("p (j c) -> p j c", c=128)
            ztm = RB[:, :].rearrange("p (c n) -> p c n", n=128)
            Eb = RB[:, :].rearrange("p (n c) -> p n c", c=128)
            x2t = RC[:, 0:L]
            ET = RC[:, :].rearrange("p (c k) -> p c k", k=128)
            RAK, RBK, RCK = "RA", "RB", "RC"

            def masked_quarter(dst, src, dkey, skey):
                S.op("dve", lambda: nc.vector.tensor_scalar(dst[:], src[:, 0:NTOK], qms[:, 0:1], None, ALU.mult), [skey, "qms"], [dkey])
                for q in range(1, 4):
                    S.op("dve", lambda q=q: nc.vector.scalar_tensor_tensor(out=dst[:], in0=src[:, q * NTOK:(q + 1) * NTOK], scalar=qms[:, q:q + 1],
                                                                          in1=dst[:], op0=ALU.mult, op1=ALU.add), [skey, "qms", dkey], [dkey])

            RBQ = [(RBK, q4) for q4 in range(4)]

            def bounce_to_ztm(src, skey):
                zv = zcm_d.rearrange("c (n1 n2) -> n1 c n2", n2=128)
                for q4 in range(4):
                    dma(zcm_d[q4 * 32:(q4 + 1) * 32, :], src[q4 * 32:(q4 + 1) * 32, :], [skey], [("zcm_d", q4)], eng="pool")
                for q4 in range(4):
                    dma(ztm[0:64, q4 * 32:(q4 + 1) * 32, :], zv[:, q4 * 32:(q4 + 1) * 32, :], [("zcm_d", q4)], [(RBK, q4)], eng="pool")

            def conv_core(go):
                def load_h(kb):
                    dma(Hc[kb % 4][:], H_d[go, :, kb * 2:kb * 2 + 2], [("H_d", go)], [f"Hc{kb % 4}"])

                def pre(kb):
                    if kb == 0:
                        load_h(0)
                        load_h(1)
                    if kb + 2 < 32:
                        load_h(kb + 2)

                def cb(kb, ps, pk):
                    s_ = kb % 4
                    y_ = kb % 2
                    zv = ps[:, :].rearrange("p (k r c) -> p k r c", k=2, r=2)
                    S.op("dve", lambda: nc.vector.tensor_tensor(Pb[y_][:], zv, Hc[s_][:, :, 0:2, :], ALU.mult), [pk, f"Hc{s_}"], [f"Pb{y_}"])
                    S.op("dve", lambda: nc.vector.tensor_tensor(Qb[y_][:], zv, Hc[s_][:, :, 2:4, :], ALU.mult), [pk, f"Hc{s_}"], [f"Qb{y_}"])

                def cb2(kb):
                    s_ = kb % 4
                    y_ = kb % 2
                    pe_ = PS[4 + kb % 2]
                    pek = f"ps{4 + kb % 2}"
                    er = pe_[:, 0:256].rearrange("p (k c) -> p k c", k=2)
                    ei = pe_[:, 256:512].rearrange("p (k c) -> p k c", k=2)
                    p0, p1 = Pb[y_][:, :, 0, :], Pb[y_][:, :, 1, :]
                    q0, q1 = Qb[y_][:, :, 0, :], Qb[y_][:, :, 1, :]
                    rk = [f"Pb{y_}", f"Qb{y_}", "finv"]
                    FR, FI, NFI, NFR = finv[:, 0, :], finv[:, 1, :], finv[:, 2, :], finv[:, 3, :]
                    mm(er, FR, p0, True, False, rk, [pek]); mm(er, NFR, p1, False, False, rk, [pek])
                    mm(er, NFI, q0, False, False, rk, [pek]); mm(er, NFI, q1, False, True, rk, [pek])
                    mm(ei, FI, p0, True, False, rk, [pek]); mm(ei, NFI, p1, False, False, rk, [pek])
                    mm(ei, FR, q0, False, False, rk, [pek]); mm(ei, FR, q1, False, True, rk, [pek])
                    S.op("act", lambda: nc.scalar.copy(ET[:, :, kb * 2:kb * 2 + 2], pe_[:, 0:256].rearrange("p (k c) -> p c k", k=2)), [pek], [RCK])
                    S.op("act", lambda: nc.scalar.copy(ET[:, :, 64 + kb * 2:64 + kb * 2 + 2], pe_[:, 256:512].rearrange("p (k c) -> p c k", k=2)),
                         [pek], [RCK])
                fft_forward(ztm, 64, AT, cb, zkey=RBK, akey=RAK, pre_batch=pre, cb_late=cb2)
                TB = [(PSB[0], "psb0"), (PSB[1], "psb1"), (PSW[0].bitcast(BF16), "ps0"), (PSW[1].bitcast(BF16), "ps2")]
                for c4 in range(32):
                    pb_, pbk = TB[c4 % 4]
                    for cc in range(4):
                        c = c4 * 4 + cc
                        S.op("pe", lambda c=c, cc=cc, pb_=pb_: nc.tensor.transpose(pb_[:, cc * 128:(cc + 1) * 128], ET[:, c, :], ident[:]),
                             [RCK, "ident"], [pbk])
                    src = pb_[:, 0:512].rearrange("p (cc n) -> p n cc", cc=4)
                    dst = Eb[:, :, c4 * 4:c4 * 4 + 4]
                    if c4 % 2 == 0:
                        S.op("act", lambda src=src, dst=dst: nc.scalar.copy(dst, src), [pbk], RBQ)
                    else:
                        S.op("dve", lambda src=src, dst=dst: nc.vector.tensor_copy(dst, src), [pbk], RBQ)

            Udv = U_d
            for g in range(4):
                for i in range(3):
                    dma(uraw[:, i, 1:L + 1], Udv[i * 512 + g * 128:i * 512 + (g + 1) * 128, :], ["U_d"], [RAK])
                S.op("pool", lambda: nc.gpsimd.memset(uraw[:, :, 0:1], 0.0), [RAK], [RAK])
                S.op("pool", lambda: nc.gpsimd.memset(uraw[:, :, L + 1:L + 2], 0.0), [RAK], [RAK])
                DG = hyst[:, 0:9 * 128].rearrange("p (i m) -> p i m", m=128)
                DGL = zzq[:, 0:9 * 128].rearrange("p (i m) -> p i m", m=128)
                wsel = cws[:, g:12:4, :]
                S.op("dve", lambda wsel=wsel: nc.vector.tensor_copy(whi[:].rearrange("p (i j) -> p i j", j=3), wsel), ["cws"], ["whi"])
                S.op("dve", lambda: nc.vector.tensor_copy(whf[:], whi[:]), ["whi"], ["whf"])
                S.op("dve", lambda wsel=wsel: nc.vector.tensor_tensor(wlf[:].rearrange("p (i j) -> p i j", j=3), wsel,
                                                                     whf[:].rearrange("p (i j) -> p i j", j=3), ALU.subtract), ["cws", "whf"], ["wlf"])
                for idx in range(9):
                    S.op("dve", lambda idx=idx: nc.vector.tensor_scalar(DG[:, idx, :], ident[:], whf[:, idx:idx + 1], None, ALU.mult), ["whf", "ident"], ["hyst"])
                    S.op("dve", lambda idx=idx: nc.vector.tensor_scalar(DGL[:, idx, :], ident[:], wlf[:, idx:idx + 1], None, ALU.mult), ["wlf", "ident"], ["zzq"])
                nq = 0
                for i, (dst, dkey) in enumerate(((vt, "vt"), (x1t, "x1t"), (x2t, RCK))):
                    ct = i * 4 + g
                    for q in range(16):
                        o0 = q * 512
                        ps = PS[nq % 2]
                        pk = f"ps{nq % 2}"
                        for j in range(3):
                            mm(ps[:, :], DG[:, i * 3 + j, :], uraw[:, i, o0 + j:o0 + j + 512], j == 0, False, ["hyst", RAK], [pk])
                        for j in range(3):
                            mm(ps[:, :], DGL[:, i * 3 + j, :], uraw[:, i, o0 + j:o0 + j + 512], False, j == 2, ["zzq", RAK], [pk])
                        if nq % 2 == 0:
                            S.op("act", lambda ps=ps, dst=dst, o0=o0, ct=ct: nc.scalar.activation(dst[:, o0:o0 + 512], ps[:, :], AF.Identity, bias=cbs[:, ct:ct + 1]),
                                 [pk, "cbs"], [dkey])
                        else:
                            S.op("dve", lambda ps=ps, dst=dst, o0=o0, ct=ct: nc.vector.tensor_scalar(dst[:, o0:o0 + 512], ps[:, :], cbs[:, ct:ct + 1], None, ALU.add),
                                 [pk, "cbs"], [dkey])
                        nq += 1
                masked_quarter(x2q, x2t, "x2q", RCK)
                snap("vt0", vt[:], "vt", L)
                snap("x1t0", x1t[:], "x1t", L)
                bounce_to_ztm(vt, "vt")
                conv_core(g * 2 + 0)
                vt3 = vt[:, :].rearrange("p (n1 n2) -> p n1 n2", n2=128)
                x13 = x1t[:, :].rearrange("p (n1 n2) -> p n1 n2", n2=128)
                for nb in range(16):
                    ps = PS[nb % 2]
                    pk = f"ps{nb % 2}"
                    for j in range(8):
                        n2 = nb * 8 + j
                        mm(ps[:, j * 64:(j + 1) * 64], Eb[:, n2, :], M2s[:, n2, :], True, True, RBQ + ["M2s"], [pk])
                    psv = ps[:, :].rearrange("p (j n) -> p n j", j=8)
                    gtv = gt[nb % 2][:, :].rearrange("p (n j) -> p n j", j=8)
                    gk_ = f"gt{nb % 2}"
                    S.op("dve", lambda psv=psv, gtv=gtv, nb=nb, g=g: nc.vector.scalar_tensor_tensor(out=gtv, in0=vt3[:, :, nb * 8:(nb + 1) * 8],
                                                                                             scalar=skps[:, g:g + 1], in1=psv, op0=ALU.mult, op1=ALU.add),
                         [pk, "vt", "skps"], [gk_])
                    ge_ = "pool" if nb % 2 == 0 else "dve"
                    S.op(ge_, lambda gtv=gtv, nb=nb, ge_=ge_: V(ge_).tensor_tensor(vt3[:, :, nb * 8:(nb + 1) * 8], gtv, x13[:, :, nb * 8:(nb + 1) * 8], ALU.mult),
                         [gk_, "x1t"], ["vt"])
                snap("zz0", vt[:], "vt", L)
                masked_quarter(zzq, vt, "zzq", "vt")
                bounce_to_ztm(vt, "vt")
                conv_core(g * 2 + 1)
                zq3 = zzq[:, :].rearrange("p (n1 n2) -> p n1 n2", n2=128)
                xq3 = x2q[:, :].rearrange("p (n1 n2) -> p n1 n2", n2=128)
                hy3 = hyst[:, :].rearrange("p (n1 n2) -> p n1 n2", n2=128)
                for nb in range(4):
                    ps = PS[nb % 2]
                    pk = f"ps{nb % 2}"
                    for j in range(32):
                        n2 = nb * 32 + j
                        mm(ps[:, j * 16:(j + 1) * 16], Eb[:, n2, :], M2qs[:, n2, :], True, True, RBQ + ["M2qs"], [pk])
                    psv = ps[:, :].rearrange("p (j n) -> p n j", j=32)
                    gtv = gt[nb % 2][:, :].rearrange("p (n j) -> p n j", j=32)
                    gk_ = f"gt{nb % 2}"
                    S.op("dve", lambda psv=psv, gtv=gtv, nb=nb, g=g: nc.vector.scalar_tensor_tensor(out=gtv, in0=zq3[:, :, nb * 32:(nb + 1) * 32],
                                                                                             scalar=skps[:, 4 + g:5 + g], in1=psv, op0=ALU.mult, op1=ALU.add),
                         [pk, "zzq", "skps"], [gk_])
                    S.op("pool", lambda gtv=gtv, nb=nb: nc.gpsimd.tensor_tensor(hy3[:, :, nb * 32:(nb + 1) * 32], gtv, xq3[:, :, nb * 32:(nb + 1) * 32], ALU.mult),
                         [gk_, "x2q"], ["hyst"])
                dma(hy_d[g * 128:(g + 1) * 128, :], hyst[:], ["hyst"], ["hy_d"])
            S.barrier()

        def phase4():
          with ExitStack() as p4:
            WoA = sb("WoA", [64, 8, D], BF16, p4)
            WoH = sb("WoH", [128, 4, D], BF16, p4)
            with ExitStack() as p4c:
                wsa = sb("wsa", [64, 8, D], F32, p4c)
                wsh = sb("wsh", [128, 4, D], F32, p4c)
                dma(wsa[:], w_out[0:512, :].rearrange("(h d) n -> d h n", d=64), [], ["wsa"])
                dma(wsh[:], w_out[512:1024, :].rearrange("(g p) n -> p g n", p=128), [], ["wsh"])
                S.op("dve", lambda: nc.vector.tensor_copy(WoA[:], wsa[:]), ["wsa"], ["WoA"])
                S.op("pool", lambda: nc.gpsimd.tensor_copy(WoH[:], wsh[:]), ["wsh"], ["WoH"])
                S.barrier()
            attc = sb("attc", [64, 8, 512], BF16, p4)
            hyc = sb("hyc", [128, 4, 512], BF16, p4)
            sqb = sb("sqb4", [128, 8, 512], BF16, p4)
            rs = sb("rs4", [128, 512], F32, p4)
            mixA = sb("mixA", [64, 8, 512], BF16, p4)
            mixH = sb("mixH", [128, 4, 512], BF16, p4)
            h1 = sb("h1", [128, 8, 512], F32, p4)
            mT = sb("mT", [128, 8, 512], BF16, p4)
            Wmi_s = [sb(f"Wmi_s{i}", [128, 8, 1024], BF16, p4) for i in range(2)]
            Wmo_s = [sb(f"Wmo_s{i}", [128, 32, 128], BF16, p4) for i in range(2)]
            fT = sb("fT", [128, 32, 512], BF16, p4)
            rb = [sb(f"rb{i}", [128, 512], F32, p4) for i in range(2)]
            xqv = xTq.rearrange("(k p) t -> p k t", p=128)
            Wmiv = Wmi_d.rearrange("(k p) n -> p k n", p=128)
            outv = outT.rearrange("(k p) t -> p k t", p=128)
            nw = 0
            nwo = 0
            for j in range(4):
                tsl = slice(j * 512, (j + 1) * 512)
                dma(attc[:], att_d[:, tsl].rearrange("(h d) t -> d h t", d=64), ["att_d"], ["attc", "na_src"])
                dma(hyc[:], hy_d[:, tsl].rearrange("(g p) t -> p g t", p=128), ["hy_d"], ["hyc", "nh_src"])
                dma(h1[:], xqv[:, :, tsl], [], ["h1", "n2_src", "nf_src"])
                rms_apply(attc[:], gas, mixA, 1.0 / 512.0, 8, "na", sqb, rs, np_=64)
                rms_apply(hyc[:], ghs, mixH, 1.0 / 512.0, 4, "nh", sqb, rs)
                for ct in range(8):
                    csl = slice(ct * 128, (ct + 1) * 128)
                    for h in range(8):
                        mm(PS[1][:, :], WoA[:, h, csl], mixA[:, h, :], h == 0, False, ["WoA", "na_dst"], ["ps1"])
                    for g in range(4):
                        mm(PS[1][:, :], WoH[:, g, csl], mixH[:, g, :], False, g == 3, ["WoH", "nh_dst"], ["ps1"])
                    S.op("dve", lambda ct=ct: nc.vector.tensor_tensor(h1[:, ct, :], h1[:, ct, :], PS[1][:, :], ALU.add), ["ps1", "h1"], ["h1", "n2_src", "nf_src"])
                rms_apply(h1[:], g2s, mT, 1.0 / D, 8, "n2", sqb, rs)
                for pc in range(4):
                    s_ = nw % 2
                    dma(Wmi_s[s_][:], Wmiv[:, :, pc * 1024:(pc + 1) * 1024], ["Wmid"], [f"Wmi_s{s_}"])
                    nw += 1
                    for ft in range(8):
                        f = pc * 8 + ft
                        pf = PS[2 + f % 2]
                        pfk = f"ps{2 + f % 2}"
                        for k in range(8):
                            mm(pf[:, :], Wmi_s[s_][:, k, ft * 128:(ft + 1) * 128], mT[:, k, :], k == 0, k == 7, [f"Wmi_s{s_}", "n2_dst"], [pfk])
                        r_ = f % 2
                        S.op("act", lambda pf=pf, r_=r_: nc.scalar.activation(rb[r_][:], pf[:, :], AF.Relu), [pfk], [f"rb{r_}"])
                        e = ("pool", "dve")[f % 2]
                        S.op(e, lambda e=e, r_=r_, f=f: V(e).tensor_tensor(fT[:, f, :], rb[r_][:], rb[r_][:], ALU.mult), [f"rb{r_}"], ["fT"])
                for ct in range(8):
                    s_ = nwo % 2
                    dma(Wmo_s[s_][:], Wmo_d[ct], ["Wmod"], [f"Wmo_s{s_}"])
                    nwo += 1
                    po = PS[4 + ct % 2]
                    pok = f"ps{4 + ct % 2}"
                    for f in range(32):
                        mm(po[:, :], Wmo_s[s_][:, f, :], fT[:, f, :], f == 0, f == 31, [f"Wmo_s{s_}", "fT"], [pok])
                    S.op("dve", lambda ct=ct, po=po: nc.vector.tensor_tensor(h1[:, ct, :], h1[:, ct, :], po[:, :], ALU.add), [pok, "h1"], ["h1", "n2_src", "nf_src"])
                rms_apply(h1[:], gfs, h1, 1.0 / D, 8, "nf", sqb, rs)
                dma(outv[:, :, tsl], h1[:], ["h1", "nf_dst"], ["outT"])
            S.barrier()

        if 0 in phases:
            phase0()
        if 1 in phases:
            phase12()
        if 3 in phases:
            phase3()
        if 4 in phases:
            phase4()
        if debug:
            with ExitStack() as pd:
                for name in debug:
                    if name not in ("att_d", "hy_d", "U_d", "H_d0"):
                        continue
                    src_, shp = {"att_d": (att_d, [512, NTOK]), "hy_d": (hy_d, [512, NTOK]), "U_d": (U_d[0:512, :], [512, L]),
                                 "H_d0": (H_d[0:4, :, 0:4].rearrange("a p k r c -> (a p) (k r c)"), [512, 2048])}[name]
                    dbo = nc.dram_tensor("dbg_" + name, shp, BF16, kind="ExternalOutput").ap()
                    dbs = sb("dbs_" + name, [128, 4, shp[1]], BF16, pd)
                    dma(dbs[:], src_.rearrange("(a p) t -> p a t", p=128), [name], ["dbs_" + name])
                    dma(dbo.rearrange("(a p) t -> p a t", p=128), dbs[:], ["dbs_" + name], ["dbg_" + name])
                S.barrier()
        S.barrier()
    return nc, S


def _core_inputs(inp, core):
    T = _tables()
    b, r = core // 4, core % 4
    f = lambda a: np.ascontiguousarray(np.asarray(a, dtype=np.float32))
    x = np.asarray(inp["x"], dtype=np.float32)
    xT = np.ascontiguousarray(x[b].T)
    tq = slice(r * NTOK, (r + 1) * NTOK)
    qm = np.zeros((128, 4), np.float32)
    qm[:, r] = 1.0
    m = {
        "xT": xT, "xTq": np.ascontiguousarray(xT[:, tq]),
        "w_in": f(inp["w_in"][0]), "w_out": f(inp["w_out"][0]), "w_mi": f(inp["w_mlp_in"][0]), "w_mo": f(inp["w_mlp_out"][0]),
        "g1": f(np.asarray(inp["norm1_g"][0]).reshape(8, 128).T), "g2": f(np.asarray(inp["norm2_g"][0]).reshape(8, 128).T),
        "gf": f(np.asarray(inp["final_g"]).reshape(8, 128).T),
        "gq": f(np.asarray(inp["q_norm_g"][0]).reshape(64, 1)), "gk": f(np.asarray(inp["k_norm_g"][0]).reshape(64, 1)),
        "ga": f(np.asarray(inp["attn_out_g"][0]).reshape(8, 64).T), "gh": f(np.asarray(inp["hy_out_g"][0]).reshape(4, 128).T),
        "cw": f(np.asarray(inp["hy_conv_w"][0]).T.reshape(12, 128, 3).transpose(1, 0, 2)),
        "cb": f(np.asarray(inp["hy_conv_b"][0]).reshape(12, 128).T),
        "fw1": f(inp["filt_w1"][0]), "fw2": f(inp["filt_w2"][0]), "fw3": f(inp["filt_w3"][0]), "fw4": f(inp["filt_w4"][0]),
        "fb": f(np.stack([np.asarray(inp["filt_b1"][0]), np.asarray(inp["filt_b2"][0]), np.asarray(inp["filt_b3"][0])], axis=1)),
        "ffreq": f(np.asarray(inp["filt_freq"][0]).reshape(64, 1)),
        "fdel": f(np.asarray(inp["filt_deltas"][0]).reshape(16, 128).T),
        "skp": f(np.asarray(inp["hy_skip_d"][0]).reshape(2, 4, 128).transpose(2, 0, 1).reshape(128, 8)),
        "qmask": qm,
        "c_cos": T["cosT"], "c_sin": T["sinT"],
        "c_cosq": np.ascontiguousarray(T["cosT"][:, tq]), "c_sinq": np.ascontiguousarray(T["sinT"][:, tq]),
        "c_zT": np.ascontiguousarray(np.stack([T["zT"], T["zrT"]])), "c_t": np.ascontiguousarray(np.stack([T["t"], T["trev"]])),
        "c_tb": T["tb"], "c_jt": T["jt"],
        "c_fpack": T["fpack"], "c_G": T["G"], "c_finv": T["finv"], "c_M2": T["M2"],
        "c_M2q": np.ascontiguousarray(T["M2"][:, :, r * 16:(r + 1) * 16]),
        "c_rot": T["rot"], "c_sel": T["sel"], "c_ones": T["ones"], "c_ident": T["ident"],
    }
    return m


def _emit(nc, S):
    from contextlib import ExitStack
    st = ExitStack()
    sems = {e: [st.enter_context(nc.semaphore(f"s_{e}{i}")) for i in range(4)] for e in S.CE}
    dsems = [st.enter_context(nc.semaphore(f"d{i}")) for i in range(S.NDS)]
    info = S.emit(sems, dsems)
    return st, info


def kernel(**inputs):
    nc, S = build_program()
    st, _ = _emit(nc, S)
    with st:
        in_maps = [_core_inputs(inputs, c) for c in range(8)]
        res = run_bass_kernel_spmd(nc, in_maps, core_ids=list(range(8)))
    out = np.empty((NB, L, D), np.float32)
    for c in range(8):
        b, r = c // 4, c % 4
        out[b, r * NTOK:(r + 1) * NTOK, :] = np.asarray(res.results[c]["outT"]).T
    return out
```

```python
import math
import numpy as np
import ml_dtypes
import concourse.bass as bass
import concourse.mybir as mybir
from concourse.bass_utils import run_bass_kernel_spmd

F32 = mybir.dt.float32
BF16 = mybir.dt.bfloat16
ALU = mybir.AluOpType
AF = mybir.ActivationFunctionType
bf16_np = ml_dtypes.bfloat16

D = 1024
L = 8192
NB = 2
NTOK = 2048
HD = 64
NQH = 8
NKV = 2
HYW = 512
DFF = 4096
EPS = 1e-6
NFFT = 2 * L
TWO_PI = 2.0 * math.pi
MAGIC = 12582912.0


class Op:
    __slots__ = ("eng", "fn", "reads", "writes", "dma", "idx", "gid", "waits", "signal", "clock",
                 "sem", "val")


class Sched:
    CE = ("pe", "act", "dve", "pool", "sp")
    EPOCH = 12000
    NDS = 40

    def __init__(self, nc):
        self.nc = nc
        self.E = {"pe": nc.tensor, "act": nc.scalar, "dve": nc.vector, "pool": nc.gpsimd,
                  "sp": nc.sync}
        self.ops = []
        self.last_w = {}
        self.readers = {}
        self.known = {e: {} for e in self.CE}
        self.known_dma = {e: set() for e in self.CE}
        self.cnt = {e: 0 for e in self.CE}
        self.last_op = {e: None for e in self.CE}
        self.live_dma = []

    def op(self, eng, fn, reads=(), writes=(), dma=False):
        o = Op()
        o.eng, o.fn, o.reads, o.writes, o.dma = eng, fn, tuple(reads), tuple(writes), dma
        o.signal = dma
        o.sem = None
        o.val = None
        o.gid = len(self.ops)
        deps = {}
        for k in o.reads:
            w = self.last_w.get(k)
            if w is not None:
                deps[w.gid] = w
        for k in o.writes:
            w = self.last_w.get(k)
            if w is not None:
                deps[w.gid] = w
            for r in self.readers.get(k, ()):
                deps[r.gid] = r
        self._finish(o, deps.values())
        for k in o.writes:
            self.last_w[k] = o
            self.readers[k] = []
        for k in o.reads:
            lst = self.readers.setdefault(k, [])
            if not dma:
                lst[:] = [r for r in lst if r.dma or r.eng != eng]
            lst.append(o)
        return o

    def _finish(self, o, deps):
        eng = o.eng
        kn = self.known[eng]
        kd = self.known_dma[eng]
        waits = []
        best = {}
        rset = set(o.reads)
        for d in deps:
            if d.dma:
                if d.gid not in kd:
                    waits.append(d)
                    kd.add(d.gid)
                continue
            if d.eng == eng:
                if eng in ("pe", "sp"):
                    continue
            if kn.get(d.eng, -1) >= d.idx:
                continue
            b = best.get(d.eng)
            if b is None or b.idx < d.idx:
                best[d.eng] = d
        for d in best.values():
            waits.append(d)
            if kn.get(d.eng, -1) < d.idx:
                kn[d.eng] = d.idx
            for e2, i2 in d.clock.items():
                if kn.get(e2, -1) < i2:
                    kn[e2] = i2
        for d in waits:
            d.signal = True
        o.waits = waits
        o.idx = self.cnt[eng]
        self.cnt[eng] += 1
        o.clock = dict(kn)
        self.ops.append(o)
        if o.fn is not None:
            self.last_op[eng] = o
        if o.dma:
            self.live_dma.append(o)

    def barrier(self):
        lasts = [self.last_op[e] for e in self.CE if self.last_op[e] is not None]
        dmas = list(self.live_dma)
        for e in self.CE:
            o = Op()
            o.eng, o.fn, o.reads, o.writes, o.dma = e, None, (), (), False
            o.signal = False
            o.sem = None
            o.val = None
            o.gid = len(self.ops)
            deps = [d for d in lasts if not d.dma and d.eng != e] + dmas
            self._finish(o, deps)
        self.live_dma = []

    def emit(self, sems, dsems):
        sig = {e: 0 for e in self.CE}
        nd = 0
        for o in self.ops:
            E = self.E[o.eng]
            for d in o.waits:
                E.wait_ge(d.sem, d.val)
            if o.fn is None:
                continue
            if o.dma:
                s = dsems[nd % self.NDS]
                rnd = nd // self.NDS
                if rnd > 0:
                    E.wait_ge(s, 16 * rnd)
                ins = o.fn()
                ins.then_inc(s, 16)
                o.sem, o.val = s, 16 * (rnd + 1)
                nd += 1
            else:
                ins = o.fn()
                if o.signal:
                    c = sig[o.eng]
                    o.sem = sems[o.eng][c // self.EPOCH]
                    o.val = c % self.EPOCH + 1
                    ins.then_inc(o.sem, 1)
                    sig[o.eng] = c + 1
        return sig, nd


_TABLES = None


def _rope_tables():
    rows = L // 64
    row = np.repeat(np.arange(rows, dtype=np.float32), 64)
    col = np.tile(np.arange(64, dtype=np.float32), rows)
    half = HD // 2
    inv_freq = (np.float32(10000.0) ** (-np.arange(0, half, 2, dtype=np.float32) / np.float32(half))).astype(np.float32)
    ang_r = (row[:, None] * inv_freq[None, :]).astype(np.float32)
    ang_c = (col[:, None] * inv_freq[None, :]).astype(np.float32)
    cos_r, sin_r = np.cos(ang_r), np.sin(ang_r)
    cos_c, sin_c = np.cos(ang_c), np.sin(ang_c)
    cosT = np.concatenate([cos_r.T, cos_r.T, cos_c.T, cos_c.T], axis=0).astype(np.float32)
    sinT = np.concatenate([sin_r.T, sin_r.T, sin_c.T, sin_c.T], axis=0).astype(np.float32)
    return np.ascontiguousarray(cosT), np.ascontiguousarray(sinT)


def _filter_pos_tables():
    f32 = np.float32
    t = np.linspace(0.0, 1.0, L, dtype=f32)
    bands = 16
    w_ang = (f32(2.0 * math.pi) * np.arange(L, dtype=f32) / f32(L)).astype(f32)
    band_f = np.linspace(1e-4, bands - 1, bands, dtype=f32)
    ang = (w_ang[:, None] * band_f[None, :]).astype(f32)
    z = np.concatenate([t[:, None], np.cos(ang), -np.sin(ang)], axis=-1).astype(f32)
    zT = np.ascontiguousarray(z.T)
    idx = (L - np.arange(L)) % L
    zrT = np.ascontiguousarray(z[idx].T)
    trev = t[idx].copy()
    return zT, zrT, t.copy(), trev


def _fft_tables():
    N = NFFT
    n1 = np.arange(128, dtype=np.float64)
    k1 = np.arange(64, dtype=np.float64)
    th = 2 * np.pi * np.outer(n1, k1 + 0.5) / 128.0
    fpack = np.stack([np.sin(th), np.cos(th), -np.sin(th)], axis=2).reshape(128, 192)
    n2 = np.arange(128, dtype=np.float64)
    k2 = np.arange(128, dtype=np.float64)
    kap = k1[None, :, None] + 128.0 * k2[None, None, :] + 0.5
    thg = 2 * np.pi * n2[:, None, None] * kap / N
    G = np.stack([np.cos(thg), -np.sin(thg)], axis=2)
    thf = 2 * np.pi * np.outer(k2, n2) / 128.0
    finv = np.stack([np.cos(thf), np.sin(thf), -np.sin(thf), -np.cos(thf)], axis=1)
    n1h = np.arange(64, dtype=np.float64)
    phi = 2 * np.pi * (k1[:, None, None] + 0.5) * (128.0 * n1h[None, None, :] + n2[None, :, None]) / N
    M2 = np.concatenate([(2.0 / N) * np.cos(phi), -(2.0 / N) * np.sin(phi)], axis=0)
    return (fpack.astype(bf16_np), G.astype(bf16_np), finv.astype(bf16_np), M2.astype(bf16_np))


def _tables():
    global _TABLES
    if _TABLES is None:
        cosT, sinT = _rope_tables()
        zT, zrT, t, trev = _filter_pos_tables()
        fpack, G, finv, M2 = _fft_tables()
        ii = np.arange(512, dtype=np.float64)
        jj = np.arange(16, dtype=np.float64)
        tbm = np.stack([ii / (L - 1), (L - ii) / (L - 1)]).astype(np.float32)
        jtm = np.stack([512.0 * jj / (L - 1), -512.0 * jj / (L - 1)]).astype(np.float32)
        rot = np.zeros((64, 64), np.float32)
        for m in range(64):
            if (m % 32) < 16:
                rot[m + 16, m] = -1.0
            else:
                rot[m - 16, m] = 1.0
        sel = np.zeros((65, 64), np.float32)
        sel[64, :] = 1.0
        _TABLES = dict(cosT=cosT, sinT=sinT, zT=zT, zrT=zrT, t=t, trev=trev, fpack=fpack, G=G,
                       finv=finv, M2=M2, rot=rot, sel=sel, tb=tbm, jt=jtm,
                       ones=np.ones((128, 128), np.float32),
                       ident=np.eye(128, dtype=np.float32).astype(bf16_np))
    return _TABLES


def build_program(phases=(0, 1, 2, 3, 4), debug=()):
    nc = bass.Bass("TRN2", target_bir_lowering=False)
    S = Sched(nc)

    def din(name, shape, dt=F32):
        return nc.dram_tensor(name, list(shape), dt, kind="ExternalInput").ap()

    def dscr(name, shape, dt):
        return nc.dram_tensor(name, list(shape), dt, kind="Internal").ap()

    xT = din("xT", [D, L])
    xTq = din("xTq", [D, NTOK])
    w_in = din("w_in", [D, 2304])
    w_out = din("w_out", [D, D])
    w_mi = din("w_mi", [D, DFF])
    w_mo = din("w_mo", [DFF, D])
    g1 = din("g1", [128, 8]); g2 = din("g2", [128, 8]); gf = din("gf", [128, 8])
    gq = din("gq", [64, 1]); gk = din("gk", [64, 1])
    ga = din("ga", [64, 8]); gh = din("gh", [128, 4])
    cw = din("cw", [128, 12, 3]); cb = din("cb", [128, 12])
    fw1 = din("fw1", [33, 64]); fw2 = din("fw2", [64, 64]); fw3 = din("fw3", [64, 64])
    fw4 = din("fw4", [64, 2048])
    fb = din("fb", [64, 3]); ffreq = din("ffreq", [64, 1])
    fdel = din("fdel", [128, 16])
    skp = din("skp", [128, 8])
    qmask = din("qmask", [128, 4])
    c_cos = din("c_cos", [64, L]); c_sin = din("c_sin", [64, L])
    c_cosq = din("c_cosq", [64, NTOK]); c_sinq = din("c_sinq", [64, NTOK])
    c_zT = din("c_zT", [2, 33, L])
    c_t = din("c_t", [2, L])
    c_tb = din("c_tb", [2, 512])
    c_jt = din("c_jt", [2, 16])
    c_fpack = din("c_fpack", [128, 192], BF16)
    c_G = din("c_G", [128, 64, 2, 128], BF16)
    c_finv = din("c_finv", [128, 4, 128], BF16)
    c_M2 = din("c_M2", [128, 128, 64], BF16)
    c_M2q = din("c_M2q", [128, 128, 16], BF16)
    c_rot = din("c_rot", [64, 64]); c_sel = din("c_sel", [65, 64]); c_ones = din("c_ones", [128, 128])
    c_ident = din("c_ident", [128, 128], BF16)
    outT = nc.dram_tensor("outT", [D, NTOK], F32, kind="ExternalOutput").ap()

    U_d = dscr("U_d", [1536, L], BF16)
    H_d = dscr("H_d", [8, 128, 64, 4, 128], BF16)
    taps_d = dscr("taps_d", [128, NFFT], BF16)
    zcm_d = dscr("zcm_d", [128, L], BF16)
    att_d = dscr("att_d", [512, NTOK], BF16)
    hy_d = dscr("hy_d", [512, NTOK], BF16)
    Wmi_d = dscr("Wmi_d", [D, DFF], BF16)
    Wmo_d = dscr("Wmo_d", [8, 128, 32, 128], BF16)

    from contextlib import ExitStack
    es = ExitStack()

    _names = {}

    def sb(name, shape, dt, stack=None):
        n_ = _names.get(name, 0)
        _names[name] = n_ + 1
        nm = name if n_ == 0 else f"{name}_r{n_}"
        return (stack or es).enter_context(nc.sbuf_tensor(nm, list(shape), dt))

    def dma(out, in_, reads, writes, eng="sp"):
        return S.op(eng, lambda: S.E[eng].dma_start(out=out, in_=in_), reads, writes, dma=True)

    def mm(out, lhsT, rhs, start, stop, reads, writes):
        return S.op("pe", lambda: nc.tensor.matmul(out, lhsT, rhs, start=start, stop=stop), reads, writes)

    rr = {"i": 0}

    def alt(engs=("dve", "pool")):
        rr["i"] += 1
        return engs[rr["i"] % len(engs)]

    def V(eng):
        return S.E[eng]

    with es:
        PSW = [es.enter_context(nc.psum_tensor(f"psw{i}", [128, 1024], F32)) for i in range(3)]
        PS = [PSW[i // 2][:, (i % 2) * 512:(i % 2 + 1) * 512] for i in range(6)]
        PSB = [es.enter_context(nc.psum_tensor(f"psb{i}", [128, 1024], BF16)) for i in range(2)]
        PSX = [PSB[i].bitcast(F32)[:, :] for i in range(2)]

        ones = sb("ones", [128, 128], F32)
        ident = sb("ident", [128, 128], BF16)
        rot = sb("rot", [64, 64], F32)
        sel = sb("sel", [65, 64], F32)
        epsT = sb("epsT", [128, 1], F32)
        onesbf = sb("onesbf", [128, 128], BF16)
        g1s = sb("g1s", [128, 8], F32); g2s = sb("g2s", [128, 8], F32); gfs = sb("gfs", [128, 8], F32)
        gqs = sb("gqs", [64, 1], F32); gks = sb("gks", [64, 1], F32)
        gas = sb("gas", [64, 8], F32); ghs = sb("ghs", [128, 4], F32)
        cws = sb("cws", [128, 12, 3], F32); cbs = sb("cbs", [128, 12], F32)
        skps = sb("skps", [128, 8], F32); qms = sb("qms", [128, 4], F32)
        for t_, d_ in ((ones, c_ones), (ident, c_ident), (rot, c_rot), (sel, c_sel), (g1s, g1), (g2s, g2),
                       (gfs, gf), (gqs, gq), (gks, gk), (gas, ga), (ghs, gh), (cws, cw), (cbs, cb),
                       (skps, skp), (qms, qmask)):
            dma(t_[:], d_, [], [t_.name if hasattr(t_, "name") else id(t_)])
        KEY = lambda t_: t_.name if hasattr(t_, "name") else id(t_)
        S.op("dve", lambda: nc.vector.memset(epsT[:], EPS), [], ["epsT"])
        S.op("pool", lambda: nc.gpsimd.memset(onesbf[:], 1.0), [], ["onesbf"])

        FC = {}
        SNAP = {}

        def snap(name, src_ap, key, ncols):
            if name not in debug or name in SNAP:
                return
            SNAP[name] = nc.dram_tensor("dbg_" + name, [128, ncols], BF16, kind="ExternalOutput").ap()
            dma(SNAP[name], src_ap, [key], ["dbg_" + name])


        def load_fft_consts(stack):
            FC["fpack"] = sb("fpack", [128, 192], BF16, stack)
            FC["finv"] = sb("finv", [128, 4, 128], BF16, stack)
            FC["Gq"] = [sb(f"Gq{i}", [128, 8, 2, 128], BF16, stack) for i in range(2)]
            dma(FC["fpack"][:], c_fpack, [], ["fpack"])
            dma(FC["finv"][:], c_finv, [], ["finv"])

        def rsqrt(out, in_, scale, reads, writes, np_=128):
            S.op("act", lambda: nc.scalar.activation(out, in_, AF.Sqrt, bias=epsT[0:np_, :], scale=scale),
                 list(reads) + ["epsT"], writes)
            S.op("dve", lambda: nc.vector.reciprocal(out, out), writes, writes)

        def fft_forward(ztm, K, AT, cb_batch, zkey="ztm", akey="AT", pre_batch=None, cb_late=None):
            fpack = FC["fpack"]
            for c2 in range(64):
                ps = PS[c2 % 4]
                pk = f"ps{c2 % 4}"
                for cc in range(2):
                    c = c2 * 2 + cc
                    mm(ps[:, cc * 192:(cc + 1) * 192], ztm[0:K, c, :], fpack[0:K, :], True, True,
                       [(zkey, c // 32), "fpack"], [pk])
                e = alt(("act", "dve"))
                src = ps[:, 0:384].rearrange("p (cc j) -> p j cc", cc=2)
                dst = AT[:, :, c2 * 2:c2 * 2 + 2]
                if e == "act":
                    S.op("act", lambda dst=dst, src=src: nc.scalar.copy(dst, src), [pk], [akey])
                else:
                    S.op("dve", lambda dst=dst, src=src: nc.vector.tensor_copy(dst, src), [pk], [akey])

            def load_g(q8):
                dma(FC["Gq"][q8 % 2][:], c_G[:, q8 * 8:(q8 + 1) * 8], [], [f"Gq{q8 % 2}"])

            load_g(0)
            for kb in range(32):
                q4 = kb // 4
                Gq = FC["Gq"][q4 % 2]
                gkey = f"Gq{q4 % 2}"
                if kb % 4 == 0 and q4 + 1 < 8:
                    load_g(q4 + 1)
                if pre_batch is not None:
                    pre_batch(kb)
                ps = PS[2 + kb % 2]
                pk = f"ps{2 + kb % 2}"
                for kk in range(2):
                    k1 = kb * 2 + kk
                    kl = k1 % 8
                    zp = ps[:, kk * 256:(kk + 1) * 256].rearrange("p (r c) -> p r c", r=2)
                    mm(zp, Gq[:, kl, 0, :], AT[:, 3 * k1 + 1:3 * k1 + 3, :], True, False, [akey, gkey], [pk])
                    mm(zp, Gq[:, kl, 1, :], AT[:, 3 * k1:3 * k1 + 2, :], False, True, [akey, gkey], [pk])
                cb_batch(kb, ps, pk)
                if cb_late is not None and kb >= 1:
                    cb_late(kb - 1)
            if cb_late is not None:
                cb_late(31)

        cast_jobs = []
        for cbk in range(8):
            cast_jobs.append((w_mi.rearrange("(k p) n -> p k n", p=128)[:, :, cbk * 512:(cbk + 1) * 512],
                              Wmi_d.rearrange("(k p) n -> p k n", p=128)[:, :, cbk * 512:(cbk + 1) * 512], None))
        for kr in range(4):
            for cbk in range(2):
                cast_jobs.append((w_mo.rearrange("(k p) n -> p k n", p=128)[:, kr * 8:(kr + 1) * 8, cbk * 512:(cbk + 1) * 512],
                                  None, (kr, cbk)))

        def emit_cast(n, wst, wbf):
            src, dst, tl = cast_jobs[n]
            s_ = n % 2
            dma(wst[s_][:], src, [], [f"wst{s_}"])
            S.op("dve", lambda s_=s_: nc.vector.tensor_copy(wbf[s_][:], wst[s_][:]), [f"wst{s_}"], [f"wbf{s_}"])
            if tl is None:
                dma(dst, wbf[s_][:], [f"wbf{s_}"], ["Wmid"])
            else:
                kr_, cbk_ = tl
                for c_ in range(4):
                    dma(Wmo_d[cbk_ * 4 + c_, :, kr_ * 8:(kr_ + 1) * 8, :], wbf[s_][:, :, c_ * 128:(c_ + 1) * 128], [f"wbf{s_}"], ["Wmod"])

        def phase0():
          with ExitStack() as p0:
            load_fft_consts(p0)

            w1s = sb("w1s", [33, 64], F32, p0); w2s = sb("w2s", [64, 64], F32, p0); w3s = sb("w3s", [64, 64], F32, p0)
            w4b = sb("w4b", [64, 2048], BF16, p0)
            fbs = sb("fbs", [64, 3], F32, p0); frs = sb("frs", [64, 1], F32, p0); ffb = sb("ffb", [64, 3], F32, p0)
            dls = sb("dls", [128, 16], F32, p0); ndl = sb("ndl", [128, 16], F32, p0)
            negpi = sb("negpi", [128, 1], F32, p0)
            H3 = sb("H3", [64, 2, L], BF16, p0)
            p0m = ExitStack()
            w4s = sb("w4s", [64, 2048], F32, p0m)
            for t_, d_, k_ in ((w1s, fw1, "w1s"), (w2s, fw2, "w2s"), (w3s, fw3, "w3s"), (w4s, fw4, "w4s"),
                               (fbs, fb, "fbs"), (frs, ffreq, "frs"), (dls, fdel, "dls")):
                dma(t_[:], d_, [], [k_])
            S.op("dve", lambda: nc.vector.tensor_copy(w4b[:], w4s[:]), ["w4s"], ["w4b"])
            S.op("dve", lambda: nc.vector.tensor_scalar(ffb[:], fbs[:], frs[:, 0:1], None, ALU.mult), ["fbs", "frs"], ["ffb"])
            S.op("dve", lambda: nc.vector.tensor_scalar(ndl[:], dls[:], -1.0, None, ALU.mult), ["dls"], ["ndl"])
            S.op("dve", lambda: nc.vector.tensor_tensor(ndl[:], ndl[:], dls[:], ALU.min), ["dls", "ndl"], ["ndl"])
            S.op("dve", lambda: nc.vector.memset(negpi[:], 0.0), [], ["negpi"])
            NCH = 8
            zc = [sb(f"zc{i}", [33, 512], F32, p0m) for i in range(NCH)]
            ybL = [sb(f"yb{i}", [64, 512], F32, p0m) for i in range(NCH)]
            ttL = [sb(f"tt{i}", [64, 512], F32, p0m) for i in range(NCH)]
            nrL = [sb(f"nr{i}", [64, 512], F32, p0m) for i in range(NCH)]
            fhL = [sb(f"fh{i}", [64, 512], F32, p0m) for i in range(NCH)]
            frs2 = sb("frs2", [64, 1], F32, p0m)
            ffb2 = sb("ffb2", [64, 3], F32, p0m)
            S.op("dve", lambda: nc.vector.tensor_scalar(frs2[:], frs[:], 1.0 / TWO_PI, None, ALU.mult), ["frs"], ["frs2"])
            S.op("dve", lambda: nc.vector.tensor_scalar(ffb2[:], fbs[:], frs2[:, 0:1], None, ALU.mult), ["fbs", "frs2"], ["ffb2"])
            BANKS = [(PS[i], f"ps{i}") for i in range(6)] + [(PSX[0], "psb0"), (PSX[1], "psb1")]
            wL = [(w1s, "w1s"), (w2s, "w2s"), (w3s, "w3s")]

            def mlp_layer(c_, li, var, j):
                psm, pmk = BANKS[c_]
                src, srck = (zc[c_], f"zc{c_}") if li == 0 else (fhL[c_], f"fh{c_}")
                w_, wk_ = wL[li]
                mm(psm[0:64, :], w_[:], src[:], True, True, [srck, wk_], [pmk])
                S.op("act", lambda: nc.scalar.activation(ybL[c_][:], psm[0:64, :], AF.Identity, bias=ffb2[:, li:li + 1], scale=frs2[:, 0:1]),
                     [pmk, "frs2", "ffb2"], [f"yb{c_}"])
                S.op("dve", lambda: nc.vector.tensor_scalar(ttL[c_][:], ybL[c_][:], MAGIC, None, ALU.add), [f"yb{c_}"], [f"tt{c_}"])
                S.op("dve", lambda: nc.vector.scalar_tensor_tensor(out=nrL[c_][:], in0=ttL[c_][:], scalar=MAGIC, in1=ybL[c_][:],
                                                                  op0=ALU.subtract, op1=ALU.subtract), [f"tt{c_}", f"yb{c_}"], [f"nr{c_}"])
                if li < 2:
                    dst, dk = fhL[c_][:], f"fh{c_}"
                else:
                    dst, dk = H3[:, var, j * 512:(j + 1) * 512], ("H3", var)
                S.op("act", lambda: nc.scalar.activation(dst, nrL[c_][:], AF.Sin, bias=negpi[0:64, :], scale=-TWO_PI), [f"nr{c_}", "negpi"], [dk])

            for jg in range(4):
                chains = [(c_, c_ % 2, jg * 4 + c_ // 2) for c_ in range(NCH)]
                for c_, var, j in chains:
                    dma(zc[c_][:], c_zT[var, :, j * 512:(j + 1) * 512], [], [f"zc{c_}"])
                for li in range(3):
                    for c_, var, j in chains:
                        mlp_layer(c_, li, var, j)

            S.barrier()
            p0m.close()
            tapsUL = [sb(f"tapsU{i}", [128, NFFT], BF16, p0) for i in range(2)]
            ztmf = sb("ztmf", [128, 128, 128], BF16, p0)
            ATf = sb("ATf", [128, 192, 128], BF16, p0)
            tb = sb("tb", [128, 2, 512], F32, p0)
            jt = sb("jt", [128, 2, 16], F32, p0)
            for hf in range(2):
                dma(tb[:, hf, :], c_tb[hf:hf + 1, :].broadcast_to([128, 512]), [], ["tb"])
                dma(jt[:, hf, :], c_jt[hf:hf + 1, :].broadcast_to([128, 16]), [], ["jt"])
            wbase1 = sb("wbase", [128, 512], F32, p0)
            wbase = [wbase1, wbase1]
            wcj = [sb(f"wcj{i}", [128, 16], F32, p0) for i in range(2)]
            tpf = [sb(f"tpf{i}", [128, 512], F32, p0) for i in range(2)]
            psum_ = sb("psum_", [128, 32], F32, p0)
            nrm = sb("nrm", [128, 2], F32, p0)
            Hst = [sb(f"Hst{i}", [128, 2, 4, 128], BF16, p0) for i in range(2)]

            def taps_chunk(go, cidx):
                g, o = go // 2, go % 2
                half, j = cidx // 16, cidx % 16
                tU = tapsUL[go % 2]
                tk = f"tapsU{go % 2}"
                col0 = o * 1024 + half * 512 + g * 128
                ct = col0 // 128
                if j == 0:
                    S.op("act", lambda: nc.scalar.activation(wbase[half][:], tb[:, half, :], AF.Exp, scale=ndl[:, ct:ct + 1]),
                         ["tb", "ndl"], ["wbase"])
                    S.op("act", lambda: nc.scalar.activation(wcj[half][:], jt[:, half, :], AF.Exp, scale=ndl[:, ct:ct + 1]),
                         ["jt", "ndl"], [f"wcj{half}"])
                s_ = j % 2
                pst = PS[4 + s_]
                ptk = f"ps{4 + s_}"
                mm(pst[:, :], w4b[:, col0:col0 + 128], H3[:, half, j * 512:(j + 1) * 512], True, True, [("H3", half), "w4b"], [ptk])
                S.op("dve", lambda: nc.vector.scalar_tensor_tensor(out=tpf[s_][:], in0=pst[:, :], scalar=wcj[half][:, j:j + 1], in1=wbase[half][:],
                                                                  op0=ALU.mult, op1=ALU.mult), [ptk, f"wcj{half}", "wbase"], [f"tpf{s_}"])
                if half == 1 and j == 0:
                    S.op("dve", lambda: nc.vector.memset(tpf[s_][:, 0:1], 0.0), [f"tpf{s_}"], [f"tpf{s_}"])
                S.op("dve", lambda: nc.vector.tensor_reduce(psum_[:, cidx:cidx + 1], tpf[s_][:], mybir.AxisListType.X, ALU.add,
                                                            apply_absolute_value=True), [f"tpf{s_}"], ["psum_"])
                S.op("act", lambda: nc.scalar.copy(tU[:, half * L + j * 512: half * L + (j + 1) * 512], tpf[s_][:]), [f"tpf{s_}"], [tk])

            def taps_finish(go):
                tU = tapsUL[go % 2]
                tk = f"tapsU{go % 2}"
                S.op("dve", lambda: nc.vector.tensor_reduce(nrm[:, 0:1], psum_[:], mybir.AxisListType.X, ALU.add), ["psum_"], ["nrm"])
                S.op("dve", lambda: nc.vector.reciprocal(nrm[:, 0:1], nrm[:, 0:1]), ["nrm"], ["nrm"])
                S.op("dve", lambda: nc.vector.tensor_scalar(nrm[:, 1:2], nrm[:, 0:1], -1.0, None, ALU.mult), ["nrm"], ["nrm"])
                S.op("dve", lambda: nc.vector.tensor_scalar(tU[:, 0:L], tU[:, 0:L], nrm[:, 0:1], None, ALU.mult), [tk, "nrm"], [tk])
                S.op("act", lambda: nc.scalar.mul(tU[:, L:NFFT], tU[:, L:NFFT], nrm[:, 1:2]), [tk, "nrm"], [tk])
                tv = taps_d.rearrange("c (n1 n2) -> n1 c n2", n2=128)
                for q4 in range(4):
                    dma(taps_d[q4 * 32:(q4 + 1) * 32, :], tU[q4 * 32:(q4 + 1) * 32, :], [tk], [("taps_d", q4)])
                for q4 in range(4):
                    dma(ztmf[:, q4 * 32:(q4 + 1) * 32, :], tv[:, q4 * 32:(q4 + 1) * 32, :], [("taps_d", q4)], [("ztm", q4)])

            for cidx in range(32):
                taps_chunk(0, cidx)
            taps_finish(0)
            for go in range(8):
                def cb_filter(kb, ps, pk, go=go):
                    s_ = kb % 2
                    zv = ps[:, :].rearrange("p (k r c) -> p k r c", k=2, r=2)
                    S.op("act", lambda: nc.scalar.copy(Hst[s_][:, :, 0:2, :], zv), [pk], [f"Hst{s_}"])
                    S.op("dve", lambda: nc.vector.tensor_copy(Hst[s_][:, :, 2, :], zv[:, :, 1, :]), [pk], [f"Hst{s_}"])
                    S.op("dve", lambda: nc.vector.tensor_copy(Hst[s_][:, :, 3, :], zv[:, :, 0, :]), [pk], [f"Hst{s_}"])
                    dma(H_d[go, :, kb * 2:kb * 2 + 2], Hst[s_][:], [f"Hst{s_}"], [("H_d", go)])

                def pre_filter(kb, go=go):
                    if go + 1 < 8:
                        if kb < 16:
                            taps_chunk(go + 1, 2 * kb)
                            taps_chunk(go + 1, 2 * kb + 1)
                        elif kb == 16:
                            taps_finish(go + 1)

                fft_forward(ztmf, 128, ATf, cb_filter, pre_batch=pre_filter)
            S.barrier()
        def rms_apply(src, gvec, dst, scale, nk, tag, sqb, rs, np_=128, ss_ps=0, sqk="sqb", rsk="rs"):
            rms_a(src, nk, tag, sqb, np_, sqk)
            rms_b(src, gvec, dst, scale, nk, tag, sqb, rs, np_, ss_ps, sqk, rsk)

        def rms_a(src, nk, tag, sqb, np_=128, sqk="sqb"):
            S.op("act", lambda: nc.scalar.activation(sqb[0:np_, 0:nk, :], src, AF.Square), [tag + "_src"], [sqk])

        def rms_b(src, gvec, dst, scale, nk, tag, sqb, rs, np_=128, ss_ps=0, sqk="sqb", rsk="rs"):
            for k in range(nk):
                mm(PS[ss_ps][0:np_, :], onesbf[0:np_, 0:np_], sqb[0:np_, k, :], k == 0, k == nk - 1, [sqk, "onesbf"], [f"ps{ss_ps}"])
            S.op("act", lambda: nc.scalar.activation(rs[0:np_, :], PS[ss_ps][0:np_, :], AF.Sqrt, bias=epsT[0:np_, :], scale=scale),
                 [f"ps{ss_ps}", "epsT"], [rsk])
            S.op("dve", lambda: nc.vector.reciprocal(rs[0:np_, :], rs[0:np_, :]), [rsk], [rsk])
            for k in range(nk):
                S.op("dve", lambda k=k: nc.vector.scalar_tensor_tensor(out=dst[:, k, :], in0=src[:, k, :], scalar=gvec[:, k:k + 1],
                                                                      in1=rs[0:np_, :], op0=ALU.mult, op1=ALU.mult),
                     [tag + "_src", rsk, "gvecs"], [tag + "_dst"])

        def phase12():
          with ExitStack() as p12:
            KT = sb("KT", [128, 2, L], BF16, p12)
            Vx = sb("Vx", [128, 64, 2, 64], BF16, p12)
            QT = sb("QT", [128, 4, NTOK], BF16, p12)
            onesb = sb("onesb", [128, 64], BF16, p12)
            S.op("pool", lambda: nc.gpsimd.memset(onesb[:], 1.0), [], ["onesb"])
            with ExitStack() as p1:
                Wb = sb("Wb", [128, 8, 2304], BF16, p1)
                with ExitStack() as p1c:
                    wst = [sb(f"wsi{i}", [128, 8, 384], F32, p1c) for i in range(2)]
                    wv = w_in.rearrange("(k p) n -> p k n", p=128)
                    for bk in range(6):
                        s_ = bk % 2
                        dma(wst[s_][:], wv[:, :, bk * 384:(bk + 1) * 384], [], [f"wst{s_}"])
                        if bk % 2 == 0:
                            S.op("dve", lambda s_=s_, bk=bk: nc.vector.tensor_copy(Wb[:, :, bk * 384:(bk + 1) * 384], wst[s_][:]), [f"wst{s_}"], ["Wb"])
                        else:
                            S.op("act", lambda s_=s_, bk=bk: nc.scalar.copy(Wb[:, :, bk * 384:(bk + 1) * 384], wst[s_][:]), [f"wst{s_}"], ["Wb"])
                    S.barrier()
                xin = [sb(f"xin{i}", [128, 8, 512], F32, p1) for i in range(2)]
                sqbL = [sb(f"sqb{i}", [128, 8, 512], BF16, p1) for i in range(2)]
                rsL = [sb(f"rs{i}", [128, 512], F32, p1) for i in range(2)]
                aTL = [sb(f"aT{i}", [128, 8, 512], BF16, p1) for i in range(2)]
                ust = sb("ust", [128, 12, 512], BF16, p1)
                hsq = sb("hsq", [64, 512], BF16, p1); hrs = sb("hrs", [64, 512], F32, p1); hkn = sb("hkn", [64, 512], F32, p1)
                ht1 = sb("ht1", [64, 512], F32, p1); ht2 = sb("ht2", [64, 512], F32, p1)
                csb = [sb(f"csb{i}", [64, 2, 512], F32, p1) for i in range(2)]
                qstg = [sb(f"qstg{i}", [64, 512], BF16, p1) for i in range(2)]

                def hp1(srcb, gvec):
                    src = PS[srcb][0:64, :]
                    sk_ = f"ps{srcb}"
                    S.op("act", lambda: nc.scalar.activation(hsq[:], src, AF.Square), [sk_], ["hsq"])
                    mm(PS[3][0:64, :], onesbf[0:64, 0:64], hsq[:], True, True, ["hsq", "onesbf"], ["ps3"])
                    S.op("act", lambda: nc.scalar.activation(hrs[:], PS[3][0:64, :], AF.Sqrt, bias=epsT[0:64, :], scale=1.0 / 64.0),
                         ["ps3", "epsT"], ["hrs"])
                    S.op("dve", lambda: nc.vector.reciprocal(hrs[:], hrs[:]), ["hrs"], ["hrs"])
                    S.op("dve", lambda: nc.vector.scalar_tensor_tensor(out=hkn[:], in0=src, scalar=gvec[:, 0:1], in1=hrs[:],
                                                                      op0=ALU.mult, op1=ALU.mult), [sk_, "hrs", "gvecs"], ["hkn"])

                def hp2(cs, cskey, dst, dstkey):
                    mm(PS[3][0:64, :], rot[:], hkn[:], True, True, ["hkn", "rot"], ["ps3"])
                    S.op("pool", lambda: nc.gpsimd.tensor_tensor(ht1[:], hkn[:], cs[:, 0, :], ALU.mult), ["hkn", cskey], ["ht1"])
                    S.op("dve", lambda: nc.vector.tensor_tensor(ht2[:], PS[3][0:64, :], cs[:, 1, :], ALU.mult), ["ps3", cskey], ["ht2"])
                    S.op("pool", lambda: nc.gpsimd.tensor_tensor(dst, ht1[:], ht2[:], ALU.add), ["ht1", "ht2"], [dstkey])

                xv = xT.rearrange("(k p) t -> p k t", p=128)
                xqv = xTq.rearrange("(k p) t -> p k t", p=128)
                Uv = U_d.rearrange("(ct p) t -> p ct t", p=128)
                work = [("seq", j) for j in range(16)] + [("own", j) for j in range(4)]

                def norm_a(w):
                    kind, j = work[w]
                    s_ = w % 2
                    tsl = slice(j * 512, (j + 1) * 512)
                    srcv, cc, ss_ = (xv, c_cos, c_sin) if kind == "seq" else (xqv, c_cosq, c_sinq)
                    dma(xin[s_][:], srcv[:, :, tsl], [], [f"xin{s_}", f"n1{s_}_src"])
                    dma(csb[s_][:, 0, :], cc[:, tsl], [], [f"csb{s_}"])
                    dma(csb[s_][:, 1, :], ss_[:, tsl], [], [f"csb{s_}"])
                    rms_a(xin[s_][:], 8, f"n1{s_}", sqbL[s_], 128, f"sqb{s_}")

                def norm_b(w):
                    s_ = w % 2
                    rms_b(xin[s_][:], g1s, aTL[s_], 1.0 / D, 8, f"n1{s_}", sqbL[s_], rsL[s_], 128, 0, f"sqb{s_}", f"rs{s_}")

                def emit_proj(w, mid_hook):
                    kind, j = work[w]
                    s_ = w % 2
                    aT = aTL[s_]
                    ak = f"n1{s_}_dst"
                    cs, csk = csb[s_], f"csb{s_}"
                    tsl = slice(j * 512, (j + 1) * 512)
                    if kind == "seq":
                        for kvh in range(2):
                            for k in range(8):
                                mm(PS[1 + kvh][0:64, :], Wb[:, k, 512 + kvh * 64:512 + (kvh + 1) * 64], aT[:, k, :], k == 0, k == 7,
                                   ["Wb", ak], [f"ps{1 + kvh}"])
                        for tt in range(4):
                            for k in range(8):
                                mm(PS[4][:, tt * 128:(tt + 1) * 128], aT[:, k, tt * 128:(tt + 1) * 128], Wb[:, k, 640:768], k == 0, k == 7,
                                   ["Wb", ak], ["ps4"])
                        S.op("act", lambda j=j: nc.scalar.copy(Vx[:, j * 4:(j + 1) * 4, :, :],
                                                             PS[4][:, :].rearrange("p (t h d) -> p t h d", t=4, h=2)), ["ps4"], ["Vx"])
                        for ct in range(12):
                            pu, puk = (PS[5], "ps5") if ct % 2 == 0 else (PSX[0], "psb0")
                            for k in range(8):
                                mm(pu[:, :], Wb[:, k, 768 + ct * 128:768 + (ct + 1) * 128], aT[:, k, :], k == 0, k == 7, ["Wb", ak], [puk])
                            if ct % 2 == 0:
                                S.op("act", lambda ct=ct, pu=pu: nc.scalar.copy(ust[:, ct, :], pu[:, :]), [puk], ["ust"])
                            else:
                                S.op("dve", lambda ct=ct, pu=pu: nc.vector.tensor_copy(ust[:, ct, :], pu[:, :]), [puk], ["ust"])
                            if ct == 1:
                                hp1(1, gks)
                            if ct == 3:
                                hp2(cs, csk, KT[0:64, 0, tsl], "KT")
                                hp1(2, gks)
                            if ct == 5:
                                hp2(cs, csk, KT[0:64, 1, tsl], "KT")
                            if ct == 7:
                                mid_hook()
                        dma(Uv[:, :, tsl], ust[:], ["ust"], ["U_d"])
                    else:
                        def qdst(h):
                            if h % 2 == 0:
                                return QT[0:64, h // 2, tsl], "QT", None
                            qs_ = (h // 2) % 2
                            return qstg[qs_][:], f"qstg{qs_}", qs_

                        def qproj(h):
                            hb = 1 + h % 2
                            for k in range(8):
                                mm(PS[hb][0:64, :], Wb[:, k, h * 64:(h + 1) * 64], aT[:, k, :], k == 0, k == 7, ["Wb", ak], [f"ps{hb}"])

                        qproj(0)
                        for h in range(8):
                            hp1(1 + h % 2, gqs)
                            if h + 1 < 8:
                                qproj(h + 1)
                            if h == 3:
                                mid_hook()
                            d_, dk_, qs_ = qdst(h)
                            hp2(cs, csk, d_, dk_)
                            if qs_ is not None:
                                dma(QT[64:128, h // 2, tsl], qstg[qs_][:], [f"qstg{qs_}"], ["QT2"])

                norm_a(0)
                norm_b(0)
                for w in range(len(work)):
                    if w + 1 < len(work):
                        norm_a(w + 1)
                        emit_proj(w, lambda w=w: norm_b(w + 1))
                    else:
                        emit_proj(w, lambda: None)
                dma(KT[64:128, :, :], KT[0:64, :, :], ["KT"], ["KT2"])
                S.barrier()
            if 2 in phases:
              with ExitStack() as p2:
                pb = [sb(f"pb{i}", [128, 1024], BF16, p2) for i in range(2)]
                cwst = [sb(f"wst{i}", [128, 8, 512], F32, p2) for i in range(2)]
                cwbf = [sb(f"wbf{i}", [128, 8, 512], BF16, p2) for i in range(2)]
                rden = [sb(f"rden{i}", [128, 512], F32, p2) for i in range(2)]
                ast = [sb(f"ast{i}", [128, 512], BF16, p2) for i in range(2)]
                n_ = 0
                for qc in range(4):
                    qsl = slice(qc * 512, (qc + 1) * 512)
                    for hp in range(4):
                        kvh = hp // 2
                        (OB, obk), (DB, dbk) = ((PS[4], "ps4"), (PS[5], "ps5")) if n_ % 2 == 0 else ((PSX[0], "psb0"), (PSX[1], "psb1"))
                        emit_cast(qc * 4 + hp, cwst, cwbf)
                        for step in range(64 + 1):
                            if step < 64:
                                kt = step
                                sl = kt % 2
                                ksl = slice(kt * 128, (kt + 1) * 128)
                                mm(PS[2 * sl][:, :], KT[0:64, kvh, ksl], QT[0:64, hp, qsl], True, True, ["KT", "QT"], [f"ps{2 * sl}"])
                                mm(PS[2 * sl + 1][:, :], KT[64:128, kvh, ksl], QT[64:128, hp, qsl], True, True, ["KT2", "QT2"], [f"ps{2 * sl + 1}"])
                                S.op("act", lambda sl=sl: nc.scalar.activation(pb[sl][:], PSW[sl][:, :], AF.Exp, scale=0.125),
                                     [f"ps{2 * sl}", f"ps{2 * sl + 1}"], [f"pb{sl}"])
                            kt = step - 1
                            if kt >= 0:
                                sl = kt % 2
                                st_, sp_ = kt == 0, kt == 63
                                pA = pb[sl][:, 0:512]
                                pB = pb[sl][:, 512:1024]
                                vv = Vx[:, kt, kvh, :]
                                S.op("pe", lambda vv=vv, pA=pA, st_=st_, sp_=sp_, OB=OB: nc.tensor.matmul(OB[0:64, :], vv, pA, start=st_, stop=sp_, tile_position=(0, 0)),
                                     ["Vx", f"pb{sl}"], [obk])
                                S.op("pe", lambda vv=vv, pB=pB, st_=st_, sp_=sp_, OB=OB: nc.tensor.matmul(OB[64:128, :], vv, pB, start=st_, stop=sp_, tile_position=(0, 64)),
                                     ["Vx", f"pb{sl}"], [obk])
                                S.op("pe", lambda pA=pA, st_=st_, sp_=sp_, DB=DB: nc.tensor.matmul(DB[0:64, :], onesb[:], pA, start=st_, stop=sp_, tile_position=(0, 0)),
                                     ["onesb", f"pb{sl}"], [dbk])
                                S.op("pe", lambda pB=pB, st_=st_, sp_=sp_, DB=DB: nc.tensor.matmul(DB[64:128, :], onesb[:], pB, start=st_, stop=sp_, tile_position=(0, 64)),
                                     ["onesb", f"pb{sl}"], [dbk])
                        a_ = n_ % 2
                        S.op("dve", lambda DB=DB, a_=a_: nc.vector.reciprocal(rden[a_][:], DB[:, :]), [dbk], [f"rden{a_}"])
                        S.op("dve", lambda a_=a_, OB=OB: nc.vector.tensor_tensor(ast[a_][:], OB[:, :], rden[a_][:], ALU.mult), [obk, f"rden{a_}"], [f"ast{a_}"])
                        dma(att_d[hp * 128:(hp + 1) * 128, qsl], ast[a_][:], [f"ast{a_}"], ["att_d"])
                        n_ += 1
                S.barrier()

        def phase3():
          with ExitStack() as p3:
            load_fft_consts(p3)
            finv = FC["finv"]
            M2s = sb("M2s", [128, 128, 64], BF16, p3)
            M2qs = sb("M2qs", [128, 128, 16], BF16, p3)
            dma(M2s[:], c_M2, [], ["M2s"])
            dma(M2qs[:], c_M2q, [], ["M2qs"])
            RA = sb("RA", [128, 192 * 128 + 8], BF16, p3)
            RB = sb("RB", [128, 128 * 128], BF16, p3)
            RC = sb("RC", [128, 128 * 128], BF16, p3)
            vt = sb("vt", [128, L], BF16, p3)
            x1t = sb("x1t", [128, L], BF16, p3)
            x2q = sb("x2q", [128, NTOK], BF16, p3)
            zzq = sb("zzq", [128, NTOK], BF16, p3)
            hyst = sb("hyst", [128, NTOK], BF16, p3)
            Hc = [sb(f"Hc{i}", [128, 2, 4, 128], BF16, p3) for i in range(4)]
            Pb = [sb(f"Pb{i}", [128, 2, 2, 128], BF16, p3) for i in range(2)]
            Qb = [sb(f"Qb{i}", [128, 2, 2, 128], BF16, p3) for i in range(2)]
            gt = [sb(f"gt{i}", [128, 512], F32, p3) for i in range(2)]
            whi = sb("whi", [128, 9], BF16, p3)
            whf = sb("whf", [128, 9], F32, p3)
            wlf = sb("wlf", [128, 9], F32, p3)
            sct = gt
            uraw = RA[:, 0:3 * 8194].rearrange("p (i t) -> p i t", i=3)
            AT = RA[:, 0:192 * 128].rearrange("p (j c) -> p j c", c=128)
            ztm = RB[:, :].rearrange("p (c n) -> p c n", n=128)
            Eb = RB[:, :].rearrange("p (n c) -> p n c", c=128)
            x2t = RC[:, 0:L]
            ET = RC[:, :].rearrange("p (c k) -> p c k", k=128)
            RAK, RBK, RCK = "RA", "RB", "RC"

            def masked_quarter(dst, src, dkey, skey):
                S.op("dve", lambda: nc.vector.tensor_scalar(dst[:], src[:, 0:NTOK], qms[:, 0:1], None, ALU.mult), [skey, "qms"], [dkey])
                for q in range(1, 4):
                    S.op("dve", lambda q=q: nc.vector.scalar_tensor_tensor(out=dst[:], in0=src[:, q * NTOK:(q + 1) * NTOK], scalar=qms[:, q:q + 1],
                                                                          in1=dst[:], op0=ALU.mult, op1=ALU.add), [skey, "qms", dkey], [dkey])

            RBQ = [(RBK, q4) for q4 in range(4)]

            def bounce_to_ztm(src, skey):
                zv = zcm_d.rearrange("c (n1 n2) -> n1 c n2", n2=128)
                for q4 in range(4):
                    dma(zcm_d[q4 * 32:(q4 + 1) * 32, :], src[q4 * 32:(q4 + 1) * 32, :], [skey], [("zcm_d", q4)])
                for q4 in range(4):
                    dma(ztm[0:64, q4 * 32:(q4 + 1) * 32, :], zv[:, q4 * 32:(q4 + 1) * 32, :], [("zcm_d", q4)], [(RBK, q4)])

            def conv_core(go):
                def load_h(kb):
                    dma(Hc[kb % 4][:], H_d[go, :, kb * 2:kb * 2 + 2], [("H_d", go)], [f"Hc{kb % 4}"])

                def pre(kb):
                    if kb == 0:
                        load_h(0)
                        load_h(1)
                    if kb + 2 < 32:
                        load_h(kb + 2)

                def cb(kb, ps, pk):
                    s_ = kb % 4
                    y_ = kb % 2
                    zv = ps[:, :].rearrange("p (k r c) -> p k r c", k=2, r=2)
                    S.op("dve", lambda: nc.vector.tensor_tensor(Pb[y_][:], zv, Hc[s_][:, :, 0:2, :], ALU.mult), [pk, f"Hc{s_}"], [f"Pb{y_}"])
                    S.op("dve", lambda: nc.vector.tensor_tensor(Qb[y_][:], zv, Hc[s_][:, :, 2:4, :], ALU.mult), [pk, f"Hc{s_}"], [f"Qb{y_}"])

                def cb2(kb):
                    s_ = kb % 4
                    y_ = kb % 2
                    pe_ = PS[4 + kb % 2]
                    pek = f"ps{4 + kb % 2}"
                    er = pe_[:, 0:256].rearrange("p (k c) -> p k c", k=2)
                    ei = pe_[:, 256:512].rearrange("p (k c) -> p k c", k=2)
                    p0, p1 = Pb[y_][:, :, 0, :], Pb[y_][:, :, 1, :]
                    q0, q1 = Qb[y_][:, :, 0, :], Qb[y_][:, :, 1, :]
                    rk = [f"Pb{y_}", f"Qb{y_}", "finv"]
                    FR, FI, NFI, NFR = finv[:, 0, :], finv[:, 1, :], finv[:, 2, :], finv[:, 3, :]
                    mm(er, FR, p0, True, False, rk, [pek]); mm(er, NFR, p1, False, False, rk, [pek])
                    mm(er, NFI, q0, False, False, rk, [pek]); mm(er, NFI, q1, False, True, rk, [pek])
                    mm(ei, FI, p0, True, False, rk, [pek]); mm(ei, NFI, p1, False, False, rk, [pek])
                    mm(ei, FR, q0, False, False, rk, [pek]); mm(ei, FR, q1, False, True, rk, [pek])
                    S.op("act", lambda: nc.scalar.copy(ET[:, :, kb * 2:kb * 2 + 2], pe_[:, 0:256].rearrange("p (k c) -> p c k", k=2)), [pek], [RCK])
                    S.op("act", lambda: nc.scalar.copy(ET[:, :, 64 + kb * 2:64 + kb * 2 + 2], pe_[:, 256:512].rearrange("p (k c) -> p c k", k=2)),
                         [pek], [RCK])
                fft_forward(ztm, 64, AT, cb, zkey=RBK, akey=RAK, pre_batch=pre, cb_late=cb2)
                TB = [(PSB[0], "psb0"), (PSB[1], "psb1"), (PSW[0].bitcast(BF16), "ps0"), (PSW[1].bitcast(BF16), "ps2")]
                for c4 in range(32):
                    pb_, pbk = TB[c4 % 4]
                    for cc in range(4):
                        c = c4 * 4 + cc
                        S.op("pe", lambda c=c, cc=cc, pb_=pb_: nc.tensor.transpose(pb_[:, cc * 128:(cc + 1) * 128], ET[:, c, :], ident[:]),
                             [RCK, "ident"], [pbk])
                    src = pb_[:, 0:512].rearrange("p (cc n) -> p n cc", cc=4)
                    dst = Eb[:, :, c4 * 4:c4 * 4 + 4]
                    if c4 % 2 == 0:
                        S.op("act", lambda src=src, dst=dst: nc.scalar.copy(dst, src), [pbk], RBQ)
                    else:
                        S.op("dve", lambda src=src, dst=dst: nc.vector.tensor_copy(dst, src), [pbk], RBQ)

            Udv = U_d
            for g in range(4):
                for i in range(3):
                    dma(uraw[:, i, 1:L + 1], Udv[i * 512 + g * 128:i * 512 + (g + 1) * 128, :], ["U_d"], [RAK])
                S.op("pool", lambda: nc.gpsimd.memset(uraw[:, :, 0:1], 0.0), [RAK], [RAK])
                S.op("pool", lambda: nc.gpsimd.memset(uraw[:, :, L + 1:L + 2], 0.0), [RAK], [RAK])
                DG = hyst[:, 0:9 * 128].rearrange("p (i m) -> p i m", m=128)
                DGL = zzq[:, 0:9 * 128].rearrange("p (i m) -> p i m", m=128)
                wsel = cws[:, g:12:4, :]
                S.op("dve", lambda wsel=wsel: nc.vector.tensor_copy(whi[:].rearrange("p (i j) -> p i j", j=3), wsel), ["cws"], ["whi"])
                S.op("dve", lambda: nc.vector.tensor_copy(whf[:], whi[:]), ["whi"], ["whf"])
                S.op("dve", lambda wsel=wsel: nc.vector.tensor_tensor(wlf[:].rearrange("p (i j) -> p i j", j=3), wsel,
                                                                     whf[:].rearrange("p (i j) -> p i j", j=3), ALU.subtract), ["cws", "whf"], ["wlf"])
                for idx in range(9):
                    S.op("dve", lambda idx=idx: nc.vector.tensor_scalar(DG[:, idx, :], ident[:], whf[:, idx:idx + 1], None, ALU.mult), ["whf", "ident"], ["hyst"])
                    S.op("dve", lambda idx=idx: nc.vector.tensor_scalar(DGL[:, idx, :], ident[:], wlf[:, idx:idx + 1], None, ALU.mult), ["wlf", "ident"], ["zzq"])
                nq = 0
                for i, (dst, dkey) in enumerate(((vt, "vt"), (x1t, "x1t"), (x2t, RCK))):
                    ct = i * 4 + g
                    for q in range(16):
                        o0 = q * 512
                        ps = PS[nq % 2]
                        pk = f"ps{nq % 2}"
                        for j in range(3):
                            mm(ps[:, :], DG[:, i * 3 + j, :], uraw[:, i, o0 + j:o0 + j + 512], j == 0, False, ["hyst", RAK], [pk])
                        for j in range(3):
                            mm(ps[:, :], DGL[:, i * 3 + j, :], uraw[:, i, o0 + j:o0 + j + 512], False, j == 2, ["zzq", RAK], [pk])
                        if nq % 2 == 0:
                            S.op("act", lambda ps=ps, dst=dst, o0=o0, ct=ct: nc.scalar.activation(dst[:, o0:o0 + 512], ps[:, :], AF.Identity, bias=cbs[:, ct:ct + 1]),
                                 [pk, "cbs"], [dkey])
                        else:
                            S.op("dve", lambda ps=ps, dst=dst, o0=o0, ct=ct: nc.vector.tensor_scalar(dst[:, o0:o0 + 512], ps[:, :], cbs[:, ct:ct + 1], None, ALU.add),
                                 [pk, "cbs"], [dkey])
                        nq += 1
                masked_quarter(x2q, x2t, "x2q", RCK)
                snap("vt0", vt[:], "vt", L)
                snap("x1t0", x1t[:], "x1t", L)
                bounce_to_ztm(vt, "vt")
                conv_core(g * 2 + 0)
                vt3 = vt[:, :].rearrange("p (n1 n2) -> p n1 n2", n2=128)
                x13 = x1t[:, :].rearrange("p (n1 n2) -> p n1 n2", n2=128)
                for nb in range(16):
                    ps = PS[nb % 2]
                    pk = f"ps{nb % 2}"
                    for j in range(8):
                        n2 = nb * 8 + j
                        mm(ps[:, j * 64:(j + 1) * 64], Eb[:, n2, :], M2s[:, n2, :], True, True, RBQ + ["M2s"], [pk])
                    psv = ps[:, :].rearrange("p (j n) -> p n j", j=8)
                    gtv = gt[nb % 2][:, :].rearrange("p (n j) -> p n j", j=8)
                    gk_ = f"gt{nb % 2}"
                    S.op("dve", lambda psv=psv, gtv=gtv, nb=nb, g=g: nc.vector.scalar_tensor_tensor(out=gtv, in0=vt3[:, :, nb * 8:(nb + 1) * 8],
                                                                                             scalar=skps[:, g:g + 1], in1=psv, op0=ALU.mult, op1=ALU.add),
                         [pk, "vt", "skps"], [gk_])
                    ge_ = "pool" if nb % 2 == 0 else "dve"
                    S.op(ge_, lambda gtv=gtv, nb=nb, ge_=ge_: V(ge_).tensor_tensor(vt3[:, :, nb * 8:(nb + 1) * 8], gtv, x13[:, :, nb * 8:(nb + 1) * 8], ALU.mult),
                         [gk_, "x1t"], ["vt"])
                snap("zz0", vt[:], "vt", L)
                masked_quarter(zzq, vt, "zzq", "vt")
                bounce_to_ztm(vt, "vt")
                conv_core(g * 2 + 1)
                zq3 = zzq[:, :].rearrange("p (n1 n2) -> p n1 n2", n2=128)
                xq3 = x2q[:, :].rearrange("p (n1 n2) -> p n1 n2", n2=128)
                hy3 = hyst[:, :].rearrange("p (n1 n2) -> p n1 n2", n2=128)
                for nb in range(4):
                    ps = PS[nb % 2]
                    pk = f"ps{nb % 2}"
                    for j in range(32):
                        n2 = nb * 32 + j
                        mm(ps[:, j * 16:(j + 1) * 16], Eb[:, n2, :], M2qs[:, n2, :], True, True, RBQ + ["M2qs"], [pk])
                    psv = ps[:, :].rearrange("p (j n) -> p n j", j=32)
                    gtv = gt[nb % 2][:, :].rearrange("p (n j) -> p n j", j=32)
                    gk_ = f"gt{nb % 2}"
                    S.op("dve", lambda psv=psv, gtv=gtv, nb=nb, g=g: nc.vector.scalar_tensor_tensor(out=gtv, in0=zq3[:, :, nb * 32:(nb + 1) * 32],
                                                                                             scalar=skps[:, 4 + g:5 + g], in1=psv, op0=ALU.mult, op1=ALU.add),
                         [pk, "zzq", "skps"], [gk_])
                    S.op("pool", lambda gtv=gtv, nb=nb: nc.gpsimd.tensor_tensor(hy3[:, :, nb * 32:(nb + 1) * 32], gtv, xq3[:, :, nb * 32:(nb + 1) * 32], ALU.mult),
                         [gk_, "x2q"], ["hyst"])
                dma(hy_d[g * 128:(g + 1) * 128, :], hyst[:], ["hyst"], ["hy_d"])
            S.barrier()

        def phase4():
          with ExitStack() as p4:
            WoA = sb("WoA", [64, 8, D], BF16, p4)
            WoH = sb("WoH", [128, 4, D], BF16, p4)
            with ExitStack() as p4c:
                wsa = sb("wsa", [64, 8, D], F32, p4c)
                wsh = sb("wsh", [128, 4, D], F32, p4c)
                dma(wsa[:], w_out[0:512, :].rearrange("(h d) n -> d h n", d=64), [], ["wsa"])
                dma(wsh[:], w_out[512:1024, :].rearrange("(g p) n -> p g n", p=128), [], ["wsh"])
                S.op("dve", lambda: nc.vector.tensor_copy(WoA[:], wsa[:]), ["wsa"], ["WoA"])
                S.op("pool", lambda: nc.gpsimd.tensor_copy(WoH[:], wsh[:]), ["wsh"], ["WoH"])
                S.barrier()
            attc = sb("attc", [64, 8, 512], BF16, p4)
            hyc = sb("hyc", [128, 4, 512], BF16, p4)
            sqb = sb("sqb4", [128, 8, 512], BF16, p4)
            rs = sb("rs4", [128, 512], F32, p4)
            mixA = sb("mixA", [64, 8, 512], BF16, p4)
            mixH = sb("mixH", [128, 4, 512], BF16, p4)
            h1 = sb("h1", [128, 8, 512], F32, p4)
            mT = sb("mT", [128, 8, 512], BF16, p4)
            Wmi_s = [sb(f"Wmi_s{i}", [128, 8, 1024], BF16, p4) for i in range(2)]
            Wmo_s = [sb(f"Wmo_s{i}", [128, 32, 128], BF16, p4) for i in range(2)]
            fT = sb("fT", [128, 32, 512], BF16, p4)
            rb = [sb(f"rb{i}", [128, 512], F32, p4) for i in range(2)]
            xqv = xTq.rearrange("(k p) t -> p k t", p=128)
            Wmiv = Wmi_d.rearrange("(k p) n -> p k n", p=128)
            outv = outT.rearrange("(k p) t -> p k t", p=128)
            nw = 0
            nwo = 0
            for j in range(4):
                tsl = slice(j * 512, (j + 1) * 512)
                dma(attc[:], att_d[:, tsl].rearrange("(h d) t -> d h t", d=64), ["att_d"], ["attc", "na_src"])
                dma(hyc[:], hy_d[:, tsl].rearrange("(g p) t -> p g t", p=128), ["hy_d"], ["hyc", "nh_src"])
                dma(h1[:], xqv[:, :, tsl], [], ["h1", "n2_src", "nf_src"])
                rms_apply(attc[:], gas, mixA, 1.0 / 512.0, 8, "na", sqb, rs, np_=64)
                rms_apply(hyc[:], ghs, mixH, 1.0 / 512.0, 4, "nh", sqb, rs)
                for ct in range(8):
                    csl = slice(ct * 128, (ct + 1) * 128)
                    for h in range(8):
                        mm(PS[1][:, :], WoA[:, h, csl], mixA[:, h, :], h == 0, False, ["WoA", "na_dst"], ["ps1"])
                    for g in range(4):
                        mm(PS[1][:, :], WoH[:, g, csl], mixH[:, g, :], False, g == 3, ["WoH", "nh_dst"], ["ps1"])
                    S.op("dve", lambda ct=ct: nc.vector.tensor_tensor(h1[:, ct, :], h1[:, ct, :], PS[1][:, :], ALU.add), ["ps1", "h1"], ["h1", "n2_src", "nf_src"])
                rms_apply(h1[:], g2s, mT, 1.0 / D, 8, "n2", sqb, rs)
                for pc in range(4):
                    s_ = nw % 2
                    dma(Wmi_s[s_][:], Wmiv[:, :, pc * 1024:(pc + 1) * 1024], ["Wmid"], [f"Wmi_s{s_}"])
                    nw += 1
                    for ft in range(8):
                        f = pc * 8 + ft
                        pf = PS[2 + f % 2]
                        pfk = f"ps{2 + f % 2}"
                        for k in range(8):
                            mm(pf[:, :], Wmi_s[s_][:, k, ft * 128:(ft + 1) * 128], mT[:, k, :], k == 0, k == 7, [f"Wmi_s{s_}", "n2_dst"], [pfk])
                        r_ = f % 2
                        S.op("act", lambda pf=pf, r_=r_: nc.scalar.activation(rb[r_][:], pf[:, :], AF.Relu), [pfk], [f"rb{r_}"])
                        e = ("pool", "dve")[f % 2]
                        S.op(e, lambda e=e, r_=r_, f=f: V(e).tensor_tensor(fT[:, f, :], rb[r_][:], rb[r_][:], ALU.mult), [f"rb{r_}"], ["fT"])
                for ct in range(8):
                    s_ = nwo % 2
                    dma(Wmo_s[s_][:], Wmo_d[ct], ["Wmod"], [f"Wmo_s{s_}"])
                    nwo += 1
                    po = PS[4 + ct % 2]
                    pok = f"ps{4 + ct % 2}"
                    for f in range(32):
                        mm(po[:, :], Wmo_s[s_][:, f, :], fT[:, f, :], f == 0, f == 31, [f"Wmo_s{s_}", "fT"], [pok])
                    S.op("dve", lambda ct=ct, po=po: nc.vector.tensor_tensor(h1[:, ct, :], h1[:, ct, :], po[:, :], ALU.add), [pok, "h1"], ["h1", "n2_src", "nf_src"])
                rms_apply(h1[:], gfs, h1, 1.0 / D, 8, "nf", sqb, rs)
                dma(outv[:, :, tsl], h1[:], ["h1", "nf_dst"], ["outT"])
            S.barrier()

        if 0 in phases:
            phase0()
        if 1 in phases:
            phase12()
        if 3 in phases:
            phase3()
        if 4 in phases:
            phase4()
        if debug:
            with ExitStack() as pd:
                for name in debug:
                    if name not in ("att_d", "hy_d", "U_d", "H_d0"):
                        continue
                    src_, shp = {"att_d": (att_d, [512, NTOK]), "hy_d": (hy_d, [512, NTOK]), "U_d": (U_d[0:512, :], [512, L]),
                                 "H_d0": (H_d[0:4, :, 0:4].rearrange("a p k r c -> (a p) (k r c)"), [512, 2048])}[name]
                    dbo = nc.dram_tensor("dbg_" + name, shp, BF16, kind="ExternalOutput").ap()
                    dbs = sb("dbs_" + name, [128, 4, shp[1]], BF16, pd)
                    dma(dbs[:], src_.rearrange("(a p) t -> p a t", p=128), [name], ["dbs_" + name])
                    dma(dbo.rearrange("(a p) t -> p a t", p=128), dbs[:], ["dbs_" + name], ["dbg_" + name])
                S.barrier()
        S.barrier()
    return nc, S


def _core_inputs(inp, core):
    T = _tables()
    b, r = core // 4, core % 4
    f = lambda a: np.ascontiguousarray(np.asarray(a, dtype=np.float32))
    x = np.asarray(inp["x"], dtype=np.float32)
    xT = np.ascontiguousarray(x[b].T)
    tq = slice(r * NTOK, (r + 1) * NTOK)
    qm = np.zeros((128, 4), np.float32)
    qm[:, r] = 1.0
    m = {
        "xT": xT, "xTq": np.ascontiguousarray(xT[:, tq]),
        "w_in": f(inp["w_in"][0]), "w_out": f(inp["w_out"][0]), "w_mi": f(inp["w_mlp_in"][0]), "w_mo": f(inp["w_mlp_out"][0]),
        "g1": f(np.asarray(inp["norm1_g"][0]).reshape(8, 128).T), "g2": f(np.asarray(inp["norm2_g"][0]).reshape(8, 128).T),
        "gf": f(np.asarray(inp["final_g"]).reshape(8, 128).T),
        "gq": f(np.asarray(inp["q_norm_g"][0]).reshape(64, 1)), "gk": f(np.asarray(inp["k_norm_g"][0]).reshape(64, 1)),
        "ga": f(np.asarray(inp["attn_out_g"][0]).reshape(8, 64).T), "gh": f(np.asarray(inp["hy_out_g"][0]).reshape(4, 128).T),
        "cw": f(np.asarray(inp["hy_conv_w"][0]).T.reshape(12, 128, 3).transpose(1, 0, 2)),
        "cb": f(np.asarray(inp["hy_conv_b"][0]).reshape(12, 128).T),
        "fw1": f(inp["filt_w1"][0]), "fw2": f(inp["filt_w2"][0]), "fw3": f(inp["filt_w3"][0]), "fw4": f(inp["filt_w4"][0]),
        "fb": f(np.stack([np.asarray(inp["filt_b1"][0]), np.asarray(inp["filt_b2"][0]), np.asarray(inp["filt_b3"][0])], axis=1)),
        "ffreq": f(np.asarray(inp["filt_freq"][0]).reshape(64, 1)),
        "fdel": f(np.asarray(inp["filt_deltas"][0]).reshape(16, 128).T),
        "skp": f(np.asarray(inp["hy_skip_d"][0]).reshape(2, 4, 128).transpose(2, 0, 1).reshape(128, 8)),
        "qmask": qm,
        "c_cos": T["cosT"], "c_sin": T["sinT"],
        "c_cosq": np.ascontiguousarray(T["cosT"][:, tq]), "c_sinq": np.ascontiguousarray(T["sinT"][:, tq]),
        "c_zT": np.ascontiguousarray(np.stack([T["zT"], T["zrT"]])), "c_t": np.ascontiguousarray(np.stack([T["t"], T["trev"]])),
        "c_tb": T["tb"], "c_jt": T["jt"],
        "c_fpack": T["fpack"], "c_G": T["G"], "c_finv": T["finv"], "c_M2": T["M2"],
        "c_M2q": np.ascontiguousarray(T["M2"][:, :, r * 16:(r + 1) * 16]),
        "c_rot": T["rot"], "c_sel": T["sel"], "c_ones": T["ones"], "c_ident": T["ident"],
    }
    return m


def _emit(nc, S):
    from contextlib import ExitStack
    st = ExitStack()
    sems = {e: [st.enter_context(nc.semaphore(f"s_{e}{i}")) for i in range(4)] for e in S.CE}
    dsems = [st.enter_context(nc.semaphore(f"d{i}")) for i in range(S.NDS)]
    info = S.emit(sems, dsems)
    return st, info


def kernel(**inputs):
    nc, S = build_program()
    st, _ = _emit(nc, S)
    with st:
        in_maps = [_core_inputs(inputs, c) for c in range(8)]
        res = run_bass_kernel_spmd(nc, in_maps, core_ids=list(range(8)))
    out = np.empty((NB, L, D), np.float32)
    for c in range(8):
        b, r = c // 4, c % 4
        out[b, r * NTOK:(r + 1) * NTOK, :] = np.asarray(res.results[c]["outT"]).T
    return out
```

```python
import math
import numpy as np
import ml_dtypes
import concourse.bass as bass
import concourse.mybir as mybir
from concourse.bass_utils import run_bass_kernel_spmd

F32 = mybir.dt.float32
BF16 = mybir.dt.bfloat16
ALU = mybir.AluOpType
AF = mybir.ActivationFunctionType
bf16_np = ml_dtypes.bfloat16

D = 1024
L = 8192
NB = 2
NTOK = 2048
HD = 64
NQH = 8
NKV = 2
HYW = 512
DFF = 4096
EPS = 1e-6
NFFT = 2 * L
TWO_PI = 2.0 * math.pi
MAGIC = 12582912.0


class Op:
    __slots__ = ("eng", "fn", "reads", "writes", "dma", "idx", "gid", "waits", "signal", "clock",
                 "sem", "val")


class Sched:
    CE = ("pe", "act", "dve", "pool", "sp")
    EPOCH = 12000
    NDS = 40

    def __init__(self, nc):
        self.nc = nc
        self.E = {"pe": nc.tensor, "act": nc.scalar, "dve": nc.vector, "pool": nc.gpsimd,
                  "sp": nc.sync}
        self.ops = []
        self.last_w = {}
        self.readers = {}
        self.known = {e: {} for e in self.CE}
        self.known_dma = {e: set() for e in self.CE}
        self.cnt = {e: 0 for e in self.CE}
        self.last_op = {e: None for e in self.CE}
        self.live_dma = []

    def op(self, eng, fn, reads=(), writes=(), dma=False):
        o = Op()
        o.eng, o.fn, o.reads, o.writes, o.dma = eng, fn, tuple(reads), tuple(writes), dma
        o.signal = dma
        o.sem = None
        o.val = None
        o.gid = len(self.ops)
        deps = {}
        for k in o.reads:
            w = self.last_w.get(k)
            if w is not None:
                deps[w.gid] = w
        for k in o.writes:
            w = self.last_w.get(k)
            if w is not None:
                deps[w.gid] = w
            for r in self.readers.get(k, ()):
                deps[r.gid] = r
        self._finish(o, deps.values())
        for k in o.writes:
            self.last_w[k] = o
            self.readers[k] = []
        for k in o.reads:
            lst = self.readers.setdefault(k, [])
            if not dma:
                lst[:] = [r for r in lst if r.dma or r.eng != eng]
            lst.append(o)
        return o

    def _finish(self, o, deps):
        eng = o.eng
        kn = self.known[eng]
        kd = self.known_dma[eng]
        waits = []
        best = {}
        rset = set(o.reads)
        for d in deps:
            if d.dma:
                if d.gid not in kd:
                    waits.append(d)
                    kd.add(d.gid)
                continue
            if d.eng == eng:
                if eng in ("pe", "sp"):
                    continue
            if kn.get(d.eng, -1) >= d.idx:
                continue
            b = best.get(d.eng)
            if b is None or b.idx < d.idx:
                best[d.eng] = d
        for d in best.values():
            waits.append(d)
            if kn.get(d.eng, -1) < d.idx:
                kn[d.eng] = d.idx
            for e2, i2 in d.clock.items():
                if kn.get(e2, -1) < i2:
                    kn[e2] = i2
        for d in waits:
            d.signal = True
        o.waits = waits
        o.idx = self.cnt[eng]
        self.cnt[eng] += 1
        o.clock = dict(kn)
        self.ops.append(o)
        if o.fn is not None:
            self.last_op[eng] = o
        if o.dma:
            self.live_dma.append(o)

    def barrier(self):
        lasts = [self.last_op[e] for e in self.CE if self.last_op[e] is not None]
        dmas = list(self.live_dma)
        for e in self.CE:
            o = Op()
            o.eng, o.fn, o.reads, o.writes, o.dma = e, None, (), (), False
            o.signal = False
            o.sem = None
            o.val = None
            o.gid = len(self.ops)
            deps = [d for d in lasts if not d.dma and d.eng != e] + dmas
            self._finish(o, deps)
        self.live_dma = []

    def emit(self, sems, dsems):
        sig = {e: 0 for e in self.CE}
        nd = 0
        for o in self.ops:
            E = self.E[o.eng]
            for d in o.waits:
                E.wait_ge(d.sem, d.val)
            if o.fn is None:
                continue
            if o.dma:
                s = dsems[nd % self.NDS]
                rnd = nd // self.NDS
                if rnd > 0:
                    E.wait_ge(s, 16 * rnd)
                ins = o.fn()
                ins.then_inc(s, 16)
                o.sem, o.val = s, 16 * (rnd + 1)
                nd += 1
            else:
                ins = o.fn()
                if o.signal:
                    c = sig[o.eng]
                    o.sem = sems[o.eng][c // self.EPOCH]
                    o.val = c % self.EPOCH + 1
                    ins.then_inc(o.sem, 1)
                    sig[o.eng] = c + 1
        return sig, nd


_TABLES = None


def _rope_tables():
    rows = L // 64
    row = np.repeat(np.arange(rows, dtype=np.float32), 64)
    col = np.tile(np.arange(64, dtype=np.float32), rows)
    half = HD // 2
    inv_freq = (np.float32(10000.0) ** (-np.arange(0, half, 2, dtype=np.float32) / np.float32(half))).astype(np.float32)
    ang_r = (row[:, None] * inv_freq[None, :]).astype(np.float32)
    ang_c = (col[:, None] * inv_freq[None, :]).astype(np.float32)
    cos_r, sin_r = np.cos(ang_r), np.sin(ang_r)
    cos_c, sin_c = np.cos(ang_c), np.sin(ang_c)
    cosT = np.concatenate([cos_r.T, cos_r.T, cos_c.T, cos_c.T], axis=0).astype(np.float32)
    sinT = np.concatenate([sin_r.T, sin_r.T, sin_c.T, sin_c.T], axis=0).astype(np.float32)
    return np.ascontiguousarray(cosT), np.ascontiguousarray(sinT)


def _filter_pos_tables():
    f32 = np.float32
    t = np.linspace(0.0, 1.0, L, dtype=f32)
    bands = 16
    w_ang = (f32(2.0 * math.pi) * np.arange(L, dtype=f32) / f32(L)).astype(f32)
    band_f = np.linspace(1e-4, bands - 1, bands, dtype=f32)
    ang = (w_ang[:, None] * band_f[None, :]).astype(f32)
    z = np.concatenate([t[:, None], np.cos(ang), -np.sin(ang)], axis=-1).astype(f32)
    zT = np.ascontiguousarray(z.T)
    idx = (L - np.arange(L)) % L
    zrT = np.ascontiguousarray(z[idx].T)
    trev = t[idx].copy()
    return zT, zrT, t.copy(), trev


def _fft_tables():
    N = NFFT
    n1 = np.arange(128, dtype=np.float64)
    k1 = np.arange(64, dtype=np.float64)
    th = 2 * np.pi * np.outer(n1, k1 + 0.5) / 128.0
    fpack = np.stack([np.sin(th), np.cos(th), -np.sin(th)], axis=2).reshape(128, 192)
    n2 = np.arange(128, dtype=np.float64)
    k2 = np.arange(128, dtype=np.float64)
    kap = k1[None, :, None] + 128.0 * k2[None, None, :] + 0.5
    thg = 2 * np.pi * n2[:, None, None] * kap / N
    G = np.stack([np.cos(thg), -np.sin(thg)], axis=2)
    thf = 2 * np.pi * np.outer(k2, n2) / 128.0
    finv = np.stack([np.cos(thf), np.sin(thf), -np.sin(thf), -np.cos(thf)], axis=1)
    n1h = np.arange(64, dtype=np.float64)
    phi = 2 * np.pi * (k1[:, None, None] + 0.5) * (128.0 * n1h[None, None, :] + n2[None, :, None]) / N
    M2 = np.concatenate([(2.0 / N) * np.cos(phi), -(2.0 / N) * np.sin(phi)], axis=0)
    return (fpack.astype(bf16_np), G.astype(bf16_np), finv.astype(bf16_np), M2.astype(bf16_np))


def _tables():
    global _TABLES
    if _TABLES is None:
        cosT, sinT = _rope_tables()
        zT, zrT, t, trev = _filter_pos_tables()
        fpack, G, finv, M2 = _fft_tables()
        ii = np.arange(512, dtype=np.float64)
        jj = np.arange(16, dtype=np.float64)
        tbm = np.stack([ii / (L - 1), (L - ii) / (L - 1)]).astype(np.float32)
        jtm = np.stack([512.0 * jj / (L - 1), -512.0 * jj / (L - 1)]).astype(np.float32)
        rot = np.zeros((64, 64), np.float32)
        for m in range(64):
            if (m % 32) < 16:
                rot[m + 16, m] = -1.0
            else:
                rot[m - 16, m] = 1.0
        sel = np.zeros((65, 64), np.float32)
        sel[64, :] = 1.0
        _TABLES = dict(cosT=cosT, sinT=sinT, zT=zT, zrT=zrT, t=t, trev=trev, fpack=fpack, G=G,
                       finv=finv, M2=M2, rot=rot, sel=sel, tb=tbm, jt=jtm,
                       ones=np.ones((128, 128), np.float32),
                       ident=np.eye(128, dtype=np.float32).astype(bf16_np))
    return _TABLES


def build_program(phases=(0, 1, 2, 3, 4), debug=()):
    nc = bass.Bass("TRN2", target_bir_lowering=False)
    S = Sched(nc)

    def din(name, shape, dt=F32):
        return nc.dram_tensor(name, list(shape), dt, kind="ExternalInput").ap()

    def dscr(name, shape, dt):
        return nc.dram_tensor(name, list(shape), dt, kind="Internal").ap()

    xT = din("xT", [D, L])
    xTq = din("xTq", [D, NTOK])
    w_in = din("w_in", [D, 2304])
    w_out = din("w_out", [D, D])
    w_mi = din("w_mi", [D, DFF])
    w_mo = din("w_mo", [DFF, D])
    g1 = din("g1", [128, 8]); g2 = din("g2", [128, 8]); gf = din("gf", [128, 8])
    gq = din("gq", [64, 1]); gk = din("gk", [64, 1])
    ga = din("ga", [128, 4]); gh = din("gh", [128, 4])
    cw = din("cw", [128, 12, 3]); cb = din("cb", [128, 12])
    fw1 = din("fw1", [33, 64]); fw2 = din("fw2", [64, 64]); fw3 = din("fw3", [64, 64])
    fw4 = din("fw4", [64, 2048])
    fb = din("fb", [64, 3]); ffreq = din("ffreq", [64, 1])
    fdel = din("fdel", [128, 16])
    skp = din("skp", [128, 8])
    qmask = din("qmask", [128, 4])
    c_cos = din("c_cos", [64, L]); c_sin = din("c_sin", [64, L])
    c_cosq = din("c_cosq", [64, NTOK]); c_sinq = din("c_sinq", [64, NTOK])
    c_zT = din("c_zT", [2, 33, L])
    c_t = din("c_t", [2, L])
    c_tb = din("c_tb", [2, 512])
    c_jt = din("c_jt", [2, 16])
    c_fpack = din("c_fpack", [128, 192], BF16)
    c_G = din("c_G", [128, 64, 2, 128], BF16)
    c_finv = din("c_finv", [128, 4, 128], BF16)
    c_M2 = din("c_M2", [128, 128, 64], BF16)
    c_M2q = din("c_M2q", [128, 128, 16], BF16)
    c_rot = din("c_rot", [64, 64]); c_sel = din("c_sel", [65, 64]); c_ones = din("c_ones", [128, 128])
    c_ident = din("c_ident", [128, 128], BF16)
    outT = nc.dram_tensor("outT", [D, NTOK], F32, kind="ExternalOutput").ap()

    U_d = dscr("U_d", [1536, L], BF16)
    H_d = dscr("H_d", [8, 128, 64, 4, 128], BF16)
    taps_d = dscr("taps_d", [128, NFFT], BF16)
    zcm_d = dscr("zcm_d", [128, L], BF16)
    att_d = dscr("att_d", [512, NTOK], BF16)
    hy_d = dscr("hy_d", [512, NTOK], BF16)
    Wmi_d = dscr("Wmi_d", [D, DFF], BF16)
    Wmo_d = dscr("Wmo_d", [8, 128, 32, 128], BF16)

    from contextlib import ExitStack
    es = ExitStack()

    _names = {}

    def sb(name, shape, dt, stack=None):
        n_ = _names.get(name, 0)
        _names[name] = n_ + 1
        nm = name if n_ == 0 else f"{name}_r{n_}"
        return (stack or es).enter_context(nc.sbuf_tensor(nm, list(shape), dt))

    def dma(out, in_, reads, writes, eng="sp"):
        return S.op(eng, lambda: S.E[eng].dma_start(out=out, in_=in_), reads, writes, dma=True)

    def mm(out, lhsT, rhs, start, stop, reads, writes):
        return S.op("pe", lambda: nc.tensor.matmul(out, lhsT, rhs, start=start, stop=stop), reads, writes)

    rr = {"i": 0}

    def alt(engs=("dve", "pool")):
        rr["i"] += 1
        return engs[rr["i"] % len(engs)]

    def V(eng):
        return S.E[eng]

    with es:
        PSW = [es.enter_context(nc.psum_tensor(f"psw{i}", [128, 1024], F32)) for i in range(3)]
        PS = [PSW[i // 2][:, (i % 2) * 512:(i % 2 + 1) * 512] for i in range(6)]
        PSB = [es.enter_context(nc.psum_tensor(f"psb{i}", [128, 1024], BF16)) for i in range(2)]
        PSX = [PSB[i].bitcast(F32)[:, :] for i in range(2)]

        ones = sb("ones", [128, 128], F32)
        ident = sb("ident", [128, 128], BF16)
        rot = sb("rot", [64, 64], F32)
        sel = sb("sel", [65, 64], F32)
        epsT = sb("epsT", [128, 1], F32)
        onesbf = sb("onesbf", [128, 128], BF16)
        g1s = sb("g1s", [128, 8], F32); g2s = sb("g2s", [128, 8], F32); gfs = sb("gfs", [128, 8], F32)
        gqs = sb("gqs", [64, 1], F32); gks = sb("gks", [64, 1], F32)
        gas = sb("gas", [128, 4], F32); ghs = sb("ghs", [128, 4], F32)
        cws = sb("cws", [128, 12, 3], F32); cbs = sb("cbs", [128, 12], F32)
        skps = sb("skps", [128, 8], F32); qms = sb("qms", [128, 4], F32)
        for t_, d_ in ((ones, c_ones), (ident, c_ident), (rot, c_rot), (sel, c_sel), (g1s, g1), (g2s, g2),
                       (gfs, gf), (gqs, gq), (gks, gk), (gas, ga), (ghs, gh), (cws, cw), (cbs, cb),
                       (skps, skp), (qms, qmask)):
            dma(t_[:], d_, [], [t_.name if hasattr(t_, "name") else id(t_)])
        KEY = lambda t_: t_.name if hasattr(t_, "name") else id(t_)
        S.op("dve", lambda: nc.vector.memset(epsT[:], EPS), [], ["epsT"])
        S.op("pool", lambda: nc.gpsimd.memset(onesbf[:], 1.0), [], ["onesbf"])

        FC = {}
        SNAP = {}

        def snap(name, src_ap, key, ncols):
            if name not in debug or name in SNAP:
                return
            SNAP[name] = nc.dram_tensor("dbg_" + name, [128, ncols], BF16, kind="ExternalOutput").ap()
            dma(SNAP[name], src_ap, [key], ["dbg_" + name])


        def load_fft_consts(stack):
            FC["fpack"] = sb("fpack", [128, 192], BF16, stack)
            FC["finv"] = sb("finv", [128, 4, 128], BF16, stack)
            FC["Gq"] = [sb(f"Gq{i}", [128, 8, 2, 128], BF16, stack) for i in range(2)]
            dma(FC["fpack"][:], c_fpack, [], ["fpack"])
            dma(FC["finv"][:], c_finv, [], ["finv"])

        def rsqrt(out, in_, scale, reads, writes, np_=128):
            S.op("act", lambda: nc.scalar.activation(out, in_, AF.Sqrt, bias=epsT[0:np_, :], scale=scale),
                 list(reads) + ["epsT"], writes)
            S.op("dve", lambda: nc.vector.reciprocal(out, out), writes, writes)

        def fft_forward(ztm, K, AT, cb_batch, zkey="ztm", akey="AT", pre_batch=None, cb_late=None):
            fpack = FC["fpack"]
            for c2 in range(64):
                ps = PS[c2 % 4]
                pk = f"ps{c2 % 4}"
                for cc in range(2):
                    c = c2 * 2 + cc
                    mm(ps[:, cc * 192:(cc + 1) * 192], ztm[0:K, c, :], fpack[0:K, :], True, True,
                       [(zkey, c // 32), "fpack"], [pk])
                e = alt(("act", "dve"))
                src = ps[:, 0:384].rearrange("p (cc j) -> p j cc", cc=2)
                dst = AT[:, :, c2 * 2:c2 * 2 + 2]
                if e == "act":
                    S.op("act", lambda dst=dst, src=src: nc.scalar.copy(dst, src), [pk], [akey])
                else:
                    S.op("dve", lambda dst=dst, src=src: nc.vector.tensor_copy(dst, src), [pk], [akey])

            def load_g(q8):
                dma(FC["Gq"][q8 % 2][:], c_G[:, q8 * 8:(q8 + 1) * 8], [], [f"Gq{q8 % 2}"])

            load_g(0)
            for kb in range(32):
                q4 = kb // 4
                Gq = FC["Gq"][q4 % 2]
                gkey = f"Gq{q4 % 2}"
                if kb % 4 == 0 and q4 + 1 < 8:
                    load_g(q4 + 1)
                if pre_batch is not None:
                    pre_batch(kb)
                ps = PS[2 + kb % 2]
                pk = f"ps{2 + kb % 2}"
                for kk in range(2):
                    k1 = kb * 2 + kk
                    kl = k1 % 8
                    zp = ps[:, kk * 256:(kk + 1) * 256].rearrange("p (r c) -> p r c", r=2)
                    mm(zp, Gq[:, kl, 0, :], AT[:, 3 * k1 + 1:3 * k1 + 3, :], True, False, [akey, gkey], [pk])
                    mm(zp, Gq[:, kl, 1, :], AT[:, 3 * k1:3 * k1 + 2, :], False, True, [akey, gkey], [pk])
                cb_batch(kb, ps, pk)
                if cb_late is not None and kb >= 1:
                    cb_late(kb - 1)
            if cb_late is not None:
                cb_late(31)

        cast_jobs = []
        for cbk in range(8):
            cast_jobs.append((w_mi.rearrange("(k p) n -> p k n", p=128)[:, :, cbk * 512:(cbk + 1) * 512],
                              Wmi_d.rearrange("(k p) n -> p k n", p=128)[:, :, cbk * 512:(cbk + 1) * 512], None))
        for kr in range(4):
            for cbk in range(2):
                cast_jobs.append((w_mo.rearrange("(k p) n -> p k n", p=128)[:, kr * 8:(kr + 1) * 8, cbk * 512:(cbk + 1) * 512],
                                  None, (kr, cbk)))

        def emit_cast(n, wst, wbf):
            src, dst, tl = cast_jobs[n]
            s_ = n % 2
            dma(wst[s_][:], src, [], [f"wst{s_}"])
            S.op("dve", lambda s_=s_: nc.vector.tensor_copy(wbf[s_][:], wst[s_][:]), [f"wst{s_}"], [f"wbf{s_}"])
            if tl is None:
                dma(dst, wbf[s_][:], [f"wbf{s_}"], ["Wmid"])
            else:
                kr_, cbk_ = tl
                for c_ in range(4):
                    dma(Wmo_d[cbk_ * 4 + c_, :, kr_ * 8:(kr_ + 1) * 8, :], wbf[s_][:, :, c_ * 128:(c_ + 1) * 128], [f"wbf{s_}"], ["Wmod"])

        def phase0():
          with ExitStack() as p0:
            load_fft_consts(p0)

            w1s = sb("w1s", [33, 64], F32, p0); w2s = sb("w2s", [64, 64], F32, p0); w3s = sb("w3s", [64, 64], F32, p0)
            w4b = sb("w4b", [64, 2048], BF16, p0)
            fbs = sb("fbs", [64, 3], F32, p0); frs = sb("frs", [64, 1], F32, p0); ffb = sb("ffb", [64, 3], F32, p0)
            dls = sb("dls", [128, 16], F32, p0); ndl = sb("ndl", [128, 16], F32, p0)
            negpi = sb("negpi", [128, 1], F32, p0)
            H3 = sb("H3", [64, 2, L], BF16, p0)
            p0m = ExitStack()
            w4s = sb("w4s", [64, 2048], F32, p0m)
            for t_, d_, k_ in ((w1s, fw1, "w1s"), (w2s, fw2, "w2s"), (w3s, fw3, "w3s"), (w4s, fw4, "w4s"),
                               (fbs, fb, "fbs"), (frs, ffreq, "frs"), (dls, fdel, "dls")):
                dma(t_[:], d_, [], [k_])
            S.op("dve", lambda: nc.vector.tensor_copy(w4b[:], w4s[:]), ["w4s"], ["w4b"])
            S.op("dve", lambda: nc.vector.tensor_scalar(ffb[:], fbs[:], frs[:, 0:1], None, ALU.mult), ["fbs", "frs"], ["ffb"])
            S.op("dve", lambda: nc.vector.tensor_scalar(ndl[:], dls[:], -1.0, None, ALU.mult), ["dls"], ["ndl"])
            S.op("dve", lambda: nc.vector.tensor_tensor(ndl[:], ndl[:], dls[:], ALU.min), ["dls", "ndl"], ["ndl"])
            S.op("dve", lambda: nc.vector.memset(negpi[:], 0.0), [], ["negpi"])
            NCH = 8
            zc = [sb(f"zc{i}", [33, 512], F32, p0m) for i in range(NCH)]
            ybL = [sb(f"yb{i}", [64, 512], F32, p0m) for i in range(NCH)]
            ttL = [sb(f"tt{i}", [64, 512], F32, p0m) for i in range(NCH)]
            nrL = [sb(f"nr{i}", [64, 512], F32, p0m) for i in range(NCH)]
            fhL = [sb(f"fh{i}", [64, 512], F32, p0m) for i in range(NCH)]
            frs2 = sb("frs2", [64, 1], F32, p0m)
            ffb2 = sb("ffb2", [64, 3], F32, p0m)
            S.op("dve", lambda: nc.vector.tensor_scalar(frs2[:], frs[:], 1.0 / TWO_PI, None, ALU.mult), ["frs"], ["frs2"])
            S.op("dve", lambda: nc.vector.tensor_scalar(ffb2[:], fbs[:], frs2[:, 0:1], None, ALU.mult), ["fbs", "frs2"], ["ffb2"])
            BANKS = [(PS[i], f"ps{i}") for i in range(6)] + [(PSX[0], "psb0"), (PSX[1], "psb1")]
            wL = [(w1s, "w1s"), (w2s, "w2s"), (w3s, "w3s")]

            def mlp_layer(c_, li, var, j):
                psm, pmk = BANKS[c_]
                src, srck = (zc[c_], f"zc{c_}") if li == 0 else (fhL[c_], f"fh{c_}")
                w_, wk_ = wL[li]
                mm(psm[0:64, :], w_[:], src[:], True, True, [srck, wk_], [pmk])
                S.op("act", lambda: nc.scalar.activation(ybL[c_][:], psm[0:64, :], AF.Identity, bias=ffb2[:, li:li + 1], scale=frs2[:, 0:1]),
                     [pmk, "frs2", "ffb2"], [f"yb{c_}"])
                S.op("dve", lambda: nc.vector.tensor_scalar(ttL[c_][:], ybL[c_][:], MAGIC, None, ALU.add), [f"yb{c_}"], [f"tt{c_}"])
                S.op("dve", lambda: nc.vector.scalar_tensor_tensor(out=nrL[c_][:], in0=ttL[c_][:], scalar=MAGIC, in1=ybL[c_][:],
                                                                  op0=ALU.subtract, op1=ALU.subtract), [f"tt{c_}", f"yb{c_}"], [f"nr{c_}"])
                if li < 2:
                    dst, dk = fhL[c_][:], f"fh{c_}"
                else:
                    dst, dk = H3[:, var, j * 512:(j + 1) * 512], ("H3", var)
                S.op("act", lambda: nc.scalar.activation(dst, nrL[c_][:], AF.Sin, bias=negpi[0:64, :], scale=-TWO_PI), [f"nr{c_}", "negpi"], [dk])

            for jg in range(4):
                chains = [(c_, c_ % 2, jg * 4 + c_ // 2) for c_ in range(NCH)]
                for c_, var, j in chains:
                    dma(zc[c_][:], c_zT[var, :, j * 512:(j + 1) * 512], [], [f"zc{c_}"])
                for li in range(3):
                    for c_, var, j in chains:
                        mlp_layer(c_, li, var, j)

            S.barrier()
            p0m.close()
            tapsUL = [sb(f"tapsU{i}", [128, NFFT], BF16, p0) for i in range(2)]
            ztmf = sb("ztmf", [128, 128, 128], BF16, p0)
            ATf = sb("ATf", [128, 192, 128], BF16, p0)
            tb = sb("tb", [128, 2, 512], F32, p0)
            jt = sb("jt", [128, 2, 16], F32, p0)
            for hf in range(2):
                dma(tb[:, hf, :], c_tb[hf:hf + 1, :].broadcast_to([128, 512]), [], ["tb"])
                dma(jt[:, hf, :], c_jt[hf:hf + 1, :].broadcast_to([128, 16]), [], ["jt"])
            wbase1 = sb("wbase", [128, 512], F32, p0)
            wbase = [wbase1, wbase1]
            wcj = [sb(f"wcj{i}", [128, 16], F32, p0) for i in range(2)]
            tpf = [sb(f"tpf{i}", [128, 512], F32, p0) for i in range(2)]
            psum_ = sb("psum_", [128, 32], F32, p0)
            nrm = sb("nrm", [128, 2], F32, p0)
            Hst = [sb(f"Hst{i}", [128, 2, 4, 128], BF16, p0) for i in range(2)]

            def taps_chunk(go, cidx):
                g, o = go // 2, go % 2
                half, j = cidx // 16, cidx % 16
                tU = tapsUL[go % 2]
                tk = f"tapsU{go % 2}"
                col0 = o * 1024 + half * 512 + g * 128
                ct = col0 // 128
                if j == 0:
                    S.op("act", lambda: nc.scalar.activation(wbase[half][:], tb[:, half, :], AF.Exp, scale=ndl[:, ct:ct + 1]),
                         ["tb", "ndl"], ["wbase"])
                    S.op("act", lambda: nc.scalar.activation(wcj[half][:], jt[:, half, :], AF.Exp, scale=ndl[:, ct:ct + 1]),
                         ["jt", "ndl"], [f"wcj{half}"])
                s_ = j % 2
                pst = PS[4 + s_]
                ptk = f"ps{4 + s_}"
                mm(pst[:, :], w4b[:, col0:col0 + 128], H3[:, half, j * 512:(j + 1) * 512], True, True, [("H3", half), "w4b"], [ptk])
                S.op("dve", lambda: nc.vector.scalar_tensor_tensor(out=tpf[s_][:], in0=pst[:, :], scalar=wcj[half][:, j:j + 1], in1=wbase[half][:],
                                                                  op0=ALU.mult, op1=ALU.mult), [ptk, f"wcj{half}", "wbase"], [f"tpf{s_}"])
                if half == 1 and j == 0:
                    S.op("dve", lambda: nc.vector.memset(tpf[s_][:, 0:1], 0.0), [f"tpf{s_}"], [f"tpf{s_}"])
                S.op("dve", lambda: nc.vector.tensor_reduce(psum_[:, cidx:cidx + 1], tpf[s_][:], mybir.AxisListType.X, ALU.add,
                                                            apply_absolute_value=True), [f"tpf{s_}"], ["psum_"])
                S.op("act", lambda: nc.scalar.copy(tU[:, half * L + j * 512: half * L + (j + 1) * 512], tpf[s_][:]), [f"tpf{s_}"], [tk])

            def taps_finish(go):
                tU = tapsUL[go % 2]
                tk = f"tapsU{go % 2}"
                S.op("dve", lambda: nc.vector.tensor_reduce(nrm[:, 0:1], psum_[:], mybir.AxisListType.X, ALU.add), ["psum_"], ["nrm"])
                S.op("dve", lambda: nc.vector.reciprocal(nrm[:, 0:1], nrm[:, 0:1]), ["nrm"], ["nrm"])
                S.op("dve", lambda: nc.vector.tensor_scalar(nrm[:, 1:2], nrm[:, 0:1], -1.0, None, ALU.mult), ["nrm"], ["nrm"])
                S.op("dve", lambda: nc.vector.tensor_scalar(tU[:, 0:L], tU[:, 0:L], nrm[:, 0:1], None, ALU.mult), [tk, "nrm"], [tk])
                S.op("act", lambda: nc.scalar.mul(tU[:, L:NFFT], tU[:, L:NFFT], nrm[:, 1:2]), [tk, "nrm"], [tk])
                tv = taps_d.rearrange("c (n1 n2) -> n1 c n2", n2=128)
                for q4 in range(4):
                    dma(taps_d[q4 * 32:(q4 + 1) * 32, :], tU[q4 * 32:(q4 + 1) * 32, :], [tk], [("taps_d", q4)])
                for q4 in range(4):
                    dma(ztmf[:, q4 * 32:(q4 + 1) * 32, :], tv[:, q4 * 32:(q4 + 1) * 32, :], [("taps_d", q4)], [("ztm", q4)])

            for cidx in range(32):
                taps_chunk(0, cidx)
            taps_finish(0)
            for go in range(8):
                def cb_filter(kb, ps, pk, go=go):
                    s_ = kb % 2
                    zv = ps[:, :].rearrange("p (k r c) -> p k r c", k=2, r=2)
                    S.op("act", lambda: nc.scalar.copy(Hst[s_][:, :, 0:2, :], zv), [pk], [f"Hst{s_}"])
                    S.op("dve", lambda: nc.vector.tensor_copy(Hst[s_][:, :, 2, :], zv[:, :, 1, :]), [pk], [f"Hst{s_}"])
                    S.op("dve", lambda: nc.vector.tensor_copy(Hst[s_][:, :, 3, :], zv[:, :, 0, :]), [pk], [f"Hst{s_}"])
                    dma(H_d[go, :, kb * 2:kb * 2 + 2], Hst[s_][:], [f"Hst{s_}"], [("H_d", go)])

                def pre_filter(kb, go=go):
                    if go + 1 < 8:
                        if kb < 16:
                            taps_chunk(go + 1, 2 * kb)
                            taps_chunk(go + 1, 2 * kb + 1)
                        elif kb == 16:
                            taps_finish(go + 1)

                fft_forward(ztmf, 128, ATf, cb_filter, pre_batch=pre_filter)
            S.barrier()
        def rms_apply(src, gvec, dst, scale, nk, tag, sqb, rs, np_=128, ss_ps=0, sqk="sqb", rsk="rs"):
            rms_a(src, nk, tag, sqb, np_, sqk)
            rms_b(src, gvec, dst, scale, nk, tag, sqb, rs, np_, ss_ps, sqk, rsk)

        def rms_a(src, nk, tag, sqb, np_=128, sqk="sqb"):
            S.op("act", lambda: nc.scalar.activation(sqb[0:np_, 0:nk, :], src, AF.Square), [tag + "_src"], [sqk])

        def rms_b(src, gvec, dst, scale, nk, tag, sqb, rs, np_=128, ss_ps=0, sqk="sqb", rsk="rs"):
            for k in range(nk):
                mm(PS[ss_ps][0:np_, :], onesbf[0:np_, 0:np_], sqb[0:np_, k, :], k == 0, k == nk - 1, [sqk, "onesbf"], [f"ps{ss_ps}"])
            S.op("act", lambda: nc.scalar.activation(rs[0:np_, :], PS[ss_ps][0:np_, :], AF.Sqrt, bias=epsT[0:np_, :], scale=scale),
                 [f"ps{ss_ps}", "epsT"], [rsk])
            S.op("dve", lambda: nc.vector.reciprocal(rs[0:np_, :], rs[0:np_, :]), [rsk], [rsk])
            for k in range(nk):
                S.op("dve", lambda k=k: nc.vector.scalar_tensor_tensor(out=dst[:, k, :], in0=src[:, k, :], scalar=gvec[:, k:k + 1],
                                                                      in1=rs[0:np_, :], op0=ALU.mult, op1=ALU.mult),
                     [tag + "_src", rsk, "gvecs"], [tag + "_dst"])

        def phase12():
          with ExitStack() as p12:
            KT = sb("KT", [128, 2, L], BF16, p12)
            Vx = sb("Vx", [128, 64, 2, 64], BF16, p12)
            QT = sb("QT", [128, 4, NTOK], BF16, p12)
            onesb = sb("onesb", [128, 64], BF16, p12)
            S.op("pool", lambda: nc.gpsimd.memset(onesb[:], 1.0), [], ["onesb"])
            with ExitStack() as p1:
                Wb = sb("Wb", [128, 8, 2304], BF16, p1)
                with ExitStack() as p1c:
                    wst = [sb(f"wsi{i}", [128, 8, 384], F32, p1c) for i in range(2)]
                    wv = w_in.rearrange("(k p) n -> p k n", p=128)
                    for bk in range(6):
                        s_ = bk % 2
                        dma(wst[s_][:], wv[:, :, bk * 384:(bk + 1) * 384], [], [f"wst{s_}"])
                        if bk % 2 == 0:
                            S.op("dve", lambda s_=s_, bk=bk: nc.vector.tensor_copy(Wb[:, :, bk * 384:(bk + 1) * 384], wst[s_][:]), [f"wst{s_}"], ["Wb"])
                        else:
                            S.op("act", lambda s_=s_, bk=bk: nc.scalar.copy(Wb[:, :, bk * 384:(bk + 1) * 384], wst[s_][:]), [f"wst{s_}"], ["Wb"])
                    S.barrier()
                xin = [sb(f"xin{i}", [128, 8, 512], F32, p1) for i in range(2)]
                sqbL = [sb(f"sqb{i}", [128, 8, 512], BF16, p1) for i in range(2)]
                rsL = [sb(f"rs{i}", [128, 512], F32, p1) for i in range(2)]
                aTL = [sb(f"aT{i}", [128, 8, 512], BF16, p1) for i in range(2)]
                ust = sb("ust", [128, 12, 512], BF16, p1)
                hsq = sb("hsq", [64, 512], BF16, p1); hrs = sb("hrs", [64, 512], F32, p1); hkn = sb("hkn", [64, 512], F32, p1)
                ht1 = sb("ht1", [64, 512], F32, p1); ht2 = sb("ht2", [64, 512], F32, p1)
                csb = [sb(f"csb{i}", [64, 2, 512], F32, p1) for i in range(2)]
                qstg = [sb(f"qstg{i}", [64, 512], BF16, p1) for i in range(2)]

                def hp1(srcb, gvec):
                    src = PS[srcb][0:64, :]
                    sk_ = f"ps{srcb}"
                    S.op("act", lambda: nc.scalar.activation(hsq[:], src, AF.Square), [sk_], ["hsq"])
                    mm(PS[3][0:64, :], onesbf[0:64, 0:64], hsq[:], True, True, ["hsq", "onesbf"], ["ps3"])
                    S.op("act", lambda: nc.scalar.activation(hrs[:], PS[3][0:64, :], AF.Sqrt, bias=epsT[0:64, :], scale=1.0 / 64.0),
                         ["ps3", "epsT"], ["hrs"])
                    S.op("dve", lambda: nc.vector.reciprocal(hrs[:], hrs[:]), ["hrs"], ["hrs"])
                    S.op("dve", lambda: nc.vector.scalar_tensor_tensor(out=hkn[:], in0=src, scalar=gvec[:, 0:1], in1=hrs[:],
                                                                      op0=ALU.mult, op1=ALU.mult), [sk_, "hrs", "gvecs"], ["hkn"])

                def hp2(cs, cskey, dst, dstkey):
                    mm(PS[3][0:64, :], rot[:], hkn[:], True, True, ["hkn", "rot"], ["ps3"])
                    S.op("pool", lambda: nc.gpsimd.tensor_tensor(ht1[:], hkn[:], cs[:, 0, :], ALU.mult), ["hkn", cskey], ["ht1"])
                    S.op("dve", lambda: nc.vector.tensor_tensor(ht2[:], PS[3][0:64, :], cs[:, 1, :], ALU.mult), ["ps3", cskey], ["ht2"])
                    S.op("pool", lambda: nc.gpsimd.tensor_tensor(dst, ht1[:], ht2[:], ALU.add), ["ht1", "ht2"], [dstkey])

                xv = xT.rearrange("(k p) t -> p k t", p=128)
                xqv = xTq.rearrange("(k p) t -> p k t", p=128)
                Uv = U_d.rearrange("(ct p) t -> p ct t", p=128)
                work = [("seq", j) for j in range(16)] + [("own", j) for j in range(4)]

                def norm_a(w):
                    kind, j = work[w]
                    s_ = w % 2
                    tsl = slice(j * 512, (j + 1) * 512)
                    srcv, cc, ss_ = (xv, c_cos, c_sin) if kind == "seq" else (xqv, c_cosq, c_sinq)
                    dma(xin[s_][:], srcv[:, :, tsl], [], [f"xin{s_}", f"n1{s_}_src"])
                    dma(csb[s_][:, 0, :], cc[:, tsl], [], [f"csb{s_}"])
                    dma(csb[s_][:, 1, :], ss_[:, tsl], [], [f"csb{s_}"])
                    rms_a(xin[s_][:], 8, f"n1{s_}", sqbL[s_], 128, f"sqb{s_}")

                def norm_b(w):
                    s_ = w % 2
                    rms_b(xin[s_][:], g1s, aTL[s_], 1.0 / D, 8, f"n1{s_}", sqbL[s_], rsL[s_], 128, 0, f"sqb{s_}", f"rs{s_}")

                def emit_proj(w, mid_hook):
                    kind, j = work[w]
                    s_ = w % 2
                    aT = aTL[s_]
                    ak = f"n1{s_}_dst"
                    cs, csk = csb[s_], f"csb{s_}"
                    tsl = slice(j * 512, (j + 1) * 512)
                    if kind == "seq":
                        for kvh in range(2):
                            for k in range(8):
                                mm(PS[1 + kvh][0:64, :], Wb[:, k, 512 + kvh * 64:512 + (kvh + 1) * 64], aT[:, k, :], k == 0, k == 7,
                                   ["Wb", ak], [f"ps{1 + kvh}"])
                        for tt in range(4):
                            for k in range(8):
                                mm(PS[4][:, tt * 128:(tt + 1) * 128], aT[:, k, tt * 128:(tt + 1) * 128], Wb[:, k, 640:768], k == 0, k == 7,
                                   ["Wb", ak], ["ps4"])
                        S.op("act", lambda j=j: nc.scalar.copy(Vx[:, j * 4:(j + 1) * 4, :, :],
                                                             PS[4][:, :].rearrange("p (t h d) -> p t h d", t=4, h=2)), ["ps4"], ["Vx"])
                        for ct in range(12):
                            pu, puk = (PS[5], "ps5") if ct % 2 == 0 else (PSX[0], "psb0")
                            for k in range(8):
                                mm(pu[:, :], Wb[:, k, 768 + ct * 128:768 + (ct + 1) * 128], aT[:, k, :], k == 0, k == 7, ["Wb", ak], [puk])
                            if ct % 2 == 0:
                                S.op("act", lambda ct=ct, pu=pu: nc.scalar.copy(ust[:, ct, :], pu[:, :]), [puk], ["ust"])
                            else:
                                S.op("dve", lambda ct=ct, pu=pu: nc.vector.tensor_copy(ust[:, ct, :], pu[:, :]), [puk], ["ust"])
                            if ct == 1:
                                hp1(1, gks)
                            if ct == 3:
                                hp2(cs, csk, KT[0:64, 0, tsl], "KT")
                                hp1(2, gks)
                            if ct == 5:
                                hp2(cs, csk, KT[0:64, 1, tsl], "KT")
                            if ct == 7:
                                mid_hook()
                        dma(Uv[:, :, tsl], ust[:], ["ust"], ["U_d"])
                    else:
                        def qdst(h):
                            if h % 2 == 0:
                                return QT[0:64, h // 2, tsl], "QT", None
                            qs_ = (h // 2) % 2
                            return qstg[qs_][:], f"qstg{qs_}", qs_

                        def qproj(h):
                            hb = 1 + h % 2
                            for k in range(8):
                                mm(PS[hb][0:64, :], Wb[:, k, h * 64:(h + 1) * 64], aT[:, k, :], k == 0, k == 7, ["Wb", ak], [f"ps{hb}"])

                        qproj(0)
                        for h in range(8):
                            hp1(1 + h % 2, gqs)
                            if h + 1 < 8:
                                qproj(h + 1)
                            if h == 3:
                                mid_hook()
                            d_, dk_, qs_ = qdst(h)
                            hp2(cs, csk, d_, dk_)
                            if qs_ is not None:
                                dma(QT[64:128, h // 2, tsl], qstg[qs_][:], [f"qstg{qs_}"], ["QT2"])

                norm_a(0)
                norm_b(0)
                for w in range(len(work)):
                    if w + 1 < len(work):
                        norm_a(w + 1)
                        emit_proj(w, lambda w=w: norm_b(w + 1))
                    else:
                        emit_proj(w, lambda: None)
                dma(KT[64:128, :, :], KT[0:64, :, :], ["KT"], ["KT2"])
                S.barrier()
            if 2 in phases:
              with ExitStack() as p2:
                pb = [sb(f"pb{i}", [128, 1024], BF16, p2) for i in range(2)]
                cwst = [sb(f"wst{i}", [128, 8, 512], F32, p2) for i in range(2)]
                cwbf = [sb(f"wbf{i}", [128, 8, 512], BF16, p2) for i in range(2)]
                rden = [sb(f"rden{i}", [128, 512], F32, p2) for i in range(2)]
                ast = [sb(f"ast{i}", [128, 512], BF16, p2) for i in range(2)]
                n_ = 0
                for qc in range(4):
                    qsl = slice(qc * 512, (qc + 1) * 512)
                    for hp in range(4):
                        kvh = hp // 2
                        (OB, obk), (DB, dbk) = ((PS[4], "ps4"), (PS[5], "ps5")) if n_ % 2 == 0 else ((PSX[0], "psb0"), (PSX[1], "psb1"))
                        emit_cast(qc * 4 + hp, cwst, cwbf)
                        for step in range(64 + 1):
                            if step < 64:
                                kt = step
                                sl = kt % 2
                                ksl = slice(kt * 128, (kt + 1) * 128)
                                mm(PS[2 * sl][:, :], KT[0:64, kvh, ksl], QT[0:64, hp, qsl], True, True, ["KT", "QT"], [f"ps{2 * sl}"])
                                mm(PS[2 * sl + 1][:, :], KT[64:128, kvh, ksl], QT[64:128, hp, qsl], True, True, ["KT2", "QT2"], [f"ps{2 * sl + 1}"])
                                S.op("act", lambda sl=sl: nc.scalar.activation(pb[sl][:], PSW[sl][:, :], AF.Exp, scale=0.125),
                                     [f"ps{2 * sl}", f"ps{2 * sl + 1}"], [f"pb{sl}"])
                            kt = step - 1
                            if kt >= 0:
                                sl = kt % 2
                                st_, sp_ = kt == 0, kt == 63
                                pA = pb[sl][:, 0:512]
                                pB = pb[sl][:, 512:1024]
                                vv = Vx[:, kt, kvh, :]
                                S.op("pe", lambda vv=vv, pA=pA, st_=st_, sp_=sp_, OB=OB: nc.tensor.matmul(OB[0:64, :], vv, pA, start=st_, stop=sp_, tile_position=(0, 0)),
                                     ["Vx", f"pb{sl}"], [obk])
                                S.op("pe", lambda vv=vv, pB=pB, st_=st_, sp_=sp_, OB=OB: nc.tensor.matmul(OB[64:128, :], vv, pB, start=st_, stop=sp_, tile_position=(0, 64)),
                                     ["Vx", f"pb{sl}"], [obk])
                                S.op("pe", lambda pA=pA, st_=st_, sp_=sp_, DB=DB: nc.tensor.matmul(DB[0:64, :], onesb[:], pA, start=st_, stop=sp_, tile_position=(0, 0)),
                                     ["onesb", f"pb{sl}"], [dbk])
                                S.op("pe", lambda pB=pB, st_=st_, sp_=sp_, DB=DB: nc.tensor.matmul(DB[64:128, :], onesb[:], pB, start=st_, stop=sp_, tile_position=(0, 64)),
                                     ["onesb", f"pb{sl}"], [dbk])
                        a_ = n_ % 2
                        S.op("dve", lambda DB=DB, a_=a_: nc.vector.reciprocal(rden[a_][:], DB[:, :]), [dbk], [f"rden{a_}"])
                        S.op("dve", lambda a_=a_, OB=OB: nc.vector.tensor_tensor(ast[a_][:], OB[:, :], rden[a_][:], ALU.mult), [obk, f"rden{a_}"], [f"ast{a_}"])
                        dma(att_d[hp * 128:(hp + 1) * 128, qsl], ast[a_][:], [f"ast{a_}"], ["att_d"])
                        n_ += 1
                S.barrier()

        def phase3():
          with ExitStack() as p3:
            load_fft_consts(p3)
            finv = FC["finv"]
            M2s = sb("M2s", [128, 128, 64], BF16, p3)
            M2qs = sb("M2qs", [128, 128, 16], BF16, p3)
            dma(M2s[:], c_M2, [], ["M2s"])
            dma(M2qs[:], c_M2q, [], ["M2qs"])
            RA = sb("RA", [128, 192 * 128 + 8], BF16, p3)
            RB = sb("RB", [128, 128 * 128], BF16, p3)
            RC = sb("RC", [128, 128 * 128], BF16, p3)
            vt = sb("vt", [128, L], BF16, p3)
            x1t = sb("x1t", [128, L], BF16, p3)
            x2q = sb("x2q", [128, NTOK], BF16, p3)
            zzq = sb("zzq", [128, NTOK], BF16, p3)
            hyst = sb("hyst", [128, NTOK], BF16, p3)
            Hc = [sb(f"Hc{i}", [128, 2, 4, 128], BF16, p3) for i in range(4)]
            Pb = [sb(f"Pb{i}", [128, 2, 2, 128], BF16, p3) for i in range(2)]
            Qb = [sb(f"Qb{i}", [128, 2, 2, 128], BF16, p3) for i in range(2)]
            gt = [sb(f"gt{i}", [128, 512], F32, p3) for i in range(2)]
            whi = sb("whi", [128, 9], BF16, p3)
            whf = sb("whf", [128, 9], F32, p3)
            wlf = sb("wlf", [128, 9], F32, p3)
            sct = gt
            uraw = RA[:, 0:3 * 8194].rearrange("p (i t) -> p i t", i=3)
            AT = RA[:, 0:192 * 128].rearrange("p (j c) -> p j c", c=128)
            ztm = RB[:, :].rearrange("p (c n) -> p c n", n=128)
            Eb = RB[:, :].rearrange("p (n c) -> p n c", c=128)
            x2t = RC[:, 0:L]
            ET = RC[:, :].rearrange("p (c k) -> p c k", k=128)
            RAK, RBK, RCK = "RA", "RB", "RC"

            def masked_quarter(dst, src, dkey, skey):
                S.op("dve", lambda: nc.vector.tensor_scalar(dst[:], src[:, 0:NTOK], qms[:, 0:1], None, ALU.mult), [skey, "qms"], [dkey])
                for q in range(1, 4):
                    S.op("dve", lambda q=q: nc.vector.scalar_tensor_tensor(out=dst[:], in0=src[:, q * NTOK:(q + 1) * NTOK], scalar=qms[:, q:q + 1],
                                                                          in1=dst[:], op0=ALU.mult, op1=ALU.add), [skey, "qms", dkey], [dkey])

            RBQ = [(RBK, q4) for q4 in range(4)]

            def bounce_to_ztm(src, skey):
                zv = zcm_d.rearrange("c (n1 n2) -> n1 c n2", n2=128)
                for q4 in range(4):
                    dma(zcm_d[q4 * 32:(q4 + 1) * 32, :], src[q4 * 32:(q4 + 1) * 32, :], [skey], [("zcm_d", q4)])
                for q4 in range(4):
                    dma(ztm[0:64, q4 * 32:(q4 + 1) * 32, :], zv[:, q4 * 32:(q4 + 1) * 32, :], [("zcm_d", q4)], [(RBK, q4)])

            def conv_core(go):
                def load_h(kb):
                    dma(Hc[kb % 4][:], H_d[go, :, kb * 2:kb * 2 + 2], [("H_d", go)], [f"Hc{kb % 4}"])

                def pre(kb):
                    if kb == 0:
                        load_h(0)
                        load_h(1)
                    if kb + 2 < 32:
                        load_h(kb + 2)

                def cb(kb, ps, pk):
                    s_ = kb % 4
                    y_ = kb % 2
                    zv = ps[:, :].rearrange("p (k r c) -> p k r c", k=2, r=2)
                    S.op("dve", lambda: nc.vector.tensor_tensor(Pb[y_][:], zv, Hc[s_][:, :, 0:2, :], ALU.mult), [pk, f"Hc{s_}"], [f"Pb{y_}"])
                    S.op("dve", lambda: nc.vector.tensor_tensor(Qb[y_][:], zv, Hc[s_][:, :, 2:4, :], ALU.mult), [pk, f"Hc{s_}"], [f"Qb{y_}"])

                def cb2(kb):
                    s_ = kb % 4
                    y_ = kb % 2
                    pe_ = PS[4 + kb % 2]
                    pek = f"ps{4 + kb % 2}"
                    er = pe_[:, 0:256].rearrange("p (k c) -> p k c", k=2)
                    ei = pe_[:, 256:512].rearrange("p (k c) -> p k c", k=2)
                    p0, p1 = Pb[y_][:, :, 0, :], Pb[y_][:, :, 1, :]
                    q0, q1 = Qb[y_][:, :, 0, :], Qb[y_][:, :, 1, :]
                    rk = [f"Pb{y_}", f"Qb{y_}", "finv"]
                    FR, FI, NFI, NFR = finv[:, 0, :], finv[:, 1, :], finv[:, 2, :], finv[:, 3, :]
                    mm(er, FR, p0, True, False, rk, [pek]); mm(er, NFR, p1, False, False, rk, [pek])
                    mm(er, NFI, q0, False, False, rk, [pek]); mm(er, NFI, q1, False, True, rk, [pek])
                    mm(ei, FI, p0, True, False, rk, [pek]); mm(ei, NFI, p1, False, False, rk, [pek])
                    mm(ei, FR, q0, False, False, rk, [pek]); mm(ei, FR, q1, False, True, rk, [pek])
                    S.op("act", lambda: nc.scalar.copy(ET[:, :, kb * 2:kb * 2 + 2], pe_[:, 0:256].rearrange("p (k c) -> p c k", k=2)), [pek], [RCK])
                    S.op("act", lambda: nc.scalar.copy(ET[:, :, 64 + kb * 2:64 + kb * 2 + 2], pe_[:, 256:512].rearrange("p (k c) -> p c k", k=2)),
                         [pek], [RCK])
                fft_forward(ztm, 64, AT, cb, zkey=RBK, akey=RAK, pre_batch=pre, cb_late=cb2)
                TB = [(PSB[0], "psb0"), (PSB[1], "psb1"), (PSW[0].bitcast(BF16), "ps0"), (PSW[1].bitcast(BF16), "ps2")]
                for c4 in range(32):
                    pb_, pbk = TB[c4 % 4]
                    for cc in range(4):
                        c = c4 * 4 + cc
                        S.op("pe", lambda c=c, cc=cc, pb_=pb_: nc.tensor.transpose(pb_[:, cc * 128:(cc + 1) * 128], ET[:, c, :], ident[:]),
                             [RCK, "ident"], [pbk])
                    src = pb_[:, 0:512].rearrange("p (cc n) -> p n cc", cc=4)
                    dst = Eb[:, :, c4 * 4:c4 * 4 + 4]
                    if c4 % 2 == 0:
                        S.op("act", lambda src=src, dst=dst: nc.scalar.copy(dst, src), [pbk], RBQ)
                    else:
                        S.op("dve", lambda src=src, dst=dst: nc.vector.tensor_copy(dst, src), [pbk], RBQ)

            Udv = U_d
            for g in range(4):
                for i in range(3):
                    dma(uraw[:, i, 1:L + 1], Udv[i * 512 + g * 128:i * 512 + (g + 1) * 128, :], ["U_d"], [RAK])
                S.op("pool", lambda: nc.gpsimd.memset(uraw[:, :, 0:1], 0.0), [RAK], [RAK])
                S.op("pool", lambda: nc.gpsimd.memset(uraw[:, :, L + 1:L + 2], 0.0), [RAK], [RAK])
                DG = hyst[:, 0:9 * 128].rearrange("p (i m) -> p i m", m=128)
                DGL = zzq[:, 0:9 * 128].rearrange("p (i m) -> p i m", m=128)
                wsel = cws[:, g:12:4, :]
                S.op("dve", lambda wsel=wsel: nc.vector.tensor_copy(whi[:].rearrange("p (i j) -> p i j", j=3), wsel), ["cws"], ["whi"])
                S.op("dve", lambda: nc.vector.tensor_copy(whf[:], whi[:]), ["whi"], ["whf"])
                S.op("dve", lambda wsel=wsel: nc.vector.tensor_tensor(wlf[:].rearrange("p (i j) -> p i j", j=3), wsel,
                                                                     whf[:].rearrange("p (i j) -> p i j", j=3), ALU.subtract), ["cws", "whf"], ["wlf"])
                for idx in range(9):
                    S.op("dve", lambda idx=idx: nc.vector.tensor_scalar(DG[:, idx, :], ident[:], whf[:, idx:idx + 1], None, ALU.mult), ["whf", "ident"], ["hyst"])
                    S.op("dve", lambda idx=idx: nc.vector.tensor_scalar(DGL[:, idx, :], ident[:], wlf[:, idx:idx + 1], None, ALU.mult), ["wlf", "ident"], ["zzq"])
                nq = 0
                for i, (dst, dkey) in enumerate(((vt, "vt"), (x1t, "x1t"), (x2t, RCK))):
                    ct = i * 4 + g
                    for q in range(16):
                        o0 = q * 512
                        ps = PS[nq % 2]
                        pk = f"ps{nq % 2}"
                        for j in range(3):
                            mm(ps[:, :], DG[:, i * 3 + j, :], uraw[:, i, o0 + j:o0 + j + 512], j == 0, False, ["hyst", RAK], [pk])
                        for j in range(3):
                            mm(ps[:, :], DGL[:, i * 3 + j, :], uraw[:, i, o0 + j:o0 + j + 512], False, j == 2, ["zzq", RAK], [pk])
                        if nq % 2 == 0:
                            S.op("act", lambda ps=ps, dst=dst, o0=o0, ct=ct: nc.scalar.activation(dst[:, o0:o0 + 512], ps[:, :], AF.Identity, bias=cbs[:, ct:ct + 1]),
                                 [pk, "cbs"], [dkey])
                        else:
                            S.op("dve", lambda ps=ps, dst=dst, o0=o0, ct=ct: nc.vector.tensor_scalar(dst[:, o0:o0 + 512], ps[:, :], cbs[:, ct:ct + 1], None, ALU.add),
                                 [pk, "cbs"], [dkey])
                        nq += 1
                masked_quarter(x2q, x2t, "x2q", RCK)
                snap("vt0", vt[:], "vt", L)
                snap("x1t0", x1t[:], "x1t", L)
                bounce_to_ztm(vt, "vt")
                conv_core(g * 2 + 0)
                vt3 = vt[:, :].rearrange("p (n1 n2) -> p n1 n2", n2=128)
                x13 = x1t[:, :].rearrange("p (n1 n2) -> p n1 n2", n2=128)
                for nb in range(16):
                    ps = PS[nb % 2]
                    pk = f"ps{nb % 2}"
                    for j in range(8):
                        n2 = nb * 8 + j
                        mm(ps[:, j * 64:(j + 1) * 64], Eb[:, n2, :], M2s[:, n2, :], True, True, RBQ + ["M2s"], [pk])
                    psv = ps[:, :].rearrange("p (j n) -> p n j", j=8)
                    gtv = gt[nb % 2][:, :].rearrange("p (n j) -> p n j", j=8)
                    gk_ = f"gt{nb % 2}"
                    S.op("dve", lambda psv=psv, gtv=gtv, nb=nb, g=g: nc.vector.scalar_tensor_tensor(out=gtv, in0=vt3[:, :, nb * 8:(nb + 1) * 8],
                                                                                             scalar=skps[:, g:g + 1], in1=psv, op0=ALU.mult, op1=ALU.add),
                         [pk, "vt", "skps"], [gk_])
                    ge_ = "pool" if nb % 2 == 0 else "dve"
                    S.op(ge_, lambda gtv=gtv, nb=nb, ge_=ge_: V(ge_).tensor_tensor(vt3[:, :, nb * 8:(nb + 1) * 8], gtv, x13[:, :, nb * 8:(nb + 1) * 8], ALU.mult),
                         [gk_, "x1t"], ["vt"])
                snap("zz0", vt[:], "vt", L)
                masked_quarter(zzq, vt, "zzq", "vt")
                bounce_to_ztm(vt, "vt")
                conv_core(g * 2 + 1)
                zq3 = zzq[:, :].rearrange("p (n1 n2) -> p n1 n2", n2=128)
                xq3 = x2q[:, :].rearrange("p (n1 n2) -> p n1 n2", n2=128)
                hy3 = hyst[:, :].rearrange("p (n1 n2) -> p n1 n2", n2=128)
                for nb in range(4):
                    ps = PS[nb % 2]
                    pk = f"ps{nb % 2}"
                    for j in range(32):
                        n2 = nb * 32 + j
                        mm(ps[:, j * 16:(j + 1) * 16], Eb[:, n2, :], M2qs[:, n2, :], True, True, RBQ + ["M2qs"], [pk])
                    psv = ps[:, :].rearrange("p (j n) -> p n j", j=32)
                    gtv = gt[nb % 2][:, :].rearrange("p (n j) -> p n j", j=32)
                    gk_ = f"gt{nb % 2}"
                    S.op("dve", lambda psv=psv, gtv=gtv, nb=nb, g=g: nc.vector.scalar_tensor_tensor(out=gtv, in0=zq3[:, :, nb * 32:(nb + 1) * 32],
                                                                                             scalar=skps[:, 4 + g:5 + g], in1=psv, op0=ALU.mult, op1=ALU.add),
                         [pk, "zzq", "skps"], [gk_])
                    S.op("pool", lambda gtv=gtv, nb=nb: nc.gpsimd.tensor_tensor(hy3[:, :, nb * 32:(nb + 1) * 32], gtv, xq3[:, :, nb * 32:(nb + 1) * 32], ALU.mult),
                         [gk_, "x2q"], ["hyst"])
                dma(hy_d[g * 128:(g + 1) * 128, :], hyst[:], ["hyst"], ["hy_d"])
            S.barrier()

        def phase4():
          with ExitStack() as p4:
            WoA = sb("WoA", [128, 4, D], BF16, p4)
            WoH = sb("WoH", [128, 4, D], BF16, p4)
            with ExitStack() as p4c:
                wsa = sb("wsa", [128, 4, D], F32, p4c)
                wsh = sb("wsh", [128, 4, D], F32, p4c)
                dma(wsa[:], w_out[0:512, :].rearrange("(a p) n -> p a n", p=128), [], ["wsa"])
                dma(wsh[:], w_out[512:1024, :].rearrange("(g p) n -> p g n", p=128), [], ["wsh"])
                S.op("dve", lambda: nc.vector.tensor_copy(WoA[:], wsa[:]), ["wsa"], ["WoA"])
                S.op("pool", lambda: nc.gpsimd.tensor_copy(WoH[:], wsh[:]), ["wsh"], ["WoH"])
                S.barrier()
            attcL = [sb(f"attc{i}", [128, 4, 512], BF16, p4) for i in range(2)]
            hycL = [sb(f"hyc{i}", [128, 4, 512], BF16, p4) for i in range(2)]
            sqb = sb("sqb4", [128, 8, 512], BF16, p4)
            rs = sb("rs4", [128, 512], F32, p4)
            sqbM = sb("sqbM", [128, 8, 512], BF16, p4)
            rsM = sb("rsM", [128, 512], F32, p4)
            mixAL = [sb(f"mixA{i}", [128, 4, 512], BF16, p4) for i in range(2)]
            mixHL = [sb(f"mixH{i}", [128, 4, 512], BF16, p4) for i in range(2)]
            h1L = [sb(f"h1_{i}", [128, 8, 512], F32, p4) for i in range(2)]
            mT = sb("mT", [128, 8, 512], BF16, p4)
            Wmi_s = [sb(f"Wmi_s{i}", [128, 8, 1024], BF16, p4) for i in range(2)]
            Wmo_s = [sb(f"Wmo_s{i}", [128, 32, 128], BF16, p4) for i in range(2)]
            fT = sb("fT", [128, 32, 512], BF16, p4)
            rb = [sb(f"rb{i}", [128, 512], F32, p4) for i in range(2)]
            xqv = xTq.rearrange("(k p) t -> p k t", p=128)
            Wmiv = Wmi_d.rearrange("(k p) n -> p k n", p=128)
            outv = outT.rearrange("(k p) t -> p k t", p=128)
            cnt = {"nw": 0, "nwo": 0}

            def hk(j):
                s2 = j % 2
                return [f"h1_{s2}", f"n2{s2}_src", f"nf{s2}_src"]

            def mix_prologue(j):
                s2 = j % 2
                tq = slice(j * 512, (j + 1) * 512)
                dma(attcL[s2][:], att_d[:, tq].rearrange("(a p) t -> p a t", p=128), ["att_d"], [f"attc{s2}", f"na{s2}_src"])
                dma(hycL[s2][:], hy_d[:, tq].rearrange("(g p) t -> p g t", p=128), ["hy_d"], [f"hyc{s2}", f"nh{s2}_src"])
                dma(h1L[s2][:], xqv[:, :, tq], [], hk(j))
                rms_apply(attcL[s2][:], gas, mixAL[s2], 1.0 / 512.0, 4, f"na{s2}", sqbM, rsM, sqk="sqbM", rsk="rsM")
                rms_apply(hycL[s2][:], ghs, mixHL[s2], 1.0 / 512.0, 4, f"nh{s2}", sqbM, rsM, sqk="sqbM", rsk="rsM")

            def outproj(j):
                s2 = j % 2
                h1 = h1L[s2]
                for ct in range(8):
                    csl = slice(ct * 128, (ct + 1) * 128)
                    for h in range(4):
                        mm(PS[1][:, :], WoA[:, h, csl], mixAL[s2][:, h, :], h == 0, False, ["WoA", f"na{s2}_dst"], ["ps1"])
                    for g in range(4):
                        mm(PS[1][:, :], WoH[:, g, csl], mixHL[s2][:, g, :], False, g == 3, ["WoH", f"nh{s2}_dst"], ["ps1"])
                    S.op("dve", lambda ct=ct, h1=h1: nc.vector.tensor_tensor(h1[:, ct, :], h1[:, ct, :], PS[1][:, :], ALU.add), ["ps1", f"h1_{s2}"], hk(j))

            mix_prologue(0)
            outproj(0)
            for j in range(4):
                tsl = slice(j * 512, (j + 1) * 512)
                s2 = j % 2
                h1 = h1L[s2]
                rms_apply(h1[:], g2s, mT, 1.0 / D, 8, f"n2{s2}", sqb, rs)
                for pc in range(4):
                    s_ = cnt["nw"] % 2
                    dma(Wmi_s[s_][:], Wmiv[:, :, pc * 1024:(pc + 1) * 1024], ["Wmid"], [f"Wmi_s{s_}"])
                    cnt["nw"] += 1
                    for ft in range(8):
                        f = pc * 8 + ft
                        pf = PS[2 + f % 2]
                        pfk = f"ps{2 + f % 2}"
                        for k in range(8):
                            mm(pf[:, :], Wmi_s[s_][:, k, ft * 128:(ft + 1) * 128], mT[:, k, :], k == 0, k == 7, [f"Wmi_s{s_}", f"n2{s2}_dst"], [pfk])
                        r_ = f % 2
                        S.op("act", lambda pf=pf, r_=r_: nc.scalar.activation(rb[r_][:], pf[:, :], AF.Relu), [pfk], [f"rb{r_}"])
                        e = ("pool", "dve")[f % 2]
                        S.op(e, lambda e=e, r_=r_, f=f: V(e).tensor_tensor(fT[:, f, :], rb[r_][:], rb[r_][:], ALU.mult), [f"rb{r_}"], ["fT"])
                    if pc == 1 and j + 1 < 4:
                        mix_prologue(j + 1)
                for ct in range(8):
                    s_ = cnt["nwo"] % 2
                    dma(Wmo_s[s_][:], Wmo_d[ct], ["Wmod"], [f"Wmo_s{s_}"])
                    cnt["nwo"] += 1
                    po = PS[4 + ct % 2]
                    pok = f"ps{4 + ct % 2}"
                    for f in range(32):
                        mm(po[:, :], Wmo_s[s_][:, f, :], fT[:, f, :], f == 0, f == 31, [f"Wmo_s{s_}", "fT"], [pok])
                    S.op("dve", lambda ct=ct, po=po, h1=h1: nc.vector.tensor_tensor(h1[:, ct, :], h1[:, ct, :], po[:, :], ALU.add), [pok, f"h1_{s2}"], hk(j))
                rms_a(h1[:], 8, f"nf{s2}", sqb, 128, "sqb")
                if j + 1 < 4:
                    outproj(j + 1)
                rms_b(h1[:], gfs, h1, 1.0 / D, 8, f"nf{s2}", sqb, rs, 128, 0, "sqb", "rs")
                dma(outv[:, :, tsl], h1[:], [f"h1_{s2}", f"nf{s2}_dst"], ["outT"])
            S.barrier()

        if 0 in phases:
            phase0()
        if 1 in phases:
            phase12()
        if 3 in phases:
            phase3()
        if 4 in phases:
            phase4()
        if debug:
            with ExitStack() as pd:
                for name in debug:
                    if name not in ("att_d", "hy_d", "U_d", "H_d0"):
                        continue
                    src_, shp = {"att_d": (att_d, [512, NTOK]), "hy_d": (hy_d, [512, NTOK]), "U_d": (U_d[0:512, :], [512, L]),
                                 "H_d0": (H_d[0:4, :, 0:4].rearrange("a p k r c -> (a p) (k r c)"), [512, 2048])}[name]
                    dbo = nc.dram_tensor("dbg_" + name, shp, BF16, kind="ExternalOutput").ap()
                    dbs = sb("dbs_" + name, [128, 4, shp[1]], BF16, pd)
                    dma(dbs[:], src_.rearrange("(a p) t -> p a t", p=128), [name], ["dbs_" + name])
                    dma(dbo.rearrange("(a p) t -> p a t", p=128), dbs[:], ["dbs_" + name], ["dbg_" + name])
                S.barrier()
        S.barrier()
    return nc, S


def _core_inputs(inp, core):
    T = _tables()
    b, r = core // 4, core % 4
    f = lambda a: np.ascontiguousarray(np.asarray(a, dtype=np.float32))
    x = np.asarray(inp["x"], dtype=np.float32)
    xT = np.ascontiguousarray(x[b].T)
    tq = slice(r * NTOK, (r + 1) * NTOK)
    qm = np.zeros((128, 4), np.float32)
    qm[:, r] = 1.0
    m = {
        "xT": xT, "xTq": np.ascontiguousarray(xT[:, tq]),
        "w_in": f(inp["w_in"][0]), "w_out": f(inp["w_out"][0]), "w_mi": f(inp["w_mlp_in"][0]), "w_mo": f(inp["w_mlp_out"][0]),
        "g1": f(np.asarray(inp["norm1_g"][0]).reshape(8, 128).T), "g2": f(np.asarray(inp["norm2_g"][0]).reshape(8, 128).T),
        "gf": f(np.asarray(inp["final_g"]).reshape(8, 128).T),
        "gq": f(np.asarray(inp["q_norm_g"][0]).reshape(64, 1)), "gk": f(np.asarray(inp["k_norm_g"][0]).reshape(64, 1)),
        "ga": f(np.asarray(inp["attn_out_g"][0]).reshape(4, 128).T), "gh": f(np.asarray(inp["hy_out_g"][0]).reshape(4, 128).T),
        "cw": f(np.asarray(inp["hy_conv_w"][0]).T.reshape(12, 128, 3).transpose(1, 0, 2)),
        "cb": f(np.asarray(inp["hy_conv_b"][0]).reshape(12, 128).T),
        "fw1": f(inp["filt_w1"][0]), "fw2": f(inp["filt_w2"][0]), "fw3": f(inp["filt_w3"][0]), "fw4": f(inp["filt_w4"][0]),
        "fb": f(np.stack([np.asarray(inp["filt_b1"][0]), np.asarray(inp["filt_b2"][0]), np.asarray(inp["filt_b3"][0])], axis=1)),
        "ffreq": f(np.asarray(inp["filt_freq"][0]).reshape(64, 1)),
        "fdel": f(np.asarray(inp["filt_deltas"][0]).reshape(16, 128).T),
        "skp": f(np.asarray(inp["hy_skip_d"][0]).reshape(2, 4, 128).transpose(2, 0, 1).reshape(128, 8)),
        "qmask": qm,
        "c_cos": T["cosT"], "c_sin": T["sinT"],
        "c_cosq": np.ascontiguousarray(T["cosT"][:, tq]), "c_sinq": np.ascontiguousarray(T["sinT"][:, tq]),
        "c_zT": np.ascontiguousarray(np.stack([T["zT"], T["zrT"]])), "c_t": np.ascontiguousarray(np.stack([T["t"], T["trev"]])),
        "c_tb": T["tb"], "c_jt": T["jt"],
        "c_fpack": T["fpack"], "c_G": T["G"], "c_finv": T["finv"], "c_M2": T["M2"],
        "c_M2q": np.ascontiguousarray(T["M2"][:, :, r * 16:(r + 1) * 16]),
        "c_rot": T["rot"], "c_sel": T["sel"], "c_ones": T["ones"], "c_ident": T["ident"],
    }
    return m


def _emit(nc, S):
    from contextlib import ExitStack
    st = ExitStack()
    sems = {e: [st.enter_context(nc.semaphore(f"s_{e}{i}")) for i in range(4)] for e in S.CE}
    dsems = [st.enter_context(nc.semaphore(f"d{i}")) for i in range(S.NDS)]
    info = S.emit(sems, dsems)
    return st, info


def kernel(**inputs):
    nc, S = build_program()
    st, _ = _emit(nc, S)
    with st:
        in_maps = [_core_inputs(inputs, c) for c in range(8)]
        res = run_bass_kernel_spmd(nc, in_maps, core_ids=list(range(8)))
    out = np.empty((NB, L, D), np.float32)
    for c in range(8):
        b, r = c // 4, c % 4
        out[b, r * NTOK:(r + 1) * NTOK, :] = np.asarray(res.results[c]["outT"]).T
    return out
```

```python
import math
import numpy as np
import ml_dtypes
import concourse.bass as bass
import concourse.mybir as mybir
from concourse.bass_utils import run_bass_kernel_spmd

F32 = mybir.dt.float32
BF16 = mybir.dt.bfloat16
ALU = mybir.AluOpType
AF = mybir.ActivationFunctionType
bf16_np = ml_dtypes.bfloat16

D = 1024
L = 8192
NB = 2
NTOK = 2048
HD = 64
NQH = 8
NKV = 2
HYW = 512
DFF = 4096
EPS = 1e-6
NFFT = 2 * L
TWO_PI = 2.0 * math.pi
MAGIC = 12582912.0


class Op:
    __slots__ = ("eng", "fn", "reads", "writes", "dma", "idx", "gid", "waits", "signal", "clock",
                 "sem", "val")


class Sched:
    CE = ("pe", "act", "dve", "pool", "sp")
    EPOCH = 12000
    NDS = 40

    def __init__(self, nc):
        self.nc = nc
        self.E = {"pe": nc.tensor, "act": nc.scalar, "dve": nc.vector, "pool": nc.gpsimd,
                  "sp": nc.sync}
        self.ops = []
        self.last_w = {}
        self.readers = {}
        self.known = {e: {} for e in self.CE}
        self.known_dma = {e: set() for e in self.CE}
        self.cnt = {e: 0 for e in self.CE}
        self.last_op = {e: None for e in self.CE}
        self.live_dma = []

    def op(self, eng, fn, reads=(), writes=(), dma=False):
        o = Op()
        o.eng, o.fn, o.reads, o.writes, o.dma = eng, fn, tuple(reads), tuple(writes), dma
        o.signal = dma
        o.sem = None
        o.val = None
        o.gid = len(self.ops)
        deps = {}
        for k in o.reads:
            w = self.last_w.get(k)
            if w is not None:
                deps[w.gid] = w
        for k in o.writes:
            w = self.last_w.get(k)
            if w is not None:
                deps[w.gid] = w
            for r in self.readers.get(k, ()):
                deps[r.gid] = r
        self._finish(o, deps.values())
        for k in o.writes:
            self.last_w[k] = o
            self.readers[k] = []
        for k in o.reads:
            lst = self.readers.setdefault(k, [])
            if not dma:
                lst[:] = [r for r in lst if r.dma or r.eng != eng]
            lst.append(o)
        return o

    def _finish(self, o, deps):
        eng = o.eng
        kn = self.known[eng]
        kd = self.known_dma[eng]
        waits = []
        best = {}
        rset = set(o.reads)
        for d in deps:
            if d.dma:
                if d.gid not in kd:
                    waits.append(d)
                    kd.add(d.gid)
                continue
            if d.eng == eng:
                if eng in ("pe", "sp"):
                    continue
            if kn.get(d.eng, -1) >= d.idx:
                continue
            b = best.get(d.eng)
            if b is None or b.idx < d.idx:
                best[d.eng] = d
        for d in best.values():
            waits.append(d)
            if kn.get(d.eng, -1) < d.idx:
                kn[d.eng] = d.idx
            for e2, i2 in d.clock.items():
                if kn.get(e2, -1) < i2:
                    kn[e2] = i2
        for d in waits:
            d.signal = True
        o.waits = waits
        o.idx = self.cnt[eng]
        self.cnt[eng] += 1
        o.clock = dict(kn)
        self.ops.append(o)
        if o.fn is not None:
            self.last_op[eng] = o
        if o.dma:
            self.live_dma.append(o)

    def barrier(self):
        lasts = [self.last_op[e] for e in self.CE if self.last_op[e] is not None]
        dmas = list(self.live_dma)
        for e in self.CE:
            o = Op()
            o.eng, o.fn, o.reads, o.writes, o.dma = e, None, (), (), False
            o.signal = False
            o.sem = None
            o.val = None
            o.gid = len(self.ops)
            deps = [d for d in lasts if not d.dma and d.eng != e] + dmas
            self._finish(o, deps)
        self.live_dma = []

    def emit(self, sems, dsems):
        sig = {e: 0 for e in self.CE}
        nd = 0
        for o in self.ops:
            E = self.E[o.eng]
            for d in o.waits:
                E.wait_ge(d.sem, d.val)
            if o.fn is None:
                continue
            if o.dma:
                s = dsems[nd % self.NDS]
                rnd = nd // self.NDS
                if rnd > 0:
                    E.wait_ge(s, 16 * rnd)
                ins = o.fn()
                ins.then_inc(s, 16)
                o.sem, o.val = s, 16 * (rnd + 1)
                nd += 1
            else:
                ins = o.fn()
                if o.signal:
                    c = sig[o.eng]
                    o.sem = sems[o.eng][c // self.EPOCH]
                    o.val = c % self.EPOCH + 1
                    ins.then_inc(o.sem, 1)
                    sig[o.eng] = c + 1
        return sig, nd


_TABLES = None


def _rope_tables():
    rows = L // 64
    row = np.repeat(np.arange(rows, dtype=np.float32), 64)
    col = np.tile(np.arange(64, dtype=np.float32), rows)
    half = HD // 2
    inv_freq = (np.float32(10000.0) ** (-np.arange(0, half, 2, dtype=np.float32) / np.float32(half))).astype(np.float32)
    ang_r = (row[:, None] * inv_freq[None, :]).astype(np.float32)
    ang_c = (col[:, None] * inv_freq[None, :]).astype(np.float32)
    cos_r, sin_r = np.cos(ang_r), np.sin(ang_r)
    cos_c, sin_c = np.cos(ang_c), np.sin(ang_c)
    cosT = np.concatenate([cos_r.T, cos_r.T, cos_c.T, cos_c.T], axis=0).astype(np.float32)
    sinT = np.concatenate([sin_r.T, sin_r.T, sin_c.T, sin_c.T], axis=0).astype(np.float32)
    return np.ascontiguousarray(cosT), np.ascontiguousarray(sinT)


def _filter_pos_tables():
    f32 = np.float32
    t = np.linspace(0.0, 1.0, L, dtype=f32)
    bands = 16
    w_ang = (f32(2.0 * math.pi) * np.arange(L, dtype=f32) / f32(L)).astype(f32)
    band_f = np.linspace(1e-4, bands - 1, bands, dtype=f32)
    ang = (w_ang[:, None] * band_f[None, :]).astype(f32)
    z = np.concatenate([t[:, None], np.cos(ang), -np.sin(ang)], axis=-1).astype(f32)
    zT = np.ascontiguousarray(z.T)
    idx = (L - np.arange(L)) % L
    zrT = np.ascontiguousarray(z[idx].T)
    trev = t[idx].copy()
    return zT, zrT, t.copy(), trev


def _fft_tables():
    N = NFFT
    n1 = np.arange(128, dtype=np.float64)
    k1 = np.arange(64, dtype=np.float64)
    th = 2 * np.pi * np.outer(n1, k1 + 0.5) / 128.0
    fpack = np.stack([np.sin(th), np.cos(th), -np.sin(th)], axis=2).reshape(128, 192)
    n2 = np.arange(128, dtype=np.float64)
    k2 = np.arange(128, dtype=np.float64)
    kap = k1[None, :, None] + 128.0 * k2[None, None, :] + 0.5
    thg = 2 * np.pi * n2[:, None, None] * kap / N
    G = np.stack([np.cos(thg), -np.sin(thg)], axis=2)
    thf = 2 * np.pi * np.outer(k2, n2) / 128.0
    finv = np.stack([np.cos(thf), np.sin(thf), -np.sin(thf), -np.cos(thf)], axis=1)
    n1h = np.arange(64, dtype=np.float64)
    phi = 2 * np.pi * (k1[:, None, None] + 0.5) * (128.0 * n1h[None, None, :] + n2[None, :, None]) / N
    M2 = np.concatenate([(2.0 / N) * np.cos(phi), -(2.0 / N) * np.sin(phi)], axis=0)
    return (fpack.astype(bf16_np), G.astype(bf16_np), finv.astype(bf16_np), M2.astype(bf16_np))


def _tables():
    global _TABLES
    if _TABLES is None:
        cosT, sinT = _rope_tables()
        zT, zrT, t, trev = _filter_pos_tables()
        fpack, G, finv, M2 = _fft_tables()
        ii = np.arange(512, dtype=np.float64)
        jj = np.arange(16, dtype=np.float64)
        tbm = np.stack([ii / (L - 1), (L - ii) / (L - 1)]).astype(np.float32)
        jtm = np.stack([512.0 * jj / (L - 1), -512.0 * jj / (L - 1)]).astype(np.float32)
        rot = np.zeros((64, 64), np.float32)
        for m in range(64):
            if (m % 32) < 16:
                rot[m + 16, m] = -1.0
            else:
                rot[m - 16, m] = 1.0
        sel = np.zeros((65, 64), np.float32)
        sel[64, :] = 1.0
        _TABLES = dict(cosT=cosT, sinT=sinT, zT=zT, zrT=zrT, t=t, trev=trev, fpack=fpack, G=G,
                       finv=finv, M2=M2, rot=rot, sel=sel, tb=tbm, jt=jtm,
                       ones=np.ones((128, 128), np.float32),
                       ident=np.eye(128, dtype=np.float32).astype(bf16_np))
    return _TABLES


def build_program(phases=(0, 1, 2, 3, 4), debug=()):
    nc = bass.Bass("TRN2", target_bir_lowering=False)
    S = Sched(nc)

    def din(name, shape, dt=F32):
        return nc.dram_tensor(name, list(shape), dt, kind="ExternalInput").ap()

    def dscr(name, shape, dt):
        return nc.dram_tensor(name, list(shape), dt, kind="Internal").ap()

    xT = din("xT", [D, L])
    xTq = din("xTq", [D, NTOK])
    w_in = din("w_in", [D, 2304])
    w_out = din("w_out", [D, D])
    w_mi = din("w_mi", [D, DFF])
    w_mo = din("w_mo", [DFF, D])
    g1 = din("g1", [128, 8]); g2 = din("g2", [128, 8]); gf = din("gf", [128, 8])
    gq = din("gq", [64, 1]); gk = din("gk", [64, 1])
    ga = din("ga", [128, 4]); gh = din("gh", [128, 4])
    cw = din("cw", [128, 12, 3]); cb = din("cb", [128, 12])
    fw1 = din("fw1", [33, 64]); fw2 = din("fw2", [64, 64]); fw3 = din("fw3", [64, 64])
    fw4 = din("fw4", [64, 2048])
    fb = din("fb", [64, 3]); ffreq = din("ffreq", [64, 1])
    fdel = din("fdel", [128, 16])
    skp = din("skp", [128, 8])
    qmask = din("qmask", [128, 4])
    c_cos = din("c_cos", [64, L]); c_sin = din("c_sin", [64, L])
    c_cosq = din("c_cosq", [64, NTOK]); c_sinq = din("c_sinq", [64, NTOK])
    c_zT = din("c_zT", [2, 33, L])
    c_t = din("c_t", [2, L])
    c_tb = din("c_tb", [2, 512])
    c_jt = din("c_jt", [2, 16])
    c_fpack = din("c_fpack", [128, 192], BF16)
    c_G = din("c_G", [128, 64, 2, 128], BF16)
    c_finv = din("c_finv", [128, 4, 128], BF16)
    c_M2 = din("c_M2", [128, 128, 64], BF16)
    c_M2q = din("c_M2q", [128, 128, 16], BF16)
    c_rot = din("c_rot", [64, 64]); c_sel = din("c_sel", [65, 64]); c_ones = din("c_ones", [128, 128])
    c_ident = din("c_ident", [128, 128], BF16)
    outT = nc.dram_tensor("outT", [D, NTOK], F32, kind="ExternalOutput").ap()

    U_d = dscr("U_d", [1536, L], BF16)
    H_d = dscr("H_d", [8, 128, 64, 4, 128], BF16)
    taps_d = dscr("taps_d", [128, NFFT], BF16)
    zcm_d = dscr("zcm_d", [128, L], BF16)
    att_d = dscr("att_d", [512, NTOK], BF16)
    hy_d = dscr("hy_d", [512, NTOK], BF16)
    Wmi_d = dscr("Wmi_d", [D, DFF], BF16)
    Wmo_d = dscr("Wmo_d", [8, 128, 32, 128], BF16)

    from contextlib import ExitStack
    es = ExitStack()

    _names = {}

    def sb(name, shape, dt, stack=None):
        n_ = _names.get(name, 0)
        _names[name] = n_ + 1
        nm = name if n_ == 0 else f"{name}_r{n_}"
        return (stack or es).enter_context(nc.sbuf_tensor(nm, list(shape), dt))

    def dma(out, in_, reads, writes, eng="sp"):
        return S.op(eng, lambda: S.E[eng].dma_start(out=out, in_=in_), reads, writes, dma=True)

    def mm(out, lhsT, rhs, start, stop, reads, writes):
        return S.op("pe", lambda: nc.tensor.matmul(out, lhsT, rhs, start=start, stop=stop), reads, writes)

    rr = {"i": 0}

    def alt(engs=("dve", "pool")):
        rr["i"] += 1
        return engs[rr["i"] % len(engs)]

    def V(eng):
        return S.E[eng]

    with es:
        PSW = [es.enter_context(nc.psum_tensor(f"psw{i}", [128, 1024], F32)) for i in range(3)]
        PS = [PSW[i // 2][:, (i % 2) * 512:(i % 2 + 1) * 512] for i in range(6)]
        PSB = [es.enter_context(nc.psum_tensor(f"psb{i}", [128, 1024], BF16)) for i in range(2)]
        PSX = [PSB[i].bitcast(F32)[:, :] for i in range(2)]

        ones = sb("ones", [128, 128], F32)
        ident = sb("ident", [128, 128], BF16)
        rot = sb("rot", [64, 64], F32)
        sel = sb("sel", [65, 64], F32)
        epsT = sb("epsT", [128, 1], F32)
        onesbf = sb("onesbf", [128, 128], BF16)
        g1s = sb("g1s", [128, 8], F32); g2s = sb("g2s", [128, 8], F32); gfs = sb("gfs", [128, 8], F32)
        gqs = sb("gqs", [64, 1], F32); gks = sb("gks", [64, 1], F32)
        gas = sb("gas", [128, 4], F32); ghs = sb("ghs", [128, 4], F32)
        cws = sb("cws", [128, 12, 3], F32); cbs = sb("cbs", [128, 12], F32)
        skps = sb("skps", [128, 8], F32); qms = sb("qms", [128, 4], F32)
        for t_, d_ in ((ones, c_ones), (ident, c_ident), (rot, c_rot), (sel, c_sel), (g1s, g1), (g2s, g2),
                       (gfs, gf), (gqs, gq), (gks, gk), (gas, ga), (ghs, gh), (cws, cw), (cbs, cb),
                       (skps, skp), (qms, qmask)):
            dma(t_[:], d_, [], [t_.name if hasattr(t_, "name") else id(t_)])
        KEY = lambda t_: t_.name if hasattr(t_, "name") else id(t_)
        S.op("dve", lambda: nc.vector.memset(epsT[:], EPS), [], ["epsT"])
        S.op("pool", lambda: nc.gpsimd.memset(onesbf[:], 1.0), [], ["onesbf"])

        FC = {}
        SNAP = {}

        def snap(name, src_ap, key, ncols):
            if name not in debug or name in SNAP:
                return
            SNAP[name] = nc.dram_tensor("dbg_" + name, [128, ncols], BF16, kind="ExternalOutput").ap()
            dma(SNAP[name], src_ap, [key], ["dbg_" + name])


        def load_fft_consts(stack):
            FC["fpack"] = sb("fpack", [128, 192], BF16, stack)
            FC["finv"] = sb("finv", [128, 4, 128], BF16, stack)
            FC["Gq"] = [sb(f"Gq{i}", [128, 8, 2, 128], BF16, stack) for i in range(2)]
            dma(FC["fpack"][:], c_fpack, [], ["fpack"])
            dma(FC["finv"][:], c_finv, [], ["finv"])

        def rsqrt(out, in_, scale, reads, writes, np_=128):
            S.op("act", lambda: nc.scalar.activation(out, in_, AF.Sqrt, bias=epsT[0:np_, :], scale=scale),
                 list(reads) + ["epsT"], writes)
            S.op("dve", lambda: nc.vector.reciprocal(out, out), writes, writes)

        def fft_forward(ztm, K, AT, cb_batch, zkey="ztm", akey="AT", pre_batch=None, cb_late=None):
            fpack = FC["fpack"]
            for c2 in range(64):
                ps = PS[c2 % 4]
                pk = f"ps{c2 % 4}"
                for cc in range(2):
                    c = c2 * 2 + cc
                    mm(ps[:, cc * 192:(cc + 1) * 192], ztm[0:K, c, :], fpack[0:K, :], True, True,
                       [(zkey, c // 32), "fpack"], [pk])
                e = alt(("act", "dve"))
                src = ps[:, 0:384].rearrange("p (cc j) -> p j cc", cc=2)
                dst = AT[:, :, c2 * 2:c2 * 2 + 2]
                if e == "act":
                    S.op("act", lambda dst=dst, src=src: nc.scalar.copy(dst, src), [pk], [akey])
                else:
                    S.op("dve", lambda dst=dst, src=src: nc.vector.tensor_copy(dst, src), [pk], [akey])

            def load_g(q8):
                dma(FC["Gq"][q8 % 2][:], c_G[:, q8 * 8:(q8 + 1) * 8], [], [f"Gq{q8 % 2}"])

            load_g(0)
            for kb in range(32):
                q4 = kb // 4
                Gq = FC["Gq"][q4 % 2]
                gkey = f"Gq{q4 % 2}"
                if kb % 4 == 0 and q4 + 1 < 8:
                    load_g(q4 + 1)
                if pre_batch is not None:
                    pre_batch(kb)
                ps = PS[2 + kb % 2]
                pk = f"ps{2 + kb % 2}"
                for kk in range(2):
                    k1 = kb * 2 + kk
                    kl = k1 % 8
                    zp = ps[:, kk * 256:(kk + 1) * 256].rearrange("p (r c) -> p r c", r=2)
                    mm(zp, Gq[:, kl, 0, :], AT[:, 3 * k1 + 1:3 * k1 + 3, :], True, False, [akey, gkey], [pk])
                    mm(zp, Gq[:, kl, 1, :], AT[:, 3 * k1:3 * k1 + 2, :], False, True, [akey, gkey], [pk])
                cb_batch(kb, ps, pk)
                if cb_late is not None and kb >= 1:
                    cb_late(kb - 1)
            if cb_late is not None:
                cb_late(31)

        cast_jobs = []
        for cbk in range(8):
            cast_jobs.append((w_mi.rearrange("(k p) n -> p k n", p=128)[:, :, cbk * 512:(cbk + 1) * 512],
                              Wmi_d.rearrange("(k p) n -> p k n", p=128)[:, :, cbk * 512:(cbk + 1) * 512], None))
        for kr in range(4):
            for cbk in range(2):
                cast_jobs.append((w_mo.rearrange("(k p) n -> p k n", p=128)[:, kr * 8:(kr + 1) * 8, cbk * 512:(cbk + 1) * 512],
                                  None, (kr, cbk)))

        def emit_cast(n, wst, wbf):
            src, dst, tl = cast_jobs[n]
            s_ = n % 2
            dma(wst[s_][:], src, [], [f"wst{s_}"])
            S.op("dve", lambda s_=s_: nc.vector.tensor_copy(wbf[s_][:], wst[s_][:]), [f"wst{s_}"], [f"wbf{s_}"])
            if tl is None:
                dma(dst, wbf[s_][:], [f"wbf{s_}"], ["Wmid"])
            else:
                kr_, cbk_ = tl
                for c_ in range(4):
                    dma(Wmo_d[cbk_ * 4 + c_, :, kr_ * 8:(kr_ + 1) * 8, :], wbf[s_][:, :, c_ * 128:(c_ + 1) * 128], [f"wbf{s_}"], ["Wmod"])

        def phase0():
          with ExitStack() as p0:
            load_fft_consts(p0)

            w1s = sb("w1s", [33, 64], F32, p0); w2s = sb("w2s", [64, 64], F32, p0); w3s = sb("w3s", [64, 64], F32, p0)
            w4b = sb("w4b", [64, 2048], BF16, p0)
            fbs = sb("fbs", [64, 3], F32, p0); frs = sb("frs", [64, 1], F32, p0); ffb = sb("ffb", [64, 3], F32, p0)
            dls = sb("dls", [128, 16], F32, p0); ndl = sb("ndl", [128, 16], F32, p0)
            negpi = sb("negpi", [128, 1], F32, p0)
            H3 = sb("H3", [64, 2, L], BF16, p0)
            p0m = ExitStack()
            w4s = sb("w4s", [64, 2048], F32, p0m)
            for t_, d_, k_ in ((w1s, fw1, "w1s"), (w2s, fw2, "w2s"), (w3s, fw3, "w3s"), (w4s, fw4, "w4s"),
                               (fbs, fb, "fbs"), (frs, ffreq, "frs"), (dls, fdel, "dls")):
                dma(t_[:], d_, [], [k_])
            S.op("dve", lambda: nc.vector.tensor_copy(w4b[:], w4s[:]), ["w4s"], ["w4b"])
            S.op("dve", lambda: nc.vector.tensor_scalar(ffb[:], fbs[:], frs[:, 0:1], None, ALU.mult), ["fbs", "frs"], ["ffb"])
            S.op("dve", lambda: nc.vector.tensor_scalar(ndl[:], dls[:], -1.0, None, ALU.mult), ["dls"], ["ndl"])
            S.op("dve", lambda: nc.vector.tensor_tensor(ndl[:], ndl[:], dls[:], ALU.min), ["dls", "ndl"], ["ndl"])
            S.op("dve", lambda: nc.vector.memset(negpi[:], 0.0), [], ["negpi"])
            NCH = 8
            zc = [sb(f"zc{i}", [33, 512], F32, p0m) for i in range(NCH)]
            ybL = [sb(f"yb{i}", [64, 512], F32, p0m) for i in range(NCH)]
            ttL = [sb(f"tt{i}", [64, 512], F32, p0m) for i in range(NCH)]
            nrL = [sb(f"nr{i}", [64, 512], F32, p0m) for i in range(NCH)]
            fhL = [sb(f"fh{i}", [64, 512], F32, p0m) for i in range(NCH)]
            frs2 = sb("frs2", [64, 1], F32, p0m)
            ffb2 = sb("ffb2", [64, 3], F32, p0m)
            S.op("dve", lambda: nc.vector.tensor_scalar(frs2[:], frs[:], 1.0 / TWO_PI, None, ALU.mult), ["frs"], ["frs2"])
            S.op("dve", lambda: nc.vector.tensor_scalar(ffb2[:], fbs[:], frs2[:, 0:1], None, ALU.mult), ["fbs", "frs2"], ["ffb2"])
            BANKS = [(PS[i], f"ps{i}") for i in range(6)] + [(PSX[0], "psb0"), (PSX[1], "psb1")]
            wL = [(w1s, "w1s"), (w2s, "w2s"), (w3s, "w3s")]

            def mlp_layer(c_, li, var, j):
                psm, pmk = BANKS[c_]
                src, srck = (zc[c_], f"zc{c_}") if li == 0 else (fhL[c_], f"fh{c_}")
                w_, wk_ = wL[li]
                mm(psm[0:64, :], w_[:], src[:], True, True, [srck, wk_], [pmk])
                S.op("act", lambda: nc.scalar.activation(ybL[c_][:], psm[0:64, :], AF.Identity, bias=ffb2[:, li:li + 1], scale=frs2[:, 0:1]),
                     [pmk, "frs2", "ffb2"], [f"yb{c_}"])
                S.op("dve", lambda: nc.vector.tensor_scalar(ttL[c_][:], ybL[c_][:], MAGIC, None, ALU.add), [f"yb{c_}"], [f"tt{c_}"])
                S.op("dve", lambda: nc.vector.scalar_tensor_tensor(out=nrL[c_][:], in0=ttL[c_][:], scalar=MAGIC, in1=ybL[c_][:],
                                                                  op0=ALU.subtract, op1=ALU.subtract), [f"tt{c_}", f"yb{c_}"], [f"nr{c_}"])
                if li < 2:
                    dst, dk = fhL[c_][:], f"fh{c_}"
                else:
                    dst, dk = H3[:, var, j * 512:(j + 1) * 512], ("H3", var)
                S.op("act", lambda: nc.scalar.activation(dst, nrL[c_][:], AF.Sin, bias=negpi[0:64, :], scale=-TWO_PI), [f"nr{c_}", "negpi"], [dk])

            for jg in range(4):
                chains = [(c_, c_ % 2, jg * 4 + c_ // 2) for c_ in range(NCH)]
                for c_, var, j in chains:
                    dma(zc[c_][:], c_zT[var, :, j * 512:(j + 1) * 512], [], [f"zc{c_}"])
                for li in range(3):
                    for c_, var, j in chains:
                        mlp_layer(c_, li, var, j)

            S.barrier()
            p0m.close()
            tapsUL = [sb(f"tapsU{i}", [128, NFFT], BF16, p0) for i in range(2)]
            ztmf = sb("ztmf", [128, 128, 128], BF16, p0)
            ATf = sb("ATf", [128, 192, 128], BF16, p0)
            tb = sb("tb", [128, 2, 512], F32, p0)
            jt = sb("jt", [128, 2, 16], F32, p0)
            for hf in range(2):
                dma(tb[:, hf, :], c_tb[hf:hf + 1, :].broadcast_to([128, 512]), [], ["tb"])
                dma(jt[:, hf, :], c_jt[hf:hf + 1, :].broadcast_to([128, 16]), [], ["jt"])
            wbase1 = sb("wbase", [128, 512], F32, p0)
            wbase = [wbase1, wbase1]
            wcj = [sb(f"wcj{i}", [128, 16], F32, p0) for i in range(2)]
            tpf = [sb(f"tpf{i}", [128, 512], F32, p0) for i in range(2)]
            psum_ = sb("psum_", [128, 32], F32, p0)
            nrm = sb("nrm", [128, 2], F32, p0)
            Hst = [sb(f"Hst{i}", [128, 2, 4, 128], BF16, p0) for i in range(2)]

            def taps_chunk(go, cidx):
                g, o = go // 2, go % 2
                half, j = cidx // 16, cidx % 16
                tU = tapsUL[go % 2]
                tk = f"tapsU{go % 2}"
                col0 = o * 1024 + half * 512 + g * 128
                ct = col0 // 128
                if j == 0:
                    S.op("act", lambda: nc.scalar.activation(wbase[half][:], tb[:, half, :], AF.Exp, scale=ndl[:, ct:ct + 1]),
                         ["tb", "ndl"], ["wbase"])
                    S.op("act", lambda: nc.scalar.activation(wcj[half][:], jt[:, half, :], AF.Exp, scale=ndl[:, ct:ct + 1]),
                         ["jt", "ndl"], [f"wcj{half}"])
                s_ = j % 2
                pst = PS[4 + s_]
                ptk = f"ps{4 + s_}"
                mm(pst[:, :], w4b[:, col0:col0 + 128], H3[:, half, j * 512:(j + 1) * 512], True, True, [("H3", half), "w4b"], [ptk])
                S.op("dve", lambda: nc.vector.scalar_tensor_tensor(out=tpf[s_][:], in0=pst[:, :], scalar=wcj[half][:, j:j + 1], in1=wbase[half][:],
                                                                  op0=ALU.mult, op1=ALU.mult), [ptk, f"wcj{half}", "wbase"], [f"tpf{s_}"])
                if half == 1 and j == 0:
                    S.op("dve", lambda: nc.vector.memset(tpf[s_][:, 0:1], 0.0), [f"tpf{s_}"], [f"tpf{s_}"])
                S.op("dve", lambda: nc.vector.tensor_reduce(psum_[:, cidx:cidx + 1], tpf[s_][:], mybir.AxisListType.X, ALU.add,
                                                            apply_absolute_value=True), [f"tpf{s_}"], ["psum_"])
                S.op("act", lambda: nc.scalar.copy(tU[:, half * L + j * 512: half * L + (j + 1) * 512], tpf[s_][:]), [f"tpf{s_}"], [tk])

            def taps_finish(go):
                tU = tapsUL[go % 2]
                tk = f"tapsU{go % 2}"
                S.op("dve", lambda: nc.vector.tensor_reduce(nrm[:, 0:1], psum_[:], mybir.AxisListType.X, ALU.add), ["psum_"], ["nrm"])
                S.op("dve", lambda: nc.vector.reciprocal(nrm[:, 0:1], nrm[:, 0:1]), ["nrm"], ["nrm"])
                S.op("dve", lambda: nc.vector.tensor_scalar(nrm[:, 1:2], nrm[:, 0:1], -1.0, None, ALU.mult), ["nrm"], ["nrm"])
                S.op("dve", lambda: nc.vector.tensor_scalar(tU[:, 0:L], tU[:, 0:L], nrm[:, 0:1], None, ALU.mult), [tk, "nrm"], [tk])
                S.op("act", lambda: nc.scalar.mul(tU[:, L:NFFT], tU[:, L:NFFT], nrm[:, 1:2]), [tk, "nrm"], [tk])
                tv = taps_d.rearrange("c (n1 n2) -> n1 c n2", n2=128)
                for q4 in range(4):
                    dma(taps_d[q4 * 32:(q4 + 1) * 32, :], tU[q4 * 32:(q4 + 1) * 32, :], [tk], [("taps_d", q4)])
                for q4 in range(4):
                    dma(ztmf[:, q4 * 32:(q4 + 1) * 32, :], tv[:, q4 * 32:(q4 + 1) * 32, :], [("taps_d", q4)], [("ztm", q4)])

            for cidx in range(32):
                taps_chunk(0, cidx)
            taps_finish(0)
            for go in range(8):
                def cb_filter(kb, ps, pk, go=go):
                    s_ = kb % 2
                    zv = ps[:, :].rearrange("p (k r c) -> p k r c", k=2, r=2)
                    S.op("act", lambda: nc.scalar.copy(Hst[s_][:, :, 0:2, :], zv), [pk], [f"Hst{s_}"])
                    S.op("dve", lambda: nc.vector.tensor_copy(Hst[s_][:, :, 2, :], zv[:, :, 1, :]), [pk], [f"Hst{s_}"])
                    S.op("dve", lambda: nc.vector.tensor_copy(Hst[s_][:, :, 3, :], zv[:, :, 0, :]), [pk], [f"Hst{s_}"])
                    dma(H_d[go, :, kb * 2:kb * 2 + 2], Hst[s_][:], [f"Hst{s_}"], [("H_d", go)])

                def pre_filter(kb, go=go):
                    if go + 1 < 8:
                        if kb < 16:
                            taps_chunk(go + 1, 2 * kb)
                            taps_chunk(go + 1, 2 * kb + 1)
                        elif kb == 16:
                            taps_finish(go + 1)

                fft_forward(ztmf, 128, ATf, cb_filter, pre_batch=pre_filter)
            S.barrier()
        def rms_apply(src, gvec, dst, scale, nk, tag, sqb, rs, np_=128, ss_ps=0, sqk="sqb", rsk="rs"):
            rms_a(src, nk, tag, sqb, np_, sqk)
            rms_b(src, gvec, dst, scale, nk, tag, sqb, rs, np_, ss_ps, sqk, rsk)

        def rms_a(src, nk, tag, sqb, np_=128, sqk="sqb"):
            S.op("act", lambda: nc.scalar.activation(sqb[0:np_, 0:nk, :], src, AF.Square), [tag + "_src"], [sqk])

        def rms_b(src, gvec, dst, scale, nk, tag, sqb, rs, np_=128, ss_ps=0, sqk="sqb", rsk="rs"):
            for k in range(nk):
                mm(PS[ss_ps][0:np_, :], onesbf[0:np_, 0:np_], sqb[0:np_, k, :], k == 0, k == nk - 1, [sqk, "onesbf"], [f"ps{ss_ps}"])
            S.op("act", lambda: nc.scalar.activation(rs[0:np_, :], PS[ss_ps][0:np_, :], AF.Sqrt, bias=epsT[0:np_, :], scale=scale),
                 [f"ps{ss_ps}", "epsT"], [rsk])
            S.op("dve", lambda: nc.vector.reciprocal(rs[0:np_, :], rs[0:np_, :]), [rsk], [rsk])
            for k in range(nk):
                S.op("dve", lambda k=k: nc.vector.scalar_tensor_tensor(out=dst[:, k, :], in0=src[:, k, :], scalar=gvec[:, k:k + 1],
                                                                      in1=rs[0:np_, :], op0=ALU.mult, op1=ALU.mult),
                     [tag + "_src", rsk, "gvecs"], [tag + "_dst"])

        def phase12():
          with ExitStack() as p12:
            KT = sb("KT", [128, 2, L], BF16, p12)
            Vx = sb("Vx", [128, 64, 2, 64], BF16, p12)
            QT = sb("QT", [128, 4, NTOK], BF16, p12)
            onesb = sb("onesb", [128, 64], BF16, p12)
            S.op("pool", lambda: nc.gpsimd.memset(onesb[:], 1.0), [], ["onesb"])
            with ExitStack() as p1:
                Wb = sb("Wb", [128, 8, 2304], BF16, p1)
                with ExitStack() as p1c:
                    wst = [sb(f"wsi{i}", [128, 8, 384], F32, p1c) for i in range(2)]
                    wv = w_in.rearrange("(k p) n -> p k n", p=128)
                    for bk in range(6):
                        s_ = bk % 2
                        dma(wst[s_][:], wv[:, :, bk * 384:(bk + 1) * 384], [], [f"wst{s_}"])
                        if bk % 2 == 0:
                            S.op("dve", lambda s_=s_, bk=bk: nc.vector.tensor_copy(Wb[:, :, bk * 384:(bk + 1) * 384], wst[s_][:]), [f"wst{s_}"], ["Wb"])
                        else:
                            S.op("act", lambda s_=s_, bk=bk: nc.scalar.copy(Wb[:, :, bk * 384:(bk + 1) * 384], wst[s_][:]), [f"wst{s_}"], ["Wb"])
                    S.barrier()
                xin = [sb(f"xin{i}", [128, 8, 512], F32, p1) for i in range(2)]
                sqbL = [sb(f"sqb{i}", [128, 8, 512], BF16, p1) for i in range(2)]
                rsL = [sb(f"rs{i}", [128, 512], F32, p1) for i in range(2)]
                aTL = [sb(f"aT{i}", [128, 8, 512], BF16, p1) for i in range(2)]
                ust = sb("ust", [128, 12, 512], BF16, p1)
                hsq = sb("hsq", [64, 512], BF16, p1); hrs = sb("hrs", [64, 512], F32, p1); hkn = sb("hkn", [64, 512], F32, p1)
                ht1 = sb("ht1", [64, 512], F32, p1); ht2 = sb("ht2", [64, 512], F32, p1)
                csb = [sb(f"csb{i}", [64, 2, 512], F32, p1) for i in range(2)]
                qstg = [sb(f"qstg{i}", [64, 512], BF16, p1) for i in range(2)]

                def hp1(srcb, gvec):
                    src = PS[srcb][0:64, :]
                    sk_ = f"ps{srcb}"
                    S.op("act", lambda: nc.scalar.activation(hsq[:], src, AF.Square), [sk_], ["hsq"])
                    mm(PS[3][0:64, :], onesbf[0:64, 0:64], hsq[:], True, True, ["hsq", "onesbf"], ["ps3"])
                    S.op("act", lambda: nc.scalar.activation(hrs[:], PS[3][0:64, :], AF.Sqrt, bias=epsT[0:64, :], scale=1.0 / 64.0),
                         ["ps3", "epsT"], ["hrs"])
                    S.op("dve", lambda: nc.vector.reciprocal(hrs[:], hrs[:]), ["hrs"], ["hrs"])
                    S.op("dve", lambda: nc.vector.scalar_tensor_tensor(out=hkn[:], in0=src, scalar=gvec[:, 0:1], in1=hrs[:],
                                                                      op0=ALU.mult, op1=ALU.mult), [sk_, "hrs", "gvecs"], ["hkn"])

                def hp2(cs, cskey, dst, dstkey):
                    mm(PS[3][0:64, :], rot[:], hkn[:], True, True, ["hkn", "rot"], ["ps3"])
                    S.op("pool", lambda: nc.gpsimd.tensor_tensor(ht1[:], hkn[:], cs[:, 0, :], ALU.mult), ["hkn", cskey], ["ht1"])
                    S.op("dve", lambda: nc.vector.tensor_tensor(ht2[:], PS[3][0:64, :], cs[:, 1, :], ALU.mult), ["ps3", cskey], ["ht2"])
                    S.op("pool", lambda: nc.gpsimd.tensor_tensor(dst, ht1[:], ht2[:], ALU.add), ["ht1", "ht2"], [dstkey])

                xv = xT.rearrange("(k p) t -> p k t", p=128)
                xqv = xTq.rearrange("(k p) t -> p k t", p=128)
                Uv = U_d.rearrange("(ct p) t -> p ct t", p=128)
                work = [("seq", j) for j in range(16)] + [("own", j) for j in range(4)]

                def norm_a(w):
                    kind, j = work[w]
                    s_ = w % 2
                    tsl = slice(j * 512, (j + 1) * 512)
                    srcv, cc, ss_ = (xv, c_cos, c_sin) if kind == "seq" else (xqv, c_cosq, c_sinq)
                    dma(xin[s_][:], srcv[:, :, tsl], [], [f"xin{s_}", f"n1{s_}_src"])
                    dma(csb[s_][:, 0, :], cc[:, tsl], [], [f"csb{s_}"])
                    dma(csb[s_][:, 1, :], ss_[:, tsl], [], [f"csb{s_}"])

                def norm_sq(w):
                    s_ = w % 2
                    rms_a(xin[s_][:], 8, f"n1{s_}", sqbL[s_], 128, f"sqb{s_}")

                def norm_b(w):
                    s_ = w % 2
                    rms_b(xin[s_][:], g1s, aTL[s_], 1.0 / D, 8, f"n1{s_}", sqbL[s_], rsL[s_], 128, 0, f"sqb{s_}", f"rs{s_}")

                def emit_proj(w, mid_hook, early_hook):
                    kind, j = work[w]
                    s_ = w % 2
                    aT = aTL[s_]
                    ak = f"n1{s_}_dst"
                    cs, csk = csb[s_], f"csb{s_}"
                    tsl = slice(j * 512, (j + 1) * 512)
                    if kind == "seq":
                        for kvh in range(2):
                            for k in range(8):
                                mm(PS[1 + kvh][0:64, :], Wb[:, k, 512 + kvh * 64:512 + (kvh + 1) * 64], aT[:, k, :], k == 0, k == 7,
                                   ["Wb", ak], [f"ps{1 + kvh}"])
                        for tt in range(4):
                            for k in range(8):
                                mm(PS[4][:, tt * 128:(tt + 1) * 128], aT[:, k, tt * 128:(tt + 1) * 128], Wb[:, k, 640:768], k == 0, k == 7,
                                   ["Wb", ak], ["ps4"])
                        S.op("act", lambda j=j: nc.scalar.copy(Vx[:, j * 4:(j + 1) * 4, :, :],
                                                             PS[4][:, :].rearrange("p (t h d) -> p t h d", t=4, h=2)), ["ps4"], ["Vx"])
                        for ct in range(12):
                            pu, puk = (PS[5], "ps5") if ct % 2 == 0 else (PSX[0], "psb0")
                            for k in range(8):
                                mm(pu[:, :], Wb[:, k, 768 + ct * 128:768 + (ct + 1) * 128], aT[:, k, :], k == 0, k == 7, ["Wb", ak], [puk])
                            if ct % 2 == 0:
                                S.op("act", lambda ct=ct, pu=pu: nc.scalar.copy(ust[:, ct, :], pu[:, :]), [puk], ["ust"])
                            else:
                                S.op("dve", lambda ct=ct, pu=pu: nc.vector.tensor_copy(ust[:, ct, :], pu[:, :]), [puk], ["ust"])
                            if ct == 1:
                                hp1(1, gks)
                            if ct == 2:
                                early_hook()
                            if ct == 3:
                                hp2(cs, csk, KT[0:64, 0, tsl], "KT")
                                hp1(2, gks)
                            if ct == 5:
                                hp2(cs, csk, KT[0:64, 1, tsl], "KT")
                            if ct == 7:
                                mid_hook()
                        dma(Uv[:, :, tsl], ust[:], ["ust"], ["U_d"])
                    else:
                        def qdst(h):
                            if h % 2 == 0:
                                return QT[0:64, h // 2, tsl], "QT", None
                            qs_ = (h // 2) % 2
                            return qstg[qs_][:], f"qstg{qs_}", qs_

                        def qproj(h):
                            hb = 1 + h % 2
                            for k in range(8):
                                mm(PS[hb][0:64, :], Wb[:, k, h * 64:(h + 1) * 64], aT[:, k, :], k == 0, k == 7, ["Wb", ak], [f"ps{hb}"])

                        qproj(0)
                        for h in range(8):
                            hp1(1 + h % 2, gqs)
                            if h + 1 < 8:
                                qproj(h + 1)
                            if h == 1:
                                early_hook()
                            if h == 3:
                                mid_hook()
                            d_, dk_, qs_ = qdst(h)
                            hp2(cs, csk, d_, dk_)
                            if qs_ is not None:
                                dma(QT[64:128, h // 2, tsl], qstg[qs_][:], [f"qstg{qs_}"], ["QT2"])

                norm_a(0)
                norm_sq(0)
                norm_b(0)
                for w in range(len(work)):
                    if w + 1 < len(work):
                        norm_a(w + 1)
                        emit_proj(w, lambda w=w: norm_b(w + 1), lambda w=w: norm_sq(w + 1))
                    else:
                        emit_proj(w, lambda: None, lambda: None)
                dma(KT[64:128, :, :], KT[0:64, :, :], ["KT"], ["KT2"])
                S.barrier()
            if 2 in phases:
              with ExitStack() as p2:
                pb = [sb(f"pb{i}", [128, 1024], BF16, p2) for i in range(2)]
                cwst = [sb(f"wst{i}", [128, 8, 512], F32, p2) for i in range(2)]
                cwbf = [sb(f"wbf{i}", [128, 8, 512], BF16, p2) for i in range(2)]
                rden = [sb(f"rden{i}", [128, 512], F32, p2) for i in range(2)]
                ast = [sb(f"ast{i}", [128, 512], BF16, p2) for i in range(2)]
                n_ = 0
                for qc in range(4):
                    qsl = slice(qc * 512, (qc + 1) * 512)
                    for hp in range(4):
                        kvh = hp // 2
                        (OB, obk), (DB, dbk) = ((PS[4], "ps4"), (PS[5], "ps5")) if n_ % 2 == 0 else ((PSX[0], "psb0"), (PSX[1], "psb1"))
                        emit_cast(qc * 4 + hp, cwst, cwbf)
                        for step in range(64 + 1):
                            if step < 64:
                                kt = step
                                sl = kt % 2
                                ksl = slice(kt * 128, (kt + 1) * 128)
                                mm(PS[2 * sl][:, :], KT[0:64, kvh, ksl], QT[0:64, hp, qsl], True, True, ["KT", "QT"], [f"ps{2 * sl}"])
                                mm(PS[2 * sl + 1][:, :], KT[64:128, kvh, ksl], QT[64:128, hp, qsl], True, True, ["KT2", "QT2"], [f"ps{2 * sl + 1}"])
                                S.op("act", lambda sl=sl: nc.scalar.activation(pb[sl][:], PSW[sl][:, :], AF.Exp, scale=0.125),
                                     [f"ps{2 * sl}", f"ps{2 * sl + 1}"], [f"pb{sl}"])
                            kt = step - 1
                            if kt >= 0:
                                sl = kt % 2
                                st_, sp_ = kt == 0, kt == 63
                                pA = pb[sl][:, 0:512]
                                pB = pb[sl][:, 512:1024]
                                vv = Vx[:, kt, kvh, :]
                                S.op("pe", lambda vv=vv, pA=pA, st_=st_, sp_=sp_, OB=OB: nc.tensor.matmul(OB[0:64, :], vv, pA, start=st_, stop=sp_, tile_position=(0, 0)),
                                     ["Vx", f"pb{sl}"], [obk])
                                S.op("pe", lambda vv=vv, pB=pB, st_=st_, sp_=sp_, OB=OB: nc.tensor.matmul(OB[64:128, :], vv, pB, start=st_, stop=sp_, tile_position=(0, 64)),
                                     ["Vx", f"pb{sl}"], [obk])
                                S.op("pe", lambda pA=pA, st_=st_, sp_=sp_, DB=DB: nc.tensor.matmul(DB[0:64, :], onesb[:], pA, start=st_, stop=sp_, tile_position=(0, 0)),
                                     ["onesb", f"pb{sl}"], [dbk])
                                S.op("pe", lambda pB=pB, st_=st_, sp_=sp_, DB=DB: nc.tensor.matmul(DB[64:128, :], onesb[:], pB, start=st_, stop=sp_, tile_position=(0, 64)),
                                     ["onesb", f"pb{sl}"], [dbk])
                        a_ = n_ % 2
                        S.op("dve", lambda DB=DB, a_=a_: nc.vector.reciprocal(rden[a_][:], DB[:, :]), [dbk], [f"rden{a_}"])
                        S.op("dve", lambda a_=a_, OB=OB: nc.vector.tensor_tensor(ast[a_][:], OB[:, :], rden[a_][:], ALU.mult), [obk, f"rden{a_}"], [f"ast{a_}"])
                        dma(att_d[hp * 128:(hp + 1) * 128, qsl], ast[a_][:], [f"ast{a_}"], ["att_d"])
                        n_ += 1
                S.barrier()

        def phase3():
          with ExitStack() as p3:
            load_fft_consts(p3)
            finv = FC["finv"]
            M2s = sb("M2s", [128, 128, 64], BF16, p3)
            M2qs = sb("M2qs", [128, 128, 16], BF16, p3)
            dma(M2s[:], c_M2, [], ["M2s"])
            dma(M2qs[:], c_M2q, [], ["M2qs"])
            RA = sb("RA", [128, 192 * 128 + 8], BF16, p3)
            RB = sb("RB", [128, 128 * 128], BF16, p3)
            RC = sb("RC", [128, 128 * 128], BF16, p3)
            vt = sb("vt", [128, L], BF16, p3)
            x1t = sb("x1t", [128, L], BF16, p3)
            x2q = sb("x2q", [128, NTOK], BF16, p3)
            zzq = sb("zzq", [128, NTOK], BF16, p3)
            hyst = sb("hyst", [128, NTOK], BF16, p3)
            Hc = [sb(f"Hc{i}", [128, 2, 4, 128], BF16, p3) for i in range(4)]
            Pb = [sb(f"Pb{i}", [128, 2, 2, 128], BF16, p3) for i in range(2)]
            Qb = [sb(f"Qb{i}", [128, 2, 2, 128], BF16, p3) for i in range(2)]
            gt = [sb(f"gt{i}", [128, 512], F32, p3) for i in range(2)]
            whi = sb("whi", [128, 9], BF16, p3)
            whf = sb("whf", [128, 9], F32, p3)
            wlf = sb("wlf", [128, 9], F32, p3)
            sct = gt
            uraw = RA[:, 0:3 * 8194].rearrange("p (i t) -> p i t", i=3)
            AT = RA[:, 0:192 * 128].rearrange("p (j c) -> p j c", c=128)
            ztm = RB[:, :].rearrange("p (c n) -> p c n", n=128)
            Eb = RB[:, :].rearrange("p (n c) -> p n c", c=128)
            x2t = RC[:, 0:L]
            ET = RC[:, :].rearrange("p (c k) -> p c k", k=128)
            RAK, RBK, RCK = "RA", "RB", "RC"

            def masked_quarter(dst, src, dkey, skey):
                S.op("dve", lambda: nc.vector.tensor_scalar(dst[:], src[:, 0:NTOK], qms[:, 0:1], None, ALU.mult), [skey, "qms"], [dkey])
                for q in range(1, 4):
                    S.op("dve", lambda q=q: nc.vector.scalar_tensor_tensor(out=dst[:], in0=src[:, q * NTOK:(q + 1) * NTOK], scalar=qms[:, q:q + 1],
                                                                          in1=dst[:], op0=ALU.mult, op1=ALU.add), [skey, "qms", dkey], [dkey])

            RBQ = [(RBK, q4) for q4 in range(4)]

            def bounce_to_ztm(src, skey):
                zv = zcm_d.rearrange("c (n1 n2) -> n1 c n2", n2=128)
                for q4 in range(4):
                    dma(zcm_d[q4 * 32:(q4 + 1) * 32, :], src[q4 * 32:(q4 + 1) * 32, :], [skey], [("zcm_d", q4)])
                for q4 in range(4):
                    dma(ztm[0:64, q4 * 32:(q4 + 1) * 32, :], zv[:, q4 * 32:(q4 + 1) * 32, :], [("zcm_d", q4)], [(RBK, q4)])

            def conv_core(go):
                def load_h(kb):
                    dma(Hc[kb % 4][:], H_d[go, :, kb * 2:kb * 2 + 2], [("H_d", go)], [f"Hc{kb % 4}"])

                def pre(kb):
                    if kb == 0:
                        load_h(0)
                        load_h(1)
                    if kb + 2 < 32:
                        load_h(kb + 2)

                def cb(kb, ps, pk):
                    s_ = kb % 4
                    y_ = kb % 2
                    zv = ps[:, :].rearrange("p (k r c) -> p k r c", k=2, r=2)
                    S.op("dve", lambda: nc.vector.tensor_tensor(Pb[y_][:], zv, Hc[s_][:, :, 0:2, :], ALU.mult), [pk, f"Hc{s_}"], [f"Pb{y_}"])
                    S.op("dve", lambda: nc.vector.tensor_tensor(Qb[y_][:], zv, Hc[s_][:, :, 2:4, :], ALU.mult), [pk, f"Hc{s_}"], [f"Qb{y_}"])

                def cb2(kb):
                    s_ = kb % 4
                    y_ = kb % 2
                    pe_ = PS[4 + kb % 2]
                    pek = f"ps{4 + kb % 2}"
                    er = pe_[:, 0:256].rearrange("p (k c) -> p k c", k=2)
                    ei = pe_[:, 256:512].rearrange("p (k c) -> p k c", k=2)
                    p0, p1 = Pb[y_][:, :, 0, :], Pb[y_][:, :, 1, :]
                    q0, q1 = Qb[y_][:, :, 0, :], Qb[y_][:, :, 1, :]
                    rk = [f"Pb{y_}", f"Qb{y_}", "finv"]
                    FR, FI, NFI, NFR = finv[:, 0, :], finv[:, 1, :], finv[:, 2, :], finv[:, 3, :]
                    mm(er, FR, p0, True, False, rk, [pek]); mm(er, NFR, p1, False, False, rk, [pek])
                    mm(er, NFI, q0, False, False, rk, [pek]); mm(er, NFI, q1, False, True, rk, [pek])
                    mm(ei, FI, p0, True, False, rk, [pek]); mm(ei, NFI, p1, False, False, rk, [pek])
                    mm(ei, FR, q0, False, False, rk, [pek]); mm(ei, FR, q1, False, True, rk, [pek])
                    S.op("act", lambda: nc.scalar.copy(ET[:, :, kb * 2:kb * 2 + 2], pe_[:, 0:256].rearrange("p (k c) -> p c k", k=2)), [pek], [RCK])
                    S.op("act", lambda: nc.scalar.copy(ET[:, :, 64 + kb * 2:64 + kb * 2 + 2], pe_[:, 256:512].rearrange("p (k c) -> p c k", k=2)),
                         [pek], [RCK])
                fft_forward(ztm, 64, AT, cb, zkey=RBK, akey=RAK, pre_batch=pre, cb_late=cb2)
                TB = [(PSB[0], "psb0"), (PSB[1], "psb1"), (PSW[0].bitcast(BF16), "ps0"), (PSW[1].bitcast(BF16), "ps2")]
                for c4 in range(32):
                    pb_, pbk = TB[c4 % 4]
                    for cc in range(4):
                        c = c4 * 4 + cc
                        S.op("pe", lambda c=c, cc=cc, pb_=pb_: nc.tensor.transpose(pb_[:, cc * 128:(cc + 1) * 128], ET[:, c, :], ident[:]),
                             [RCK, "ident"], [pbk])
                    src = pb_[:, 0:512].rearrange("p (cc n) -> p n cc", cc=4)
                    dst = Eb[:, :, c4 * 4:c4 * 4 + 4]
                    if c4 % 2 == 0:
                        S.op("act", lambda src=src, dst=dst: nc.scalar.copy(dst, src), [pbk], RBQ)
                    else:
                        S.op("dve", lambda src=src, dst=dst: nc.vector.tensor_copy(dst, src), [pbk], RBQ)

            Udv = U_d
            for g in range(4):
                for i in range(3):
                    dma(uraw[:, i, 1:L + 1], Udv[i * 512 + g * 128:i * 512 + (g + 1) * 128, :], ["U_d"], [RAK])
                S.op("pool", lambda: nc.gpsimd.memset(uraw[:, :, 0:1], 0.0), [RAK], [RAK])
                S.op("pool", lambda: nc.gpsimd.memset(uraw[:, :, L + 1:L + 2], 0.0), [RAK], [RAK])
                DG = hyst[:, 0:9 * 128].rearrange("p (i m) -> p i m", m=128)
                DGL = zzq[:, 0:9 * 128].rearrange("p (i m) -> p i m", m=128)
                wsel = cws[:, g:12:4, :]
                S.op("dve", lambda wsel=wsel: nc.vector.tensor_copy(whi[:].rearrange("p (i j) -> p i j", j=3), wsel), ["cws"], ["whi"])
                S.op("dve", lambda: nc.vector.tensor_copy(whf[:], whi[:]), ["whi"], ["whf"])
                S.op("dve", lambda wsel=wsel: nc.vector.tensor_tensor(wlf[:].rearrange("p (i j) -> p i j", j=3), wsel,
                                                                     whf[:].rearrange("p (i j) -> p i j", j=3), ALU.subtract), ["cws", "whf"], ["wlf"])
                for idx in range(9):
                    S.op("dve", lambda idx=idx: nc.vector.tensor_scalar(DG[:, idx, :], ident[:], whf[:, idx:idx + 1], None, ALU.mult), ["whf", "ident"], ["hyst"])
                    S.op("dve", lambda idx=idx: nc.vector.tensor_scalar(DGL[:, idx, :], ident[:], wlf[:, idx:idx + 1], None, ALU.mult), ["wlf", "ident"], ["zzq"])
                nq = 0
                for i, (dst, dkey) in enumerate(((vt, "vt"), (x1t, "x1t"), (x2t, RCK))):
                    ct = i * 4 + g
                    for q in range(16):
                        o0 = q * 512
                        ps = PS[nq % 2]
                        pk = f"ps{nq % 2}"
                        for j in range(3):
                            mm(ps[:, :], DG[:, i * 3 + j, :], uraw[:, i, o0 + j:o0 + j + 512], j == 0, False, ["hyst", RAK], [pk])
                        for j in range(3):
                            mm(ps[:, :], DGL[:, i * 3 + j, :], uraw[:, i, o0 + j:o0 + j + 512], False, j == 2, ["zzq", RAK], [pk])
                        if nq % 2 == 0:
                            S.op("act", lambda ps=ps, dst=dst, o0=o0, ct=ct: nc.scalar.activation(dst[:, o0:o0 + 512], ps[:, :], AF.Identity, bias=cbs[:, ct:ct + 1]),
                                 [pk, "cbs"], [dkey])
                        else:
                            S.op("dve", lambda ps=ps, dst=dst, o0=o0, ct=ct: nc.vector.tensor_scalar(dst[:, o0:o0 + 512], ps[:, :], cbs[:, ct:ct + 1], None, ALU.add),
                                 [pk, "cbs"], [dkey])
                        nq += 1
                masked_quarter(x2q, x2t, "x2q", RCK)
                snap("vt0", vt[:], "vt", L)
                snap("x1t0", x1t[:], "x1t", L)
                bounce_to_ztm(vt, "vt")
                conv_core(g * 2 + 0)
                vt3 = vt[:, :].rearrange("p (n1 n2) -> p n1 n2", n2=128)
                x13 = x1t[:, :].rearrange("p (n1 n2) -> p n1 n2", n2=128)
                for nb in range(16):
                    ps = PS[nb % 2]
                    pk = f"ps{nb % 2}"
                    for j in range(8):
                        n2 = nb * 8 + j
                        mm(ps[:, j * 64:(j + 1) * 64], Eb[:, n2, :], M2s[:, n2, :], True, True, RBQ + ["M2s"], [pk])
                    psv = ps[:, :].rearrange("p (j n) -> p n j", j=8)
                    gtv = gt[nb % 2][:, :].rearrange("p (n j) -> p n j", j=8)
                    gk_ = f"gt{nb % 2}"
                    S.op("dve", lambda psv=psv, gtv=gtv, nb=nb, g=g: nc.vector.scalar_tensor_tensor(out=gtv, in0=vt3[:, :, nb * 8:(nb + 1) * 8],
                                                                                             scalar=skps[:, g:g + 1], in1=psv, op0=ALU.mult, op1=ALU.add),
                         [pk, "vt", "skps"], [gk_])
                    ge_ = "pool" if nb % 2 == 0 else "dve"
                    S.op(ge_, lambda gtv=gtv, nb=nb, ge_=ge_: V(ge_).tensor_tensor(vt3[:, :, nb * 8:(nb + 1) * 8], gtv, x13[:, :, nb * 8:(nb + 1) * 8], ALU.mult),
                         [gk_, "x1t"], ["vt"])
                snap("zz0", vt[:], "vt", L)
                masked_quarter(zzq, vt, "zzq", "vt")
                bounce_to_ztm(vt, "vt")
                conv_core(g * 2 + 1)
                zq3 = zzq[:, :].rearrange("p (n1 n2) -> p n1 n2", n2=128)
                xq3 = x2q[:, :].rearrange("p (n1 n2) -> p n1 n2", n2=128)
                hy3 = hyst[:, :].rearrange("p (n1 n2) -> p n1 n2", n2=128)
                for nb in range(4):
                    ps = PS[nb % 2]
                    pk = f"ps{nb % 2}"
                    for j in range(32):
                        n2 = nb * 32 + j
                        mm(ps[:, j * 16:(j + 1) * 16], Eb[:, n2, :], M2qs[:, n2, :], True, True, RBQ + ["M2qs"], [pk])
                    psv = ps[:, :].rearrange("p (j n) -> p n j", j=32)
                    gtv = gt[nb % 2][:, :].rearrange("p (n j) -> p n j", j=32)
                    gk_ = f"gt{nb % 2}"
                    S.op("dve", lambda psv=psv, gtv=gtv, nb=nb, g=g: nc.vector.scalar_tensor_tensor(out=gtv, in0=zq3[:, :, nb * 32:(nb + 1) * 32],
                                                                                             scalar=skps[:, 4 + g:5 + g], in1=psv, op0=ALU.mult, op1=ALU.add),
                         [pk, "zzq", "skps"], [gk_])
                    S.op("pool", lambda gtv=gtv, nb=nb: nc.gpsimd.tensor_tensor(hy3[:, :, nb * 32:(nb + 1) * 32], gtv, xq3[:, :, nb * 32:(nb + 1) * 32], ALU.mult),
                         [gk_, "x2q"], ["hyst"])
                dma(hy_d[g * 128:(g + 1) * 128, :], hyst[:], ["hyst"], ["hy_d"])
            S.barrier()

        def phase4():
          with ExitStack() as p4:
            WoA = sb("WoA", [128, 4, D], BF16, p4)
            WoH = sb("WoH", [128, 4, D], BF16, p4)
            with ExitStack() as p4c:
                wsa = sb("wsa", [128, 4, D], F32, p4c)
                wsh = sb("wsh", [128, 4, D], F32, p4c)
                dma(wsa[:], w_out[0:512, :].rearrange("(a p) n -> p a n", p=128), [], ["wsa"])
                dma(wsh[:], w_out[512:1024, :].rearrange("(g p) n -> p g n", p=128), [], ["wsh"])
                S.op("dve", lambda: nc.vector.tensor_copy(WoA[:], wsa[:]), ["wsa"], ["WoA"])
                S.op("pool", lambda: nc.gpsimd.tensor_copy(WoH[:], wsh[:]), ["wsh"], ["WoH"])
                S.barrier()
            attcL = [sb(f"attc{i}", [128, 4, 512], BF16, p4) for i in range(2)]
            hycL = [sb(f"hyc{i}", [128, 4, 512], BF16, p4) for i in range(2)]
            sqb = sb("sqb4", [128, 8, 512], BF16, p4)
            rs = sb("rs4", [128, 512], F32, p4)
            sqbM = sb("sqbM", [128, 8, 512], BF16, p4)
            rsM = sb("rsM", [128, 512], F32, p4)
            mixAL = [sb(f"mixA{i}", [128, 4, 512], BF16, p4) for i in range(2)]
            mixHL = [sb(f"mixH{i}", [128, 4, 512], BF16, p4) for i in range(2)]
            h1L = [sb(f"h1_{i}", [128, 8, 512], F32, p4) for i in range(2)]
            mT = sb("mT", [128, 8, 512], BF16, p4)
            Wmi_s = [sb(f"Wmi_s{i}", [128, 8, 1024], BF16, p4) for i in range(2)]
            Wmo_s = [sb(f"Wmo_s{i}", [128, 32, 128], BF16, p4) for i in range(2)]
            fT = sb("fT", [128, 32, 512], BF16, p4)
            rb = [sb(f"rb{i}", [128, 512], F32, p4) for i in range(2)]
            xqv = xTq.rearrange("(k p) t -> p k t", p=128)
            Wmiv = Wmi_d.rearrange("(k p) n -> p k n", p=128)
            outv = outT.rearrange("(k p) t -> p k t", p=128)
            cnt = {"nw": 0, "nwo": 0}

            def hk(j):
                s2 = j % 2
                return [f"h1_{s2}", f"n2{s2}_src", f"nf{s2}_src"]

            def mix_prologue(j):
                s2 = j % 2
                tq = slice(j * 512, (j + 1) * 512)
                dma(attcL[s2][:], att_d[:, tq].rearrange("(a p) t -> p a t", p=128), ["att_d"], [f"attc{s2}", f"na{s2}_src"])
                dma(hycL[s2][:], hy_d[:, tq].rearrange("(g p) t -> p g t", p=128), ["hy_d"], [f"hyc{s2}", f"nh{s2}_src"])
                dma(h1L[s2][:], xqv[:, :, tq], [], hk(j))

            def mix_norms(j):
                s2 = j % 2
                rms_apply(attcL[s2][:], gas, mixAL[s2], 1.0 / 512.0, 4, f"na{s2}", sqbM, rsM, sqk="sqbM", rsk="rsM")
                rms_apply(hycL[s2][:], ghs, mixHL[s2], 1.0 / 512.0, 4, f"nh{s2}", sqbM, rsM, sqk="sqbM", rsk="rsM")

            def outproj(j):
                s2 = j % 2
                h1 = h1L[s2]
                for ct in range(8):
                    csl = slice(ct * 128, (ct + 1) * 128)
                    for h in range(4):
                        mm(PS[1][:, :], WoA[:, h, csl], mixAL[s2][:, h, :], h == 0, False, ["WoA", f"na{s2}_dst"], ["ps1"])
                    for g in range(4):
                        mm(PS[1][:, :], WoH[:, g, csl], mixHL[s2][:, g, :], False, g == 3, ["WoH", f"nh{s2}_dst"], ["ps1"])
                    S.op("dve", lambda ct=ct, h1=h1: nc.vector.tensor_tensor(h1[:, ct, :], h1[:, ct, :], PS[1][:, :], ALU.add), ["ps1", f"h1_{s2}"], hk(j))

            mix_prologue(0)
            mix_norms(0)
            outproj(0)
            for j in range(4):
                tsl = slice(j * 512, (j + 1) * 512)
                s2 = j % 2
                h1 = h1L[s2]
                rms_apply(h1[:], g2s, mT, 1.0 / D, 8, f"n2{s2}", sqb, rs)
                for pc in range(4):
                    s_ = cnt["nw"] % 2
                    dma(Wmi_s[s_][:], Wmiv[:, :, pc * 1024:(pc + 1) * 1024], ["Wmid"], [f"Wmi_s{s_}"])
                    cnt["nw"] += 1
                    for ft in range(8):
                        f = pc * 8 + ft
                        pf = PS[2 + f % 2]
                        pfk = f"ps{2 + f % 2}"
                        for k in range(8):
                            mm(pf[:, :], Wmi_s[s_][:, k, ft * 128:(ft + 1) * 128], mT[:, k, :], k == 0, k == 7, [f"Wmi_s{s_}", f"n2{s2}_dst"], [pfk])
                        r_ = f % 2
                        S.op("act", lambda pf=pf, r_=r_: nc.scalar.activation(rb[r_][:], pf[:, :], AF.Relu), [pfk], [f"rb{r_}"])
                        e = ("pool", "dve")[f % 2]
                        S.op(e, lambda e=e, r_=r_, f=f: V(e).tensor_tensor(fT[:, f, :], rb[r_][:], rb[r_][:], ALU.mult), [f"rb{r_}"], ["fT"])
                    if pc == 0 and j + 1 < 4:
                        mix_prologue(j + 1)
                    if pc == 1 and j + 1 < 4:
                        mix_norms(j + 1)
                for ct in range(8):
                    s_ = cnt["nwo"] % 2
                    dma(Wmo_s[s_][:], Wmo_d[ct], ["Wmod"], [f"Wmo_s{s_}"])
                    cnt["nwo"] += 1
                    po = PS[4 + ct % 2]
                    pok = f"ps{4 + ct % 2}"
                    for f in range(32):
                        mm(po[:, :], Wmo_s[s_][:, f, :], fT[:, f, :], f == 0, f == 31, [f"Wmo_s{s_}", "fT"], [pok])
                    S.op("dve", lambda ct=ct, po=po, h1=h1: nc.vector.tensor_tensor(h1[:, ct, :], h1[:, ct, :], po[:, :], ALU.add), [pok, f"h1_{s2}"], hk(j))
                rms_a(h1[:], 8, f"nf{s2}", sqb, 128, "sqb")
                if j + 1 < 4:
                    outproj(j + 1)
                rms_b(h1[:], gfs, h1, 1.0 / D, 8, f"nf{s2}", sqb, rs, 128, 0, "sqb", "rs")
                dma(outv[:, :, tsl], h1[:], [f"h1_{s2}", f"nf{s2}_dst"], ["outT"])
            S.barrier()

        if 0 in phases:
            phase0()
        if 1 in phases:
            phase12()
        if 3 in phases:
            phase3()
        if 4 in phases:
            phase4()
        if debug:
            with ExitStack() as pd:
                for name in debug:
                    if name not in ("att_d", "hy_d", "U_d", "H_d0"):
                        continue
                    src_, shp = {"att_d": (att_d, [512, NTOK]), "hy_d": (hy_d, [512, NTOK]), "U_d": (U_d[0:512, :], [512, L]),
                                 "H_d0": (H_d[0:4, :, 0:4].rearrange("a p k r c -> (a p) (k r c)"), [512, 2048])}[name]
                    dbo = nc.dram_tensor("dbg_" + name, shp, BF16, kind="ExternalOutput").ap()
                    dbs = sb("dbs_" + name, [128, 4, shp[1]], BF16, pd)
                    dma(dbs[:], src_.rearrange("(a p) t -> p a t", p=128), [name], ["dbs_" + name])
                    dma(dbo.rearrange("(a p) t -> p a t", p=128), dbs[:], ["dbs_" + name], ["dbg_" + name])
                S.barrier()
        S.barrier()
    return nc, S


def _core_inputs(inp, core):
    T = _tables()
    b, r = core // 4, core % 4
    f = lambda a: np.ascontiguousarray(np.asarray(a, dtype=np.float32))
    x = np.asarray(inp["x"], dtype=np.float32)
    xT = np.ascontiguousarray(x[b].T)
    tq = slice(r * NTOK, (r + 1) * NTOK)
    qm = np.zeros((128, 4), np.float32)
    qm[:, r] = 1.0
    m = {
        "xT": xT, "xTq": np.ascontiguousarray(xT[:, tq]),
        "w_in": f(inp["w_in"][0]), "w_out": f(inp["w_out"][0]), "w_mi": f(inp["w_mlp_in"][0]), "w_mo": f(inp["w_mlp_out"][0]),
        "g1": f(np.asarray(inp["norm1_g"][0]).reshape(8, 128).T), "g2": f(np.asarray(inp["norm2_g"][0]).reshape(8, 128).T),
        "gf": f(np.asarray(inp["final_g"]).reshape(8, 128).T),
        "gq": f(np.asarray(inp["q_norm_g"][0]).reshape(64, 1)), "gk": f(np.asarray(inp["k_norm_g"][0]).reshape(64, 1)),
        "ga": f(np.asarray(inp["attn_out_g"][0]).reshape(4, 128).T), "gh": f(np.asarray(inp["hy_out_g"][0]).reshape(4, 128).T),
        "cw": f(np.asarray(inp["hy_conv_w"][0]).T.reshape(12, 128, 3).transpose(1, 0, 2)),
        "cb": f(np.asarray(inp["hy_conv_b"][0]).reshape(12, 128).T),
        "fw1": f(inp["filt_w1"][0]), "fw2": f(inp["filt_w2"][0]), "fw3": f(inp["filt_w3"][0]), "fw4": f(inp["filt_w4"][0]),
        "fb": f(np.stack([np.asarray(inp["filt_b1"][0]), np.asarray(inp["filt_b2"][0]), np.asarray(inp["filt_b3"][0])], axis=1)),
        "ffreq": f(np.asarray(inp["filt_freq"][0]).reshape(64, 1)),
        "fdel": f(np.asarray(inp["filt_deltas"][0]).reshape(16, 128).T),
        "skp": f(np.asarray(inp["hy_skip_d"][0]).reshape(2, 4, 128).transpose(2, 0, 1).reshape(128, 8)),
        "qmask": qm,
        "c_cos": T["cosT"], "c_sin": T["sinT"],
        "c_cosq": np.ascontiguousarray(T["cosT"][:, tq]), "c_sinq": np.ascontiguousarray(T["sinT"][:, tq]),
        "c_zT": np.ascontiguousarray(np.stack([T["zT"], T["zrT"]])), "c_t": np.ascontiguousarray(np.stack([T["t"], T["trev"]])),
        "c_tb": T["tb"], "c_jt": T["jt"],
        "c_fpack": T["fpack"], "c_G": T["G"], "c_finv": T["finv"], "c_M2": T["M2"],
        "c_M2q": np.ascontiguousarray(T["M2"][:, :, r * 16:(r + 1) * 16]),
        "c_rot": T["rot"], "c_sel": T["sel"], "c_ones": T["ones"], "c_ident": T["ident"],
    }
    return m


def _emit(nc, S):
    from contextlib import ExitStack
    st = ExitStack()
    sems = {e: [st.enter_context(nc.semaphore(f"s_{e}{i}")) for i in range(4)] for e in S.CE}
    dsems = [st.enter_context(nc.semaphore(f"d{i}")) for i in range(S.NDS)]
    info = S.emit(sems, dsems)
    return st, info


def kernel(**inputs):
    nc, S = build_program()
    st, _ = _emit(nc, S)
    with st:
        in_maps = [_core_inputs(inputs, c) for c in range(8)]
        res = run_bass_kernel_spmd(nc, in_maps, core_ids=list(range(8)))
    out = np.empty((NB, L, D), np.float32)
    for c in range(8):
        b, r = c // 4, c % 4
        out[b, r * NTOK:(r + 1) * NTOK, :] = np.asarray(res.results[c]["outT"]).T
    return out
```
